# Optimizing a Trainium2 kernel written in Bass

```python
import math
import jax, jax.numpy as jnp
from jax import lax
import numpy as np

D_MODEL = 1024
BATCH = 2
SEQ = 8192
DEPTH = 2

CHUNK = 64
Q_BLOCK = 128
NORM_EPS = 1e-6

MLA_HEADS = 8
MLA_Q_RANK = 384
MLA_KV_RANK = 256
MLA_NOPE = 64
MLA_ROPE = 32
MLA_QK_DIM = MLA_NOPE + MLA_ROPE
MLA_V = 64
ROPE_THETA = 10000.0

FOX_HEADS = 8
FOX_HEAD_DIM = 64
FOX_WIDTH = FOX_HEADS * FOX_HEAD_DIM

SSM_HEADS = 16
SSM_HEAD_DIM = 64
SSM_INNER = SSM_HEADS * SSM_HEAD_DIM
SSM_GROUPS = 2
SSM_STATE = 128
SSM_CONV = 4
SSM_XBC = SSM_INNER + 2 * SSM_GROUPS * SSM_STATE

D_FF = 2816
FFN_CONV = 3

N_BRANCH = 3
MLA_OUT = MLA_HEADS * MLA_V
MLA_IN = MLA_Q_RANK + MLA_KV_RANK + MLA_ROPE
FOX_IN = 3 * FOX_WIDTH + FOX_HEADS
SSM_IN = SSM_INNER + SSM_XBC + SSM_HEADS
GATE_IN = N_BRANCH * D_MODEL
D_IN = MLA_IN + FOX_IN + SSM_IN + GATE_IN

kernel_name = "hybrid_mla_fox_ssd_gated_block"


def rms_norm(x, gain):
    xf = x.astype(jnp.float32)
    y = xf * lax.rsqrt(jnp.mean(xf * xf, axis=-1, keepdims=True) + NORM_EPS)
    return (y * gain.astype(jnp.float32)).astype(x.dtype)


def causal_dwconv(x, w, b):
    K, C = w.shape
    y = lax.conv_general_dilated(x, w[:, None, :].astype(x.dtype), window_strides=(1,),
                                 padding=[(K - 1, 0)], dimension_numbers=('NWC', 'WIO', 'NWC'),
                                 feature_group_count=C)
    return y + b


def rope_tables(positions):
    inv = 1.0 / (ROPE_THETA ** (jnp.arange(0, MLA_ROPE, 2, dtype=jnp.float32) / MLA_ROPE))
    ang = positions.astype(jnp.float32)[..., None] * inv
    return jnp.cos(ang)[:, :, None, :], jnp.sin(ang)[:, :, None, :]


def rope_tail(x, cos, sin):
    x_nope, x_rope = jnp.split(x, [MLA_NOPE], axis=-1)
    x1, x2 = jnp.split(x_rope.astype(jnp.float32), 2, axis=-1)
    rot = jnp.concatenate([x1 * cos - x2 * sin, x1 * sin + x2 * cos], axis=-1).astype(x.dtype)
    return jnp.concatenate([x_nope, rot], axis=-1)


def blocked_attention(q, k, v, bias_mask):
    Bsz, S, H, Dk = q.shape
    nb = S // Q_BLOCK
    qb = jnp.moveaxis(q.reshape(Bsz, nb, Q_BLOCK, H, Dk), 1, 0)
    scale = Dk ** -0.5

    def one_block(args):
        i, q_blk = args
        s = jnp.einsum('bqhd,bkhd->bhqk', q_blk, k).astype(jnp.float32) * scale
        p = jax.nn.softmax(bias_mask(i, s), axis=-1).astype(v.dtype)
        return jnp.einsum('bhqk,bkhd->bqhd', p, v)

    out = lax.map(one_block, (jnp.arange(nb), qb))
    return jnp.moveaxis(out, 0, 1).reshape(Bsz, S, H, v.shape[-1])


def mla_branch(u, cos, sin, q_norm_g, w_uq, kv_norm_g, w_ukv, q_gain, k_gain):
    Bsz, S, _ = u.shape
    c_q, c_kv, k_rope = jnp.split(u, [MLA_Q_RANK, MLA_Q_RANK + MLA_KV_RANK], axis=-1)
    q = (rms_norm(c_q, q_norm_g) @ w_uq).reshape(Bsz, S, MLA_HEADS, MLA_QK_DIM)
    kv = (rms_norm(c_kv, kv_norm_g) @ w_ukv).reshape(Bsz, S, MLA_HEADS, MLA_NOPE + MLA_V)
    k_nope, v = jnp.split(kv, [MLA_NOPE], axis=-1)
    k_rope = jnp.broadcast_to(k_rope[:, :, None, :], (Bsz, S, MLA_HEADS, MLA_ROPE))
    k = jnp.concatenate([k_nope, k_rope], axis=-1)
    q = rope_tail(rms_norm(q, q_gain), cos, sin)
    k = rope_tail(rms_norm(k, k_gain), cos, sin)

    def chunk_mask(i, s):
        q_idx = i * Q_BLOCK + jnp.arange(Q_BLOCK)
        k_idx = jnp.arange(S)
        allowed = (k_idx[None, :] // CHUNK) <= (q_idx[:, None] // CHUNK)
        return jnp.where(allowed, s, -jnp.inf)

    return blocked_attention(q, k, v, chunk_mask).reshape(Bsz, S, MLA_OUT)


def fox_branch(u, q_gain, k_gain, b_f):
    Bsz, S, _ = u.shape
    q, k, v, f_raw = jnp.split(u, [FOX_WIDTH, 2 * FOX_WIDTH, 3 * FOX_WIDTH], axis=-1)
    q = rms_norm(q.reshape(Bsz, S, FOX_HEADS, FOX_HEAD_DIM), q_gain)
    k = rms_norm(k.reshape(Bsz, S, FOX_HEADS, FOX_HEAD_DIM), k_gain)
    v = v.reshape(Bsz, S, FOX_HEADS, FOX_HEAD_DIM)
    log_f = jax.nn.log_sigmoid(f_raw.astype(jnp.float32) + b_f.astype(jnp.float32))
    F = jnp.transpose(jnp.cumsum(log_f, axis=1), (0, 2, 1))

    def decay_mask(i, s):
        q_idx = i * Q_BLOCK + jnp.arange(Q_BLOCK)
        k_idx = jnp.arange(S)
        Fq = lax.dynamic_slice_in_dim(F, i * Q_BLOCK, Q_BLOCK, axis=2)
        s = s + Fq[..., :, None] - F[..., None, :]
        return jnp.where(k_idx[None, :] <= q_idx[:, None], s, -jnp.inf)

    return blocked_attention(q, k, v, decay_mask).reshape(Bsz, S, FOX_WIDTH)


def ssd_scan(x, dt, A, Bm, Cm):
    Bsz, L, H, P = x.shape
    G, N = Bm.shape[-2:]
    Hg = H // G
    nc = L // CHUNK
    xf = x.astype(jnp.float32).reshape(Bsz, nc, CHUNK, G, Hg, P)
    dtc = dt.reshape(Bsz, nc, CHUNK, G, Hg)
    Bc = Bm.astype(jnp.float32).reshape(Bsz, nc, CHUNK, G, N)
    Cc = Cm.astype(jnp.float32).reshape(Bsz, nc, CHUNK, G, N)
    a_cum = jnp.cumsum(dtc * A.astype(jnp.float32).reshape(G, Hg), axis=2)
    xdt = xf * dtc[..., None]
    causal = jnp.tril(jnp.ones((CHUNK, CHUNK), dtype=bool))[None, None, :, :, None, None]
    seg = a_cum[:, :, :, None] - a_cum[:, :, None, :]
    decay = jnp.where(causal, jnp.exp(jnp.where(causal, seg, 0.0)), 0.0)
    cb = jnp.einsum('bcign,bcjgn->bcijg', Cc, Bc)
    y_diag = jnp.einsum('bcijg,bcijgh,bcjghp->bcighp', cb, decay, xdt)
    decay_to_end = jnp.exp(a_cum[:, :, -1:] - a_cum)
    states = jnp.einsum('bcjgn,bcjgh,bcjghp->bcghpn', Bc, decay_to_end, xdt)
    chunk_decay = jnp.exp(a_cum[:, :, -1])

    def step(h, inp):
        st, dec = inp
        return h * dec[..., None, None] + st, h

    h0 = jnp.zeros((Bsz, G, Hg, P, N), jnp.float32)
    _, h_in = lax.scan(step, h0, (jnp.moveaxis(states, 1, 0), jnp.moveaxis(chunk_decay, 1, 0)))
    h_in = jnp.moveaxis(h_in, 0, 1)
    y_off = jnp.einsum('bcign,bcghpn,bcigh->bcighp', Cc, h_in, jnp.exp(a_cum))
    return (y_diag + y_off).reshape(Bsz, L, H, P).astype(x.dtype)


def ssm_branch(u, conv_w, conv_b, dt_bias, A_log, D, norm_g):
    Bsz, S, _ = u.shape
    z, xBC, dt_raw = jnp.split(u, [SSM_INNER, SSM_INNER + SSM_XBC], axis=-1)
    xBC = jax.nn.silu(causal_dwconv(xBC, conv_w, conv_b))
    xs, Bm, Cm = jnp.split(xBC, [SSM_INNER, SSM_INNER + SSM_GROUPS * SSM_STATE], axis=-1)
    dt = jax.nn.softplus(dt_raw.astype(jnp.float32) + dt_bias.astype(jnp.float32))
    A = -jnp.exp(A_log.astype(jnp.float32))
    xh = xs.reshape(Bsz, S, SSM_HEADS, SSM_HEAD_DIM)
    y = ssd_scan(xh, dt, A, Bm.reshape(Bsz, S, SSM_GROUPS, SSM_STATE),
                 Cm.reshape(Bsz, S, SSM_GROUPS, SSM_STATE))
    y = (y + xh * D[:, None]).reshape(Bsz, S, SSM_INNER)
    return rms_norm(y * jax.nn.silu(z), norm_g)


def conv_ffn(h, w_up, conv_w, conv_b, w_down):
    a = causal_dwconv(h @ w_up, conv_w, conv_b)
    gate, val = jnp.split(a, 2, axis=-1)
    return (jax.nn.silu(gate) * val) @ w_down


def setup_inputs(seed: int = 0) -> dict:
    key = jax.random.key(seed)
    ks = iter(jax.random.split(key, 48))
    f32 = jnp.float32
    L = DEPTH

    def nrm(shape, scale):
        return scale * jax.random.normal(next(ks), shape, f32)

    def gain(shape):
        return 1.0 + 0.02 * jax.random.normal(next(ks), shape, f32)

    x = jax.random.normal(next(ks), (BATCH, SEQ, D_MODEL), f32)
    steps = jax.random.randint(next(ks), (BATCH, SEQ), 1, 3, dtype=jnp.int32)
    positions = (jnp.cumsum(steps, axis=1) - 1).astype(jnp.int32)
    dt0 = jnp.exp(jax.random.uniform(next(ks), (L, SSM_HEADS), f32, math.log(1e-3), math.log(1e-1)))
    return {
        'x': x,
        'positions': positions,
        'norm_mix_g': gain((L, D_MODEL)),
        'w_in': nrm((L, D_MODEL, D_IN), D_MODEL ** -0.5),
        'b_gate': nrm((L, GATE_IN), 0.01),
        'mla_q_norm_g': gain((L, MLA_Q_RANK)),
        'mla_w_uq': nrm((L, MLA_Q_RANK, MLA_HEADS * MLA_QK_DIM), MLA_Q_RANK ** -0.5),
        'mla_kv_norm_g': gain((L, MLA_KV_RANK)),
        'mla_w_ukv': nrm((L, MLA_KV_RANK, MLA_HEADS * (MLA_NOPE + MLA_V)), MLA_KV_RANK ** -0.5),
        'mla_q_gain': gain((L, MLA_QK_DIM)),
        'mla_k_gain': gain((L, MLA_QK_DIM)),
        'fox_q_gain': gain((L, FOX_HEAD_DIM)),
        'fox_k_gain': gain((L, FOX_HEAD_DIM)),
        'fox_b_f': jax.random.uniform(next(ks), (L, FOX_HEADS), f32, 2.0, 6.0),
        'ssm_conv_w': nrm((L, SSM_CONV, SSM_XBC), SSM_CONV ** -0.5),
        'ssm_conv_b': nrm((L, SSM_XBC), 0.01),
        'ssm_dt_bias': dt0 + jnp.log(-jnp.expm1(-dt0)),
        'ssm_A_log': jnp.log(jax.random.uniform(next(ks), (L, SSM_HEADS), f32, 1.0, 16.0)),
        'ssm_D': gain((L, SSM_HEADS)),
        'ssm_norm_g': gain((L, SSM_INNER)),
        'w_br_mla': nrm((L, MLA_OUT, D_MODEL), MLA_OUT ** -0.5),
        'w_br_fox': nrm((L, FOX_WIDTH, D_MODEL), FOX_WIDTH ** -0.5),
        'w_br_ssm': nrm((L, SSM_INNER, D_MODEL), SSM_INNER ** -0.5),
        'w_out': nrm((L, D_MODEL, D_MODEL), D_MODEL ** -0.5),
        'norm_ffn_g': gain((L, D_MODEL)),
        'ffn_w_up': nrm((L, D_MODEL, 2 * D_FF), D_MODEL ** -0.5),
        'ffn_conv_w': nrm((L, FFN_CONV, 2 * D_FF), FFN_CONV ** -0.5),
        'ffn_conv_b': nrm((L, 2 * D_FF), 0.01),
        'ffn_w_down': nrm((L, D_FF, D_MODEL), D_FF ** -0.5),
    }


def reference(x, positions, norm_mix_g, w_in, b_gate, mla_q_norm_g, mla_w_uq, mla_kv_norm_g,
              mla_w_ukv, mla_q_gain, mla_k_gain, fox_q_gain, fox_k_gain, fox_b_f, ssm_conv_w,
              ssm_conv_b, ssm_dt_bias, ssm_A_log, ssm_D, ssm_norm_g, w_br_mla, w_br_fox, w_br_ssm,
              w_out, norm_ffn_g, ffn_w_up, ffn_conv_w, ffn_conv_b, ffn_w_down):
    Bsz, S, _ = x.shape
    cos, sin = rope_tables(positions)
    split_at = [MLA_IN, MLA_IN + FOX_IN, MLA_IN + FOX_IN + SSM_IN]
    for l in range(DEPTH):
        hn = rms_norm(x, norm_mix_g[l])
        u = hn @ w_in[l]
        u_mla, u_fox, u_ssm, u_gate = jnp.split(u, split_at, axis=-1)
        y_a = mla_branch(u_mla, cos, sin, mla_q_norm_g[l], mla_w_uq[l], mla_kv_norm_g[l],
                         mla_w_ukv[l], mla_q_gain[l], mla_k_gain[l]) @ w_br_mla[l]
        y_b = fox_branch(u_fox, fox_q_gain[l], fox_k_gain[l], fox_b_f[l]) @ w_br_fox[l]
        y_c = ssm_branch(u_ssm, ssm_conv_w[l], ssm_conv_b[l], ssm_dt_bias[l], ssm_A_log[l],
                         ssm_D[l], ssm_norm_g[l]) @ w_br_ssm[l]
        g = jax.nn.sigmoid(u_gate + b_gate[l]).reshape(Bsz, S, N_BRANCH, D_MODEL)
        merged = g[:, :, 0] * y_a + g[:, :, 1] * y_b + g[:, :, 2] * y_c
        x = x + merged @ w_out[l]
        x = x + conv_ffn(rms_norm(x, norm_ffn_g[l]), ffn_w_up[l], ffn_conv_w[l], ffn_conv_b[l],
                         ffn_w_down[l])
    return x
```

```python
from contextlib import ExitStack
import numpy as np
import concourse.bass as bass
import concourse.mybir as mybir

F32 = mybir.dt.float32
BF16 = mybir.dt.bfloat16
I32 = mybir.dt.int32
ALU = mybir.AluOpType
AF = mybir.ActivationFunctionType
AX = mybir.AxisListType

COMPUTE = ("tensor", "vector", "scalar", "gpsimd")
QUEUES = ("sync", "gpsimd", "scalar")
NRING = 8


class View:
    __slots__ = ("buf", "ap", "key")

    def __init__(self, buf, ap, key=None):
        self.buf = buf
        self.ap = ap
        self.key = key

    def __getitem__(self, k):
        return View(self.buf, self.ap[k], self.key)

    def re(self, s, **kw):
        return View(self.buf, self.ap.rearrange(s, **kw), self.key)

    def bc(self, shape):
        return View(self.buf, self.ap.to_broadcast(shape), self.key)

    def bitcast(self, dt):
        return View(self.buf, self.ap.bitcast(dt), self.key)

    def k(self, key):
        return View(self.buf, self.ap, key)

    def f(self, fn):
        return View(self.buf, fn(self.ap), self.key)


class Buf:
    def __init__(self, name, handle, is_dram=False):
        self.name = name
        self.h = handle
        self.is_dram = is_dram
        self.regions = {}

    def full(self):
        ap = self.h.ap() if hasattr(self.h, "ap") and callable(getattr(self.h, "ap")) else self.h[:]
        return View(self, ap)

    def __getitem__(self, k):
        return View(self, self.h[k])


class Op:
    __slots__ = ("id", "eng", "meth", "kw", "deps", "is_dma", "signaled", "sem", "val", "prewait", "eidx")


class Eng:
    def __init__(self, P, name):
        self.P = P
        self.name = name

    def __getattr__(self, meth):
        def call(*a, **kw):
            assert not a, "use kwargs"
            return self.P._record(self.name, meth, kw)
        return call


class Prog:
    def __init__(self, nc):
        self.nc = nc
        self.ops = []
        self.gstack = ExitStack()
        self.stack = ExitStack()
        self.pe = Eng(self, "tensor")
        self.dve = Eng(self, "vector")
        self.act = Eng(self, "scalar")
        self.pool = Eng(self, "gpsimd")
        self.sp = Eng(self, "sync")
        st = self.gstack
        self.csem = {e: st.enter_context(nc.semaphore(f"c_{e}")) for e in COMPUTE}
        self.rings = {q: [st.enter_context(nc.semaphore(f"d_{q}{i}")) for i in range(NRING)] for q in QUEUES}
        self.ccsem = st.enter_context(nc.semaphore("ccsem"))
        self.cccount = 0
        self.ccount = {e: 0 for e in COMPUTE}
        self.dcount = {q: 0 for q in QUEUES}
        self.waited = {e: {} for e in ("sync",) + COMPUTE}
        self.emitted = 0
        self.barrier = []
        self.stats = {}
        self.nwaits = 0

    def sb(self, name, shape, dtype):
        self.nuid = getattr(self, "nuid", 0) + 1
        name = f"{name}_s{self.nuid}"
        t = self.stack.enter_context(self.nc.sbuf_tensor(name, list(shape), dtype))
        return Buf(name, t)

    def ps(self, name, shape, dtype):
        self.nuid = getattr(self, "nuid", 0) + 1
        name = f"{name}_p{self.nuid}"
        t = self.stack.enter_context(self.nc.psum_tensor(name, list(shape), dtype))
        return Buf(name, t)

    def dram(self, name, shape, dtype, kind="Internal"):
        t = self.nc.dram_tensor(name, list(shape), dtype, kind=kind)
        return Buf(name, t, is_dram=True)

    def _record(self, eng, meth, kw):
        op = Op()
        op.id = len(self.ops)
        op.eng = eng
        op.meth = meth
        op.kw = kw
        op.is_dma = meth in ("dma_start", "dma_start_transpose", "collective_compute")
        op.signaled = False
        op.sem = None
        op.val = 0
        op.prewait = None
        deps = set()
        extra_r = kw.pop("_reads", [])
        extra_w = kw.pop("_writes", [])
        writes, reads = [], []
        for k, v in kw.items():
            vs = v if isinstance(v, (list, tuple)) else [v]
            for x in vs:
                if isinstance(x, View):
                    if k in ("out", "accum_out", "outs") or (k == "ap" and meth in ("memset", "memzero")):
                        writes.append(x)
                    else:
                        reads.append(x)
        reads += extra_r
        writes += extra_w
        for v in reads:
            self._gather(v, False, deps)
        for v in writes:
            self._gather(v, True, deps)
        for v in reads:
            self._update(v, False, op.id)
        for v in writes:
            self._update(v, True, op.id)
        deps.discard(op.id)
        op.deps = deps
        self.ops.append(op)
        return op

    def _gather(self, v, is_write, deps):
        R = v.buf.regions
        if v.key is None:
            regs = list(R.values())
        else:
            regs = [R[k] for k in (v.key, None) if k in R]
        for reg in regs:
            if reg[0] is not None:
                deps.add(reg[0])
            if is_write:
                deps.update(reg[1])

    def _update(self, v, is_write, oid):
        R = v.buf.regions
        if is_write:
            if v.key is None:
                R.clear()
            R[v.key] = [oid, []]
        else:
            R.setdefault(v.key, [None, []])[1].append(oid)

    def dma(self, q, out, in_, **kw):
        eng = {"sync": self.sp, "gpsimd": self.pool, "scalar": self.act}[q]
        return eng.dma_start(out=out, in_=in_, **kw)

    def emit(self, final=True):
        nc = self.nc
        ops = self.ops
        phase = ops[self.emitted:]
        first_id = self.emitted
        self.emitted = len(ops)
        for op in phase:
            for d in op.deps:
                dop = ops[d]
                if d < first_id:
                    continue
                if dop.eng == "tensor" and op.eng == "tensor" and not dop.is_dma and not op.is_dma:
                    continue
                dop.signaled = True
        per = {}
        for op in phase:
            per.setdefault(op.eng, []).append(op)
        for e, lst in per.items():
            for op in reversed(lst):
                if not op.is_dma:
                    op.signaled = True
                    break
        for op in phase:
            if op.meth == "collective_compute":
                self.cccount += 1
                op.sem = self.ccsem
                op.val = self.cccount
                op.signaled = True
            elif op.is_dma:
                k = self.dcount[op.eng]
                self.dcount[op.eng] += 1
                op.sem = self.rings[op.eng][k % NRING]
                op.val = 16 * (k // NRING + 1)
                if k >= NRING:
                    op.prewait = (op.sem, 16 * (k // NRING))
                op.signaled = True
            elif op.signaled:
                self.ccount[op.eng] += 1
                op.sem = self.csem[op.eng]
                op.val = self.ccount[op.eng]
        for e, v in per.items():
            self.stats[e] = self.stats.get(e, 0) + len(v)
        barrier_in = list(self.barrier)
        dcount = self.dcount
        rings = self.rings

        def dma_final_waits():
            ws = []
            for q in QUEUES:
                n = dcount[q]
                for i in range(min(n, NRING)):
                    cnt = (n - 1 - i) // NRING + 1
                    ws.append((rings[q][i], 16 * cnt))
            if self.cccount > 0:
                ws.append((self.ccsem, self.cccount))
            return ws

        def run(engname, e):
            waited = self.waited[engname]

            def do_waits(ws):
                for sem, val in ws:
                    key = id(sem)
                    if waited.get(key, 0) >= val:
                        continue
                    waited[key] = val
                    e.wait_ge(sem, val)
                    self.nwaits += 1

            do_waits(barrier_in)
            for op in per.get(engname, []):
                ws = []
                if op.prewait is not None:
                    ws.append(op.prewait)
                for d in sorted(op.deps):
                    dop = ops[d]
                    if dop.sem is None:
                        continue
                    if dop.eng == "tensor" and op.eng == "tensor" and not dop.is_dma and not op.is_dma:
                        continue
                    ws.append((dop.sem, dop.val))
                do_waits(ws)
                kw = {}
                for k, v in op.kw.items():
                    if isinstance(v, View):
                        kw[k] = v.ap
                    elif isinstance(v, (list, tuple)) and v and isinstance(v[0], View):
                        kw[k] = [x.ap for x in v]
                    else:
                        kw[k] = v
                ins = getattr(e, op.meth)(**kw)
                if op.signaled:
                    ins.then_inc(op.sem, 16 if (op.is_dma and op.meth != "collective_compute") else 1)
            if final and engname == "sync":
                do_waits(dma_final_waits())

        with nc.Block() as block:
            @block.sync
            def _(e):
                run("sync", e)

            @block.tensor
            def _(e):
                run("tensor", e)

            @block.vector
            def _(e):
                run("vector", e)

            @block.scalar
            def _(e):
                run("scalar", e)

            @block.gpsimd
            def _(e):
                run("gpsimd", e)
        bar = dma_final_waits()
        for e in COMPUTE:
            if self.ccount[e] > 0:
                bar.append((self.csem[e], self.ccount[e]))
        self.barrier = bar
        self.stats["waits"] = self.nwaits
        self.stack.close()
        self.stack = ExitStack()
        if final:
            self.gstack.close()


from concourse.bass_utils import run_bass_kernel_spmd
import ml_dtypes

NBF = ml_dtypes.bfloat16
D = 1024
T = 2048
HALO = 4
NEG = -30000.0


class ColPack:
    def __init__(self):
        self.cols = []
        self.off = {}
        self.n = 0

    def add(self, name, vec, rows=128):
        vec = np.asarray(vec, np.float32).reshape(-1)
        assert vec.size % rows == 0
        m = vec.reshape(-1, rows).T
        a = np.zeros((128, m.shape[1]), np.float32)
        a[:rows] = m
        self.off[name] = (self.n, m.shape[1], rows)
        self.cols.append(a)
        self.n += m.shape[1]

    def array(self):
        return np.ascontiguousarray(np.concatenate(self.cols, axis=1))


class Cst:
    def __init__(self, P, buf, off):
        self.buf = buf
        self.off = off

    def col(self, name, j=0, rows=None):
        o, n, r = self.off[name]
        r = rows or r
        return self.buf[0:r, o + j:o + j + 1]

    def cols(self, name):
        o, n, r = self.off[name]
        return self.buf[0:r, o:o + n]


def new_nc():
    return bass.Bass("TRN2", target_bir_lowering=False)


def load_cast(P, q, dram_view, stage_view, bf_view, cast_eng):
    P.dma(q, out=stage_view, in_=dram_view)
    cast_eng.tensor_copy(out=bf_view, in_=stage_view)


A_OFF = None


def a_colpack(inp, l):
    cp = ColPack()
    cp.add("g_mix", inp["norm_mix_g"][l])
    cp.add("g_cq", inp["mla_q_norm_g"][l])
    cp.add("g_ckv", inp["mla_kv_norm_g"][l])
    cp.add("g_q", inp["mla_q_gain"][l], 96)
    cp.add("g_k", inp["mla_k_gain"][l], 96)
    cp.add("g_fq", inp["fox_q_gain"][l], 64)
    cp.add("g_fk", inp["fox_k_gain"][l], 64)
    cp.add("b_f", inp["fox_b_f"][l], 8)
    cw = inp["ssm_conv_w"][l]
    for k in range(4):
        cp.add(f"cw{k}", cw[k])
    cp.add("cb", inp["ssm_conv_b"][l])
    cp.add("dt_b", inp["ssm_dt_bias"][l], 16)
    cp.add("A_log", inp["ssm_A_log"][l], 16)
    cp.add("b_gate", inp["b_gate"][l])
    inv = 1.0 / (10000.0 ** (np.arange(0, 32, 2, dtype=np.float32) / 32.0))
    invf = np.zeros(96, np.float32)
    invf[64:80] = inv
    invf[80:96] = inv
    cp.add("invf", invf, 96)
    return cp


def build_A(off):
    nc = new_nc()
    P = Prog(nc)
    TT = T + HALO
    NT = T // 512
    EI, EO = "ExternalInput", "ExternalOutput"
    xT = P.dram("xT", [D, TT], F32, EI)
    pos = P.dram("pos", [1, T], I32, EI)
    w_in = P.dram("w_in", [D, 7864], F32, EI)
    w_uq = P.dram("w_uq", [384, 768], F32, EI)
    w_kp = P.dram("w_kp", [256, 768], F32, EI)
    w_v = P.dram("w_v", [256, 512], F32, EI)
    cst_d = P.dram("cst", [128, off["_n"]], F32, EI)
    mats = P.dram("mats", [128, 2 * 96], F32, EI)
    o_qm = P.dram("o_qm", [8, 96, T], BF16, EO)
    o_km = P.dram("o_km", [8, 96, T], BF16, EO)
    o_vm = P.dram("o_vm", [512, T], BF16, EO)
    o_qf = P.dram("o_qf", [8, 64, T], BF16, EO)
    o_kf = P.dram("o_kf", [8, 64, T], BF16, EO)
    o_vf = P.dram("o_vf", [512, T], BF16, EO)
    o_lf = P.dram("o_lf", [8, T], F32, EO)
    o_sz = P.dram("o_sz", [1024, T], BF16, EO)
    o_xbc = P.dram("o_xbc", [1536, T], BF16, EO)
    o_dt = P.dram("o_dt", [16, T], F32, EO)
    o_a = P.dram("o_a", [16, T], F32, EO)
    o_g = P.dram("o_g", [3072, T], BF16, EO)

    cstb = P.sb("cstb", [128, off["_n"]], F32)
    C = Cst(P, cstb, off)
    P.dma("sync", out=cstb.full(), in_=cst_d.full())
    matf = P.sb("matf", [128, 192], F32)
    matb = P.sb("matb", [128, 192], BF16)
    P.dma("sync", out=matf.full(), in_=mats.full())
    P.dve.tensor_copy(out=matb.full(), in_=matf.full())
    prh = matb[0:96, 0:96]
    sel = matb[0:32, 96:192]
    ones = P.sb("ones", [128, 128], F32)
    P.dve.memset(ap=ones.full(), constant=1.0)
    eps = P.sb("eps", [128, 1], F32)
    P.dve.memset(ap=eps.full(), constant=1e-6)
    one1 = P.sb("one1", [128, 1], F32)
    P.dve.memset(ap=one1.full(), constant=1.0)
    nbf = P.sb("nbf", [8, 1], F32)
    P.dve.tensor_scalar(out=nbf.full(), in0=C.col("b_f"), scalar1=-1.0, scalar2=None, op0=ALU.mult)
    Aneg = P.sb("Aneg", [16, 1], F32)
    P.act.activation(out=Aneg.full(), in_=C.col("A_log"), func=AF.Exp)
    P.dve.tensor_scalar(out=Aneg.full(), in0=Aneg.full(), scalar1=-1.0, scalar2=None, op0=ALU.mult)

    pb = [P.ps(f"pb{i}", [128, 512], F32) for i in range(8)]
    pbi = {}

    def nxt_ps(lo=0, hi=4):
        i = pbi.get(lo, 0)
        pbi[lo] = (i + 1) % (hi - lo)
        return pb[lo + i]

    Ctab = P.sb("Ctab", [96, T], F32)
    Stab = P.sb("Stab", [96, T], F32)
    posi = P.sb("posi", [96, 512], I32)
    posf = P.sb("posf", [96, 512], F32)
    rr_tmp = P.sb("rr_tmp", [96, 512], F32)
    rr_i = P.sb("rr_i", [96, 512], I32)
    rr_m = P.sb("rr_m", [96, 512], F32)

    def sin_table(outv, phase):
        P.dve.tensor_scalar(out=rr_tmp.full(), in0=posf.full(), scalar1=C.col("invf"), scalar2=phase,
                            op0=ALU.mult, op1=ALU.add)
        P.dve.tensor_scalar(out=rr_m.full(), in0=rr_tmp.full(), scalar1=1.0 / (2 * np.pi), scalar2=None, op0=ALU.mult)
        P.dve.tensor_copy(out=rr_i.full(), in_=rr_m.full())
        P.dve.tensor_copy(out=rr_m.full(), in_=rr_i.full())
        P.dve.scalar_tensor_tensor(out=rr_tmp.full(), in0=rr_m.full(), scalar=-2 * np.pi, in1=rr_tmp.full(),
                                   op0=ALU.mult, op1=ALU.add)
        P.dve.tensor_scalar(out=rr_m.full(), in0=rr_tmp.full(), scalar1=np.pi, scalar2=-2 * np.pi, op0=ALU.is_gt, op1=ALU.mult)
        P.dve.tensor_tensor(out=rr_tmp.full(), in0=rr_tmp.full(), in1=rr_m.full(), op=ALU.add)
        P.dve.tensor_scalar(out=rr_m.full(), in0=rr_tmp.full(), scalar1=-np.pi, scalar2=2 * np.pi, op0=ALU.is_lt, op1=ALU.mult)
        P.dve.tensor_tensor(out=rr_tmp.full(), in0=rr_tmp.full(), in1=rr_m.full(), op=ALU.add)
        P.act.activation(out=outv, in_=rr_tmp.full(), func=AF.Sin)

    for i in range(NT):
        P.dma("sync", out=posi.full(), in_=pos[:, i * 512:(i + 1) * 512].f(lambda a: a.partition_broadcast(96)))
        P.dve.tensor_copy(out=posf.full(), in_=posi.full())
        sin_table(Stab[:, i * 512:(i + 1) * 512], 0.0)
        sin_table(Ctab[:, i * 512:(i + 1) * 512], np.pi / 2)
    P.dve.memset(ap=Stab[0:64, :], constant=0.0)
    P.dve.memset(ap=Ctab[0:64, :], constant=1.0)

    hn = P.sb("hn", [128, 8, TT], BF16)
    xst = P.sb("xst", [128, 8, 512], F32)
    sq = P.sb("sq", [128, 512], F32)
    rstd = P.sb("rstd", [128, 512], F32)
    xTv = xT.full().re("(kc p) n -> p kc n", p=128)

    def rstd_from(ps_view, n_feat, rows, width, rstd_view):
        P.act.activation(out=rstd_view, in_=ps_view, func=AF.Sqrt, bias=eps[0:rows, 0:1], scale=1.0 / n_feat)
        P.dve.reciprocal(out=rstd_view, in_=rstd_view)

    tiles = [(0, HALO)] + [(HALO + i * 512, 512) for i in range(NT)]
    for (c0, w) in tiles:
        P.dma("sync", out=xst[:, :, 0:w], in_=xTv[:, :, c0:c0 + w])
        ps = nxt_ps(4, 6)
        for kc in range(8):
            P.act.activation(out=sq[:, 0:w], in_=xst[:, kc, 0:w], func=AF.Square)
            P.pe.matmul(out=ps[:, 0:w], lhsT=ones.full(), rhs=sq[:, 0:w], start=(kc == 0), stop=(kc == 7))
        rstd_from(ps[:, 0:w], 1024.0, 128, w, rstd[:, 0:w])
        for kc in range(8):
            P.dve.scalar_tensor_tensor(out=hn[:, kc, c0:c0 + w], in0=xst[:, kc, 0:w], scalar=C.col("g_mix", kc),
                                       in1=rstd[:, 0:w], op0=ALU.mult, op1=ALU.mult)

    wst = [P.sb(f"wst{i}", [128, 8, 512], F32) for i in range(2)]
    wbf = [P.sb(f"wbf{i}", [128, 8, 512], BF16) for i in range(2)]
    wcnt = [0]
    w_inv = w_in.full().re("(kc p) n -> p kc n", p=128)

    def load_w(c0, ncols):
        i = wcnt[0] % 2
        wcnt[0] += 1
        q = "sync" if i == 0 else "gpsimd"
        P.dma(q, out=wst[i][:, :, 0:ncols], in_=w_inv[:, :, c0:c0 + ncols])
        P.pool.tensor_copy(out=wbf[i][:, :, 0:ncols], in_=wst[i][:, :, 0:ncols])
        return wbf[i]

    def proj(wb, wc0, m, c0, w, ps_view):
        for kc in range(8):
            P.pe.matmul(out=ps_view, lhsT=wb[:, kc, wc0:wc0 + m], rhs=hn[:, kc, c0:c0 + w],
                        start=(kc == 0), stop=(kc == 7))

    ostg_cnt = [0]
    ostg = [P.sb(f"ostg{i}", [128, 512], BF16) for i in range(4)]

    def next_ostg():
        i = ostg_cnt[0] % 4
        ostg_cnt[0] += 1
        return ostg[i]

    def out_dma(dst_view, src_view):
        q = "sync" if ostg_cnt[0] % 2 else "gpsimd"
        P.dma(q, out=dst_view, in_=src_view)

    hraw = P.sb("hraw", [96, 512], F32)
    hsq = P.sb("hsq", [96, 512], F32)
    hrs = P.sb("hrs", [96, 512], F32)
    hnf = P.sb("hnf", [96, 512], F32)
    hnb = P.sb("hnb", [96, 512], BF16)
    ht1 = P.sb("ht1", [96, 512], F32)
    ht2 = P.sb("ht2", [96, 512], F32)

    def headnorm(ps_view, d, gain_col, rope, tok0, dst_view):
        P.act.activation(out=hsq[0:d, :], in_=ps_view, func=AF.Square)
        P.act.copy(out=hraw[0:d, :], in_=ps_view)
        ps2 = nxt_ps(4, 6)
        P.pe.matmul(out=ps2[0:d, :], lhsT=ones[0:d, 0:d], rhs=hsq[0:d, :], start=True, stop=True)
        rstd_from(ps2[0:d, :], float(d), d, 512, hrs[0:d, :])
        og = next_ostg()
        if not rope:
            P.dve.scalar_tensor_tensor(out=og[0:d, :], in0=hraw[0:d, :], scalar=gain_col, in1=hrs[0:d, :],
                                       op0=ALU.mult, op1=ALU.mult)
        else:
            P.dve.scalar_tensor_tensor(out=hnf[0:d, :], in0=hraw[0:d, :], scalar=gain_col, in1=hrs[0:d, :],
                                       op0=ALU.mult, op1=ALU.mult)
            P.act.copy(out=hnb[0:d, :], in_=hnf[0:d, :])
            ps3 = nxt_ps(6, 8)
            P.pe.matmul(out=ps3[0:d, :], lhsT=prh, rhs=hnb[0:d, :], start=True, stop=True)
            P.dve.tensor_tensor(out=ht1[0:d, :], in0=hnf[0:d, :], in1=Ctab[0:d, tok0:tok0 + 512], op=ALU.mult)
            P.dve.tensor_tensor(out=ht2[0:d, :], in0=ps3[0:d, :], in1=Stab[0:d, tok0:tok0 + 512], op=ALU.mult)
            P.pool.tensor_tensor(out=og[0:d, :], in0=ht1[0:d, :], in1=ht2[0:d, :], op=ALU.add)
        out_dma(dst_view, og[0:d, :])

    lat = P.sb("lat", [128, 3, 512], F32)
    latn = P.sb("latn", [128, 3, 512], BF16)

    def latent_norm(ps_list, gname):
        nch = len(ps_list)
        ps2 = nxt_ps(4, 6)
        for i, psv in enumerate(ps_list):
            P.act.activation(out=sq.full(), in_=psv, func=AF.Square)
            P.act.copy(out=lat[:, i, :], in_=psv)
            P.pe.matmul(out=ps2.full(), lhsT=ones.full(), rhs=sq.full(), start=(i == 0), stop=(i == nch - 1))
        rstd_from(ps2.full(), 128.0 * nch, 128, 512, rstd.full())
        for i in range(nch):
            P.dve.scalar_tensor_tensor(out=latn[:, i, :], in0=lat[:, i, :], scalar=C.col(gname, i), in1=rstd.full(),
                                       op0=ALU.mult, op1=ALU.mult)

    def small_w(name, dram, kc_n, ncols, i):
        stg = wst[i].full().re("p a b -> p (a b)")[:, 0:kc_n * ncols].re("p (a b) -> p a b", a=kc_n)
        bfb = P.sb(name, [128, kc_n, ncols], BF16)
        P.dma("gpsimd", out=stg, in_=dram.full().re("(kc p) n -> p kc n", p=128))
        P.pool.tensor_copy(out=bfb.full(), in_=stg)
        return bfb

    uqb = small_w("uqb", w_uq, 3, 768, 0)
    kpb = small_w("kpb", w_kp, 2, 768, 1)
    wvb = small_w("wvb", w_v, 2, 512, 0)
    main = tiles[1:]
    wb = load_w(0, 384)
    for ti, (c0, w) in enumerate(main):
        pss = []
        for ch in range(3):
            ps = nxt_ps(0, 4)
            proj(wb, ch * 128, 128, c0, 512, ps.full())
            pss.append(ps.full())
        latent_norm(pss, "g_cq")
        for h in range(8):
            ps = nxt_ps(0, 4)
            for kc in range(3):
                P.pe.matmul(out=ps[0:96, :], lhsT=uqb[:, kc, h * 96:(h + 1) * 96], rhs=latn[:, kc, :],
                            start=(kc == 0), stop=(kc == 2))
            headnorm(ps[0:96, :], 96, C.col("g_q"), True, ti * 512, o_qm[h, :, ti * 512:(ti + 1) * 512])
    wb = load_w(384, 288)
    krb = P.sb("krb", [32, 512], BF16)
    for ti, (c0, w) in enumerate(main):
        pss = []
        for ch in range(2):
            ps = nxt_ps(0, 4)
            proj(wb, ch * 128, 128, c0, 512, ps.full())
            pss.append(ps.full())
        ps = nxt_ps(0, 4)
        proj(wb, 256, 32, c0, 512, ps[0:32, :])
        P.act.copy(out=krb.full(), in_=ps[0:32, :])
        latent_norm(pss, "g_ckv")
        for h in range(8):
            ps = nxt_ps(0, 4)
            for kc in range(2):
                P.pe.matmul(out=ps[0:96, :], lhsT=kpb[:, kc, h * 96:(h + 1) * 96], rhs=latn[:, kc, :],
                            start=(kc == 0), stop=False)
            P.pe.matmul(out=ps[0:96, :], lhsT=sel, rhs=krb.full(), start=False, stop=True)
            headnorm(ps[0:96, :], 96, C.col("g_k"), True, ti * 512, o_km[h, :, ti * 512:(ti + 1) * 512])
        for ch in range(4):
            ps = nxt_ps(0, 4)
            for kc in range(2):
                P.pe.matmul(out=ps.full(), lhsT=wvb[:, kc, ch * 128:(ch + 1) * 128], rhs=latn[:, kc, :],
                            start=(kc == 0), stop=(kc == 1))
            og = next_ostg()
            P.act.copy(out=og.full(), in_=ps.full())
            out_dma(o_vm[ch * 128:(ch + 1) * 128, ti * 512:(ti + 1) * 512], og.full())
    for (base, gname, dst) in ((672, "g_fq", o_qf), (672 + 512, "g_fk", o_kf)):
        wb = load_w(base, 512)
        for ti, (c0, w) in enumerate(main):
            for h in range(8):
                ps = nxt_ps(0, 4)
                proj(wb, h * 64, 64, c0, 512, ps[0:64, :])
                headnorm(ps[0:64, :], 64, C.col(gname), False, ti * 512, dst[h, :, ti * 512:(ti + 1) * 512])
    def plain_group(base, ncols, func, bias_name, dst, dst_row0):
        wb = load_w(base, ncols)
        for ti, (c0, w) in enumerate(main):
            for ch in range(ncols // 128):
                ps = nxt_ps(0, 4)
                proj(wb, ch * 128, 128, c0, 512, ps.full())
                og = next_ostg()
                if bias_name is None:
                    P.act.activation(out=og.full(), in_=ps.full(), func=func)
                else:
                    P.act.activation(out=og.full(), in_=ps.full(), func=func,
                                     bias=C.col(bias_name, (dst_row0 // 128) + ch))
                out_dma(dst[dst_row0 + ch * 128:dst_row0 + (ch + 1) * 128, ti * 512:(ti + 1) * 512], og.full())

    plain_group(672 + 1024, 512, AF.Copy, None, o_vf, 0)
    FB = 672 + 1536
    SB = 672 + 1544
    wf = load_w(FB, 8)
    lf1 = P.sb("lf1", [16, 512], F32)
    lf2 = P.sb("lf2", [16, 512], F32)
    for ti, (c0, w) in enumerate(main):
        ps = nxt_ps(0, 4)
        proj(wf, 0, 8, c0, 512, ps[0:8, :])
        P.act.activation(out=lf1[0:8, :], in_=ps[0:8, :], func=AF.Exp, bias=nbf[0:8, 0:1], scale=-1.0)
        P.act.activation(out=lf1[0:8, :], in_=lf1[0:8, :], func=AF.Ln, bias=one1[0:8, 0:1], scale=1.0)
        P.dve.tensor_scalar(out=lf2[0:8, :], in0=lf1[0:8, :], scalar1=-1.0, scalar2=None, op0=ALU.mult)
        P.dma("sync", out=o_lf[:, ti * 512:(ti + 1) * 512], in_=lf2[0:8, :])
    wd = load_w(SB + 1024 + 1536, 16)
    dt1 = P.sb("dt1", [16, 512], F32)
    dt2 = P.sb("dt2", [16, 512], F32)
    for ti, (c0, w) in enumerate(main):
        ps = nxt_ps(0, 4)
        proj(wd, 0, 16, c0, 512, ps[0:16, :])
        P.act.activation(out=dt1.full(), in_=ps[0:16, :], func=AF.Exp, bias=C.col("dt_b"), scale=1.0)
        P.act.activation(out=dt1.full(), in_=dt1.full(), func=AF.Ln, bias=one1[0:16, 0:1], scale=1.0)
        P.dma("sync", out=o_dt[:, ti * 512:(ti + 1) * 512], in_=dt1.full())
        P.dve.tensor_scalar(out=dt2.full(), in0=dt1.full(), scalar1=Aneg[:, 0:1], scalar2=None, op0=ALU.mult)
        P.dma("sync", out=o_a[:, ti * 512:(ti + 1) * 512], in_=dt2.full())
    for blk in range(2):
        plain_group(SB + blk * 512, 512, AF.Silu, None, o_sz, blk * 512)
    upre = P.sb("upre", [128, 516], F32)
    carry = P.sb("carry", [128, 12, 4], F32)
    acc = [P.sb(f"acc{i}", [128, 512], F32) for i in range(2)]
    for blk in range(3):
        wb = load_w(SB + 1024 + blk * 512, 512)
        for ch in range(4):
            cg = blk * 4 + ch
            ps = nxt_ps(0, 4)
            proj(wb, ch * 128, 128, 0, HALO, ps[:, 0:HALO])
            P.act.copy(out=carry[:, cg, :], in_=ps[:, 0:HALO])
        for ti, (c0, w) in enumerate(main):
            for ch in range(4):
                cg = blk * 4 + ch
                ps = nxt_ps(0, 4)
                proj(wb, ch * 128, 128, c0, 512, ps.full())
                P.act.copy(out=upre[:, 4:516], in_=ps.full())
                P.dve.tensor_copy(out=upre[:, 0:4], in_=carry[:, cg, :])
                P.pool.tensor_copy(out=carry[:, cg, :], in_=upre[:, 512:516])
                a0 = acc[0]
                P.dve.tensor_scalar(out=a0.full(), in0=upre[:, 4:516], scalar1=C.col("cw3", cg), scalar2=C.col("cb", cg),
                                    op0=ALU.mult, op1=ALU.add)
                for k in range(3):
                    P.dve.scalar_tensor_tensor(out=a0.full(), in0=upre[:, 1 + k:513 + k], scalar=C.col(f"cw{k}", cg),
                                               in1=a0.full(), op0=ALU.mult, op1=ALU.add)
                og = next_ostg()
                P.act.activation(out=og.full(), in_=a0.full(), func=AF.Silu)
                out_dma(o_xbc[cg * 128:(cg + 1) * 128, ti * 512:(ti + 1) * 512], og.full())
    GB = SB + 2576
    for blk in range(6):
        plain_group(GB + blk * 512, 512, AF.Sigmoid, "b_gate", o_g, blk * 512)
    P.emit()
    return nc, P


def _bf(a):
    return np.asarray(a).astype(np.float32)


_PROG_CACHE = {}


def _const_mats():
    m = np.zeros((128, 192), np.float32)
    for i in range(16):
        m[80 + i, 64 + i] = -1.0
        m[64 + i, 80 + i] = 1.0
    for i in range(32):
        m[i, 96 + 64 + i] = 1.0
    return m


def run_A(inp, l, x_full, pos_full):
    cp = a_colpack(inp, l)
    off = dict(cp.off)
    off["_n"] = cp.n
    if "A" not in _PROG_CACHE:
        _PROG_CACHE["A"] = build_A(off)[0]
    nc = _PROG_CACHE["A"]
    cst = cp.array()
    wukv = inp["mla_w_ukv"][l].reshape(256, 8, 128)
    w_kp = np.zeros((256, 8, 96), np.float32)
    w_kp[:, :, 0:64] = wukv[:, :, 0:64]
    w_v = np.ascontiguousarray(wukv[:, :, 64:128].reshape(256, 512))
    mats = _const_mats()
    xf = x_full.reshape(16384, D)
    in_maps = []
    for c in range(8):
        t0 = c * T
        xt = np.zeros((D, T + HALO), np.float32)
        xt[:, HALO:] = xf[t0:t0 + T].T
        if c % 4 != 0:
            xt[:, 0:HALO] = xf[t0 - HALO:t0].T
        in_maps.append({
            "xT": np.ascontiguousarray(xt),
            "pos": np.ascontiguousarray(pos_full.reshape(1, 16384)[:, t0:t0 + T]).astype(np.int32),
            "w_in": np.ascontiguousarray(inp["w_in"][l]),
            "w_uq": np.ascontiguousarray(inp["mla_w_uq"][l]),
            "w_kp": np.ascontiguousarray(w_kp.reshape(256, 768)),
            "w_v": w_v, "cst": cst, "mats": mats,
        })
    res = run_bass_kernel_spmd(nc, in_maps, core_ids=list(range(8)))
    return res.results


S_ = 8192
NKT = S_ // 128
NQT = S_ // 512


def build_BC():
    nc = new_nc()
    P = Prog(nc)
    EI, EO = "ExternalInput", "ExternalOutput"
    qm = P.dram("qm", [2, 96, S_], BF16, EI)
    km = P.dram("km", [2, 96, S_], BF16, EI)
    vm = P.dram("vm", [2, 128, NKT, 64], BF16, EI)
    qf = P.dram("qf", [2, 64, S_], BF16, EI)
    kf = P.dram("kf", [2, 64, S_], BF16, EI)
    vf = P.dram("vf", [2, 128, NKT, 64], BF16, EI)
    lf = P.dram("lf", [2, 128, NKT], F32, EI)
    msk = P.dram("msk", [128, 8, 512], F32, EI)
    cm = P.dram("cm", [128, 4, 128], F32, EI)
    x_tm = P.dram("x_tm", [128, NKT, 256], BF16, EI)
    B_tm = P.dram("B_tm", [128, NKT, 128], BF16, EI)
    BT = P.dram("BT", [128, S_], BF16, EI)
    CT = P.dram("CT", [128, S_], BF16, EI)
    dt_tm = P.dram("dt_tm", [128, NKT, 4], F32, EI)
    a_tm = P.dram("a_tm", [128, NKT, 4], F32, EI)
    Dv = P.dram("Dv", [128, 4], F32, EI)
    o_m = P.dram("o_m", [2, 64, S_], BF16, EO)
    o_f = P.dram("o_f", [2, 64, S_], BF16, EO)
    o_y = P.dram("o_y", [128, NKT, 256], F32, EO)
    fsc = P.dram("fsc", [3, S_], BF16)

    cmb = P.sb("cmb", [128, 4, 128], F32)
    P.dma("sync", out=cmb.full(), in_=cm.full())
    tri, trimask, ident, ones = cmb[:, 0, :], cmb[:, 1, :], cmb[:, 2, :], cmb[:, 3, :]
    mskb = P.sb("mskb", [128, 8, 512], F32)
    P.dma("gpsimd", out=mskb.full(), in_=msk.full())
    zero = P.sb("zero", [128, 1], F32)
    P.dve.memset(ap=zero.full(), constant=0.0)

    pb = [P.ps(f"pb{i}", [128, 512], F32) for i in range(8)]
    K_sb = P.sb("K_sb", [128, S_], BF16)
    Q_sb = P.sb("Q_sb", [128, S_], BF16)
    V_sb = P.sb("V_sb", [128, NKT, 128], BF16)
    P.dve.memset(ap=V_sb[:, :, 64:128], constant=1.0)
    pt = [P.sb(f"pt{i}", [128, 512], BF16) for i in range(3)]
    mt = [P.sb(f"mt{i}", [128, 512], F32) for i in range(2)]
    rl = P.sb("rl", [128, 512], F32)
    rl2 = P.sb("rl2", [64, 512], F32)
    ot = [P.sb(f"ot{i}", [64, 512], BF16) for i in range(2)]
    negF = P.sb("negF", [128, NKT], F32)

    cnt = [0, 0, 0]

    def attention(dk, scale, mask0, bias_fn, out_dram_h):
        for qt in range(NQT):
            oacc = pb[3 + qt % 2]
            nk = 4 * qt + 4
            for kt in range(nk):
                i3 = cnt[0] % 3
                cnt[0] += 1
                ps = pb[i3]
                P.pe.matmul(out=ps.full(), lhsT=K_sb[0:dk, kt * 128:(kt + 1) * 128],
                            rhs=Q_sb[0:dk, qt * 512:(qt + 1) * 512], start=True, stop=True)
                if kt >= 4 * qt:
                    m = mt[cnt[1] % 2]
                    cnt[1] += 1
                    P.dve.tensor_tensor(out=m.full(), in0=ps.full(), in1=mskb[:, mask0 + kt - 4 * qt, :], op=ALU.add)
                    src = m.full()
                else:
                    src = ps.full()
                P.act.activation(out=pt[i3].full(), in_=src, func=AF.Exp, scale=scale, bias=bias_fn(kt))
                P.pe.matmul(out=oacc.full(), lhsT=V_sb[:, kt, :], rhs=pt[i3].full(), start=(kt == 0), stop=(kt == nk - 1))
            P.dve.reciprocal(out=rl[64:128, :], in_=oacc[64:128, :])
            P.dve.tensor_copy(out=rl2.full(), in_=rl[64:128, :])
            o = ot[qt % 2]
            P.dve.tensor_tensor(out=o.full(), in0=oacc[0:64, :], in1=rl2.full(), op=ALU.mult)
            P.dma("sync", out=out_dram_h[:, qt * 512:(qt + 1) * 512], in_=o.full())

    for h in range(2):
        P.dma("sync", out=K_sb[0:96, :], in_=km[h])
        P.dma("gpsimd", out=Q_sb[0:96, :], in_=qm[h])
        P.dma("sync", out=V_sb[:, :, 0:64], in_=vm[h])
        attention(96, 96.0 ** -0.5, 0, lambda kt: zero[:, 0:1], o_m[h])

    lfs = P.sb("lfs", [128, NKT], F32)
    wi = P.sb("wi", [128, NKT], F32)
    sc = [P.sb(f"sc{i}", [128, NKT], F32) for i in range(2)]
    Ff = P.sb("Ff", [128, NKT], F32)
    FT = P.sb("FT", [64, 128], F32)
    r1 = P.sb("r1", [64, 128], F32)
    fh = [P.sb(f"fh{i}", [64, 128], BF16) for i in range(3)]
    for h in range(2):
        P.dma("sync", out=lfs.full(), in_=lf[h])
        ps = pb[5]
        P.pe.matmul(out=ps[:, 0:NKT], lhsT=tri, rhs=lfs.full(), start=True, stop=True)
        P.act.copy(out=wi.full(), in_=ps[:, 0:NKT])
        ps = pb[6]
        P.pe.matmul(out=ps[:, 0:NKT], lhsT=ones, rhs=lfs.full(), start=True, stop=True)
        P.act.copy(out=sc[0].full(), in_=ps[:, 0:NKT])
        P.dve.tensor_tensor(out=wi.full(), in0=wi.full(), in1=sc[0].full(), op=ALU.subtract)
        cur = 0
        d = 1
        while d < NKT:
            nx = 1 - cur
            P.dve.tensor_copy(out=sc[nx][:, 0:d], in_=sc[cur][:, 0:d])
            P.dve.tensor_tensor(out=sc[nx][:, d:NKT], in0=sc[cur][:, d:NKT], in1=sc[cur][:, 0:NKT - d], op=ALU.add)
            cur = nx
            d *= 2
        P.dve.tensor_tensor(out=Ff.full(), in0=wi.full(), in1=sc[cur].full(), op=ALU.add)
        P.dve.tensor_scalar(out=negF.full(), in0=Ff.full(), scalar1=-1.0, scalar2=None, op0=ALU.mult)
        ps = pb[7]
        P.pe.transpose(out=ps[0:64, 0:128], in_=Ff.full(), identity=ident)
        P.act.copy(out=FT.full(), in_=ps[0:64, 0:128])
        P.dve.tensor_copy(out=fh[0].full(), in_=FT.full())
        P.dve.tensor_tensor(out=r1.full(), in0=FT.full(), in1=fh[0].full(), op=ALU.subtract)
        P.dve.tensor_copy(out=fh[1].full(), in_=r1.full())
        P.dve.tensor_tensor(out=r1.full(), in0=r1.full(), in1=fh[1].full(), op=ALU.subtract)
        P.dve.tensor_copy(out=fh[2].full(), in_=r1.full())
        for r in range(3):
            P.dma("sync", out=fsc[r].re("(kt p) -> kt p", p=128), in_=fh[r].full())
        P.dma("sync", out=K_sb[0:64, :], in_=kf[h])
        P.dve.memset(ap=K_sb[64:67, :], constant=8.0)
        P.dma("gpsimd", out=Q_sb[0:64, :], in_=qf[h])
        P.dma("gpsimd", out=Q_sb[64:67, :], in_=fsc.full())
        P.dma("sync", out=V_sb[:, :, 0:64], in_=vf[h])
        attention(67, 0.125, 4, lambda kt: negF[:, kt:kt + 1], o_f[h])

    a_sb = P.sb("a_sb", [128, NKT, 4], F32)
    dt_sb = P.sb("dt_sb", [128, NKT, 4], F32)
    Dsb = P.sb("Dsb", [128, 4], F32)
    P.dma("sync", out=a_sb.full(), in_=a_tm.full())
    P.dma("sync", out=dt_sb.full(), in_=dt_tm.full())
    P.dma("sync", out=Dsb.full(), in_=Dv.full())
    BTs = K_sb
    CTs = Q_sb
    P.dma("sync", out=BTs.full(), in_=BT.full())
    P.dma("gpsimd", out=CTs.full(), in_=CT.full())
    Acum = P.sb("Acum", [128, NKT, 4], F32)
    nAcum = P.sb("nAcum", [128, NKT, 4], F32)
    Atot = P.sb("Atot", [128, NKT, 4], F32)
    eA = P.sb("eA", [128, NKT, 4], F32)
    wdec = P.sb("wdec", [128, NKT, 4], F32)
    eAtot = P.sb("eAtot", [128, NKT, 4], F32)
    fl = lambda b: b.full().re("p c h -> p (c h)")
    ps = pb[0]
    P.pe.matmul(out=ps[:, 0:256], lhsT=tri, rhs=fl(a_sb), start=True, stop=True)
    P.act.copy(out=fl(Acum), in_=ps[:, 0:256])
    ps = pb[1]
    P.pe.matmul(out=ps[:, 0:256], lhsT=ones, rhs=fl(a_sb), start=True, stop=True)
    P.act.copy(out=fl(Atot), in_=ps[:, 0:256])
    P.dve.tensor_scalar(out=fl(nAcum), in0=fl(Acum), scalar1=-1.0, scalar2=None, op0=ALU.mult)
    P.act.activation(out=fl(eA), in_=fl(Acum), func=AF.Exp)
    P.act.activation(out=fl(eAtot), in_=fl(Atot), func=AF.Exp)
    P.dve.tensor_tensor(out=fl(wdec), in0=fl(Atot), in1=fl(Acum), op=ALU.subtract)
    P.act.activation(out=fl(wdec), in_=fl(wdec), func=AF.Exp)

    Hs = P.sb("Hs", [128, 256], F32)
    Hb = P.sb("Hb", [128, 256], BF16)
    P.dve.memset(ap=Hs.full(), constant=0.0)
    P.dve.memset(ap=Hb.full(), constant=0.0)
    xc = [P.sb(f"xc{i}", [128, 256], BF16) for i in range(2)]
    Bc = [P.sb(f"Bc{i}", [128, 128], BF16) for i in range(2)]
    cb = P.sb("cb", [128, 128], F32)
    xdt = P.sb("xdt", [128, 256], BF16)
    xdts = P.sb("xdts", [128, 256], BF16)
    at = [P.sb(f"at{i}", [128, 128], F32) for i in range(2)]
    tm = [P.sb(f"tm{i}", [128, 128], F32) for i in range(2)]
    dec = [P.sb(f"dec{i}", [128, 128], F32) for i in range(2)]
    MT = [P.sb(f"MT{i}", [128, 128], BF16) for i in range(2)]
    t1 = P.sb("t1", [128, 256], F32)
    t3 = P.sb("t3", [128, 256], F32)
    yo = [P.sb(f"yo{i}", [128, 256], F32) for i in range(2)]
    v3 = lambda v: v.re("p (h d) -> p h d", h=4)
    bc3 = lambda v: v.f(lambda a: a.unsqueeze(2).to_broadcast([128, 4, 64]))
    for c in range(NKT):
        x_c = xc[c % 2]
        B_c = Bc[c % 2]
        P.dma("sync", out=x_c.full(), in_=x_tm[:, c, :])
        P.dma("gpsimd", out=B_c.full(), in_=B_tm[:, c, :])
        BT_c = BTs[:, c * 128:(c + 1) * 128]
        CT_c = CTs[:, c * 128:(c + 1) * 128]
        ps_cb = pb[0]
        P.pe.matmul(out=ps_cb[:, 0:128], lhsT=BT_c, rhs=CT_c, start=True, stop=True)
        P.act.copy(out=cb.full(), in_=ps_cb[:, 0:128])
        P.dve.tensor_tensor(out=v3(xdt.full()), in0=v3(x_c.full()), in1=bc3(dt_sb[:, c, :]), op=ALU.mult)
        P.pool.tensor_tensor(out=v3(xdts.full()), in0=v3(xdt.full()), in1=bc3(wdec[:, c, :]), op=ALU.mult)
        ps_off = pb[1]
        P.pe.matmul(out=ps_off[:, 0:256], lhsT=CT_c, rhs=Hb.full(), start=True, stop=True)
        ps_y = pb[2]
        for h in range(4):
            i2 = h % 2
            P.dve.tensor_scalar(out=at[i2].full(), in0=tri, scalar1=a_sb[:, c, h:h + 1], scalar2=None, op0=ALU.mult)
            ps_A = pb[3 + i2]
            P.pe.matmul(out=ps_A[:, 0:128], lhsT=ones, rhs=at[i2].full(), start=True, stop=True)
            P.dve.tensor_tensor(out=tm[i2].full(), in0=ps_A[:, 0:128], in1=trimask, op=ALU.add)
            P.act.activation(out=dec[i2].full(), in_=tm[i2].full(), func=AF.Exp, bias=nAcum[:, c, h:h + 1], scale=1.0)
            P.pool.tensor_tensor(out=MT[i2].full(), in0=cb.full(), in1=dec[i2].full(), op=ALU.mult)
            P.pe.matmul(out=ps_y[:, h * 64:(h + 1) * 64], lhsT=MT[i2].full(), rhs=xdt[:, h * 64:(h + 1) * 64],
                        start=True, stop=True)
        P.dve.tensor_tensor(out=v3(t1.full()), in0=v3(ps_off[:, 0:256]), in1=bc3(eA[:, c, :]), op=ALU.mult)
        P.dve.tensor_tensor(out=t1.full(), in0=t1.full(), in1=ps_y[:, 0:256], op=ALU.add)
        P.pool.tensor_tensor(out=v3(t3.full()), in0=v3(x_c.full()), in1=bc3(Dsb.full()), op=ALU.mult)
        y_ = yo[c % 2]
        P.pool.tensor_tensor(out=y_.full(), in0=t1.full(), in1=t3.full(), op=ALU.add)
        P.dma("sync", out=o_y[:, c, :], in_=y_.full())
        ps_h = pb[5]
        P.pe.matmul(out=ps_h[:, 0:256], lhsT=B_c.full(), rhs=xdts.full(), start=True, stop=True)
        P.dve.tensor_tensor(out=v3(Hs.full()), in0=v3(Hs.full()), in1=bc3(eAtot[:, c, :]), op=ALU.mult)
        P.dve.tensor_tensor(out=Hs.full(), in0=Hs.full(), in1=ps_h[:, 0:256], op=ALU.add)
        P.act.copy(out=Hb.full(), in_=Hs.full())
    P.emit()
    return nc, P


def _bc_consts():
    msk = np.zeros((128, 8, 512), np.float32)
    p = np.arange(128)[:, None]
    q = np.arange(512)[None, :]
    for j in range(4):
        key = j * 128 + p
        msk[:, j, :] = np.where((key // 64) > (q // 64), NEG, 0.0)
        msk[:, 4 + j, :] = np.where(key > q, NEG, 0.0)
    cm = np.zeros((128, 4, 128), np.float32)
    jj = np.arange(128)[:, None]
    ii = np.arange(128)[None, :]
    cm[:, 0, :] = (jj <= ii).astype(np.float32)
    cm[:, 1, :] = np.where(jj > ii, NEG, 0.0)
    cm[:, 2, :] = np.eye(128, dtype=np.float32)
    cm[:, 3, :] = 1.0
    return msk, cm


def _tm(a):
    S, n = a.shape
    return np.ascontiguousarray(a.reshape(S // 128, 128, n).transpose(1, 0, 2))


def run_BC(inp, l, resA):
    if "BC" not in _PROG_CACHE:
        _PROG_CACHE["BC"] = build_BC()[0]
    nc = _PROG_CACHE["BC"]
    msk, cm = _bc_consts()

    def gather(name, b):
        return np.concatenate([np.asarray(resA[b * 4 + i][name]) for i in range(4)], axis=-1)

    in_maps = []
    for c in range(8):
        b, hg = c // 4, c % 4
        qm = gather("o_qm", b)[2 * hg:2 * hg + 2]
        km = gather("o_km", b)[2 * hg:2 * hg + 2]
        vmf = gather("o_vm", b)
        qf = gather("o_qf", b)[2 * hg:2 * hg + 2]
        kf = gather("o_kf", b)[2 * hg:2 * hg + 2]
        vff = gather("o_vf", b)
        lff = gather("o_lf", b)
        xbc = gather("o_xbc", b)
        dtf = gather("o_dt", b)
        af = gather("o_a", b)
        g = hg // 2
        vm = np.stack([_tm(vmf[(2 * hg + h) * 64:(2 * hg + h + 1) * 64].T) for h in range(2)])
        vf = np.stack([_tm(vff[(2 * hg + h) * 64:(2 * hg + h + 1) * 64].T) for h in range(2)])
        lf = np.stack([np.ascontiguousarray(lff[2 * hg + h].reshape(NKT, 128).T) for h in range(2)])
        x_tm = _tm(xbc[hg * 256:(hg + 1) * 256].T)
        Bf = xbc[1024 + g * 128:1024 + (g + 1) * 128]
        Cf = xbc[1280 + g * 128:1280 + (g + 1) * 128]
        Dv = np.broadcast_to(inp["ssm_D"][l][4 * hg:4 * hg + 4][None, :], (128, 4)).astype(np.float32)
        in_maps.append({
            "qm": np.ascontiguousarray(qm), "km": np.ascontiguousarray(km), "vm": vm,
            "qf": np.ascontiguousarray(qf), "kf": np.ascontiguousarray(kf), "vf": vf, "lf": lf,
            "msk": msk, "cm": cm, "x_tm": x_tm, "B_tm": _tm(Bf.T), "BT": np.ascontiguousarray(Bf),
            "CT": np.ascontiguousarray(Cf), "dt_tm": _tm(dtf[4 * hg:4 * hg + 4].T),
            "a_tm": _tm(af[4 * hg:4 * hg + 4].T), "Dv": np.ascontiguousarray(Dv),
        })
    res = run_bass_kernel_spmd(nc, in_maps, core_ids=list(range(8))).results
    om = np.zeros((2, 512, S_), NBF)
    of = np.zeros((2, 512, S_), NBF)
    y = np.zeros((2, 1024, S_), np.float32)
    for c in range(8):
        b, hg = c // 4, c % 4
        om[b, hg * 128:(hg + 1) * 128] = np.asarray(res[c]["o_m"]).reshape(128, S_)
        of[b, hg * 128:(hg + 1) * 128] = np.asarray(res[c]["o_f"]).reshape(128, S_)
        yy = np.asarray(res[c]["o_y"])
        y[b, hg * 256:(hg + 1) * 256] = yy.transpose(2, 1, 0).reshape(256, S_)
    return om, of, y


def build_D1():
    nc = new_nc()
    P = Prog(nc)
    EI, EO = "ExternalInput", "ExternalOutput"
    NT = T // 512
    omT = P.dram("omT", [512, T], BF16, EI)
    ofT = P.dram("ofT", [512, T], BF16, EI)
    yT = P.dram("yT", [1024, T], F32, EI)
    szT = P.dram("szT", [1024, T], BF16, EI)
    gT = P.dram("gT", [3072, T], BF16, EI)
    xT = P.dram("xT", [D, T], F32, EI)
    w_a = P.dram("w_a", [512, D], F32, EI)
    w_b = P.dram("w_b", [512, D], F32, EI)
    w_c = P.dram("w_c", [1024, D], F32, EI)
    w_o = P.dram("w_o", [1024, D], F32, EI)
    cst_d = P.dram("cst", [128, 8], F32, EI)
    o_x = P.dram("o_x", [D, T], F32, EO)

    cstb = P.sb("cstb", [128, 8], F32)
    P.dma("sync", out=cstb.full(), in_=cst_d.full())
    ones = P.sb("ones", [128, 128], F32)
    P.dve.memset(ap=ones.full(), constant=1.0)
    eps = P.sb("eps", [128, 1], F32)
    P.dve.memset(ap=eps.full(), constant=1e-6)
    pb = [P.ps(f"pb{i}", [128, 512], F32) for i in range(8)]
    wst = [P.sb(f"wst{i}", [128, 4, 1024], F32) for i in range(2)]
    wcnt = [0]

    def load_w(dram, kc_n, name):
        bfb = P.sb(name, [128, kc_n, D], BF16)
        v = dram.full().re("(kc p) n -> p kc n", p=128)
        for k0 in range(0, kc_n, 4):
            i = wcnt[0] % 2
            wcnt[0] += 1
            P.dma("sync" if i == 0 else "gpsimd", out=wst[i].full(), in_=v[:, k0:k0 + 4, :])
            P.pool.tensor_copy(out=bfb[:, k0:k0 + 4, :], in_=wst[i].full())
        return bfb

    Wa = load_w(w_a, 4, "Wa")
    Wb = load_w(w_b, 4, "Wb")
    Wc = load_w(w_c, 8, "Wc")
    Wo = load_w(w_o, 8, "Wo")

    ys = P.sb("ys", [128, 8, 512], F32)
    szs = P.sb("szs", [128, 8, 512], BF16)
    yn = P.sb("yn", [128, 8, 512], BF16)
    oms = P.sb("oms", [128, 4, 512], BF16)
    ofs = P.sb("ofs", [128, 4, 512], BF16)
    gs = P.sb("gs", [128, 24, 512], BF16)
    xs = P.sb("xs", [128, 8, 512], F32)
    sq = P.sb("sq", [128, 512], F32)
    rstd = P.sb("rstd", [128, 512], F32)
    m1 = [P.sb(f"m1_{i}", [128, 512], F32) for i in range(2)]
    m2 = [P.sb(f"m2_{i}", [128, 512], F32) for i in range(2)]
    m3 = [P.sb(f"m3_{i}", [128, 512], F32) for i in range(2)]
    mg = P.sb("mg", [128, 8, 512], BF16)
    xo = [P.sb(f"xo{i}", [128, 512], F32) for i in range(2)]
    ch = lambda d: d.full().re("(kc p) n -> p kc n", p=128)
    for ti in range(NT):
        ts = slice(ti * 512, (ti + 1) * 512)
        P.dma("sync", out=ys.full(), in_=ch(yT)[:, :, ts])
        P.dma("gpsimd", out=szs.full(), in_=ch(szT)[:, :, ts])
        P.dma("sync", out=oms.full(), in_=ch(omT)[:, :, ts])
        P.dma("gpsimd", out=ofs.full(), in_=ch(ofT)[:, :, ts])
        P.dma("sync", out=gs.full(), in_=ch(gT)[:, :, ts])
        P.dma("gpsimd", out=xs.full(), in_=ch(xT)[:, :, ts])
        ps = pb[7]
        for kc in range(8):
            P.dve.tensor_tensor(out=ys[:, kc, :], in0=ys[:, kc, :], in1=szs[:, kc, :], op=ALU.mult)
            P.act.activation(out=sq.full(), in_=ys[:, kc, :], func=AF.Square)
            P.pe.matmul(out=ps.full(), lhsT=ones.full(), rhs=sq.full(), start=(kc == 0), stop=(kc == 7))
        P.act.activation(out=rstd.full(), in_=ps.full(), func=AF.Sqrt, bias=eps[:, 0:1], scale=1.0 / 1024.0)
        P.dve.reciprocal(out=rstd.full(), in_=rstd.full())
        for kc in range(8):
            P.dve.scalar_tensor_tensor(out=yn[:, kc, :], in0=ys[:, kc, :], scalar=cstb[:, kc:kc + 1], in1=rstd.full(),
                                       op0=ALU.mult, op1=ALU.mult)
        for oc in range(8):
            i2 = oc % 2
            osl = slice(oc * 128, (oc + 1) * 128)
            pa, pbb, pc = pb[0 + i2 * 3], pb[1 + i2 * 3], pb[2 + i2 * 3]
            for kc in range(4):
                P.pe.matmul(out=pa.full(), lhsT=Wa[:, kc, osl], rhs=oms[:, kc, :], start=(kc == 0), stop=(kc == 3))
            for kc in range(4):
                P.pe.matmul(out=pbb.full(), lhsT=Wb[:, kc, osl], rhs=ofs[:, kc, :], start=(kc == 0), stop=(kc == 3))
            for kc in range(8):
                P.pe.matmul(out=pc.full(), lhsT=Wc[:, kc, osl], rhs=yn[:, kc, :], start=(kc == 0), stop=(kc == 7))
            P.dve.tensor_tensor(out=m1[i2].full(), in0=pa.full(), in1=gs[:, oc, :], op=ALU.mult)
            P.dve.tensor_tensor(out=m2[i2].full(), in0=pbb.full(), in1=gs[:, 8 + oc, :], op=ALU.mult)
            P.dve.tensor_tensor(out=m3[i2].full(), in0=pc.full(), in1=gs[:, 16 + oc, :], op=ALU.mult)
            P.pool.tensor_tensor(out=m1[i2].full(), in0=m1[i2].full(), in1=m2[i2].full(), op=ALU.add)
            P.pool.tensor_tensor(out=mg[:, oc, :], in0=m1[i2].full(), in1=m3[i2].full(), op=ALU.add)
        for oc in range(8):
            i2 = oc % 2
            ps = pb[6 + i2]
            for kc in range(8):
                P.pe.matmul(out=ps.full(), lhsT=Wo[:, kc, oc * 128:(oc + 1) * 128], rhs=mg[:, kc, :],
                            start=(kc == 0), stop=(kc == 7))
            P.dve.tensor_tensor(out=xo[i2].full(), in0=ps.full(), in1=xs[:, oc, :], op=ALU.add)
            P.dma("sync" if i2 else "gpsimd", out=o_x[oc * 128:(oc + 1) * 128, ts], in_=xo[i2].full())
    P.emit()
    return nc, P


def d2_colpack(inp, l):
    cp = ColPack()
    cp.add("g_ffn", inp["norm_ffn_g"][l])
    cw = inp["ffn_conv_w"][l]
    for k in range(3):
        cp.add(f"fw{k}", cw[k])
    cp.add("fb", inp["ffn_conv_b"][l])
    return cp


def build_D2(off):
    nc = new_nc()
    P = Prog(nc)
    EI, EO = "ExternalInput", "ExternalOutput"
    NT = T // 512
    TT = T + HALO
    xT = P.dram("xT", [D, TT], F32, EI)
    w_up = P.dram("w_up", [D, 5632], F32, EI)
    w_dn = P.dram("w_dn", [2816, D], F32, EI)
    cst_d = P.dram("cst", [128, off["_n"]], F32, EI)
    o_x = P.dram("o_x", [D, T], F32, EO)

    cstb = P.sb("cstb", [128, off["_n"]], F32)
    C = Cst(P, cstb, off)
    P.dma("sync", out=cstb.full(), in_=cst_d.full())
    ones = P.sb("ones", [128, 128], F32)
    P.dve.memset(ap=ones.full(), constant=1.0)
    eps = P.sb("eps", [128, 1], F32)
    P.dve.memset(ap=eps.full(), constant=1e-6)
    pb = [P.ps(f"pb{i}", [128, 512], F32) for i in range(8)]
    wst = [P.sb(f"wst{i}", [128, 1024], F32) for i in range(2)]
    wcnt = [0]
    Wu = P.sb("Wu", [128, 8, 5632], BF16)
    Wd = P.sb("Wd", [128, 22, D], BF16)
    wuv = w_up.full().re("(kc p) n -> p kc n", p=128)
    for kc in range(8):
        for c0 in range(0, 5632, 1024):
            n = min(1024, 5632 - c0)
            i = wcnt[0] % 2
            wcnt[0] += 1
            P.dma("sync" if i == 0 else "gpsimd", out=wst[i][:, 0:n], in_=wuv[:, kc, c0:c0 + n])
            P.pool.tensor_copy(out=Wu[:, kc, c0:c0 + n], in_=wst[i][:, 0:n])
    wdv = w_dn.full().re("(kc p) n -> p kc n", p=128)
    for k0 in range(22):
        i = wcnt[0] % 2
        wcnt[0] += 1
        P.dma("sync" if i == 0 else "gpsimd", out=wst[i].full(), in_=wdv[:, k0, :])
        P.pool.tensor_copy(out=Wd[:, k0, :], in_=wst[i].full())

    xst = P.sb("xst", [128, 8, 512], F32)
    hn = P.sb("hn", [128, 8, 512], BF16)
    sq = P.sb("sq", [128, 512], F32)
    rstd = P.sb("rstd", [128, 512], F32)
    act = P.sb("act", [128, 22, 512], BF16)
    upre = [P.sb(f"upre{i}", [128, 516], F32) for i in range(2)]
    acc = [P.sb(f"acc{i}", [128, 512], F32) for i in range(2)]
    sg = P.sb("sg", [128, 512], F32)
    carry = P.sb("carry", [128, 44, 4], F32)
    xo = [P.sb(f"xo{i}", [128, 512], F32) for i in range(2)]
    xTv = xT.full().re("(kc p) n -> p kc n", p=128)
    tiles = [(0, HALO)] + [(HALO + i * 512, 512) for i in range(NT)]
    pcnt = [0]
    for tix, (c0, w) in enumerate(tiles):
        P.dma("sync", out=xst[:, :, 0:w], in_=xTv[:, :, c0:c0 + w])
        ps = pb[7]
        for kc in range(8):
            P.act.activation(out=sq[:, 0:w], in_=xst[:, kc, 0:w], func=AF.Square)
            P.pe.matmul(out=ps[:, 0:w], lhsT=ones.full(), rhs=sq[:, 0:w], start=(kc == 0), stop=(kc == 7))
        P.act.activation(out=rstd[:, 0:w], in_=ps[:, 0:w], func=AF.Sqrt, bias=eps[:, 0:1], scale=1.0 / 1024.0)
        P.dve.reciprocal(out=rstd[:, 0:w], in_=rstd[:, 0:w])
        for kc in range(8):
            P.dve.scalar_tensor_tensor(out=hn[:, kc, 0:w], in0=xst[:, kc, 0:w], scalar=C.col("g_ffn", kc),
                                       in1=rstd[:, 0:w], op0=ALU.mult, op1=ALU.mult)
        for i in range(22):
            accs = []
            for j, cg in enumerate((i, 22 + i)):
                ps = pb[pcnt[0] % 4]
                pcnt[0] += 1
                for kc in range(8):
                    P.pe.matmul(out=ps[:, 0:w], lhsT=Wu[:, kc, cg * 128:(cg + 1) * 128], rhs=hn[:, kc, 0:w],
                                start=(kc == 0), stop=(kc == 7))
                if tix == 0:
                    P.act.copy(out=carry[:, cg, :], in_=ps[:, 0:HALO])
                    continue
                up = upre[j]
                P.act.copy(out=up[:, 4:516], in_=ps.full())
                P.dve.tensor_copy(out=up[:, 0:4], in_=carry[:, cg, :])
                P.pool.tensor_copy(out=carry[:, cg, :], in_=up[:, 512:516])
                a0 = acc[j]
                P.dve.tensor_scalar(out=a0.full(), in0=up[:, 4:516], scalar1=C.col("fw2", cg), scalar2=C.col("fb", cg),
                                    op0=ALU.mult, op1=ALU.add)
                P.dve.scalar_tensor_tensor(out=a0.full(), in0=up[:, 3:515], scalar=C.col("fw1", cg), in1=a0.full(),
                                           op0=ALU.mult, op1=ALU.add)
                P.dve.scalar_tensor_tensor(out=a0.full(), in0=up[:, 2:514], scalar=C.col("fw0", cg), in1=a0.full(),
                                           op0=ALU.mult, op1=ALU.add)
                accs.append(a0)
            if tix == 0:
                continue
            P.act.activation(out=sg.full(), in_=accs[0].full(), func=AF.Silu)
            P.pool.tensor_tensor(out=act[:, i, :], in0=sg.full(), in1=accs[1].full(), op=ALU.mult)
        if tix == 0:
            continue
        ti = tix - 1
        for oc in range(8):
            i2 = oc % 2
            ps = pb[4 + i2]
            for i in range(22):
                P.pe.matmul(out=ps.full(), lhsT=Wd[:, i, oc * 128:(oc + 1) * 128], rhs=act[:, i, :],
                            start=(i == 0), stop=(i == 21))
            P.dve.tensor_tensor(out=xo[i2].full(), in0=ps.full(), in1=xst[:, oc, :], op=ALU.add)
            P.dma("sync" if i2 else "gpsimd", out=o_x[oc * 128:(oc + 1) * 128, ti * 512:(ti + 1) * 512], in_=xo[i2].full())
    P.emit()
    return nc, P


def run_D1(inp, l, resA, om, of, y, x_full):
    if "D1" not in _PROG_CACHE:
        _PROG_CACHE["D1"] = build_D1()[0]
    nc = _PROG_CACHE["D1"]
    cst = np.ascontiguousarray(inp["ssm_norm_g"][l].reshape(8, 128).T)
    xf = x_full.reshape(16384, D)
    in_maps = []
    for c in range(8):
        b, q = c // 4, c % 4
        ts = slice(q * T, (q + 1) * T)
        in_maps.append({
            "omT": np.ascontiguousarray(om[b][:, ts]), "ofT": np.ascontiguousarray(of[b][:, ts]),
            "yT": np.ascontiguousarray(y[b][:, ts]), "szT": np.asarray(resA[c]["o_sz"]),
            "gT": np.asarray(resA[c]["o_g"]), "xT": np.ascontiguousarray(xf[c * T:(c + 1) * T].T),
            "w_a": np.ascontiguousarray(inp["w_br_mla"][l]), "w_b": np.ascontiguousarray(inp["w_br_fox"][l]),
            "w_c": np.ascontiguousarray(inp["w_br_ssm"][l]), "w_o": np.ascontiguousarray(inp["w_out"][l]),
            "cst": cst,
        })
    res = run_bass_kernel_spmd(nc, in_maps, core_ids=list(range(8))).results
    xm = np.concatenate([np.asarray(r["o_x"]).T for r in res], axis=0)
    return xm.reshape(2, S_, D)


def run_D2(inp, l, xm_full):
    cp = d2_colpack(inp, l)
    off = dict(cp.off)
    off["_n"] = cp.n
    if "D2" not in _PROG_CACHE:
        _PROG_CACHE["D2"] = build_D2(off)[0]
    nc = _PROG_CACHE["D2"]
    cst = cp.array()
    xf = xm_full.reshape(16384, D)
    in_maps = []
    for c in range(8):
        t0 = c * T
        xt = np.zeros((D, T + HALO), np.float32)
        xt[:, HALO:] = xf[t0:t0 + T].T
        if c % 4 != 0:
            xt[:, 0:HALO] = xf[t0 - HALO:t0].T
        in_maps.append({"xT": np.ascontiguousarray(xt), "w_up": np.ascontiguousarray(inp["ffn_w_up"][l]),
                        "w_dn": np.ascontiguousarray(inp["ffn_w_down"][l]), "cst": cst})
    res = run_bass_kernel_spmd(nc, in_maps, core_ids=list(range(8))).results
    xo = np.concatenate([np.asarray(r["o_x"]).T for r in res], axis=0)
    return xo.reshape(2, S_, D)


def kernel_unfused(**inp):
    inp = {k: np.asarray(v) for k, v in inp.items()}
    x = inp["x"].astype(np.float32)
    pos = inp["positions"]
    for l in range(2):
        resA = run_A(inp, l, x, pos)
        om, of, y = run_BC(inp, l, resA)
        xm = run_D1(inp, l, resA, om, of, y, x)
        x = run_D2(inp, l, xm)
    return np.ascontiguousarray(x.astype(np.float32))


def kernel(**inp):
    return kernel_fused(**inp)


SW = 516
RG = [[0, 1, 2, 3], [4, 5, 6, 7]]
KT_L = T // 128


def fused_rowpack(inp, l):
    r = np.concatenate([inp["fox_b_f"][l], inp["ssm_dt_bias"][l], inp["ssm_A_log"][l], inp["ssm_D"][l]]).astype(np.float32)
    return np.ascontiguousarray(np.broadcast_to(r[None, :], (128, r.size)))


def build_fused(offA, offD2, stop=None, dbg=()):
    nc = new_nc()
    P = Prog(nc)
    EI, EO = "ExternalInput", "ExternalOutput"
    L = 2
    x0 = P.dram("x0", [D, 4 * SW], F32, EI)
    pos = P.dram("pos", [1, T], I32, EI)
    w_in = P.dram("w_in", [L, D, 7864], F32, EI)
    w_uq = P.dram("w_uq", [L, 384, 768], F32, EI)
    w_kp = P.dram("w_kp", [L, 256, 768], F32, EI)
    w_v = P.dram("w_v", [L, 256, 512], F32, EI)
    w_a = P.dram("w_a", [L, 512, D], F32, EI)
    w_b = P.dram("w_b", [L, 512, D], F32, EI)
    w_c = P.dram("w_c", [L, 1024, D], F32, EI)
    w_o = P.dram("w_o", [L, 1024, D], F32, EI)
    w_up = P.dram("w_up", [L, D, 5632], F32, EI)
    w_dn = P.dram("w_dn", [L, 2816, D], F32, EI)
    cstA_d = P.dram("cstA", [L, 128, offA["_n"]], F32, EI)
    cstD_d = P.dram("cstD", [L, 128, offD2["_n"]], F32, EI)
    gssm_d = P.dram("gssm", [L, 128, 8], F32, EI)
    rowc_d = P.dram("rowc", [L, 128, 56], F32, EI)
    sel_d = P.dram("sel", [128, 32], F32, EI)
    msk_d = P.dram("msk", [128, 8, 512], F32, EI)
    cm_d = P.dram("cm", [128, 4, 128], F32, EI)
    mats_d = P.dram("mats", [128, 192], F32, EI)
    out = P.dram("out", [D, T], F32, EO)
    xb = [x0, P.dram("xb1", [D, 4 * SW], F32)]
    xmid = P.dram("xmid", [D, 4 * SW], F32)
    qm = P.dram("qm", [8, 96, T], BF16)
    qf = P.dram("qf", [8, 64, T], BF16)
    fq = P.dram("fq", [8, 3, T], BF16)
    szd = P.dram("szd", [1024, T], BF16)
    gd = P.dram("gd", [3072, T], BF16)
    xtm = P.dram("xtm", [128, KT_L, 1024], BF16)
    btm = P.dram("btm", [128, KT_L, 256], BF16)
    bct = P.dram("bct", [512, T], BF16)
    dtd = P.dram("dtd", [128, KT_L, 16], F32)
    atd = P.dram("atd", [128, KT_L, 16], F32)
    omd = P.dram("omd", [512, T], BF16)
    ofd = P.dram("ofd", [512, T], BF16)
    yd = P.dram("yd", [1024, T], F32)
    kxm = [P.dram(f"kxm{m}", [768, 512], BF16) for m in range(4)]
    kxmg = [P.dram(f"kxmg{m}", [4 * 768, 512], BF16) for m in range(4)]
    kxf = [P.dram(f"kxf{m}", [512, 512], BF16) for m in range(4)]
    kxfg = [P.dram(f"kxfg{m}", [4 * 512, 512], BF16) for m in range(4)]
    vx = [P.dram(f"vx{m}", [2048, 256], BF16) for m in range(4)]
    vxg = [P.dram(f"vxg{m}", [4 * 2048, 256], BF16) for m in range(4)]
    sx = [P.dram(f"sx{i}", [256, 1024], F32) for i in range(2)]
    sxg = [P.dram(f"sxg{i}", [4 * 256, 1024], F32) for i in range(2)]
    fx = P.dram("fx", [128, 224], F32)
    fxg = P.dram("fxg", [4 * 128, 224], F32)
    tx = P.dram("tx", [128, 128], F32)
    txg = P.dram("txg", [4 * 128, 128], F32)
    dbg_out = {}

    def gather_pairs(pairs):
        for (a, b) in pairs:
            P.pool.collective_compute(kind="AllGather", op=ALU.bypass, replica_groups=RG,
                                      ins=[a.full().re("(p a) c -> p (a c)", p=128)],
                                      outs=[b.full().re("(q a) c -> q (a c)", q=512)])

    def load_consts():
        d = {}
        d["cm"] = P.sb("cmb", [128, 4, 128], F32)
        P.dma("sync", out=d["cm"].full(), in_=cm_d.full())
        d["sel"] = P.sb("selb", [128, 32], F32)
        P.dma("sync", out=d["sel"].full(), in_=sel_d.full())
        d["eps"] = P.sb("eps", [128, 1], F32)
        P.dve.memset(ap=d["eps"].full(), constant=1e-6)
        d["one1"] = P.sb("one1", [128, 1], F32)
        P.dve.memset(ap=d["one1"].full(), constant=1.0)
        d["zero"] = P.sb("zero", [128, 1], F32)
        P.dve.memset(ap=d["zero"].full(), constant=0.0)
        return d

    def phase_A(l):
        K = load_consts()
        cmb = K["cm"]
        tri, ident, ones = cmb[:, 0, :], cmb[:, 2, :], cmb[:, 3, :]
        eps, one1 = K["eps"], K["one1"]
        xin = xb[l]
        cstb = P.sb("cstb", [128, offA["_n"]], F32)
        C = Cst(P, cstb, offA)
        P.dma("sync", out=cstb.full(), in_=cstA_d[l])
        rowc = P.sb("rowc", [128, 56], F32)
        P.dma("sync", out=rowc.full(), in_=rowc_d[l])
        matf = P.sb("matf", [128, 192], F32)
        matb = P.sb("matb", [128, 192], BF16)
        P.dma("sync", out=matf.full(), in_=mats_d.full())
        P.dve.tensor_copy(out=matb.full(), in_=matf.full())
        prh = matb[0:96, 0:96]
        selm = matb[0:32, 96:192]
        identb = P.sb("identb", [128, 128], BF16)
        P.dve.tensor_copy(out=identb.full(), in_=ident)
        Aneg_r = P.sb("Aneg_r", [128, 16], F32)
        P.act.activation(out=Aneg_r.full(), in_=rowc[:, 24:40], func=AF.Exp)
        P.dve.tensor_scalar(out=Aneg_r.full(), in0=Aneg_r.full(), scalar1=-1.0, scalar2=None, op0=ALU.mult)

        pb = [P.ps(f"pb{i}", [128, 512], F32) for i in range(7)]
        pbt = P.ps("pbt", [128, 1024], BF16)
        pbi = {}

        def nxt_ps(lo=0, hi=4):
            i = pbi.get(lo, 0)
            pbi[lo] = (i + 1) % (hi - lo)
            return pb[lo + i]

        Ctab = P.sb("Ctab", [96, T], F32)
        Stab = P.sb("Stab", [96, T], F32)
        hraw = P.sb("hraw", [96, 512], F32)
        hsq = P.sb("hsq", [96, 512], F32)
        hrs = P.sb("hrs", [96, 512], F32)
        hnf = P.sb("hnf", [96, 512], F32)
        hnb = P.sb("hnb", [96, 512], BF16)
        ht1 = P.sb("ht1", [96, 512], F32)
        ht2 = P.sb("ht2", [96, 512], F32)
        posf, rr_tmp, rr_m = hrs, hraw, hsq

        class _IV:
            def __init__(self, b):
                self.b = b

            def full(self):
                return self.b.full().bitcast(I32)
        posi, rr_i = _IV(ht1), _IV(ht2)

        def sin_table(outv, phase):
            P.dve.tensor_scalar(out=rr_tmp.full(), in0=posf.full(), scalar1=C.col("invf"), scalar2=phase,
                                op0=ALU.mult, op1=ALU.add)
            P.dve.tensor_scalar(out=rr_m.full(), in0=rr_tmp.full(), scalar1=1.0 / (2 * np.pi), scalar2=None, op0=ALU.mult)
            P.dve.tensor_copy(out=rr_i.full(), in_=rr_m.full())
            P.dve.tensor_copy(out=rr_m.full(), in_=rr_i.full())
            P.dve.scalar_tensor_tensor(out=rr_tmp.full(), in0=rr_m.full(), scalar=-2 * np.pi, in1=rr_tmp.full(),
                                       op0=ALU.mult, op1=ALU.add)
            P.dve.tensor_scalar(out=rr_m.full(), in0=rr_tmp.full(), scalar1=np.pi, scalar2=-2 * np.pi, op0=ALU.is_gt, op1=ALU.mult)
            P.dve.tensor_tensor(out=rr_tmp.full(), in0=rr_tmp.full(), in1=rr_m.full(), op=ALU.add)
            P.dve.tensor_scalar(out=rr_m.full(), in0=rr_tmp.full(), scalar1=-np.pi, scalar2=2 * np.pi, op0=ALU.is_lt, op1=ALU.mult)
            P.dve.tensor_tensor(out=rr_tmp.full(), in0=rr_tmp.full(), in1=rr_m.full(), op=ALU.add)
            P.act.activation(out=outv, in_=rr_tmp.full(), func=AF.Sin)

        for i in range(4):
            P.dma("sync", out=posi.full(), in_=pos[:, i * 512:(i + 1) * 512].f(lambda a: a.partition_broadcast(96)))
            P.dve.tensor_copy(out=posf.full(), in_=posi.full())
            sin_table(Stab[:, i * 512:(i + 1) * 512], 0.0)
            sin_table(Ctab[:, i * 512:(i + 1) * 512], np.pi / 2)
        P.dve.memset(ap=Stab[0:64, :], constant=0.0)
        P.dve.memset(ap=Ctab[0:64, :], constant=1.0)

        hn = P.sb("hn", [128, 8, 4 * SW], BF16)
        xst = P.sb("xst", [128, 8, 512], F32)
        sq = P.sb("sq", [128, 512], F32)
        rstd = P.sb("rstd", [128, 512], F32)
        xTv = xin.full().re("(kc p) n -> p kc n", p=128)

        def rstd_from(ps_view, n_feat, rows, rstd_view):
            P.act.activation(out=rstd_view, in_=ps_view, func=AF.Sqrt, bias=eps[0:rows, 0:1], scale=1.0 / n_feat)
            P.dve.reciprocal(out=rstd_view, in_=rstd_view)

        halos = [(m * SW, 4) for m in range(4)]
        main = [(m * SW + 4, 512) for m in range(4)]
        for (c0, w) in halos + main:
            P.dma("sync", out=xst[:, :, 0:w], in_=xTv[:, :, c0:c0 + w])
            ps = nxt_ps(4, 6)
            for kc in range(8):
                P.act.activation(out=sq[:, 0:w], in_=xst[:, kc, 0:w], func=AF.Square)
                P.pe.matmul(out=ps[:, 0:w], lhsT=ones, rhs=sq[:, 0:w], start=(kc == 0), stop=(kc == 7))
            rstd_from(ps[:, 0:w], 1024.0, 128, rstd[:, 0:w])
            for kc in range(8):
                P.dve.scalar_tensor_tensor(out=hn[:, kc, c0:c0 + w], in0=xst[:, kc, 0:w], scalar=C.col("g_mix", kc),
                                           in1=rstd[:, 0:w], op0=ALU.mult, op1=ALU.mult)

        wst = [P.sb(f"wst{i}", [128, 8, 256], F32) for i in range(2)]
        wbf = [P.sb(f"wbf{i}", [128, 8, 512], BF16) for i in range(2)]
        wcnt = [0]
        scnt = [0]
        w_inv = w_in[l].re("(kc p) n -> p kc n", p=128)

        SBv = 672 + 1544
        wplan = [(0, 384), (384, 288), (672, 512), (672 + 512, 512), (672 + 1024, 512), (672 + 1536, 8),
                 (SBv + 1024 + 1536, 16), (SBv, 512), (SBv + 512, 512)]
        wplan += [(SBv + 1024 + b_ * 512, 512) for b_ in range(3)]
        wplan += [(SBv + 2576 + b_ * 512, 512) for b_ in range(6)]
        wpend = {}

        def w_issue(g):
            c0, ncols = wplan[g]
            lst = []
            for h0 in range(0, ncols, 256):
                n = min(256, ncols - h0)
                si = scnt[0] % 2
                scnt[0] += 1
                P.dma("sync", out=wst[si][:, :, 0:n], in_=w_inv[:, :, c0 + h0:c0 + h0 + n])
                lst.append((si, h0, n))
            wpend[g] = lst

        def load_w(c0, ncols):
            g = wcnt[0]
            wcnt[0] += 1
            assert wplan[g] == (c0, ncols), (g, wplan[g], c0, ncols)
            i = g % 2
            if g not in wpend:
                w_issue(g)
            lst = wpend.pop(g)
            for (si, h0, n) in lst:
                P.act.copy(out=wbf[i][:, :, h0:h0 + n], in_=wst[si][:, :, 0:n])
            if g + 1 < len(wplan):
                w_issue(g + 1)
            return wbf[i]

        def proj(wb, wc0, mcols, c0, w, ps_view):
            for kc in range(8):
                P.pe.matmul(out=ps_view, lhsT=wb[:, kc, wc0:wc0 + mcols], rhs=hn[:, kc, c0:c0 + w],
                            start=(kc == 0), stop=(kc == 7))

        def proj_tm(wb, wc0, ncols, tok0, ps_view):
            for kc in range(8):
                P.pe.matmul(out=ps_view, lhsT=hn[:, kc, tok0:tok0 + 128], rhs=wb[:, kc, wc0:wc0 + ncols],
                            start=(kc == 0), stop=(kc == 7))

        ostg_cnt = [0]
        ostg = [P.sb(f"ostg{i}", [128, 512], BF16) for i in range(4)]

        def next_ostg():
            i = ostg_cnt[0] % 4
            ostg_cnt[0] += 1
            return ostg[i]

        def out_dma(dst_view, src_view):
            P.dma("sync" if ostg_cnt[0] % 2 else "scalar", out=dst_view, in_=src_view)

        hsets = [dict(hraw=hraw.full(), hsq=hsq.full(), hrs=hrs.full(), hnf=hnf.full(), hnb=hnb.full(),
                      ht1=ht1.full(), ht2=ht2.full())]
        hnb1 = P.sb("hnb1", [96, 512], BF16)
        hsets.append(dict(hraw=xst[0:96, 0, :].k(0), hsq=xst[0:96, 1, :].k(1), hrs=xst[0:96, 2, :].k(2),
                          hnf=xst[0:96, 3, :].k(3), hnb=hnb1.full(), ht1=xst[0:96, 4, :].k(4), ht2=xst[0:96, 5, :].k(5)))
        hb2 = P.sb("hb2", [96, 6, 512], F32)
        hnb2 = P.sb("hnb2", [96, 512], BF16)
        hsets.append(dict(hraw=hb2[:, 0, :].k(0), hsq=hb2[:, 1, :].k(1), hrs=hb2[:, 2, :].k(2),
                          hnf=hb2[:, 3, :].k(3), hnb=hnb2.full(), ht1=hb2[:, 4, :].k(4), ht2=hb2[:, 5, :].k(5)))
        hcnt = [0]

        def headnorm(projfn, d, gain_col, rope, tok0, dst_view):
            H = hsets[hcnt[0] % 3]
            hcnt[0] += 1
            ps_view = projfn()
            P.act.activation(out=H["hsq"][0:d, :], in_=ps_view, func=AF.Square)
            P.act.copy(out=H["hraw"][0:d, :], in_=ps_view)
            yield
            ps2 = nxt_ps(4, 6)
            P.pe.matmul(out=ps2[0:d, :], lhsT=cmb[0:d, 3, 0:d], rhs=H["hsq"][0:d, :], start=True, stop=True)
            rstd_from(ps2[0:d, :], float(d), d, H["hrs"][0:d, :])
            og = next_ostg()
            if not rope:
                P.dve.scalar_tensor_tensor(out=og[0:d, :], in0=H["hraw"][0:d, :], scalar=gain_col, in1=H["hrs"][0:d, :],
                                           op0=ALU.mult, op1=ALU.mult)
            else:
                P.dve.scalar_tensor_tensor(out=H["hnf"][0:d, :], in0=H["hraw"][0:d, :], scalar=gain_col, in1=H["hrs"][0:d, :],
                                           op0=ALU.mult, op1=ALU.mult)
                P.act.copy(out=H["hnb"][0:d, :], in_=H["hnf"][0:d, :])
                yield
                ps3 = nxt_ps(6, 7)
                P.pe.matmul(out=ps3[0:d, :], lhsT=prh, rhs=H["hnb"][0:d, :], start=True, stop=True)
                P.dve.tensor_tensor(out=H["ht1"][0:d, :], in0=H["hnf"][0:d, :], in1=Ctab[0:d, tok0:tok0 + 512], op=ALU.mult)
                P.dve.tensor_tensor(out=H["ht2"][0:d, :], in0=ps3[0:d, :], in1=Stab[0:d, tok0:tok0 + 512], op=ALU.mult)
                P.pool.tensor_tensor(out=og[0:d, :], in0=H["ht1"][0:d, :], in1=H["ht2"][0:d, :], op=ALU.add)
            out_dma(dst_view, og[0:d, :])

        def run_pipe(gens, depth=3):
            gens = iter(gens)
            active = []
            while True:
                started = False
                if len(active) < depth:
                    g = next(gens, None)
                    if g is not None:
                        started = True
                        try:
                            next(g)
                            active.append(g)
                        except StopIteration:
                            pass
                if not active and not started:
                    break
                olds = active[:-1] if (started and active) else list(active)
                for g in olds:
                    try:
                        next(g)
                    except StopIteration:
                        active.remove(g)

        lat = P.sb("lat", [128, 3, 512], F32)
        latn = P.sb("latn", [128, 3, 512], BF16)

        def latent_norm(ps_list, gname):
            nch = len(ps_list)
            ps2 = nxt_ps(4, 6)
            for i, psv in enumerate(ps_list):
                P.act.activation(out=sq.full(), in_=psv, func=AF.Square)
                P.act.copy(out=lat[:, i, :], in_=psv)
                P.pe.matmul(out=ps2.full(), lhsT=ones, rhs=sq.full(), start=(i == 0), stop=(i == nch - 1))
            rstd_from(ps2.full(), 128.0 * nch, 128, rstd.full())
            for i in range(nch):
                P.dve.scalar_tensor_tensor(out=latn[:, i, :], in0=lat[:, i, :], scalar=C.col(gname, i), in1=rstd.full(),
                                           op0=ALU.mult, op1=ALU.mult)

        def small_w(name, dram_l, kc_n, ncols, i):
            bfb = P.sb(name, [128, kc_n, ncols], BF16)
            dv = dram_l.re("(kc p) n -> p kc n", p=128)
            for kc in range(kc_n):
                si = scnt[0] % 2
                scnt[0] += 1
                stg = wst[si].full().re("p a b -> p (a b)")[:, 0:ncols]
                P.dma("sync", out=stg, in_=dv[:, kc, :])
                P.act.copy(out=bfb[:, kc, :], in_=stg)
            return bfb

        uqb = small_w("uqb", w_uq[l], 3, 768, 0)
        kpb = small_w("kpb", w_kp[l], 2, 768, 1)
        wvb = small_w("wvb", w_v[l], 2, 512, 0)

        vstg = [P.sb(f"vstg{i}", [128, 512], BF16) for i in range(2)]
        vcnt = [0]

        def v_out(kind, ktl, ps_view):
            vs = vstg[vcnt[0] % 2]
            vcnt[0] += 1
            P.act.copy(out=vs.full(), in_=ps_view)
            P.dma("sync" if vcnt[0] % 2 else "scalar",
                  out=vx[ktl // 4][kind * 1024:(kind + 1) * 1024, (ktl % 4) * 64:(ktl % 4 + 1) * 64].re("(h p) d -> p h d", p=128),
                  in_=vs.full().re("p (h d) -> p h d", h=8))

        wb = load_w(0, 384)
        for m, (c0, w) in enumerate(main):
            pss = []
            for ch in range(3):
                ps = nxt_ps(0, 4)
                proj(wb, ch * 128, 128, c0, 512, ps.full())
                pss.append(ps.full())
            latent_norm(pss, "g_cq")
            def mkq(h):
                def f():
                    ps = nxt_ps(0, 4)
                    for kc in range(3):
                        P.pe.matmul(out=ps[0:96, :], lhsT=uqb[:, kc, h * 96:(h + 1) * 96], rhs=latn[:, kc, :],
                                    start=(kc == 0), stop=(kc == 2))
                    return ps[0:96, :]
                return f
            run_pipe(headnorm(mkq(h), 96, C.col("g_q"), True, m * 512, qm[h, :, m * 512:(m + 1) * 512]) for h in range(8))
        wb = load_w(384, 288)
        krb = P.sb("krb", [32, 512], BF16)
        for m, (c0, w) in enumerate(main):
            pss = []
            for ch in range(2):
                ps = nxt_ps(0, 4)
                proj(wb, ch * 128, 128, c0, 512, ps.full())
                pss.append(ps.full())
            ps = nxt_ps(0, 4)
            proj(wb, 256, 32, c0, 512, ps[0:32, :])
            P.act.copy(out=krb.full(), in_=ps[0:32, :])
            latent_norm(pss, "g_ckv")
            def mkk(h):
                def f():
                    ps = nxt_ps(0, 4)
                    for kc in range(2):
                        P.pe.matmul(out=ps[0:96, :], lhsT=kpb[:, kc, h * 96:(h + 1) * 96], rhs=latn[:, kc, :],
                                    start=(kc == 0), stop=False)
                    P.pe.matmul(out=ps[0:96, :], lhsT=selm, rhs=krb.full(), start=False, stop=True)
                    return ps[0:96, :]
                return f
            run_pipe(headnorm(mkk(h), 96, C.col("g_k"), True, m * 512, kxm[m][h * 96:(h + 1) * 96, :]) for h in range(8))
            for j in range(4):
                ps = nxt_ps(0, 4)
                for kc in range(2):
                    P.pe.matmul(out=ps.full(), lhsT=latn[:, kc, j * 128:(j + 1) * 128], rhs=wvb[:, kc, :],
                                start=(kc == 0), stop=(kc == 1))
                v_out(0, m * 4 + j, ps.full())
        for (base, gname, isq) in ((672, "g_fq", True), (672 + 512, "g_fk", False)):
            wb = load_w(base, 512)
            def mkf(wb_, h, c0):
                def f():
                    ps = nxt_ps(0, 4)
                    proj(wb_, h * 64, 64, c0, 512, ps[0:64, :])
                    return ps[0:64, :]
                return f
            gl = []
            for m, (c0, w) in enumerate(main):
                for h in range(8):
                    dst = qf[h, :, m * 512:(m + 1) * 512] if isq else kxf[m][h * 64:(h + 1) * 64, :]
                    gl.append(headnorm(mkf(wb, h, c0), 64, C.col(gname), False, m * 512, dst))
            run_pipe(gl)
        wb = load_w(672 + 1024, 512)
        for m, (c0, w) in enumerate(main):
            for j in range(4):
                ps = nxt_ps(0, 4)
                proj_tm(wb, 0, 512, c0 + j * 128, ps.full())
                v_out(1, m * 4 + j, ps.full())
        gather_pairs(list(zip(kxm, kxmg)) + list(zip(vx, vxg)) + list(zip(kxf, kxfg)))
        FB = 672 + 1536
        SB = 672 + 1544
        lf_tm = P.sb("lf_tm", [128, KT_L, 8], F32)
        dt_tm = P.sb("dt_tm", [128, KT_L, 16], F32)
        a_tm = P.sb("a_tm", [128, KT_L, 16], F32)
        tmpr = P.sb("tmpr", [128, 16], F32)
        wf = load_w(FB, 8)
        for m, (c0, w) in enumerate(main):
            for j in range(4):
                kt = m * 4 + j
                ps = nxt_ps(0, 4)
                proj_tm(wf, 0, 8, c0 + j * 128, ps[:, 0:8])
                P.dve.tensor_tensor(out=tmpr[:, 0:8], in0=ps[:, 0:8], in1=rowc[:, 0:8], op=ALU.add)
                P.act.activation(out=tmpr[:, 0:8], in_=tmpr[:, 0:8], func=AF.Exp, scale=-1.0)
                P.act.activation(out=tmpr[:, 0:8], in_=tmpr[:, 0:8], func=AF.Ln, bias=one1[:, 0:1], scale=1.0)
                P.dve.tensor_scalar(out=lf_tm[:, kt, :], in0=tmpr[:, 0:8], scalar1=-1.0, scalar2=None, op0=ALU.mult)
        wd = load_w(SB + 1024 + 1536, 16)
        for m, (c0, w) in enumerate(main):
            for j in range(4):
                kt = m * 4 + j
                ps = nxt_ps(0, 4)
                proj_tm(wd, 0, 16, c0 + j * 128, ps[:, 0:16])
                P.dve.tensor_tensor(out=tmpr.full(), in0=ps[:, 0:16], in1=rowc[:, 8:24], op=ALU.add)
                P.act.activation(out=tmpr.full(), in_=tmpr.full(), func=AF.Exp)
                P.act.activation(out=dt_tm[:, kt, :], in_=tmpr.full(), func=AF.Ln, bias=one1[:, 0:1], scale=1.0)
                P.dve.tensor_tensor(out=a_tm[:, kt, :], in0=dt_tm[:, kt, :], in1=Aneg_r.full(), op=ALU.mult)
        P.dma("sync", out=dtd.full(), in_=dt_tm.full())
        P.dma("sync", out=atd.full(), in_=a_tm.full())
        def plain_group(base, ncols, func, bias_name, dst, dst_row0):
            wb_ = load_w(base, ncols)
            for m, (c0, w) in enumerate(main):
                for ch in range(ncols // 128):
                    ps = nxt_ps(0, 4)
                    proj(wb_, ch * 128, 128, c0, 512, ps.full())
                    og = next_ostg()
                    if bias_name is None:
                        P.act.activation(out=og.full(), in_=ps.full(), func=func)
                    else:
                        P.act.activation(out=og.full(), in_=ps.full(), func=func,
                                         bias=C.col(bias_name, (dst_row0 // 128) + ch))
                    out_dma(dst[dst_row0 + ch * 128:dst_row0 + (ch + 1) * 128, m * 512:(m + 1) * 512], og.full())

        for blk in range(2):
            plain_group(SB + blk * 512, 512, AF.Silu, None, szd, blk * 512)
        upre = P.sb("upre", [128, 516], F32)
        carry = P.sb("carry", [128, 4], F32)
        acc0 = P.sb("acc0", [128, 512], F32)
        tstg = [P.sb(f"tstg{i}", [128, 4, 128], BF16) for i in range(2)]
        tcnt = [0]
        for blk in range(3):
            wb = load_w(SB + 1024 + blk * 512, 512)
            for m, (c0, w) in enumerate(main):
                for ch in range(4):
                    cg = blk * 4 + ch
                    ps = nxt_ps(0, 4)
                    proj(wb, ch * 128, 128, c0 - 4, 4, ps[:, 0:4])
                    P.act.copy(out=upre[:, 0:4], in_=ps[:, 0:4])
                    ps = nxt_ps(0, 4)
                    proj(wb, ch * 128, 128, c0, 512, ps.full())
                    P.act.copy(out=upre[:, 4:516], in_=ps.full())
                    P.dve.tensor_scalar(out=acc0.full(), in0=upre[:, 4:516], scalar1=C.col("cw3", cg), scalar2=C.col("cb", cg),
                                        op0=ALU.mult, op1=ALU.add)
                    for k in range(3):
                        P.dve.scalar_tensor_tensor(out=acc0.full(), in0=upre[:, 1 + k:513 + k], scalar=C.col(f"cw{k}", cg),
                                                   in1=acc0.full(), op0=ALU.mult, op1=ALU.add)
                    og = next_ostg()
                    P.act.activation(out=og.full(), in_=acc0.full(), func=AF.Silu)
                    if cg >= 8:
                        out_dma(bct[(cg - 8) * 128:(cg - 7) * 128, m * 512:(m + 1) * 512], og.full())
                    if cg < 10:
                        i2 = tcnt[0] % 2
                        tcnt[0] += 1
                        for j in range(4):
                            P.pe.transpose(out=pbt[:, i2 * 512 + j * 128:i2 * 512 + (j + 1) * 128],
                                           in_=og[:, j * 128:(j + 1) * 128], identity=identb.full())
                        ts_ = tstg[i2]
                        P.dve.tensor_copy(out=ts_.full().re("p j f -> p (j f)"), in_=pbt[:, i2 * 512:(i2 + 1) * 512])
                        if cg < 8:
                            P.dma("sync", out=xtm[:, m * 4:(m + 1) * 4, cg * 128:(cg + 1) * 128], in_=ts_.full())
                        else:
                            P.dma("sync", out=btm[:, m * 4:(m + 1) * 4, (cg - 8) * 128:(cg - 7) * 128], in_=ts_.full())
        GB = SB + 2576
        for blk in range(6):
            plain_group(GB + blk * 512, 512, AF.Sigmoid, "b_gate", gd, blk * 512)
        fxs = P.sb("fxs", [128, 224], F32)
        within = P.sb("within", [128, KT_L, 8], F32)
        ttot = P.sb("ttot", [128, KT_L, 8], F32)
        f2 = lambda b: b.full().re("p a b -> p (a b)")
        ps = nxt_ps(0, 4)
        P.pe.matmul(out=ps[:, 0:128], lhsT=tri, rhs=f2(lf_tm), start=True, stop=True)
        P.act.copy(out=f2(within), in_=ps[:, 0:128])
        ps = nxt_ps(0, 4)
        P.pe.matmul(out=ps[:, 0:128], lhsT=ones, rhs=f2(lf_tm), start=True, stop=True)
        P.act.copy(out=f2(ttot), in_=ps[:, 0:128])
        Floc = fxs[:, 0:128].re("p (a b) -> p a b", b=8)
        totv = fxs[:, 128:160].re("p (a b) -> p a b", b=8)
        cacc = P.sb("cacc", [128, 8], F32)
        for m in range(4):
            P.dve.tensor_copy(out=Floc[:, 4 * m, :], in_=within[:, 4 * m, :])
            P.dve.tensor_copy(out=cacc.full(), in_=ttot[:, 4 * m, :])
            for j in range(1, 4):
                P.dve.tensor_tensor(out=Floc[:, 4 * m + j, :], in0=within[:, 4 * m + j, :], in1=cacc.full(), op=ALU.add)
                P.dve.tensor_tensor(out=cacc.full(), in0=cacc.full(), in1=ttot[:, 4 * m + j, :], op=ALU.add)
            P.dve.tensor_copy(out=totv[:, m, :], in_=cacc.full())
        ps = nxt_ps(0, 4)
        P.pe.transpose(out=ps[:, 0:128], in_=fxs[:, 0:128], identity=ident)
        FT = P.sb("FT", [128, 128], F32)
        r1 = P.sb("r1", [128, 128], F32)
        fh = [P.sb(f"fh{i}", [128, 128], BF16) for i in range(3)]
        P.act.copy(out=FT.full(), in_=ps[:, 0:128])
        P.dve.tensor_copy(out=fh[0].full(), in_=FT.full())
        P.dve.tensor_tensor(out=r1.full(), in0=FT.full(), in1=fh[0].full(), op=ALU.subtract)
        P.dve.tensor_copy(out=fh[1].full(), in_=r1.full())
        P.dve.tensor_tensor(out=r1.full(), in0=r1.full(), in1=fh[1].full(), op=ALU.subtract)
        P.dve.tensor_copy(out=fh[2].full(), in_=r1.full())
        for r in range(3):
            for kt in range(KT_L):
                P.dma("sync" if kt % 2 else "scalar", out=fq[:, r, kt * 128:(kt + 1) * 128], in_=fh[r][kt * 8:(kt + 1) * 8, :])
        P.dma("sync", out=fx[:, 0:160], in_=fxs[:, 0:160])

    def ssd_scan(l, K, pass1, fxs=None, dt_tm=None, a_tm=None, Hinit=None, rowc=None, pb=None):
        cmb = K["cm"]
        tri, trimask, ones = cmb[:, 0, :], cmb[:, 1, :], cmb[:, 3, :]
        if pb is None:
            pb = [P.ps(f"spb{i}", [128, 512], F32) for i in range(7)]
        if pass1:
            fxs = P.sb("decs", [128, 224], F32)
        if dt_tm is None:
            dt_tm = P.sb("dt_tm", [128, KT_L, 16], F32)
            a_tm = P.sb("a_tm", [128, KT_L, 16], F32)
            P.dma("sync", out=dt_tm.full(), in_=dtd.full())
            P.dma("sync", out=a_tm.full(), in_=atd.full())
        fl = lambda b: b.full().re("p c h -> p (c h)")
        Acum = P.sb("Acum", [128, KT_L, 16], F32)
        Atot = P.sb("Atot", [128, KT_L, 16], F32)
        wdec = P.sb("wdec", [128, KT_L, 16], F32)
        eAtot = P.sb("eAtot", [128, KT_L, 16], F32)
        psA = pb[0]
        P.pe.matmul(out=psA[:, 0:256], lhsT=tri, rhs=fl(a_tm), start=True, stop=True)
        P.act.copy(out=fl(Acum), in_=psA[:, 0:256])
        P.pe.matmul(out=psA[:, 256:512], lhsT=ones, rhs=fl(a_tm), start=True, stop=True)
        P.act.copy(out=fl(Atot), in_=psA[:, 256:512])
        P.act.activation(out=fl(eAtot), in_=fl(Atot), func=AF.Exp)
        P.dve.tensor_tensor(out=fl(wdec), in0=fl(Atot), in1=fl(Acum), op=ALU.subtract)
        P.act.activation(out=fl(wdec), in_=fl(wdec), func=AF.Exp)
        if not pass1:
            nAcum = P.sb("nAcum", [128, KT_L, 16], F32)
            eA = P.sb("eA", [128, KT_L, 16], F32)
            P.dve.tensor_scalar(out=fl(nAcum), in0=fl(Acum), scalar1=-1.0, scalar2=None, op0=ALU.mult)
            P.act.activation(out=fl(eA), in_=fl(Acum), func=AF.Exp)
            BCs = P.sb("BCs", [128, 4, T], BF16)
            P.dma("gpsimd", out=BCs.full(), in_=bct.full().re("(a p) t -> p a t", p=128))
            cb = P.sb("cb", [128, 2, 128], F32)
            at = [P.sb(f"at{i}", [128, 128], F32) for i in range(2)]
            tm = [P.sb(f"tm{i}", [128, 128], F32) for i in range(2)]
            dec = [P.sb(f"dec{i}", [128, 128], F32) for i in range(2)]
            MT = [P.sb(f"MT{i}", [128, 128], BF16) for i in range(2)]
            t1 = P.sb("t1", [128, 1024], F32)
            t3 = P.sb("t3", [128, 1024], F32)
            yo = P.sb("yo", [128, 1024], BF16)
            yT = [P.sb(f"yT{i}", [128, 4, 128], F32) for i in range(2)]
        Hs = P.sb("Hs", [128, 1024], F32)
        Hb = P.sb("Hb", [128, 1024], BF16)
        xc = [P.sb(f"xc{i}", [128, 1024], BF16) for i in range(2)]
        Bc = [P.sb(f"Bc{i}", [128, 256], BF16) for i in range(2)]
        xdt = P.sb("xdt", [128, 1024], BF16)
        xdts = P.sb("xdts", [128, 1024], BF16)
        dsum = P.sb("dsum", [128, 16], F32)
        v3 = lambda v: v.re("p (h d) -> p h d", h=16)
        bc3 = lambda v: v.f(lambda a: a.unsqueeze(2).to_broadcast([128, 16, 64]))
        for m in range(4):
            if pass1:
                P.dve.memset(ap=Hs.full(), constant=0.0)
                P.dve.memset(ap=dsum.full(), constant=0.0)
            else:
                P.dve.tensor_copy(out=Hs.full(), in_=Hinit[:, m, :])
                P.act.copy(out=Hb.full(), in_=Hinit[:, m, :])
            for j in range(4):
                c = m * 4 + j
                x_c = xc[c % 2]
                B_c = Bc[c % 2]
                P.dma("sync", out=x_c.full(), in_=xtm[:, c, :])
                P.dma("gpsimd", out=B_c.full(), in_=btm[:, c, :])
                P.dve.tensor_tensor(out=v3(xdt.full()), in0=v3(x_c.full()), in1=bc3(dt_tm[:, c, :]), op=ALU.mult)
                P.pool.tensor_tensor(out=v3(xdts.full()), in0=v3(xdt.full()), in1=bc3(wdec[:, c, :]), op=ALU.mult)
                if not pass1:
                    cs = slice(c * 128, (c + 1) * 128)
                    ps_cb = pb[1]
                    for g in range(2):
                        P.pe.matmul(out=ps_cb[:, g * 128:(g + 1) * 128], lhsT=BCs[:, g, cs], rhs=BCs[:, 2 + g, cs],
                                    start=True, stop=True)
                    P.act.copy(out=cb.full().re("p a b -> p (a b)"), in_=ps_cb[:, 0:256])
                    ps_off = [pb[2], pb[3]]
                    for g in range(2):
                        P.pe.matmul(out=ps_off[g].full(), lhsT=BCs[:, 2 + g, cs], rhs=Hb[:, g * 512:(g + 1) * 512],
                                    start=True, stop=True)
                    ps_y = [pb[4], pb[5]]
                    for h in range(16):
                        i2 = h % 2
                        g = h // 8
                        P.dve.tensor_scalar(out=at[i2].full(), in0=tri, scalar1=a_tm[:, c, h:h + 1], scalar2=None, op0=ALU.mult)
                        ps_A = pb[6]
                        P.pe.matmul(out=ps_A[:, i2 * 128:(i2 + 1) * 128], lhsT=ones, rhs=at[i2].full(), start=True, stop=True)
                        P.dve.tensor_tensor(out=tm[i2].full(), in0=ps_A[:, i2 * 128:(i2 + 1) * 128], in1=trimask, op=ALU.add)
                        P.act.activation(out=dec[i2].full(), in_=tm[i2].full(), func=AF.Exp, bias=nAcum[:, c, h:h + 1], scale=1.0)
                        P.pool.tensor_tensor(out=MT[i2].full(), in0=cb[:, g, :], in1=dec[i2].full(), op=ALU.mult)
                        hh = h % 8
                        P.pe.matmul(out=ps_y[g][:, hh * 64:(hh + 1) * 64], lhsT=MT[i2].full(), rhs=xdt[:, h * 64:(h + 1) * 64],
                                    start=True, stop=True)
                    for g in range(2):
                        gs_ = slice(g * 512, (g + 1) * 512)
                        v8 = lambda v: v.re("p (h d) -> p h d", h=8)
                        b8 = lambda v: v.f(lambda a: a.unsqueeze(2).to_broadcast([128, 8, 64]))
                        P.dve.tensor_tensor(out=v8(t1[:, gs_]), in0=v8(ps_off[g].full()), in1=b8(eA[:, c, g * 8:(g + 1) * 8]), op=ALU.mult)
                        P.dve.tensor_tensor(out=t1[:, gs_], in0=t1[:, gs_], in1=ps_y[g].full(), op=ALU.add)
                    P.pool.tensor_tensor(out=v3(t3.full()), in0=v3(x_c.full()), in1=bc3(rowc[:, 40:56]), op=ALU.mult)
                    P.pool.tensor_tensor(out=t3.full(), in0=t1.full(), in1=t3.full(), op=ALU.add)
                    for q4 in range(2):
                        pst = pb[2 + q4]
                        for jj in range(4):
                            fc = q4 * 4 + jj
                            P.pe.transpose(out=pst[:, jj * 128:(jj + 1) * 128], in_=t3[:, fc * 128:(fc + 1) * 128],
                                           identity=cmb[:, 2, :])
                        yt = yT[q4]
                        P.act.copy(out=yt.full().re("p a b -> p (a b)"), in_=pst.full())
                        P.dma("sync", out=yd[q4 * 512:(q4 + 1) * 512, c * 128:(c + 1) * 128].re("(a p) t -> p a t", p=128),
                              in_=yt.full())
                ps_h = [pb[0], pb[1]] if pass1 else [pb[4], pb[5]]
                for g in range(2):
                    P.pe.matmul(out=ps_h[g].full(), lhsT=B_c[:, g * 128:(g + 1) * 128], rhs=xdts[:, g * 512:(g + 1) * 512],
                                start=True, stop=True)
                P.dve.tensor_tensor(out=v3(Hs.full()), in0=v3(Hs.full()), in1=bc3(eAtot[:, c, :]), op=ALU.mult)
                for g in range(2):
                    P.dve.tensor_tensor(out=Hs[:, g * 512:(g + 1) * 512], in0=Hs[:, g * 512:(g + 1) * 512], in1=ps_h[g].full(), op=ALU.add)
                if pass1:
                    P.dve.tensor_tensor(out=dsum.full(), in0=dsum.full(), in1=Atot[:, c, :], op=ALU.add)
                else:
                    P.act.copy(out=Hb.full(), in_=Hs.full())
            if pass1:
                P.dma("sync", out=sx[m // 2][(m % 2) * 128:(m % 2 + 1) * 128, :], in_=Hs.full())
                P.act.activation(out=fxs[:, 160 + m * 16:160 + (m + 1) * 16], in_=dsum.full(), func=AF.Exp)
        if pass1:
            P.dma("sync", out=fx[:, 160:224], in_=fxs[:, 160:224])

    def load_fg():
        fg = P.sb("fg", [128, 4, 224], F32)
        P.dma("sync", out=fg.full(), in_=fxg.full().re("(r p) c -> p r c", p=128))
        return fg

    def phase_attn(l):
        K = load_consts()
        sel, zero = K["sel"], K["zero"]
        mskb = P.sb("mskb", [128, 8, 512], F32)
        P.dma("gpsimd", out=mskb.full(), in_=msk_d.full())
        fg = load_fg()
        offs = P.sb("offs", [128, 16, 8], F32)
        run = P.sb("run", [128, 8], F32)
        P.dve.memset(ap=run.full(), constant=0.0)
        for s_ in range(16):
            m, r = divmod(s_, 4)
            P.dve.tensor_copy(out=offs[:, s_, :], in_=run.full())
            P.dve.tensor_tensor(out=run.full(), in0=run.full(), in1=fg[:, r, 128 + m * 8:128 + (m + 1) * 8], op=ALU.add)
        offown = P.sb("offown", [128, 4, 8], F32)
        P.dve.memset(ap=offown.full(), constant=0.0)
        for m in range(4):
            for r in range(4):
                P.dve.scalar_tensor_tensor(out=offown[:, m, :], in0=offs[:, 4 * m + r, :], scalar=sel[:, 8 + 4 * m + r:9 + 4 * m + r],
                                           in1=offown[:, m, :], op0=ALU.mult, op1=ALU.add)
        negFg = P.sb("negFg", [128, 64, 8], F32)
        for s_ in range(16):
            m, r = divmod(s_, 4)
            src = fg[:, r, 0:128].re("p (a b) -> p a b", b=8)[:, 4 * m:4 * m + 4, :]
            P.dve.tensor_tensor(out=negFg[:, 4 * s_:4 * s_ + 4, :], in0=src,
                                in1=offs[:, s_, :].f(lambda a: a.unsqueeze(1).to_broadcast([128, 4, 8])), op=ALU.add)
        P.dve.tensor_scalar(out=negFg.full(), in0=negFg.full(), scalar1=-1.0, scalar2=None, op0=ALU.mult)
        biasm = P.sb("biasm", [128, 4, 64, 8], F32)
        for m in range(4):
            nk = (4 * m + 4) * 4
            P.dve.tensor_tensor(out=biasm[:, m, 0:nk, :], in0=negFg[:, 0:nk, :],
                                in1=offown[:, m, :].f(lambda a: a.unsqueeze(1).to_broadcast([128, nk, 8])), op=ALU.add)
            for jr in range(4):
                k0 = (4 * m + jr) * 4
                P.dve.tensor_scalar(out=biasm[:, m, k0:k0 + 4, :], in0=biasm[:, m, k0:k0 + 4, :],
                                    scalar1=sel[:, 28 + jr:29 + jr], scalar2=None, op0=ALU.add)

        pb = [P.ps(f"pb{i}", [128, 512], F32) for i in range(8)]
        K_sb = [P.sb(f"K_sb{i}", [96, S_], BF16) for i in range(2)]
        Q_sb = [P.sb(f"Q_sb{i}", [96, T], BF16) for i in range(2)]
        V_sb = [P.sb(f"V_sb{i}", [128, NKT, 128], BF16) for i in range(2)]
        for i in range(2):
            P.dve.memset(ap=V_sb[i][:, :, 64:128], constant=1.0)
        NSB = 4
        LA = 2
        WARM = True
        pt = [P.sb(f"pt{i}", [128, 512], BF16) for i in range(NSB)]
        mt = [P.sb(f"mt{i}", [128, 512], F32) for i in range(2)]
        rl = P.sb("rl", [128, 512], F32)
        rl2 = P.sb("rl2", [64, 512], F32)
        ot = [P.sb(f"ot{i}", [64, 512], BF16) for i in range(2)]
        cnt = [0, 0, 0]
        heads = [(0, h) for h in range(8)] + [(1, h) for h in range(8)]

        def loads(idx):
            kind, h = heads[idx]
            i = idx % 2
            nd = 96 if kind == 0 else 64
            for r in range(4):
                vr = r * 2048 + kind * 1024 + h * 128
                for m in range(4):
                    s0 = (4 * m + r) * 512
                    if kind == 0:
                        ksrc = kxmg[m][r * 768 + h * 96:r * 768 + (h + 1) * 96, :]
                    else:
                        ksrc = kxfg[m][r * 512 + h * 64:r * 512 + (h + 1) * 64, :]
                    P.dma("sync" if (r + m) % 2 == 0 else "gpsimd", out=K_sb[i][0:nd, s0:s0 + 512], in_=ksrc)
                    g0 = (4 * m + r) * 4
                    P.dma("gpsimd" if (r + m) % 2 == 0 else "sync",
                          out=V_sb[i][:, g0:g0 + 4, 0:64],
                          in_=vxg[m][vr:vr + 128, :].re("p (j d) -> p j d", j=4))
            if kind == 0:
                P.dma("sync", out=Q_sb[i][0:96, :], in_=qm[h])
            else:
                P.dve.memset(ap=K_sb[i][64:96, :], constant=0.0)
                P.dve.memset(ap=K_sb[i][64:67, :], constant=8.0)
                P.dma("sync", out=Q_sb[i][0:64, :], in_=qf[h])
                P.pool.memset(ap=Q_sb[i][64:96, :], constant=0.0)
                P.dma("gpsimd", out=Q_sb[i][64:67, :], in_=fq[h])

        iters = []
        for idx in range(16):
            for m in range(4):
                nk = (4 * m + 4) * 4
                for kt in range(nk):
                    iters.append((idx, m, kt, nk))

        def stage_qk(n):
            idx, m, kt, nk = iters[n]
            kind, h = heads[idx]
            i = idx % 2
            dk = 96
            scale = 96.0 ** -0.5 if kind == 0 else 0.125
            i3 = n % NSB
            ps = pb[i3]
            P.pe.matmul(out=ps.full(), lhsT=K_sb[i][0:dk, kt * 128:(kt + 1) * 128],
                        rhs=Q_sb[i][0:dk, m * 512:(m + 1) * 512], start=True, stop=True)
            blk = kt // 4
            if blk >= 4 * m:
                jr = blk - 4 * m
                mm = mt[cnt[1] % 2]
                cnt[1] += 1
                P.dve.scalar_tensor_tensor(out=mm.full(), in0=mskb[:, kind * 4 + kt % 4, :],
                                           scalar=sel[:, 24 + jr:25 + jr], in1=ps.full(), op0=ALU.mult, op1=ALU.add)
                src = mm.full()
                bias = sel[:, 28 + jr:29 + jr] if kind == 0 else biasm[:, m, kt, h:h + 1]
            else:
                src = ps.full()
                bias = zero[:, 0:1] if kind == 0 else biasm[:, m, kt, h:h + 1]
            P.act.activation(out=pt[i3].full(), in_=src, func=AF.Exp, scale=scale, bias=bias)
            if WARM:
                P.pe.matmul(out=pb[6][:, 0:256], lhsT=K_sb[i][0:dk, kt * 128:(kt + 1) * 128],
                            rhs=Q_sb[i][0:dk, m * 512:m * 512 + 256], start=True, stop=True)

        def stage_pv(n):
            idx, m, kt, nk = iters[n]
            kind, h = heads[idx]
            i = idx % 2
            oacc = pb[4 + (idx * 4 + m) % 2]
            P.pe.matmul(out=oacc.full(), lhsT=V_sb[i][:, kt, :], rhs=pt[n % NSB].full(), start=(kt == 0), stop=(kt == nk - 1))
            if kt == nk - 1:
                odst = omd if kind == 0 else ofd
                P.dve.reciprocal(out=rl[64:128, :], in_=oacc[64:128, :])
                P.dve.tensor_copy(out=rl2.full(), in_=rl[64:128, :])
                o = ot[m % 2]
                P.dve.tensor_tensor(out=o.full(), in0=oacc[0:64, :], in1=rl2.full(), op=ALU.mult)
                P.dma("sync", out=odst[h * 64:(h + 1) * 64, m * 512:(m + 1) * 512], in_=o.full())

        loads(0)
        loads(1)
        for n in range(len(iters) + LA):
            if n < len(iters):
                stage_qk(n)
            if n >= LA:
                stage_pv(n - LA)
                idx_p, m_p, kt_p, nk_p = iters[n - LA]
                if m_p == 3 and kt_p == nk_p - 1 and idx_p + 2 < 16:
                    loads(idx_p + 2)

    def phase_ssd2(l):
        K = load_consts()
        sel = K["sel"]
        rowc = P.sb("rowc", [128, 56], F32)
        P.dma("sync", out=rowc.full(), in_=rowc_d[l])
        fg = load_fg()
        Hin = P.sb("Hin", [128, 1024], F32)
        Hsel = P.sb("Hsel", [128, 4, 1024], F32)
        Sst = [P.sb(f"Sst{i}", [128, 1024], F32) for i in range(2)]
        P.dve.memset(ap=Hin.full(), constant=0.0)
        P.dve.memset(ap=Hsel.full(), constant=0.0)
        v3 = lambda v: v.re("p (h d) -> p h d", h=16)
        for s_ in range(16):
            m, r = divmod(s_, 4)
            P.dve.scalar_tensor_tensor(out=Hsel[:, m, :], in0=Hin.full(), scalar=sel[:, 8 + s_:9 + s_], in1=Hsel[:, m, :],
                                       op0=ALU.mult, op1=ALU.add)
            if s_ < 15:
                st_ = Sst[s_ % 2]
                P.dma("sync" if s_ % 2 else "gpsimd", out=st_.full(),
                      in_=sxg[m // 2][r * 256 + (m % 2) * 128:r * 256 + (m % 2 + 1) * 128, :])
                dcs = fg[:, r, 160 + m * 16:160 + (m + 1) * 16]
                P.dve.tensor_tensor(out=v3(Hin.full()), in0=v3(Hin.full()),
                                    in1=dcs.f(lambda a: a.unsqueeze(2).to_broadcast([128, 16, 64])), op=ALU.mult)
                P.pool.tensor_tensor(out=Hin.full(), in0=Hin.full(), in1=st_.full(), op=ALU.add)
        ssd_scan(l, K, pass1=False, Hinit=Hsel, rowc=rowc)

    def write_tails(txs):
        P.dma("sync", out=tx.full(), in_=txs.full().re("p m k c -> p (m k c)"))

    def halo_exchange(dst):
        K = load_consts()
        sel = K["sel"]
        P.pool.collective_compute(kind="AllGather", op=ALU.bypass, replica_groups=RG, ins=[tx.full()], outs=[txg.full()])
        tg = P.sb("tg", [128, 4, 128], F32)
        P.dma("sync", out=tg.full(), in_=txg.full().re("(r p) c -> p r c", p=128))
        hl = P.sb("hl", [128, 4, 32], F32)
        P.dve.memset(ap=hl.full(), constant=0.0)
        for m in range(4):
            for r in range(4):
                P.dve.scalar_tensor_tensor(out=hl[:, m, :], in0=tg[:, r, m * 32:(m + 1) * 32], scalar=sel[:, r:r + 1],
                                           in1=hl[:, m, :], op0=ALU.mult, op1=ALU.add)
            if m >= 1:
                P.dve.scalar_tensor_tensor(out=hl[:, m, :], in0=tg[:, 3, (m - 1) * 32:m * 32], scalar=sel[:, 4:5],
                                           in1=hl[:, m, :], op0=ALU.mult, op1=ALU.add)
        dv = dst.full().re("(kc p) n -> p kc n", p=128)
        for m in range(4):
            P.dma("sync", out=dv[:, :, m * SW:m * SW + 4], in_=hl[:, m, :].re("p (k c) -> p k c", c=4))

    def phase_merge(l):
        K = load_consts()
        ones, eps = K["cm"][:, 3, :], K["eps"]
        gsb = P.sb("gsb", [128, 8], F32)
        P.dma("sync", out=gsb.full(), in_=gssm_d[l])
        pb = [P.ps(f"pb{i}", [128, 512], F32) for i in range(8)]
        wst = [P.sb(f"wst{i}", [128, 4, 1024], F32) for i in range(2)]
        wcnt = [0]

        def load_w(dram_l, kc_n, name):
            bfb = P.sb(name, [128, kc_n, D], BF16)
            v = dram_l.re("(kc p) n -> p kc n", p=128)
            for k0 in range(0, kc_n, 4):
                i = wcnt[0] % 2
                wcnt[0] += 1
                P.dma("sync" if i == 0 else "scalar", out=wst[i].full(), in_=v[:, k0:k0 + 4, :])
                if i == 0:
                    P.pool.tensor_copy(out=bfb[:, k0:k0 + 4, :], in_=wst[i].full())
                else:
                    P.act.copy(out=bfb[:, k0:k0 + 4, :], in_=wst[i].full())
            return bfb

        Wa = load_w(w_a[l], 4, "Wa")
        Wb = load_w(w_b[l], 4, "Wb")
        Wc = load_w(w_c[l], 8, "Wc")
        Wo = load_w(w_o[l], 8, "Wo")
        ys = P.sb("ys", [128, 8, 512], F32)
        szs = P.sb("szs", [128, 8, 512], BF16)
        yn = P.sb("yn", [128, 8, 512], BF16)
        oms = P.sb("oms", [128, 4, 512], BF16)
        ofs = P.sb("ofs", [128, 4, 512], BF16)
        gs = P.sb("gs", [128, 24, 512], BF16)
        xs = P.sb("xs", [128, 8, 512], F32)
        sq = P.sb("sq", [128, 512], F32)
        rstd = P.sb("rstd", [128, 512], F32)
        m1 = [P.sb(f"m1_{i}", [128, 512], F32) for i in range(2)]
        m2 = [P.sb(f"m2_{i}", [128, 512], F32) for i in range(2)]
        m3 = [P.sb(f"m3_{i}", [128, 512], F32) for i in range(2)]
        mg = P.sb("mg", [128, 8, 512], BF16)
        xo = [P.sb(f"xo{i}", [128, 512], F32) for i in range(2)]
        txs = P.sb("txs", [128, 4, 8, 4], F32)
        ch = lambda d: d.full().re("(kc p) n -> p kc n", p=128)
        xmv = ch(xmid)
        for ti in range(4):
            ts = slice(ti * 512, (ti + 1) * 512)
            xsl = slice(ti * SW + 4, ti * SW + 516)
            P.dma("sync", out=ys.full(), in_=ch(yd)[:, :, ts])
            P.dma("gpsimd", out=szs.full(), in_=ch(szd)[:, :, ts])
            P.dma("sync", out=oms.full(), in_=ch(omd)[:, :, ts])
            P.dma("gpsimd", out=ofs.full(), in_=ch(ofd)[:, :, ts])
            P.dma("sync", out=gs.full(), in_=ch(gd)[:, :, ts])
            P.dma("gpsimd", out=xs.full(), in_=ch(xb[l])[:, :, xsl])
            ps = pb[7]
            for kc in range(8):
                P.dve.tensor_tensor(out=ys[:, kc, :], in0=ys[:, kc, :], in1=szs[:, kc, :], op=ALU.mult)
                P.act.activation(out=sq.full(), in_=ys[:, kc, :], func=AF.Square)
                P.pe.matmul(out=ps.full(), lhsT=ones, rhs=sq.full(), start=(kc == 0), stop=(kc == 7))
            P.act.activation(out=rstd.full(), in_=ps.full(), func=AF.Sqrt, bias=eps[:, 0:1], scale=1.0 / 1024.0)
            P.dve.reciprocal(out=rstd.full(), in_=rstd.full())
            for kc in range(8):
                P.dve.scalar_tensor_tensor(out=yn[:, kc, :], in0=ys[:, kc, :], scalar=gsb[:, kc:kc + 1], in1=rstd.full(),
                                           op0=ALU.mult, op1=ALU.mult)
            for oc in range(8):
                i2 = oc % 2
                osl = slice(oc * 128, (oc + 1) * 128)
                pa, pbb, pc = pb[0 + i2 * 3], pb[1 + i2 * 3], pb[2 + i2 * 3]
                for kc in range(4):
                    P.pe.matmul(out=pa.full(), lhsT=Wa[:, kc, osl], rhs=oms[:, kc, :], start=(kc == 0), stop=(kc == 3))
                for kc in range(4):
                    P.pe.matmul(out=pbb.full(), lhsT=Wb[:, kc, osl], rhs=ofs[:, kc, :], start=(kc == 0), stop=(kc == 3))
                for kc in range(8):
                    P.pe.matmul(out=pc.full(), lhsT=Wc[:, kc, osl], rhs=yn[:, kc, :], start=(kc == 0), stop=(kc == 7))
                P.dve.tensor_tensor(out=m1[i2].full(), in0=pa.full(), in1=gs[:, oc, :], op=ALU.mult)
                P.dve.tensor_tensor(out=m2[i2].full(), in0=pbb.full(), in1=gs[:, 8 + oc, :], op=ALU.mult)
                P.dve.tensor_tensor(out=m3[i2].full(), in0=pc.full(), in1=gs[:, 16 + oc, :], op=ALU.mult)
                P.pool.tensor_tensor(out=m1[i2].full(), in0=m1[i2].full(), in1=m2[i2].full(), op=ALU.add)
                P.pool.tensor_tensor(out=mg[:, oc, :], in0=m1[i2].full(), in1=m3[i2].full(), op=ALU.add)
            for oc in range(8):
                i2 = oc % 2
                ps = pb[6 + i2]
                for kc in range(8):
                    P.pe.matmul(out=ps.full(), lhsT=Wo[:, kc, oc * 128:(oc + 1) * 128], rhs=mg[:, kc, :],
                                start=(kc == 0), stop=(kc == 7))
                P.dve.tensor_tensor(out=xo[i2].full(), in0=ps.full(), in1=xs[:, oc, :], op=ALU.add)
                P.pool.tensor_copy(out=txs[:, ti, oc, :], in_=xo[i2][:, 508:512])
                P.dma("sync" if i2 else "gpsimd", out=xmv[:, oc, xsl], in_=xo[i2].full())
        write_tails(txs)

    def phase_ffn(l, last):
        K = load_consts()
        ones, eps = K["cm"][:, 3, :], K["eps"]
        cstb = P.sb("cstb", [128, offD2["_n"]], F32)
        C = Cst(P, cstb, offD2)
        P.dma("sync", out=cstb.full(), in_=cstD_d[l])
        pb = [P.ps(f"pb{i}", [128, 512], F32) for i in range(8)]
        wst = [P.sb(f"wst{i}", [128, 512], F32) for i in range(2)]
        wcnt = [0]
        Wu = P.sb("Wu", [128, 8, 5632], BF16)
        Wd = P.sb("Wd", [128, 22, D], BF16)
        wuv = w_up[l].re("(kc p) n -> p kc n", p=128)
        wdv = w_dn[l].re("(kc p) n -> p kc n", p=128)

        def wcast(dst, src):
            i = wcnt[0] % 2
            wcnt[0] += 1
            P.dma("sync" if i == 0 else "scalar", out=wst[i].full(), in_=src)
            if i == 0:
                P.pool.tensor_copy(out=dst, in_=wst[i].full())
            else:
                P.act.copy(out=dst, in_=wst[i].full())

        for ib in range(0, 22, 4):
            n = min(4, 22 - ib) * 128
            for base in (0, 2816):
                for kc in range(8):
                    wcast(Wu[:, kc, base + ib * 128:base + ib * 128 + n], wuv[:, kc, base + ib * 128:base + ib * 128 + n]) if n == 512 else None
                    if n != 512:
                        i = wcnt[0] % 2
                        wcnt[0] += 1
                        P.dma("sync" if i == 0 else "scalar", out=wst[i][:, 0:n], in_=wuv[:, kc, base + ib * 128:base + ib * 128 + n])
                        (P.pool.tensor_copy if i == 0 else P.act.copy)(out=Wu[:, kc, base + ib * 128:base + ib * 128 + n], in_=wst[i][:, 0:n])
        for k0 in range(22):
            for c0 in (0, 512):
                wcast(Wd[:, k0, c0:c0 + 512], wdv[:, k0, c0:c0 + 512])
        xst = P.sb("xst", [128, 8, 512], F32)
        hn = P.sb("hn", [128, 8, 512], BF16)
        sq = P.sb("sq", [128, 512], F32)
        rstd = P.sb("rstd", [128, 512], F32)
        act = P.sb("act", [128, 22, 512], BF16)
        upre = [P.sb(f"upre{i}", [128, 516], F32) for i in range(2)]
        acc = [P.sb(f"acc{i}", [128, 512], F32) for i in range(2)]
        sg = P.sb("sg", [128, 512], F32)
        carry = P.sb("carry", [128, 44, 4], F32)
        xo = [P.sb(f"xo{i}", [128, 512], F32) for i in range(2)]
        txs = P.sb("txs", [128, 4, 8, 4], F32)
        xTv = xmid.full().re("(kc p) n -> p kc n", p=128)
        dst = out if last else xb[l + 1]
        dv = dst.full().re("(kc p) n -> p kc n", p=128)
        tiles = []
        for m in range(4):
            tiles.append((m * SW, 4, True, m))
            tiles.append((m * SW + 4, 512, False, m))
        pcnt = [0]
        for (c0, w, is_halo, m) in tiles:
            P.dma("sync", out=xst[:, :, 0:w], in_=xTv[:, :, c0:c0 + w])
            ps = pb[7]
            for kc in range(8):
                P.act.activation(out=sq[:, 0:w], in_=xst[:, kc, 0:w], func=AF.Square)
                P.pe.matmul(out=ps[:, 0:w], lhsT=ones, rhs=sq[:, 0:w], start=(kc == 0), stop=(kc == 7))
            P.act.activation(out=rstd[:, 0:w], in_=ps[:, 0:w], func=AF.Sqrt, bias=eps[:, 0:1], scale=1.0 / 1024.0)
            P.dve.reciprocal(out=rstd[:, 0:w], in_=rstd[:, 0:w])
            for kc in range(8):
                P.dve.scalar_tensor_tensor(out=hn[:, kc, 0:w], in0=xst[:, kc, 0:w], scalar=C.col("g_ffn", kc),
                                           in1=rstd[:, 0:w], op0=ALU.mult, op1=ALU.mult)
            for i in range(22):
                accs = []
                for j, cg in enumerate((i, 22 + i)):
                    ps = pb[pcnt[0] % 4]
                    pcnt[0] += 1
                    for kc in range(8):
                        P.pe.matmul(out=ps[:, 0:w], lhsT=Wu[:, kc, cg * 128:(cg + 1) * 128], rhs=hn[:, kc, 0:w],
                                    start=(kc == 0), stop=(kc == 7))
                    if is_halo:
                        P.act.copy(out=carry[:, cg, :], in_=ps[:, 0:4])
                        continue
                    up = upre[j]
                    P.act.copy(out=up[:, 4:516], in_=ps.full())
                    P.dve.tensor_copy(out=up[:, 0:4], in_=carry[:, cg, :])
                    a0 = acc[j]
                    P.dve.tensor_scalar(out=a0.full(), in0=up[:, 4:516], scalar1=C.col("fw2", cg), scalar2=C.col("fb", cg),
                                        op0=ALU.mult, op1=ALU.add)
                    P.dve.scalar_tensor_tensor(out=a0.full(), in0=up[:, 3:515], scalar=C.col("fw1", cg), in1=a0.full(),
                                               op0=ALU.mult, op1=ALU.add)
                    P.dve.scalar_tensor_tensor(out=a0.full(), in0=up[:, 2:514], scalar=C.col("fw0", cg), in1=a0.full(),
                                               op0=ALU.mult, op1=ALU.add)
                    accs.append(a0)
                if is_halo:
                    continue
                P.act.activation(out=sg.full(), in_=accs[0].full(), func=AF.Silu)
                P.pool.tensor_tensor(out=act[:, i, :], in0=sg.full(), in1=accs[1].full(), op=ALU.mult)
            if is_halo:
                continue
            for oc in range(8):
                i2 = oc % 2
                ps = pb[4 + i2]
                for i in range(22):
                    P.pe.matmul(out=ps.full(), lhsT=Wd[:, i, oc * 128:(oc + 1) * 128], rhs=act[:, i, :],
                                start=(i == 0), stop=(i == 21))
                P.dve.tensor_tensor(out=xo[i2].full(), in0=ps.full(), in1=xst[:, oc, :], op=ALU.add)
                if last:
                    P.dma("sync" if i2 else "gpsimd", out=dv[:, oc, m * 512:(m + 1) * 512], in_=xo[i2].full())
                else:
                    P.pool.tensor_copy(out=txs[:, m, oc, :], in_=xo[i2][:, 508:512])
                    P.dma("sync" if i2 else "gpsimd", out=dv[:, oc, m * SW + 4:m * SW + 516], in_=xo[i2].full())
        if not last:
            write_tails(txs)

    def gather_e1():
        gather_pairs(list(zip(sx, sxg)) + [(fx, fxg)])

    nl = L if stop is None else stop[0]
    done = False
    for l in range(nl):
        last_l = (stop is not None and l == nl - 1)
        phase_A(l)
        P.emit(final=False)
        ssd_scan(l, load_consts(), pass1=True)
        P.emit(final=False)
        if last_l and stop[1] == "A":
            break
        gather_e1()
        phase_attn(l)
        P.emit(final=False)
        phase_ssd2(l)
        P.emit(final=False)
        if last_l and stop[1] == "B":
            break
        phase_merge(l)
        P.emit(final=False)
        halo_exchange(xmid)
        P.emit(final=False)
        if last_l and stop[1] == "C":
            break
        phase_ffn(l, last=(l == L - 1))
        P.emit(final=False)
        if l < L - 1:
            halo_exchange(xb[l + 1])
            P.emit(final=False)
    loc = {"kxmg0": kxmg[0], "vxg0": vxg[0], "sxg0": sxg[0], "fxg": fxg, "qm": qm, "qf": qf, "fq": fq, "omd": omd, "ofd": ofd, "yd": yd,
           "xmid": xmid, "xb1": xb[1], "szd": szd, "gd": gd, "xtm": xtm, "btm": btm, "bct": bct, "dtd": dtd, "atd": atd}
    for name in dbg:
        src = loc[name]
        shp = list(src.h.shape) if hasattr(src.h, "shape") else None
        dd = P.dram("dbg_" + name, shp, src.h.dtype, EO)
        P.dma("sync", out=dd.full(), in_=src.full())
    P.emit(final=True)
    return nc, P


def _stripe_tokens(p):
    return np.concatenate([np.arange((4 * m + p) * 512, (4 * m + p + 1) * 512) for m in range(4)])


def fused_in_maps(inp):
    L = 2
    cpsA = [a_colpack(inp, l) for l in range(L)]
    cpsD = [d2_colpack(inp, l) for l in range(L)]
    offA = dict(cpsA[0].off)
    offA["_n"] = cpsA[0].n
    offD = dict(cpsD[0].off)
    offD["_n"] = cpsD[0].n
    cstA = np.stack([c.array() for c in cpsA])
    cstD = np.stack([c.array() for c in cpsD])
    w_kp = np.zeros((L, 256, 8, 96), np.float32)
    wukv = inp["mla_w_ukv"].reshape(L, 256, 8, 128)
    w_kp[:, :, :, 0:64] = wukv[:, :, :, 0:64]
    w_v = np.ascontiguousarray(wukv[:, :, :, 64:128].reshape(L, 256, 512))
    gssm = np.ascontiguousarray(inp["ssm_norm_g"].reshape(L, 8, 128).transpose(0, 2, 1))
    rowc = np.stack([fused_rowpack(inp, l) for l in range(L)])
    msk, cm = _bc_consts()
    mats = _const_mats()
    shared = {
        "w_in": np.ascontiguousarray(inp["w_in"]), "w_uq": np.ascontiguousarray(inp["mla_w_uq"]),
        "w_kp": np.ascontiguousarray(w_kp.reshape(L, 256, 768)), "w_v": w_v,
        "w_a": np.ascontiguousarray(inp["w_br_mla"]), "w_b": np.ascontiguousarray(inp["w_br_fox"]),
        "w_c": np.ascontiguousarray(inp["w_br_ssm"]), "w_o": np.ascontiguousarray(inp["w_out"]),
        "w_up": np.ascontiguousarray(inp["ffn_w_up"]), "w_dn": np.ascontiguousarray(inp["ffn_w_down"]),
        "cstA": cstA, "cstD": cstD, "gssm": gssm, "rowc": rowc, "msk": msk, "cm": cm, "mats": mats,
    }
    in_maps = []
    for c in range(8):
        b, p = c // 4, c % 4
        xT = np.zeros((D, 4 * SW), np.float32)
        xbT = inp["x"][b].T
        for m in range(4):
            s_ = 4 * m + p
            xT[:, m * SW + 4:m * SW + 516] = xbT[:, s_ * 512:(s_ + 1) * 512]
            if s_ > 0:
                xT[:, m * SW:m * SW + 4] = xbT[:, s_ * 512 - 4:s_ * 512]
        sel = np.zeros((128, 32), np.float32)
        if p >= 1:
            sel[:, p - 1] = 1.0
        else:
            sel[:, 4] = 1.0
        for s_ in range(16):
            if s_ % 4 == p:
                sel[:, 8 + s_] = 1.0
        for jr in range(4):
            sel[:, 24 + jr] = 1.0 if jr == p else 0.0
            sel[:, 28 + jr] = NEG if jr > p else 0.0
        d = dict(shared)
        d["x0"] = np.ascontiguousarray(xT)
        d["pos"] = np.ascontiguousarray(inp["positions"][b][_stripe_tokens(p)][None, :]).astype(np.int32)
        d["sel"] = sel
        in_maps.append(d)
    return in_maps, offA, offD


def kernel_fused(**inp):
    inp = {k: np.asarray(v) for k, v in inp.items()}
    in_maps, offA, offD = fused_in_maps(inp)
    if "F" not in _PROG_CACHE:
        _PROG_CACHE["F"] = build_fused(offA, offD)[0]
    res = run_bass_kernel_spmd(_PROG_CACHE["F"], in_maps, core_ids=list(range(8))).results
    xo = np.zeros((2, S_, D), np.float32)
    for c in range(8):
        b, p = c // 4, c % 4
        xo[b, _stripe_tokens(p), :] = np.asarray(res[c]["out"]).T
    return xo
```

```python
from contextlib import ExitStack
import numpy as np
import concourse.bass as bass
import concourse.mybir as mybir

F32 = mybir.dt.float32
BF16 = mybir.dt.bfloat16
I32 = mybir.dt.int32
ALU = mybir.AluOpType
AF = mybir.ActivationFunctionType
AX = mybir.AxisListType

COMPUTE = ("tensor", "vector", "scalar", "gpsimd")
QUEUES = ("sync", "gpsimd", "scalar")
NRING = 8


class View:
    __slots__ = ("buf", "ap", "key")

    def __init__(self, buf, ap, key=None):
        self.buf = buf
        self.ap = ap
        self.key = key

    def __getitem__(self, k):
        return View(self.buf, self.ap[k], self.key)

    def re(self, s, **kw):
        return View(self.buf, self.ap.rearrange(s, **kw), self.key)

    def bc(self, shape):
        return View(self.buf, self.ap.to_broadcast(shape), self.key)

    def bitcast(self, dt):
        return View(self.buf, self.ap.bitcast(dt), self.key)

    def k(self, key):
        return View(self.buf, self.ap, key)

    def f(self, fn):
        return View(self.buf, fn(self.ap), self.key)


class Buf:
    def __init__(self, name, handle, is_dram=False):
        self.name = name
        self.h = handle
        self.is_dram = is_dram
        self.regions = {}

    def full(self):
        ap = self.h.ap() if hasattr(self.h, "ap") and callable(getattr(self.h, "ap")) else self.h[:]
        return View(self, ap)

    def __getitem__(self, k):
        return View(self, self.h[k])


class Op:
    __slots__ = ("id", "eng", "meth", "kw", "deps", "is_dma", "signaled", "sem", "val", "prewait", "eidx")


class Eng:
    def __init__(self, P, name):
        self.P = P
        self.name = name

    def __getattr__(self, meth):
        def call(*a, **kw):
            assert not a, "use kwargs"
            return self.P._record(self.name, meth, kw)
        return call


class Prog:
    def __init__(self, nc):
        self.nc = nc
        self.ops = []
        self.gstack = ExitStack()
        self.stack = ExitStack()
        self.pe = Eng(self, "tensor")
        self.dve = Eng(self, "vector")
        self.act = Eng(self, "scalar")
        self.pool = Eng(self, "gpsimd")
        self.sp = Eng(self, "sync")
        st = self.gstack
        self.csem = {e: st.enter_context(nc.semaphore(f"c_{e}")) for e in COMPUTE}
        self.rings = {q: [st.enter_context(nc.semaphore(f"d_{q}{i}")) for i in range(NRING)] for q in QUEUES}
        self.ccsem = st.enter_context(nc.semaphore("ccsem"))
        self.cccount = 0
        self.ccount = {e: 0 for e in COMPUTE}
        self.dcount = {q: 0 for q in QUEUES}
        self.waited = {e: {} for e in ("sync",) + COMPUTE}
        self.emitted = 0
        self.barrier = []
        self.stats = {}
        self.nwaits = 0

    def sb(self, name, shape, dtype):
        self.nuid = getattr(self, "nuid", 0) + 1
        name = f"{name}_s{self.nuid}"
        t = self.stack.enter_context(self.nc.sbuf_tensor(name, list(shape), dtype))
        return Buf(name, t)

    def ps(self, name, shape, dtype):
        self.nuid = getattr(self, "nuid", 0) + 1
        name = f"{name}_p{self.nuid}"
        t = self.stack.enter_context(self.nc.psum_tensor(name, list(shape), dtype))
        return Buf(name, t)

    def dram(self, name, shape, dtype, kind="Internal"):
        t = self.nc.dram_tensor(name, list(shape), dtype, kind=kind)
        return Buf(name, t, is_dram=True)

    def _record(self, eng, meth, kw):
        op = Op()
        op.id = len(self.ops)
        op.eng = eng
        op.meth = meth
        op.kw = kw
        op.is_dma = meth in ("dma_start", "dma_start_transpose", "collective_compute")
        op.signaled = False
        op.sem = None
        op.val = 0
        op.prewait = None
        deps = set()
        extra_r = kw.pop("_reads", [])
        extra_w = kw.pop("_writes", [])
        writes, reads = [], []
        for k, v in kw.items():
            vs = v if isinstance(v, (list, tuple)) else [v]
            for x in vs:
                if isinstance(x, View):
                    if k in ("out", "accum_out", "outs") or (k == "ap" and meth in ("memset", "memzero")):
                        writes.append(x)
                    else:
                        reads.append(x)
        reads += extra_r
        writes += extra_w
        for v in reads:
            self._gather(v, False, deps)
        for v in writes:
            self._gather(v, True, deps)
        for v in reads:
            self._update(v, False, op.id)
        for v in writes:
            self._update(v, True, op.id)
        deps.discard(op.id)
        op.deps = deps
        self.ops.append(op)
        return op

    def _gather(self, v, is_write, deps):
        R = v.buf.regions
        if v.key is None:
            regs = list(R.values())
        else:
            regs = [R[k] for k in (v.key, None) if k in R]
        for reg in regs:
            if reg[0] is not None:
                deps.add(reg[0])
            if is_write:
                deps.update(reg[1])

    def _update(self, v, is_write, oid):
        R = v.buf.regions
        if is_write:
            if v.key is None:
                R.clear()
            R[v.key] = [oid, []]
        else:
            R.setdefault(v.key, [None, []])[1].append(oid)

    def dma(self, q, out, in_, **kw):
        eng = {"sync": self.sp, "gpsimd": self.pool, "scalar": self.act}[q]
        return eng.dma_start(out=out, in_=in_, **kw)

    def emit(self, final=True):
        nc = self.nc
        ops = self.ops
        phase = ops[self.emitted:]
        first_id = self.emitted
        self.emitted = len(ops)
        for op in phase:
            for d in op.deps:
                dop = ops[d]
                if d < first_id:
                    continue
                if dop.eng == "tensor" and op.eng == "tensor" and not dop.is_dma and not op.is_dma:
                    continue
                dop.signaled = True
        per = {}
        for op in phase:
            per.setdefault(op.eng, []).append(op)
        for e, lst in per.items():
            for op in reversed(lst):
                if not op.is_dma:
                    op.signaled = True
                    break
        for op in phase:
            if op.meth == "collective_compute":
                self.cccount += 1
                op.sem = self.ccsem
                op.val = self.cccount
                op.signaled = True
            elif op.is_dma:
                k = self.dcount[op.eng]
                self.dcount[op.eng] += 1
                op.sem = self.rings[op.eng][k % NRING]
                op.val = 16 * (k // NRING + 1)
                if k >= NRING:
                    op.prewait = (op.sem, 16 * (k // NRING))
                op.signaled = True
            elif op.signaled:
                self.ccount[op.eng] += 1
                op.sem = self.csem[op.eng]
                op.val = self.ccount[op.eng]
        for e, v in per.items():
            self.stats[e] = self.stats.get(e, 0) + len(v)
        barrier_in = list(self.barrier)
        dcount = self.dcount
        rings = self.rings

        def dma_final_waits():
            ws = []
            for q in QUEUES:
                n = dcount[q]
                for i in range(min(n, NRING)):
                    cnt = (n - 1 - i) // NRING + 1
                    ws.append((rings[q][i], 16 * cnt))
            if self.cccount > 0:
                ws.append((self.ccsem, self.cccount))
            return ws

        def run(engname, e):
            waited = self.waited[engname]

            def do_waits(ws):
                for sem, val in ws:
                    key = id(sem)
                    if waited.get(key, 0) >= val:
                        continue
                    waited[key] = val
                    e.wait_ge(sem, val)
                    self.nwaits += 1

            do_waits(barrier_in)
            for op in per.get(engname, []):
                ws = []
                if op.prewait is not None:
                    ws.append(op.prewait)
                for d in sorted(op.deps):
                    dop = ops[d]
                    if dop.sem is None:
                        continue
                    if dop.eng == "tensor" and op.eng == "tensor" and not dop.is_dma and not op.is_dma:
                        continue
                    ws.append((dop.sem, dop.val))
                do_waits(ws)
                kw = {}
                for k, v in op.kw.items():
                    if isinstance(v, View):
                        kw[k] = v.ap
                    elif isinstance(v, (list, tuple)) and v and isinstance(v[0], View):
                        kw[k] = [x.ap for x in v]
                    else:
                        kw[k] = v
                ins = getattr(e, op.meth)(**kw)
                if op.signaled:
                    ins.then_inc(op.sem, 16 if (op.is_dma and op.meth != "collective_compute") else 1)
            if final and engname == "sync":
                do_waits(dma_final_waits())

        with nc.Block() as block:
            @block.sync
            def _(e):
                run("sync", e)

            @block.tensor
            def _(e):
                run("tensor", e)

            @block.vector
            def _(e):
                run("vector", e)

            @block.scalar
            def _(e):
                run("scalar", e)

            @block.gpsimd
            def _(e):
                run("gpsimd", e)
        bar = dma_final_waits()
        for e in COMPUTE:
            if self.ccount[e] > 0:
                bar.append((self.csem[e], self.ccount[e]))
        self.barrier = bar
        self.stats["waits"] = self.nwaits
        self.stack.close()
        self.stack = ExitStack()
        if final:
            self.gstack.close()


from concourse.bass_utils import run_bass_kernel_spmd
import ml_dtypes

NBF = ml_dtypes.bfloat16
D = 1024
T = 2048
HALO = 4
NEG = -30000.0


class ColPack:
    def __init__(self):
        self.cols = []
        self.off = {}
        self.n = 0

    def add(self, name, vec, rows=128):
        vec = np.asarray(vec, np.float32).reshape(-1)
        assert vec.size % rows == 0
        m = vec.reshape(-1, rows).T
        a = np.zeros((128, m.shape[1]), np.float32)
        a[:rows] = m
        self.off[name] = (self.n, m.shape[1], rows)
        self.cols.append(a)
        self.n += m.shape[1]

    def array(self):
        return np.ascontiguousarray(np.concatenate(self.cols, axis=1))


class Cst:
    def __init__(self, P, buf, off):
        self.buf = buf
        self.off = off

    def col(self, name, j=0, rows=None):
        o, n, r = self.off[name]
        r = rows or r
        return self.buf[0:r, o + j:o + j + 1]

    def cols(self, name):
        o, n, r = self.off[name]
        return self.buf[0:r, o:o + n]


def new_nc():
    return bass.Bass("TRN2", target_bir_lowering=False)


def load_cast(P, q, dram_view, stage_view, bf_view, cast_eng):
    P.dma(q, out=stage_view, in_=dram_view)
    cast_eng.tensor_copy(out=bf_view, in_=stage_view)


A_OFF = None


def a_colpack(inp, l):
    cp = ColPack()
    cp.add("g_mix", inp["norm_mix_g"][l])
    cp.add("g_cq", inp["mla_q_norm_g"][l])
    cp.add("g_ckv", inp["mla_kv_norm_g"][l])
    cp.add("g_q", inp["mla_q_gain"][l], 96)
    cp.add("g_k", inp["mla_k_gain"][l], 96)
    cp.add("g_fq", inp["fox_q_gain"][l], 64)
    cp.add("g_fk", inp["fox_k_gain"][l], 64)
    cp.add("b_f", inp["fox_b_f"][l], 8)
    cw = inp["ssm_conv_w"][l]
    for k in range(4):
        cp.add(f"cw{k}", cw[k])
    cp.add("cb", inp["ssm_conv_b"][l])
    cp.add("dt_b", inp["ssm_dt_bias"][l], 16)
    cp.add("A_log", inp["ssm_A_log"][l], 16)
    cp.add("b_gate", inp["b_gate"][l])
    inv = 1.0 / (10000.0 ** (np.arange(0, 32, 2, dtype=np.float32) / 32.0))
    invf = np.zeros(96, np.float32)
    invf[64:80] = inv
    invf[80:96] = inv
    cp.add("invf", invf, 96)
    return cp


def build_A(off):
    nc = new_nc()
    P = Prog(nc)
    TT = T + HALO
    NT = T // 512
    EI, EO = "ExternalInput", "ExternalOutput"
    xT = P.dram("xT", [D, TT], F32, EI)
    pos = P.dram("pos", [1, T], I32, EI)
    w_in = P.dram("w_in", [D, 7864], F32, EI)
    w_uq = P.dram("w_uq", [384, 768], F32, EI)
    w_kp = P.dram("w_kp", [256, 768], F32, EI)
    w_v = P.dram("w_v", [256, 512], F32, EI)
    cst_d = P.dram("cst", [128, off["_n"]], F32, EI)
    mats = P.dram("mats", [128, 2 * 96], F32, EI)
    o_qm = P.dram("o_qm", [8, 96, T], BF16, EO)
    o_km = P.dram("o_km", [8, 96, T], BF16, EO)
    o_vm = P.dram("o_vm", [512, T], BF16, EO)
    o_qf = P.dram("o_qf", [8, 64, T], BF16, EO)
    o_kf = P.dram("o_kf", [8, 64, T], BF16, EO)
    o_vf = P.dram("o_vf", [512, T], BF16, EO)
    o_lf = P.dram("o_lf", [8, T], F32, EO)
    o_sz = P.dram("o_sz", [1024, T], BF16, EO)
    o_xbc = P.dram("o_xbc", [1536, T], BF16, EO)
    o_dt = P.dram("o_dt", [16, T], F32, EO)
    o_a = P.dram("o_a", [16, T], F32, EO)
    o_g = P.dram("o_g", [3072, T], BF16, EO)

    cstb = P.sb("cstb", [128, off["_n"]], F32)
    C = Cst(P, cstb, off)
    P.dma("sync", out=cstb.full(), in_=cst_d.full())
    matf = P.sb("matf", [128, 192], F32)
    matb = P.sb("matb", [128, 192], BF16)
    P.dma("sync", out=matf.full(), in_=mats.full())
    P.dve.tensor_copy(out=matb.full(), in_=matf.full())
    prh = matb[0:96, 0:96]
    sel = matb[0:32, 96:192]
    ones = P.sb("ones", [128, 128], F32)
    P.dve.memset(ap=ones.full(), constant=1.0)
    eps = P.sb("eps", [128, 1], F32)
    P.dve.memset(ap=eps.full(), constant=1e-6)
    one1 = P.sb("one1", [128, 1], F32)
    P.dve.memset(ap=one1.full(), constant=1.0)
    nbf = P.sb("nbf", [8, 1], F32)
    P.dve.tensor_scalar(out=nbf.full(), in0=C.col("b_f"), scalar1=-1.0, scalar2=None, op0=ALU.mult)
    Aneg = P.sb("Aneg", [16, 1], F32)
    P.act.activation(out=Aneg.full(), in_=C.col("A_log"), func=AF.Exp)
    P.dve.tensor_scalar(out=Aneg.full(), in0=Aneg.full(), scalar1=-1.0, scalar2=None, op0=ALU.mult)

    pb = [P.ps(f"pb{i}", [128, 512], F32) for i in range(8)]
    pbi = {}

    def nxt_ps(lo=0, hi=4):
        i = pbi.get(lo, 0)
        pbi[lo] = (i + 1) % (hi - lo)
        return pb[lo + i]

    Ctab = P.sb("Ctab", [96, T], F32)
    Stab = P.sb("Stab", [96, T], F32)
    posi = P.sb("posi", [96, 512], I32)
    posf = P.sb("posf", [96, 512], F32)
    rr_tmp = P.sb("rr_tmp", [96, 512], F32)
    rr_i = P.sb("rr_i", [96, 512], I32)
    rr_m = P.sb("rr_m", [96, 512], F32)

    def sin_table(outv, phase):
        P.dve.tensor_scalar(out=rr_tmp.full(), in0=posf.full(), scalar1=C.col("invf"), scalar2=phase,
                            op0=ALU.mult, op1=ALU.add)
        P.dve.tensor_scalar(out=rr_m.full(), in0=rr_tmp.full(), scalar1=1.0 / (2 * np.pi), scalar2=None, op0=ALU.mult)
        P.dve.tensor_copy(out=rr_i.full(), in_=rr_m.full())
        P.dve.tensor_copy(out=rr_m.full(), in_=rr_i.full())
        P.dve.scalar_tensor_tensor(out=rr_tmp.full(), in0=rr_m.full(), scalar=-2 * np.pi, in1=rr_tmp.full(),
                                   op0=ALU.mult, op1=ALU.add)
        P.dve.tensor_scalar(out=rr_m.full(), in0=rr_tmp.full(), scalar1=np.pi, scalar2=-2 * np.pi, op0=ALU.is_gt, op1=ALU.mult)
        P.dve.tensor_tensor(out=rr_tmp.full(), in0=rr_tmp.full(), in1=rr_m.full(), op=ALU.add)
        P.dve.tensor_scalar(out=rr_m.full(), in0=rr_tmp.full(), scalar1=-np.pi, scalar2=2 * np.pi, op0=ALU.is_lt, op1=ALU.mult)
        P.dve.tensor_tensor(out=rr_tmp.full(), in0=rr_tmp.full(), in1=rr_m.full(), op=ALU.add)
        P.act.activation(out=outv, in_=rr_tmp.full(), func=AF.Sin)

    for i in range(NT):
        P.dma("sync", out=posi.full(), in_=pos[:, i * 512:(i + 1) * 512].f(lambda a: a.partition_broadcast(96)))
        P.dve.tensor_copy(out=posf.full(), in_=posi.full())
        sin_table(Stab[:, i * 512:(i + 1) * 512], 0.0)
        sin_table(Ctab[:, i * 512:(i + 1) * 512], np.pi / 2)
    P.dve.memset(ap=Stab[0:64, :], constant=0.0)
    P.dve.memset(ap=Ctab[0:64, :], constant=1.0)

    hn = P.sb("hn", [128, 8, TT], BF16)
    xst = P.sb("xst", [128, 8, 512], F32)
    sq = P.sb("sq", [128, 512], F32)
    rstd = P.sb("rstd", [128, 512], F32)
    xTv = xT.full().re("(kc p) n -> p kc n", p=128)

    def rstd_from(ps_view, n_feat, rows, width, rstd_view):
        P.act.activation(out=rstd_view, in_=ps_view, func=AF.Sqrt, bias=eps[0:rows, 0:1], scale=1.0 / n_feat)
        P.dve.reciprocal(out=rstd_view, in_=rstd_view)

    tiles = [(0, HALO)] + [(HALO + i * 512, 512) for i in range(NT)]
    for (c0, w) in tiles:
        P.dma("sync", out=xst[:, :, 0:w], in_=xTv[:, :, c0:c0 + w])
        ps = nxt_ps(4, 6)
        for kc in range(8):
            P.act.activation(out=sq[:, 0:w], in_=xst[:, kc, 0:w], func=AF.Square)
            P.pe.matmul(out=ps[:, 0:w], lhsT=ones.full(), rhs=sq[:, 0:w], start=(kc == 0), stop=(kc == 7))
        rstd_from(ps[:, 0:w], 1024.0, 128, w, rstd[:, 0:w])
        for kc in range(8):
            P.dve.scalar_tensor_tensor(out=hn[:, kc, c0:c0 + w], in0=xst[:, kc, 0:w], scalar=C.col("g_mix", kc),
                                       in1=rstd[:, 0:w], op0=ALU.mult, op1=ALU.mult)

    wst = [P.sb(f"wst{i}", [128, 8, 512], F32) for i in range(2)]
    wbf = [P.sb(f"wbf{i}", [128, 8, 512], BF16) for i in range(2)]
    wcnt = [0]
    w_inv = w_in.full().re("(kc p) n -> p kc n", p=128)

    def load_w(c0, ncols):
        i = wcnt[0] % 2
        wcnt[0] += 1
        q = "sync" if i == 0 else "gpsimd"
        P.dma(q, out=wst[i][:, :, 0:ncols], in_=w_inv[:, :, c0:c0 + ncols])
        P.pool.tensor_copy(out=wbf[i][:, :, 0:ncols], in_=wst[i][:, :, 0:ncols])
        return wbf[i]

    def proj(wb, wc0, m, c0, w, ps_view):
        for kc in range(8):
            P.pe.matmul(out=ps_view, lhsT=wb[:, kc, wc0:wc0 + m], rhs=hn[:, kc, c0:c0 + w],
                        start=(kc == 0), stop=(kc == 7))

    ostg_cnt = [0]
    ostg = [P.sb(f"ostg{i}", [128, 512], BF16) for i in range(4)]

    def next_ostg():
        i = ostg_cnt[0] % 4
        ostg_cnt[0] += 1
        return ostg[i]

    def out_dma(dst_view, src_view):
        q = "sync" if ostg_cnt[0] % 2 else "gpsimd"
        P.dma(q, out=dst_view, in_=src_view)

    hraw = P.sb("hraw", [96, 512], F32)
    hsq = P.sb("hsq", [96, 512], F32)
    hrs = P.sb("hrs", [96, 512], F32)
    hnf = P.sb("hnf", [96, 512], F32)
    hnb = P.sb("hnb", [96, 512], BF16)
    ht1 = P.sb("ht1", [96, 512], F32)
    ht2 = P.sb("ht2", [96, 512], F32)

    def headnorm(ps_view, d, gain_col, rope, tok0, dst_view):
        P.act.activation(out=hsq[0:d, :], in_=ps_view, func=AF.Square)
        P.act.copy(out=hraw[0:d, :], in_=ps_view)
        ps2 = nxt_ps(4, 6)
        P.pe.matmul(out=ps2[0:d, :], lhsT=ones[0:d, 0:d], rhs=hsq[0:d, :], start=True, stop=True)
        rstd_from(ps2[0:d, :], float(d), d, 512, hrs[0:d, :])
        og = next_ostg()
        if not rope:
            P.dve.scalar_tensor_tensor(out=og[0:d, :], in0=hraw[0:d, :], scalar=gain_col, in1=hrs[0:d, :],
                                       op0=ALU.mult, op1=ALU.mult)
        else:
            P.dve.scalar_tensor_tensor(out=hnf[0:d, :], in0=hraw[0:d, :], scalar=gain_col, in1=hrs[0:d, :],
                                       op0=ALU.mult, op1=ALU.mult)
            P.act.copy(out=hnb[0:d, :], in_=hnf[0:d, :])
            ps3 = nxt_ps(6, 8)
            P.pe.matmul(out=ps3[0:d, :], lhsT=prh, rhs=hnb[0:d, :], start=True, stop=True)
            P.dve.tensor_tensor(out=ht1[0:d, :], in0=hnf[0:d, :], in1=Ctab[0:d, tok0:tok0 + 512], op=ALU.mult)
            P.dve.tensor_tensor(out=ht2[0:d, :], in0=ps3[0:d, :], in1=Stab[0:d, tok0:tok0 + 512], op=ALU.mult)
            P.pool.tensor_tensor(out=og[0:d, :], in0=ht1[0:d, :], in1=ht2[0:d, :], op=ALU.add)
        out_dma(dst_view, og[0:d, :])

    lat = P.sb("lat", [128, 3, 512], F32)
    latn = P.sb("latn", [128, 3, 512], BF16)

    def latent_norm(ps_list, gname):
        nch = len(ps_list)
        ps2 = nxt_ps(4, 6)
        for i, psv in enumerate(ps_list):
            P.act.activation(out=sq.full(), in_=psv, func=AF.Square)
            P.act.copy(out=lat[:, i, :], in_=psv)
            P.pe.matmul(out=ps2.full(), lhsT=ones.full(), rhs=sq.full(), start=(i == 0), stop=(i == nch - 1))
        rstd_from(ps2.full(), 128.0 * nch, 128, 512, rstd.full())
        for i in range(nch):
            P.dve.scalar_tensor_tensor(out=latn[:, i, :], in0=lat[:, i, :], scalar=C.col(gname, i), in1=rstd.full(),
                                       op0=ALU.mult, op1=ALU.mult)

    def small_w(name, dram, kc_n, ncols, i):
        stg = wst[i].full().re("p a b -> p (a b)")[:, 0:kc_n * ncols].re("p (a b) -> p a b", a=kc_n)
        bfb = P.sb(name, [128, kc_n, ncols], BF16)
        P.dma("gpsimd", out=stg, in_=dram.full().re("(kc p) n -> p kc n", p=128))
        P.pool.tensor_copy(out=bfb.full(), in_=stg)
        return bfb

    uqb = small_w("uqb", w_uq, 3, 768, 0)
    kpb = small_w("kpb", w_kp, 2, 768, 1)
    wvb = small_w("wvb", w_v, 2, 512, 0)
    main = tiles[1:]
    wb = load_w(0, 384)
    for ti, (c0, w) in enumerate(main):
        pss = []
        for ch in range(3):
            ps = nxt_ps(0, 4)
            proj(wb, ch * 128, 128, c0, 512, ps.full())
            pss.append(ps.full())
        latent_norm(pss, "g_cq")
        for h in range(8):
            ps = nxt_ps(0, 4)
            for kc in range(3):
                P.pe.matmul(out=ps[0:96, :], lhsT=uqb[:, kc, h * 96:(h + 1) * 96], rhs=latn[:, kc, :],
                            start=(kc == 0), stop=(kc == 2))
            headnorm(ps[0:96, :], 96, C.col("g_q"), True, ti * 512, o_qm[h, :, ti * 512:(ti + 1) * 512])
    wb = load_w(384, 288)
    krb = P.sb("krb", [32, 512], BF16)
    for ti, (c0, w) in enumerate(main):
        pss = []
        for ch in range(2):
            ps = nxt_ps(0, 4)
            proj(wb, ch * 128, 128, c0, 512, ps.full())
            pss.append(ps.full())
        ps = nxt_ps(0, 4)
        proj(wb, 256, 32, c0, 512, ps[0:32, :])
        P.act.copy(out=krb.full(), in_=ps[0:32, :])
        latent_norm(pss, "g_ckv")
        for h in range(8):
            ps = nxt_ps(0, 4)
            for kc in range(2):
                P.pe.matmul(out=ps[0:96, :], lhsT=kpb[:, kc, h * 96:(h + 1) * 96], rhs=latn[:, kc, :],
                            start=(kc == 0), stop=False)
            P.pe.matmul(out=ps[0:96, :], lhsT=sel, rhs=krb.full(), start=False, stop=True)
            headnorm(ps[0:96, :], 96, C.col("g_k"), True, ti * 512, o_km[h, :, ti * 512:(ti + 1) * 512])
        for ch in range(4):
            ps = nxt_ps(0, 4)
            for kc in range(2):
                P.pe.matmul(out=ps.full(), lhsT=wvb[:, kc, ch * 128:(ch + 1) * 128], rhs=latn[:, kc, :],
                            start=(kc == 0), stop=(kc == 1))
            og = next_ostg()
            P.act.copy(out=og.full(), in_=ps.full())
            out_dma(o_vm[ch * 128:(ch + 1) * 128, ti * 512:(ti + 1) * 512], og.full())
    for (base, gname, dst) in ((672, "g_fq", o_qf), (672 + 512, "g_fk", o_kf)):
        wb = load_w(base, 512)
        for ti, (c0, w) in enumerate(main):
            for h in range(8):
                ps = nxt_ps(0, 4)
                proj(wb, h * 64, 64, c0, 512, ps[0:64, :])
                headnorm(ps[0:64, :], 64, C.col(gname), False, ti * 512, dst[h, :, ti * 512:(ti + 1) * 512])
    def plain_group(base, ncols, func, bias_name, dst, dst_row0):
        wb = load_w(base, ncols)
        for ti, (c0, w) in enumerate(main):
            for ch in range(ncols // 128):
                ps = nxt_ps(0, 4)
                proj(wb, ch * 128, 128, c0, 512, ps.full())
                og = next_ostg()
                if bias_name is None:
                    P.act.activation(out=og.full(), in_=ps.full(), func=func)
                else:
                    P.act.activation(out=og.full(), in_=ps.full(), func=func,
                                     bias=C.col(bias_name, (dst_row0 // 128) + ch))
                out_dma(dst[dst_row0 + ch * 128:dst_row0 + (ch + 1) * 128, ti * 512:(ti + 1) * 512], og.full())

    plain_group(672 + 1024, 512, AF.Copy, None, o_vf, 0)
    FB = 672 + 1536
    SB = 672 + 1544
    wf = load_w(FB, 8)
    lf1 = P.sb("lf1", [16, 512], F32)
    lf2 = P.sb("lf2", [16, 512], F32)
    for ti, (c0, w) in enumerate(main):
        ps = nxt_ps(0, 4)
        proj(wf, 0, 8, c0, 512, ps[0:8, :])
        P.act.activation(out=lf1[0:8, :], in_=ps[0:8, :], func=AF.Exp, bias=nbf[0:8, 0:1], scale=-1.0)
        P.act.activation(out=lf1[0:8, :], in_=lf1[0:8, :], func=AF.Ln, bias=one1[0:8, 0:1], scale=1.0)
        P.dve.tensor_scalar(out=lf2[0:8, :], in0=lf1[0:8, :], scalar1=-1.0, scalar2=None, op0=ALU.mult)
        P.dma("sync", out=o_lf[:, ti * 512:(ti + 1) * 512], in_=lf2[0:8, :])
    wd = load_w(SB + 1024 + 1536, 16)
    dt1 = P.sb("dt1", [16, 512], F32)
    dt2 = P.sb("dt2", [16, 512], F32)
    for ti, (c0, w) in enumerate(main):
        ps = nxt_ps(0, 4)
        proj(wd, 0, 16, c0, 512, ps[0:16, :])
        P.act.activation(out=dt1.full(), in_=ps[0:16, :], func=AF.Exp, bias=C.col("dt_b"), scale=1.0)
        P.act.activation(out=dt1.full(), in_=dt1.full(), func=AF.Ln, bias=one1[0:16, 0:1], scale=1.0)
        P.dma("sync", out=o_dt[:, ti * 512:(ti + 1) * 512], in_=dt1.full())
        P.dve.tensor_scalar(out=dt2.full(), in0=dt1.full(), scalar1=Aneg[:, 0:1], scalar2=None, op0=ALU.mult)
        P.dma("sync", out=o_a[:, ti * 512:(ti + 1) * 512], in_=dt2.full())
    for blk in range(2):
        plain_group(SB + blk * 512, 512, AF.Silu, None, o_sz, blk * 512)
    upre = P.sb("upre", [128, 516], F32)
    carry = P.sb("carry", [128, 12, 4], F32)
    acc = [P.sb(f"acc{i}", [128, 512], F32) for i in range(2)]
    for blk in range(3):
        wb = load_w(SB + 1024 + blk * 512, 512)
        for ch in range(4):
            cg = blk * 4 + ch
            ps = nxt_ps(0, 4)
            proj(wb, ch * 128, 128, 0, HALO, ps[:, 0:HALO])
            P.act.copy(out=carry[:, cg, :], in_=ps[:, 0:HALO])
        for ti, (c0, w) in enumerate(main):
            for ch in range(4):
                cg = blk * 4 + ch
                ps = nxt_ps(0, 4)
                proj(wb, ch * 128, 128, c0, 512, ps.full())
                P.act.copy(out=upre[:, 4:516], in_=ps.full())
                P.dve.tensor_copy(out=upre[:, 0:4], in_=carry[:, cg, :])
                P.pool.tensor_copy(out=carry[:, cg, :], in_=upre[:, 512:516])
                a0 = acc[0]
                P.dve.tensor_scalar(out=a0.full(), in0=upre[:, 4:516], scalar1=C.col("cw3", cg), scalar2=C.col("cb", cg),
                                    op0=ALU.mult, op1=ALU.add)
                for k in range(3):
                    P.dve.scalar_tensor_tensor(out=a0.full(), in0=upre[:, 1 + k:513 + k], scalar=C.col(f"cw{k}", cg),
                                               in1=a0.full(), op0=ALU.mult, op1=ALU.add)
                og = next_ostg()
                P.act.activation(out=og.full(), in_=a0.full(), func=AF.Silu)
                out_dma(o_xbc[cg * 128:(cg + 1) * 128, ti * 512:(ti + 1) * 512], og.full())
    GB = SB + 2576
    for blk in range(6):
        plain_group(GB + blk * 512, 512, AF.Sigmoid, "b_gate", o_g, blk * 512)
    P.emit()
    return nc, P


def _bf(a):
    return np.asarray(a).astype(np.float32)


_PROG_CACHE = {}


def _const_mats():
    m = np.zeros((128, 192), np.float32)
    for i in range(16):
        m[80 + i, 64 + i] = -1.0
        m[64 + i, 80 + i] = 1.0
    for i in range(32):
        m[i, 96 + 64 + i] = 1.0
    return m


def run_A(inp, l, x_full, pos_full):
    cp = a_colpack(inp, l)
    off = dict(cp.off)
    off["_n"] = cp.n
    if "A" not in _PROG_CACHE:
        _PROG_CACHE["A"] = build_A(off)[0]
    nc = _PROG_CACHE["A"]
    cst = cp.array()
    wukv = inp["mla_w_ukv"][l].reshape(256, 8, 128)
    w_kp = np.zeros((256, 8, 96), np.float32)
    w_kp[:, :, 0:64] = wukv[:, :, 0:64]
    w_v = np.ascontiguousarray(wukv[:, :, 64:128].reshape(256, 512))
    mats = _const_mats()
    xf = x_full.reshape(16384, D)
    in_maps = []
    for c in range(8):
        t0 = c * T
        xt = np.zeros((D, T + HALO), np.float32)
        xt[:, HALO:] = xf[t0:t0 + T].T
        if c % 4 != 0:
            xt[:, 0:HALO] = xf[t0 - HALO:t0].T
        in_maps.append({
            "xT": np.ascontiguousarray(xt),
            "pos": np.ascontiguousarray(pos_full.reshape(1, 16384)[:, t0:t0 + T]).astype(np.int32),
            "w_in": np.ascontiguousarray(inp["w_in"][l]),
            "w_uq": np.ascontiguousarray(inp["mla_w_uq"][l]),
            "w_kp": np.ascontiguousarray(w_kp.reshape(256, 768)),
            "w_v": w_v, "cst": cst, "mats": mats,
        })
    res = run_bass_kernel_spmd(nc, in_maps, core_ids=list(range(8)))
    return res.results


S_ = 8192
NKT = S_ // 128
NQT = S_ // 512


def build_BC():
    nc = new_nc()
    P = Prog(nc)
    EI, EO = "ExternalInput", "ExternalOutput"
    qm = P.dram("qm", [2, 96, S_], BF16, EI)
    km = P.dram("km", [2, 96, S_], BF16, EI)
    vm = P.dram("vm", [2, 128, NKT, 64], BF16, EI)
    qf = P.dram("qf", [2, 64, S_], BF16, EI)
    kf = P.dram("kf", [2, 64, S_], BF16, EI)
    vf = P.dram("vf", [2, 128, NKT, 64], BF16, EI)
    lf = P.dram("lf", [2, 128, NKT], F32, EI)
    msk = P.dram("msk", [128, 8, 512], F32, EI)
    cm = P.dram("cm", [128, 4, 128], F32, EI)
    x_tm = P.dram("x_tm", [128, NKT, 256], BF16, EI)
    B_tm = P.dram("B_tm", [128, NKT, 128], BF16, EI)
    BT = P.dram("BT", [128, S_], BF16, EI)
    CT = P.dram("CT", [128, S_], BF16, EI)
    dt_tm = P.dram("dt_tm", [128, NKT, 4], F32, EI)
    a_tm = P.dram("a_tm", [128, NKT, 4], F32, EI)
    Dv = P.dram("Dv", [128, 4], F32, EI)
    o_m = P.dram("o_m", [2, 64, S_], BF16, EO)
    o_f = P.dram("o_f", [2, 64, S_], BF16, EO)
    o_y = P.dram("o_y", [128, NKT, 256], F32, EO)
    fsc = P.dram("fsc", [3, S_], BF16)

    cmb = P.sb("cmb", [128, 4, 128], F32)
    P.dma("sync", out=cmb.full(), in_=cm.full())
    tri, trimask, ident, ones = cmb[:, 0, :], cmb[:, 1, :], cmb[:, 2, :], cmb[:, 3, :]
    mskb = P.sb("mskb", [128, 8, 512], F32)
    P.dma("gpsimd", out=mskb.full(), in_=msk.full())
    zero = P.sb("zero", [128, 1], F32)
    P.dve.memset(ap=zero.full(), constant=0.0)

    pb = [P.ps(f"pb{i}", [128, 512], F32) for i in range(8)]
    K_sb = P.sb("K_sb", [128, S_], BF16)
    Q_sb = P.sb("Q_sb", [128, S_], BF16)
    V_sb = P.sb("V_sb", [128, NKT, 128], BF16)
    P.dve.memset(ap=V_sb[:, :, 64:128], constant=1.0)
    pt = [P.sb(f"pt{i}", [128, 512], BF16) for i in range(3)]
    mt = [P.sb(f"mt{i}", [128, 512], F32) for i in range(2)]
    rl = P.sb("rl", [128, 512], F32)
    rl2 = P.sb("rl2", [64, 512], F32)
    ot = [P.sb(f"ot{i}", [64, 512], BF16) for i in range(2)]
    negF = P.sb("negF", [128, NKT], F32)

    cnt = [0, 0, 0]

    def attention(dk, scale, mask0, bias_fn, out_dram_h):
        for qt in range(NQT):
            oacc = pb[3 + qt % 2]
            nk = 4 * qt + 4
            for kt in range(nk):
                i3 = cnt[0] % 3
                cnt[0] += 1
                ps = pb[i3]
                P.pe.matmul(out=ps.full(), lhsT=K_sb[0:dk, kt * 128:(kt + 1) * 128],
                            rhs=Q_sb[0:dk, qt * 512:(qt + 1) * 512], start=True, stop=True)
                if kt >= 4 * qt:
                    m = mt[cnt[1] % 2]
                    cnt[1] += 1
                    P.dve.tensor_tensor(out=m.full(), in0=ps.full(), in1=mskb[:, mask0 + kt - 4 * qt, :], op=ALU.add)
                    src = m.full()
                else:
                    src = ps.full()
                P.act.activation(out=pt[i3].full(), in_=src, func=AF.Exp, scale=scale, bias=bias_fn(kt))
                P.pe.matmul(out=oacc.full(), lhsT=V_sb[:, kt, :], rhs=pt[i3].full(), start=(kt == 0), stop=(kt == nk - 1))
            P.dve.reciprocal(out=rl[64:128, :], in_=oacc[64:128, :])
            P.dve.tensor_copy(out=rl2.full(), in_=rl[64:128, :])
            o = ot[qt % 2]
            P.dve.tensor_tensor(out=o.full(), in0=oacc[0:64, :], in1=rl2.full(), op=ALU.mult)
            P.dma("sync", out=out_dram_h[:, qt * 512:(qt + 1) * 512], in_=o.full())

    for h in range(2):
        P.dma("sync", out=K_sb[0:96, :], in_=km[h])
        P.dma("gpsimd", out=Q_sb[0:96, :], in_=qm[h])
        P.dma("sync", out=V_sb[:, :, 0:64], in_=vm[h])
        attention(96, 96.0 ** -0.5, 0, lambda kt: zero[:, 0:1], o_m[h])

    lfs = P.sb("lfs", [128, NKT], F32)
    wi = P.sb("wi", [128, NKT], F32)
    sc = [P.sb(f"sc{i}", [128, NKT], F32) for i in range(2)]
    Ff = P.sb("Ff", [128, NKT], F32)
    FT = P.sb("FT", [64, 128], F32)
    r1 = P.sb("r1", [64, 128], F32)
    fh = [P.sb(f"fh{i}", [64, 128], BF16) for i in range(3)]
    for h in range(2):
        P.dma("sync", out=lfs.full(), in_=lf[h])
        ps = pb[5]
        P.pe.matmul(out=ps[:, 0:NKT], lhsT=tri, rhs=lfs.full(), start=True, stop=True)
        P.act.copy(out=wi.full(), in_=ps[:, 0:NKT])
        ps = pb[6]
        P.pe.matmul(out=ps[:, 0:NKT], lhsT=ones, rhs=lfs.full(), start=True, stop=True)
        P.act.copy(out=sc[0].full(), in_=ps[:, 0:NKT])
        P.dve.tensor_tensor(out=wi.full(), in0=wi.full(), in1=sc[0].full(), op=ALU.subtract)
        cur = 0
        d = 1
        while d < NKT:
            nx = 1 - cur
            P.dve.tensor_copy(out=sc[nx][:, 0:d], in_=sc[cur][:, 0:d])
            P.dve.tensor_tensor(out=sc[nx][:, d:NKT], in0=sc[cur][:, d:NKT], in1=sc[cur][:, 0:NKT - d], op=ALU.add)
            cur = nx
            d *= 2
        P.dve.tensor_tensor(out=Ff.full(), in0=wi.full(), in1=sc[cur].full(), op=ALU.add)
        P.dve.tensor_scalar(out=negF.full(), in0=Ff.full(), scalar1=-1.0, scalar2=None, op0=ALU.mult)
        ps = pb[7]
        P.pe.transpose(out=ps[0:64, 0:128], in_=Ff.full(), identity=ident)
        P.act.copy(out=FT.full(), in_=ps[0:64, 0:128])
        P.dve.tensor_copy(out=fh[0].full(), in_=FT.full())
        P.dve.tensor_tensor(out=r1.full(), in0=FT.full(), in1=fh[0].full(), op=ALU.subtract)
        P.dve.tensor_copy(out=fh[1].full(), in_=r1.full())
        P.dve.tensor_tensor(out=r1.full(), in0=r1.full(), in1=fh[1].full(), op=ALU.subtract)
        P.dve.tensor_copy(out=fh[2].full(), in_=r1.full())
        for r in range(3):
            P.dma("sync", out=fsc[r].re("(kt p) -> kt p", p=128), in_=fh[r].full())
        P.dma("sync", out=K_sb[0:64, :], in_=kf[h])
        P.dve.memset(ap=K_sb[64:67, :], constant=8.0)
        P.dma("gpsimd", out=Q_sb[0:64, :], in_=qf[h])
        P.dma("gpsimd", out=Q_sb[64:67, :], in_=fsc.full())
        P.dma("sync", out=V_sb[:, :, 0:64], in_=vf[h])
        attention(67, 0.125, 4, lambda kt: negF[:, kt:kt + 1], o_f[h])

    a_sb = P.sb("a_sb", [128, NKT, 4], F32)
    dt_sb = P.sb("dt_sb", [128, NKT, 4], F32)
    Dsb = P.sb("Dsb", [128, 4], F32)
    P.dma("sync", out=a_sb.full(), in_=a_tm.full())
    P.dma("sync", out=dt_sb.full(), in_=dt_tm.full())
    P.dma("sync", out=Dsb.full(), in_=Dv.full())
    BTs = K_sb
    CTs = Q_sb
    P.dma("sync", out=BTs.full(), in_=BT.full())
    P.dma("gpsimd", out=CTs.full(), in_=CT.full())
    Acum = P.sb("Acum", [128, NKT, 4], F32)
    nAcum = P.sb("nAcum", [128, NKT, 4], F32)
    Atot = P.sb("Atot", [128, NKT, 4], F32)
    eA = P.sb("eA", [128, NKT, 4], F32)
    wdec = P.sb("wdec", [128, NKT, 4], F32)
    eAtot = P.sb("eAtot", [128, NKT, 4], F32)
    fl = lambda b: b.full().re("p c h -> p (c h)")
    ps = pb[0]
    P.pe.matmul(out=ps[:, 0:256], lhsT=tri, rhs=fl(a_sb), start=True, stop=True)
    P.act.copy(out=fl(Acum), in_=ps[:, 0:256])
    ps = pb[1]
    P.pe.matmul(out=ps[:, 0:256], lhsT=ones, rhs=fl(a_sb), start=True, stop=True)
    P.act.copy(out=fl(Atot), in_=ps[:, 0:256])
    P.dve.tensor_scalar(out=fl(nAcum), in0=fl(Acum), scalar1=-1.0, scalar2=None, op0=ALU.mult)
    P.act.activation(out=fl(eA), in_=fl(Acum), func=AF.Exp)
    P.act.activation(out=fl(eAtot), in_=fl(Atot), func=AF.Exp)
    P.dve.tensor_tensor(out=fl(wdec), in0=fl(Atot), in1=fl(Acum), op=ALU.subtract)
    P.act.activation(out=fl(wdec), in_=fl(wdec), func=AF.Exp)

    Hs = P.sb("Hs", [128, 256], F32)
    Hb = P.sb("Hb", [128, 256], BF16)
    P.dve.memset(ap=Hs.full(), constant=0.0)
    P.dve.memset(ap=Hb.full(), constant=0.0)
    xc = [P.sb(f"xc{i}", [128, 256], BF16) for i in range(2)]
    Bc = [P.sb(f"Bc{i}", [128, 128], BF16) for i in range(2)]
    cb = P.sb("cb", [128, 128], F32)
    xdt = P.sb("xdt", [128, 256], BF16)
    xdts = P.sb("xdts", [128, 256], BF16)
    at = [P.sb(f"at{i}", [128, 128], F32) for i in range(2)]
    tm = [P.sb(f"tm{i}", [128, 128], F32) for i in range(2)]
    dec = [P.sb(f"dec{i}", [128, 128], F32) for i in range(2)]
    MT = [P.sb(f"MT{i}", [128, 128], BF16) for i in range(2)]
    t1 = P.sb("t1", [128, 256], F32)
    t3 = P.sb("t3", [128, 256], F32)
    yo = [P.sb(f"yo{i}", [128, 256], F32) for i in range(2)]
    v3 = lambda v: v.re("p (h d) -> p h d", h=4)
    bc3 = lambda v: v.f(lambda a: a.unsqueeze(2).to_broadcast([128, 4, 64]))
    for c in range(NKT):
        x_c = xc[c % 2]
        B_c = Bc[c % 2]
        P.dma("sync", out=x_c.full(), in_=x_tm[:, c, :])
        P.dma("gpsimd", out=B_c.full(), in_=B_tm[:, c, :])
        BT_c = BTs[:, c * 128:(c + 1) * 128]
        CT_c = CTs[:, c * 128:(c + 1) * 128]
        ps_cb = pb[0]
        P.pe.matmul(out=ps_cb[:, 0:128], lhsT=BT_c, rhs=CT_c, start=True, stop=True)
        P.act.copy(out=cb.full(), in_=ps_cb[:, 0:128])
        P.dve.tensor_tensor(out=v3(xdt.full()), in0=v3(x_c.full()), in1=bc3(dt_sb[:, c, :]), op=ALU.mult)
        P.pool.tensor_tensor(out=v3(xdts.full()), in0=v3(xdt.full()), in1=bc3(wdec[:, c, :]), op=ALU.mult)
        ps_off = pb[1]
        P.pe.matmul(out=ps_off[:, 0:256], lhsT=CT_c, rhs=Hb.full(), start=True, stop=True)
        ps_y = pb[2]
        for h in range(4):
            i2 = h % 2
            P.dve.tensor_scalar(out=at[i2].full(), in0=tri, scalar1=a_sb[:, c, h:h + 1], scalar2=None, op0=ALU.mult)
            ps_A = pb[3 + i2]
            P.pe.matmul(out=ps_A[:, 0:128], lhsT=ones, rhs=at[i2].full(), start=True, stop=True)
            P.dve.tensor_tensor(out=tm[i2].full(), in0=ps_A[:, 0:128], in1=trimask, op=ALU.add)
            P.act.activation(out=dec[i2].full(), in_=tm[i2].full(), func=AF.Exp, bias=nAcum[:, c, h:h + 1], scale=1.0)
            P.pool.tensor_tensor(out=MT[i2].full(), in0=cb.full(), in1=dec[i2].full(), op=ALU.mult)
            P.pe.matmul(out=ps_y[:, h * 64:(h + 1) * 64], lhsT=MT[i2].full(), rhs=xdt[:, h * 64:(h + 1) * 64],
                        start=True, stop=True)
        P.dve.tensor_tensor(out=v3(t1.full()), in0=v3(ps_off[:, 0:256]), in1=bc3(eA[:, c, :]), op=ALU.mult)
        P.dve.tensor_tensor(out=t1.full(), in0=t1.full(), in1=ps_y[:, 0:256], op=ALU.add)
        P.pool.tensor_tensor(out=v3(t3.full()), in0=v3(x_c.full()), in1=bc3(Dsb.full()), op=ALU.mult)
        y_ = yo[c % 2]
        P.pool.tensor_tensor(out=y_.full(), in0=t1.full(), in1=t3.full(), op=ALU.add)
        P.dma("sync", out=o_y[:, c, :], in_=y_.full())
        ps_h = pb[5]
        P.pe.matmul(out=ps_h[:, 0:256], lhsT=B_c.full(), rhs=xdts.full(), start=True, stop=True)
        P.dve.tensor_tensor(out=v3(Hs.full()), in0=v3(Hs.full()), in1=bc3(eAtot[:, c, :]), op=ALU.mult)
        P.dve.tensor_tensor(out=Hs.full(), in0=Hs.full(), in1=ps_h[:, 0:256], op=ALU.add)
        P.act.copy(out=Hb.full(), in_=Hs.full())
    P.emit()
    return nc, P


def _bc_consts():
    msk = np.zeros((128, 8, 512), np.float32)
    p = np.arange(128)[:, None]
    q = np.arange(512)[None, :]
    for j in range(4):
        key = j * 128 + p
        msk[:, j, :] = np.where((key // 64) > (q // 64), NEG, 0.0)
        msk[:, 4 + j, :] = np.where(key > q, NEG, 0.0)
    cm = np.zeros((128, 4, 128), np.float32)
    jj = np.arange(128)[:, None]
    ii = np.arange(128)[None, :]
    cm[:, 0, :] = (jj <= ii).astype(np.float32)
    cm[:, 1, :] = np.where(jj > ii, NEG, 0.0)
    cm[:, 2, :] = np.eye(128, dtype=np.float32)
    cm[:, 3, :] = 1.0
    return msk, cm


def _tm(a):
    S, n = a.shape
    return np.ascontiguousarray(a.reshape(S // 128, 128, n).transpose(1, 0, 2))


def run_BC(inp, l, resA):
    if "BC" not in _PROG_CACHE:
        _PROG_CACHE["BC"] = build_BC()[0]
    nc = _PROG_CACHE["BC"]
    msk, cm = _bc_consts()

    def gather(name, b):
        return np.concatenate([np.asarray(resA[b * 4 + i][name]) for i in range(4)], axis=-1)

    in_maps = []
    for c in range(8):
        b, hg = c // 4, c % 4
        qm = gather("o_qm", b)[2 * hg:2 * hg + 2]
        km = gather("o_km", b)[2 * hg:2 * hg + 2]
        vmf = gather("o_vm", b)
        qf = gather("o_qf", b)[2 * hg:2 * hg + 2]
        kf = gather("o_kf", b)[2 * hg:2 * hg + 2]
        vff = gather("o_vf", b)
        lff = gather("o_lf", b)
        xbc = gather("o_xbc", b)
        dtf = gather("o_dt", b)
        af = gather("o_a", b)
        g = hg // 2
        vm = np.stack([_tm(vmf[(2 * hg + h) * 64:(2 * hg + h + 1) * 64].T) for h in range(2)])
        vf = np.stack([_tm(vff[(2 * hg + h) * 64:(2 * hg + h + 1) * 64].T) for h in range(2)])
        lf = np.stack([np.ascontiguousarray(lff[2 * hg + h].reshape(NKT, 128).T) for h in range(2)])
        x_tm = _tm(xbc[hg * 256:(hg + 1) * 256].T)
        Bf = xbc[1024 + g * 128:1024 + (g + 1) * 128]
        Cf = xbc[1280 + g * 128:1280 + (g + 1) * 128]
        Dv = np.broadcast_to(inp["ssm_D"][l][4 * hg:4 * hg + 4][None, :], (128, 4)).astype(np.float32)
        in_maps.append({
            "qm": np.ascontiguousarray(qm), "km": np.ascontiguousarray(km), "vm": vm,
            "qf": np.ascontiguousarray(qf), "kf": np.ascontiguousarray(kf), "vf": vf, "lf": lf,
            "msk": msk, "cm": cm, "x_tm": x_tm, "B_tm": _tm(Bf.T), "BT": np.ascontiguousarray(Bf),
            "CT": np.ascontiguousarray(Cf), "dt_tm": _tm(dtf[4 * hg:4 * hg + 4].T),
            "a_tm": _tm(af[4 * hg:4 * hg + 4].T), "Dv": np.ascontiguousarray(Dv),
        })
    res = run_bass_kernel_spmd(nc, in_maps, core_ids=list(range(8))).results
    om = np.zeros((2, 512, S_), NBF)
    of = np.zeros((2, 512, S_), NBF)
    y = np.zeros((2, 1024, S_), np.float32)
    for c in range(8):
        b, hg = c // 4, c % 4
        om[b, hg * 128:(hg + 1) * 128] = np.asarray(res[c]["o_m"]).reshape(128, S_)
        of[b, hg * 128:(hg + 1) * 128] = np.asarray(res[c]["o_f"]).reshape(128, S_)
        yy = np.asarray(res[c]["o_y"])
        y[b, hg * 256:(hg + 1) * 256] = yy.transpose(2, 1, 0).reshape(256, S_)
    return om, of, y


def build_D1():
    nc = new_nc()
    P = Prog(nc)
    EI, EO = "ExternalInput", "ExternalOutput"
    NT = T // 512
    omT = P.dram("omT", [512, T], BF16, EI)
    ofT = P.dram("ofT", [512, T], BF16, EI)
    yT = P.dram("yT", [1024, T], F32, EI)
    szT = P.dram("szT", [1024, T], BF16, EI)
    gT = P.dram("gT", [3072, T], BF16, EI)
    xT = P.dram("xT", [D, T], F32, EI)
    w_a = P.dram("w_a", [512, D], F32, EI)
    w_b = P.dram("w_b", [512, D], F32, EI)
    w_c = P.dram("w_c", [1024, D], F32, EI)
    w_o = P.dram("w_o", [1024, D], F32, EI)
    cst_d = P.dram("cst", [128, 8], F32, EI)
    o_x = P.dram("o_x", [D, T], F32, EO)

    cstb = P.sb("cstb", [128, 8], F32)
    P.dma("sync", out=cstb.full(), in_=cst_d.full())
    ones = P.sb("ones", [128, 128], F32)
    P.dve.memset(ap=ones.full(), constant=1.0)
    eps = P.sb("eps", [128, 1], F32)
    P.dve.memset(ap=eps.full(), constant=1e-6)
    pb = [P.ps(f"pb{i}", [128, 512], F32) for i in range(8)]
    wst = [P.sb(f"wst{i}", [128, 4, 1024], F32) for i in range(2)]
    wcnt = [0]

    def load_w(dram, kc_n, name):
        bfb = P.sb(name, [128, kc_n, D], BF16)
        v = dram.full().re("(kc p) n -> p kc n", p=128)
        for k0 in range(0, kc_n, 4):
            i = wcnt[0] % 2
            wcnt[0] += 1
            P.dma("sync" if i == 0 else "gpsimd", out=wst[i].full(), in_=v[:, k0:k0 + 4, :])
            P.pool.tensor_copy(out=bfb[:, k0:k0 + 4, :], in_=wst[i].full())
        return bfb

    Wa = load_w(w_a, 4, "Wa")
    Wb = load_w(w_b, 4, "Wb")
    Wc = load_w(w_c, 8, "Wc")
    Wo = load_w(w_o, 8, "Wo")

    ys = P.sb("ys", [128, 8, 512], F32)
    szs = P.sb("szs", [128, 8, 512], BF16)
    yn = P.sb("yn", [128, 8, 512], BF16)
    oms = P.sb("oms", [128, 4, 512], BF16)
    ofs = P.sb("ofs", [128, 4, 512], BF16)
    gs = P.sb("gs", [128, 24, 512], BF16)
    xs = P.sb("xs", [128, 8, 512], F32)
    sq = P.sb("sq", [128, 512], F32)
    rstd = P.sb("rstd", [128, 512], F32)
    m1 = [P.sb(f"m1_{i}", [128, 512], F32) for i in range(2)]
    m2 = [P.sb(f"m2_{i}", [128, 512], F32) for i in range(2)]
    m3 = [P.sb(f"m3_{i}", [128, 512], F32) for i in range(2)]
    mg = P.sb("mg", [128, 8, 512], BF16)
    xo = [P.sb(f"xo{i}", [128, 512], F32) for i in range(2)]
    ch = lambda d: d.full().re("(kc p) n -> p kc n", p=128)
    for ti in range(NT):
        ts = slice(ti * 512, (ti + 1) * 512)
        P.dma("sync", out=ys.full(), in_=ch(yT)[:, :, ts])
        P.dma("gpsimd", out=szs.full(), in_=ch(szT)[:, :, ts])
        P.dma("sync", out=oms.full(), in_=ch(omT)[:, :, ts])
        P.dma("gpsimd", out=ofs.full(), in_=ch(ofT)[:, :, ts])
        P.dma("sync", out=gs.full(), in_=ch(gT)[:, :, ts])
        P.dma("gpsimd", out=xs.full(), in_=ch(xT)[:, :, ts])
        ps = pb[7]
        for kc in range(8):
            P.dve.tensor_tensor(out=ys[:, kc, :], in0=ys[:, kc, :], in1=szs[:, kc, :], op=ALU.mult)
            P.act.activation(out=sq.full(), in_=ys[:, kc, :], func=AF.Square)
            P.pe.matmul(out=ps.full(), lhsT=ones.full(), rhs=sq.full(), start=(kc == 0), stop=(kc == 7))
        P.act.activation(out=rstd.full(), in_=ps.full(), func=AF.Sqrt, bias=eps[:, 0:1], scale=1.0 / 1024.0)
        P.dve.reciprocal(out=rstd.full(), in_=rstd.full())
        for kc in range(8):
            P.dve.scalar_tensor_tensor(out=yn[:, kc, :], in0=ys[:, kc, :], scalar=cstb[:, kc:kc + 1], in1=rstd.full(),
                                       op0=ALU.mult, op1=ALU.mult)
        for oc in range(8):
            i2 = oc % 2
            osl = slice(oc * 128, (oc + 1) * 128)
            pa, pbb, pc = pb[0 + i2 * 3], pb[1 + i2 * 3], pb[2 + i2 * 3]
            for kc in range(4):
                P.pe.matmul(out=pa.full(), lhsT=Wa[:, kc, osl], rhs=oms[:, kc, :], start=(kc == 0), stop=(kc == 3))
            for kc in range(4):
                P.pe.matmul(out=pbb.full(), lhsT=Wb[:, kc, osl], rhs=ofs[:, kc, :], start=(kc == 0), stop=(kc == 3))
            for kc in range(8):
                P.pe.matmul(out=pc.full(), lhsT=Wc[:, kc, osl], rhs=yn[:, kc, :], start=(kc == 0), stop=(kc == 7))
            P.dve.tensor_tensor(out=m1[i2].full(), in0=pa.full(), in1=gs[:, oc, :], op=ALU.mult)
            P.dve.tensor_tensor(out=m2[i2].full(), in0=pbb.full(), in1=gs[:, 8 + oc, :], op=ALU.mult)
            P.dve.tensor_tensor(out=m3[i2].full(), in0=pc.full(), in1=gs[:, 16 + oc, :], op=ALU.mult)
            P.pool.tensor_tensor(out=m1[i2].full(), in0=m1[i2].full(), in1=m2[i2].full(), op=ALU.add)
            P.pool.tensor_tensor(out=mg[:, oc, :], in0=m1[i2].full(), in1=m3[i2].full(), op=ALU.add)
        for oc in range(8):
            i2 = oc % 2
            ps = pb[6 + i2]
            for kc in range(8):
                P.pe.matmul(out=ps.full(), lhsT=Wo[:, kc, oc * 128:(oc + 1) * 128], rhs=mg[:, kc, :],
                            start=(kc == 0), stop=(kc == 7))
            P.dve.tensor_tensor(out=xo[i2].full(), in0=ps.full(), in1=xs[:, oc, :], op=ALU.add)
            P.dma("sync" if i2 else "gpsimd", out=o_x[oc * 128:(oc + 1) * 128, ts], in_=xo[i2].full())
    P.emit()
    return nc, P


def d2_colpack(inp, l):
    cp = ColPack()
    cp.add("g_ffn", inp["norm_ffn_g"][l])
    cw = inp["ffn_conv_w"][l]
    for k in range(3):
        cp.add(f"fw{k}", cw[k])
    cp.add("fb", inp["ffn_conv_b"][l])
    return cp


def build_D2(off):
    nc = new_nc()
    P = Prog(nc)
    EI, EO = "ExternalInput", "ExternalOutput"
    NT = T // 512
    TT = T + HALO
    xT = P.dram("xT", [D, TT], F32, EI)
    w_up = P.dram("w_up", [D, 5632], F32, EI)
    w_dn = P.dram("w_dn", [2816, D], F32, EI)
    cst_d = P.dram("cst", [128, off["_n"]], F32, EI)
    o_x = P.dram("o_x", [D, T], F32, EO)

    cstb = P.sb("cstb", [128, off["_n"]], F32)
    C = Cst(P, cstb, off)
    P.dma("sync", out=cstb.full(), in_=cst_d.full())
    ones = P.sb("ones", [128, 128], F32)
    P.dve.memset(ap=ones.full(), constant=1.0)
    eps = P.sb("eps", [128, 1], F32)
    P.dve.memset(ap=eps.full(), constant=1e-6)
    pb = [P.ps(f"pb{i}", [128, 512], F32) for i in range(8)]
    wst = [P.sb(f"wst{i}", [128, 1024], F32) for i in range(2)]
    wcnt = [0]
    Wu = P.sb("Wu", [128, 8, 5632], BF16)
    Wd = P.sb("Wd", [128, 22, D], BF16)
    wuv = w_up.full().re("(kc p) n -> p kc n", p=128)
    for kc in range(8):
        for c0 in range(0, 5632, 1024):
            n = min(1024, 5632 - c0)
            i = wcnt[0] % 2
            wcnt[0] += 1
            P.dma("sync" if i == 0 else "gpsimd", out=wst[i][:, 0:n], in_=wuv[:, kc, c0:c0 + n])
            P.pool.tensor_copy(out=Wu[:, kc, c0:c0 + n], in_=wst[i][:, 0:n])
    wdv = w_dn.full().re("(kc p) n -> p kc n", p=128)
    for k0 in range(22):
        i = wcnt[0] % 2
        wcnt[0] += 1
        P.dma("sync" if i == 0 else "gpsimd", out=wst[i].full(), in_=wdv[:, k0, :])
        P.pool.tensor_copy(out=Wd[:, k0, :], in_=wst[i].full())

    xst = P.sb("xst", [128, 8, 512], F32)
    hn = P.sb("hn", [128, 8, 512], BF16)
    sq = P.sb("sq", [128, 512], F32)
    rstd = P.sb("rstd", [128, 512], F32)
    act = P.sb("act", [128, 22, 512], BF16)
    upre = [P.sb(f"upre{i}", [128, 516], F32) for i in range(2)]
    acc = [P.sb(f"acc{i}", [128, 512], F32) for i in range(2)]
    sg = P.sb("sg", [128, 512], F32)
    carry = P.sb("carry", [128, 44, 4], F32)
    xo = [P.sb(f"xo{i}", [128, 512], F32) for i in range(2)]
    xTv = xT.full().re("(kc p) n -> p kc n", p=128)
    tiles = [(0, HALO)] + [(HALO + i * 512, 512) for i in range(NT)]
    pcnt = [0]
    for tix, (c0, w) in enumerate(tiles):
        P.dma("sync", out=xst[:, :, 0:w], in_=xTv[:, :, c0:c0 + w])
        ps = pb[7]
        for kc in range(8):
            P.act.activation(out=sq[:, 0:w], in_=xst[:, kc, 0:w], func=AF.Square)
            P.pe.matmul(out=ps[:, 0:w], lhsT=ones.full(), rhs=sq[:, 0:w], start=(kc == 0), stop=(kc == 7))
        P.act.activation(out=rstd[:, 0:w], in_=ps[:, 0:w], func=AF.Sqrt, bias=eps[:, 0:1], scale=1.0 / 1024.0)
        P.dve.reciprocal(out=rstd[:, 0:w], in_=rstd[:, 0:w])
        for kc in range(8):
            P.dve.scalar_tensor_tensor(out=hn[:, kc, 0:w], in0=xst[:, kc, 0:w], scalar=C.col("g_ffn", kc),
                                       in1=rstd[:, 0:w], op0=ALU.mult, op1=ALU.mult)
        for i in range(22):
            accs = []
            for j, cg in enumerate((i, 22 + i)):
                ps = pb[pcnt[0] % 4]
                pcnt[0] += 1
                for kc in range(8):
                    P.pe.matmul(out=ps[:, 0:w], lhsT=Wu[:, kc, cg * 128:(cg + 1) * 128], rhs=hn[:, kc, 0:w],
                                start=(kc == 0), stop=(kc == 7))
                if tix == 0:
                    P.act.copy(out=carry[:, cg, :], in_=ps[:, 0:HALO])
                    continue
                up = upre[j]
                P.act.copy(out=up[:, 4:516], in_=ps.full())
                P.dve.tensor_copy(out=up[:, 0:4], in_=carry[:, cg, :])
                P.pool.tensor_copy(out=carry[:, cg, :], in_=up[:, 512:516])
                a0 = acc[j]
                P.dve.tensor_scalar(out=a0.full(), in0=up[:, 4:516], scalar1=C.col("fw2", cg), scalar2=C.col("fb", cg),
                                    op0=ALU.mult, op1=ALU.add)
                P.dve.scalar_tensor_tensor(out=a0.full(), in0=up[:, 3:515], scalar=C.col("fw1", cg), in1=a0.full(),
                                           op0=ALU.mult, op1=ALU.add)
                P.dve.scalar_tensor_tensor(out=a0.full(), in0=up[:, 2:514], scalar=C.col("fw0", cg), in1=a0.full(),
                                           op0=ALU.mult, op1=ALU.add)
                accs.append(a0)
            if tix == 0:
                continue
            P.act.activation(out=sg.full(), in_=accs[0].full(), func=AF.Silu)
            P.pool.tensor_tensor(out=act[:, i, :], in0=sg.full(), in1=accs[1].full(), op=ALU.mult)
        if tix == 0:
            continue
        ti = tix - 1
        for oc in range(8):
            i2 = oc % 2
            ps = pb[4 + i2]
            for i in range(22):
                P.pe.matmul(out=ps.full(), lhsT=Wd[:, i, oc * 128:(oc + 1) * 128], rhs=act[:, i, :],
                            start=(i == 0), stop=(i == 21))
            P.dve.tensor_tensor(out=xo[i2].full(), in0=ps.full(), in1=xst[:, oc, :], op=ALU.add)
            P.dma("sync" if i2 else "gpsimd", out=o_x[oc * 128:(oc + 1) * 128, ti * 512:(ti + 1) * 512], in_=xo[i2].full())
    P.emit()
    return nc, P


def run_D1(inp, l, resA, om, of, y, x_full):
    if "D1" not in _PROG_CACHE:
        _PROG_CACHE["D1"] = build_D1()[0]
    nc = _PROG_CACHE["D1"]
    cst = np.ascontiguousarray(inp["ssm_norm_g"][l].reshape(8, 128).T)
    xf = x_full.reshape(16384, D)
    in_maps = []
    for c in range(8):
        b, q = c // 4, c % 4
        ts = slice(q * T, (q + 1) * T)
        in_maps.append({
            "omT": np.ascontiguousarray(om[b][:, ts]), "ofT": np.ascontiguousarray(of[b][:, ts]),
            "yT": np.ascontiguousarray(y[b][:, ts]), "szT": np.asarray(resA[c]["o_sz"]),
            "gT": np.asarray(resA[c]["o_g"]), "xT": np.ascontiguousarray(xf[c * T:(c + 1) * T].T),
            "w_a": np.ascontiguousarray(inp["w_br_mla"][l]), "w_b": np.ascontiguousarray(inp["w_br_fox"][l]),
            "w_c": np.ascontiguousarray(inp["w_br_ssm"][l]), "w_o": np.ascontiguousarray(inp["w_out"][l]),
            "cst": cst,
        })
    res = run_bass_kernel_spmd(nc, in_maps, core_ids=list(range(8))).results
    xm = np.concatenate([np.asarray(r["o_x"]).T for r in res], axis=0)
    return xm.reshape(2, S_, D)


def run_D2(inp, l, xm_full):
    cp = d2_colpack(inp, l)
    off = dict(cp.off)
    off["_n"] = cp.n
    if "D2" not in _PROG_CACHE:
        _PROG_CACHE["D2"] = build_D2(off)[0]
    nc = _PROG_CACHE["D2"]
    cst = cp.array()
    xf = xm_full.reshape(16384, D)
    in_maps = []
    for c in range(8):
        t0 = c * T
        xt = np.zeros((D, T + HALO), np.float32)
        xt[:, HALO:] = xf[t0:t0 + T].T
        if c % 4 != 0:
            xt[:, 0:HALO] = xf[t0 - HALO:t0].T
        in_maps.append({"xT": np.ascontiguousarray(xt), "w_up": np.ascontiguousarray(inp["ffn_w_up"][l]),
                        "w_dn": np.ascontiguousarray(inp["ffn_w_down"][l]), "cst": cst})
    res = run_bass_kernel_spmd(nc, in_maps, core_ids=list(range(8))).results
    xo = np.concatenate([np.asarray(r["o_x"]).T for r in res], axis=0)
    return xo.reshape(2, S_, D)


def kernel_unfused(**inp):
    inp = {k: np.asarray(v) for k, v in inp.items()}
    x = inp["x"].astype(np.float32)
    pos = inp["positions"]
    for l in range(2):
        resA = run_A(inp, l, x, pos)
        om, of, y = run_BC(inp, l, resA)
        xm = run_D1(inp, l, resA, om, of, y, x)
        x = run_D2(inp, l, xm)
    return np.ascontiguousarray(x.astype(np.float32))


def kernel(**inp):
    return kernel_fused(**inp)


SW = 516
RG = [[0, 1, 2, 3], [4, 5, 6, 7]]
KT_L = T // 128


def fused_rowpack(inp, l):
    r = np.concatenate([inp["fox_b_f"][l], inp["ssm_dt_bias"][l], inp["ssm_A_log"][l], inp["ssm_D"][l]]).astype(np.float32)
    return np.ascontiguousarray(np.broadcast_to(r[None, :], (128, r.size)))


def build_fused(offA, offD2, stop=None, dbg=()):
    nc = new_nc()
    P = Prog(nc)
    EI, EO = "ExternalInput", "ExternalOutput"
    L = 2
    x0 = P.dram("x0", [D, 4 * SW], F32, EI)
    pos = P.dram("pos", [1, T], I32, EI)
    w_in = P.dram("w_in", [L, D, 7864], F32, EI)
    w_uq = P.dram("w_uq", [L, 384, 768], F32, EI)
    w_kp = P.dram("w_kp", [L, 256, 768], F32, EI)
    w_v = P.dram("w_v", [L, 256, 512], F32, EI)
    w_a = P.dram("w_a", [L, 512, D], F32, EI)
    w_b = P.dram("w_b", [L, 512, D], F32, EI)
    w_c = P.dram("w_c", [L, 1024, D], F32, EI)
    w_o = P.dram("w_o", [L, 1024, D], F32, EI)
    w_up = P.dram("w_up", [L, D, 5632], F32, EI)
    w_dn = P.dram("w_dn", [L, 2816, D], F32, EI)
    cstA_d = P.dram("cstA", [L, 128, offA["_n"]], F32, EI)
    cstD_d = P.dram("cstD", [L, 128, offD2["_n"]], F32, EI)
    gssm_d = P.dram("gssm", [L, 128, 8], F32, EI)
    rowc_d = P.dram("rowc", [L, 128, 56], F32, EI)
    sel_d = P.dram("sel", [128, 32], F32, EI)
    msk_d = P.dram("msk", [128, 8, 512], F32, EI)
    cm_d = P.dram("cm", [128, 4, 128], F32, EI)
    mats_d = P.dram("mats", [128, 192], F32, EI)
    out = P.dram("out", [D, T], F32, EO)
    xb = [x0, P.dram("xb1", [D, 4 * SW], F32)]
    xmid = P.dram("xmid", [D, 4 * SW], F32)
    qm = P.dram("qm", [8, 96, T], BF16)
    qf = P.dram("qf", [8, 64, T], BF16)
    fq = P.dram("fq", [8, 3, T], BF16)
    szd = P.dram("szd", [1024, T], BF16)
    gd = P.dram("gd", [3072, T], BF16)
    xtm = P.dram("xtm", [128, KT_L, 1024], BF16)
    btm = P.dram("btm", [128, KT_L, 256], BF16)
    bct = P.dram("bct", [512, T], BF16)
    dtd = P.dram("dtd", [128, KT_L, 16], F32)
    atd = P.dram("atd", [128, KT_L, 16], F32)
    omd = P.dram("omd", [512, T], BF16)
    ofd = P.dram("ofd", [512, T], BF16)
    yd = P.dram("yd", [1024, T], F32)
    kxm = [P.dram(f"kxm{m}", [768, 512], BF16) for m in range(4)]
    kxmg = [P.dram(f"kxmg{m}", [4 * 768, 512], BF16) for m in range(4)]
    kxf = [P.dram(f"kxf{m}", [512, 512], BF16) for m in range(4)]
    kxfg = [P.dram(f"kxfg{m}", [4 * 512, 512], BF16) for m in range(4)]
    vx = [P.dram(f"vx{m}", [2048, 256], BF16) for m in range(4)]
    vxg = [P.dram(f"vxg{m}", [4 * 2048, 256], BF16) for m in range(4)]
    sx = [P.dram(f"sx{i}", [256, 1024], F32) for i in range(2)]
    sxg = [P.dram(f"sxg{i}", [4 * 256, 1024], F32) for i in range(2)]
    fx = P.dram("fx", [128, 224], F32)
    fxg = P.dram("fxg", [4 * 128, 224], F32)
    tx = P.dram("tx", [128, 128], F32)
    txg = P.dram("txg", [4 * 128, 128], F32)
    wub = P.dram("wub", [D, 5632], BF16)
    wdb = P.dram("wdb", [2816, D], BF16)
    wab = P.dram("wab", [512, D], BF16)
    wbb = P.dram("wbb", [512, D], BF16)
    wcb = P.dram("wcb", [1024, D], BF16)
    wob = P.dram("wob", [1024, D], BF16)
    dbg_out = {}

    def gather_pairs(pairs):
        for (a, b) in pairs:
            P.pool.collective_compute(kind="AllGather", op=ALU.bypass, replica_groups=RG,
                                      ins=[a.full().re("(p a) c -> p (a c)", p=128)],
                                      outs=[b.full().re("(q a) c -> q (a c)", q=512)])

    def load_consts():
        d = {}
        d["cm"] = P.sb("cmb", [128, 4, 128], F32)
        P.dma("sync", out=d["cm"].full(), in_=cm_d.full())
        d["sel"] = P.sb("selb", [128, 32], F32)
        P.dma("sync", out=d["sel"].full(), in_=sel_d.full())
        d["eps"] = P.sb("eps", [128, 1], F32)
        P.dve.memset(ap=d["eps"].full(), constant=1e-6)
        d["one1"] = P.sb("one1", [128, 1], F32)
        P.dve.memset(ap=d["one1"].full(), constant=1.0)
        d["zero"] = P.sb("zero", [128, 1], F32)
        P.dve.memset(ap=d["zero"].full(), constant=0.0)
        return d

    def phase_A(l):
        K = load_consts()
        cmb = K["cm"]
        tri, ident, ones = cmb[:, 0, :], cmb[:, 2, :], cmb[:, 3, :]
        eps, one1 = K["eps"], K["one1"]
        xin = xb[l]
        cstb = P.sb("cstb", [128, offA["_n"]], F32)
        C = Cst(P, cstb, offA)
        P.dma("sync", out=cstb.full(), in_=cstA_d[l])
        rowc = P.sb("rowc", [128, 56], F32)
        P.dma("sync", out=rowc.full(), in_=rowc_d[l])
        matf = P.sb("matf", [128, 192], F32)
        matb = P.sb("matb", [128, 192], BF16)
        P.dma("sync", out=matf.full(), in_=mats_d.full())
        P.dve.tensor_copy(out=matb.full(), in_=matf.full())
        prh = matb[0:96, 0:96]
        selm = matb[0:32, 96:192]
        identb = P.sb("identb", [128, 128], BF16)
        P.dve.tensor_copy(out=identb.full(), in_=ident)
        Aneg_r = P.sb("Aneg_r", [128, 16], F32)
        P.act.activation(out=Aneg_r.full(), in_=rowc[:, 24:40], func=AF.Exp)
        P.dve.tensor_scalar(out=Aneg_r.full(), in0=Aneg_r.full(), scalar1=-1.0, scalar2=None, op0=ALU.mult)

        pb = [P.ps(f"pb{i}", [128, 512], F32) for i in range(7)]
        pbt = P.ps("pbt", [128, 1024], BF16)
        pbi = {}

        def nxt_ps(lo=0, hi=4):
            i = pbi.get(lo, 0)
            pbi[lo] = (i + 1) % (hi - lo)
            return pb[lo + i]

        Ctab = P.sb("Ctab", [96, T], F32)
        Stab = P.sb("Stab", [96, T], F32)
        hraw = P.sb("hraw", [96, 512], F32)
        hsq = P.sb("hsq", [96, 512], F32)
        hrs = P.sb("hrs", [96, 512], F32)
        hnf = P.sb("hnf", [96, 512], F32)
        hnb = P.sb("hnb", [96, 512], BF16)
        ht1 = P.sb("ht1", [96, 512], F32)
        ht2 = P.sb("ht2", [96, 512], F32)
        posf, rr_tmp, rr_m = hrs, hraw, hsq

        class _IV:
            def __init__(self, b):
                self.b = b

            def full(self):
                return self.b.full().bitcast(I32)
        posi, rr_i = _IV(ht1), _IV(ht2)

        def sin_table(outv, phase):
            P.dve.tensor_scalar(out=rr_tmp.full(), in0=posf.full(), scalar1=C.col("invf"), scalar2=phase,
                                op0=ALU.mult, op1=ALU.add)
            P.dve.tensor_scalar(out=rr_m.full(), in0=rr_tmp.full(), scalar1=1.0 / (2 * np.pi), scalar2=None, op0=ALU.mult)
            P.dve.tensor_copy(out=rr_i.full(), in_=rr_m.full())
            P.dve.tensor_copy(out=rr_m.full(), in_=rr_i.full())
            P.dve.scalar_tensor_tensor(out=rr_tmp.full(), in0=rr_m.full(), scalar=-2 * np.pi, in1=rr_tmp.full(),
                                       op0=ALU.mult, op1=ALU.add)
            P.dve.tensor_scalar(out=rr_m.full(), in0=rr_tmp.full(), scalar1=np.pi, scalar2=-2 * np.pi, op0=ALU.is_gt, op1=ALU.mult)
            P.dve.tensor_tensor(out=rr_tmp.full(), in0=rr_tmp.full(), in1=rr_m.full(), op=ALU.add)
            P.dve.tensor_scalar(out=rr_m.full(), in0=rr_tmp.full(), scalar1=-np.pi, scalar2=2 * np.pi, op0=ALU.is_lt, op1=ALU.mult)
            P.dve.tensor_tensor(out=rr_tmp.full(), in0=rr_tmp.full(), in1=rr_m.full(), op=ALU.add)
            P.act.activation(out=outv, in_=rr_tmp.full(), func=AF.Sin)

        for i in range(4):
            P.dma("sync", out=posi.full(), in_=pos[:, i * 512:(i + 1) * 512].f(lambda a: a.partition_broadcast(96)))
            P.dve.tensor_copy(out=posf.full(), in_=posi.full())
            sin_table(Stab[:, i * 512:(i + 1) * 512], 0.0)
            sin_table(Ctab[:, i * 512:(i + 1) * 512], np.pi / 2)
        P.dve.memset(ap=Stab[0:64, :], constant=0.0)
        P.dve.memset(ap=Ctab[0:64, :], constant=1.0)

        hn = P.sb("hn", [128, 8, 4 * SW], BF16)
        xst = P.sb("xst", [128, 8, 512], F32)
        sq = P.sb("sq", [128, 512], F32)
        rstd = P.sb("rstd", [128, 512], F32)
        xTv = xin.full().re("(kc p) n -> p kc n", p=128)

        def rstd_from(ps_view, n_feat, rows, rstd_view):
            P.act.activation(out=rstd_view, in_=ps_view, func=AF.Sqrt, bias=eps[0:rows, 0:1], scale=1.0 / n_feat)
            P.dve.reciprocal(out=rstd_view, in_=rstd_view)

        halos = [(m * SW, 4) for m in range(4)]
        main = [(m * SW + 4, 512) for m in range(4)]
        for (c0, w) in halos + main:
            P.dma("sync", out=xst[:, :, 0:w], in_=xTv[:, :, c0:c0 + w])
            ps = nxt_ps(4, 6)
            for kc in range(8):
                P.act.activation(out=sq[:, 0:w], in_=xst[:, kc, 0:w], func=AF.Square)
                P.pe.matmul(out=ps[:, 0:w], lhsT=ones, rhs=sq[:, 0:w], start=(kc == 0), stop=(kc == 7))
            rstd_from(ps[:, 0:w], 1024.0, 128, rstd[:, 0:w])
            for kc in range(8):
                P.dve.scalar_tensor_tensor(out=hn[:, kc, c0:c0 + w], in0=xst[:, kc, 0:w], scalar=C.col("g_mix", kc),
                                           in1=rstd[:, 0:w], op0=ALU.mult, op1=ALU.mult)

        wst = [P.sb(f"wst{i}", [128, 8, 256], F32) for i in range(2)]
        wbf = [P.sb(f"wbf{i}", [128, 8, 512], BF16) for i in range(2)]
        wcnt = [0]
        scnt = [0]
        w_inv = w_in[l].re("(kc p) n -> p kc n", p=128)

        SBv = 672 + 1544
        wplan = [(0, 384), (384, 288), (672, 512), (672 + 512, 512), (672 + 1024, 512), (672 + 1536, 8),
                 (SBv + 1024 + 1536, 16), (SBv, 512), (SBv + 512, 512)]
        wplan += [(SBv + 1024 + b_ * 512, 512) for b_ in range(3)]
        wplan += [(SBv + 2576 + b_ * 512, 512) for b_ in range(6)]
        wpend = {}

        def w_issue(g):
            c0, ncols = wplan[g]
            lst = []
            for h0 in range(0, ncols, 256):
                n = min(256, ncols - h0)
                si = scnt[0] % 2
                scnt[0] += 1
                P.dma("sync", out=wst[si][:, :, 0:n], in_=w_inv[:, :, c0 + h0:c0 + h0 + n])
                lst.append((si, h0, n))
            wpend[g] = lst

        def load_w(c0, ncols):
            g = wcnt[0]
            wcnt[0] += 1
            assert wplan[g] == (c0, ncols), (g, wplan[g], c0, ncols)
            i = g % 2
            if g not in wpend:
                w_issue(g)
            lst = wpend.pop(g)
            for (si, h0, n) in lst:
                P.act.copy(out=wbf[i][:, :, h0:h0 + n], in_=wst[si][:, :, 0:n])
            if g + 1 < len(wplan):
                w_issue(g + 1)
            return wbf[i]

        def proj(wb, wc0, mcols, c0, w, ps_view):
            for kc in range(8):
                P.pe.matmul(out=ps_view, lhsT=wb[:, kc, wc0:wc0 + mcols], rhs=hn[:, kc, c0:c0 + w],
                            start=(kc == 0), stop=(kc == 7))

        def proj_tm(wb, wc0, ncols, tok0, ps_view):
            for kc in range(8):
                P.pe.matmul(out=ps_view, lhsT=hn[:, kc, tok0:tok0 + 128], rhs=wb[:, kc, wc0:wc0 + ncols],
                            start=(kc == 0), stop=(kc == 7))

        ostg_cnt = [0]
        ostg = [P.sb(f"ostg{i}", [128, 512], BF16) for i in range(4)]

        def next_ostg():
            i = ostg_cnt[0] % 4
            ostg_cnt[0] += 1
            return ostg[i]

        def out_dma(dst_view, src_view):
            P.dma("sync" if ostg_cnt[0] % 2 else "scalar", out=dst_view, in_=src_view)

        hsets = [dict(hraw=hraw.full(), hsq=hsq.full(), hrs=hrs.full(), hnf=hnf.full(), hnb=hnb.full(),
                      ht1=ht1.full(), ht2=ht2.full())]
        hnb1 = P.sb("hnb1", [96, 512], BF16)
        hsets.append(dict(hraw=xst[0:96, 0, :].k(0), hsq=xst[0:96, 1, :].k(1), hrs=xst[0:96, 2, :].k(2),
                          hnf=xst[0:96, 3, :].k(3), hnb=hnb1.full(), ht1=xst[0:96, 4, :].k(4), ht2=xst[0:96, 5, :].k(5)))
        hb2 = P.sb("hb2", [96, 6, 512], F32)
        hnb2 = P.sb("hnb2", [96, 512], BF16)
        hsets.append(dict(hraw=hb2[:, 0, :].k(0), hsq=hb2[:, 1, :].k(1), hrs=hb2[:, 2, :].k(2),
                          hnf=hb2[:, 3, :].k(3), hnb=hnb2.full(), ht1=hb2[:, 4, :].k(4), ht2=hb2[:, 5, :].k(5)))
        hcnt = [0]

        def headnorm(projfn, d, gain_col, rope, tok0, dst_view):
            H = hsets[hcnt[0] % 3]
            hcnt[0] += 1
            ps_view = projfn()
            P.act.activation(out=H["hsq"][0:d, :], in_=ps_view, func=AF.Square)
            P.act.copy(out=H["hraw"][0:d, :], in_=ps_view)
            yield
            ps2 = nxt_ps(4, 6)
            P.pe.matmul(out=ps2[0:d, :], lhsT=cmb[0:d, 3, 0:d], rhs=H["hsq"][0:d, :], start=True, stop=True)
            rstd_from(ps2[0:d, :], float(d), d, H["hrs"][0:d, :])
            og = next_ostg()
            if not rope:
                P.dve.scalar_tensor_tensor(out=og[0:d, :], in0=H["hraw"][0:d, :], scalar=gain_col, in1=H["hrs"][0:d, :],
                                           op0=ALU.mult, op1=ALU.mult)
            else:
                P.dve.scalar_tensor_tensor(out=H["hnf"][0:d, :], in0=H["hraw"][0:d, :], scalar=gain_col, in1=H["hrs"][0:d, :],
                                           op0=ALU.mult, op1=ALU.mult)
                P.act.copy(out=H["hnb"][0:d, :], in_=H["hnf"][0:d, :])
                yield
                ps3 = nxt_ps(6, 7)
                P.pe.matmul(out=ps3[0:d, :], lhsT=prh, rhs=H["hnb"][0:d, :], start=True, stop=True)
                P.dve.tensor_tensor(out=H["ht1"][0:d, :], in0=H["hnf"][0:d, :], in1=Ctab[0:d, tok0:tok0 + 512], op=ALU.mult)
                P.dve.tensor_tensor(out=H["ht2"][0:d, :], in0=ps3[0:d, :], in1=Stab[0:d, tok0:tok0 + 512], op=ALU.mult)
                P.pool.tensor_tensor(out=og[0:d, :], in0=H["ht1"][0:d, :], in1=H["ht2"][0:d, :], op=ALU.add)
            out_dma(dst_view, og[0:d, :])

        def run_pipe(gens, depth=3):
            gens = iter(gens)
            active = []
            while True:
                started = False
                if len(active) < depth:
                    g = next(gens, None)
                    if g is not None:
                        started = True
                        try:
                            next(g)
                            active.append(g)
                        except StopIteration:
                            pass
                if not active and not started:
                    break
                olds = active[:-1] if (started and active) else list(active)
                for g in olds:
                    try:
                        next(g)
                    except StopIteration:
                        active.remove(g)

        lat = P.sb("lat", [128, 3, 512], F32)
        latn = P.sb("latn", [128, 3, 512], BF16)

        def latent_norm(ps_list, gname):
            nch = len(ps_list)
            ps2 = nxt_ps(4, 6)
            for i, psv in enumerate(ps_list):
                P.act.activation(out=sq.full(), in_=psv, func=AF.Square)
                P.act.copy(out=lat[:, i, :], in_=psv)
                P.pe.matmul(out=ps2.full(), lhsT=ones, rhs=sq.full(), start=(i == 0), stop=(i == nch - 1))
            rstd_from(ps2.full(), 128.0 * nch, 128, rstd.full())
            for i in range(nch):
                P.dve.scalar_tensor_tensor(out=latn[:, i, :], in0=lat[:, i, :], scalar=C.col(gname, i), in1=rstd.full(),
                                           op0=ALU.mult, op1=ALU.mult)

        def small_w(name, dram_l, kc_n, ncols, i):
            bfb = P.sb(name, [128, kc_n, ncols], BF16)
            dv = dram_l.re("(kc p) n -> p kc n", p=128)
            for kc in range(kc_n):
                si = scnt[0] % 2
                scnt[0] += 1
                stg = wst[si].full().re("p a b -> p (a b)")[:, 0:ncols]
                P.dma("sync", out=stg, in_=dv[:, kc, :])
                P.act.copy(out=bfb[:, kc, :], in_=stg)
            return bfb

        uqb = small_w("uqb", w_uq[l], 3, 768, 0)
        kpb = small_w("kpb", w_kp[l], 2, 768, 1)
        wvb = small_w("wvb", w_v[l], 2, 512, 0)

        vstg = [P.sb(f"vstg{i}", [128, 512], BF16) for i in range(2)]
        vcnt = [0]

        def v_out(kind, ktl, ps_view):
            vs = vstg[vcnt[0] % 2]
            vcnt[0] += 1
            P.act.copy(out=vs.full(), in_=ps_view)
            P.dma("sync" if vcnt[0] % 2 else "scalar",
                  out=vx[ktl // 4][kind * 1024:(kind + 1) * 1024, (ktl % 4) * 64:(ktl % 4 + 1) * 64].re("(h p) d -> p h d", p=128),
                  in_=vs.full().re("p (h d) -> p h d", h=8))

        wb = load_w(0, 384)
        for m, (c0, w) in enumerate(main):
            pss = []
            for ch in range(3):
                ps = nxt_ps(0, 4)
                proj(wb, ch * 128, 128, c0, 512, ps.full())
                pss.append(ps.full())
            latent_norm(pss, "g_cq")
            def mkq(h):
                def f():
                    ps = nxt_ps(0, 4)
                    for kc in range(3):
                        P.pe.matmul(out=ps[0:96, :], lhsT=uqb[:, kc, h * 96:(h + 1) * 96], rhs=latn[:, kc, :],
                                    start=(kc == 0), stop=(kc == 2))
                    return ps[0:96, :]
                return f
            run_pipe(headnorm(mkq(h), 96, C.col("g_q"), True, m * 512, qm[h, :, m * 512:(m + 1) * 512]) for h in range(8))
        wb = load_w(384, 288)
        krb = P.sb("krb", [32, 512], BF16)
        for m, (c0, w) in enumerate(main):
            pss = []
            for ch in range(2):
                ps = nxt_ps(0, 4)
                proj(wb, ch * 128, 128, c0, 512, ps.full())
                pss.append(ps.full())
            ps = nxt_ps(0, 4)
            proj(wb, 256, 32, c0, 512, ps[0:32, :])
            P.act.copy(out=krb.full(), in_=ps[0:32, :])
            latent_norm(pss, "g_ckv")
            def mkk(h):
                def f():
                    ps = nxt_ps(0, 4)
                    for kc in range(2):
                        P.pe.matmul(out=ps[0:96, :], lhsT=kpb[:, kc, h * 96:(h + 1) * 96], rhs=latn[:, kc, :],
                                    start=(kc == 0), stop=False)
                    P.pe.matmul(out=ps[0:96, :], lhsT=selm, rhs=krb.full(), start=False, stop=True)
                    return ps[0:96, :]
                return f
            run_pipe(headnorm(mkk(h), 96, C.col("g_k"), True, m * 512, kxm[m][h * 96:(h + 1) * 96, :]) for h in range(8))
            for j in range(4):
                ps = nxt_ps(0, 4)
                for kc in range(2):
                    P.pe.matmul(out=ps.full(), lhsT=latn[:, kc, j * 128:(j + 1) * 128], rhs=wvb[:, kc, :],
                                start=(kc == 0), stop=(kc == 1))
                v_out(0, m * 4 + j, ps.full())
        for (base, gname, isq) in ((672, "g_fq", True), (672 + 512, "g_fk", False)):
            wb = load_w(base, 512)
            def mkf(wb_, h, c0):
                def f():
                    ps = nxt_ps(0, 4)
                    proj(wb_, h * 64, 64, c0, 512, ps[0:64, :])
                    return ps[0:64, :]
                return f
            gl = []
            for m, (c0, w) in enumerate(main):
                for h in range(8):
                    dst = qf[h, :, m * 512:(m + 1) * 512] if isq else kxf[m][h * 64:(h + 1) * 64, :]
                    gl.append(headnorm(mkf(wb, h, c0), 64, C.col(gname), False, m * 512, dst))
            run_pipe(gl)
        wb = load_w(672 + 1024, 512)
        for m, (c0, w) in enumerate(main):
            for j in range(4):
                ps = nxt_ps(0, 4)
                proj_tm(wb, 0, 512, c0 + j * 128, ps.full())
                v_out(1, m * 4 + j, ps.full())
        gather_pairs(list(zip(kxm, kxmg)) + list(zip(vx, vxg)) + list(zip(kxf, kxfg)))
        FB = 672 + 1536
        SB = 672 + 1544
        lf_tm = P.sb("lf_tm", [128, KT_L, 8], F32)
        dt_tm = P.sb("dt_tm", [128, KT_L, 16], F32)
        a_tm = P.sb("a_tm", [128, KT_L, 16], F32)
        tmpr = P.sb("tmpr", [128, 16], F32)
        wf = load_w(FB, 8)
        for m, (c0, w) in enumerate(main):
            for j in range(4):
                kt = m * 4 + j
                ps = nxt_ps(0, 4)
                proj_tm(wf, 0, 8, c0 + j * 128, ps[:, 0:8])
                P.dve.tensor_tensor(out=tmpr[:, 0:8], in0=ps[:, 0:8], in1=rowc[:, 0:8], op=ALU.add)
                P.act.activation(out=tmpr[:, 0:8], in_=tmpr[:, 0:8], func=AF.Exp, scale=-1.0)
                P.act.activation(out=tmpr[:, 0:8], in_=tmpr[:, 0:8], func=AF.Ln, bias=one1[:, 0:1], scale=1.0)
                P.dve.tensor_scalar(out=lf_tm[:, kt, :], in0=tmpr[:, 0:8], scalar1=-1.0, scalar2=None, op0=ALU.mult)
        wd = load_w(SB + 1024 + 1536, 16)
        for m, (c0, w) in enumerate(main):
            for j in range(4):
                kt = m * 4 + j
                ps = nxt_ps(0, 4)
                proj_tm(wd, 0, 16, c0 + j * 128, ps[:, 0:16])
                P.dve.tensor_tensor(out=tmpr.full(), in0=ps[:, 0:16], in1=rowc[:, 8:24], op=ALU.add)
                P.act.activation(out=tmpr.full(), in_=tmpr.full(), func=AF.Exp)
                P.act.activation(out=dt_tm[:, kt, :], in_=tmpr.full(), func=AF.Ln, bias=one1[:, 0:1], scale=1.0)
                P.dve.tensor_tensor(out=a_tm[:, kt, :], in0=dt_tm[:, kt, :], in1=Aneg_r.full(), op=ALU.mult)
        P.dma("sync", out=dtd.full(), in_=dt_tm.full())
        P.dma("sync", out=atd.full(), in_=a_tm.full())
        def plain_group(base, ncols, func, bias_name, dst, dst_row0):
            wb_ = load_w(base, ncols)
            for m, (c0, w) in enumerate(main):
                for ch in range(ncols // 128):
                    ps = nxt_ps(0, 4)
                    proj(wb_, ch * 128, 128, c0, 512, ps.full())
                    og = next_ostg()
                    if bias_name is None:
                        P.act.activation(out=og.full(), in_=ps.full(), func=func)
                    else:
                        P.act.activation(out=og.full(), in_=ps.full(), func=func,
                                         bias=C.col(bias_name, (dst_row0 // 128) + ch))
                    out_dma(dst[dst_row0 + ch * 128:dst_row0 + (ch + 1) * 128, m * 512:(m + 1) * 512], og.full())

        for blk in range(2):
            plain_group(SB + blk * 512, 512, AF.Silu, None, szd, blk * 512)
        upre = P.sb("upre", [128, 516], F32)
        carry = P.sb("carry", [128, 4], F32)
        acc0 = P.sb("acc0", [128, 512], F32)
        tstg = [P.sb(f"tstg{i}", [128, 4, 128], BF16) for i in range(2)]
        tcnt = [0]
        for blk in range(3):
            wb = load_w(SB + 1024 + blk * 512, 512)
            for m, (c0, w) in enumerate(main):
                for ch in range(4):
                    cg = blk * 4 + ch
                    ps = nxt_ps(0, 4)
                    proj(wb, ch * 128, 128, c0 - 4, 4, ps[:, 0:4])
                    P.act.copy(out=upre[:, 0:4], in_=ps[:, 0:4])
                    ps = nxt_ps(0, 4)
                    proj(wb, ch * 128, 128, c0, 512, ps.full())
                    P.act.copy(out=upre[:, 4:516], in_=ps.full())
                    P.dve.tensor_scalar(out=acc0.full(), in0=upre[:, 4:516], scalar1=C.col("cw3", cg), scalar2=C.col("cb", cg),
                                        op0=ALU.mult, op1=ALU.add)
                    for k in range(3):
                        P.dve.scalar_tensor_tensor(out=acc0.full(), in0=upre[:, 1 + k:513 + k], scalar=C.col(f"cw{k}", cg),
                                                   in1=acc0.full(), op0=ALU.mult, op1=ALU.add)
                    og = next_ostg()
                    P.act.activation(out=og.full(), in_=acc0.full(), func=AF.Silu)
                    if cg >= 8:
                        out_dma(bct[(cg - 8) * 128:(cg - 7) * 128, m * 512:(m + 1) * 512], og.full())
                    if cg < 10:
                        i2 = tcnt[0] % 2
                        tcnt[0] += 1
                        for j in range(4):
                            P.pe.transpose(out=pbt[:, i2 * 512 + j * 128:i2 * 512 + (j + 1) * 128],
                                           in_=og[:, j * 128:(j + 1) * 128], identity=identb.full())
                        ts_ = tstg[i2]
                        P.dve.tensor_copy(out=ts_.full().re("p j f -> p (j f)"), in_=pbt[:, i2 * 512:(i2 + 1) * 512])
                        if cg < 8:
                            P.dma("sync", out=xtm[:, m * 4:(m + 1) * 4, cg * 128:(cg + 1) * 128], in_=ts_.full())
                        else:
                            P.dma("sync", out=btm[:, m * 4:(m + 1) * 4, (cg - 8) * 128:(cg - 7) * 128], in_=ts_.full())
        GB = SB + 2576
        for blk in range(6):
            plain_group(GB + blk * 512, 512, AF.Sigmoid, "b_gate", gd, blk * 512)
        fxs = P.sb("fxs", [128, 224], F32)
        within = P.sb("within", [128, KT_L, 8], F32)
        ttot = P.sb("ttot", [128, KT_L, 8], F32)
        f2 = lambda b: b.full().re("p a b -> p (a b)")
        ps = nxt_ps(0, 4)
        P.pe.matmul(out=ps[:, 0:128], lhsT=tri, rhs=f2(lf_tm), start=True, stop=True)
        P.act.copy(out=f2(within), in_=ps[:, 0:128])
        ps = nxt_ps(0, 4)
        P.pe.matmul(out=ps[:, 0:128], lhsT=ones, rhs=f2(lf_tm), start=True, stop=True)
        P.act.copy(out=f2(ttot), in_=ps[:, 0:128])
        Floc = fxs[:, 0:128].re("p (a b) -> p a b", b=8)
        totv = fxs[:, 128:160].re("p (a b) -> p a b", b=8)
        cacc = P.sb("cacc", [128, 8], F32)
        for m in range(4):
            P.dve.tensor_copy(out=Floc[:, 4 * m, :], in_=within[:, 4 * m, :])
            P.dve.tensor_copy(out=cacc.full(), in_=ttot[:, 4 * m, :])
            for j in range(1, 4):
                P.dve.tensor_tensor(out=Floc[:, 4 * m + j, :], in0=within[:, 4 * m + j, :], in1=cacc.full(), op=ALU.add)
                P.dve.tensor_tensor(out=cacc.full(), in0=cacc.full(), in1=ttot[:, 4 * m + j, :], op=ALU.add)
            P.dve.tensor_copy(out=totv[:, m, :], in_=cacc.full())
        ps = nxt_ps(0, 4)
        P.pe.transpose(out=ps[:, 0:128], in_=fxs[:, 0:128], identity=ident)
        FT = P.sb("FT", [128, 128], F32)
        r1 = P.sb("r1", [128, 128], F32)
        fh = [P.sb(f"fh{i}", [128, 128], BF16) for i in range(3)]
        P.act.copy(out=FT.full(), in_=ps[:, 0:128])
        P.dve.tensor_copy(out=fh[0].full(), in_=FT.full())
        P.dve.tensor_tensor(out=r1.full(), in0=FT.full(), in1=fh[0].full(), op=ALU.subtract)
        P.dve.tensor_copy(out=fh[1].full(), in_=r1.full())
        P.dve.tensor_tensor(out=r1.full(), in0=r1.full(), in1=fh[1].full(), op=ALU.subtract)
        P.dve.tensor_copy(out=fh[2].full(), in_=r1.full())
        for r in range(3):
            for kt in range(KT_L):
                P.dma("sync" if kt % 2 else "scalar", out=fq[:, r, kt * 128:(kt + 1) * 128], in_=fh[r][kt * 8:(kt + 1) * 8, :])
        P.dma("sync", out=fx[:, 0:160], in_=fxs[:, 0:160])

    def ssd_scan(l, K, pass1, fxs=None, dt_tm=None, a_tm=None, Hinit=None, rowc=None, pb=None):
        cmb = K["cm"]
        tri, trimask, ones = cmb[:, 0, :], cmb[:, 1, :], cmb[:, 3, :]
        if pb is None:
            pb = [P.ps(f"spb{i}", [128, 512], F32) for i in range(7)]
        if pass1:
            fxs = P.sb("decs", [128, 224], F32)
        if dt_tm is None:
            dt_tm = P.sb("dt_tm", [128, KT_L, 16], F32)
            a_tm = P.sb("a_tm", [128, KT_L, 16], F32)
            P.dma("sync", out=dt_tm.full(), in_=dtd.full())
            P.dma("sync", out=a_tm.full(), in_=atd.full())
        fl = lambda b: b.full().re("p c h -> p (c h)")
        Acum = P.sb("Acum", [128, KT_L, 16], F32)
        Atot = P.sb("Atot", [128, KT_L, 16], F32)
        wdec = P.sb("wdec", [128, KT_L, 16], F32)
        eAtot = P.sb("eAtot", [128, KT_L, 16], F32)
        psA = pb[0]
        P.pe.matmul(out=psA[:, 0:256], lhsT=tri, rhs=fl(a_tm), start=True, stop=True)
        P.act.copy(out=fl(Acum), in_=psA[:, 0:256])
        P.pe.matmul(out=psA[:, 256:512], lhsT=ones, rhs=fl(a_tm), start=True, stop=True)
        P.act.copy(out=fl(Atot), in_=psA[:, 256:512])
        P.act.activation(out=fl(eAtot), in_=fl(Atot), func=AF.Exp)
        P.dve.tensor_tensor(out=fl(wdec), in0=fl(Atot), in1=fl(Acum), op=ALU.subtract)
        P.act.activation(out=fl(wdec), in_=fl(wdec), func=AF.Exp)
        if not pass1:
            nAcum = P.sb("nAcum", [128, KT_L, 16], F32)
            eA = P.sb("eA", [128, KT_L, 16], F32)
            P.dve.tensor_scalar(out=fl(nAcum), in0=fl(Acum), scalar1=-1.0, scalar2=None, op0=ALU.mult)
            P.act.activation(out=fl(eA), in_=fl(Acum), func=AF.Exp)
            BCs = P.sb("BCs", [128, 4, T], BF16)
            P.dma("gpsimd", out=BCs.full(), in_=bct.full().re("(a p) t -> p a t", p=128))
            cb = P.sb("cb", [128, 2, 128], F32)
            NH = 4
            at = [P.sb(f"at{i}", [128, 128], F32) for i in range(NH)]
            tm = [P.sb(f"tm{i}", [128, 128], F32) for i in range(NH)]
            dec = [P.sb(f"dec{i}", [128, 128], F32) for i in range(NH)]
            MT = [P.sb(f"MT{i}", [128, 128], BF16) for i in range(NH)]
            t1 = P.sb("t1", [128, 1024], F32)
            t3 = P.sb("t3", [128, 1024], F32)
            yo = P.sb("yo", [128, 1024], BF16)
            yT = [P.sb(f"yT{i}", [128, 4, 128], F32) for i in range(2)]
        Hs = P.sb("Hs", [128, 1024], F32)
        Hb = P.sb("Hb", [128, 1024], BF16)
        xc = [P.sb(f"xc{i}", [128, 1024], BF16) for i in range(2)]
        Bc = [P.sb(f"Bc{i}", [128, 256], BF16) for i in range(2)]
        xdt = P.sb("xdt", [128, 1024], BF16)
        xdts = P.sb("xdts", [128, 1024], BF16)
        dsum = P.sb("dsum", [128, 16], F32)
        v3 = lambda v: v.re("p (h d) -> p h d", h=16)
        bc3 = lambda v: v.f(lambda a: a.unsqueeze(2).to_broadcast([128, 16, 64]))
        for m in range(4):
            if pass1:
                P.dve.memset(ap=Hs.full(), constant=0.0)
                P.dve.memset(ap=dsum.full(), constant=0.0)
            else:
                P.dve.tensor_copy(out=Hs.full(), in_=Hinit[:, m, :])
                P.act.copy(out=Hb.full(), in_=Hinit[:, m, :])
            for j in range(4):
                c = m * 4 + j
                x_c = xc[c % 2]
                B_c = Bc[c % 2]
                P.dma("sync", out=x_c.full(), in_=xtm[:, c, :])
                P.dma("gpsimd", out=B_c.full(), in_=btm[:, c, :])
                P.dve.tensor_tensor(out=v3(xdt.full()), in0=v3(x_c.full()), in1=bc3(dt_tm[:, c, :]), op=ALU.mult)
                P.pool.tensor_tensor(out=v3(xdts.full()), in0=v3(xdt.full()), in1=bc3(wdec[:, c, :]), op=ALU.mult)
                if not pass1:
                    cs = slice(c * 128, (c + 1) * 128)
                    ps_cb = pb[1]
                    for g in range(2):
                        P.pe.matmul(out=ps_cb[:, g * 128:(g + 1) * 128], lhsT=BCs[:, g, cs], rhs=BCs[:, 2 + g, cs],
                                    start=True, stop=True)
                    P.act.copy(out=cb.full().re("p a b -> p (a b)"), in_=ps_cb[:, 0:256])
                    ps_off = [pb[2], pb[3]]
                    for g in range(2):
                        P.pe.matmul(out=ps_off[g].full(), lhsT=BCs[:, 2 + g, cs], rhs=Hb[:, g * 512:(g + 1) * 512],
                                    start=True, stop=True)
                    ps_y = [pb[4], pb[5]]
                    def st1(h):
                        i2 = h % NH
                        g = h // 8
                        P.dve.tensor_scalar(out=at[i2].full(), in0=tri, scalar1=a_tm[:, c, h:h + 1], scalar2=None, op0=ALU.mult)
                        ps_A = pb[6]
                        P.pe.matmul(out=ps_A[:, i2 * 128:(i2 + 1) * 128], lhsT=ones, rhs=at[i2].full(), start=True, stop=True)
                        P.dve.tensor_tensor(out=tm[i2].full(), in0=ps_A[:, i2 * 128:(i2 + 1) * 128], in1=trimask, op=ALU.add)
                        P.act.activation(out=dec[i2].full(), in_=tm[i2].full(), func=AF.Exp, bias=nAcum[:, c, h:h + 1], scale=1.0)
                        P.pool.tensor_tensor(out=MT[i2].full(), in0=cb[:, g, :], in1=dec[i2].full(), op=ALU.mult)

                    def st2(h):
                        i2 = h % NH
                        g = h // 8
                        hh = h % 8
                        P.pe.matmul(out=ps_y[g][:, hh * 64:(hh + 1) * 64], lhsT=MT[i2].full(), rhs=xdt[:, h * 64:(h + 1) * 64],
                                    start=True, stop=True)

                    for hq in range(16 + 3):
                        if hq < 16:
                            st1(hq)
                        if hq >= 3:
                            st2(hq - 3)
                    for g in range(2):
                        gs_ = slice(g * 512, (g + 1) * 512)
                        v8 = lambda v: v.re("p (h d) -> p h d", h=8)
                        b8 = lambda v: v.f(lambda a: a.unsqueeze(2).to_broadcast([128, 8, 64]))
                        P.dve.tensor_tensor(out=v8(t1[:, gs_]), in0=v8(ps_off[g].full()), in1=b8(eA[:, c, g * 8:(g + 1) * 8]), op=ALU.mult)
                        P.dve.tensor_tensor(out=t1[:, gs_], in0=t1[:, gs_], in1=ps_y[g].full(), op=ALU.add)
                    P.pool.tensor_tensor(out=v3(t3.full()), in0=v3(x_c.full()), in1=bc3(rowc[:, 40:56]), op=ALU.mult)
                    P.pool.tensor_tensor(out=t3.full(), in0=t1.full(), in1=t3.full(), op=ALU.add)
                    for q4 in range(2):
                        pst = pb[2 + q4]
                        for jj in range(4):
                            fc = q4 * 4 + jj
                            P.pe.transpose(out=pst[:, jj * 128:(jj + 1) * 128], in_=t3[:, fc * 128:(fc + 1) * 128],
                                           identity=cmb[:, 2, :])
                        yt = yT[q4]
                        P.act.copy(out=yt.full().re("p a b -> p (a b)"), in_=pst.full())
                        P.dma("sync", out=yd[q4 * 512:(q4 + 1) * 512, c * 128:(c + 1) * 128].re("(a p) t -> p a t", p=128),
                              in_=yt.full())
                ps_h = [pb[0], pb[1]] if pass1 else [pb[4], pb[5]]
                for g in range(2):
                    P.pe.matmul(out=ps_h[g].full(), lhsT=B_c[:, g * 128:(g + 1) * 128], rhs=xdts[:, g * 512:(g + 1) * 512],
                                start=True, stop=True)
                P.dve.tensor_tensor(out=v3(Hs.full()), in0=v3(Hs.full()), in1=bc3(eAtot[:, c, :]), op=ALU.mult)
                for g in range(2):
                    P.dve.tensor_tensor(out=Hs[:, g * 512:(g + 1) * 512], in0=Hs[:, g * 512:(g + 1) * 512], in1=ps_h[g].full(), op=ALU.add)
                if pass1:
                    P.dve.tensor_tensor(out=dsum.full(), in0=dsum.full(), in1=Atot[:, c, :], op=ALU.add)
                else:
                    P.act.copy(out=Hb.full(), in_=Hs.full())
            if pass1:
                P.dma("sync", out=sx[m // 2][(m % 2) * 128:(m % 2 + 1) * 128, :], in_=Hs.full())
                P.act.activation(out=fxs[:, 160 + m * 16:160 + (m + 1) * 16], in_=dsum.full(), func=AF.Exp)
        if pass1:
            P.dma("sync", out=fx[:, 160:224], in_=fxs[:, 160:224])

    def load_fg():
        fg = P.sb("fg", [128, 4, 224], F32)
        P.dma("sync", out=fg.full(), in_=fxg.full().re("(r p) c -> p r c", p=128))
        return fg

    def phase_attn(l):
        K = load_consts()
        sel, zero = K["sel"], K["zero"]
        mskb = P.sb("mskb", [128, 8, 512], F32)
        P.dma("gpsimd", out=mskb.full(), in_=msk_d.full())
        fg = load_fg()
        offs = P.sb("offs", [128, 16, 8], F32)
        run = P.sb("run", [128, 8], F32)
        P.dve.memset(ap=run.full(), constant=0.0)
        for s_ in range(16):
            m, r = divmod(s_, 4)
            P.dve.tensor_copy(out=offs[:, s_, :], in_=run.full())
            P.dve.tensor_tensor(out=run.full(), in0=run.full(), in1=fg[:, r, 128 + m * 8:128 + (m + 1) * 8], op=ALU.add)
        offown = P.sb("offown", [128, 4, 8], F32)
        P.dve.memset(ap=offown.full(), constant=0.0)
        for m in range(4):
            for r in range(4):
                P.dve.scalar_tensor_tensor(out=offown[:, m, :], in0=offs[:, 4 * m + r, :], scalar=sel[:, 8 + 4 * m + r:9 + 4 * m + r],
                                           in1=offown[:, m, :], op0=ALU.mult, op1=ALU.add)
        negFg = P.sb("negFg", [128, 64, 8], F32)
        for s_ in range(16):
            m, r = divmod(s_, 4)
            src = fg[:, r, 0:128].re("p (a b) -> p a b", b=8)[:, 4 * m:4 * m + 4, :]
            P.dve.tensor_tensor(out=negFg[:, 4 * s_:4 * s_ + 4, :], in0=src,
                                in1=offs[:, s_, :].f(lambda a: a.unsqueeze(1).to_broadcast([128, 4, 8])), op=ALU.add)
        P.dve.tensor_scalar(out=negFg.full(), in0=negFg.full(), scalar1=-1.0, scalar2=None, op0=ALU.mult)
        biasm = P.sb("biasm", [128, 4, 64, 8], F32)
        for m in range(4):
            nk = (4 * m + 4) * 4
            P.dve.tensor_tensor(out=biasm[:, m, 0:nk, :], in0=negFg[:, 0:nk, :],
                                in1=offown[:, m, :].f(lambda a: a.unsqueeze(1).to_broadcast([128, nk, 8])), op=ALU.add)
            for jr in range(4):
                k0 = (4 * m + jr) * 4
                P.dve.tensor_scalar(out=biasm[:, m, k0:k0 + 4, :], in0=biasm[:, m, k0:k0 + 4, :],
                                    scalar1=sel[:, 28 + jr:29 + jr], scalar2=None, op0=ALU.add)

        pb = [P.ps(f"pb{i}", [128, 512], F32) for i in range(8)]
        K_sb = [P.sb(f"K_sb{i}", [96, S_], BF16) for i in range(2)]
        Q_sb = [P.sb(f"Q_sb{i}", [96, T], BF16) for i in range(2)]
        V_sb = [P.sb(f"V_sb{i}", [128, NKT, 128], BF16) for i in range(2)]
        for i in range(2):
            P.dve.memset(ap=V_sb[i][:, :, 64:128], constant=1.0)
        NSB = 4
        LA = 2
        WARM = False
        pt = [P.sb(f"pt{i}", [128, 512], BF16) for i in range(NSB)]
        mt = [P.sb(f"mt{i}", [128, 512], F32) for i in range(2)]
        rl = P.sb("rl", [128, 512], F32)
        rl2 = P.sb("rl2", [64, 512], F32)
        ot = [P.sb(f"ot{i}", [64, 512], BF16) for i in range(2)]
        cnt = [0, 0, 0]
        heads = [(0, h) for h in range(8)] + [(1, h) for h in range(8)]

        def loads(idx):
            kind, h = heads[idx]
            i = idx % 2
            nd = 96 if kind == 0 else 64
            for r in range(4):
                vr = r * 2048 + kind * 1024 + h * 128
                for m in range(4):
                    s0 = (4 * m + r) * 512
                    if kind == 0:
                        ksrc = kxmg[m][r * 768 + h * 96:r * 768 + (h + 1) * 96, :]
                    else:
                        ksrc = kxfg[m][r * 512 + h * 64:r * 512 + (h + 1) * 64, :]
                    P.dma("sync" if (r + m) % 2 == 0 else "gpsimd", out=K_sb[i][0:nd, s0:s0 + 512], in_=ksrc)
                    g0 = (4 * m + r) * 4
                    P.dma("gpsimd" if (r + m) % 2 == 0 else "sync",
                          out=V_sb[i][:, g0:g0 + 4, 0:64],
                          in_=vxg[m][vr:vr + 128, :].re("p (j d) -> p j d", j=4))
            if kind == 0:
                P.dma("sync", out=Q_sb[i][0:96, :], in_=qm[h])
            else:
                P.dve.memset(ap=K_sb[i][64:96, :], constant=0.0)
                P.dve.memset(ap=K_sb[i][64:67, :], constant=8.0)
                P.dma("sync", out=Q_sb[i][0:64, :], in_=qf[h])
                P.pool.memset(ap=Q_sb[i][64:96, :], constant=0.0)
                P.dma("gpsimd", out=Q_sb[i][64:67, :], in_=fq[h])

        iters = []
        for idx in range(16):
            for m in range(4):
                nk = (4 * m + 4) * 4
                for kt in range(nk):
                    iters.append((idx, m, kt, nk))

        def stage_qk(n):
            idx, m, kt, nk = iters[n]
            kind, h = heads[idx]
            i = idx % 2
            dk = 96
            scale = 96.0 ** -0.5 if kind == 0 else 0.125
            i3 = n % NSB
            ps = pb[i3]
            P.pe.matmul(out=ps.full(), lhsT=K_sb[i][0:dk, kt * 128:(kt + 1) * 128],
                        rhs=Q_sb[i][0:dk, m * 512:(m + 1) * 512], start=True, stop=True)
            blk = kt // 4
            if blk >= 4 * m:
                jr = blk - 4 * m
                mm = mt[cnt[1] % 2]
                cnt[1] += 1
                P.dve.scalar_tensor_tensor(out=mm.full(), in0=mskb[:, kind * 4 + kt % 4, :],
                                           scalar=sel[:, 24 + jr:25 + jr], in1=ps.full(), op0=ALU.mult, op1=ALU.add)
                src = mm.full()
                bias = sel[:, 28 + jr:29 + jr] if kind == 0 else biasm[:, m, kt, h:h + 1]
            else:
                src = ps.full()
                bias = zero[:, 0:1] if kind == 0 else biasm[:, m, kt, h:h + 1]
            P.act.activation(out=pt[i3].full(), in_=src, func=AF.Exp, scale=scale, bias=bias)
            if WARM:
                P.pe.matmul(out=pb[6][:, 0:256], lhsT=K_sb[i][0:dk, kt * 128:(kt + 1) * 128],
                            rhs=Q_sb[i][0:dk, m * 512:m * 512 + 256], start=True, stop=True)

        def stage_pv(n):
            idx, m, kt, nk = iters[n]
            kind, h = heads[idx]
            i = idx % 2
            oacc = pb[4 + (idx * 4 + m) % 2]
            P.pe.matmul(out=oacc.full(), lhsT=V_sb[i][:, kt, :], rhs=pt[n % NSB].full(), start=(kt == 0), stop=(kt == nk - 1))
            if kt == nk - 1:
                odst = omd if kind == 0 else ofd
                P.dve.reciprocal(out=rl[64:128, :], in_=oacc[64:128, :])
                P.dve.tensor_copy(out=rl2.full(), in_=rl[64:128, :])
                o = ot[m % 2]
                P.dve.tensor_tensor(out=o.full(), in0=oacc[0:64, :], in1=rl2.full(), op=ALU.mult)
                P.dma("sync", out=odst[h * 64:(h + 1) * 64, m * 512:(m + 1) * 512], in_=o.full())

        pcs = [P.sb(f"pcs{i}", [128, 2048], F32) for i in range(2)]
        pcb = [P.sb(f"pcb{i}", [128, 2048], BF16) for i in range(2)]
        jobs = []
        for (src, dst, rows, cols) in ((w_a[l], wab, 512, D), (w_b[l], wbb, 512, D), (w_c[l], wcb, 1024, D), (w_o[l], wob, 1024, D),
                                       (w_up[l], wub, D, 5632), (w_dn[l], wdb, 2816, D)):
            for r0 in range(0, rows, 128):
                for c0 in range(0, cols, 2048):
                    n_ = min(2048, cols - c0)
                    jobs.append((src[r0:r0 + 128, c0:c0 + n_], dst[r0:r0 + 128, c0:c0 + n_], n_))
        jcnt = [0]

        def precast_one():
            if jcnt[0] >= len(jobs):
                return
            src, dst, n_ = jobs[jcnt[0]]
            i = jcnt[0] % 2
            jcnt[0] += 1
            P.dma("gpsimd", out=pcs[i][:, 0:n_], in_=src)
            P.pool.tensor_copy(out=pcb[i][:, 0:n_], in_=pcs[i][:, 0:n_])
            P.dma("gpsimd", out=dst, in_=pcb[i][:, 0:n_])

        every = max(1, len(iters) // (len(jobs) + 4))
        loads(0)
        loads(1)
        for n in range(len(iters) + LA):
            if n < len(iters):
                stage_qk(n)
            if n >= LA:
                stage_pv(n - LA)
                idx_p, m_p, kt_p, nk_p = iters[n - LA]
                if m_p == 3 and kt_p == nk_p - 1 and idx_p + 2 < 16:
                    loads(idx_p + 2)
            if n % every == every - 1:
                precast_one()
        while jcnt[0] < len(jobs):
            precast_one()

    def phase_ssd2(l):
        K = load_consts()
        sel = K["sel"]
        rowc = P.sb("rowc", [128, 56], F32)
        P.dma("sync", out=rowc.full(), in_=rowc_d[l])
        fg = load_fg()
        Hin = P.sb("Hin", [128, 1024], F32)
        Hsel = P.sb("Hsel", [128, 4, 1024], F32)
        Sst = [P.sb(f"Sst{i}", [128, 1024], F32) for i in range(2)]
        P.dve.memset(ap=Hin.full(), constant=0.0)
        P.dve.memset(ap=Hsel.full(), constant=0.0)
        v3 = lambda v: v.re("p (h d) -> p h d", h=16)
        for s_ in range(16):
            m, r = divmod(s_, 4)
            P.dve.scalar_tensor_tensor(out=Hsel[:, m, :], in0=Hin.full(), scalar=sel[:, 8 + s_:9 + s_], in1=Hsel[:, m, :],
                                       op0=ALU.mult, op1=ALU.add)
            if s_ < 15:
                st_ = Sst[s_ % 2]
                P.dma("sync" if s_ % 2 else "gpsimd", out=st_.full(),
                      in_=sxg[m // 2][r * 256 + (m % 2) * 128:r * 256 + (m % 2 + 1) * 128, :])
                dcs = fg[:, r, 160 + m * 16:160 + (m + 1) * 16]
                P.dve.tensor_tensor(out=v3(Hin.full()), in0=v3(Hin.full()),
                                    in1=dcs.f(lambda a: a.unsqueeze(2).to_broadcast([128, 16, 64])), op=ALU.mult)
                P.pool.tensor_tensor(out=Hin.full(), in0=Hin.full(), in1=st_.full(), op=ALU.add)
        ssd_scan(l, K, pass1=False, Hinit=Hsel, rowc=rowc)

    def write_tails(txs):
        P.dma("sync", out=tx.full(), in_=txs.full().re("p m k c -> p (m k c)"))

    def halo_exchange(dst):
        K = load_consts()
        sel = K["sel"]
        P.pool.collective_compute(kind="AllGather", op=ALU.bypass, replica_groups=RG, ins=[tx.full()], outs=[txg.full()])
        tg = P.sb("tg", [128, 4, 128], F32)
        P.dma("sync", out=tg.full(), in_=txg.full().re("(r p) c -> p r c", p=128))
        hl = P.sb("hl", [128, 4, 32], F32)
        P.dve.memset(ap=hl.full(), constant=0.0)
        for m in range(4):
            for r in range(4):
                P.dve.scalar_tensor_tensor(out=hl[:, m, :], in0=tg[:, r, m * 32:(m + 1) * 32], scalar=sel[:, r:r + 1],
                                           in1=hl[:, m, :], op0=ALU.mult, op1=ALU.add)
            if m >= 1:
                P.dve.scalar_tensor_tensor(out=hl[:, m, :], in0=tg[:, 3, (m - 1) * 32:m * 32], scalar=sel[:, 4:5],
                                           in1=hl[:, m, :], op0=ALU.mult, op1=ALU.add)
        dv = dst.full().re("(kc p) n -> p kc n", p=128)
        for m in range(4):
            P.dma("sync", out=dv[:, :, m * SW:m * SW + 4], in_=hl[:, m, :].re("p (k c) -> p k c", c=4))

    def phase_merge(l):
        K = load_consts()
        ones, eps = K["cm"][:, 3, :], K["eps"]
        gsb = P.sb("gsb", [128, 8], F32)
        P.dma("sync", out=gsb.full(), in_=gssm_d[l])
        pb = [P.ps(f"pb{i}", [128, 512], F32) for i in range(8)]
        def load_wb(dram_bf, kc_n, name, q):
            bfb = P.sb(name, [128, kc_n, D], BF16)
            P.dma(q, out=bfb.full(), in_=dram_bf.full().re("(kc p) n -> p kc n", p=128))
            return bfb

        Wa = load_wb(wab, 4, "Wa", "sync")
        Wb = load_wb(wbb, 4, "Wb", "gpsimd")
        Wc = load_wb(wcb, 8, "Wc", "sync")
        Wo = load_wb(wob, 8, "Wo", "gpsimd")
        ys = P.sb("ys", [128, 8, 512], F32)
        szs = P.sb("szs", [128, 8, 512], BF16)
        yn = P.sb("yn", [128, 8, 512], BF16)
        oms = P.sb("oms", [128, 4, 512], BF16)
        ofs = P.sb("ofs", [128, 4, 512], BF16)
        gs = P.sb("gs", [128, 24, 512], BF16)
        xs = P.sb("xs", [128, 8, 512], F32)
        sq = P.sb("sq", [128, 512], F32)
        rstd = P.sb("rstd", [128, 512], F32)
        m1 = [P.sb(f"m1_{i}", [128, 512], F32) for i in range(2)]
        m2 = [P.sb(f"m2_{i}", [128, 512], F32) for i in range(2)]
        m3 = [P.sb(f"m3_{i}", [128, 512], F32) for i in range(2)]
        mg = P.sb("mg", [128, 8, 512], BF16)
        xo = [P.sb(f"xo{i}", [128, 512], F32) for i in range(2)]
        txs = P.sb("txs", [128, 4, 8, 4], F32)
        ch = lambda d: d.full().re("(kc p) n -> p kc n", p=128)
        xmv = ch(xmid)
        for ti in range(4):
            ts = slice(ti * 512, (ti + 1) * 512)
            xsl = slice(ti * SW + 4, ti * SW + 516)
            P.dma("sync", out=ys.full(), in_=ch(yd)[:, :, ts])
            P.dma("gpsimd", out=szs.full(), in_=ch(szd)[:, :, ts])
            P.dma("sync", out=oms.full(), in_=ch(omd)[:, :, ts])
            P.dma("gpsimd", out=ofs.full(), in_=ch(ofd)[:, :, ts])
            P.dma("sync", out=gs.full(), in_=ch(gd)[:, :, ts])
            P.dma("gpsimd", out=xs.full(), in_=ch(xb[l])[:, :, xsl])
            ps = pb[7]
            for kc in range(8):
                P.dve.tensor_tensor(out=ys[:, kc, :], in0=ys[:, kc, :], in1=szs[:, kc, :], op=ALU.mult)
                P.act.activation(out=sq.full(), in_=ys[:, kc, :], func=AF.Square)
                P.pe.matmul(out=ps.full(), lhsT=ones, rhs=sq.full(), start=(kc == 0), stop=(kc == 7))
            P.act.activation(out=rstd.full(), in_=ps.full(), func=AF.Sqrt, bias=eps[:, 0:1], scale=1.0 / 1024.0)
            P.dve.reciprocal(out=rstd.full(), in_=rstd.full())
            for kc in range(8):
                P.dve.scalar_tensor_tensor(out=yn[:, kc, :], in0=ys[:, kc, :], scalar=gsb[:, kc:kc + 1], in1=rstd.full(),
                                           op0=ALU.mult, op1=ALU.mult)
            for oc in range(8):
                i2 = oc % 2
                osl = slice(oc * 128, (oc + 1) * 128)
                pa, pbb, pc = pb[0 + i2 * 3], pb[1 + i2 * 3], pb[2 + i2 * 3]
                for kc in range(4):
                    P.pe.matmul(out=pa.full(), lhsT=Wa[:, kc, osl], rhs=oms[:, kc, :], start=(kc == 0), stop=(kc == 3))
                for kc in range(4):
                    P.pe.matmul(out=pbb.full(), lhsT=Wb[:, kc, osl], rhs=ofs[:, kc, :], start=(kc == 0), stop=(kc == 3))
                for kc in range(8):
                    P.pe.matmul(out=pc.full(), lhsT=Wc[:, kc, osl], rhs=yn[:, kc, :], start=(kc == 0), stop=(kc == 7))
                P.dve.tensor_tensor(out=m1[i2].full(), in0=pa.full(), in1=gs[:, oc, :], op=ALU.mult)
                P.dve.tensor_tensor(out=m2[i2].full(), in0=pbb.full(), in1=gs[:, 8 + oc, :], op=ALU.mult)
                P.dve.tensor_tensor(out=m3[i2].full(), in0=pc.full(), in1=gs[:, 16 + oc, :], op=ALU.mult)
                P.pool.tensor_tensor(out=m1[i2].full(), in0=m1[i2].full(), in1=m2[i2].full(), op=ALU.add)
                P.pool.tensor_tensor(out=mg[:, oc, :], in0=m1[i2].full(), in1=m3[i2].full(), op=ALU.add)
            for oc in range(8):
                i2 = oc % 2
                ps = pb[6 + i2]
                for kc in range(8):
                    P.pe.matmul(out=ps.full(), lhsT=Wo[:, kc, oc * 128:(oc + 1) * 128], rhs=mg[:, kc, :],
                                start=(kc == 0), stop=(kc == 7))
                P.dve.tensor_tensor(out=xo[i2].full(), in0=ps.full(), in1=xs[:, oc, :], op=ALU.add)
                P.pool.tensor_copy(out=txs[:, ti, oc, :], in_=xo[i2][:, 508:512])
                P.dma("sync" if i2 else "gpsimd", out=xmv[:, oc, xsl], in_=xo[i2].full())
        write_tails(txs)

    def phase_ffn(l, last):
        K = load_consts()
        ones, eps = K["cm"][:, 3, :], K["eps"]
        cstb = P.sb("cstb", [128, offD2["_n"]], F32)
        C = Cst(P, cstb, offD2)
        P.dma("sync", out=cstb.full(), in_=cstD_d[l])
        pb = [P.ps(f"pb{i}", [128, 512], F32) for i in range(8)]
        Wu = P.sb("Wu", [128, 8, 5632], BF16)
        Wd = P.sb("Wd", [128, 22, D], BF16)
        wubv = wub.full().re("(kc p) n -> p kc n", p=128)
        for (c0, c1) in ((0, 512), (2816, 3328), (512, 2816), (3328, 5632)):
            P.dma("sync" if c0 < 2816 else "gpsimd", out=Wu[:, :, c0:c1], in_=wubv[:, :, c0:c1])
        wdbv = wdb.full().re("(kc p) n -> p kc n", p=128)
        P.dma("sync", out=Wd[:, 0:11, :], in_=wdbv[:, 0:11, :])
        P.dma("gpsimd", out=Wd[:, 11:22, :], in_=wdbv[:, 11:22, :])
        xst = P.sb("xst", [128, 8, 512], F32)
        hn = P.sb("hn", [128, 8, 512], BF16)
        sq = P.sb("sq", [128, 512], F32)
        rstd = P.sb("rstd", [128, 512], F32)
        act = P.sb("act", [128, 22, 512], BF16)
        upre = [P.sb(f"upre{i}", [128, 516], F32) for i in range(2)]
        acc = [P.sb(f"acc{i}", [128, 512], F32) for i in range(2)]
        sg = P.sb("sg", [128, 512], F32)
        carry = P.sb("carry", [128, 44, 4], F32)
        xo = [P.sb(f"xo{i}", [128, 512], F32) for i in range(2)]
        txs = P.sb("txs", [128, 4, 8, 4], F32)
        xTv = xmid.full().re("(kc p) n -> p kc n", p=128)
        dst = out if last else xb[l + 1]
        dv = dst.full().re("(kc p) n -> p kc n", p=128)
        tiles = []
        for m in range(4):
            tiles.append((m * SW, 4, True, m))
            tiles.append((m * SW + 4, 512, False, m))
        pcnt = [0]
        for (c0, w, is_halo, m) in tiles:
            P.dma("sync", out=xst[:, :, 0:w], in_=xTv[:, :, c0:c0 + w])
            ps = pb[7]
            for kc in range(8):
                P.act.activation(out=sq[:, 0:w], in_=xst[:, kc, 0:w], func=AF.Square)
                P.pe.matmul(out=ps[:, 0:w], lhsT=ones, rhs=sq[:, 0:w], start=(kc == 0), stop=(kc == 7))
            P.act.activation(out=rstd[:, 0:w], in_=ps[:, 0:w], func=AF.Sqrt, bias=eps[:, 0:1], scale=1.0 / 1024.0)
            P.dve.reciprocal(out=rstd[:, 0:w], in_=rstd[:, 0:w])
            for kc in range(8):
                P.dve.scalar_tensor_tensor(out=hn[:, kc, 0:w], in0=xst[:, kc, 0:w], scalar=C.col("g_ffn", kc),
                                           in1=rstd[:, 0:w], op0=ALU.mult, op1=ALU.mult)
            for i in range(22):
                accs = []
                for j, cg in enumerate((i, 22 + i)):
                    ps = pb[pcnt[0] % 4]
                    pcnt[0] += 1
                    for kc in range(8):
                        P.pe.matmul(out=ps[:, 0:w], lhsT=Wu[:, kc, cg * 128:(cg + 1) * 128], rhs=hn[:, kc, 0:w],
                                    start=(kc == 0), stop=(kc == 7))
                    if is_halo:
                        P.act.copy(out=carry[:, cg, :], in_=ps[:, 0:4])
                        continue
                    up = upre[j]
                    P.act.copy(out=up[:, 4:516], in_=ps.full())
                    P.dve.tensor_copy(out=up[:, 0:4], in_=carry[:, cg, :])
                    a0 = acc[j]
                    P.dve.tensor_scalar(out=a0.full(), in0=up[:, 4:516], scalar1=C.col("fw2", cg), scalar2=C.col("fb", cg),
                                        op0=ALU.mult, op1=ALU.add)
                    P.dve.scalar_tensor_tensor(out=a0.full(), in0=up[:, 3:515], scalar=C.col("fw1", cg), in1=a0.full(),
                                               op0=ALU.mult, op1=ALU.add)
                    P.dve.scalar_tensor_tensor(out=a0.full(), in0=up[:, 2:514], scalar=C.col("fw0", cg), in1=a0.full(),
                                               op0=ALU.mult, op1=ALU.add)
                    accs.append(a0)
                if is_halo:
                    continue
                P.act.activation(out=sg.full(), in_=accs[0].full(), func=AF.Silu)
                P.pool.tensor_tensor(out=act[:, i, :], in0=sg.full(), in1=accs[1].full(), op=ALU.mult)
            if is_halo:
                continue
            for oc in range(8):
                i2 = oc % 2
                ps = pb[4 + i2]
                for i in range(22):
                    P.pe.matmul(out=ps.full(), lhsT=Wd[:, i, oc * 128:(oc + 1) * 128], rhs=act[:, i, :],
                                start=(i == 0), stop=(i == 21))
                P.dve.tensor_tensor(out=xo[i2].full(), in0=ps.full(), in1=xst[:, oc, :], op=ALU.add)
                if last:
                    P.dma("sync" if i2 else "gpsimd", out=dv[:, oc, m * 512:(m + 1) * 512], in_=xo[i2].full())
                else:
                    P.pool.tensor_copy(out=txs[:, m, oc, :], in_=xo[i2][:, 508:512])
                    P.dma("sync" if i2 else "gpsimd", out=dv[:, oc, m * SW + 4:m * SW + 516], in_=xo[i2].full())
        if not last:
            write_tails(txs)

    def gather_e1():
        gather_pairs(list(zip(sx, sxg)) + [(fx, fxg)])

    nl = L if stop is None else stop[0]
    done = False
    for l in range(nl):
        last_l = (stop is not None and l == nl - 1)
        phase_A(l)
        P.emit(final=False)
        ssd_scan(l, load_consts(), pass1=True)
        P.emit(final=False)
        if last_l and stop[1] == "A":
            break
        gather_e1()
        phase_attn(l)
        P.emit(final=False)
        phase_ssd2(l)
        P.emit(final=False)
        if last_l and stop[1] == "B":
            break
        phase_merge(l)
        P.emit(final=False)
        halo_exchange(xmid)
        P.emit(final=False)
        if last_l and stop[1] == "C":
            break
        phase_ffn(l, last=(l == L - 1))
        P.emit(final=False)
        if l < L - 1:
            halo_exchange(xb[l + 1])
            P.emit(final=False)
    loc = {"kxmg0": kxmg[0], "vxg0": vxg[0], "sxg0": sxg[0], "fxg": fxg, "qm": qm, "qf": qf, "fq": fq, "omd": omd, "ofd": ofd, "yd": yd,
           "xmid": xmid, "xb1": xb[1], "szd": szd, "gd": gd, "xtm": xtm, "btm": btm, "bct": bct, "dtd": dtd, "atd": atd}
    for name in dbg:
        src = loc[name]
        shp = list(src.h.shape) if hasattr(src.h, "shape") else None
        dd = P.dram("dbg_" + name, shp, src.h.dtype, EO)
        P.dma("sync", out=dd.full(), in_=src.full())
    P.emit(final=True)
    return nc, P


def _stripe_tokens(p):
    return np.concatenate([np.arange((4 * m + p) * 512, (4 * m + p + 1) * 512) for m in range(4)])


def fused_in_maps(inp):
    L = 2
    cpsA = [a_colpack(inp, l) for l in range(L)]
    cpsD = [d2_colpack(inp, l) for l in range(L)]
    offA = dict(cpsA[0].off)
    offA["_n"] = cpsA[0].n
    offD = dict(cpsD[0].off)
    offD["_n"] = cpsD[0].n
    cstA = np.stack([c.array() for c in cpsA])
    cstD = np.stack([c.array() for c in cpsD])
    w_kp = np.zeros((L, 256, 8, 96), np.float32)
    wukv = inp["mla_w_ukv"].reshape(L, 256, 8, 128)
    w_kp[:, :, :, 0:64] = wukv[:, :, :, 0:64]
    w_v = np.ascontiguousarray(wukv[:, :, :, 64:128].reshape(L, 256, 512))
    gssm = np.ascontiguousarray(inp["ssm_norm_g"].reshape(L, 8, 128).transpose(0, 2, 1))
    rowc = np.stack([fused_rowpack(inp, l) for l in range(L)])
    msk, cm = _bc_consts()
    mats = _const_mats()
    shared = {
        "w_in": np.ascontiguousarray(inp["w_in"]), "w_uq": np.ascontiguousarray(inp["mla_w_uq"]),
        "w_kp": np.ascontiguousarray(w_kp.reshape(L, 256, 768)), "w_v": w_v,
        "w_a": np.ascontiguousarray(inp["w_br_mla"]), "w_b": np.ascontiguousarray(inp["w_br_fox"]),
        "w_c": np.ascontiguousarray(inp["w_br_ssm"]), "w_o": np.ascontiguousarray(inp["w_out"]),
        "w_up": np.ascontiguousarray(inp["ffn_w_up"]), "w_dn": np.ascontiguousarray(inp["ffn_w_down"]),
        "cstA": cstA, "cstD": cstD, "gssm": gssm, "rowc": rowc, "msk": msk, "cm": cm, "mats": mats,
    }
    in_maps = []
    for c in range(8):
        b, p = c // 4, c % 4
        xT = np.zeros((D, 4 * SW), np.float32)
        xbT = inp["x"][b].T
        for m in range(4):
            s_ = 4 * m + p
            xT[:, m * SW + 4:m * SW + 516] = xbT[:, s_ * 512:(s_ + 1) * 512]
            if s_ > 0:
                xT[:, m * SW:m * SW + 4] = xbT[:, s_ * 512 - 4:s_ * 512]
        sel = np.zeros((128, 32), np.float32)
        if p >= 1:
            sel[:, p - 1] = 1.0
        else:
            sel[:, 4] = 1.0
        for s_ in range(16):
            if s_ % 4 == p:
                sel[:, 8 + s_] = 1.0
        for jr in range(4):
            sel[:, 24 + jr] = 1.0 if jr == p else 0.0
            sel[:, 28 + jr] = NEG if jr > p else 0.0
        d = dict(shared)
        d["x0"] = np.ascontiguousarray(xT)
        d["pos"] = np.ascontiguousarray(inp["positions"][b][_stripe_tokens(p)][None, :]).astype(np.int32)
        d["sel"] = sel
        in_maps.append(d)
    return in_maps, offA, offD


def kernel_fused(**inp):
    inp = {k: np.asarray(v) for k, v in inp.items()}
    in_maps, offA, offD = fused_in_maps(inp)
    if "F" not in _PROG_CACHE:
        _PROG_CACHE["F"] = build_fused(offA, offD)[0]
    res = run_bass_kernel_spmd(_PROG_CACHE["F"], in_maps, core_ids=list(range(8))).results
    xo = np.zeros((2, S_, D), np.float32)
    for c in range(8):
        b, p = c // 4, c % 4
        xo[b, _stripe_tokens(p), :] = np.asarray(res[c]["out"]).T
    return xo
```

```python
from contextlib import ExitStack
import numpy as np
import concourse.bass as bass
import concourse.mybir as mybir

F32 = mybir.dt.float32
BF16 = mybir.dt.bfloat16
I32 = mybir.dt.int32
ALU = mybir.AluOpType
AF = mybir.ActivationFunctionType
AX = mybir.AxisListType

COMPUTE = ("tensor", "vector", "scalar", "gpsimd")
QUEUES = ("sync", "gpsimd", "scalar")
NRING = 8


class View:
    __slots__ = ("buf", "ap", "key")

    def __init__(self, buf, ap, key=None):
        self.buf = buf
        self.ap = ap
        self.key = key

    def __getitem__(self, k):
        return View(self.buf, self.ap[k], self.key)

    def re(self, s, **kw):
        return View(self.buf, self.ap.rearrange(s, **kw), self.key)

    def bc(self, shape):
        return View(self.buf, self.ap.to_broadcast(shape), self.key)

    def bitcast(self, dt):
        return View(self.buf, self.ap.bitcast(dt), self.key)

    def k(self, key):
        return View(self.buf, self.ap, key)

    def f(self, fn):
        return View(self.buf, fn(self.ap), self.key)


class Buf:
    def __init__(self, name, handle, is_dram=False):
        self.name = name
        self.h = handle
        self.is_dram = is_dram
        self.regions = {}

    def full(self):
        ap = self.h.ap() if hasattr(self.h, "ap") and callable(getattr(self.h, "ap")) else self.h[:]
        return View(self, ap)

    def __getitem__(self, k):
        return View(self, self.h[k])


class Op:
    __slots__ = ("id", "eng", "meth", "kw", "deps", "is_dma", "signaled", "sem", "val", "prewait", "eidx")


class Eng:
    def __init__(self, P, name):
        self.P = P
        self.name = name

    def __getattr__(self, meth):
        def call(*a, **kw):
            assert not a, "use kwargs"
            return self.P._record(self.name, meth, kw)
        return call


class Prog:
    def __init__(self, nc):
        self.nc = nc
        self.ops = []
        self.gstack = ExitStack()
        self.stack = ExitStack()
        self.pe = Eng(self, "tensor")
        self.dve = Eng(self, "vector")
        self.act = Eng(self, "scalar")
        self.pool = Eng(self, "gpsimd")
        self.sp = Eng(self, "sync")
        st = self.gstack
        self.csem = {e: st.enter_context(nc.semaphore(f"c_{e}")) for e in COMPUTE}
        self.rings = {q: [st.enter_context(nc.semaphore(f"d_{q}{i}")) for i in range(NRING)] for q in QUEUES}
        self.ccsem = st.enter_context(nc.semaphore("ccsem"))
        self.cccount = 0
        self.ccount = {e: 0 for e in COMPUTE}
        self.dcount = {q: 0 for q in QUEUES}
        self.waited = {e: {} for e in ("sync",) + COMPUTE}
        self.emitted = 0
        self.barrier = []
        self.stats = {}
        self.nwaits = 0

    def sb(self, name, shape, dtype):
        self.nuid = getattr(self, "nuid", 0) + 1
        name = f"{name}_s{self.nuid}"
        t = self.stack.enter_context(self.nc.sbuf_tensor(name, list(shape), dtype))
        return Buf(name, t)

    def ps(self, name, shape, dtype):
        self.nuid = getattr(self, "nuid", 0) + 1
        name = f"{name}_p{self.nuid}"
        t = self.stack.enter_context(self.nc.psum_tensor(name, list(shape), dtype))
        return Buf(name, t)

    def dram(self, name, shape, dtype, kind="Internal"):
        t = self.nc.dram_tensor(name, list(shape), dtype, kind=kind)
        return Buf(name, t, is_dram=True)

    def _record(self, eng, meth, kw):
        op = Op()
        op.id = len(self.ops)
        op.eng = eng
        op.meth = meth
        op.kw = kw
        op.is_dma = meth in ("dma_start", "dma_start_transpose", "collective_compute")
        op.signaled = False
        op.sem = None
        op.val = 0
        op.prewait = None
        deps = set()
        extra_r = kw.pop("_reads", [])
        extra_w = kw.pop("_writes", [])
        writes, reads = [], []
        for k, v in kw.items():
            vs = v if isinstance(v, (list, tuple)) else [v]
            for x in vs:
                if isinstance(x, View):
                    if k in ("out", "accum_out", "outs") or (k == "ap" and meth in ("memset", "memzero")):
                        writes.append(x)
                    else:
                        reads.append(x)
        reads += extra_r
        writes += extra_w
        for v in reads:
            self._gather(v, False, deps)
        for v in writes:
            self._gather(v, True, deps)
        for v in reads:
            self._update(v, False, op.id)
        for v in writes:
            self._update(v, True, op.id)
        deps.discard(op.id)
        op.deps = deps
        self.ops.append(op)
        return op

    def _gather(self, v, is_write, deps):
        R = v.buf.regions
        if v.key is None:
            regs = list(R.values())
        else:
            regs = [R[k] for k in (v.key, None) if k in R]
        for reg in regs:
            if reg[0] is not None:
                deps.add(reg[0])
            if is_write:
                deps.update(reg[1])

    def _update(self, v, is_write, oid):
        R = v.buf.regions
        if is_write:
            if v.key is None:
                R.clear()
            R[v.key] = [oid, []]
        else:
            R.setdefault(v.key, [None, []])[1].append(oid)

    def dma(self, q, out, in_, **kw):
        eng = {"sync": self.sp, "gpsimd": self.pool, "scalar": self.act}[q]
        return eng.dma_start(out=out, in_=in_, **kw)

    def emit(self, final=True):
        nc = self.nc
        ops = self.ops
        phase = ops[self.emitted:]
        first_id = self.emitted
        self.emitted = len(ops)
        for op in phase:
            for d in op.deps:
                dop = ops[d]
                if d < first_id:
                    continue
                if dop.eng == "tensor" and op.eng == "tensor" and not dop.is_dma and not op.is_dma:
                    continue
                dop.signaled = True
        per = {}
        for op in phase:
            per.setdefault(op.eng, []).append(op)
        for e, lst in per.items():
            for op in reversed(lst):
                if not op.is_dma:
                    op.signaled = True
                    break
        for op in phase:
            if op.meth == "collective_compute":
                self.cccount += 1
                op.sem = self.ccsem
                op.val = self.cccount
                op.signaled = True
            elif op.is_dma:
                k = self.dcount[op.eng]
                self.dcount[op.eng] += 1
                op.sem = self.rings[op.eng][k % NRING]
                op.val = 16 * (k // NRING + 1)
                if k >= NRING:
                    op.prewait = (op.sem, 16 * (k // NRING))
                op.signaled = True
            elif op.signaled:
                self.ccount[op.eng] += 1
                op.sem = self.csem[op.eng]
                op.val = self.ccount[op.eng]
        for e, v in per.items():
            self.stats[e] = self.stats.get(e, 0) + len(v)
        barrier_in = list(self.barrier)
        dcount = self.dcount
        rings = self.rings

        def dma_final_waits():
            ws = []
            for q in QUEUES:
                n = dcount[q]
                for i in range(min(n, NRING)):
                    cnt = (n - 1 - i) // NRING + 1
                    ws.append((rings[q][i], 16 * cnt))
            if self.cccount > 0:
                ws.append((self.ccsem, self.cccount))
            return ws

        def run(engname, e):
            waited = self.waited[engname]

            def do_waits(ws):
                for sem, val in ws:
                    key = id(sem)
                    if waited.get(key, 0) >= val:
                        continue
                    waited[key] = val
                    e.wait_ge(sem, val)
                    self.nwaits += 1

            do_waits(barrier_in)
            for op in per.get(engname, []):
                ws = []
                if op.prewait is not None:
                    ws.append(op.prewait)
                for d in sorted(op.deps):
                    dop = ops[d]
                    if dop.sem is None:
                        continue
                    if dop.eng == "tensor" and op.eng == "tensor" and not dop.is_dma and not op.is_dma:
                        continue
                    ws.append((dop.sem, dop.val))
                do_waits(ws)
                kw = {}
                for k, v in op.kw.items():
                    if isinstance(v, View):
                        kw[k] = v.ap
                    elif isinstance(v, (list, tuple)) and v and isinstance(v[0], View):
                        kw[k] = [x.ap for x in v]
                    else:
                        kw[k] = v
                ins = getattr(e, op.meth)(**kw)
                if op.signaled:
                    ins.then_inc(op.sem, 16 if (op.is_dma and op.meth != "collective_compute") else 1)
            if final and engname == "sync":
                do_waits(dma_final_waits())

        with nc.Block() as block:
            @block.sync
            def _(e):
                run("sync", e)

            @block.tensor
            def _(e):
                run("tensor", e)

            @block.vector
            def _(e):
                run("vector", e)

            @block.scalar
            def _(e):
                run("scalar", e)

            @block.gpsimd
            def _(e):
                run("gpsimd", e)
        bar = dma_final_waits()
        for e in COMPUTE:
            if self.ccount[e] > 0:
                bar.append((self.csem[e], self.ccount[e]))
        self.barrier = bar
        self.stats["waits"] = self.nwaits
        self.stack.close()
        self.stack = ExitStack()
        if final:
            self.gstack.close()


from concourse.bass_utils import run_bass_kernel_spmd
import ml_dtypes

NBF = ml_dtypes.bfloat16
D = 1024
T = 2048
HALO = 4
NEG = -30000.0


class ColPack:
    def __init__(self):
        self.cols = []
        self.off = {}
        self.n = 0

    def add(self, name, vec, rows=128):
        vec = np.asarray(vec, np.float32).reshape(-1)
        assert vec.size % rows == 0
        m = vec.reshape(-1, rows).T
        a = np.zeros((128, m.shape[1]), np.float32)
        a[:rows] = m
        self.off[name] = (self.n, m.shape[1], rows)
        self.cols.append(a)
        self.n += m.shape[1]

    def array(self):
        return np.ascontiguousarray(np.concatenate(self.cols, axis=1))


class Cst:
    def __init__(self, P, buf, off):
        self.buf = buf
        self.off = off

    def col(self, name, j=0, rows=None):
        o, n, r = self.off[name]
        r = rows or r
        return self.buf[0:r, o + j:o + j + 1]

    def cols(self, name):
        o, n, r = self.off[name]
        return self.buf[0:r, o:o + n]


def new_nc():
    return bass.Bass("TRN2", target_bir_lowering=False)


def load_cast(P, q, dram_view, stage_view, bf_view, cast_eng):
    P.dma(q, out=stage_view, in_=dram_view)
    cast_eng.tensor_copy(out=bf_view, in_=stage_view)


A_OFF = None


def a_colpack(inp, l):
    cp = ColPack()
    cp.add("g_mix", inp["norm_mix_g"][l])
    cp.add("g_cq", inp["mla_q_norm_g"][l])
    cp.add("g_ckv", inp["mla_kv_norm_g"][l])
    cp.add("g_q", inp["mla_q_gain"][l], 96)
    cp.add("g_k", inp["mla_k_gain"][l], 96)
    cp.add("g_fq", inp["fox_q_gain"][l], 64)
    cp.add("g_fk", inp["fox_k_gain"][l], 64)
    cp.add("b_f", inp["fox_b_f"][l], 8)
    cw = inp["ssm_conv_w"][l]
    for k in range(4):
        cp.add(f"cw{k}", cw[k])
    cp.add("cb", inp["ssm_conv_b"][l])
    cp.add("dt_b", inp["ssm_dt_bias"][l], 16)
    cp.add("A_log", inp["ssm_A_log"][l], 16)
    cp.add("b_gate", inp["b_gate"][l])
    inv = 1.0 / (10000.0 ** (np.arange(0, 32, 2, dtype=np.float32) / 32.0))
    invf = np.zeros(96, np.float32)
    invf[64:80] = inv
    invf[80:96] = inv
    cp.add("invf", invf, 96)
    return cp


def build_A(off):
    nc = new_nc()
    P = Prog(nc)
    TT = T + HALO
    NT = T // 512
    EI, EO = "ExternalInput", "ExternalOutput"
    xT = P.dram("xT", [D, TT], F32, EI)
    pos = P.dram("pos", [1, T], I32, EI)
    w_in = P.dram("w_in", [D, 7864], F32, EI)
    w_uq = P.dram("w_uq", [384, 768], F32, EI)
    w_kp = P.dram("w_kp", [256, 768], F32, EI)
    w_v = P.dram("w_v", [256, 512], F32, EI)
    cst_d = P.dram("cst", [128, off["_n"]], F32, EI)
    mats = P.dram("mats", [128, 2 * 96], F32, EI)
    o_qm = P.dram("o_qm", [8, 96, T], BF16, EO)
    o_km = P.dram("o_km", [8, 96, T], BF16, EO)
    o_vm = P.dram("o_vm", [512, T], BF16, EO)
    o_qf = P.dram("o_qf", [8, 64, T], BF16, EO)
    o_kf = P.dram("o_kf", [8, 64, T], BF16, EO)
    o_vf = P.dram("o_vf", [512, T], BF16, EO)
    o_lf = P.dram("o_lf", [8, T], F32, EO)
    o_sz = P.dram("o_sz", [1024, T], BF16, EO)
    o_xbc = P.dram("o_xbc", [1536, T], BF16, EO)
    o_dt = P.dram("o_dt", [16, T], F32, EO)
    o_a = P.dram("o_a", [16, T], F32, EO)
    o_g = P.dram("o_g", [3072, T], BF16, EO)

    cstb = P.sb("cstb", [128, off["_n"]], F32)
    C = Cst(P, cstb, off)
    P.dma("sync", out=cstb.full(), in_=cst_d.full())
    matf = P.sb("matf", [128, 192], F32)
    matb = P.sb("matb", [128, 192], BF16)
    P.dma("sync", out=matf.full(), in_=mats.full())
    P.dve.tensor_copy(out=matb.full(), in_=matf.full())
    prh = matb[0:96, 0:96]
    sel = matb[0:32, 96:192]
    ones = P.sb("ones", [128, 128], F32)
    P.dve.memset(ap=ones.full(), constant=1.0)
    eps = P.sb("eps", [128, 1], F32)
    P.dve.memset(ap=eps.full(), constant=1e-6)
    one1 = P.sb("one1", [128, 1], F32)
    P.dve.memset(ap=one1.full(), constant=1.0)
    nbf = P.sb("nbf", [8, 1], F32)
    P.dve.tensor_scalar(out=nbf.full(), in0=C.col("b_f"), scalar1=-1.0, scalar2=None, op0=ALU.mult)
    Aneg = P.sb("Aneg", [16, 1], F32)
    P.act.activation(out=Aneg.full(), in_=C.col("A_log"), func=AF.Exp)
    P.dve.tensor_scalar(out=Aneg.full(), in0=Aneg.full(), scalar1=-1.0, scalar2=None, op0=ALU.mult)

    pb = [P.ps(f"pb{i}", [128, 512], F32) for i in range(8)]
    pbi = {}

    def nxt_ps(lo=0, hi=4):
        i = pbi.get(lo, 0)
        pbi[lo] = (i + 1) % (hi - lo)
        return pb[lo + i]

    Ctab = P.sb("Ctab", [96, T], F32)
    Stab = P.sb("Stab", [96, T], F32)
    posi = P.sb("posi", [96, 512], I32)
    posf = P.sb("posf", [96, 512], F32)
    rr_tmp = P.sb("rr_tmp", [96, 512], F32)
    rr_i = P.sb("rr_i", [96, 512], I32)
    rr_m = P.sb("rr_m", [96, 512], F32)

    def sin_table(outv, phase):
        P.dve.tensor_scalar(out=rr_tmp.full(), in0=posf.full(), scalar1=C.col("invf"), scalar2=phase,
                            op0=ALU.mult, op1=ALU.add)
        P.dve.tensor_scalar(out=rr_m.full(), in0=rr_tmp.full(), scalar1=1.0 / (2 * np.pi), scalar2=None, op0=ALU.mult)
        P.dve.tensor_copy(out=rr_i.full(), in_=rr_m.full())
        P.dve.tensor_copy(out=rr_m.full(), in_=rr_i.full())
        P.dve.scalar_tensor_tensor(out=rr_tmp.full(), in0=rr_m.full(), scalar=-2 * np.pi, in1=rr_tmp.full(),
                                   op0=ALU.mult, op1=ALU.add)
        P.dve.tensor_scalar(out=rr_m.full(), in0=rr_tmp.full(), scalar1=np.pi, scalar2=-2 * np.pi, op0=ALU.is_gt, op1=ALU.mult)
        P.dve.tensor_tensor(out=rr_tmp.full(), in0=rr_tmp.full(), in1=rr_m.full(), op=ALU.add)
        P.dve.tensor_scalar(out=rr_m.full(), in0=rr_tmp.full(), scalar1=-np.pi, scalar2=2 * np.pi, op0=ALU.is_lt, op1=ALU.mult)
        P.dve.tensor_tensor(out=rr_tmp.full(), in0=rr_tmp.full(), in1=rr_m.full(), op=ALU.add)
        P.act.activation(out=outv, in_=rr_tmp.full(), func=AF.Sin)

    for i in range(NT):
        P.dma("sync", out=posi.full(), in_=pos[:, i * 512:(i + 1) * 512].f(lambda a: a.partition_broadcast(96)))
        P.dve.tensor_copy(out=posf.full(), in_=posi.full())
        sin_table(Stab[:, i * 512:(i + 1) * 512], 0.0)
        sin_table(Ctab[:, i * 512:(i + 1) * 512], np.pi / 2)
    P.dve.memset(ap=Stab[0:64, :], constant=0.0)
    P.dve.memset(ap=Ctab[0:64, :], constant=1.0)

    hn = P.sb("hn", [128, 8, TT], BF16)
    xst = P.sb("xst", [128, 8, 512], F32)
    sq = P.sb("sq", [128, 512], F32)
    rstd = P.sb("rstd", [128, 512], F32)
    xTv = xT.full().re("(kc p) n -> p kc n", p=128)

    def rstd_from(ps_view, n_feat, rows, width, rstd_view):
        P.act.activation(out=rstd_view, in_=ps_view, func=AF.Sqrt, bias=eps[0:rows, 0:1], scale=1.0 / n_feat)
        P.dve.reciprocal(out=rstd_view, in_=rstd_view)

    tiles = [(0, HALO)] + [(HALO + i * 512, 512) for i in range(NT)]
    for (c0, w) in tiles:
        P.dma("sync", out=xst[:, :, 0:w], in_=xTv[:, :, c0:c0 + w])
        ps = nxt_ps(4, 6)
        for kc in range(8):
            P.act.activation(out=sq[:, 0:w], in_=xst[:, kc, 0:w], func=AF.Square)
            P.pe.matmul(out=ps[:, 0:w], lhsT=ones.full(), rhs=sq[:, 0:w], start=(kc == 0), stop=(kc == 7))
        rstd_from(ps[:, 0:w], 1024.0, 128, w, rstd[:, 0:w])
        for kc in range(8):
            P.dve.scalar_tensor_tensor(out=hn[:, kc, c0:c0 + w], in0=xst[:, kc, 0:w], scalar=C.col("g_mix", kc),
                                       in1=rstd[:, 0:w], op0=ALU.mult, op1=ALU.mult)

    wst = [P.sb(f"wst{i}", [128, 8, 512], F32) for i in range(2)]
    wbf = [P.sb(f"wbf{i}", [128, 8, 512], BF16) for i in range(2)]
    wcnt = [0]
    w_inv = w_in.full().re("(kc p) n -> p kc n", p=128)

    def load_w(c0, ncols):
        i = wcnt[0] % 2
        wcnt[0] += 1
        q = "sync" if i == 0 else "gpsimd"
        P.dma(q, out=wst[i][:, :, 0:ncols], in_=w_inv[:, :, c0:c0 + ncols])
        P.pool.tensor_copy(out=wbf[i][:, :, 0:ncols], in_=wst[i][:, :, 0:ncols])
        return wbf[i]

    def proj(wb, wc0, m, c0, w, ps_view):
        for kc in range(8):
            P.pe.matmul(out=ps_view, lhsT=wb[:, kc, wc0:wc0 + m], rhs=hn[:, kc, c0:c0 + w],
                        start=(kc == 0), stop=(kc == 7))

    ostg_cnt = [0]
    ostg = [P.sb(f"ostg{i}", [128, 512], BF16) for i in range(4)]

    def next_ostg():
        i = ostg_cnt[0] % 4
        ostg_cnt[0] += 1
        return ostg[i]

    def out_dma(dst_view, src_view):
        q = "sync" if ostg_cnt[0] % 2 else "gpsimd"
        P.dma(q, out=dst_view, in_=src_view)

    hraw = P.sb("hraw", [96, 512], F32)
    hsq = P.sb("hsq", [96, 512], F32)
    hrs = P.sb("hrs", [96, 512], F32)
    hnf = P.sb("hnf", [96, 512], F32)
    hnb = P.sb("hnb", [96, 512], BF16)
    ht1 = P.sb("ht1", [96, 512], F32)
    ht2 = P.sb("ht2", [96, 512], F32)

    def headnorm(ps_view, d, gain_col, rope, tok0, dst_view):
        P.act.activation(out=hsq[0:d, :], in_=ps_view, func=AF.Square)
        P.act.copy(out=hraw[0:d, :], in_=ps_view)
        ps2 = nxt_ps(4, 6)
        P.pe.matmul(out=ps2[0:d, :], lhsT=ones[0:d, 0:d], rhs=hsq[0:d, :], start=True, stop=True)
        rstd_from(ps2[0:d, :], float(d), d, 512, hrs[0:d, :])
        og = next_ostg()
        if not rope:
            P.dve.scalar_tensor_tensor(out=og[0:d, :], in0=hraw[0:d, :], scalar=gain_col, in1=hrs[0:d, :],
                                       op0=ALU.mult, op1=ALU.mult)
        else:
            P.dve.scalar_tensor_tensor(out=hnf[0:d, :], in0=hraw[0:d, :], scalar=gain_col, in1=hrs[0:d, :],
                                       op0=ALU.mult, op1=ALU.mult)
            P.act.copy(out=hnb[0:d, :], in_=hnf[0:d, :])
            ps3 = nxt_ps(6, 8)
            P.pe.matmul(out=ps3[0:d, :], lhsT=prh, rhs=hnb[0:d, :], start=True, stop=True)
            P.dve.tensor_tensor(out=ht1[0:d, :], in0=hnf[0:d, :], in1=Ctab[0:d, tok0:tok0 + 512], op=ALU.mult)
            P.dve.tensor_tensor(out=ht2[0:d, :], in0=ps3[0:d, :], in1=Stab[0:d, tok0:tok0 + 512], op=ALU.mult)
            P.pool.tensor_tensor(out=og[0:d, :], in0=ht1[0:d, :], in1=ht2[0:d, :], op=ALU.add)
        out_dma(dst_view, og[0:d, :])

    lat = P.sb("lat", [128, 3, 512], F32)
    latn = P.sb("latn", [128, 3, 512], BF16)

    def latent_norm(ps_list, gname):
        nch = len(ps_list)
        ps2 = nxt_ps(4, 6)
        for i, psv in enumerate(ps_list):
            P.act.activation(out=sq.full(), in_=psv, func=AF.Square)
            P.act.copy(out=lat[:, i, :], in_=psv)
            P.pe.matmul(out=ps2.full(), lhsT=ones.full(), rhs=sq.full(), start=(i == 0), stop=(i == nch - 1))
        rstd_from(ps2.full(), 128.0 * nch, 128, 512, rstd.full())
        for i in range(nch):
            P.dve.scalar_tensor_tensor(out=latn[:, i, :], in0=lat[:, i, :], scalar=C.col(gname, i), in1=rstd.full(),
                                       op0=ALU.mult, op1=ALU.mult)

    def small_w(name, dram, kc_n, ncols, i):
        stg = wst[i].full().re("p a b -> p (a b)")[:, 0:kc_n * ncols].re("p (a b) -> p a b", a=kc_n)
        bfb = P.sb(name, [128, kc_n, ncols], BF16)
        P.dma("gpsimd", out=stg, in_=dram.full().re("(kc p) n -> p kc n", p=128))
        P.pool.tensor_copy(out=bfb.full(), in_=stg)
        return bfb

    uqb = small_w("uqb", w_uq, 3, 768, 0)
    kpb = small_w("kpb", w_kp, 2, 768, 1)
    wvb = small_w("wvb", w_v, 2, 512, 0)
    main = tiles[1:]
    wb = load_w(0, 384)
    for ti, (c0, w) in enumerate(main):
        pss = []
        for ch in range(3):
            ps = nxt_ps(0, 4)
            proj(wb, ch * 128, 128, c0, 512, ps.full())
            pss.append(ps.full())
        latent_norm(pss, "g_cq")
        for h in range(8):
            ps = nxt_ps(0, 4)
            for kc in range(3):
                P.pe.matmul(out=ps[0:96, :], lhsT=uqb[:, kc, h * 96:(h + 1) * 96], rhs=latn[:, kc, :],
                            start=(kc == 0), stop=(kc == 2))
            headnorm(ps[0:96, :], 96, C.col("g_q"), True, ti * 512, o_qm[h, :, ti * 512:(ti + 1) * 512])
    wb = load_w(384, 288)
    krb = P.sb("krb", [32, 512], BF16)
    for ti, (c0, w) in enumerate(main):
        pss = []
        for ch in range(2):
            ps = nxt_ps(0, 4)
            proj(wb, ch * 128, 128, c0, 512, ps.full())
            pss.append(ps.full())
        ps = nxt_ps(0, 4)
        proj(wb, 256, 32, c0, 512, ps[0:32, :])
        P.act.copy(out=krb.full(), in_=ps[0:32, :])
        latent_norm(pss, "g_ckv")
        for h in range(8):
            ps = nxt_ps(0, 4)
            for kc in range(2):
                P.pe.matmul(out=ps[0:96, :], lhsT=kpb[:, kc, h * 96:(h + 1) * 96], rhs=latn[:, kc, :],
                            start=(kc == 0), stop=False)
            P.pe.matmul(out=ps[0:96, :], lhsT=sel, rhs=krb.full(), start=False, stop=True)
            headnorm(ps[0:96, :], 96, C.col("g_k"), True, ti * 512, o_km[h, :, ti * 512:(ti + 1) * 512])
        for ch in range(4):
            ps = nxt_ps(0, 4)
            for kc in range(2):
                P.pe.matmul(out=ps.full(), lhsT=wvb[:, kc, ch * 128:(ch + 1) * 128], rhs=latn[:, kc, :],
                            start=(kc == 0), stop=(kc == 1))
            og = next_ostg()
            P.act.copy(out=og.full(), in_=ps.full())
            out_dma(o_vm[ch * 128:(ch + 1) * 128, ti * 512:(ti + 1) * 512], og.full())
    for (base, gname, dst) in ((672, "g_fq", o_qf), (672 + 512, "g_fk", o_kf)):
        wb = load_w(base, 512)
        for ti, (c0, w) in enumerate(main):
            for h in range(8):
                ps = nxt_ps(0, 4)
                proj(wb, h * 64, 64, c0, 512, ps[0:64, :])
                headnorm(ps[0:64, :], 64, C.col(gname), False, ti * 512, dst[h, :, ti * 512:(ti + 1) * 512])
    def plain_group(base, ncols, func, bias_name, dst, dst_row0):
        wb = load_w(base, ncols)
        for ti, (c0, w) in enumerate(main):
            for ch in range(ncols // 128):
                ps = nxt_ps(0, 4)
                proj(wb, ch * 128, 128, c0, 512, ps.full())
                og = next_ostg()
                if bias_name is None:
                    P.act.activation(out=og.full(), in_=ps.full(), func=func)
                else:
                    P.act.activation(out=og.full(), in_=ps.full(), func=func,
                                     bias=C.col(bias_name, (dst_row0 // 128) + ch))
                out_dma(dst[dst_row0 + ch * 128:dst_row0 + (ch + 1) * 128, ti * 512:(ti + 1) * 512], og.full())

    plain_group(672 + 1024, 512, AF.Copy, None, o_vf, 0)
    FB = 672 + 1536
    SB = 672 + 1544
    wf = load_w(FB, 8)
    lf1 = P.sb("lf1", [16, 512], F32)
    lf2 = P.sb("lf2", [16, 512], F32)
    for ti, (c0, w) in enumerate(main):
        ps = nxt_ps(0, 4)
        proj(wf, 0, 8, c0, 512, ps[0:8, :])
        P.act.activation(out=lf1[0:8, :], in_=ps[0:8, :], func=AF.Exp, bias=nbf[0:8, 0:1], scale=-1.0)
        P.act.activation(out=lf1[0:8, :], in_=lf1[0:8, :], func=AF.Ln, bias=one1[0:8, 0:1], scale=1.0)
        P.dve.tensor_scalar(out=lf2[0:8, :], in0=lf1[0:8, :], scalar1=-1.0, scalar2=None, op0=ALU.mult)
        P.dma("sync", out=o_lf[:, ti * 512:(ti + 1) * 512], in_=lf2[0:8, :])
    wd = load_w(SB + 1024 + 1536, 16)
    dt1 = P.sb("dt1", [16, 512], F32)
    dt2 = P.sb("dt2", [16, 512], F32)
    for ti, (c0, w) in enumerate(main):
        ps = nxt_ps(0, 4)
        proj(wd, 0, 16, c0, 512, ps[0:16, :])
        P.act.activation(out=dt1.full(), in_=ps[0:16, :], func=AF.Exp, bias=C.col("dt_b"), scale=1.0)
        P.act.activation(out=dt1.full(), in_=dt1.full(), func=AF.Ln, bias=one1[0:16, 0:1], scale=1.0)
        P.dma("sync", out=o_dt[:, ti * 512:(ti + 1) * 512], in_=dt1.full())
        P.dve.tensor_scalar(out=dt2.full(), in0=dt1.full(), scalar1=Aneg[:, 0:1], scalar2=None, op0=ALU.mult)
        P.dma("sync", out=o_a[:, ti * 512:(ti + 1) * 512], in_=dt2.full())
    for blk in range(2):
        plain_group(SB + blk * 512, 512, AF.Silu, None, o_sz, blk * 512)
    upre = P.sb("upre", [128, 516], F32)
    carry = P.sb("carry", [128, 12, 4], F32)
    acc = [P.sb(f"acc{i}", [128, 512], F32) for i in range(2)]
    for blk in range(3):
        wb = load_w(SB + 1024 + blk * 512, 512)
        for ch in range(4):
            cg = blk * 4 + ch
            ps = nxt_ps(0, 4)
            proj(wb, ch * 128, 128, 0, HALO, ps[:, 0:HALO])
            P.act.copy(out=carry[:, cg, :], in_=ps[:, 0:HALO])
        for ti, (c0, w) in enumerate(main):
            for ch in range(4):
                cg = blk * 4 + ch
                ps = nxt_ps(0, 4)
                proj(wb, ch * 128, 128, c0, 512, ps.full())
                P.act.copy(out=upre[:, 4:516], in_=ps.full())
                P.dve.tensor_copy(out=upre[:, 0:4], in_=carry[:, cg, :])
                P.pool.tensor_copy(out=carry[:, cg, :], in_=upre[:, 512:516])
                a0 = acc[0]
                P.dve.tensor_scalar(out=a0.full(), in0=upre[:, 4:516], scalar1=C.col("cw3", cg), scalar2=C.col("cb", cg),
                                    op0=ALU.mult, op1=ALU.add)
                for k in range(3):
                    P.dve.scalar_tensor_tensor(out=a0.full(), in0=upre[:, 1 + k:513 + k], scalar=C.col(f"cw{k}", cg),
                                               in1=a0.full(), op0=ALU.mult, op1=ALU.add)
                og = next_ostg()
                P.act.activation(out=og.full(), in_=a0.full(), func=AF.Silu)
                out_dma(o_xbc[cg * 128:(cg + 1) * 128, ti * 512:(ti + 1) * 512], og.full())
    GB = SB + 2576
    for blk in range(6):
        plain_group(GB + blk * 512, 512, AF.Sigmoid, "b_gate", o_g, blk * 512)
    P.emit()
    return nc, P


def _bf(a):
    return np.asarray(a).astype(np.float32)


_PROG_CACHE = {}


def _const_mats():
    m = np.zeros((128, 192), np.float32)
    for i in range(16):
        m[80 + i, 64 + i] = -1.0
        m[64 + i, 80 + i] = 1.0
    for i in range(32):
        m[i, 96 + 64 + i] = 1.0
    return m


def run_A(inp, l, x_full, pos_full):
    cp = a_colpack(inp, l)
    off = dict(cp.off)
    off["_n"] = cp.n
    if "A" not in _PROG_CACHE:
        _PROG_CACHE["A"] = build_A(off)[0]
    nc = _PROG_CACHE["A"]
    cst = cp.array()
    wukv = inp["mla_w_ukv"][l].reshape(256, 8, 128)
    w_kp = np.zeros((256, 8, 96), np.float32)
    w_kp[:, :, 0:64] = wukv[:, :, 0:64]
    w_v = np.ascontiguousarray(wukv[:, :, 64:128].reshape(256, 512))
    mats = _const_mats()
    xf = x_full.reshape(16384, D)
    in_maps = []
    for c in range(8):
        t0 = c * T
        xt = np.zeros((D, T + HALO), np.float32)
        xt[:, HALO:] = xf[t0:t0 + T].T
        if c % 4 != 0:
            xt[:, 0:HALO] = xf[t0 - HALO:t0].T
        in_maps.append({
            "xT": np.ascontiguousarray(xt),
            "pos": np.ascontiguousarray(pos_full.reshape(1, 16384)[:, t0:t0 + T]).astype(np.int32),
            "w_in": np.ascontiguousarray(inp["w_in"][l]),
            "w_uq": np.ascontiguousarray(inp["mla_w_uq"][l]),
            "w_kp": np.ascontiguousarray(w_kp.reshape(256, 768)),
            "w_v": w_v, "cst": cst, "mats": mats,
        })
    res = run_bass_kernel_spmd(nc, in_maps, core_ids=list(range(8)))
    return res.results


S_ = 8192
NKT = S_ // 128
NQT = S_ // 512


def build_BC():
    nc = new_nc()
    P = Prog(nc)
    EI, EO = "ExternalInput", "ExternalOutput"
    qm = P.dram("qm", [2, 96, S_], BF16, EI)
    km = P.dram("km", [2, 96, S_], BF16, EI)
    vm = P.dram("vm", [2, 128, NKT, 64], BF16, EI)
    qf = P.dram("qf", [2, 64, S_], BF16, EI)
    kf = P.dram("kf", [2, 64, S_], BF16, EI)
    vf = P.dram("vf", [2, 128, NKT, 64], BF16, EI)
    lf = P.dram("lf", [2, 128, NKT], F32, EI)
    msk = P.dram("msk", [128, 8, 512], F32, EI)
    cm = P.dram("cm", [128, 4, 128], F32, EI)
    x_tm = P.dram("x_tm", [128, NKT, 256], BF16, EI)
    B_tm = P.dram("B_tm", [128, NKT, 128], BF16, EI)
    BT = P.dram("BT", [128, S_], BF16, EI)
    CT = P.dram("CT", [128, S_], BF16, EI)
    dt_tm = P.dram("dt_tm", [128, NKT, 4], F32, EI)
    a_tm = P.dram("a_tm", [128, NKT, 4], F32, EI)
    Dv = P.dram("Dv", [128, 4], F32, EI)
    o_m = P.dram("o_m", [2, 64, S_], BF16, EO)
    o_f = P.dram("o_f", [2, 64, S_], BF16, EO)
    o_y = P.dram("o_y", [128, NKT, 256], F32, EO)
    fsc = P.dram("fsc", [3, S_], BF16)

    cmb = P.sb("cmb", [128, 4, 128], F32)
    P.dma("sync", out=cmb.full(), in_=cm.full())
    tri, trimask, ident, ones = cmb[:, 0, :], cmb[:, 1, :], cmb[:, 2, :], cmb[:, 3, :]
    mskb = P.sb("mskb", [128, 8, 512], F32)
    P.dma("gpsimd", out=mskb.full(), in_=msk.full())
    zero = P.sb("zero", [128, 1], F32)
    P.dve.memset(ap=zero.full(), constant=0.0)

    pb = [P.ps(f"pb{i}", [128, 512], F32) for i in range(8)]
    K_sb = P.sb("K_sb", [128, S_], BF16)
    Q_sb = P.sb("Q_sb", [128, S_], BF16)
    V_sb = P.sb("V_sb", [128, NKT, 128], BF16)
    P.dve.memset(ap=V_sb[:, :, 64:128], constant=1.0)
    pt = [P.sb(f"pt{i}", [128, 512], BF16) for i in range(3)]
    mt = [P.sb(f"mt{i}", [128, 512], F32) for i in range(2)]
    rl = P.sb("rl", [128, 512], F32)
    rl2 = P.sb("rl2", [64, 512], F32)
    ot = [P.sb(f"ot{i}", [64, 512], BF16) for i in range(2)]
    negF = P.sb("negF", [128, NKT], F32)

    cnt = [0, 0, 0]

    def attention(dk, scale, mask0, bias_fn, out_dram_h):
        for qt in range(NQT):
            oacc = pb[3 + qt % 2]
            nk = 4 * qt + 4
            for kt in range(nk):
                i3 = cnt[0] % 3
                cnt[0] += 1
                ps = pb[i3]
                P.pe.matmul(out=ps.full(), lhsT=K_sb[0:dk, kt * 128:(kt + 1) * 128],
                            rhs=Q_sb[0:dk, qt * 512:(qt + 1) * 512], start=True, stop=True)
                if kt >= 4 * qt:
                    m = mt[cnt[1] % 2]
                    cnt[1] += 1
                    P.dve.tensor_tensor(out=m.full(), in0=ps.full(), in1=mskb[:, mask0 + kt - 4 * qt, :], op=ALU.add)
                    src = m.full()
                else:
                    src = ps.full()
                P.act.activation(out=pt[i3].full(), in_=src, func=AF.Exp, scale=scale, bias=bias_fn(kt))
                P.pe.matmul(out=oacc.full(), lhsT=V_sb[:, kt, :], rhs=pt[i3].full(), start=(kt == 0), stop=(kt == nk - 1))
            P.dve.reciprocal(out=rl[64:128, :], in_=oacc[64:128, :])
            P.dve.tensor_copy(out=rl2.full(), in_=rl[64:128, :])
            o = ot[qt % 2]
            P.dve.tensor_tensor(out=o.full(), in0=oacc[0:64, :], in1=rl2.full(), op=ALU.mult)
            P.dma("sync", out=out_dram_h[:, qt * 512:(qt + 1) * 512], in_=o.full())

    for h in range(2):
        P.dma("sync", out=K_sb[0:96, :], in_=km[h])
        P.dma("gpsimd", out=Q_sb[0:96, :], in_=qm[h])
        P.dma("sync", out=V_sb[:, :, 0:64], in_=vm[h])
        attention(96, 96.0 ** -0.5, 0, lambda kt: zero[:, 0:1], o_m[h])

    lfs = P.sb("lfs", [128, NKT], F32)
    wi = P.sb("wi", [128, NKT], F32)
    sc = [P.sb(f"sc{i}", [128, NKT], F32) for i in range(2)]
    Ff = P.sb("Ff", [128, NKT], F32)
    FT = P.sb("FT", [64, 128], F32)
    r1 = P.sb("r1", [64, 128], F32)
    fh = [P.sb(f"fh{i}", [64, 128], BF16) for i in range(3)]
    for h in range(2):
        P.dma("sync", out=lfs.full(), in_=lf[h])
        ps = pb[5]
        P.pe.matmul(out=ps[:, 0:NKT], lhsT=tri, rhs=lfs.full(), start=True, stop=True)
        P.act.copy(out=wi.full(), in_=ps[:, 0:NKT])
        ps = pb[6]
        P.pe.matmul(out=ps[:, 0:NKT], lhsT=ones, rhs=lfs.full(), start=True, stop=True)
        P.act.copy(out=sc[0].full(), in_=ps[:, 0:NKT])
        P.dve.tensor_tensor(out=wi.full(), in0=wi.full(), in1=sc[0].full(), op=ALU.subtract)
        cur = 0
        d = 1
        while d < NKT:
            nx = 1 - cur
            P.dve.tensor_copy(out=sc[nx][:, 0:d], in_=sc[cur][:, 0:d])
            P.dve.tensor_tensor(out=sc[nx][:, d:NKT], in0=sc[cur][:, d:NKT], in1=sc[cur][:, 0:NKT - d], op=ALU.add)
            cur = nx
            d *= 2
        P.dve.tensor_tensor(out=Ff.full(), in0=wi.full(), in1=sc[cur].full(), op=ALU.add)
        P.dve.tensor_scalar(out=negF.full(), in0=Ff.full(), scalar1=-1.0, scalar2=None, op0=ALU.mult)
        ps = pb[7]
        P.pe.transpose(out=ps[0:64, 0:128], in_=Ff.full(), identity=ident)
        P.act.copy(out=FT.full(), in_=ps[0:64, 0:128])
        P.dve.tensor_copy(out=fh[0].full(), in_=FT.full())
        P.dve.tensor_tensor(out=r1.full(), in0=FT.full(), in1=fh[0].full(), op=ALU.subtract)
        P.dve.tensor_copy(out=fh[1].full(), in_=r1.full())
        P.dve.tensor_tensor(out=r1.full(), in0=r1.full(), in1=fh[1].full(), op=ALU.subtract)
        P.dve.tensor_copy(out=fh[2].full(), in_=r1.full())
        for r in range(3):
            P.dma("sync", out=fsc[r].re("(kt p) -> kt p", p=128), in_=fh[r].full())
        P.dma("sync", out=K_sb[0:64, :], in_=kf[h])
        P.dve.memset(ap=K_sb[64:67, :], constant=8.0)
        P.dma("gpsimd", out=Q_sb[0:64, :], in_=qf[h])
        P.dma("gpsimd", out=Q_sb[64:67, :], in_=fsc.full())
        P.dma("sync", out=V_sb[:, :, 0:64], in_=vf[h])
        attention(67, 0.125, 4, lambda kt: negF[:, kt:kt + 1], o_f[h])

    a_sb = P.sb("a_sb", [128, NKT, 4], F32)
    dt_sb = P.sb("dt_sb", [128, NKT, 4], F32)
    Dsb = P.sb("Dsb", [128, 4], F32)
    P.dma("sync", out=a_sb.full(), in_=a_tm.full())
    P.dma("sync", out=dt_sb.full(), in_=dt_tm.full())
    P.dma("sync", out=Dsb.full(), in_=Dv.full())
    BTs = K_sb
    CTs = Q_sb
    P.dma("sync", out=BTs.full(), in_=BT.full())
    P.dma("gpsimd", out=CTs.full(), in_=CT.full())
    Acum = P.sb("Acum", [128, NKT, 4], F32)
    nAcum = P.sb("nAcum", [128, NKT, 4], F32)
    Atot = P.sb("Atot", [128, NKT, 4], F32)
    eA = P.sb("eA", [128, NKT, 4], F32)
    wdec = P.sb("wdec", [128, NKT, 4], F32)
    eAtot = P.sb("eAtot", [128, NKT, 4], F32)
    fl = lambda b: b.full().re("p c h -> p (c h)")
    ps = pb[0]
    P.pe.matmul(out=ps[:, 0:256], lhsT=tri, rhs=fl(a_sb), start=True, stop=True)
    P.act.copy(out=fl(Acum), in_=ps[:, 0:256])
    ps = pb[1]
    P.pe.matmul(out=ps[:, 0:256], lhsT=ones, rhs=fl(a_sb), start=True, stop=True)
    P.act.copy(out=fl(Atot), in_=ps[:, 0:256])
    P.dve.tensor_scalar(out=fl(nAcum), in0=fl(Acum), scalar1=-1.0, scalar2=None, op0=ALU.mult)
    P.act.activation(out=fl(eA), in_=fl(Acum), func=AF.Exp)
    P.act.activation(out=fl(eAtot), in_=fl(Atot), func=AF.Exp)
    P.dve.tensor_tensor(out=fl(wdec), in0=fl(Atot), in1=fl(Acum), op=ALU.subtract)
    P.act.activation(out=fl(wdec), in_=fl(wdec), func=AF.Exp)

    Hs = P.sb("Hs", [128, 256], F32)
    Hb = P.sb("Hb", [128, 256], BF16)
    P.dve.memset(ap=Hs.full(), constant=0.0)
    P.dve.memset(ap=Hb.full(), constant=0.0)
    xc = [P.sb(f"xc{i}", [128, 256], BF16) for i in range(2)]
    Bc = [P.sb(f"Bc{i}", [128, 128], BF16) for i in range(2)]
    cb = P.sb("cb", [128, 128], F32)
    xdt = P.sb("xdt", [128, 256], BF16)
    xdts = P.sb("xdts", [128, 256], BF16)
    at = [P.sb(f"at{i}", [128, 128], F32) for i in range(2)]
    tm = [P.sb(f"tm{i}", [128, 128], F32) for i in range(2)]
    dec = [P.sb(f"dec{i}", [128, 128], F32) for i in range(2)]
    MT = [P.sb(f"MT{i}", [128, 128], BF16) for i in range(2)]
    t1 = P.sb("t1", [128, 256], F32)
    t3 = P.sb("t3", [128, 256], F32)
    yo = [P.sb(f"yo{i}", [128, 256], F32) for i in range(2)]
    v3 = lambda v: v.re("p (h d) -> p h d", h=4)
    bc3 = lambda v: v.f(lambda a: a.unsqueeze(2).to_broadcast([128, 4, 64]))
    for c in range(NKT):
        x_c = xc[c % 2]
        B_c = Bc[c % 2]
        P.dma("sync", out=x_c.full(), in_=x_tm[:, c, :])
        P.dma("gpsimd", out=B_c.full(), in_=B_tm[:, c, :])
        BT_c = BTs[:, c * 128:(c + 1) * 128]
        CT_c = CTs[:, c * 128:(c + 1) * 128]
        ps_cb = pb[0]
        P.pe.matmul(out=ps_cb[:, 0:128], lhsT=BT_c, rhs=CT_c, start=True, stop=True)
        P.act.copy(out=cb.full(), in_=ps_cb[:, 0:128])
        P.dve.tensor_tensor(out=v3(xdt.full()), in0=v3(x_c.full()), in1=bc3(dt_sb[:, c, :]), op=ALU.mult)
        P.pool.tensor_tensor(out=v3(xdts.full()), in0=v3(xdt.full()), in1=bc3(wdec[:, c, :]), op=ALU.mult)
        ps_off = pb[1]
        P.pe.matmul(out=ps_off[:, 0:256], lhsT=CT_c, rhs=Hb.full(), start=True, stop=True)
        ps_y = pb[2]
        for h in range(4):
            i2 = h % 2
            P.dve.tensor_scalar(out=at[i2].full(), in0=tri, scalar1=a_sb[:, c, h:h + 1], scalar2=None, op0=ALU.mult)
            ps_A = pb[3 + i2]
            P.pe.matmul(out=ps_A[:, 0:128], lhsT=ones, rhs=at[i2].full(), start=True, stop=True)
            P.dve.tensor_tensor(out=tm[i2].full(), in0=ps_A[:, 0:128], in1=trimask, op=ALU.add)
            P.act.activation(out=dec[i2].full(), in_=tm[i2].full(), func=AF.Exp, bias=nAcum[:, c, h:h + 1], scale=1.0)
            P.pool.tensor_tensor(out=MT[i2].full(), in0=cb.full(), in1=dec[i2].full(), op=ALU.mult)
            P.pe.matmul(out=ps_y[:, h * 64:(h + 1) * 64], lhsT=MT[i2].full(), rhs=xdt[:, h * 64:(h + 1) * 64],
                        start=True, stop=True)
        P.dve.tensor_tensor(out=v3(t1.full()), in0=v3(ps_off[:, 0:256]), in1=bc3(eA[:, c, :]), op=ALU.mult)
        P.dve.tensor_tensor(out=t1.full(), in0=t1.full(), in1=ps_y[:, 0:256], op=ALU.add)
        P.pool.tensor_tensor(out=v3(t3.full()), in0=v3(x_c.full()), in1=bc3(Dsb.full()), op=ALU.mult)
        y_ = yo[c % 2]
        P.pool.tensor_tensor(out=y_.full(), in0=t1.full(), in1=t3.full(), op=ALU.add)
        P.dma("sync", out=o_y[:, c, :], in_=y_.full())
        ps_h = pb[5]
        P.pe.matmul(out=ps_h[:, 0:256], lhsT=B_c.full(), rhs=xdts.full(), start=True, stop=True)
        P.dve.tensor_tensor(out=v3(Hs.full()), in0=v3(Hs.full()), in1=bc3(eAtot[:, c, :]), op=ALU.mult)
        P.dve.tensor_tensor(out=Hs.full(), in0=Hs.full(), in1=ps_h[:, 0:256], op=ALU.add)
        P.act.copy(out=Hb.full(), in_=Hs.full())
    P.emit()
    return nc, P


def _bc_consts():
    msk = np.zeros((128, 8, 512), np.float32)
    p = np.arange(128)[:, None]
    q = np.arange(512)[None, :]
    for j in range(4):
        key = j * 128 + p
        msk[:, j, :] = np.where((key // 64) > (q // 64), NEG, 0.0)
        msk[:, 4 + j, :] = np.where(key > q, NEG, 0.0)
    cm = np.zeros((128, 4, 128), np.float32)
    jj = np.arange(128)[:, None]
    ii = np.arange(128)[None, :]
    cm[:, 0, :] = (jj <= ii).astype(np.float32)
    cm[:, 1, :] = np.where(jj > ii, NEG, 0.0)
    cm[:, 2, :] = np.eye(128, dtype=np.float32)
    cm[:, 3, :] = 1.0
    return msk, cm


def _tm(a):
    S, n = a.shape
    return np.ascontiguousarray(a.reshape(S // 128, 128, n).transpose(1, 0, 2))


def run_BC(inp, l, resA):
    if "BC" not in _PROG_CACHE:
        _PROG_CACHE["BC"] = build_BC()[0]
    nc = _PROG_CACHE["BC"]
    msk, cm = _bc_consts()

    def gather(name, b):
        return np.concatenate([np.asarray(resA[b * 4 + i][name]) for i in range(4)], axis=-1)

    in_maps = []
    for c in range(8):
        b, hg = c // 4, c % 4
        qm = gather("o_qm", b)[2 * hg:2 * hg + 2]
        km = gather("o_km", b)[2 * hg:2 * hg + 2]
        vmf = gather("o_vm", b)
        qf = gather("o_qf", b)[2 * hg:2 * hg + 2]
        kf = gather("o_kf", b)[2 * hg:2 * hg + 2]
        vff = gather("o_vf", b)
        lff = gather("o_lf", b)
        xbc = gather("o_xbc", b)
        dtf = gather("o_dt", b)
        af = gather("o_a", b)
        g = hg // 2
        vm = np.stack([_tm(vmf[(2 * hg + h) * 64:(2 * hg + h + 1) * 64].T) for h in range(2)])
        vf = np.stack([_tm(vff[(2 * hg + h) * 64:(2 * hg + h + 1) * 64].T) for h in range(2)])
        lf = np.stack([np.ascontiguousarray(lff[2 * hg + h].reshape(NKT, 128).T) for h in range(2)])
        x_tm = _tm(xbc[hg * 256:(hg + 1) * 256].T)
        Bf = xbc[1024 + g * 128:1024 + (g + 1) * 128]
        Cf = xbc[1280 + g * 128:1280 + (g + 1) * 128]
        Dv = np.broadcast_to(inp["ssm_D"][l][4 * hg:4 * hg + 4][None, :], (128, 4)).astype(np.float32)
        in_maps.append({
            "qm": np.ascontiguousarray(qm), "km": np.ascontiguousarray(km), "vm": vm,
            "qf": np.ascontiguousarray(qf), "kf": np.ascontiguousarray(kf), "vf": vf, "lf": lf,
            "msk": msk, "cm": cm, "x_tm": x_tm, "B_tm": _tm(Bf.T), "BT": np.ascontiguousarray(Bf),
            "CT": np.ascontiguousarray(Cf), "dt_tm": _tm(dtf[4 * hg:4 * hg + 4].T),
            "a_tm": _tm(af[4 * hg:4 * hg + 4].T), "Dv": np.ascontiguousarray(Dv),
        })
    res = run_bass_kernel_spmd(nc, in_maps, core_ids=list(range(8))).results
    om = np.zeros((2, 512, S_), NBF)
    of = np.zeros((2, 512, S_), NBF)
    y = np.zeros((2, 1024, S_), np.float32)
    for c in range(8):
        b, hg = c // 4, c % 4
        om[b, hg * 128:(hg + 1) * 128] = np.asarray(res[c]["o_m"]).reshape(128, S_)
        of[b, hg * 128:(hg + 1) * 128] = np.asarray(res[c]["o_f"]).reshape(128, S_)
        yy = np.asarray(res[c]["o_y"])
        y[b, hg * 256:(hg + 1) * 256] = yy.transpose(2, 1, 0).reshape(256, S_)
    return om, of, y


def build_D1():
    nc = new_nc()
    P = Prog(nc)
    EI, EO = "ExternalInput", "ExternalOutput"
    NT = T // 512
    omT = P.dram("omT", [512, T], BF16, EI)
    ofT = P.dram("ofT", [512, T], BF16, EI)
    yT = P.dram("yT", [1024, T], F32, EI)
    szT = P.dram("szT", [1024, T], BF16, EI)
    gT = P.dram("gT", [3072, T], BF16, EI)
    xT = P.dram("xT", [D, T], F32, EI)
    w_a = P.dram("w_a", [512, D], F32, EI)
    w_b = P.dram("w_b", [512, D], F32, EI)
    w_c = P.dram("w_c", [1024, D], F32, EI)
    w_o = P.dram("w_o", [1024, D], F32, EI)
    cst_d = P.dram("cst", [128, 8], F32, EI)
    o_x = P.dram("o_x", [D, T], F32, EO)

    cstb = P.sb("cstb", [128, 8], F32)
    P.dma("sync", out=cstb.full(), in_=cst_d.full())
    ones = P.sb("ones", [128, 128], F32)
    P.dve.memset(ap=ones.full(), constant=1.0)
    eps = P.sb("eps", [128, 1], F32)
    P.dve.memset(ap=eps.full(), constant=1e-6)
    pb = [P.ps(f"pb{i}", [128, 512], F32) for i in range(8)]
    wst = [P.sb(f"wst{i}", [128, 4, 1024], F32) for i in range(2)]
    wcnt = [0]

    def load_w(dram, kc_n, name):
        bfb = P.sb(name, [128, kc_n, D], BF16)
        v = dram.full().re("(kc p) n -> p kc n", p=128)
        for k0 in range(0, kc_n, 4):
            i = wcnt[0] % 2
            wcnt[0] += 1
            P.dma("sync" if i == 0 else "gpsimd", out=wst[i].full(), in_=v[:, k0:k0 + 4, :])
            P.pool.tensor_copy(out=bfb[:, k0:k0 + 4, :], in_=wst[i].full())
        return bfb

    Wa = load_w(w_a, 4, "Wa")
    Wb = load_w(w_b, 4, "Wb")
    Wc = load_w(w_c, 8, "Wc")
    Wo = load_w(w_o, 8, "Wo")

    ys = P.sb("ys", [128, 8, 512], F32)
    szs = P.sb("szs", [128, 8, 512], BF16)
    yn = P.sb("yn", [128, 8, 512], BF16)
    oms = P.sb("oms", [128, 4, 512], BF16)
    ofs = P.sb("ofs", [128, 4, 512], BF16)
    gs = P.sb("gs", [128, 24, 512], BF16)
    xs = P.sb("xs", [128, 8, 512], F32)
    sq = P.sb("sq", [128, 512], F32)
    rstd = P.sb("rstd", [128, 512], F32)
    m1 = [P.sb(f"m1_{i}", [128, 512], F32) for i in range(2)]
    m2 = [P.sb(f"m2_{i}", [128, 512], F32) for i in range(2)]
    m3 = [P.sb(f"m3_{i}", [128, 512], F32) for i in range(2)]
    mg = P.sb("mg", [128, 8, 512], BF16)
    xo = [P.sb(f"xo{i}", [128, 512], F32) for i in range(2)]
    ch = lambda d: d.full().re("(kc p) n -> p kc n", p=128)
    for ti in range(NT):
        ts = slice(ti * 512, (ti + 1) * 512)
        P.dma("sync", out=ys.full(), in_=ch(yT)[:, :, ts])
        P.dma("gpsimd", out=szs.full(), in_=ch(szT)[:, :, ts])
        P.dma("sync", out=oms.full(), in_=ch(omT)[:, :, ts])
        P.dma("gpsimd", out=ofs.full(), in_=ch(ofT)[:, :, ts])
        P.dma("sync", out=gs.full(), in_=ch(gT)[:, :, ts])
        P.dma("gpsimd", out=xs.full(), in_=ch(xT)[:, :, ts])
        ps = pb[7]
        for kc in range(8):
            P.dve.tensor_tensor(out=ys[:, kc, :], in0=ys[:, kc, :], in1=szs[:, kc, :], op=ALU.mult)
            P.act.activation(out=sq.full(), in_=ys[:, kc, :], func=AF.Square)
            P.pe.matmul(out=ps.full(), lhsT=ones.full(), rhs=sq.full(), start=(kc == 0), stop=(kc == 7))
        P.act.activation(out=rstd.full(), in_=ps.full(), func=AF.Sqrt, bias=eps[:, 0:1], scale=1.0 / 1024.0)
        P.dve.reciprocal(out=rstd.full(), in_=rstd.full())
        for kc in range(8):
            P.dve.scalar_tensor_tensor(out=yn[:, kc, :], in0=ys[:, kc, :], scalar=cstb[:, kc:kc + 1], in1=rstd.full(),
                                       op0=ALU.mult, op1=ALU.mult)
        for oc in range(8):
            i2 = oc % 2
            osl = slice(oc * 128, (oc + 1) * 128)
            pa, pbb, pc = pb[0 + i2 * 3], pb[1 + i2 * 3], pb[2 + i2 * 3]
            for kc in range(4):
                P.pe.matmul(out=pa.full(), lhsT=Wa[:, kc, osl], rhs=oms[:, kc, :], start=(kc == 0), stop=(kc == 3))
            for kc in range(4):
                P.pe.matmul(out=pbb.full(), lhsT=Wb[:, kc, osl], rhs=ofs[:, kc, :], start=(kc == 0), stop=(kc == 3))
            for kc in range(8):
                P.pe.matmul(out=pc.full(), lhsT=Wc[:, kc, osl], rhs=yn[:, kc, :], start=(kc == 0), stop=(kc == 7))
            P.dve.tensor_tensor(out=m1[i2].full(), in0=pa.full(), in1=gs[:, oc, :], op=ALU.mult)
            P.dve.tensor_tensor(out=m2[i2].full(), in0=pbb.full(), in1=gs[:, 8 + oc, :], op=ALU.mult)
            P.dve.tensor_tensor(out=m3[i2].full(), in0=pc.full(), in1=gs[:, 16 + oc, :], op=ALU.mult)
            P.pool.tensor_tensor(out=m1[i2].full(), in0=m1[i2].full(), in1=m2[i2].full(), op=ALU.add)
            P.pool.tensor_tensor(out=mg[:, oc, :], in0=m1[i2].full(), in1=m3[i2].full(), op=ALU.add)
        for oc in range(8):
            i2 = oc % 2
            ps = pb[6 + i2]
            for kc in range(8):
                P.pe.matmul(out=ps.full(), lhsT=Wo[:, kc, oc * 128:(oc + 1) * 128], rhs=mg[:, kc, :],
                            start=(kc == 0), stop=(kc == 7))
            P.dve.tensor_tensor(out=xo[i2].full(), in0=ps.full(), in1=xs[:, oc, :], op=ALU.add)
            P.dma("sync" if i2 else "gpsimd", out=o_x[oc * 128:(oc + 1) * 128, ts], in_=xo[i2].full())
    P.emit()
    return nc, P


def d2_colpack(inp, l):
    cp = ColPack()
    cp.add("g_ffn", inp["norm_ffn_g"][l])
    cw = inp["ffn_conv_w"][l]
    for k in range(3):
        cp.add(f"fw{k}", cw[k])
    cp.add("fb", inp["ffn_conv_b"][l])
    return cp


def build_D2(off):
    nc = new_nc()
    P = Prog(nc)
    EI, EO = "ExternalInput", "ExternalOutput"
    NT = T // 512
    TT = T + HALO
    xT = P.dram("xT", [D, TT], F32, EI)
    w_up = P.dram("w_up", [D, 5632], F32, EI)
    w_dn = P.dram("w_dn", [2816, D], F32, EI)
    cst_d = P.dram("cst", [128, off["_n"]], F32, EI)
    o_x = P.dram("o_x", [D, T], F32, EO)

    cstb = P.sb("cstb", [128, off["_n"]], F32)
    C = Cst(P, cstb, off)
    P.dma("sync", out=cstb.full(), in_=cst_d.full())
    ones = P.sb("ones", [128, 128], F32)
    P.dve.memset(ap=ones.full(), constant=1.0)
    eps = P.sb("eps", [128, 1], F32)
    P.dve.memset(ap=eps.full(), constant=1e-6)
    pb = [P.ps(f"pb{i}", [128, 512], F32) for i in range(8)]
    wst = [P.sb(f"wst{i}", [128, 1024], F32) for i in range(2)]
    wcnt = [0]
    Wu = P.sb("Wu", [128, 8, 5632], BF16)
    Wd = P.sb("Wd", [128, 22, D], BF16)
    wuv = w_up.full().re("(kc p) n -> p kc n", p=128)
    for kc in range(8):
        for c0 in range(0, 5632, 1024):
            n = min(1024, 5632 - c0)
            i = wcnt[0] % 2
            wcnt[0] += 1
            P.dma("sync" if i == 0 else "gpsimd", out=wst[i][:, 0:n], in_=wuv[:, kc, c0:c0 + n])
            P.pool.tensor_copy(out=Wu[:, kc, c0:c0 + n], in_=wst[i][:, 0:n])
    wdv = w_dn.full().re("(kc p) n -> p kc n", p=128)
    for k0 in range(22):
        i = wcnt[0] % 2
        wcnt[0] += 1
        P.dma("sync" if i == 0 else "gpsimd", out=wst[i].full(), in_=wdv[:, k0, :])
        P.pool.tensor_copy(out=Wd[:, k0, :], in_=wst[i].full())

    xst = P.sb("xst", [128, 8, 512], F32)
    hn = P.sb("hn", [128, 8, 512], BF16)
    sq = P.sb("sq", [128, 512], F32)
    rstd = P.sb("rstd", [128, 512], F32)
    act = P.sb("act", [128, 22, 512], BF16)
    upre = [P.sb(f"upre{i}", [128, 516], F32) for i in range(2)]
    acc = [P.sb(f"acc{i}", [128, 512], F32) for i in range(2)]
    sg = P.sb("sg", [128, 512], F32)
    carry = P.sb("carry", [128, 44, 4], F32)
    xo = [P.sb(f"xo{i}", [128, 512], F32) for i in range(2)]
    xTv = xT.full().re("(kc p) n -> p kc n", p=128)
    tiles = [(0, HALO)] + [(HALO + i * 512, 512) for i in range(NT)]
    pcnt = [0]
    for tix, (c0, w) in enumerate(tiles):
        P.dma("sync", out=xst[:, :, 0:w], in_=xTv[:, :, c0:c0 + w])
        ps = pb[7]
        for kc in range(8):
            P.act.activation(out=sq[:, 0:w], in_=xst[:, kc, 0:w], func=AF.Square)
            P.pe.matmul(out=ps[:, 0:w], lhsT=ones.full(), rhs=sq[:, 0:w], start=(kc == 0), stop=(kc == 7))
        P.act.activation(out=rstd[:, 0:w], in_=ps[:, 0:w], func=AF.Sqrt, bias=eps[:, 0:1], scale=1.0 / 1024.0)
        P.dve.reciprocal(out=rstd[:, 0:w], in_=rstd[:, 0:w])
        for kc in range(8):
            P.dve.scalar_tensor_tensor(out=hn[:, kc, 0:w], in0=xst[:, kc, 0:w], scalar=C.col("g_ffn", kc),
                                       in1=rstd[:, 0:w], op0=ALU.mult, op1=ALU.mult)
        for i in range(22):
            accs = []
            for j, cg in enumerate((i, 22 + i)):
                ps = pb[pcnt[0] % 4]
                pcnt[0] += 1
                for kc in range(8):
                    P.pe.matmul(out=ps[:, 0:w], lhsT=Wu[:, kc, cg * 128:(cg + 1) * 128], rhs=hn[:, kc, 0:w],
                                start=(kc == 0), stop=(kc == 7))
                if tix == 0:
                    P.act.copy(out=carry[:, cg, :], in_=ps[:, 0:HALO])
                    continue
                up = upre[j]
                P.act.copy(out=up[:, 4:516], in_=ps.full())
                P.dve.tensor_copy(out=up[:, 0:4], in_=carry[:, cg, :])
                P.pool.tensor_copy(out=carry[:, cg, :], in_=up[:, 512:516])
                a0 = acc[j]
                P.dve.tensor_scalar(out=a0.full(), in0=up[:, 4:516], scalar1=C.col("fw2", cg), scalar2=C.col("fb", cg),
                                    op0=ALU.mult, op1=ALU.add)
                P.dve.scalar_tensor_tensor(out=a0.full(), in0=up[:, 3:515], scalar=C.col("fw1", cg), in1=a0.full(),
                                           op0=ALU.mult, op1=ALU.add)
                P.dve.scalar_tensor_tensor(out=a0.full(), in0=up[:, 2:514], scalar=C.col("fw0", cg), in1=a0.full(),
                                           op0=ALU.mult, op1=ALU.add)
                accs.append(a0)
            if tix == 0:
                continue
            P.act.activation(out=sg.full(), in_=accs[0].full(), func=AF.Silu)
            P.pool.tensor_tensor(out=act[:, i, :], in0=sg.full(), in1=accs[1].full(), op=ALU.mult)
        if tix == 0:
            continue
        ti = tix - 1
        for oc in range(8):
            i2 = oc % 2
            ps = pb[4 + i2]
            for i in range(22):
                P.pe.matmul(out=ps.full(), lhsT=Wd[:, i, oc * 128:(oc + 1) * 128], rhs=act[:, i, :],
                            start=(i == 0), stop=(i == 21))
            P.dve.tensor_tensor(out=xo[i2].full(), in0=ps.full(), in1=xst[:, oc, :], op=ALU.add)
            P.dma("sync" if i2 else "gpsimd", out=o_x[oc * 128:(oc + 1) * 128, ti * 512:(ti + 1) * 512], in_=xo[i2].full())
    P.emit()
    return nc, P


def run_D1(inp, l, resA, om, of, y, x_full):
    if "D1" not in _PROG_CACHE:
        _PROG_CACHE["D1"] = build_D1()[0]
    nc = _PROG_CACHE["D1"]
    cst = np.ascontiguousarray(inp["ssm_norm_g"][l].reshape(8, 128).T)
    xf = x_full.reshape(16384, D)
    in_maps = []
    for c in range(8):
        b, q = c // 4, c % 4
        ts = slice(q * T, (q + 1) * T)
        in_maps.append({
            "omT": np.ascontiguousarray(om[b][:, ts]), "ofT": np.ascontiguousarray(of[b][:, ts]),
            "yT": np.ascontiguousarray(y[b][:, ts]), "szT": np.asarray(resA[c]["o_sz"]),
            "gT": np.asarray(resA[c]["o_g"]), "xT": np.ascontiguousarray(xf[c * T:(c + 1) * T].T),
            "w_a": np.ascontiguousarray(inp["w_br_mla"][l]), "w_b": np.ascontiguousarray(inp["w_br_fox"][l]),
            "w_c": np.ascontiguousarray(inp["w_br_ssm"][l]), "w_o": np.ascontiguousarray(inp["w_out"][l]),
            "cst": cst,
        })
    res = run_bass_kernel_spmd(nc, in_maps, core_ids=list(range(8))).results
    xm = np.concatenate([np.asarray(r["o_x"]).T for r in res], axis=0)
    return xm.reshape(2, S_, D)


def run_D2(inp, l, xm_full):
    cp = d2_colpack(inp, l)
    off = dict(cp.off)
    off["_n"] = cp.n
    if "D2" not in _PROG_CACHE:
        _PROG_CACHE["D2"] = build_D2(off)[0]
    nc = _PROG_CACHE["D2"]
    cst = cp.array()
    xf = xm_full.reshape(16384, D)
    in_maps = []
    for c in range(8):
        t0 = c * T
        xt = np.zeros((D, T + HALO), np.float32)
        xt[:, HALO:] = xf[t0:t0 + T].T
        if c % 4 != 0:
            xt[:, 0:HALO] = xf[t0 - HALO:t0].T
        in_maps.append({"xT": np.ascontiguousarray(xt), "w_up": np.ascontiguousarray(inp["ffn_w_up"][l]),
                        "w_dn": np.ascontiguousarray(inp["ffn_w_down"][l]), "cst": cst})
    res = run_bass_kernel_spmd(nc, in_maps, core_ids=list(range(8))).results
    xo = np.concatenate([np.asarray(r["o_x"]).T for r in res], axis=0)
    return xo.reshape(2, S_, D)


def kernel_unfused(**inp):
    inp = {k: np.asarray(v) for k, v in inp.items()}
    x = inp["x"].astype(np.float32)
    pos = inp["positions"]
    for l in range(2):
        resA = run_A(inp, l, x, pos)
        om, of, y = run_BC(inp, l, resA)
        xm = run_D1(inp, l, resA, om, of, y, x)
        x = run_D2(inp, l, xm)
    return np.ascontiguousarray(x.astype(np.float32))


def kernel(**inp):
    return kernel_fused(**inp)


SW = 516
RG = [[0, 1, 2, 3], [4, 5, 6, 7]]
KT_L = T // 128


def fused_rowpack(inp, l):
    r = np.concatenate([inp["fox_b_f"][l], inp["ssm_dt_bias"][l], inp["ssm_A_log"][l], inp["ssm_D"][l]]).astype(np.float32)
    return np.ascontiguousarray(np.broadcast_to(r[None, :], (128, r.size)))


def build_fused(offA, offD2, stop=None, dbg=()):
    nc = new_nc()
    P = Prog(nc)
    EI, EO = "ExternalInput", "ExternalOutput"
    L = 2
    x0 = P.dram("x0", [D, 4 * SW], F32, EI)
    pos = P.dram("pos", [1, T], I32, EI)
    w_in = P.dram("w_in", [L, D, 7864], F32, EI)
    w_uq = P.dram("w_uq", [L, 384, 768], F32, EI)
    w_kp = P.dram("w_kp", [L, 256, 768], F32, EI)
    w_v = P.dram("w_v", [L, 256, 512], F32, EI)
    w_a = P.dram("w_a", [L, 512, D], F32, EI)
    w_b = P.dram("w_b", [L, 512, D], F32, EI)
    w_c = P.dram("w_c", [L, 1024, D], F32, EI)
    w_o = P.dram("w_o", [L, 1024, D], F32, EI)
    w_up = P.dram("w_up", [L, D, 5632], F32, EI)
    w_dn = P.dram("w_dn", [L, 2816, D], F32, EI)
    cstA_d = P.dram("cstA", [L, 128, offA["_n"]], F32, EI)
    cstD_d = P.dram("cstD", [L, 128, offD2["_n"]], F32, EI)
    gssm_d = P.dram("gssm", [L, 128, 8], F32, EI)
    rowc_d = P.dram("rowc", [L, 128, 56], F32, EI)
    sel_d = P.dram("sel", [128, 32], F32, EI)
    msk_d = P.dram("msk", [128, 8, 512], F32, EI)
    cm_d = P.dram("cm", [128, 4, 128], F32, EI)
    mats_d = P.dram("mats", [128, 192], F32, EI)
    out = P.dram("out", [D, T], F32, EO)
    xb = [x0, P.dram("xb1", [D, 4 * SW], F32)]
    xmid = P.dram("xmid", [D, 4 * SW], F32)
    qm = P.dram("qm", [8, 96, T], BF16)
    qf = P.dram("qf", [8, 64, T], BF16)
    fq = P.dram("fq", [8, 3, T], BF16)
    szd = P.dram("szd", [1024, T], BF16)
    gd = P.dram("gd", [3072, T], BF16)
    xtm = P.dram("xtm", [128, KT_L, 1024], BF16)
    btm = P.dram("btm", [128, KT_L, 256], BF16)
    bct = P.dram("bct", [512, T], BF16)
    dtd = P.dram("dtd", [128, KT_L, 16], F32)
    atd = P.dram("atd", [128, KT_L, 16], F32)
    omd = P.dram("omd", [512, T], BF16)
    ofd = P.dram("ofd", [512, T], BF16)
    yd = P.dram("yd", [1024, T], F32)
    kxm = [P.dram(f"kxm{m}", [768, 512], BF16) for m in range(4)]
    kxmg = [P.dram(f"kxmg{m}", [4 * 768, 512], BF16) for m in range(4)]
    kxf = [P.dram(f"kxf{m}", [512, 512], BF16) for m in range(4)]
    kxfg = [P.dram(f"kxfg{m}", [4 * 512, 512], BF16) for m in range(4)]
    vx = [P.dram(f"vx{m}", [2048, 256], BF16) for m in range(4)]
    vxg = [P.dram(f"vxg{m}", [4 * 2048, 256], BF16) for m in range(4)]
    sx = [P.dram(f"sx{i}", [256, 1024], F32) for i in range(2)]
    sxg = [P.dram(f"sxg{i}", [4 * 256, 1024], F32) for i in range(2)]
    fx = P.dram("fx", [128, 224], F32)
    fxg = P.dram("fxg", [4 * 128, 224], F32)
    tx = P.dram("tx", [128, 128], F32)
    txg = P.dram("txg", [4 * 128, 128], F32)
    wub = P.dram("wub", [D, 5632], BF16)
    wdb = P.dram("wdb", [2816, D], BF16)
    wab = P.dram("wab", [512, D], BF16)
    wbb = P.dram("wbb", [512, D], BF16)
    wcb = P.dram("wcb", [1024, D], BF16)
    wob = P.dram("wob", [1024, D], BF16)
    dbg_out = {}

    def gather_pairs(pairs):
        for (a, b) in pairs:
            P.pool.collective_compute(kind="AllGather", op=ALU.bypass, replica_groups=RG,
                                      ins=[a.full().re("(p a) c -> p (a c)", p=128)],
                                      outs=[b.full().re("(q a) c -> q (a c)", q=512)])

    def load_consts():
        d = {}
        d["cm"] = P.sb("cmb", [128, 4, 128], F32)
        P.dma("sync", out=d["cm"].full(), in_=cm_d.full())
        d["sel"] = P.sb("selb", [128, 32], F32)
        P.dma("sync", out=d["sel"].full(), in_=sel_d.full())
        d["eps"] = P.sb("eps", [128, 1], F32)
        P.dve.memset(ap=d["eps"].full(), constant=1e-6)
        d["one1"] = P.sb("one1", [128, 1], F32)
        P.dve.memset(ap=d["one1"].full(), constant=1.0)
        d["zero"] = P.sb("zero", [128, 1], F32)
        P.dve.memset(ap=d["zero"].full(), constant=0.0)
        return d

    def phase_A(l):
        K = load_consts()
        cmb = K["cm"]
        tri, ident, ones = cmb[:, 0, :], cmb[:, 2, :], cmb[:, 3, :]
        eps, one1 = K["eps"], K["one1"]
        xin = xb[l]
        cstb = P.sb("cstb", [128, offA["_n"]], F32)
        C = Cst(P, cstb, offA)
        P.dma("sync", out=cstb.full(), in_=cstA_d[l])
        rowc = P.sb("rowc", [128, 56], F32)
        P.dma("sync", out=rowc.full(), in_=rowc_d[l])
        matf = P.sb("matf", [128, 192], F32)
        matb = P.sb("matb", [128, 192], BF16)
        P.dma("sync", out=matf.full(), in_=mats_d.full())
        P.dve.tensor_copy(out=matb.full(), in_=matf.full())
        prh = matb[0:96, 0:96]
        selm = matb[0:32, 96:192]
        identb = P.sb("identb", [128, 128], BF16)
        P.dve.tensor_copy(out=identb.full(), in_=ident)
        Aneg_r = P.sb("Aneg_r", [128, 16], F32)
        P.act.activation(out=Aneg_r.full(), in_=rowc[:, 24:40], func=AF.Exp)
        P.dve.tensor_scalar(out=Aneg_r.full(), in0=Aneg_r.full(), scalar1=-1.0, scalar2=None, op0=ALU.mult)

        pb = [P.ps(f"pb{i}", [128, 512], F32) for i in range(7)]
        pbt = P.ps("pbt", [128, 1024], BF16)
        pbi = {}

        def nxt_ps(lo=0, hi=4):
            i = pbi.get(lo, 0)
            pbi[lo] = (i + 1) % (hi - lo)
            return pb[lo + i]

        Ctab = P.sb("Ctab", [96, T], F32)
        Stab = P.sb("Stab", [96, T], F32)
        hraw = P.sb("hraw", [96, 512], F32)
        hsq = P.sb("hsq", [96, 512], F32)
        hrs = P.sb("hrs", [96, 512], F32)
        hnf = P.sb("hnf", [96, 512], F32)
        hnb = P.sb("hnb", [96, 512], BF16)
        ht1 = P.sb("ht1", [96, 512], F32)
        ht2 = P.sb("ht2", [96, 512], F32)
        posf, rr_tmp, rr_m = hrs, hraw, hsq

        class _IV:
            def __init__(self, b):
                self.b = b

            def full(self):
                return self.b.full().bitcast(I32)
        posi, rr_i = _IV(ht1), _IV(ht2)

        def sin_table(outv, phase):
            P.dve.tensor_scalar(out=rr_tmp.full(), in0=posf.full(), scalar1=C.col("invf"), scalar2=phase,
                                op0=ALU.mult, op1=ALU.add)
            P.dve.tensor_scalar(out=rr_m.full(), in0=rr_tmp.full(), scalar1=1.0 / (2 * np.pi), scalar2=None, op0=ALU.mult)
            P.dve.tensor_copy(out=rr_i.full(), in_=rr_m.full())
            P.dve.tensor_copy(out=rr_m.full(), in_=rr_i.full())
            P.dve.scalar_tensor_tensor(out=rr_tmp.full(), in0=rr_m.full(), scalar=-2 * np.pi, in1=rr_tmp.full(),
                                       op0=ALU.mult, op1=ALU.add)
            P.dve.tensor_scalar(out=rr_m.full(), in0=rr_tmp.full(), scalar1=np.pi, scalar2=-2 * np.pi, op0=ALU.is_gt, op1=ALU.mult)
            P.dve.tensor_tensor(out=rr_tmp.full(), in0=rr_tmp.full(), in1=rr_m.full(), op=ALU.add)
            P.dve.tensor_scalar(out=rr_m.full(), in0=rr_tmp.full(), scalar1=-np.pi, scalar2=2 * np.pi, op0=ALU.is_lt, op1=ALU.mult)
            P.dve.tensor_tensor(out=rr_tmp.full(), in0=rr_tmp.full(), in1=rr_m.full(), op=ALU.add)
            P.act.activation(out=outv, in_=rr_tmp.full(), func=AF.Sin)

        for i in range(4):
            P.dma("sync", out=posi.full(), in_=pos[:, i * 512:(i + 1) * 512].f(lambda a: a.partition_broadcast(96)))
            P.dve.tensor_copy(out=posf.full(), in_=posi.full())
            sin_table(Stab[:, i * 512:(i + 1) * 512], 0.0)
            sin_table(Ctab[:, i * 512:(i + 1) * 512], np.pi / 2)
        P.dve.memset(ap=Stab[0:64, :], constant=0.0)
        P.dve.memset(ap=Ctab[0:64, :], constant=1.0)

        hn = P.sb("hn", [128, 8, 4 * SW], BF16)
        xst = P.sb("xst", [128, 8, 512], F32)
        sq = P.sb("sq", [128, 512], F32)
        rstd = P.sb("rstd", [128, 512], F32)
        xTv = xin.full().re("(kc p) n -> p kc n", p=128)

        def rstd_from(ps_view, n_feat, rows, rstd_view):
            P.act.activation(out=rstd_view, in_=ps_view, func=AF.Sqrt, bias=eps[0:rows, 0:1], scale=1.0 / n_feat)
            P.dve.reciprocal(out=rstd_view, in_=rstd_view)

        halos = [(m * SW, 4) for m in range(4)]
        main = [(m * SW + 4, 512) for m in range(4)]
        for (c0, w) in halos + main:
            P.dma("sync", out=xst[:, :, 0:w], in_=xTv[:, :, c0:c0 + w])
            ps = nxt_ps(4, 6)
            for kc in range(8):
                P.act.activation(out=sq[:, 0:w], in_=xst[:, kc, 0:w], func=AF.Square)
                P.pe.matmul(out=ps[:, 0:w], lhsT=ones, rhs=sq[:, 0:w], start=(kc == 0), stop=(kc == 7))
            rstd_from(ps[:, 0:w], 1024.0, 128, rstd[:, 0:w])
            for kc in range(8):
                P.dve.scalar_tensor_tensor(out=hn[:, kc, c0:c0 + w], in0=xst[:, kc, 0:w], scalar=C.col("g_mix", kc),
                                           in1=rstd[:, 0:w], op0=ALU.mult, op1=ALU.mult)

        wst = [P.sb(f"wst{i}", [128, 8, 256], F32) for i in range(2)]
        wbf = [P.sb(f"wbf{i}", [128, 8, 512], BF16) for i in range(2)]
        wcnt = [0]
        scnt = [0]
        w_inv = w_in[l].re("(kc p) n -> p kc n", p=128)

        SBv = 672 + 1544
        wplan = [(0, 384), (384, 288), (672, 512), (672 + 512, 512), (672 + 1024, 512), (672 + 1536, 8),
                 (SBv + 1024 + 1536, 16), (SBv, 512), (SBv + 512, 512)]
        wplan += [(SBv + 1024 + b_ * 512, 512) for b_ in range(3)]
        wplan += [(SBv + 2576 + b_ * 512, 512) for b_ in range(6)]
        wpend = {}

        def w_issue(g):
            c0, ncols = wplan[g]
            lst = []
            for h0 in range(0, ncols, 256):
                n = min(256, ncols - h0)
                si = scnt[0] % 2
                scnt[0] += 1
                P.dma("sync", out=wst[si][:, :, 0:n], in_=w_inv[:, :, c0 + h0:c0 + h0 + n])
                lst.append((si, h0, n))
            wpend[g] = lst

        def load_w(c0, ncols):
            g = wcnt[0]
            wcnt[0] += 1
            assert wplan[g] == (c0, ncols), (g, wplan[g], c0, ncols)
            i = g % 2
            if g not in wpend:
                w_issue(g)
            lst = wpend.pop(g)
            for (si, h0, n) in lst:
                P.act.copy(out=wbf[i][:, :, h0:h0 + n], in_=wst[si][:, :, 0:n])
            if g + 1 < len(wplan):
                w_issue(g + 1)
            return wbf[i]

        def proj(wb, wc0, mcols, c0, w, ps_view):
            for kc in range(8):
                P.pe.matmul(out=ps_view, lhsT=wb[:, kc, wc0:wc0 + mcols], rhs=hn[:, kc, c0:c0 + w],
                            start=(kc == 0), stop=(kc == 7))

        def proj_tm(wb, wc0, ncols, tok0, ps_view):
            for kc in range(8):
                P.pe.matmul(out=ps_view, lhsT=hn[:, kc, tok0:tok0 + 128], rhs=wb[:, kc, wc0:wc0 + ncols],
                            start=(kc == 0), stop=(kc == 7))

        ostg_cnt = [0]
        ostg = [P.sb(f"ostg{i}", [128, 512], BF16) for i in range(4)]

        def next_ostg():
            i = ostg_cnt[0] % 4
            ostg_cnt[0] += 1
            return ostg[i]

        def out_dma(dst_view, src_view):
            P.dma("sync" if ostg_cnt[0] % 2 else "scalar", out=dst_view, in_=src_view)

        hsets = [dict(hraw=hraw.full(), hsq=hsq.full(), hrs=hrs.full(), hnf=hnf.full(), hnb=hnb.full(),
                      ht1=ht1.full(), ht2=ht2.full())]
        hnb1 = P.sb("hnb1", [96, 512], BF16)
        hsets.append(dict(hraw=xst[0:96, 0, :].k(0), hsq=xst[0:96, 1, :].k(1), hrs=xst[0:96, 2, :].k(2),
                          hnf=xst[0:96, 3, :].k(3), hnb=hnb1.full(), ht1=xst[0:96, 4, :].k(4), ht2=xst[0:96, 5, :].k(5)))
        hb2 = P.sb("hb2", [96, 6, 512], F32)
        hnb2 = P.sb("hnb2", [96, 512], BF16)
        hsets.append(dict(hraw=hb2[:, 0, :].k(0), hsq=hb2[:, 1, :].k(1), hrs=hb2[:, 2, :].k(2),
                          hnf=hb2[:, 3, :].k(3), hnb=hnb2.full(), ht1=hb2[:, 4, :].k(4), ht2=hb2[:, 5, :].k(5)))
        hcnt = [0]

        def headnorm(projfn, d, gain_col, rope, tok0, dst_view):
            H = hsets[hcnt[0] % 3]
            hcnt[0] += 1
            ps_view = projfn()
            P.act.activation(out=H["hsq"][0:d, :], in_=ps_view, func=AF.Square)
            P.act.copy(out=H["hraw"][0:d, :], in_=ps_view)
            yield
            ps2 = nxt_ps(4, 6)
            P.pe.matmul(out=ps2[0:d, :], lhsT=cmb[0:d, 3, 0:d], rhs=H["hsq"][0:d, :], start=True, stop=True)
            rstd_from(ps2[0:d, :], float(d), d, H["hrs"][0:d, :])
            og = next_ostg()
            if not rope:
                P.dve.scalar_tensor_tensor(out=og[0:d, :], in0=H["hraw"][0:d, :], scalar=gain_col, in1=H["hrs"][0:d, :],
                                           op0=ALU.mult, op1=ALU.mult)
            else:
                P.dve.scalar_tensor_tensor(out=H["hnf"][0:d, :], in0=H["hraw"][0:d, :], scalar=gain_col, in1=H["hrs"][0:d, :],
                                           op0=ALU.mult, op1=ALU.mult)
                P.act.copy(out=H["hnb"][0:d, :], in_=H["hnf"][0:d, :])
                yield
                ps3 = nxt_ps(6, 7)
                P.pe.matmul(out=ps3[0:d, :], lhsT=prh, rhs=H["hnb"][0:d, :], start=True, stop=True)
                P.dve.tensor_tensor(out=H["ht1"][0:d, :], in0=H["hnf"][0:d, :], in1=Ctab[0:d, tok0:tok0 + 512], op=ALU.mult)
                P.dve.tensor_tensor(out=H["ht2"][0:d, :], in0=ps3[0:d, :], in1=Stab[0:d, tok0:tok0 + 512], op=ALU.mult)
                P.pool.tensor_tensor(out=og[0:d, :], in0=H["ht1"][0:d, :], in1=H["ht2"][0:d, :], op=ALU.add)
            out_dma(dst_view, og[0:d, :])

        def run_pipe(gens, depth=3):
            gens = iter(gens)
            active = []
            while True:
                started = False
                if len(active) < depth:
                    g = next(gens, None)
                    if g is not None:
                        started = True
                        try:
                            next(g)
                            active.append(g)
                        except StopIteration:
                            pass
                if not active and not started:
                    break
                olds = active[:-1] if (started and active) else list(active)
                for g in olds:
                    try:
                        next(g)
                    except StopIteration:
                        active.remove(g)

        lat = P.sb("lat", [128, 3, 512], F32)
        latn = P.sb("latn", [128, 3, 512], BF16)

        def latent_norm(ps_list, gname):
            nch = len(ps_list)
            ps2 = nxt_ps(4, 6)
            for i, psv in enumerate(ps_list):
                P.act.activation(out=sq.full(), in_=psv, func=AF.Square)
                P.act.copy(out=lat[:, i, :], in_=psv)
                P.pe.matmul(out=ps2.full(), lhsT=ones, rhs=sq.full(), start=(i == 0), stop=(i == nch - 1))
            rstd_from(ps2.full(), 128.0 * nch, 128, rstd.full())
            for i in range(nch):
                P.dve.scalar_tensor_tensor(out=latn[:, i, :], in0=lat[:, i, :], scalar=C.col(gname, i), in1=rstd.full(),
                                           op0=ALU.mult, op1=ALU.mult)

        def small_w(name, dram_l, kc_n, ncols, i):
            bfb = P.sb(name, [128, kc_n, ncols], BF16)
            dv = dram_l.re("(kc p) n -> p kc n", p=128)
            for kc in range(kc_n):
                si = scnt[0] % 2
                scnt[0] += 1
                stg = wst[si].full().re("p a b -> p (a b)")[:, 0:ncols]
                P.dma("sync", out=stg, in_=dv[:, kc, :])
                P.act.copy(out=bfb[:, kc, :], in_=stg)
            return bfb

        uqb = small_w("uqb", w_uq[l], 3, 768, 0)
        kpb = small_w("kpb", w_kp[l], 2, 768, 1)
        wvb = small_w("wvb", w_v[l], 2, 512, 0)

        vstg = [P.sb(f"vstg{i}", [128, 512], BF16) for i in range(2)]
        vcnt = [0]

        def v_out(kind, ktl, ps_view):
            vs = vstg[vcnt[0] % 2]
            vcnt[0] += 1
            P.act.copy(out=vs.full(), in_=ps_view)
            P.dma("sync" if vcnt[0] % 2 else "scalar",
                  out=vx[ktl // 4][kind * 1024:(kind + 1) * 1024, (ktl % 4) * 64:(ktl % 4 + 1) * 64].re("(h p) d -> p h d", p=128),
                  in_=vs.full().re("p (h d) -> p h d", h=8))

        wb = load_w(0, 384)
        for m, (c0, w) in enumerate(main):
            pss = []
            for ch in range(3):
                ps = nxt_ps(0, 4)
                proj(wb, ch * 128, 128, c0, 512, ps.full())
                pss.append(ps.full())
            latent_norm(pss, "g_cq")
            def mkq(h):
                def f():
                    ps = nxt_ps(0, 4)
                    for kc in range(3):
                        P.pe.matmul(out=ps[0:96, :], lhsT=uqb[:, kc, h * 96:(h + 1) * 96], rhs=latn[:, kc, :],
                                    start=(kc == 0), stop=(kc == 2))
                    return ps[0:96, :]
                return f
            run_pipe(headnorm(mkq(h), 96, C.col("g_q"), True, m * 512, qm[h, :, m * 512:(m + 1) * 512]) for h in range(8))
        wb = load_w(384, 288)
        krb = P.sb("krb", [32, 512], BF16)
        for m, (c0, w) in enumerate(main):
            pss = []
            for ch in range(2):
                ps = nxt_ps(0, 4)
                proj(wb, ch * 128, 128, c0, 512, ps.full())
                pss.append(ps.full())
            ps = nxt_ps(0, 4)
            proj(wb, 256, 32, c0, 512, ps[0:32, :])
            P.act.copy(out=krb.full(), in_=ps[0:32, :])
            latent_norm(pss, "g_ckv")
            def mkk(h):
                def f():
                    ps = nxt_ps(0, 4)
                    for kc in range(2):
                        P.pe.matmul(out=ps[0:96, :], lhsT=kpb[:, kc, h * 96:(h + 1) * 96], rhs=latn[:, kc, :],
                                    start=(kc == 0), stop=False)
                    P.pe.matmul(out=ps[0:96, :], lhsT=selm, rhs=krb.full(), start=False, stop=True)
                    return ps[0:96, :]
                return f
            run_pipe(headnorm(mkk(h), 96, C.col("g_k"), True, m * 512, kxm[m][h * 96:(h + 1) * 96, :]) for h in range(8))
            for j in range(4):
                ps = nxt_ps(0, 4)
                for kc in range(2):
                    P.pe.matmul(out=ps.full(), lhsT=latn[:, kc, j * 128:(j + 1) * 128], rhs=wvb[:, kc, :],
                                start=(kc == 0), stop=(kc == 1))
                v_out(0, m * 4 + j, ps.full())
        for (base, gname, isq) in ((672, "g_fq", True), (672 + 512, "g_fk", False)):
            wb = load_w(base, 512)
            def mkf(wb_, h, c0):
                def f():
                    ps = nxt_ps(0, 4)
                    proj(wb_, h * 64, 64, c0, 512, ps[0:64, :])
                    return ps[0:64, :]
                return f
            gl = []
            for m, (c0, w) in enumerate(main):
                for h in range(8):
                    dst = qf[h, :, m * 512:(m + 1) * 512] if isq else kxf[m][h * 64:(h + 1) * 64, :]
                    gl.append(headnorm(mkf(wb, h, c0), 64, C.col(gname), False, m * 512, dst))
            run_pipe(gl)
        wb = load_w(672 + 1024, 512)
        for m, (c0, w) in enumerate(main):
            for j in range(4):
                ps = nxt_ps(0, 4)
                proj_tm(wb, 0, 512, c0 + j * 128, ps.full())
                v_out(1, m * 4 + j, ps.full())
        gather_pairs(list(zip(kxm, kxmg)) + list(zip(vx, vxg)) + list(zip(kxf, kxfg)))
        FB = 672 + 1536
        SB = 672 + 1544
        lf_tm = P.sb("lf_tm", [128, KT_L, 8], F32)
        dt_tm = P.sb("dt_tm", [128, KT_L, 16], F32)
        a_tm = P.sb("a_tm", [128, KT_L, 16], F32)
        tmpr = P.sb("tmpr", [128, 16], F32)
        wf = load_w(FB, 8)
        for m, (c0, w) in enumerate(main):
            for j in range(4):
                kt = m * 4 + j
                ps = nxt_ps(0, 4)
                proj_tm(wf, 0, 8, c0 + j * 128, ps[:, 0:8])
                P.dve.tensor_tensor(out=tmpr[:, 0:8], in0=ps[:, 0:8], in1=rowc[:, 0:8], op=ALU.add)
                P.act.activation(out=tmpr[:, 0:8], in_=tmpr[:, 0:8], func=AF.Exp, scale=-1.0)
                P.act.activation(out=tmpr[:, 0:8], in_=tmpr[:, 0:8], func=AF.Ln, bias=one1[:, 0:1], scale=1.0)
                P.dve.tensor_scalar(out=lf_tm[:, kt, :], in0=tmpr[:, 0:8], scalar1=-1.0, scalar2=None, op0=ALU.mult)
        wd = load_w(SB + 1024 + 1536, 16)
        for m, (c0, w) in enumerate(main):
            for j in range(4):
                kt = m * 4 + j
                ps = nxt_ps(0, 4)
                proj_tm(wd, 0, 16, c0 + j * 128, ps[:, 0:16])
                P.dve.tensor_tensor(out=tmpr.full(), in0=ps[:, 0:16], in1=rowc[:, 8:24], op=ALU.add)
                P.act.activation(out=tmpr.full(), in_=tmpr.full(), func=AF.Exp)
                P.act.activation(out=dt_tm[:, kt, :], in_=tmpr.full(), func=AF.Ln, bias=one1[:, 0:1], scale=1.0)
                P.dve.tensor_tensor(out=a_tm[:, kt, :], in0=dt_tm[:, kt, :], in1=Aneg_r.full(), op=ALU.mult)
        P.dma("sync", out=dtd.full(), in_=dt_tm.full())
        P.dma("sync", out=atd.full(), in_=a_tm.full())
        def plain_group(base, ncols, func, bias_name, dst, dst_row0):
            wb_ = load_w(base, ncols)
            for m, (c0, w) in enumerate(main):
                for ch in range(ncols // 128):
                    ps = nxt_ps(0, 4)
                    proj(wb_, ch * 128, 128, c0, 512, ps.full())
                    og = next_ostg()
                    if bias_name is None:
                        P.act.activation(out=og.full(), in_=ps.full(), func=func)
                    else:
                        P.act.activation(out=og.full(), in_=ps.full(), func=func,
                                         bias=C.col(bias_name, (dst_row0 // 128) + ch))
                    out_dma(dst[dst_row0 + ch * 128:dst_row0 + (ch + 1) * 128, m * 512:(m + 1) * 512], og.full())

        for blk in range(2):
            plain_group(SB + blk * 512, 512, AF.Silu, None, szd, blk * 512)
        upre = P.sb("upre", [128, 516], F32)
        carry = P.sb("carry", [128, 4], F32)
        acc0 = P.sb("acc0", [128, 512], F32)
        tstg = [P.sb(f"tstg{i}", [128, 4, 128], BF16) for i in range(2)]
        tcnt = [0]
        for blk in range(3):
            wb = load_w(SB + 1024 + blk * 512, 512)
            for m, (c0, w) in enumerate(main):
                for ch in range(4):
                    cg = blk * 4 + ch
                    ps = nxt_ps(0, 4)
                    proj(wb, ch * 128, 128, c0 - 4, 4, ps[:, 0:4])
                    P.act.copy(out=upre[:, 0:4], in_=ps[:, 0:4])
                    ps = nxt_ps(0, 4)
                    proj(wb, ch * 128, 128, c0, 512, ps.full())
                    P.act.copy(out=upre[:, 4:516], in_=ps.full())
                    P.act.activation(out=acc0.full(), in_=ps.full(), func=AF.Identity, scale=C.col("cw3", cg), bias=C.col("cb", cg))
                    for k in range(3):
                        P.dve.scalar_tensor_tensor(out=acc0.full(), in0=upre[:, 1 + k:513 + k], scalar=C.col(f"cw{k}", cg),
                                                   in1=acc0.full(), op0=ALU.mult, op1=ALU.add)
                    og = next_ostg()
                    P.act.activation(out=og.full(), in_=acc0.full(), func=AF.Silu)
                    if cg >= 8:
                        out_dma(bct[(cg - 8) * 128:(cg - 7) * 128, m * 512:(m + 1) * 512], og.full())
                    if cg < 10:
                        i2 = tcnt[0] % 2
                        tcnt[0] += 1
                        for j in range(4):
                            P.pe.transpose(out=pbt[:, i2 * 512 + j * 128:i2 * 512 + (j + 1) * 128],
                                           in_=og[:, j * 128:(j + 1) * 128], identity=identb.full())
                        ts_ = tstg[i2]
                        P.dve.tensor_copy(out=ts_.full().re("p j f -> p (j f)"), in_=pbt[:, i2 * 512:(i2 + 1) * 512])
                        if cg < 8:
                            P.dma("sync", out=xtm[:, m * 4:(m + 1) * 4, cg * 128:(cg + 1) * 128], in_=ts_.full())
                        else:
                            P.dma("sync", out=btm[:, m * 4:(m + 1) * 4, (cg - 8) * 128:(cg - 7) * 128], in_=ts_.full())
        GB = SB + 2576
        for blk in range(6):
            plain_group(GB + blk * 512, 512, AF.Sigmoid, "b_gate", gd, blk * 512)
        fxs = P.sb("fxs", [128, 224], F32)
        within = P.sb("within", [128, KT_L, 8], F32)
        ttot = P.sb("ttot", [128, KT_L, 8], F32)
        f2 = lambda b: b.full().re("p a b -> p (a b)")
        ps = nxt_ps(0, 4)
        P.pe.matmul(out=ps[:, 0:128], lhsT=tri, rhs=f2(lf_tm), start=True, stop=True)
        P.act.copy(out=f2(within), in_=ps[:, 0:128])
        ps = nxt_ps(0, 4)
        P.pe.matmul(out=ps[:, 0:128], lhsT=ones, rhs=f2(lf_tm), start=True, stop=True)
        P.act.copy(out=f2(ttot), in_=ps[:, 0:128])
        Floc = fxs[:, 0:128].re("p (a b) -> p a b", b=8)
        totv = fxs[:, 128:160].re("p (a b) -> p a b", b=8)
        cacc = P.sb("cacc", [128, 8], F32)
        for m in range(4):
            P.dve.tensor_copy(out=Floc[:, 4 * m, :], in_=within[:, 4 * m, :])
            P.dve.tensor_copy(out=cacc.full(), in_=ttot[:, 4 * m, :])
            for j in range(1, 4):
                P.dve.tensor_tensor(out=Floc[:, 4 * m + j, :], in0=within[:, 4 * m + j, :], in1=cacc.full(), op=ALU.add)
                P.dve.tensor_tensor(out=cacc.full(), in0=cacc.full(), in1=ttot[:, 4 * m + j, :], op=ALU.add)
            P.dve.tensor_copy(out=totv[:, m, :], in_=cacc.full())
        ps = nxt_ps(0, 4)
        P.pe.transpose(out=ps[:, 0:128], in_=fxs[:, 0:128], identity=ident)
        FT = P.sb("FT", [128, 128], F32)
        r1 = P.sb("r1", [128, 128], F32)
        fh = [P.sb(f"fh{i}", [128, 128], BF16) for i in range(3)]
        P.act.copy(out=FT.full(), in_=ps[:, 0:128])
        P.dve.tensor_copy(out=fh[0].full(), in_=FT.full())
        P.dve.tensor_tensor(out=r1.full(), in0=FT.full(), in1=fh[0].full(), op=ALU.subtract)
        P.dve.tensor_copy(out=fh[1].full(), in_=r1.full())
        P.dve.tensor_tensor(out=r1.full(), in0=r1.full(), in1=fh[1].full(), op=ALU.subtract)
        P.dve.tensor_copy(out=fh[2].full(), in_=r1.full())
        for r in range(3):
            for kt in range(KT_L):
                P.dma("sync" if kt % 2 else "scalar", out=fq[:, r, kt * 128:(kt + 1) * 128], in_=fh[r][kt * 8:(kt + 1) * 8, :])
        P.dma("sync", out=fx[:, 0:160], in_=fxs[:, 0:160])

    def ssd_scan(l, K, pass1, fxs=None, dt_tm=None, a_tm=None, Hinit=None, rowc=None, pb=None):
        cmb = K["cm"]
        tri, trimask, ones = cmb[:, 0, :], cmb[:, 1, :], cmb[:, 3, :]
        if pb is None:
            pb = [P.ps(f"spb{i}", [128, 512], F32) for i in range(7)]
        if pass1:
            fxs = P.sb("decs", [128, 224], F32)
        if dt_tm is None:
            dt_tm = P.sb("dt_tm", [128, KT_L, 16], F32)
            a_tm = P.sb("a_tm", [128, KT_L, 16], F32)
            P.dma("sync", out=dt_tm.full(), in_=dtd.full())
            P.dma("sync", out=a_tm.full(), in_=atd.full())
        fl = lambda b: b.full().re("p c h -> p (c h)")
        Acum = P.sb("Acum", [128, KT_L, 16], F32)
        Atot = P.sb("Atot", [128, KT_L, 16], F32)
        wdec = P.sb("wdec", [128, KT_L, 16], F32)
        eAtot = P.sb("eAtot", [128, KT_L, 16], F32)
        psA = pb[0]
        P.pe.matmul(out=psA[:, 0:256], lhsT=tri, rhs=fl(a_tm), start=True, stop=True)
        P.act.copy(out=fl(Acum), in_=psA[:, 0:256])
        P.pe.matmul(out=psA[:, 256:512], lhsT=ones, rhs=fl(a_tm), start=True, stop=True)
        P.act.copy(out=fl(Atot), in_=psA[:, 256:512])
        P.act.activation(out=fl(eAtot), in_=fl(Atot), func=AF.Exp)
        P.dve.tensor_tensor(out=fl(wdec), in0=fl(Atot), in1=fl(Acum), op=ALU.subtract)
        P.act.activation(out=fl(wdec), in_=fl(wdec), func=AF.Exp)
        if not pass1:
            nAcum = P.sb("nAcum", [128, KT_L, 16], F32)
            eA = P.sb("eA", [128, KT_L, 16], F32)
            P.dve.tensor_scalar(out=fl(nAcum), in0=fl(Acum), scalar1=-1.0, scalar2=None, op0=ALU.mult)
            P.act.activation(out=fl(eA), in_=fl(Acum), func=AF.Exp)
            BCs = P.sb("BCs", [128, 4, T], BF16)
            P.dma("gpsimd", out=BCs.full(), in_=bct.full().re("(a p) t -> p a t", p=128))
            cb = P.sb("cb", [128, 2, 128], F32)
            NH = 4
            at = [P.sb(f"at{i}", [128, 128], F32) for i in range(NH)]
            tm = [P.sb(f"tm{i}", [128, 128], F32) for i in range(NH)]
            dec = [P.sb(f"dec{i}", [128, 128], F32) for i in range(NH)]
            MT = [P.sb(f"MT{i}", [128, 128], BF16) for i in range(NH)]
            t1 = P.sb("t1", [128, 1024], F32)
            t3 = P.sb("t3", [128, 1024], F32)
            yo = P.sb("yo", [128, 1024], BF16)
            yT = [P.sb(f"yT{i}", [128, 4, 128], F32) for i in range(2)]
        Hs = P.sb("Hs", [128, 1024], F32)
        Hb = P.sb("Hb", [128, 1024], BF16)
        xc = [P.sb(f"xc{i}", [128, 1024], BF16) for i in range(2)]
        Bc = [P.sb(f"Bc{i}", [128, 256], BF16) for i in range(2)]
        xdt = P.sb("xdt", [128, 1024], BF16)
        xdts = P.sb("xdts", [128, 1024], BF16)
        dsum = P.sb("dsum", [128, 16], F32)
        v3 = lambda v: v.re("p (h d) -> p h d", h=16)
        bc3 = lambda v: v.f(lambda a: a.unsqueeze(2).to_broadcast([128, 16, 64]))
        for m in range(4):
            if pass1:
                P.dve.memset(ap=Hs.full(), constant=0.0)
                P.dve.memset(ap=dsum.full(), constant=0.0)
            else:
                P.dve.tensor_copy(out=Hs.full(), in_=Hinit[:, m, :])
                P.act.copy(out=Hb.full(), in_=Hinit[:, m, :])
            for j in range(4):
                c = m * 4 + j
                x_c = xc[c % 2]
                B_c = Bc[c % 2]
                P.dma("sync", out=x_c.full(), in_=xtm[:, c, :])
                P.dma("gpsimd", out=B_c.full(), in_=btm[:, c, :])
                P.dve.tensor_tensor(out=v3(xdt.full()), in0=v3(x_c.full()), in1=bc3(dt_tm[:, c, :]), op=ALU.mult)
                P.pool.tensor_tensor(out=v3(xdts.full()), in0=v3(xdt.full()), in1=bc3(wdec[:, c, :]), op=ALU.mult)
                if not pass1:
                    cs = slice(c * 128, (c + 1) * 128)
                    ps_cb = pb[1]
                    for g in range(2):
                        P.pe.matmul(out=ps_cb[:, g * 128:(g + 1) * 128], lhsT=BCs[:, g, cs], rhs=BCs[:, 2 + g, cs],
                                    start=True, stop=True)
                    P.act.copy(out=cb.full().re("p a b -> p (a b)"), in_=ps_cb[:, 0:256])
                    ps_off = [pb[2], pb[3]]
                    for g in range(2):
                        P.pe.matmul(out=ps_off[g].full(), lhsT=BCs[:, 2 + g, cs], rhs=Hb[:, g * 512:(g + 1) * 512],
                                    start=True, stop=True)
                    ps_y = [pb[4], pb[5]]
                    def st1(h):
                        i2 = h % NH
                        g = h // 8
                        P.dve.tensor_scalar(out=at[i2].full(), in0=tri, scalar1=a_tm[:, c, h:h + 1], scalar2=None, op0=ALU.mult)
                        ps_A = pb[6]
                        P.pe.matmul(out=ps_A[:, i2 * 128:(i2 + 1) * 128], lhsT=ones, rhs=at[i2].full(), start=True, stop=True)
                        P.dve.tensor_tensor(out=tm[i2].full(), in0=ps_A[:, i2 * 128:(i2 + 1) * 128], in1=trimask, op=ALU.add)
                        P.act.activation(out=dec[i2].full(), in_=tm[i2].full(), func=AF.Exp, bias=nAcum[:, c, h:h + 1], scale=1.0)
                        P.pool.tensor_tensor(out=MT[i2].full(), in0=cb[:, g, :], in1=dec[i2].full(), op=ALU.mult)

                    def st2(h):
                        i2 = h % NH
                        g = h // 8
                        hh = h % 8
                        P.pe.matmul(out=ps_y[g][:, hh * 64:(hh + 1) * 64], lhsT=MT[i2].full(), rhs=xdt[:, h * 64:(h + 1) * 64],
                                    start=True, stop=True)

                    for hq in range(16 + 3):
                        if hq < 16:
                            st1(hq)
                        if hq >= 3:
                            st2(hq - 3)
                    for g in range(2):
                        gs_ = slice(g * 512, (g + 1) * 512)
                        v8 = lambda v: v.re("p (h d) -> p h d", h=8)
                        b8 = lambda v: v.f(lambda a: a.unsqueeze(2).to_broadcast([128, 8, 64]))
                        P.dve.tensor_tensor(out=v8(t1[:, gs_]), in0=v8(ps_off[g].full()), in1=b8(eA[:, c, g * 8:(g + 1) * 8]), op=ALU.mult)
                        P.dve.tensor_tensor(out=t1[:, gs_], in0=t1[:, gs_], in1=ps_y[g].full(), op=ALU.add)
                    P.pool.tensor_tensor(out=v3(t3.full()), in0=v3(x_c.full()), in1=bc3(rowc[:, 40:56]), op=ALU.mult)
                    P.pool.tensor_tensor(out=t3.full(), in0=t1.full(), in1=t3.full(), op=ALU.add)
                    for q4 in range(2):
                        pst = pb[2 + q4]
                        for jj in range(4):
                            fc = q4 * 4 + jj
                            P.pe.transpose(out=pst[:, jj * 128:(jj + 1) * 128], in_=t3[:, fc * 128:(fc + 1) * 128],
                                           identity=cmb[:, 2, :])
                        yt = yT[q4]
                        P.act.copy(out=yt.full().re("p a b -> p (a b)"), in_=pst.full())
                        P.dma("sync", out=yd[q4 * 512:(q4 + 1) * 512, c * 128:(c + 1) * 128].re("(a p) t -> p a t", p=128),
                              in_=yt.full())
                ps_h = [pb[0], pb[1]] if pass1 else [pb[4], pb[5]]
                for g in range(2):
                    P.pe.matmul(out=ps_h[g].full(), lhsT=B_c[:, g * 128:(g + 1) * 128], rhs=xdts[:, g * 512:(g + 1) * 512],
                                start=True, stop=True)
                P.dve.tensor_tensor(out=v3(Hs.full()), in0=v3(Hs.full()), in1=bc3(eAtot[:, c, :]), op=ALU.mult)
                for g in range(2):
                    P.dve.tensor_tensor(out=Hs[:, g * 512:(g + 1) * 512], in0=Hs[:, g * 512:(g + 1) * 512], in1=ps_h[g].full(), op=ALU.add)
                if pass1:
                    P.dve.tensor_tensor(out=dsum.full(), in0=dsum.full(), in1=Atot[:, c, :], op=ALU.add)
                else:
                    P.act.copy(out=Hb.full(), in_=Hs.full())
            if pass1:
                P.dma("sync", out=sx[m // 2][(m % 2) * 128:(m % 2 + 1) * 128, :], in_=Hs.full())
                P.act.activation(out=fxs[:, 160 + m * 16:160 + (m + 1) * 16], in_=dsum.full(), func=AF.Exp)
        if pass1:
            P.dma("sync", out=fx[:, 160:224], in_=fxs[:, 160:224])

    def load_fg():
        fg = P.sb("fg", [128, 4, 224], F32)
        P.dma("sync", out=fg.full(), in_=fxg.full().re("(r p) c -> p r c", p=128))
        return fg

    def phase_attn(l):
        K = load_consts()
        sel, zero = K["sel"], K["zero"]
        mskb = P.sb("mskb", [128, 8, 512], F32)
        P.dma("gpsimd", out=mskb.full(), in_=msk_d.full())
        fg = load_fg()
        offs = P.sb("offs", [128, 16, 8], F32)
        run = P.sb("run", [128, 8], F32)
        P.dve.memset(ap=run.full(), constant=0.0)
        for s_ in range(16):
            m, r = divmod(s_, 4)
            P.dve.tensor_copy(out=offs[:, s_, :], in_=run.full())
            P.dve.tensor_tensor(out=run.full(), in0=run.full(), in1=fg[:, r, 128 + m * 8:128 + (m + 1) * 8], op=ALU.add)
        offown = P.sb("offown", [128, 4, 8], F32)
        P.dve.memset(ap=offown.full(), constant=0.0)
        for m in range(4):
            for r in range(4):
                P.dve.scalar_tensor_tensor(out=offown[:, m, :], in0=offs[:, 4 * m + r, :], scalar=sel[:, 8 + 4 * m + r:9 + 4 * m + r],
                                           in1=offown[:, m, :], op0=ALU.mult, op1=ALU.add)
        negFg = P.sb("negFg", [128, 64, 8], F32)
        for s_ in range(16):
            m, r = divmod(s_, 4)
            src = fg[:, r, 0:128].re("p (a b) -> p a b", b=8)[:, 4 * m:4 * m + 4, :]
            P.dve.tensor_tensor(out=negFg[:, 4 * s_:4 * s_ + 4, :], in0=src,
                                in1=offs[:, s_, :].f(lambda a: a.unsqueeze(1).to_broadcast([128, 4, 8])), op=ALU.add)
        P.dve.tensor_scalar(out=negFg.full(), in0=negFg.full(), scalar1=-1.0, scalar2=None, op0=ALU.mult)
        biasm = P.sb("biasm", [128, 4, 64, 8], F32)
        for m in range(4):
            nk = (4 * m + 4) * 4
            P.dve.tensor_tensor(out=biasm[:, m, 0:nk, :], in0=negFg[:, 0:nk, :],
                                in1=offown[:, m, :].f(lambda a: a.unsqueeze(1).to_broadcast([128, nk, 8])), op=ALU.add)
            for jr in range(4):
                k0 = (4 * m + jr) * 4
                P.dve.tensor_scalar(out=biasm[:, m, k0:k0 + 4, :], in0=biasm[:, m, k0:k0 + 4, :],
                                    scalar1=sel[:, 28 + jr:29 + jr], scalar2=None, op0=ALU.add)

        pb = [P.ps(f"pb{i}", [128, 512], F32) for i in range(8)]
        K_sb = [P.sb(f"K_sb{i}", [96, S_], BF16) for i in range(2)]
        Q_sb = [P.sb(f"Q_sb{i}", [96, T], BF16) for i in range(2)]
        V_sb = [P.sb(f"V_sb{i}", [128, NKT, 128], BF16) for i in range(2)]
        for i in range(2):
            P.dve.memset(ap=V_sb[i][:, :, 64:128], constant=1.0)
        NSB = 5
        LA = 3
        WARM = False
        pt = [P.sb(f"pt{i}", [128, 512], BF16) for i in range(NSB)]
        mt = [P.sb(f"mt{i}", [128, 512], F32) for i in range(3)]
        rl = P.sb("rl", [128, 512], F32)
        rl2 = P.sb("rl2", [64, 512], F32)
        ot = [P.sb(f"ot{i}", [64, 512], BF16) for i in range(2)]
        cnt = [0, 0, 0]
        heads = [(0, h) for h in range(8)] + [(1, h) for h in range(8)]

        def loads(idx):
            kind, h = heads[idx]
            i = idx % 2
            nd = 96 if kind == 0 else 64
            for r in range(4):
                vr = r * 2048 + kind * 1024 + h * 128
                for m in range(4):
                    s0 = (4 * m + r) * 512
                    if kind == 0:
                        ksrc = kxmg[m][r * 768 + h * 96:r * 768 + (h + 1) * 96, :]
                    else:
                        ksrc = kxfg[m][r * 512 + h * 64:r * 512 + (h + 1) * 64, :]
                    P.dma("sync" if (r + m) % 2 == 0 else "gpsimd", out=K_sb[i][0:nd, s0:s0 + 512], in_=ksrc)
                    g0 = (4 * m + r) * 4
                    P.dma("gpsimd" if (r + m) % 2 == 0 else "sync",
                          out=V_sb[i][:, g0:g0 + 4, 0:64],
                          in_=vxg[m][vr:vr + 128, :].re("p (j d) -> p j d", j=4))
            if kind == 0:
                P.dma("sync", out=Q_sb[i][0:96, :], in_=qm[h])
            else:
                P.dve.memset(ap=K_sb[i][64:96, :], constant=0.0)
                P.dve.memset(ap=K_sb[i][64:67, :], constant=8.0)
                P.dma("sync", out=Q_sb[i][0:64, :], in_=qf[h])
                P.pool.memset(ap=Q_sb[i][64:96, :], constant=0.0)
                P.dma("gpsimd", out=Q_sb[i][64:67, :], in_=fq[h])

        iters = []
        for idx in range(16):
            for m in range(4):
                nk = (4 * m + 4) * 4
                for kt in range(nk):
                    iters.append((idx, m, kt, nk))

        def stage_qk(n):
            idx, m, kt, nk = iters[n]
            kind, h = heads[idx]
            i = idx % 2
            dk = 96
            scale = 96.0 ** -0.5 if kind == 0 else 0.125
            i3 = n % NSB
            ps = pb[i3]
            P.pe.matmul(out=ps.full(), lhsT=K_sb[i][0:dk, kt * 128:(kt + 1) * 128],
                        rhs=Q_sb[i][0:dk, m * 512:(m + 1) * 512], start=True, stop=True)
            blk = kt // 4
            if blk >= 4 * m:
                jr = blk - 4 * m
                mm = mt[cnt[1] % 3]
                cnt[1] += 1
                P.dve.scalar_tensor_tensor(out=mm.full(), in0=mskb[:, kind * 4 + kt % 4, :],
                                           scalar=sel[:, 24 + jr:25 + jr], in1=ps.full(), op0=ALU.mult, op1=ALU.add)
                src = mm.full()
                bias = sel[:, 28 + jr:29 + jr] if kind == 0 else biasm[:, m, kt, h:h + 1]
            else:
                src = ps.full()
                bias = zero[:, 0:1] if kind == 0 else biasm[:, m, kt, h:h + 1]
            P.act.activation(out=pt[i3].full(), in_=src, func=AF.Exp, scale=scale, bias=bias)
            if WARM:
                P.pe.matmul(out=pb[6][:, 0:256], lhsT=K_sb[i][0:dk, kt * 128:(kt + 1) * 128],
                            rhs=Q_sb[i][0:dk, m * 512:m * 512 + 256], start=True, stop=True)

        def stage_pv(n):
            idx, m, kt, nk = iters[n]
            kind, h = heads[idx]
            i = idx % 2
            oacc = pb[5 + (idx * 4 + m) % 2]
            P.pe.matmul(out=oacc.full(), lhsT=V_sb[i][:, kt, :], rhs=pt[n % NSB].full(), start=(kt == 0), stop=(kt == nk - 1))
            if kt == nk - 1:
                odst = omd if kind == 0 else ofd
                P.dve.reciprocal(out=rl[64:128, :], in_=oacc[64:128, :])
                P.dve.tensor_copy(out=rl2.full(), in_=rl[64:128, :])
                o = ot[m % 2]
                P.dve.tensor_tensor(out=o.full(), in0=oacc[0:64, :], in1=rl2.full(), op=ALU.mult)
                P.dma("sync", out=odst[h * 64:(h + 1) * 64, m * 512:(m + 1) * 512], in_=o.full())

        pcs = [P.sb(f"pcs{i}", [128, 2048], F32) for i in range(2)]
        pcb = [P.sb(f"pcb{i}", [128, 2048], BF16) for i in range(2)]
        jobs = []
        for (src, dst, rows, cols) in ((w_a[l], wab, 512, D), (w_b[l], wbb, 512, D), (w_c[l], wcb, 1024, D), (w_o[l], wob, 1024, D),
                                       (w_up[l], wub, D, 5632), (w_dn[l], wdb, 2816, D)):
            for r0 in range(0, rows, 128):
                for c0 in range(0, cols, 2048):
                    n_ = min(2048, cols - c0)
                    jobs.append((src[r0:r0 + 128, c0:c0 + n_], dst[r0:r0 + 128, c0:c0 + n_], n_))
        jcnt = [0]

        def precast_one():
            if jcnt[0] >= len(jobs):
                return
            src, dst, n_ = jobs[jcnt[0]]
            i = jcnt[0] % 2
            jcnt[0] += 1
            P.dma("gpsimd", out=pcs[i][:, 0:n_], in_=src)
            P.pool.tensor_copy(out=pcb[i][:, 0:n_], in_=pcs[i][:, 0:n_])
            P.dma("gpsimd", out=dst, in_=pcb[i][:, 0:n_])

        every = max(1, len(iters) // (len(jobs) + 4))
        loads(0)
        loads(1)
        for n in range(len(iters) + LA):
            if n < len(iters):
                stage_qk(n)
            if n >= LA:
                stage_pv(n - LA)
                idx_p, m_p, kt_p, nk_p = iters[n - LA]
                if m_p == 3 and kt_p == nk_p - 1 and idx_p + 2 < 16:
                    loads(idx_p + 2)
            if n % every == every - 1:
                precast_one()
        while jcnt[0] < len(jobs):
            precast_one()

    def phase_ssd2(l):
        K = load_consts()
        sel = K["sel"]
        rowc = P.sb("rowc", [128, 56], F32)
        P.dma("sync", out=rowc.full(), in_=rowc_d[l])
        fg = load_fg()
        Hin = P.sb("Hin", [128, 1024], F32)
        Hsel = P.sb("Hsel", [128, 4, 1024], F32)
        Sst = [P.sb(f"Sst{i}", [128, 1024], F32) for i in range(2)]
        P.dve.memset(ap=Hin.full(), constant=0.0)
        P.dve.memset(ap=Hsel.full(), constant=0.0)
        v3 = lambda v: v.re("p (h d) -> p h d", h=16)
        for s_ in range(16):
            m, r = divmod(s_, 4)
            P.dve.scalar_tensor_tensor(out=Hsel[:, m, :], in0=Hin.full(), scalar=sel[:, 8 + s_:9 + s_], in1=Hsel[:, m, :],
                                       op0=ALU.mult, op1=ALU.add)
            if s_ < 15:
                st_ = Sst[s_ % 2]
                P.dma("sync" if s_ % 2 else "gpsimd", out=st_.full(),
                      in_=sxg[m // 2][r * 256 + (m % 2) * 128:r * 256 + (m % 2 + 1) * 128, :])
                dcs = fg[:, r, 160 + m * 16:160 + (m + 1) * 16]
                P.dve.tensor_tensor(out=v3(Hin.full()), in0=v3(Hin.full()),
                                    in1=dcs.f(lambda a: a.unsqueeze(2).to_broadcast([128, 16, 64])), op=ALU.mult)
                P.pool.tensor_tensor(out=Hin.full(), in0=Hin.full(), in1=st_.full(), op=ALU.add)
        ssd_scan(l, K, pass1=False, Hinit=Hsel, rowc=rowc)

    def write_tails(txs):
        P.dma("sync", out=tx.full(), in_=txs.full().re("p m k c -> p (m k c)"))

    def halo_exchange(dst):
        K = load_consts()
        sel = K["sel"]
        P.pool.collective_compute(kind="AllGather", op=ALU.bypass, replica_groups=RG, ins=[tx.full()], outs=[txg.full()])
        tg = P.sb("tg", [128, 4, 128], F32)
        P.dma("sync", out=tg.full(), in_=txg.full().re("(r p) c -> p r c", p=128))
        hl = P.sb("hl", [128, 4, 32], F32)
        P.dve.memset(ap=hl.full(), constant=0.0)
        for m in range(4):
            for r in range(4):
                P.dve.scalar_tensor_tensor(out=hl[:, m, :], in0=tg[:, r, m * 32:(m + 1) * 32], scalar=sel[:, r:r + 1],
                                           in1=hl[:, m, :], op0=ALU.mult, op1=ALU.add)
            if m >= 1:
                P.dve.scalar_tensor_tensor(out=hl[:, m, :], in0=tg[:, 3, (m - 1) * 32:m * 32], scalar=sel[:, 4:5],
                                           in1=hl[:, m, :], op0=ALU.mult, op1=ALU.add)
        dv = dst.full().re("(kc p) n -> p kc n", p=128)
        for m in range(4):
            P.dma("sync", out=dv[:, :, m * SW:m * SW + 4], in_=hl[:, m, :].re("p (k c) -> p k c", c=4))

    def phase_merge(l):
        K = load_consts()
        ones, eps = K["cm"][:, 3, :], K["eps"]
        gsb = P.sb("gsb", [128, 8], F32)
        P.dma("sync", out=gsb.full(), in_=gssm_d[l])
        pb = [P.ps(f"pb{i}", [128, 512], F32) for i in range(8)]
        def load_wb(dram_bf, kc_n, name, q):
            bfb = P.sb(name, [128, kc_n, D], BF16)
            P.dma(q, out=bfb.full(), in_=dram_bf.full().re("(kc p) n -> p kc n", p=128))
            return bfb

        Wa = load_wb(wab, 4, "Wa", "sync")
        Wb = load_wb(wbb, 4, "Wb", "gpsimd")
        Wc = load_wb(wcb, 8, "Wc", "sync")
        Wo = load_wb(wob, 8, "Wo", "gpsimd")
        ys = P.sb("ys", [128, 8, 512], F32)
        szs = P.sb("szs", [128, 8, 512], BF16)
        yn = P.sb("yn", [128, 8, 512], BF16)
        oms = P.sb("oms", [128, 4, 512], BF16)
        ofs = P.sb("ofs", [128, 4, 512], BF16)
        gs = P.sb("gs", [128, 24, 512], BF16)
        xs = P.sb("xs", [128, 8, 512], F32)
        sq = P.sb("sq", [128, 512], F32)
        rstd = P.sb("rstd", [128, 512], F32)
        m1 = [P.sb(f"m1_{i}", [128, 512], F32) for i in range(2)]
        m2 = [P.sb(f"m2_{i}", [128, 512], F32) for i in range(2)]
        m3 = [P.sb(f"m3_{i}", [128, 512], F32) for i in range(2)]
        mg = P.sb("mg", [128, 8, 512], BF16)
        xo = [P.sb(f"xo{i}", [128, 512], F32) for i in range(2)]
        txs = P.sb("txs", [128, 4, 8, 4], F32)
        ch = lambda d: d.full().re("(kc p) n -> p kc n", p=128)
        xmv = ch(xmid)
        for ti in range(4):
            ts = slice(ti * 512, (ti + 1) * 512)
            xsl = slice(ti * SW + 4, ti * SW + 516)
            P.dma("sync", out=ys.full(), in_=ch(yd)[:, :, ts])
            P.dma("gpsimd", out=szs.full(), in_=ch(szd)[:, :, ts])
            P.dma("sync", out=oms.full(), in_=ch(omd)[:, :, ts])
            P.dma("gpsimd", out=ofs.full(), in_=ch(ofd)[:, :, ts])
            P.dma("sync", out=gs.full(), in_=ch(gd)[:, :, ts])
            P.dma("gpsimd", out=xs.full(), in_=ch(xb[l])[:, :, xsl])
            ps = pb[7]
            for kc in range(8):
                P.dve.tensor_tensor(out=ys[:, kc, :], in0=ys[:, kc, :], in1=szs[:, kc, :], op=ALU.mult)
                P.act.activation(out=sq.full(), in_=ys[:, kc, :], func=AF.Square)
                P.pe.matmul(out=ps.full(), lhsT=ones, rhs=sq.full(), start=(kc == 0), stop=(kc == 7))
            P.act.activation(out=rstd.full(), in_=ps.full(), func=AF.Sqrt, bias=eps[:, 0:1], scale=1.0 / 1024.0)
            P.dve.reciprocal(out=rstd.full(), in_=rstd.full())
            for kc in range(8):
                P.dve.scalar_tensor_tensor(out=yn[:, kc, :], in0=ys[:, kc, :], scalar=gsb[:, kc:kc + 1], in1=rstd.full(),
                                           op0=ALU.mult, op1=ALU.mult)
            for oc in range(8):
                i2 = oc % 2
                osl = slice(oc * 128, (oc + 1) * 128)
                pa, pbb, pc = pb[0 + i2 * 3], pb[1 + i2 * 3], pb[2 + i2 * 3]
                for kc in range(4):
                    P.pe.matmul(out=pa.full(), lhsT=Wa[:, kc, osl], rhs=oms[:, kc, :], start=(kc == 0), stop=(kc == 3))
                for kc in range(4):
                    P.pe.matmul(out=pbb.full(), lhsT=Wb[:, kc, osl], rhs=ofs[:, kc, :], start=(kc == 0), stop=(kc == 3))
                for kc in range(8):
                    P.pe.matmul(out=pc.full(), lhsT=Wc[:, kc, osl], rhs=yn[:, kc, :], start=(kc == 0), stop=(kc == 7))
                P.dve.tensor_tensor(out=m1[i2].full(), in0=pa.full(), in1=gs[:, oc, :], op=ALU.mult)
                P.dve.tensor_tensor(out=m2[i2].full(), in0=pbb.full(), in1=gs[:, 8 + oc, :], op=ALU.mult)
                P.dve.tensor_tensor(out=m3[i2].full(), in0=pc.full(), in1=gs[:, 16 + oc, :], op=ALU.mult)
                P.pool.tensor_tensor(out=m1[i2].full(), in0=m1[i2].full(), in1=m2[i2].full(), op=ALU.add)
                P.pool.tensor_tensor(out=mg[:, oc, :], in0=m1[i2].full(), in1=m3[i2].full(), op=ALU.add)
            for oc in range(8):
                i2 = oc % 2
                ps = pb[6 + i2]
                for kc in range(8):
                    P.pe.matmul(out=ps.full(), lhsT=Wo[:, kc, oc * 128:(oc + 1) * 128], rhs=mg[:, kc, :],
                                start=(kc == 0), stop=(kc == 7))
                P.dve.tensor_tensor(out=xo[i2].full(), in0=ps.full(), in1=xs[:, oc, :], op=ALU.add)
                P.pool.tensor_copy(out=txs[:, ti, oc, :], in_=xo[i2][:, 508:512])
                P.dma("sync" if i2 else "gpsimd", out=xmv[:, oc, xsl], in_=xo[i2].full())
        write_tails(txs)

    def phase_ffn(l, last):
        K = load_consts()
        ones, eps = K["cm"][:, 3, :], K["eps"]
        cstb = P.sb("cstb", [128, offD2["_n"]], F32)
        C = Cst(P, cstb, offD2)
        P.dma("sync", out=cstb.full(), in_=cstD_d[l])
        pb = [P.ps(f"pb{i}", [128, 512], F32) for i in range(8)]
        Wu = P.sb("Wu", [128, 8, 5632], BF16)
        Wd = P.sb("Wd", [128, 22, D], BF16)
        wubv = wub.full().re("(kc p) n -> p kc n", p=128)
        for (c0, c1) in ((0, 512), (2816, 3328), (512, 2816), (3328, 5632)):
            P.dma("sync" if c0 < 2816 else "gpsimd", out=Wu[:, :, c0:c1], in_=wubv[:, :, c0:c1])
        wdbv = wdb.full().re("(kc p) n -> p kc n", p=128)
        P.dma("sync", out=Wd[:, 0:11, :], in_=wdbv[:, 0:11, :])
        P.dma("gpsimd", out=Wd[:, 11:22, :], in_=wdbv[:, 11:22, :])
        xst = P.sb("xst", [128, 8, 512], F32)
        hn = P.sb("hn", [128, 8, 512], BF16)
        sq = P.sb("sq", [128, 512], F32)
        rstd = P.sb("rstd", [128, 512], F32)
        act = P.sb("act", [128, 22, 512], BF16)
        upre = [P.sb(f"upre{i}", [128, 516], F32) for i in range(2)]
        acc = [P.sb(f"acc{i}", [128, 512], F32) for i in range(2)]
        sg = P.sb("sg", [128, 512], F32)
        carry = P.sb("carry", [128, 44, 4], F32)
        xo = [P.sb(f"xo{i}", [128, 512], F32) for i in range(2)]
        txs = P.sb("txs", [128, 4, 8, 4], F32)
        xTv = xmid.full().re("(kc p) n -> p kc n", p=128)
        dst = out if last else xb[l + 1]
        dv = dst.full().re("(kc p) n -> p kc n", p=128)
        tiles = []
        for m in range(4):
            tiles.append((m * SW, 4, True, m))
            tiles.append((m * SW + 4, 512, False, m))
        pcnt = [0]
        for (c0, w, is_halo, m) in tiles:
            P.dma("sync", out=xst[:, :, 0:w], in_=xTv[:, :, c0:c0 + w])
            ps = pb[7]
            for kc in range(8):
                P.act.activation(out=sq[:, 0:w], in_=xst[:, kc, 0:w], func=AF.Square)
                P.pe.matmul(out=ps[:, 0:w], lhsT=ones, rhs=sq[:, 0:w], start=(kc == 0), stop=(kc == 7))
            P.act.activation(out=rstd[:, 0:w], in_=ps[:, 0:w], func=AF.Sqrt, bias=eps[:, 0:1], scale=1.0 / 1024.0)
            P.dve.reciprocal(out=rstd[:, 0:w], in_=rstd[:, 0:w])
            for kc in range(8):
                P.dve.scalar_tensor_tensor(out=hn[:, kc, 0:w], in0=xst[:, kc, 0:w], scalar=C.col("g_ffn", kc),
                                           in1=rstd[:, 0:w], op0=ALU.mult, op1=ALU.mult)
            for i in range(22):
                accs = []
                for j, cg in enumerate((i, 22 + i)):
                    ps = pb[pcnt[0] % 4]
                    pcnt[0] += 1
                    for kc in range(8):
                        P.pe.matmul(out=ps[:, 0:w], lhsT=Wu[:, kc, cg * 128:(cg + 1) * 128], rhs=hn[:, kc, 0:w],
                                    start=(kc == 0), stop=(kc == 7))
                    if is_halo:
                        P.act.copy(out=carry[:, cg, :], in_=ps[:, 0:4])
                        continue
                    up = upre[j]
                    a0 = acc[j]
                    P.act.copy(out=up[:, 4:516], in_=ps.full())
                    P.act.activation(out=a0.full(), in_=ps.full(), func=AF.Identity, scale=C.col("fw2", cg), bias=C.col("fb", cg))
                    P.pool.tensor_copy(out=up[:, 0:4], in_=carry[:, cg, :])
                    P.dve.scalar_tensor_tensor(out=a0.full(), in0=up[:, 3:515], scalar=C.col("fw1", cg), in1=a0.full(),
                                               op0=ALU.mult, op1=ALU.add)
                    P.dve.scalar_tensor_tensor(out=a0.full(), in0=up[:, 2:514], scalar=C.col("fw0", cg), in1=a0.full(),
                                               op0=ALU.mult, op1=ALU.add)
                    accs.append(a0)
                if is_halo:
                    continue
                P.act.activation(out=sg.full(), in_=accs[0].full(), func=AF.Silu)
                P.pool.tensor_tensor(out=act[:, i, :], in0=sg.full(), in1=accs[1].full(), op=ALU.mult)
            if is_halo:
                continue
            for oc in range(8):
                i2 = oc % 2
                ps = pb[4 + i2]
                for i in range(22):
                    P.pe.matmul(out=ps.full(), lhsT=Wd[:, i, oc * 128:(oc + 1) * 128], rhs=act[:, i, :],
                                start=(i == 0), stop=(i == 21))
                P.dve.tensor_tensor(out=xo[i2].full(), in0=ps.full(), in1=xst[:, oc, :], op=ALU.add)
                if last:
                    P.dma("sync" if i2 else "gpsimd", out=dv[:, oc, m * 512:(m + 1) * 512], in_=xo[i2].full())
                else:
                    P.pool.tensor_copy(out=txs[:, m, oc, :], in_=xo[i2][:, 508:512])
                    P.dma("sync" if i2 else "gpsimd", out=dv[:, oc, m * SW + 4:m * SW + 516], in_=xo[i2].full())
        if not last:
            write_tails(txs)

    def gather_e1():
        gather_pairs(list(zip(sx, sxg)) + [(fx, fxg)])

    nl = L if stop is None else stop[0]
    done = False
    for l in range(nl):
        last_l = (stop is not None and l == nl - 1)
        phase_A(l)
        P.emit(final=False)
        ssd_scan(l, load_consts(), pass1=True)
        P.emit(final=False)
        if last_l and stop[1] == "A":
            break
        gather_e1()
        phase_attn(l)
        P.emit(final=False)
        phase_ssd2(l)
        P.emit(final=False)
        if last_l and stop[1] == "B":
            break
        phase_merge(l)
        P.emit(final=False)
        halo_exchange(xmid)
        P.emit(final=False)
        if last_l and stop[1] == "C":
            break
        phase_ffn(l, last=(l == L - 1))
        P.emit(final=False)
        if l < L - 1:
            halo_exchange(xb[l + 1])
            P.emit(final=False)
    loc = {"kxmg0": kxmg[0], "vxg0": vxg[0], "sxg0": sxg[0], "fxg": fxg, "qm": qm, "qf": qf, "fq": fq, "omd": omd, "ofd": ofd, "yd": yd,
           "xmid": xmid, "xb1": xb[1], "szd": szd, "gd": gd, "xtm": xtm, "btm": btm, "bct": bct, "dtd": dtd, "atd": atd}
    for name in dbg:
        src = loc[name]
        shp = list(src.h.shape) if hasattr(src.h, "shape") else None
        dd = P.dram("dbg_" + name, shp, src.h.dtype, EO)
        P.dma("sync", out=dd.full(), in_=src.full())
    P.emit(final=True)
    return nc, P


def _stripe_tokens(p):
    return np.concatenate([np.arange((4 * m + p) * 512, (4 * m + p + 1) * 512) for m in range(4)])


def fused_in_maps(inp):
    L = 2
    cpsA = [a_colpack(inp, l) for l in range(L)]
    cpsD = [d2_colpack(inp, l) for l in range(L)]
    offA = dict(cpsA[0].off)
    offA["_n"] = cpsA[0].n
    offD = dict(cpsD[0].off)
    offD["_n"] = cpsD[0].n
    cstA = np.stack([c.array() for c in cpsA])
    cstD = np.stack([c.array() for c in cpsD])
    w_kp = np.zeros((L, 256, 8, 96), np.float32)
    wukv = inp["mla_w_ukv"].reshape(L, 256, 8, 128)
    w_kp[:, :, :, 0:64] = wukv[:, :, :, 0:64]
    w_v = np.ascontiguousarray(wukv[:, :, :, 64:128].reshape(L, 256, 512))
    gssm = np.ascontiguousarray(inp["ssm_norm_g"].reshape(L, 8, 128).transpose(0, 2, 1))
    rowc = np.stack([fused_rowpack(inp, l) for l in range(L)])
    msk, cm = _bc_consts()
    mats = _const_mats()
    shared = {
        "w_in": np.ascontiguousarray(inp["w_in"]), "w_uq": np.ascontiguousarray(inp["mla_w_uq"]),
        "w_kp": np.ascontiguousarray(w_kp.reshape(L, 256, 768)), "w_v": w_v,
        "w_a": np.ascontiguousarray(inp["w_br_mla"]), "w_b": np.ascontiguousarray(inp["w_br_fox"]),
        "w_c": np.ascontiguousarray(inp["w_br_ssm"]), "w_o": np.ascontiguousarray(inp["w_out"]),
        "w_up": np.ascontiguousarray(inp["ffn_w_up"]), "w_dn": np.ascontiguousarray(inp["ffn_w_down"]),
        "cstA": cstA, "cstD": cstD, "gssm": gssm, "rowc": rowc, "msk": msk, "cm": cm, "mats": mats,
    }
    in_maps = []
    for c in range(8):
        b, p = c // 4, c % 4
        xT = np.zeros((D, 4 * SW), np.float32)
        xbT = inp["x"][b].T
        for m in range(4):
            s_ = 4 * m + p
            xT[:, m * SW + 4:m * SW + 516] = xbT[:, s_ * 512:(s_ + 1) * 512]
            if s_ > 0:
                xT[:, m * SW:m * SW + 4] = xbT[:, s_ * 512 - 4:s_ * 512]
        sel = np.zeros((128, 32), np.float32)
        if p >= 1:
            sel[:, p - 1] = 1.0
        else:
            sel[:, 4] = 1.0
        for s_ in range(16):
            if s_ % 4 == p:
                sel[:, 8 + s_] = 1.0
        for jr in range(4):
            sel[:, 24 + jr] = 1.0 if jr == p else 0.0
            sel[:, 28 + jr] = NEG if jr > p else 0.0
        d = dict(shared)
        d["x0"] = np.ascontiguousarray(xT)
        d["pos"] = np.ascontiguousarray(inp["positions"][b][_stripe_tokens(p)][None, :]).astype(np.int32)
        d["sel"] = sel
        in_maps.append(d)
    return in_maps, offA, offD


def kernel_fused(**inp):
    inp = {k: np.asarray(v) for k, v in inp.items()}
    in_maps, offA, offD = fused_in_maps(inp)
    if "F" not in _PROG_CACHE:
        _PROG_CACHE["F"] = build_fused(offA, offD)[0]
    res = run_bass_kernel_spmd(_PROG_CACHE["F"], in_maps, core_ids=list(range(8))).results
    xo = np.zeros((2, S_, D), np.float32)
    for c in range(8):
        b, p = c // 4, c % 4
        xo[b, _stripe_tokens(p), :] = np.asarray(res[c]["out"]).T
    return xo
```

```python
from contextlib import ExitStack
import numpy as np
import concourse.bass as bass
import concourse.mybir as mybir

F32 = mybir.dt.float32
BF16 = mybir.dt.bfloat16
I32 = mybir.dt.int32
ALU = mybir.AluOpType
AF = mybir.ActivationFunctionType
AX = mybir.AxisListType

COMPUTE = ("tensor", "vector", "scalar", "gpsimd")
QUEUES = ("sync", "gpsimd", "scalar")
NRING = 8


class View:
    __slots__ = ("buf", "ap", "key")

    def __init__(self, buf, ap, key=None):
        self.buf = buf
        self.ap = ap
        self.key = key

    def __getitem__(self, k):
        return View(self.buf, self.ap[k], self.key)

    def re(self, s, **kw):
        return View(self.buf, self.ap.rearrange(s, **kw), self.key)

    def bc(self, shape):
        return View(self.buf, self.ap.to_broadcast(shape), self.key)

    def bitcast(self, dt):
        return View(self.buf, self.ap.bitcast(dt), self.key)

    def k(self, key):
        return View(self.buf, self.ap, key)

    def f(self, fn):
        return View(self.buf, fn(self.ap), self.key)


class Buf:
    def __init__(self, name, handle, is_dram=False):
        self.name = name
        self.h = handle
        self.is_dram = is_dram
        self.regions = {}

    def full(self):
        ap = self.h.ap() if hasattr(self.h, "ap") and callable(getattr(self.h, "ap")) else self.h[:]
        return View(self, ap)

    def __getitem__(self, k):
        return View(self, self.h[k])


class Op:
    __slots__ = ("id", "eng", "meth", "kw", "deps", "is_dma", "signaled", "sem", "val", "prewait", "eidx")


class Eng:
    def __init__(self, P, name):
        self.P = P
        self.name = name

    def __getattr__(self, meth):
        def call(*a, **kw):
            assert not a, "use kwargs"
            return self.P._record(self.name, meth, kw)
        return call


class Prog:
    def __init__(self, nc):
        self.nc = nc
        self.ops = []
        self.gstack = ExitStack()
        self.stack = ExitStack()
        self.pe = Eng(self, "tensor")
        self.dve = Eng(self, "vector")
        self.act = Eng(self, "scalar")
        self.pool = Eng(self, "gpsimd")
        self.sp = Eng(self, "sync")
        st = self.gstack
        self.csem = {e: st.enter_context(nc.semaphore(f"c_{e}")) for e in COMPUTE}
        self.rings = {q: [st.enter_context(nc.semaphore(f"d_{q}{i}")) for i in range(NRING)] for q in QUEUES}
        self.ccsem = st.enter_context(nc.semaphore("ccsem"))
        self.cccount = 0
        self.ccount = {e: 0 for e in COMPUTE}
        self.dcount = {q: 0 for q in QUEUES}
        self.waited = {e: {} for e in ("sync",) + COMPUTE}
        self.emitted = 0
        self.barrier = []
        self.stats = {}
        self.nwaits = 0

    def sb(self, name, shape, dtype):
        self.nuid = getattr(self, "nuid", 0) + 1
        name = f"{name}_s{self.nuid}"
        t = self.stack.enter_context(self.nc.sbuf_tensor(name, list(shape), dtype))
        return Buf(name, t)

    def ps(self, name, shape, dtype):
        self.nuid = getattr(self, "nuid", 0) + 1
        name = f"{name}_p{self.nuid}"
        t = self.stack.enter_context(self.nc.psum_tensor(name, list(shape), dtype))
        return Buf(name, t)

    def dram(self, name, shape, dtype, kind="Internal"):
        t = self.nc.dram_tensor(name, list(shape), dtype, kind=kind)
        return Buf(name, t, is_dram=True)

    def _record(self, eng, meth, kw):
        op = Op()
        op.id = len(self.ops)
        op.eng = eng
        op.meth = meth
        op.kw = kw
        op.is_dma = meth in ("dma_start", "dma_start_transpose", "collective_compute")
        op.signaled = False
        op.sem = None
        op.val = 0
        op.prewait = None
        deps = set()
        extra_r = kw.pop("_reads", [])
        extra_w = kw.pop("_writes", [])
        writes, reads = [], []
        for k, v in kw.items():
            vs = v if isinstance(v, (list, tuple)) else [v]
            for x in vs:
                if isinstance(x, View):
                    if k in ("out", "accum_out", "outs") or (k == "ap" and meth in ("memset", "memzero")):
                        writes.append(x)
                    else:
                        reads.append(x)
        reads += extra_r
        writes += extra_w
        for v in reads:
            self._gather(v, False, deps)
        for v in writes:
            self._gather(v, True, deps)
        for v in reads:
            self._update(v, False, op.id)
        for v in writes:
            self._update(v, True, op.id)
        deps.discard(op.id)
        op.deps = deps
        self.ops.append(op)
        return op

    def _gather(self, v, is_write, deps):
        R = v.buf.regions
        if v.key is None:
            regs = list(R.values())
        else:
            regs = [R[k] for k in (v.key, None) if k in R]
        for reg in regs:
            if reg[0] is not None:
                deps.add(reg[0])
            if is_write:
                deps.update(reg[1])

    def _update(self, v, is_write, oid):
        R = v.buf.regions
        if is_write:
            if v.key is None:
                R.clear()
            R[v.key] = [oid, []]
        else:
            R.setdefault(v.key, [None, []])[1].append(oid)

    def dma(self, q, out, in_, **kw):
        eng = {"sync": self.sp, "gpsimd": self.pool, "scalar": self.act}[q]
        return eng.dma_start(out=out, in_=in_, **kw)

    def emit(self, final=True):
        nc = self.nc
        ops = self.ops
        phase = ops[self.emitted:]
        first_id = self.emitted
        self.emitted = len(ops)
        for op in phase:
            for d in op.deps:
                dop = ops[d]
                if d < first_id:
                    continue
                if dop.eng == "tensor" and op.eng == "tensor" and not dop.is_dma and not op.is_dma:
                    continue
                dop.signaled = True
        per = {}
        for op in phase:
            per.setdefault(op.eng, []).append(op)
        for e, lst in per.items():
            for op in reversed(lst):
                if not op.is_dma:
                    op.signaled = True
                    break
        for op in phase:
            if op.meth == "collective_compute":
                self.cccount += 1
                op.sem = self.ccsem
                op.val = self.cccount
                op.signaled = True
            elif op.is_dma:
                k = self.dcount[op.eng]
                self.dcount[op.eng] += 1
                op.sem = self.rings[op.eng][k % NRING]
                op.val = 16 * (k // NRING + 1)
                if k >= NRING:
                    op.prewait = (op.sem, 16 * (k // NRING))
                op.signaled = True
            elif op.signaled:
                self.ccount[op.eng] += 1
                op.sem = self.csem[op.eng]
                op.val = self.ccount[op.eng]
        for e, v in per.items():
            self.stats[e] = self.stats.get(e, 0) + len(v)
        barrier_in = list(self.barrier)
        dcount = self.dcount
        rings = self.rings

        def dma_final_waits():
            ws = []
            for q in QUEUES:
                n = dcount[q]
                for i in range(min(n, NRING)):
                    cnt = (n - 1 - i) // NRING + 1
                    ws.append((rings[q][i], 16 * cnt))
            if self.cccount > 0:
                ws.append((self.ccsem, self.cccount))
            return ws

        def run(engname, e):
            waited = self.waited[engname]

            def do_waits(ws):
                for sem, val in ws:
                    key = id(sem)
                    if waited.get(key, 0) >= val:
                        continue
                    waited[key] = val
                    e.wait_ge(sem, val)
                    self.nwaits += 1

            do_waits(barrier_in)
            for op in per.get(engname, []):
                ws = []
                if op.prewait is not None:
                    ws.append(op.prewait)
                for d in sorted(op.deps):
                    dop = ops[d]
                    if dop.sem is None:
                        continue
                    if dop.eng == "tensor" and op.eng == "tensor" and not dop.is_dma and not op.is_dma:
                        continue
                    ws.append((dop.sem, dop.val))
                do_waits(ws)
                kw = {}
                for k, v in op.kw.items():
                    if isinstance(v, View):
                        kw[k] = v.ap
                    elif isinstance(v, (list, tuple)) and v and isinstance(v[0], View):
                        kw[k] = [x.ap for x in v]
                    else:
                        kw[k] = v
                ins = getattr(e, op.meth)(**kw)
                if op.signaled:
                    ins.then_inc(op.sem, 16 if (op.is_dma and op.meth != "collective_compute") else 1)
            if final and engname == "sync":
                do_waits(dma_final_waits())

        with nc.Block() as block:
            @block.sync
            def _(e):
                run("sync", e)

            @block.tensor
            def _(e):
                run("tensor", e)

            @block.vector
            def _(e):
                run("vector", e)

            @block.scalar
            def _(e):
                run("scalar", e)

            @block.gpsimd
            def _(e):
                run("gpsimd", e)
        bar = dma_final_waits()
        for e in COMPUTE:
            if self.ccount[e] > 0:
                bar.append((self.csem[e], self.ccount[e]))
        self.barrier = bar
        self.stats["waits"] = self.nwaits
        self.stack.close()
        self.stack = ExitStack()
        if final:
            self.gstack.close()


from concourse.bass_utils import run_bass_kernel_spmd
import ml_dtypes

NBF = ml_dtypes.bfloat16
D = 1024
T = 2048
HALO = 4
NEG = -30000.0


class ColPack:
    def __init__(self):
        self.cols = []
        self.off = {}
        self.n = 0

    def add(self, name, vec, rows=128):
        vec = np.asarray(vec, np.float32).reshape(-1)
        assert vec.size % rows == 0
        m = vec.reshape(-1, rows).T
        a = np.zeros((128, m.shape[1]), np.float32)
        a[:rows] = m
        self.off[name] = (self.n, m.shape[1], rows)
        self.cols.append(a)
        self.n += m.shape[1]

    def array(self):
        return np.ascontiguousarray(np.concatenate(self.cols, axis=1))


class Cst:
    def __init__(self, P, buf, off):
        self.buf = buf
        self.off = off

    def col(self, name, j=0, rows=None):
        o, n, r = self.off[name]
        r = rows or r
        return self.buf[0:r, o + j:o + j + 1]

    def cols(self, name):
        o, n, r = self.off[name]
        return self.buf[0:r, o:o + n]


def new_nc():
    return bass.Bass("TRN2", target_bir_lowering=False)


def load_cast(P, q, dram_view, stage_view, bf_view, cast_eng):
    P.dma(q, out=stage_view, in_=dram_view)
    cast_eng.tensor_copy(out=bf_view, in_=stage_view)


A_OFF = None


def a_colpack(inp, l):
    cp = ColPack()
    cp.add("g_mix", inp["norm_mix_g"][l])
    cp.add("g_cq", inp["mla_q_norm_g"][l])
    cp.add("g_ckv", inp["mla_kv_norm_g"][l])
    cp.add("g_q", inp["mla_q_gain"][l], 96)
    cp.add("g_k", inp["mla_k_gain"][l], 96)
    cp.add("g_fq", inp["fox_q_gain"][l], 64)
    cp.add("g_fk", inp["fox_k_gain"][l], 64)
    cp.add("b_f", inp["fox_b_f"][l], 8)
    cw = inp["ssm_conv_w"][l]
    for k in range(4):
        cp.add(f"cw{k}", cw[k])
    cp.add("cb", inp["ssm_conv_b"][l])
    cp.add("dt_b", inp["ssm_dt_bias"][l], 16)
    cp.add("A_log", inp["ssm_A_log"][l], 16)
    cp.add("b_gate", inp["b_gate"][l])
    inv = 1.0 / (10000.0 ** (np.arange(0, 32, 2, dtype=np.float32) / 32.0))
    invf = np.zeros(96, np.float32)
    invf[64:80] = inv
    invf[80:96] = inv
    cp.add("invf", invf, 96)
    return cp


def build_A(off):
    nc = new_nc()
    P = Prog(nc)
    TT = T + HALO
    NT = T // 512
    EI, EO = "ExternalInput", "ExternalOutput"
    xT = P.dram("xT", [D, TT], F32, EI)
    pos = P.dram("pos", [1, T], I32, EI)
    w_in = P.dram("w_in", [D, 7864], F32, EI)
    w_uq = P.dram("w_uq", [384, 768], F32, EI)
    w_kp = P.dram("w_kp", [256, 768], F32, EI)
    w_v = P.dram("w_v", [256, 512], F32, EI)
    cst_d = P.dram("cst", [128, off["_n"]], F32, EI)
    mats = P.dram("mats", [128, 2 * 96], F32, EI)
    o_qm = P.dram("o_qm", [8, 96, T], BF16, EO)
    o_km = P.dram("o_km", [8, 96, T], BF16, EO)
    o_vm = P.dram("o_vm", [512, T], BF16, EO)
    o_qf = P.dram("o_qf", [8, 64, T], BF16, EO)
    o_kf = P.dram("o_kf", [8, 64, T], BF16, EO)
    o_vf = P.dram("o_vf", [512, T], BF16, EO)
    o_lf = P.dram("o_lf", [8, T], F32, EO)
    o_sz = P.dram("o_sz", [1024, T], BF16, EO)
    o_xbc = P.dram("o_xbc", [1536, T], BF16, EO)
    o_dt = P.dram("o_dt", [16, T], F32, EO)
    o_a = P.dram("o_a", [16, T], F32, EO)
    o_g = P.dram("o_g", [3072, T], BF16, EO)

    cstb = P.sb("cstb", [128, off["_n"]], F32)
    C = Cst(P, cstb, off)
    P.dma("sync", out=cstb.full(), in_=cst_d.full())
    matf = P.sb("matf", [128, 192], F32)
    matb = P.sb("matb", [128, 192], BF16)
    P.dma("sync", out=matf.full(), in_=mats.full())
    P.dve.tensor_copy(out=matb.full(), in_=matf.full())
    prh = matb[0:96, 0:96]
    sel = matb[0:32, 96:192]
    ones = P.sb("ones", [128, 128], F32)
    P.dve.memset(ap=ones.full(), constant=1.0)
    eps = P.sb("eps", [128, 1], F32)
    P.dve.memset(ap=eps.full(), constant=1e-6)
    one1 = P.sb("one1", [128, 1], F32)
    P.dve.memset(ap=one1.full(), constant=1.0)
    nbf = P.sb("nbf", [8, 1], F32)
    P.dve.tensor_scalar(out=nbf.full(), in0=C.col("b_f"), scalar1=-1.0, scalar2=None, op0=ALU.mult)
    Aneg = P.sb("Aneg", [16, 1], F32)
    P.act.activation(out=Aneg.full(), in_=C.col("A_log"), func=AF.Exp)
    P.dve.tensor_scalar(out=Aneg.full(), in0=Aneg.full(), scalar1=-1.0, scalar2=None, op0=ALU.mult)

    pb = [P.ps(f"pb{i}", [128, 512], F32) for i in range(8)]
    pbi = {}

    def nxt_ps(lo=0, hi=4):
        i = pbi.get(lo, 0)
        pbi[lo] = (i + 1) % (hi - lo)
        return pb[lo + i]

    Ctab = P.sb("Ctab", [96, T], F32)
    Stab = P.sb("Stab", [96, T], F32)
    posi = P.sb("posi", [96, 512], I32)
    posf = P.sb("posf", [96, 512], F32)
    rr_tmp = P.sb("rr_tmp", [96, 512], F32)
    rr_i = P.sb("rr_i", [96, 512], I32)
    rr_m = P.sb("rr_m", [96, 512], F32)

    def sin_table(outv, phase):
        P.dve.tensor_scalar(out=rr_tmp.full(), in0=posf.full(), scalar1=C.col("invf"), scalar2=phase,
                            op0=ALU.mult, op1=ALU.add)
        P.dve.tensor_scalar(out=rr_m.full(), in0=rr_tmp.full(), scalar1=1.0 / (2 * np.pi), scalar2=None, op0=ALU.mult)
        P.dve.tensor_copy(out=rr_i.full(), in_=rr_m.full())
        P.dve.tensor_copy(out=rr_m.full(), in_=rr_i.full())
        P.dve.scalar_tensor_tensor(out=rr_tmp.full(), in0=rr_m.full(), scalar=-2 * np.pi, in1=rr_tmp.full(),
                                   op0=ALU.mult, op1=ALU.add)
        P.dve.tensor_scalar(out=rr_m.full(), in0=rr_tmp.full(), scalar1=np.pi, scalar2=-2 * np.pi, op0=ALU.is_gt, op1=ALU.mult)
        P.dve.tensor_tensor(out=rr_tmp.full(), in0=rr_tmp.full(), in1=rr_m.full(), op=ALU.add)
        P.dve.tensor_scalar(out=rr_m.full(), in0=rr_tmp.full(), scalar1=-np.pi, scalar2=2 * np.pi, op0=ALU.is_lt, op1=ALU.mult)
        P.dve.tensor_tensor(out=rr_tmp.full(), in0=rr_tmp.full(), in1=rr_m.full(), op=ALU.add)
        P.act.activation(out=outv, in_=rr_tmp.full(), func=AF.Sin)

    for i in range(NT):
        P.dma("sync", out=posi.full(), in_=pos[:, i * 512:(i + 1) * 512].f(lambda a: a.partition_broadcast(96)))
        P.dve.tensor_copy(out=posf.full(), in_=posi.full())
        sin_table(Stab[:, i * 512:(i + 1) * 512], 0.0)
        sin_table(Ctab[:, i * 512:(i + 1) * 512], np.pi / 2)
    P.dve.memset(ap=Stab[0:64, :], constant=0.0)
    P.dve.memset(ap=Ctab[0:64, :], constant=1.0)

    hn = P.sb("hn", [128, 8, TT], BF16)
    xst = P.sb("xst", [128, 8, 512], F32)
    sq = P.sb("sq", [128, 512], F32)
    rstd = P.sb("rstd", [128, 512], F32)
    xTv = xT.full().re("(kc p) n -> p kc n", p=128)

    def rstd_from(ps_view, n_feat, rows, width, rstd_view):
        P.act.activation(out=rstd_view, in_=ps_view, func=AF.Sqrt, bias=eps[0:rows, 0:1], scale=1.0 / n_feat)
        P.dve.reciprocal(out=rstd_view, in_=rstd_view)

    tiles = [(0, HALO)] + [(HALO + i * 512, 512) for i in range(NT)]
    for (c0, w) in tiles:
        P.dma("sync", out=xst[:, :, 0:w], in_=xTv[:, :, c0:c0 + w])
        ps = nxt_ps(4, 6)
        for kc in range(8):
            P.act.activation(out=sq[:, 0:w], in_=xst[:, kc, 0:w], func=AF.Square)
            P.pe.matmul(out=ps[:, 0:w], lhsT=ones.full(), rhs=sq[:, 0:w], start=(kc == 0), stop=(kc == 7))
        rstd_from(ps[:, 0:w], 1024.0, 128, w, rstd[:, 0:w])
        for kc in range(8):
            P.dve.scalar_tensor_tensor(out=hn[:, kc, c0:c0 + w], in0=xst[:, kc, 0:w], scalar=C.col("g_mix", kc),
                                       in1=rstd[:, 0:w], op0=ALU.mult, op1=ALU.mult)

    wst = [P.sb(f"wst{i}", [128, 8, 512], F32) for i in range(2)]
    wbf = [P.sb(f"wbf{i}", [128, 8, 512], BF16) for i in range(2)]
    wcnt = [0]
    w_inv = w_in.full().re("(kc p) n -> p kc n", p=128)

    def load_w(c0, ncols):
        i = wcnt[0] % 2
        wcnt[0] += 1
        q = "sync" if i == 0 else "gpsimd"
        P.dma(q, out=wst[i][:, :, 0:ncols], in_=w_inv[:, :, c0:c0 + ncols])
        P.pool.tensor_copy(out=wbf[i][:, :, 0:ncols], in_=wst[i][:, :, 0:ncols])
        return wbf[i]

    def proj(wb, wc0, m, c0, w, ps_view):
        for kc in range(8):
            P.pe.matmul(out=ps_view, lhsT=wb[:, kc, wc0:wc0 + m], rhs=hn[:, kc, c0:c0 + w],
                        start=(kc == 0), stop=(kc == 7))

    ostg_cnt = [0]
    ostg = [P.sb(f"ostg{i}", [128, 512], BF16) for i in range(4)]

    def next_ostg():
        i = ostg_cnt[0] % 4
        ostg_cnt[0] += 1
        return ostg[i]

    def out_dma(dst_view, src_view):
        q = "sync" if ostg_cnt[0] % 2 else "gpsimd"
        P.dma(q, out=dst_view, in_=src_view)

    hraw = P.sb("hraw", [96, 512], F32)
    hsq = P.sb("hsq", [96, 512], F32)
    hrs = P.sb("hrs", [96, 512], F32)
    hnf = P.sb("hnf", [96, 512], F32)
    hnb = P.sb("hnb", [96, 512], BF16)
    ht1 = P.sb("ht1", [96, 512], F32)
    ht2 = P.sb("ht2", [96, 512], F32)

    def headnorm(ps_view, d, gain_col, rope, tok0, dst_view):
        P.act.activation(out=hsq[0:d, :], in_=ps_view, func=AF.Square)
        P.act.copy(out=hraw[0:d, :], in_=ps_view)
        ps2 = nxt_ps(4, 6)
        P.pe.matmul(out=ps2[0:d, :], lhsT=ones[0:d, 0:d], rhs=hsq[0:d, :], start=True, stop=True)
        rstd_from(ps2[0:d, :], float(d), d, 512, hrs[0:d, :])
        og = next_ostg()
        if not rope:
            P.dve.scalar_tensor_tensor(out=og[0:d, :], in0=hraw[0:d, :], scalar=gain_col, in1=hrs[0:d, :],
                                       op0=ALU.mult, op1=ALU.mult)
        else:
            P.dve.scalar_tensor_tensor(out=hnf[0:d, :], in0=hraw[0:d, :], scalar=gain_col, in1=hrs[0:d, :],
                                       op0=ALU.mult, op1=ALU.mult)
            P.act.copy(out=hnb[0:d, :], in_=hnf[0:d, :])
            ps3 = nxt_ps(6, 8)
            P.pe.matmul(out=ps3[0:d, :], lhsT=prh, rhs=hnb[0:d, :], start=True, stop=True)
            P.dve.tensor_tensor(out=ht1[0:d, :], in0=hnf[0:d, :], in1=Ctab[0:d, tok0:tok0 + 512], op=ALU.mult)
            P.dve.tensor_tensor(out=ht2[0:d, :], in0=ps3[0:d, :], in1=Stab[0:d, tok0:tok0 + 512], op=ALU.mult)
            P.pool.tensor_tensor(out=og[0:d, :], in0=ht1[0:d, :], in1=ht2[0:d, :], op=ALU.add)
        out_dma(dst_view, og[0:d, :])

    lat = P.sb("lat", [128, 3, 512], F32)
    latn = P.sb("latn", [128, 3, 512], BF16)

    def latent_norm(ps_list, gname):
        nch = len(ps_list)
        ps2 = nxt_ps(4, 6)
        for i, psv in enumerate(ps_list):
            P.act.activation(out=sq.full(), in_=psv, func=AF.Square)
            P.act.copy(out=lat[:, i, :], in_=psv)
            P.pe.matmul(out=ps2.full(), lhsT=ones.full(), rhs=sq.full(), start=(i == 0), stop=(i == nch - 1))
        rstd_from(ps2.full(), 128.0 * nch, 128, 512, rstd.full())
        for i in range(nch):
            P.dve.scalar_tensor_tensor(out=latn[:, i, :], in0=lat[:, i, :], scalar=C.col(gname, i), in1=rstd.full(),
                                       op0=ALU.mult, op1=ALU.mult)

    def small_w(name, dram, kc_n, ncols, i):
        stg = wst[i].full().re("p a b -> p (a b)")[:, 0:kc_n * ncols].re("p (a b) -> p a b", a=kc_n)
        bfb = P.sb(name, [128, kc_n, ncols], BF16)
        P.dma("gpsimd", out=stg, in_=dram.full().re("(kc p) n -> p kc n", p=128))
        P.pool.tensor_copy(out=bfb.full(), in_=stg)
        return bfb

    uqb = small_w("uqb", w_uq, 3, 768, 0)
    kpb = small_w("kpb", w_kp, 2, 768, 1)
    wvb = small_w("wvb", w_v, 2, 512, 0)
    main = tiles[1:]
    wb = load_w(0, 384)
    for ti, (c0, w) in enumerate(main):
        pss = []
        for ch in range(3):
            ps = nxt_ps(0, 4)
            proj(wb, ch * 128, 128, c0, 512, ps.full())
            pss.append(ps.full())
        latent_norm(pss, "g_cq")
        for h in range(8):
            ps = nxt_ps(0, 4)
            for kc in range(3):
                P.pe.matmul(out=ps[0:96, :], lhsT=uqb[:, kc, h * 96:(h + 1) * 96], rhs=latn[:, kc, :],
                            start=(kc == 0), stop=(kc == 2))
            headnorm(ps[0:96, :], 96, C.col("g_q"), True, ti * 512, o_qm[h, :, ti * 512:(ti + 1) * 512])
    wb = load_w(384, 288)
    krb = P.sb("krb", [32, 512], BF16)
    for ti, (c0, w) in enumerate(main):
        pss = []
        for ch in range(2):
            ps = nxt_ps(0, 4)
            proj(wb, ch * 128, 128, c0, 512, ps.full())
            pss.append(ps.full())
        ps = nxt_ps(0, 4)
        proj(wb, 256, 32, c0, 512, ps[0:32, :])
        P.act.copy(out=krb.full(), in_=ps[0:32, :])
        latent_norm(pss, "g_ckv")
        for h in range(8):
            ps = nxt_ps(0, 4)
            for kc in range(2):
                P.pe.matmul(out=ps[0:96, :], lhsT=kpb[:, kc, h * 96:(h + 1) * 96], rhs=latn[:, kc, :],
                            start=(kc == 0), stop=False)
            P.pe.matmul(out=ps[0:96, :], lhsT=sel, rhs=krb.full(), start=False, stop=True)
            headnorm(ps[0:96, :], 96, C.col("g_k"), True, ti * 512, o_km[h, :, ti * 512:(ti + 1) * 512])
        for ch in range(4):
            ps = nxt_ps(0, 4)
            for kc in range(2):
                P.pe.matmul(out=ps.full(), lhsT=wvb[:, kc, ch * 128:(ch + 1) * 128], rhs=latn[:, kc, :],
                            start=(kc == 0), stop=(kc == 1))
            og = next_ostg()
            P.act.copy(out=og.full(), in_=ps.full())
            out_dma(o_vm[ch * 128:(ch + 1) * 128, ti * 512:(ti + 1) * 512], og.full())
    for (base, gname, dst) in ((672, "g_fq", o_qf), (672 + 512, "g_fk", o_kf)):
        wb = load_w(base, 512)
        for ti, (c0, w) in enumerate(main):
            for h in range(8):
                ps = nxt_ps(0, 4)
                proj(wb, h * 64, 64, c0, 512, ps[0:64, :])
                headnorm(ps[0:64, :], 64, C.col(gname), False, ti * 512, dst[h, :, ti * 512:(ti + 1) * 512])
    def plain_group(base, ncols, func, bias_name, dst, dst_row0):
        wb = load_w(base, ncols)
        for ti, (c0, w) in enumerate(main):
            for ch in range(ncols // 128):
                ps = nxt_ps(0, 4)
                proj(wb, ch * 128, 128, c0, 512, ps.full())
                og = next_ostg()
                if bias_name is None:
                    P.act.activation(out=og.full(), in_=ps.full(), func=func)
                else:
                    P.act.activation(out=og.full(), in_=ps.full(), func=func,
                                     bias=C.col(bias_name, (dst_row0 // 128) + ch))
                out_dma(dst[dst_row0 + ch * 128:dst_row0 + (ch + 1) * 128, ti * 512:(ti + 1) * 512], og.full())

    plain_group(672 + 1024, 512, AF.Copy, None, o_vf, 0)
    FB = 672 + 1536
    SB = 672 + 1544
    wf = load_w(FB, 8)
    lf1 = P.sb("lf1", [16, 512], F32)
    lf2 = P.sb("lf2", [16, 512], F32)
    for ti, (c0, w) in enumerate(main):
        ps = nxt_ps(0, 4)
        proj(wf, 0, 8, c0, 512, ps[0:8, :])
        P.act.activation(out=lf1[0:8, :], in_=ps[0:8, :], func=AF.Exp, bias=nbf[0:8, 0:1], scale=-1.0)
        P.act.activation(out=lf1[0:8, :], in_=lf1[0:8, :], func=AF.Ln, bias=one1[0:8, 0:1], scale=1.0)
        P.dve.tensor_scalar(out=lf2[0:8, :], in0=lf1[0:8, :], scalar1=-1.0, scalar2=None, op0=ALU.mult)
        P.dma("sync", out=o_lf[:, ti * 512:(ti + 1) * 512], in_=lf2[0:8, :])
    wd = load_w(SB + 1024 + 1536, 16)
    dt1 = P.sb("dt1", [16, 512], F32)
    dt2 = P.sb("dt2", [16, 512], F32)
    for ti, (c0, w) in enumerate(main):
        ps = nxt_ps(0, 4)
        proj(wd, 0, 16, c0, 512, ps[0:16, :])
        P.act.activation(out=dt1.full(), in_=ps[0:16, :], func=AF.Exp, bias=C.col("dt_b"), scale=1.0)
        P.act.activation(out=dt1.full(), in_=dt1.full(), func=AF.Ln, bias=one1[0:16, 0:1], scale=1.0)
        P.dma("sync", out=o_dt[:, ti * 512:(ti + 1) * 512], in_=dt1.full())
        P.dve.tensor_scalar(out=dt2.full(), in0=dt1.full(), scalar1=Aneg[:, 0:1], scalar2=None, op0=ALU.mult)
        P.dma("sync", out=o_a[:, ti * 512:(ti + 1) * 512], in_=dt2.full())
    for blk in range(2):
        plain_group(SB + blk * 512, 512, AF.Silu, None, o_sz, blk * 512)
    upre = P.sb("upre", [128, 516], F32)
    carry = P.sb("carry", [128, 12, 4], F32)
    acc = [P.sb(f"acc{i}", [128, 512], F32) for i in range(2)]
    for blk in range(3):
        wb = load_w(SB + 1024 + blk * 512, 512)
        for ch in range(4):
            cg = blk * 4 + ch
            ps = nxt_ps(0, 4)
            proj(wb, ch * 128, 128, 0, HALO, ps[:, 0:HALO])
            P.act.copy(out=carry[:, cg, :], in_=ps[:, 0:HALO])
        for ti, (c0, w) in enumerate(main):
            for ch in range(4):
                cg = blk * 4 + ch
                ps = nxt_ps(0, 4)
                proj(wb, ch * 128, 128, c0, 512, ps.full())
                P.act.copy(out=upre[:, 4:516], in_=ps.full())
                P.dve.tensor_copy(out=upre[:, 0:4], in_=carry[:, cg, :])
                P.pool.tensor_copy(out=carry[:, cg, :], in_=upre[:, 512:516])
                a0 = acc[0]
                P.dve.tensor_scalar(out=a0.full(), in0=upre[:, 4:516], scalar1=C.col("cw3", cg), scalar2=C.col("cb", cg),
                                    op0=ALU.mult, op1=ALU.add)
                for k in range(3):
                    P.dve.scalar_tensor_tensor(out=a0.full(), in0=upre[:, 1 + k:513 + k], scalar=C.col(f"cw{k}", cg),
                                               in1=a0.full(), op0=ALU.mult, op1=ALU.add)
                og = next_ostg()
                P.act.activation(out=og.full(), in_=a0.full(), func=AF.Silu)
                out_dma(o_xbc[cg * 128:(cg + 1) * 128, ti * 512:(ti + 1) * 512], og.full())
    GB = SB + 2576
    for blk in range(6):
        plain_group(GB + blk * 512, 512, AF.Sigmoid, "b_gate", o_g, blk * 512)
    P.emit()
    return nc, P


def _bf(a):
    return np.asarray(a).astype(np.float32)


_PROG_CACHE = {}


def _const_mats():
    m = np.zeros((128, 192), np.float32)
    for i in range(16):
        m[80 + i, 64 + i] = -1.0
        m[64 + i, 80 + i] = 1.0
    for i in range(32):
        m[i, 96 + 64 + i] = 1.0
    return m


def run_A(inp, l, x_full, pos_full):
    cp = a_colpack(inp, l)
    off = dict(cp.off)
    off["_n"] = cp.n
    if "A" not in _PROG_CACHE:
        _PROG_CACHE["A"] = build_A(off)[0]
    nc = _PROG_CACHE["A"]
    cst = cp.array()
    wukv = inp["mla_w_ukv"][l].reshape(256, 8, 128)
    w_kp = np.zeros((256, 8, 96), np.float32)
    w_kp[:, :, 0:64] = wukv[:, :, 0:64]
    w_v = np.ascontiguousarray(wukv[:, :, 64:128].reshape(256, 512))
    mats = _const_mats()
    xf = x_full.reshape(16384, D)
    in_maps = []
    for c in range(8):
        t0 = c * T
        xt = np.zeros((D, T + HALO), np.float32)
        xt[:, HALO:] = xf[t0:t0 + T].T
        if c % 4 != 0:
            xt[:, 0:HALO] = xf[t0 - HALO:t0].T
        in_maps.append({
            "xT": np.ascontiguousarray(xt),
            "pos": np.ascontiguousarray(pos_full.reshape(1, 16384)[:, t0:t0 + T]).astype(np.int32),
            "w_in": np.ascontiguousarray(inp["w_in"][l]),
            "w_uq": np.ascontiguousarray(inp["mla_w_uq"][l]),
            "w_kp": np.ascontiguousarray(w_kp.reshape(256, 768)),
            "w_v": w_v, "cst": cst, "mats": mats,
        })
    res = run_bass_kernel_spmd(nc, in_maps, core_ids=list(range(8)))
    return res.results


S_ = 8192
NKT = S_ // 128
NQT = S_ // 512


def build_BC():
    nc = new_nc()
    P = Prog(nc)
    EI, EO = "ExternalInput", "ExternalOutput"
    qm = P.dram("qm", [2, 96, S_], BF16, EI)
    km = P.dram("km", [2, 96, S_], BF16, EI)
    vm = P.dram("vm", [2, 128, NKT, 64], BF16, EI)
    qf = P.dram("qf", [2, 64, S_], BF16, EI)
    kf = P.dram("kf", [2, 64, S_], BF16, EI)
    vf = P.dram("vf", [2, 128, NKT, 64], BF16, EI)
    lf = P.dram("lf", [2, 128, NKT], F32, EI)
    msk = P.dram("msk", [128, 8, 512], F32, EI)
    cm = P.dram("cm", [128, 4, 128], F32, EI)
    x_tm = P.dram("x_tm", [128, NKT, 256], BF16, EI)
    B_tm = P.dram("B_tm", [128, NKT, 128], BF16, EI)
    BT = P.dram("BT", [128, S_], BF16, EI)
    CT = P.dram("CT", [128, S_], BF16, EI)
    dt_tm = P.dram("dt_tm", [128, NKT, 4], F32, EI)
    a_tm = P.dram("a_tm", [128, NKT, 4], F32, EI)
    Dv = P.dram("Dv", [128, 4], F32, EI)
    o_m = P.dram("o_m", [2, 64, S_], BF16, EO)
    o_f = P.dram("o_f", [2, 64, S_], BF16, EO)
    o_y = P.dram("o_y", [128, NKT, 256], F32, EO)
    fsc = P.dram("fsc", [3, S_], BF16)

    cmb = P.sb("cmb", [128, 4, 128], F32)
    P.dma("sync", out=cmb.full(), in_=cm.full())
    tri, trimask, ident, ones = cmb[:, 0, :], cmb[:, 1, :], cmb[:, 2, :], cmb[:, 3, :]
    mskb = P.sb("mskb", [128, 8, 512], F32)
    P.dma("gpsimd", out=mskb.full(), in_=msk.full())
    zero = P.sb("zero", [128, 1], F32)
    P.dve.memset(ap=zero.full(), constant=0.0)

    pb = [P.ps(f"pb{i}", [128, 512], F32) for i in range(8)]
    K_sb = P.sb("K_sb", [128, S_], BF16)
    Q_sb = P.sb("Q_sb", [128, S_], BF16)
    V_sb = P.sb("V_sb", [128, NKT, 128], BF16)
    P.dve.memset(ap=V_sb[:, :, 64:128], constant=1.0)
    pt = [P.sb(f"pt{i}", [128, 512], BF16) for i in range(3)]
    mt = [P.sb(f"mt{i}", [128, 512], F32) for i in range(2)]
    rl = P.sb("rl", [128, 512], F32)
    rl2 = P.sb("rl2", [64, 512], F32)
    ot = [P.sb(f"ot{i}", [64, 512], BF16) for i in range(2)]
    negF = P.sb("negF", [128, NKT], F32)

    cnt = [0, 0, 0]

    def attention(dk, scale, mask0, bias_fn, out_dram_h):
        for qt in range(NQT):
            oacc = pb[3 + qt % 2]
            nk = 4 * qt + 4
            for kt in range(nk):
                i3 = cnt[0] % 3
                cnt[0] += 1
                ps = pb[i3]
                P.pe.matmul(out=ps.full(), lhsT=K_sb[0:dk, kt * 128:(kt + 1) * 128],
                            rhs=Q_sb[0:dk, qt * 512:(qt + 1) * 512], start=True, stop=True)
                if kt >= 4 * qt:
                    m = mt[cnt[1] % 2]
                    cnt[1] += 1
                    P.dve.tensor_tensor(out=m.full(), in0=ps.full(), in1=mskb[:, mask0 + kt - 4 * qt, :], op=ALU.add)
                    src = m.full()
                else:
                    src = ps.full()
                P.act.activation(out=pt[i3].full(), in_=src, func=AF.Exp, scale=scale, bias=bias_fn(kt))
                P.pe.matmul(out=oacc.full(), lhsT=V_sb[:, kt, :], rhs=pt[i3].full(), start=(kt == 0), stop=(kt == nk - 1))
            P.dve.reciprocal(out=rl[64:128, :], in_=oacc[64:128, :])
            P.dve.tensor_copy(out=rl2.full(), in_=rl[64:128, :])
            o = ot[qt % 2]
            P.dve.tensor_tensor(out=o.full(), in0=oacc[0:64, :], in1=rl2.full(), op=ALU.mult)
            P.dma("sync", out=out_dram_h[:, qt * 512:(qt + 1) * 512], in_=o.full())

    for h in range(2):
        P.dma("sync", out=K_sb[0:96, :], in_=km[h])
        P.dma("gpsimd", out=Q_sb[0:96, :], in_=qm[h])
        P.dma("sync", out=V_sb[:, :, 0:64], in_=vm[h])
        attention(96, 96.0 ** -0.5, 0, lambda kt: zero[:, 0:1], o_m[h])

    lfs = P.sb("lfs", [128, NKT], F32)
    wi = P.sb("wi", [128, NKT], F32)
    sc = [P.sb(f"sc{i}", [128, NKT], F32) for i in range(2)]
    Ff = P.sb("Ff", [128, NKT], F32)
    FT = P.sb("FT", [64, 128], F32)
    r1 = P.sb("r1", [64, 128], F32)
    fh = [P.sb(f"fh{i}", [64, 128], BF16) for i in range(3)]
    for h in range(2):
        P.dma("sync", out=lfs.full(), in_=lf[h])
        ps = pb[5]
        P.pe.matmul(out=ps[:, 0:NKT], lhsT=tri, rhs=lfs.full(), start=True, stop=True)
        P.act.copy(out=wi.full(), in_=ps[:, 0:NKT])
        ps = pb[6]
        P.pe.matmul(out=ps[:, 0:NKT], lhsT=ones, rhs=lfs.full(), start=True, stop=True)
        P.act.copy(out=sc[0].full(), in_=ps[:, 0:NKT])
        P.dve.tensor_tensor(out=wi.full(), in0=wi.full(), in1=sc[0].full(), op=ALU.subtract)
        cur = 0
        d = 1
        while d < NKT:
            nx = 1 - cur
            P.dve.tensor_copy(out=sc[nx][:, 0:d], in_=sc[cur][:, 0:d])
            P.dve.tensor_tensor(out=sc[nx][:, d:NKT], in0=sc[cur][:, d:NKT], in1=sc[cur][:, 0:NKT - d], op=ALU.add)
            cur = nx
            d *= 2
        P.dve.tensor_tensor(out=Ff.full(), in0=wi.full(), in1=sc[cur].full(), op=ALU.add)
        P.dve.tensor_scalar(out=negF.full(), in0=Ff.full(), scalar1=-1.0, scalar2=None, op0=ALU.mult)
        ps = pb[7]
        P.pe.transpose(out=ps[0:64, 0:128], in_=Ff.full(), identity=ident)
        P.act.copy(out=FT.full(), in_=ps[0:64, 0:128])
        P.dve.tensor_copy(out=fh[0].full(), in_=FT.full())
        P.dve.tensor_tensor(out=r1.full(), in0=FT.full(), in1=fh[0].full(), op=ALU.subtract)
        P.dve.tensor_copy(out=fh[1].full(), in_=r1.full())
        P.dve.tensor_tensor(out=r1.full(), in0=r1.full(), in1=fh[1].full(), op=ALU.subtract)
        P.dve.tensor_copy(out=fh[2].full(), in_=r1.full())
        for r in range(3):
            P.dma("sync", out=fsc[r].re("(kt p) -> kt p", p=128), in_=fh[r].full())
        P.dma("sync", out=K_sb[0:64, :], in_=kf[h])
        P.dve.memset(ap=K_sb[64:67, :], constant=8.0)
        P.dma("gpsimd", out=Q_sb[0:64, :], in_=qf[h])
        P.dma("gpsimd", out=Q_sb[64:67, :], in_=fsc.full())
        P.dma("sync", out=V_sb[:, :, 0:64], in_=vf[h])
        attention(67, 0.125, 4, lambda kt: negF[:, kt:kt + 1], o_f[h])

    a_sb = P.sb("a_sb", [128, NKT, 4], F32)
    dt_sb = P.sb("dt_sb", [128, NKT, 4], F32)
    Dsb = P.sb("Dsb", [128, 4], F32)
    P.dma("sync", out=a_sb.full(), in_=a_tm.full())
    P.dma("sync", out=dt_sb.full(), in_=dt_tm.full())
    P.dma("sync", out=Dsb.full(), in_=Dv.full())
    BTs = K_sb
    CTs = Q_sb
    P.dma("sync", out=BTs.full(), in_=BT.full())
    P.dma("gpsimd", out=CTs.full(), in_=CT.full())
    Acum = P.sb("Acum", [128, NKT, 4], F32)
    nAcum = P.sb("nAcum", [128, NKT, 4], F32)
    Atot = P.sb("Atot", [128, NKT, 4], F32)
    eA = P.sb("eA", [128, NKT, 4], F32)
    wdec = P.sb("wdec", [128, NKT, 4], F32)
    eAtot = P.sb("eAtot", [128, NKT, 4], F32)
    fl = lambda b: b.full().re("p c h -> p (c h)")
    ps = pb[0]
    P.pe.matmul(out=ps[:, 0:256], lhsT=tri, rhs=fl(a_sb), start=True, stop=True)
    P.act.copy(out=fl(Acum), in_=ps[:, 0:256])
    ps = pb[1]
    P.pe.matmul(out=ps[:, 0:256], lhsT=ones, rhs=fl(a_sb), start=True, stop=True)
    P.act.copy(out=fl(Atot), in_=ps[:, 0:256])
    P.dve.tensor_scalar(out=fl(nAcum), in0=fl(Acum), scalar1=-1.0, scalar2=None, op0=ALU.mult)
    P.act.activation(out=fl(eA), in_=fl(Acum), func=AF.Exp)
    P.act.activation(out=fl(eAtot), in_=fl(Atot), func=AF.Exp)
    P.dve.tensor_tensor(out=fl(wdec), in0=fl(Atot), in1=fl(Acum), op=ALU.subtract)
    P.act.activation(out=fl(wdec), in_=fl(wdec), func=AF.Exp)

    Hs = P.sb("Hs", [128, 256], F32)
    Hb = P.sb("Hb", [128, 256], BF16)
    P.dve.memset(ap=Hs.full(), constant=0.0)
    P.dve.memset(ap=Hb.full(), constant=0.0)
    xc = [P.sb(f"xc{i}", [128, 256], BF16) for i in range(2)]
    Bc = [P.sb(f"Bc{i}", [128, 128], BF16) for i in range(2)]
    cb = P.sb("cb", [128, 128], F32)
    xdt = P.sb("xdt", [128, 256], BF16)
    xdts = P.sb("xdts", [128, 256], BF16)
    at = [P.sb(f"at{i}", [128, 128], F32) for i in range(2)]
    tm = [P.sb(f"tm{i}", [128, 128], F32) for i in range(2)]
    dec = [P.sb(f"dec{i}", [128, 128], F32) for i in range(2)]
    MT = [P.sb(f"MT{i}", [128, 128], BF16) for i in range(2)]
    t1 = P.sb("t1", [128, 256], F32)
    t3 = P.sb("t3", [128, 256], F32)
    yo = [P.sb(f"yo{i}", [128, 256], F32) for i in range(2)]
    v3 = lambda v: v.re("p (h d) -> p h d", h=4)
    bc3 = lambda v: v.f(lambda a: a.unsqueeze(2).to_broadcast([128, 4, 64]))
    for c in range(NKT):
        x_c = xc[c % 2]
        B_c = Bc[c % 2]
        P.dma("sync", out=x_c.full(), in_=x_tm[:, c, :])
        P.dma("gpsimd", out=B_c.full(), in_=B_tm[:, c, :])
        BT_c = BTs[:, c * 128:(c + 1) * 128]
        CT_c = CTs[:, c * 128:(c + 1) * 128]
        ps_cb = pb[0]
        P.pe.matmul(out=ps_cb[:, 0:128], lhsT=BT_c, rhs=CT_c, start=True, stop=True)
        P.act.copy(out=cb.full(), in_=ps_cb[:, 0:128])
        P.dve.tensor_tensor(out=v3(xdt.full()), in0=v3(x_c.full()), in1=bc3(dt_sb[:, c, :]), op=ALU.mult)
        P.pool.tensor_tensor(out=v3(xdts.full()), in0=v3(xdt.full()), in1=bc3(wdec[:, c, :]), op=ALU.mult)
        ps_off = pb[1]
        P.pe.matmul(out=ps_off[:, 0:256], lhsT=CT_c, rhs=Hb.full(), start=True, stop=True)
        ps_y = pb[2]
        for h in range(4):
            i2 = h % 2
            P.dve.tensor_scalar(out=at[i2].full(), in0=tri, scalar1=a_sb[:, c, h:h + 1], scalar2=None, op0=ALU.mult)
            ps_A = pb[3 + i2]
            P.pe.matmul(out=ps_A[:, 0:128], lhsT=ones, rhs=at[i2].full(), start=True, stop=True)
            P.dve.tensor_tensor(out=tm[i2].full(), in0=ps_A[:, 0:128], in1=trimask, op=ALU.add)
            P.act.activation(out=dec[i2].full(), in_=tm[i2].full(), func=AF.Exp, bias=nAcum[:, c, h:h + 1], scale=1.0)
            P.pool.tensor_tensor(out=MT[i2].full(), in0=cb.full(), in1=dec[i2].full(), op=ALU.mult)
            P.pe.matmul(out=ps_y[:, h * 64:(h + 1) * 64], lhsT=MT[i2].full(), rhs=xdt[:, h * 64:(h + 1) * 64],
                        start=True, stop=True)
        P.dve.tensor_tensor(out=v3(t1.full()), in0=v3(ps_off[:, 0:256]), in1=bc3(eA[:, c, :]), op=ALU.mult)
        P.dve.tensor_tensor(out=t1.full(), in0=t1.full(), in1=ps_y[:, 0:256], op=ALU.add)
        P.pool.tensor_tensor(out=v3(t3.full()), in0=v3(x_c.full()), in1=bc3(Dsb.full()), op=ALU.mult)
        y_ = yo[c % 2]
        P.pool.tensor_tensor(out=y_.full(), in0=t1.full(), in1=t3.full(), op=ALU.add)
        P.dma("sync", out=o_y[:, c, :], in_=y_.full())
        ps_h = pb[5]
        P.pe.matmul(out=ps_h[:, 0:256], lhsT=B_c.full(), rhs=xdts.full(), start=True, stop=True)
        P.dve.tensor_tensor(out=v3(Hs.full()), in0=v3(Hs.full()), in1=bc3(eAtot[:, c, :]), op=ALU.mult)
        P.dve.tensor_tensor(out=Hs.full(), in0=Hs.full(), in1=ps_h[:, 0:256], op=ALU.add)
        P.act.copy(out=Hb.full(), in_=Hs.full())
    P.emit()
    return nc, P


def _bc_consts():
    msk = np.zeros((128, 8, 512), np.float32)
    p = np.arange(128)[:, None]
    q = np.arange(512)[None, :]
    for j in range(4):
        key = j * 128 + p
        msk[:, j, :] = np.where((key // 64) > (q // 64), NEG, 0.0)
        msk[:, 4 + j, :] = np.where(key > q, NEG, 0.0)
    cm = np.zeros((128, 4, 128), np.float32)
    jj = np.arange(128)[:, None]
    ii = np.arange(128)[None, :]
    cm[:, 0, :] = (jj <= ii).astype(np.float32)
    cm[:, 1, :] = np.where(jj > ii, NEG, 0.0)
    cm[:, 2, :] = np.eye(128, dtype=np.float32)
    cm[:, 3, :] = 1.0
    return msk, cm


def _tm(a):
    S, n = a.shape
    return np.ascontiguousarray(a.reshape(S // 128, 128, n).transpose(1, 0, 2))


def run_BC(inp, l, resA):
    if "BC" not in _PROG_CACHE:
        _PROG_CACHE["BC"] = build_BC()[0]
    nc = _PROG_CACHE["BC"]
    msk, cm = _bc_consts()

    def gather(name, b):
        return np.concatenate([np.asarray(resA[b * 4 + i][name]) for i in range(4)], axis=-1)

    in_maps = []
    for c in range(8):
        b, hg = c // 4, c % 4
        qm = gather("o_qm", b)[2 * hg:2 * hg + 2]
        km = gather("o_km", b)[2 * hg:2 * hg + 2]
        vmf = gather("o_vm", b)
        qf = gather("o_qf", b)[2 * hg:2 * hg + 2]
        kf = gather("o_kf", b)[2 * hg:2 * hg + 2]
        vff = gather("o_vf", b)
        lff = gather("o_lf", b)
        xbc = gather("o_xbc", b)
        dtf = gather("o_dt", b)
        af = gather("o_a", b)
        g = hg // 2
        vm = np.stack([_tm(vmf[(2 * hg + h) * 64:(2 * hg + h + 1) * 64].T) for h in range(2)])
        vf = np.stack([_tm(vff[(2 * hg + h) * 64:(2 * hg + h + 1) * 64].T) for h in range(2)])
        lf = np.stack([np.ascontiguousarray(lff[2 * hg + h].reshape(NKT, 128).T) for h in range(2)])
        x_tm = _tm(xbc[hg * 256:(hg + 1) * 256].T)
        Bf = xbc[1024 + g * 128:1024 + (g + 1) * 128]
        Cf = xbc[1280 + g * 128:1280 + (g + 1) * 128]
        Dv = np.broadcast_to(inp["ssm_D"][l][4 * hg:4 * hg + 4][None, :], (128, 4)).astype(np.float32)
        in_maps.append({
            "qm": np.ascontiguousarray(qm), "km": np.ascontiguousarray(km), "vm": vm,
            "qf": np.ascontiguousarray(qf), "kf": np.ascontiguousarray(kf), "vf": vf, "lf": lf,
            "msk": msk, "cm": cm, "x_tm": x_tm, "B_tm": _tm(Bf.T), "BT": np.ascontiguousarray(Bf),
            "CT": np.ascontiguousarray(Cf), "dt_tm": _tm(dtf[4 * hg:4 * hg + 4].T),
            "a_tm": _tm(af[4 * hg:4 * hg + 4].T), "Dv": np.ascontiguousarray(Dv),
        })
    res = run_bass_kernel_spmd(nc, in_maps, core_ids=list(range(8))).results
    om = np.zeros((2, 512, S_), NBF)
    of = np.zeros((2, 512, S_), NBF)
    y = np.zeros((2, 1024, S_), np.float32)
    for c in range(8):
        b, hg = c // 4, c % 4
        om[b, hg * 128:(hg + 1) * 128] = np.asarray(res[c]["o_m"]).reshape(128, S_)
        of[b, hg * 128:(hg + 1) * 128] = np.asarray(res[c]["o_f"]).reshape(128, S_)
        yy = np.asarray(res[c]["o_y"])
        y[b, hg * 256:(hg + 1) * 256] = yy.transpose(2, 1, 0).reshape(256, S_)
    return om, of, y


def build_D1():
    nc = new_nc()
    P = Prog(nc)
    EI, EO = "ExternalInput", "ExternalOutput"
    NT = T // 512
    omT = P.dram("omT", [512, T], BF16, EI)
    ofT = P.dram("ofT", [512, T], BF16, EI)
    yT = P.dram("yT", [1024, T], F32, EI)
    szT = P.dram("szT", [1024, T], BF16, EI)
    gT = P.dram("gT", [3072, T], BF16, EI)
    xT = P.dram("xT", [D, T], F32, EI)
    w_a = P.dram("w_a", [512, D], F32, EI)
    w_b = P.dram("w_b", [512, D], F32, EI)
    w_c = P.dram("w_c", [1024, D], F32, EI)
    w_o = P.dram("w_o", [1024, D], F32, EI)
    cst_d = P.dram("cst", [128, 8], F32, EI)
    o_x = P.dram("o_x", [D, T], F32, EO)

    cstb = P.sb("cstb", [128, 8], F32)
    P.dma("sync", out=cstb.full(), in_=cst_d.full())
    ones = P.sb("ones", [128, 128], F32)
    P.dve.memset(ap=ones.full(), constant=1.0)
    eps = P.sb("eps", [128, 1], F32)
    P.dve.memset(ap=eps.full(), constant=1e-6)
    pb = [P.ps(f"pb{i}", [128, 512], F32) for i in range(8)]
    wst = [P.sb(f"wst{i}", [128, 4, 1024], F32) for i in range(2)]
    wcnt = [0]

    def load_w(dram, kc_n, name):
        bfb = P.sb(name, [128, kc_n, D], BF16)
        v = dram.full().re("(kc p) n -> p kc n", p=128)
        for k0 in range(0, kc_n, 4):
            i = wcnt[0] % 2
            wcnt[0] += 1
            P.dma("sync" if i == 0 else "gpsimd", out=wst[i].full(), in_=v[:, k0:k0 + 4, :])
            P.pool.tensor_copy(out=bfb[:, k0:k0 + 4, :], in_=wst[i].full())
        return bfb

    Wa = load_w(w_a, 4, "Wa")
    Wb = load_w(w_b, 4, "Wb")
    Wc = load_w(w_c, 8, "Wc")
    Wo = load_w(w_o, 8, "Wo")

    ys = P.sb("ys", [128, 8, 512], F32)
    szs = P.sb("szs", [128, 8, 512], BF16)
    yn = P.sb("yn", [128, 8, 512], BF16)
    oms = P.sb("oms", [128, 4, 512], BF16)
    ofs = P.sb("ofs", [128, 4, 512], BF16)
    gs = P.sb("gs", [128, 24, 512], BF16)
    xs = P.sb("xs", [128, 8, 512], F32)
    sq = P.sb("sq", [128, 512], F32)
    rstd = P.sb("rstd", [128, 512], F32)
    m1 = [P.sb(f"m1_{i}", [128, 512], F32) for i in range(2)]
    m2 = [P.sb(f"m2_{i}", [128, 512], F32) for i in range(2)]
    m3 = [P.sb(f"m3_{i}", [128, 512], F32) for i in range(2)]
    mg = P.sb("mg", [128, 8, 512], BF16)
    xo = [P.sb(f"xo{i}", [128, 512], F32) for i in range(2)]
    ch = lambda d: d.full().re("(kc p) n -> p kc n", p=128)
    for ti in range(NT):
        ts = slice(ti * 512, (ti + 1) * 512)
        P.dma("sync", out=ys.full(), in_=ch(yT)[:, :, ts])
        P.dma("gpsimd", out=szs.full(), in_=ch(szT)[:, :, ts])
        P.dma("sync", out=oms.full(), in_=ch(omT)[:, :, ts])
        P.dma("gpsimd", out=ofs.full(), in_=ch(ofT)[:, :, ts])
        P.dma("sync", out=gs.full(), in_=ch(gT)[:, :, ts])
        P.dma("gpsimd", out=xs.full(), in_=ch(xT)[:, :, ts])
        ps = pb[7]
        for kc in range(8):
            P.dve.tensor_tensor(out=ys[:, kc, :], in0=ys[:, kc, :], in1=szs[:, kc, :], op=ALU.mult)
            P.act.activation(out=sq.full(), in_=ys[:, kc, :], func=AF.Square)
            P.pe.matmul(out=ps.full(), lhsT=ones.full(), rhs=sq.full(), start=(kc == 0), stop=(kc == 7))
        P.act.activation(out=rstd.full(), in_=ps.full(), func=AF.Sqrt, bias=eps[:, 0:1], scale=1.0 / 1024.0)
        P.dve.reciprocal(out=rstd.full(), in_=rstd.full())
        for kc in range(8):
            P.dve.scalar_tensor_tensor(out=yn[:, kc, :], in0=ys[:, kc, :], scalar=cstb[:, kc:kc + 1], in1=rstd.full(),
                                       op0=ALU.mult, op1=ALU.mult)
        for oc in range(8):
            i2 = oc % 2
            osl = slice(oc * 128, (oc + 1) * 128)
            pa, pbb, pc = pb[0 + i2 * 3], pb[1 + i2 * 3], pb[2 + i2 * 3]
            for kc in range(4):
                P.pe.matmul(out=pa.full(), lhsT=Wa[:, kc, osl], rhs=oms[:, kc, :], start=(kc == 0), stop=(kc == 3))
            for kc in range(4):
                P.pe.matmul(out=pbb.full(), lhsT=Wb[:, kc, osl], rhs=ofs[:, kc, :], start=(kc == 0), stop=(kc == 3))
            for kc in range(8):
                P.pe.matmul(out=pc.full(), lhsT=Wc[:, kc, osl], rhs=yn[:, kc, :], start=(kc == 0), stop=(kc == 7))
            P.dve.tensor_tensor(out=m1[i2].full(), in0=pa.full(), in1=gs[:, oc, :], op=ALU.mult)
            P.dve.tensor_tensor(out=m2[i2].full(), in0=pbb.full(), in1=gs[:, 8 + oc, :], op=ALU.mult)
            P.dve.tensor_tensor(out=m3[i2].full(), in0=pc.full(), in1=gs[:, 16 + oc, :], op=ALU.mult)
            P.pool.tensor_tensor(out=m1[i2].full(), in0=m1[i2].full(), in1=m2[i2].full(), op=ALU.add)
            P.pool.tensor_tensor(out=mg[:, oc, :], in0=m1[i2].full(), in1=m3[i2].full(), op=ALU.add)
        for oc in range(8):
            i2 = oc % 2
            ps = pb[6 + i2]
            for kc in range(8):
                P.pe.matmul(out=ps.full(), lhsT=Wo[:, kc, oc * 128:(oc + 1) * 128], rhs=mg[:, kc, :],
                            start=(kc == 0), stop=(kc == 7))
            P.dve.tensor_tensor(out=xo[i2].full(), in0=ps.full(), in1=xs[:, oc, :], op=ALU.add)
            P.dma("sync" if i2 else "gpsimd", out=o_x[oc * 128:(oc + 1) * 128, ts], in_=xo[i2].full())
    P.emit()
    return nc, P


def d2_colpack(inp, l):
    cp = ColPack()
    cp.add("g_ffn", inp["norm_ffn_g"][l])
    cw = inp["ffn_conv_w"][l]
    for k in range(3):
        cp.add(f"fw{k}", cw[k])
    cp.add("fb", inp["ffn_conv_b"][l])
    return cp


def build_D2(off):
    nc = new_nc()
    P = Prog(nc)
    EI, EO = "ExternalInput", "ExternalOutput"
    NT = T // 512
    TT = T + HALO
    xT = P.dram("xT", [D, TT], F32, EI)
    w_up = P.dram("w_up", [D, 5632], F32, EI)
    w_dn = P.dram("w_dn", [2816, D], F32, EI)
    cst_d = P.dram("cst", [128, off["_n"]], F32, EI)
    o_x = P.dram("o_x", [D, T], F32, EO)

    cstb = P.sb("cstb", [128, off["_n"]], F32)
    C = Cst(P, cstb, off)
    P.dma("sync", out=cstb.full(), in_=cst_d.full())
    ones = P.sb("ones", [128, 128], F32)
    P.dve.memset(ap=ones.full(), constant=1.0)
    eps = P.sb("eps", [128, 1], F32)
    P.dve.memset(ap=eps.full(), constant=1e-6)
    pb = [P.ps(f"pb{i}", [128, 512], F32) for i in range(8)]
    wst = [P.sb(f"wst{i}", [128, 1024], F32) for i in range(2)]
    wcnt = [0]
    Wu = P.sb("Wu", [128, 8, 5632], BF16)
    Wd = P.sb("Wd", [128, 22, D], BF16)
    wuv = w_up.full().re("(kc p) n -> p kc n", p=128)
    for kc in range(8):
        for c0 in range(0, 5632, 1024):
            n = min(1024, 5632 - c0)
            i = wcnt[0] % 2
            wcnt[0] += 1
            P.dma("sync" if i == 0 else "gpsimd", out=wst[i][:, 0:n], in_=wuv[:, kc, c0:c0 + n])
            P.pool.tensor_copy(out=Wu[:, kc, c0:c0 + n], in_=wst[i][:, 0:n])
    wdv = w_dn.full().re("(kc p) n -> p kc n", p=128)
    for k0 in range(22):
        i = wcnt[0] % 2
        wcnt[0] += 1
        P.dma("sync" if i == 0 else "gpsimd", out=wst[i].full(), in_=wdv[:, k0, :])
        P.pool.tensor_copy(out=Wd[:, k0, :], in_=wst[i].full())

    xst = P.sb("xst", [128, 8, 512], F32)
    hn = P.sb("hn", [128, 8, 512], BF16)
    sq = P.sb("sq", [128, 512], F32)
    rstd = P.sb("rstd", [128, 512], F32)
    act = P.sb("act", [128, 22, 512], BF16)
    upre = [P.sb(f"upre{i}", [128, 516], F32) for i in range(2)]
    acc = [P.sb(f"acc{i}", [128, 512], F32) for i in range(2)]
    sg = P.sb("sg", [128, 512], F32)
    carry = P.sb("carry", [128, 44, 4], F32)
    xo = [P.sb(f"xo{i}", [128, 512], F32) for i in range(2)]
    xTv = xT.full().re("(kc p) n -> p kc n", p=128)
    tiles = [(0, HALO)] + [(HALO + i * 512, 512) for i in range(NT)]
    pcnt = [0]
    for tix, (c0, w) in enumerate(tiles):
        P.dma("sync", out=xst[:, :, 0:w], in_=xTv[:, :, c0:c0 + w])
        ps = pb[7]
        for kc in range(8):
            P.act.activation(out=sq[:, 0:w], in_=xst[:, kc, 0:w], func=AF.Square)
            P.pe.matmul(out=ps[:, 0:w], lhsT=ones.full(), rhs=sq[:, 0:w], start=(kc == 0), stop=(kc == 7))
        P.act.activation(out=rstd[:, 0:w], in_=ps[:, 0:w], func=AF.Sqrt, bias=eps[:, 0:1], scale=1.0 / 1024.0)
        P.dve.reciprocal(out=rstd[:, 0:w], in_=rstd[:, 0:w])
        for kc in range(8):
            P.dve.scalar_tensor_tensor(out=hn[:, kc, 0:w], in0=xst[:, kc, 0:w], scalar=C.col("g_ffn", kc),
                                       in1=rstd[:, 0:w], op0=ALU.mult, op1=ALU.mult)
        for i in range(22):
            accs = []
            for j, cg in enumerate((i, 22 + i)):
                ps = pb[pcnt[0] % 4]
                pcnt[0] += 1
                for kc in range(8):
                    P.pe.matmul(out=ps[:, 0:w], lhsT=Wu[:, kc, cg * 128:(cg + 1) * 128], rhs=hn[:, kc, 0:w],
                                start=(kc == 0), stop=(kc == 7))
                if tix == 0:
                    P.act.copy(out=carry[:, cg, :], in_=ps[:, 0:HALO])
                    continue
                up = upre[j]
                P.act.copy(out=up[:, 4:516], in_=ps.full())
                P.dve.tensor_copy(out=up[:, 0:4], in_=carry[:, cg, :])
                P.pool.tensor_copy(out=carry[:, cg, :], in_=up[:, 512:516])
                a0 = acc[j]
                P.dve.tensor_scalar(out=a0.full(), in0=up[:, 4:516], scalar1=C.col("fw2", cg), scalar2=C.col("fb", cg),
                                    op0=ALU.mult, op1=ALU.add)
                P.dve.scalar_tensor_tensor(out=a0.full(), in0=up[:, 3:515], scalar=C.col("fw1", cg), in1=a0.full(),
                                           op0=ALU.mult, op1=ALU.add)
                P.dve.scalar_tensor_tensor(out=a0.full(), in0=up[:, 2:514], scalar=C.col("fw0", cg), in1=a0.full(),
                                           op0=ALU.mult, op1=ALU.add)
                accs.append(a0)
            if tix == 0:
                continue
            P.act.activation(out=sg.full(), in_=accs[0].full(), func=AF.Silu)
            P.pool.tensor_tensor(out=act[:, i, :], in0=sg.full(), in1=accs[1].full(), op=ALU.mult)
        if tix == 0:
            continue
        ti = tix - 1
        for oc in range(8):
            i2 = oc % 2
            ps = pb[4 + i2]
            for i in range(22):
                P.pe.matmul(out=ps.full(), lhsT=Wd[:, i, oc * 128:(oc + 1) * 128], rhs=act[:, i, :],
                            start=(i == 0), stop=(i == 21))
            P.dve.tensor_tensor(out=xo[i2].full(), in0=ps.full(), in1=xst[:, oc, :], op=ALU.add)
            P.dma("sync" if i2 else "gpsimd", out=o_x[oc * 128:(oc + 1) * 128, ti * 512:(ti + 1) * 512], in_=xo[i2].full())
    P.emit()
    return nc, P


def run_D1(inp, l, resA, om, of, y, x_full):
    if "D1" not in _PROG_CACHE:
        _PROG_CACHE["D1"] = build_D1()[0]
    nc = _PROG_CACHE["D1"]
    cst = np.ascontiguousarray(inp["ssm_norm_g"][l].reshape(8, 128).T)
    xf = x_full.reshape(16384, D)
    in_maps = []
    for c in range(8):
        b, q = c // 4, c % 4
        ts = slice(q * T, (q + 1) * T)
        in_maps.append({
            "omT": np.ascontiguousarray(om[b][:, ts]), "ofT": np.ascontiguousarray(of[b][:, ts]),
            "yT": np.ascontiguousarray(y[b][:, ts]), "szT": np.asarray(resA[c]["o_sz"]),
            "gT": np.asarray(resA[c]["o_g"]), "xT": np.ascontiguousarray(xf[c * T:(c + 1) * T].T),
            "w_a": np.ascontiguousarray(inp["w_br_mla"][l]), "w_b": np.ascontiguousarray(inp["w_br_fox"][l]),
            "w_c": np.ascontiguousarray(inp["w_br_ssm"][l]), "w_o": np.ascontiguousarray(inp["w_out"][l]),
            "cst": cst,
        })
    res = run_bass_kernel_spmd(nc, in_maps, core_ids=list(range(8))).results
    xm = np.concatenate([np.asarray(r["o_x"]).T for r in res], axis=0)
    return xm.reshape(2, S_, D)


def run_D2(inp, l, xm_full):
    cp = d2_colpack(inp, l)
    off = dict(cp.off)
    off["_n"] = cp.n
    if "D2" not in _PROG_CACHE:
        _PROG_CACHE["D2"] = build_D2(off)[0]
    nc = _PROG_CACHE["D2"]
    cst = cp.array()
    xf = xm_full.reshape(16384, D)
    in_maps = []
    for c in range(8):
        t0 = c * T
        xt = np.zeros((D, T + HALO), np.float32)
        xt[:, HALO:] = xf[t0:t0 + T].T
        if c % 4 != 0:
            xt[:, 0:HALO] = xf[t0 - HALO:t0].T
        in_maps.append({"xT": np.ascontiguousarray(xt), "w_up": np.ascontiguousarray(inp["ffn_w_up"][l]),
                        "w_dn": np.ascontiguousarray(inp["ffn_w_down"][l]), "cst": cst})
    res = run_bass_kernel_spmd(nc, in_maps, core_ids=list(range(8))).results
    xo = np.concatenate([np.asarray(r["o_x"]).T for r in res], axis=0)
    return xo.reshape(2, S_, D)


def kernel_unfused(**inp):
    inp = {k: np.asarray(v) for k, v in inp.items()}
    x = inp["x"].astype(np.float32)
    pos = inp["positions"]
    for l in range(2):
        resA = run_A(inp, l, x, pos)
        om, of, y = run_BC(inp, l, resA)
        xm = run_D1(inp, l, resA, om, of, y, x)
        x = run_D2(inp, l, xm)
    return np.ascontiguousarray(x.astype(np.float32))


def kernel(**inp):
    return kernel_fused(**inp)


SW = 516
RG = [[0, 1, 2, 3], [4, 5, 6, 7]]
KT_L = T // 128


def fused_rowpack(inp, l):
    r = np.concatenate([inp["fox_b_f"][l], inp["ssm_dt_bias"][l], inp["ssm_A_log"][l], inp["ssm_D"][l]]).astype(np.float32)
    return np.ascontiguousarray(np.broadcast_to(r[None, :], (128, r.size)))


def build_fused(offA, offD2, stop=None, dbg=()):
    nc = new_nc()
    P = Prog(nc)
    EI, EO = "ExternalInput", "ExternalOutput"
    L = 2
    x0 = P.dram("x0", [D, 4 * SW], F32, EI)
    pos = P.dram("pos", [1, T], I32, EI)
    w_in = P.dram("w_in", [L, D, 7864], F32, EI)
    w_uq = P.dram("w_uq", [L, 384, 768], F32, EI)
    w_kp = P.dram("w_kp", [L, 256, 768], F32, EI)
    w_v = P.dram("w_v", [L, 256, 512], F32, EI)
    w_a = P.dram("w_a", [L, 512, D], F32, EI)
    w_b = P.dram("w_b", [L, 512, D], F32, EI)
    w_c = P.dram("w_c", [L, 1024, D], F32, EI)
    w_o = P.dram("w_o", [L, 1024, D], F32, EI)
    w_up = P.dram("w_up", [L, D, 5632], F32, EI)
    w_dn = P.dram("w_dn", [L, 2816, D], F32, EI)
    cstA_d = P.dram("cstA", [L, 128, offA["_n"]], F32, EI)
    cstD_d = P.dram("cstD", [L, 128, offD2["_n"]], F32, EI)
    gssm_d = P.dram("gssm", [L, 128, 8], F32, EI)
    rowc_d = P.dram("rowc", [L, 128, 56], F32, EI)
    sel_d = P.dram("sel", [128, 32], F32, EI)
    msk_d = P.dram("msk", [128, 8, 512], F32, EI)
    cm_d = P.dram("cm", [128, 4, 128], F32, EI)
    mats_d = P.dram("mats", [128, 192], F32, EI)
    out = P.dram("out", [D, T], F32, EO)
    xb = [x0, P.dram("xb1", [D, 4 * SW], F32)]
    xmid = P.dram("xmid", [D, 4 * SW], F32)
    qm = P.dram("qm", [8, 96, T], BF16)
    qf = P.dram("qf", [8, 64, T], BF16)
    fq = P.dram("fq", [8, 3, T], BF16)
    szd = P.dram("szd", [1024, T], BF16)
    gd = P.dram("gd", [3072, T], BF16)
    xtm = P.dram("xtm", [128, KT_L, 1024], BF16)
    btm = P.dram("btm", [128, KT_L, 256], BF16)
    bct = P.dram("bct", [512, T], BF16)
    dtd = P.dram("dtd", [128, KT_L, 16], F32)
    atd = P.dram("atd", [128, KT_L, 16], F32)
    omd = P.dram("omd", [512, T], BF16)
    ofd = P.dram("ofd", [512, T], BF16)
    yd = P.dram("yd", [1024, T], F32)
    kxm = [P.dram(f"kxm{m}", [768, 512], BF16) for m in range(4)]
    kxmg = [P.dram(f"kxmg{m}", [4 * 768, 512], BF16) for m in range(4)]
    kxf = [P.dram(f"kxf{m}", [512, 512], BF16) for m in range(4)]
    kxfg = [P.dram(f"kxfg{m}", [4 * 512, 512], BF16) for m in range(4)]
    vx = [P.dram(f"vx{m}", [2048, 256], BF16) for m in range(4)]
    vxg = [P.dram(f"vxg{m}", [4 * 2048, 256], BF16) for m in range(4)]
    sx = [P.dram(f"sx{i}", [256, 1024], F32) for i in range(2)]
    sxg = [P.dram(f"sxg{i}", [4 * 256, 1024], F32) for i in range(2)]
    fx = P.dram("fx", [128, 224], F32)
    fxg = P.dram("fxg", [4 * 128, 224], F32)
    tx = P.dram("tx", [128, 128], F32)
    txg = P.dram("txg", [4 * 128, 128], F32)
    wub = P.dram("wub", [D, 5632], BF16)
    wdb = P.dram("wdb", [2816, D], BF16)
    wab = P.dram("wab", [512, D], BF16)
    wbb = P.dram("wbb", [512, D], BF16)
    wcb = P.dram("wcb", [1024, D], BF16)
    wob = P.dram("wob", [1024, D], BF16)
    dbg_out = {}

    def gather_pairs(pairs):
        for (a, b) in pairs:
            P.pool.collective_compute(kind="AllGather", op=ALU.bypass, replica_groups=RG,
                                      ins=[a.full().re("(p a) c -> p (a c)", p=128)],
                                      outs=[b.full().re("(q a) c -> q (a c)", q=512)])

    def load_consts():
        d = {}
        d["cm"] = P.sb("cmb", [128, 4, 128], F32)
        P.dma("sync", out=d["cm"].full(), in_=cm_d.full())
        d["sel"] = P.sb("selb", [128, 32], F32)
        P.dma("sync", out=d["sel"].full(), in_=sel_d.full())
        d["eps"] = P.sb("eps", [128, 1], F32)
        P.dve.memset(ap=d["eps"].full(), constant=1e-6)
        d["one1"] = P.sb("one1", [128, 1], F32)
        P.dve.memset(ap=d["one1"].full(), constant=1.0)
        d["zero"] = P.sb("zero", [128, 1], F32)
        P.dve.memset(ap=d["zero"].full(), constant=0.0)
        return d

    def phase_A(l):
        K = load_consts()
        cmb = K["cm"]
        tri, ident, ones = cmb[:, 0, :], cmb[:, 2, :], cmb[:, 3, :]
        eps, one1 = K["eps"], K["one1"]
        xin = xb[l]
        cstb = P.sb("cstb", [128, offA["_n"]], F32)
        C = Cst(P, cstb, offA)
        P.dma("sync", out=cstb.full(), in_=cstA_d[l])
        rowc = P.sb("rowc", [128, 56], F32)
        P.dma("sync", out=rowc.full(), in_=rowc_d[l])
        matf = P.sb("matf", [128, 192], F32)
        matb = P.sb("matb", [128, 192], BF16)
        P.dma("sync", out=matf.full(), in_=mats_d.full())
        P.dve.tensor_copy(out=matb.full(), in_=matf.full())
        prh = matb[0:96, 0:96]
        selm = matb[0:32, 96:192]
        identb = P.sb("identb", [128, 128], BF16)
        P.dve.tensor_copy(out=identb.full(), in_=ident)
        Aneg_r = P.sb("Aneg_r", [128, 16], F32)
        P.act.activation(out=Aneg_r.full(), in_=rowc[:, 24:40], func=AF.Exp)
        P.dve.tensor_scalar(out=Aneg_r.full(), in0=Aneg_r.full(), scalar1=-1.0, scalar2=None, op0=ALU.mult)

        pb = [P.ps(f"pb{i}", [128, 512], F32) for i in range(7)]
        pbt = P.ps("pbt", [128, 1024], BF16)
        pbi = {}

        def nxt_ps(lo=0, hi=4):
            i = pbi.get(lo, 0)
            pbi[lo] = (i + 1) % (hi - lo)
            return pb[lo + i]

        Ctab = P.sb("Ctab", [96, T], F32)
        Stab = P.sb("Stab", [96, T], F32)
        hraw = P.sb("hraw", [96, 512], F32)
        hsq = P.sb("hsq", [96, 512], F32)
        hrs = P.sb("hrs", [96, 512], F32)
        hnf = P.sb("hnf", [96, 512], F32)
        hnb = P.sb("hnb", [96, 512], BF16)
        ht1 = P.sb("ht1", [96, 512], F32)
        ht2 = P.sb("ht2", [96, 512], F32)
        posf, rr_tmp, rr_m = hrs, hraw, hsq

        class _IV:
            def __init__(self, b):
                self.b = b

            def full(self):
                return self.b.full().bitcast(I32)
        posi, rr_i = _IV(ht1), _IV(ht2)

        def sin_table(outv, phase):
            P.dve.tensor_scalar(out=rr_tmp.full(), in0=posf.full(), scalar1=C.col("invf"), scalar2=phase,
                                op0=ALU.mult, op1=ALU.add)
            P.dve.tensor_scalar(out=rr_m.full(), in0=rr_tmp.full(), scalar1=1.0 / (2 * np.pi), scalar2=None, op0=ALU.mult)
            P.dve.tensor_copy(out=rr_i.full(), in_=rr_m.full())
            P.dve.tensor_copy(out=rr_m.full(), in_=rr_i.full())
            P.dve.scalar_tensor_tensor(out=rr_tmp.full(), in0=rr_m.full(), scalar=-2 * np.pi, in1=rr_tmp.full(),
                                       op0=ALU.mult, op1=ALU.add)
            P.dve.tensor_scalar(out=rr_m.full(), in0=rr_tmp.full(), scalar1=np.pi, scalar2=-2 * np.pi, op0=ALU.is_gt, op1=ALU.mult)
            P.dve.tensor_tensor(out=rr_tmp.full(), in0=rr_tmp.full(), in1=rr_m.full(), op=ALU.add)
            P.dve.tensor_scalar(out=rr_m.full(), in0=rr_tmp.full(), scalar1=-np.pi, scalar2=2 * np.pi, op0=ALU.is_lt, op1=ALU.mult)
            P.dve.tensor_tensor(out=rr_tmp.full(), in0=rr_tmp.full(), in1=rr_m.full(), op=ALU.add)
            P.act.activation(out=outv, in_=rr_tmp.full(), func=AF.Sin)

        for i in range(4):
            P.dma("sync", out=posi.full(), in_=pos[:, i * 512:(i + 1) * 512].f(lambda a: a.partition_broadcast(96)))
            P.dve.tensor_copy(out=posf.full(), in_=posi.full())
            sin_table(Stab[:, i * 512:(i + 1) * 512], 0.0)
            sin_table(Ctab[:, i * 512:(i + 1) * 512], np.pi / 2)
        P.dve.memset(ap=Stab[0:64, :], constant=0.0)
        P.dve.memset(ap=Ctab[0:64, :], constant=1.0)

        hn = P.sb("hn", [128, 8, 4 * SW], BF16)
        xst = P.sb("xst", [128, 8, 512], F32)
        sq = P.sb("sq", [128, 512], F32)
        rstd = P.sb("rstd", [128, 512], F32)
        xTv = xin.full().re("(kc p) n -> p kc n", p=128)

        def rstd_from(ps_view, n_feat, rows, rstd_view):
            P.act.activation(out=rstd_view, in_=ps_view, func=AF.Ln, bias=eps[0:rows, 0:1], scale=1.0 / n_feat)
            P.act.activation(out=rstd_view, in_=rstd_view, func=AF.Exp, scale=-0.5)

        halos = [(m * SW, 4) for m in range(4)]
        main = [(m * SW + 4, 512) for m in range(4)]
        for (c0, w) in halos + main:
            P.dma("sync", out=xst[:, :, 0:w], in_=xTv[:, :, c0:c0 + w])
            ps = nxt_ps(4, 6)
            for kc in range(8):
                P.act.activation(out=sq[:, 0:w], in_=xst[:, kc, 0:w], func=AF.Square)
                P.pe.matmul(out=ps[:, 0:w], lhsT=ones, rhs=sq[:, 0:w], start=(kc == 0), stop=(kc == 7))
            rstd_from(ps[:, 0:w], 1024.0, 128, rstd[:, 0:w])
            for kc in range(8):
                P.dve.scalar_tensor_tensor(out=hn[:, kc, c0:c0 + w], in0=xst[:, kc, 0:w], scalar=C.col("g_mix", kc),
                                           in1=rstd[:, 0:w], op0=ALU.mult, op1=ALU.mult)

        wst = [P.sb(f"wst{i}", [128, 8, 256], F32) for i in range(2)]
        wbf = [P.sb(f"wbf{i}", [128, 8, 512], BF16) for i in range(2)]
        wcnt = [0]
        scnt = [0]
        w_inv = w_in[l].re("(kc p) n -> p kc n", p=128)

        SBv = 672 + 1544
        wplan = [(0, 384), (384, 288), (672, 512), (672 + 512, 512), (672 + 1024, 512), (672 + 1536, 8),
                 (SBv + 1024 + 1536, 16), (SBv, 512), (SBv + 512, 512)]
        wplan += [(SBv + 1024 + b_ * 512, 512) for b_ in range(3)]
        wplan += [(SBv + 2576 + b_ * 512, 512) for b_ in range(6)]
        wpend = {}

        def w_issue(g):
            c0, ncols = wplan[g]
            lst = []
            for h0 in range(0, ncols, 256):
                n = min(256, ncols - h0)
                si = scnt[0] % 2
                scnt[0] += 1
                P.dma("sync", out=wst[si][:, :, 0:n], in_=w_inv[:, :, c0 + h0:c0 + h0 + n])
                lst.append((si, h0, n))
            wpend[g] = lst

        def load_w(c0, ncols):
            g = wcnt[0]
            wcnt[0] += 1
            assert wplan[g] == (c0, ncols), (g, wplan[g], c0, ncols)
            i = g % 2
            if g not in wpend:
                w_issue(g)
            lst = wpend.pop(g)
            for (si, h0, n) in lst:
                P.act.copy(out=wbf[i][:, :, h0:h0 + n], in_=wst[si][:, :, 0:n])
            if g + 1 < len(wplan):
                w_issue(g + 1)
            return wbf[i]

        def proj(wb, wc0, mcols, c0, w, ps_view):
            for kc in range(8):
                P.pe.matmul(out=ps_view, lhsT=wb[:, kc, wc0:wc0 + mcols], rhs=hn[:, kc, c0:c0 + w],
                            start=(kc == 0), stop=(kc == 7))

        def proj_tm(wb, wc0, ncols, tok0, ps_view):
            for kc in range(8):
                P.pe.matmul(out=ps_view, lhsT=hn[:, kc, tok0:tok0 + 128], rhs=wb[:, kc, wc0:wc0 + ncols],
                            start=(kc == 0), stop=(kc == 7))

        ostg_cnt = [0]
        ostg = [P.sb(f"ostg{i}", [128, 512], BF16) for i in range(4)]

        def next_ostg():
            i = ostg_cnt[0] % 4
            ostg_cnt[0] += 1
            return ostg[i]

        def out_dma(dst_view, src_view):
            P.dma("sync" if ostg_cnt[0] % 2 else "scalar", out=dst_view, in_=src_view)

        hsets = [dict(hraw=hraw.full(), hsq=hsq.full(), hrs=hrs.full(), hnf=hnf.full(), hnb=hnb.full(),
                      ht1=ht1.full(), ht2=ht2.full())]
        hnb1 = P.sb("hnb1", [96, 512], BF16)
        hsets.append(dict(hraw=xst[0:96, 0, :].k(0), hsq=xst[0:96, 1, :].k(1), hrs=xst[0:96, 2, :].k(2),
                          hnf=xst[0:96, 3, :].k(3), hnb=hnb1.full(), ht1=xst[0:96, 4, :].k(4), ht2=xst[0:96, 5, :].k(5)))
        hb2 = P.sb("hb2", [96, 6, 512], F32)
        hnb2 = P.sb("hnb2", [96, 512], BF16)
        hsets.append(dict(hraw=hb2[:, 0, :].k(0), hsq=hb2[:, 1, :].k(1), hrs=hb2[:, 2, :].k(2),
                          hnf=hb2[:, 3, :].k(3), hnb=hnb2.full(), ht1=hb2[:, 4, :].k(4), ht2=hb2[:, 5, :].k(5)))
        hcnt = [0]

        def headnorm(projfn, d, gain_col, rope, tok0, dst_view):
            H = hsets[hcnt[0] % 3]
            hcnt[0] += 1
            ps_view = projfn()
            P.act.activation(out=H["hsq"][0:d, :], in_=ps_view, func=AF.Square)
            P.act.copy(out=H["hraw"][0:d, :], in_=ps_view)
            yield
            ps2 = nxt_ps(4, 6)
            P.pe.matmul(out=ps2[0:d, :], lhsT=cmb[0:d, 3, 0:d], rhs=H["hsq"][0:d, :], start=True, stop=True)
            rstd_from(ps2[0:d, :], float(d), d, H["hrs"][0:d, :])
            og = next_ostg()
            if not rope:
                P.dve.scalar_tensor_tensor(out=og[0:d, :], in0=H["hraw"][0:d, :], scalar=gain_col, in1=H["hrs"][0:d, :],
                                           op0=ALU.mult, op1=ALU.mult)
            else:
                P.dve.scalar_tensor_tensor(out=H["hnf"][0:d, :], in0=H["hraw"][0:d, :], scalar=gain_col, in1=H["hrs"][0:d, :],
                                           op0=ALU.mult, op1=ALU.mult)
                P.act.copy(out=H["hnb"][0:d, :], in_=H["hnf"][0:d, :])
                yield
                ps3 = nxt_ps(6, 7)
                P.pe.matmul(out=ps3[0:d, :], lhsT=prh, rhs=H["hnb"][0:d, :], start=True, stop=True)
                P.dve.tensor_tensor(out=H["ht1"][0:d, :], in0=H["hnf"][0:d, :], in1=Ctab[0:d, tok0:tok0 + 512], op=ALU.mult)
                P.dve.tensor_tensor(out=H["ht2"][0:d, :], in0=ps3[0:d, :], in1=Stab[0:d, tok0:tok0 + 512], op=ALU.mult)
                P.pool.tensor_tensor(out=og[0:d, :], in0=H["ht1"][0:d, :], in1=H["ht2"][0:d, :], op=ALU.add)
            out_dma(dst_view, og[0:d, :])

        def run_pipe(gens, depth=3):
            gens = iter(gens)
            active = []
            while True:
                started = False
                if len(active) < depth:
                    g = next(gens, None)
                    if g is not None:
                        started = True
                        try:
                            next(g)
                            active.append(g)
                        except StopIteration:
                            pass
                if not active and not started:
                    break
                olds = active[:-1] if (started and active) else list(active)
                for g in olds:
                    try:
                        next(g)
                    except StopIteration:
                        active.remove(g)

        lat = P.sb("lat", [128, 3, 512], F32)
        latn = P.sb("latn", [128, 3, 512], BF16)

        def latent_norm(ps_list, gname):
            nch = len(ps_list)
            ps2 = nxt_ps(4, 6)
            for i, psv in enumerate(ps_list):
                P.act.activation(out=sq.full(), in_=psv, func=AF.Square)
                P.act.copy(out=lat[:, i, :], in_=psv)
                P.pe.matmul(out=ps2.full(), lhsT=ones, rhs=sq.full(), start=(i == 0), stop=(i == nch - 1))
            rstd_from(ps2.full(), 128.0 * nch, 128, rstd.full())
            for i in range(nch):
                P.dve.scalar_tensor_tensor(out=latn[:, i, :], in0=lat[:, i, :], scalar=C.col(gname, i), in1=rstd.full(),
                                           op0=ALU.mult, op1=ALU.mult)

        def small_w(name, dram_l, kc_n, ncols, i):
            bfb = P.sb(name, [128, kc_n, ncols], BF16)
            dv = dram_l.re("(kc p) n -> p kc n", p=128)
            for kc in range(kc_n):
                si = scnt[0] % 2
                scnt[0] += 1
                stg = wst[si].full().re("p a b -> p (a b)")[:, 0:ncols]
                P.dma("sync", out=stg, in_=dv[:, kc, :])
                P.act.copy(out=bfb[:, kc, :], in_=stg)
            return bfb

        uqb = small_w("uqb", w_uq[l], 3, 768, 0)
        kpb = small_w("kpb", w_kp[l], 2, 768, 1)
        wvb = small_w("wvb", w_v[l], 2, 512, 0)

        vstg = [P.sb(f"vstg{i}", [128, 512], BF16) for i in range(2)]
        vcnt = [0]

        def v_out(kind, ktl, ps_view):
            vs = vstg[vcnt[0] % 2]
            vcnt[0] += 1
            P.act.copy(out=vs.full(), in_=ps_view)
            P.dma("sync" if vcnt[0] % 2 else "scalar",
                  out=vx[ktl // 4][kind * 1024:(kind + 1) * 1024, (ktl % 4) * 64:(ktl % 4 + 1) * 64].re("(h p) d -> p h d", p=128),
                  in_=vs.full().re("p (h d) -> p h d", h=8))

        wb = load_w(0, 384)
        for m, (c0, w) in enumerate(main):
            pss = []
            for ch in range(3):
                ps = nxt_ps(0, 4)
                proj(wb, ch * 128, 128, c0, 512, ps.full())
                pss.append(ps.full())
            latent_norm(pss, "g_cq")
            def mkq(h):
                def f():
                    ps = nxt_ps(0, 4)
                    for kc in range(3):
                        P.pe.matmul(out=ps[0:96, :], lhsT=uqb[:, kc, h * 96:(h + 1) * 96], rhs=latn[:, kc, :],
                                    start=(kc == 0), stop=(kc == 2))
                    return ps[0:96, :]
                return f
            run_pipe(headnorm(mkq(h), 96, C.col("g_q"), True, m * 512, qm[h, :, m * 512:(m + 1) * 512]) for h in range(8))
        wb = load_w(384, 288)
        krb = P.sb("krb", [32, 512], BF16)
        for m, (c0, w) in enumerate(main):
            pss = []
            for ch in range(2):
                ps = nxt_ps(0, 4)
                proj(wb, ch * 128, 128, c0, 512, ps.full())
                pss.append(ps.full())
            ps = nxt_ps(0, 4)
            proj(wb, 256, 32, c0, 512, ps[0:32, :])
            P.act.copy(out=krb.full(), in_=ps[0:32, :])
            latent_norm(pss, "g_ckv")
            def mkk(h):
                def f():
                    ps = nxt_ps(0, 4)
                    for kc in range(2):
                        P.pe.matmul(out=ps[0:96, :], lhsT=kpb[:, kc, h * 96:(h + 1) * 96], rhs=latn[:, kc, :],
                                    start=(kc == 0), stop=False)
                    P.pe.matmul(out=ps[0:96, :], lhsT=selm, rhs=krb.full(), start=False, stop=True)
                    return ps[0:96, :]
                return f
            run_pipe(headnorm(mkk(h), 96, C.col("g_k"), True, m * 512, kxm[m][h * 96:(h + 1) * 96, :]) for h in range(8))
            for j in range(4):
                ps = nxt_ps(0, 4)
                for kc in range(2):
                    P.pe.matmul(out=ps.full(), lhsT=latn[:, kc, j * 128:(j + 1) * 128], rhs=wvb[:, kc, :],
                                start=(kc == 0), stop=(kc == 1))
                v_out(0, m * 4 + j, ps.full())
        for (base, gname, isq) in ((672, "g_fq", True), (672 + 512, "g_fk", False)):
            wb = load_w(base, 512)
            def mkf(wb_, h, c0):
                def f():
                    ps = nxt_ps(0, 4)
                    proj(wb_, h * 64, 64, c0, 512, ps[0:64, :])
                    return ps[0:64, :]
                return f
            gl = []
            for m, (c0, w) in enumerate(main):
                for h in range(8):
                    dst = qf[h, :, m * 512:(m + 1) * 512] if isq else kxf[m][h * 64:(h + 1) * 64, :]
                    gl.append(headnorm(mkf(wb, h, c0), 64, C.col(gname), False, m * 512, dst))
            run_pipe(gl)
        wb = load_w(672 + 1024, 512)
        for m, (c0, w) in enumerate(main):
            for j in range(4):
                ps = nxt_ps(0, 4)
                proj_tm(wb, 0, 512, c0 + j * 128, ps.full())
                v_out(1, m * 4 + j, ps.full())
        gather_pairs(list(zip(kxm, kxmg)) + list(zip(vx, vxg)) + list(zip(kxf, kxfg)))
        FB = 672 + 1536
        SB = 672 + 1544
        lf_tm = P.sb("lf_tm", [128, KT_L, 8], F32)
        dt_tm = P.sb("dt_tm", [128, KT_L, 16], F32)
        a_tm = P.sb("a_tm", [128, KT_L, 16], F32)
        tmpr = P.sb("tmpr", [128, 16], F32)
        wf = load_w(FB, 8)
        for m, (c0, w) in enumerate(main):
            for j in range(4):
                kt = m * 4 + j
                ps = nxt_ps(0, 4)
                proj_tm(wf, 0, 8, c0 + j * 128, ps[:, 0:8])
                P.dve.tensor_tensor(out=tmpr[:, 0:8], in0=ps[:, 0:8], in1=rowc[:, 0:8], op=ALU.add)
                P.act.activation(out=tmpr[:, 0:8], in_=tmpr[:, 0:8], func=AF.Exp, scale=-1.0)
                P.act.activation(out=tmpr[:, 0:8], in_=tmpr[:, 0:8], func=AF.Ln, bias=one1[:, 0:1], scale=1.0)
                P.dve.tensor_scalar(out=lf_tm[:, kt, :], in0=tmpr[:, 0:8], scalar1=-1.0, scalar2=None, op0=ALU.mult)
        wd = load_w(SB + 1024 + 1536, 16)
        for m, (c0, w) in enumerate(main):
            for j in range(4):
                kt = m * 4 + j
                ps = nxt_ps(0, 4)
                proj_tm(wd, 0, 16, c0 + j * 128, ps[:, 0:16])
                P.dve.tensor_tensor(out=tmpr.full(), in0=ps[:, 0:16], in1=rowc[:, 8:24], op=ALU.add)
                P.act.activation(out=tmpr.full(), in_=tmpr.full(), func=AF.Exp)
                P.act.activation(out=dt_tm[:, kt, :], in_=tmpr.full(), func=AF.Ln, bias=one1[:, 0:1], scale=1.0)
                P.dve.tensor_tensor(out=a_tm[:, kt, :], in0=dt_tm[:, kt, :], in1=Aneg_r.full(), op=ALU.mult)
        P.dma("sync", out=dtd.full(), in_=dt_tm.full())
        P.dma("sync", out=atd.full(), in_=a_tm.full())
        def plain_group(base, ncols, func, bias_name, dst, dst_row0):
            wb_ = load_w(base, ncols)
            for m, (c0, w) in enumerate(main):
                for ch in range(ncols // 128):
                    ps = nxt_ps(0, 4)
                    proj(wb_, ch * 128, 128, c0, 512, ps.full())
                    og = next_ostg()
                    if bias_name is None:
                        P.act.activation(out=og.full(), in_=ps.full(), func=func)
                    else:
                        P.act.activation(out=og.full(), in_=ps.full(), func=func,
                                         bias=C.col(bias_name, (dst_row0 // 128) + ch))
                    out_dma(dst[dst_row0 + ch * 128:dst_row0 + (ch + 1) * 128, m * 512:(m + 1) * 512], og.full())

        for blk in range(2):
            plain_group(SB + blk * 512, 512, AF.Silu, None, szd, blk * 512)
        upre = P.sb("upre", [128, 516], F32)
        carry = P.sb("carry", [128, 4], F32)
        acc0 = P.sb("acc0", [128, 512], F32)
        tstg = [P.sb(f"tstg{i}", [128, 4, 128], BF16) for i in range(2)]
        tcnt = [0]
        for blk in range(3):
            wb = load_w(SB + 1024 + blk * 512, 512)
            for m, (c0, w) in enumerate(main):
                for ch in range(4):
                    cg = blk * 4 + ch
                    ps = nxt_ps(0, 4)
                    proj(wb, ch * 128, 128, c0 - 4, 4, ps[:, 0:4])
                    P.act.copy(out=upre[:, 0:4], in_=ps[:, 0:4])
                    ps = nxt_ps(0, 4)
                    proj(wb, ch * 128, 128, c0, 512, ps.full())
                    P.act.copy(out=upre[:, 4:516], in_=ps.full())
                    P.act.activation(out=acc0.full(), in_=ps.full(), func=AF.Identity, scale=C.col("cw3", cg), bias=C.col("cb", cg))
                    for k in range(3):
                        P.dve.scalar_tensor_tensor(out=acc0.full(), in0=upre[:, 1 + k:513 + k], scalar=C.col(f"cw{k}", cg),
                                                   in1=acc0.full(), op0=ALU.mult, op1=ALU.add)
                    og = next_ostg()
                    P.act.activation(out=og.full(), in_=acc0.full(), func=AF.Silu)
                    if cg >= 8:
                        out_dma(bct[(cg - 8) * 128:(cg - 7) * 128, m * 512:(m + 1) * 512], og.full())
                    if cg < 10:
                        i2 = tcnt[0] % 2
                        tcnt[0] += 1
                        for j in range(4):
                            P.pe.transpose(out=pbt[:, i2 * 512 + j * 128:i2 * 512 + (j + 1) * 128],
                                           in_=og[:, j * 128:(j + 1) * 128], identity=identb.full())
                        ts_ = tstg[i2]
                        P.dve.tensor_copy(out=ts_.full().re("p j f -> p (j f)"), in_=pbt[:, i2 * 512:(i2 + 1) * 512])
                        if cg < 8:
                            P.dma("sync", out=xtm[:, m * 4:(m + 1) * 4, cg * 128:(cg + 1) * 128], in_=ts_.full())
                        else:
                            P.dma("sync", out=btm[:, m * 4:(m + 1) * 4, (cg - 8) * 128:(cg - 7) * 128], in_=ts_.full())
        GB = SB + 2576
        for blk in range(6):
            plain_group(GB + blk * 512, 512, AF.Sigmoid, "b_gate", gd, blk * 512)
        fxs = P.sb("fxs", [128, 224], F32)
        within = P.sb("within", [128, KT_L, 8], F32)
        ttot = P.sb("ttot", [128, KT_L, 8], F32)
        f2 = lambda b: b.full().re("p a b -> p (a b)")
        ps = nxt_ps(0, 4)
        P.pe.matmul(out=ps[:, 0:128], lhsT=tri, rhs=f2(lf_tm), start=True, stop=True)
        P.act.copy(out=f2(within), in_=ps[:, 0:128])
        ps = nxt_ps(0, 4)
        P.pe.matmul(out=ps[:, 0:128], lhsT=ones, rhs=f2(lf_tm), start=True, stop=True)
        P.act.copy(out=f2(ttot), in_=ps[:, 0:128])
        Floc = fxs[:, 0:128].re("p (a b) -> p a b", b=8)
        totv = fxs[:, 128:160].re("p (a b) -> p a b", b=8)
        cacc = P.sb("cacc", [128, 8], F32)
        for m in range(4):
            P.dve.tensor_copy(out=Floc[:, 4 * m, :], in_=within[:, 4 * m, :])
            P.dve.tensor_copy(out=cacc.full(), in_=ttot[:, 4 * m, :])
            for j in range(1, 4):
                P.dve.tensor_tensor(out=Floc[:, 4 * m + j, :], in0=within[:, 4 * m + j, :], in1=cacc.full(), op=ALU.add)
                P.dve.tensor_tensor(out=cacc.full(), in0=cacc.full(), in1=ttot[:, 4 * m + j, :], op=ALU.add)
            P.dve.tensor_copy(out=totv[:, m, :], in_=cacc.full())
        ps = nxt_ps(0, 4)
        P.pe.transpose(out=ps[:, 0:128], in_=fxs[:, 0:128], identity=ident)
        FT = P.sb("FT", [128, 128], F32)
        r1 = P.sb("r1", [128, 128], F32)
        fh = [P.sb(f"fh{i}", [128, 128], BF16) for i in range(3)]
        P.act.copy(out=FT.full(), in_=ps[:, 0:128])
        P.dve.tensor_copy(out=fh[0].full(), in_=FT.full())
        P.dve.tensor_tensor(out=r1.full(), in0=FT.full(), in1=fh[0].full(), op=ALU.subtract)
        P.dve.tensor_copy(out=fh[1].full(), in_=r1.full())
        P.dve.tensor_tensor(out=r1.full(), in0=r1.full(), in1=fh[1].full(), op=ALU.subtract)
        P.dve.tensor_copy(out=fh[2].full(), in_=r1.full())
        for r in range(3):
            for kt in range(KT_L):
                P.dma("sync" if kt % 2 else "scalar", out=fq[:, r, kt * 128:(kt + 1) * 128], in_=fh[r][kt * 8:(kt + 1) * 8, :])
        P.dma("sync", out=fx[:, 0:160], in_=fxs[:, 0:160])

    def ssd_scan(l, K, pass1, fxs=None, dt_tm=None, a_tm=None, Hinit=None, rowc=None, pb=None):
        cmb = K["cm"]
        tri, trimask, ones = cmb[:, 0, :], cmb[:, 1, :], cmb[:, 3, :]
        if pb is None:
            pb = [P.ps(f"spb{i}", [128, 512], F32) for i in range(7)]
        if pass1:
            fxs = P.sb("decs", [128, 224], F32)
        if dt_tm is None:
            dt_tm = P.sb("dt_tm", [128, KT_L, 16], F32)
            a_tm = P.sb("a_tm", [128, KT_L, 16], F32)
            P.dma("sync", out=dt_tm.full(), in_=dtd.full())
            P.dma("sync", out=a_tm.full(), in_=atd.full())
        fl = lambda b: b.full().re("p c h -> p (c h)")
        Acum = P.sb("Acum", [128, KT_L, 16], F32)
        Atot = P.sb("Atot", [128, KT_L, 16], F32)
        wdec = P.sb("wdec", [128, KT_L, 16], F32)
        eAtot = P.sb("eAtot", [128, KT_L, 16], F32)
        psA = pb[0]
        P.pe.matmul(out=psA[:, 0:256], lhsT=tri, rhs=fl(a_tm), start=True, stop=True)
        P.act.copy(out=fl(Acum), in_=psA[:, 0:256])
        P.pe.matmul(out=psA[:, 256:512], lhsT=ones, rhs=fl(a_tm), start=True, stop=True)
        P.act.copy(out=fl(Atot), in_=psA[:, 256:512])
        P.act.activation(out=fl(eAtot), in_=fl(Atot), func=AF.Exp)
        P.dve.tensor_tensor(out=fl(wdec), in0=fl(Atot), in1=fl(Acum), op=ALU.subtract)
        P.act.activation(out=fl(wdec), in_=fl(wdec), func=AF.Exp)
        if not pass1:
            nAcum = P.sb("nAcum", [128, KT_L, 16], F32)
            eA = P.sb("eA", [128, KT_L, 16], F32)
            P.dve.tensor_scalar(out=fl(nAcum), in0=fl(Acum), scalar1=-1.0, scalar2=None, op0=ALU.mult)
            P.act.activation(out=fl(eA), in_=fl(Acum), func=AF.Exp)
            BCs = P.sb("BCs", [128, 4, T], BF16)
            P.dma("gpsimd", out=BCs.full(), in_=bct.full().re("(a p) t -> p a t", p=128))
            cb = P.sb("cb", [128, 2, 128], F32)
            NH = 4
            at = [P.sb(f"at{i}", [128, 128], F32) for i in range(NH)]
            tm = [P.sb(f"tm{i}", [128, 128], F32) for i in range(NH)]
            dec = [P.sb(f"dec{i}", [128, 128], F32) for i in range(NH)]
            MT = [P.sb(f"MT{i}", [128, 128], BF16) for i in range(NH)]
            t1 = P.sb("t1", [128, 1024], F32)
            t3 = P.sb("t3", [128, 1024], F32)
            yo = P.sb("yo", [128, 1024], BF16)
            yT = [P.sb(f"yT{i}", [128, 4, 128], F32) for i in range(2)]
        Hs = P.sb("Hs", [128, 1024], F32)
        Hb = P.sb("Hb", [128, 1024], BF16)
        xc = [P.sb(f"xc{i}", [128, 1024], BF16) for i in range(2)]
        Bc = [P.sb(f"Bc{i}", [128, 256], BF16) for i in range(2)]
        xdt = P.sb("xdt", [128, 1024], BF16)
        xdts = P.sb("xdts", [128, 1024], BF16)
        dsum = P.sb("dsum", [128, 16], F32)
        v3 = lambda v: v.re("p (h d) -> p h d", h=16)
        bc3 = lambda v: v.f(lambda a: a.unsqueeze(2).to_broadcast([128, 16, 64]))
        for m in range(4):
            if pass1:
                P.dve.memset(ap=Hs.full(), constant=0.0)
                P.dve.memset(ap=dsum.full(), constant=0.0)
            else:
                P.dve.tensor_copy(out=Hs.full(), in_=Hinit[:, m, :])
                P.act.copy(out=Hb.full(), in_=Hinit[:, m, :])
            for j in range(4):
                c = m * 4 + j
                x_c = xc[c % 2]
                B_c = Bc[c % 2]
                P.dma("sync", out=x_c.full(), in_=xtm[:, c, :])
                P.dma("gpsimd", out=B_c.full(), in_=btm[:, c, :])
                P.dve.tensor_tensor(out=v3(xdt.full()), in0=v3(x_c.full()), in1=bc3(dt_tm[:, c, :]), op=ALU.mult)
                P.pool.tensor_tensor(out=v3(xdts.full()), in0=v3(xdt.full()), in1=bc3(wdec[:, c, :]), op=ALU.mult)
                if not pass1:
                    cs = slice(c * 128, (c + 1) * 128)
                    ps_cb = pb[1]
                    for g in range(2):
                        P.pe.matmul(out=ps_cb[:, g * 128:(g + 1) * 128], lhsT=BCs[:, g, cs], rhs=BCs[:, 2 + g, cs],
                                    start=True, stop=True)
                    P.act.copy(out=cb.full().re("p a b -> p (a b)"), in_=ps_cb[:, 0:256])
                    ps_off = [pb[2], pb[3]]
                    for g in range(2):
                        P.pe.matmul(out=ps_off[g].full(), lhsT=BCs[:, 2 + g, cs], rhs=Hb[:, g * 512:(g + 1) * 512],
                                    start=True, stop=True)
                    ps_y = [pb[4], pb[5]]
                    def st1(h):
                        i2 = h % NH
                        g = h // 8
                        P.dve.tensor_scalar(out=at[i2].full(), in0=tri, scalar1=a_tm[:, c, h:h + 1], scalar2=None, op0=ALU.mult)
                        ps_A = pb[6]
                        P.pe.matmul(out=ps_A[:, i2 * 128:(i2 + 1) * 128], lhsT=ones, rhs=at[i2].full(), start=True, stop=True)
                        P.dve.tensor_tensor(out=tm[i2].full(), in0=ps_A[:, i2 * 128:(i2 + 1) * 128], in1=trimask, op=ALU.add)
                        P.act.activation(out=dec[i2].full(), in_=tm[i2].full(), func=AF.Exp, bias=nAcum[:, c, h:h + 1], scale=1.0)
                        P.pool.tensor_tensor(out=MT[i2].full(), in0=cb[:, g, :], in1=dec[i2].full(), op=ALU.mult)

                    def st2(h):
                        i2 = h % NH
                        g = h // 8
                        hh = h % 8
                        P.pe.matmul(out=ps_y[g][:, hh * 64:(hh + 1) * 64], lhsT=MT[i2].full(), rhs=xdt[:, h * 64:(h + 1) * 64],
                                    start=True, stop=True)

                    for hq in range(16 + 3):
                        if hq < 16:
                            st1(hq)
                        if hq >= 3:
                            st2(hq - 3)
                    for g in range(2):
                        gs_ = slice(g * 512, (g + 1) * 512)
                        v8 = lambda v: v.re("p (h d) -> p h d", h=8)
                        b8 = lambda v: v.f(lambda a: a.unsqueeze(2).to_broadcast([128, 8, 64]))
                        P.dve.tensor_tensor(out=v8(t1[:, gs_]), in0=v8(ps_off[g].full()), in1=b8(eA[:, c, g * 8:(g + 1) * 8]), op=ALU.mult)
                        P.dve.tensor_tensor(out=t1[:, gs_], in0=t1[:, gs_], in1=ps_y[g].full(), op=ALU.add)
                    P.pool.tensor_tensor(out=v3(t3.full()), in0=v3(x_c.full()), in1=bc3(rowc[:, 40:56]), op=ALU.mult)
                    P.pool.tensor_tensor(out=t3.full(), in0=t1.full(), in1=t3.full(), op=ALU.add)
                    for q4 in range(2):
                        pst = pb[2 + q4]
                        for jj in range(4):
                            fc = q4 * 4 + jj
                            P.pe.transpose(out=pst[:, jj * 128:(jj + 1) * 128], in_=t3[:, fc * 128:(fc + 1) * 128],
                                           identity=cmb[:, 2, :])
                        yt = yT[q4]
                        P.act.copy(out=yt.full().re("p a b -> p (a b)"), in_=pst.full())
                        P.dma("sync", out=yd[q4 * 512:(q4 + 1) * 512, c * 128:(c + 1) * 128].re("(a p) t -> p a t", p=128),
                              in_=yt.full())
                ps_h = [pb[0], pb[1]] if pass1 else [pb[4], pb[5]]
                for g in range(2):
                    P.pe.matmul(out=ps_h[g].full(), lhsT=B_c[:, g * 128:(g + 1) * 128], rhs=xdts[:, g * 512:(g + 1) * 512],
                                start=True, stop=True)
                P.dve.tensor_tensor(out=v3(Hs.full()), in0=v3(Hs.full()), in1=bc3(eAtot[:, c, :]), op=ALU.mult)
                for g in range(2):
                    P.dve.tensor_tensor(out=Hs[:, g * 512:(g + 1) * 512], in0=Hs[:, g * 512:(g + 1) * 512], in1=ps_h[g].full(), op=ALU.add)
                if pass1:
                    P.dve.tensor_tensor(out=dsum.full(), in0=dsum.full(), in1=Atot[:, c, :], op=ALU.add)
                else:
                    P.act.copy(out=Hb.full(), in_=Hs.full())
            if pass1:
                P.dma("sync", out=sx[m // 2][(m % 2) * 128:(m % 2 + 1) * 128, :], in_=Hs.full())
                P.act.activation(out=fxs[:, 160 + m * 16:160 + (m + 1) * 16], in_=dsum.full(), func=AF.Exp)
        if pass1:
            P.dma("sync", out=fx[:, 160:224], in_=fxs[:, 160:224])

    def load_fg():
        fg = P.sb("fg", [128, 4, 224], F32)
        P.dma("sync", out=fg.full(), in_=fxg.full().re("(r p) c -> p r c", p=128))
        return fg

    def phase_attn(l):
        K = load_consts()
        sel, zero = K["sel"], K["zero"]
        mskb = P.sb("mskb", [128, 8, 512], F32)
        P.dma("gpsimd", out=mskb.full(), in_=msk_d.full())
        fg = load_fg()
        offs = P.sb("offs", [128, 16, 8], F32)
        run = P.sb("run", [128, 8], F32)
        P.dve.memset(ap=run.full(), constant=0.0)
        for s_ in range(16):
            m, r = divmod(s_, 4)
            P.dve.tensor_copy(out=offs[:, s_, :], in_=run.full())
            P.dve.tensor_tensor(out=run.full(), in0=run.full(), in1=fg[:, r, 128 + m * 8:128 + (m + 1) * 8], op=ALU.add)
        offown = P.sb("offown", [128, 4, 8], F32)
        P.dve.memset(ap=offown.full(), constant=0.0)
        for m in range(4):
            for r in range(4):
                P.dve.scalar_tensor_tensor(out=offown[:, m, :], in0=offs[:, 4 * m + r, :], scalar=sel[:, 8 + 4 * m + r:9 + 4 * m + r],
                                           in1=offown[:, m, :], op0=ALU.mult, op1=ALU.add)
        negFg = P.sb("negFg", [128, 64, 8], F32)
        for s_ in range(16):
            m, r = divmod(s_, 4)
            src = fg[:, r, 0:128].re("p (a b) -> p a b", b=8)[:, 4 * m:4 * m + 4, :]
            P.dve.tensor_tensor(out=negFg[:, 4 * s_:4 * s_ + 4, :], in0=src,
                                in1=offs[:, s_, :].f(lambda a: a.unsqueeze(1).to_broadcast([128, 4, 8])), op=ALU.add)
        P.dve.tensor_scalar(out=negFg.full(), in0=negFg.full(), scalar1=-1.0, scalar2=None, op0=ALU.mult)
        biasm = P.sb("biasm", [128, 4, 64, 8], F32)
        for m in range(4):
            nk = (4 * m + 4) * 4
            P.dve.tensor_tensor(out=biasm[:, m, 0:nk, :], in0=negFg[:, 0:nk, :],
                                in1=offown[:, m, :].f(lambda a: a.unsqueeze(1).to_broadcast([128, nk, 8])), op=ALU.add)
            for jr in range(4):
                k0 = (4 * m + jr) * 4
                P.dve.tensor_scalar(out=biasm[:, m, k0:k0 + 4, :], in0=biasm[:, m, k0:k0 + 4, :],
                                    scalar1=sel[:, 28 + jr:29 + jr], scalar2=None, op0=ALU.add)

        pb = [P.ps(f"pb{i}", [128, 512], F32) for i in range(8)]
        K_sb = [P.sb(f"K_sb{i}", [96, S_], BF16) for i in range(2)]
        Q_sb = [P.sb(f"Q_sb{i}", [96, T], BF16) for i in range(2)]
        V_sb = [P.sb(f"V_sb{i}", [128, NKT, 128], BF16) for i in range(2)]
        for i in range(2):
            P.dve.memset(ap=V_sb[i][:, :, 64:128], constant=1.0)
        NSB = 5
        LA = 3
        WARM = False
        pt = [P.sb(f"pt{i}", [128, 512], BF16) for i in range(NSB)]
        mt = [P.sb(f"mt{i}", [128, 512], F32) for i in range(3)]
        rl = P.sb("rl", [128, 512], F32)
        rl2 = P.sb("rl2", [64, 512], F32)
        ot = [P.sb(f"ot{i}", [64, 512], BF16) for i in range(2)]
        cnt = [0, 0, 0]
        heads = [(0, h) for h in range(8)] + [(1, h) for h in range(8)]

        def loads(idx):
            kind, h = heads[idx]
            i = idx % 2
            nd = 96 if kind == 0 else 64
            for r in range(4):
                vr = r * 2048 + kind * 1024 + h * 128
                for m in range(4):
                    s0 = (4 * m + r) * 512
                    if kind == 0:
                        ksrc = kxmg[m][r * 768 + h * 96:r * 768 + (h + 1) * 96, :]
                    else:
                        ksrc = kxfg[m][r * 512 + h * 64:r * 512 + (h + 1) * 64, :]
                    P.dma("sync" if (r + m) % 2 == 0 else "gpsimd", out=K_sb[i][0:nd, s0:s0 + 512], in_=ksrc)
                    g0 = (4 * m + r) * 4
                    P.dma("gpsimd" if (r + m) % 2 == 0 else "sync",
                          out=V_sb[i][:, g0:g0 + 4, 0:64],
                          in_=vxg[m][vr:vr + 128, :].re("p (j d) -> p j d", j=4))
            if kind == 0:
                P.dma("sync", out=Q_sb[i][0:96, :], in_=qm[h])
            else:
                P.dve.memset(ap=K_sb[i][64:96, :], constant=0.0)
                P.dve.memset(ap=K_sb[i][64:67, :], constant=8.0)
                P.dma("sync", out=Q_sb[i][0:64, :], in_=qf[h])
                P.pool.memset(ap=Q_sb[i][64:96, :], constant=0.0)
                P.dma("gpsimd", out=Q_sb[i][64:67, :], in_=fq[h])

        iters = []
        for idx in range(16):
            for m in range(4):
                nk = (4 * m + 4) * 4
                for kt in range(nk):
                    iters.append((idx, m, kt, nk))

        def stage_qk(n):
            idx, m, kt, nk = iters[n]
            kind, h = heads[idx]
            i = idx % 2
            dk = 96
            scale = 96.0 ** -0.5 if kind == 0 else 0.125
            i3 = n % NSB
            ps = pb[i3]
            P.pe.matmul(out=ps.full(), lhsT=K_sb[i][0:dk, kt * 128:(kt + 1) * 128],
                        rhs=Q_sb[i][0:dk, m * 512:(m + 1) * 512], start=True, stop=True)
            blk = kt // 4
            if blk >= 4 * m:
                jr = blk - 4 * m
                mm = mt[cnt[1] % 3]
                cnt[1] += 1
                P.dve.scalar_tensor_tensor(out=mm.full(), in0=mskb[:, kind * 4 + kt % 4, :],
                                           scalar=sel[:, 24 + jr:25 + jr], in1=ps.full(), op0=ALU.mult, op1=ALU.add)
                src = mm.full()
                bias = sel[:, 28 + jr:29 + jr] if kind == 0 else biasm[:, m, kt, h:h + 1]
            else:
                src = ps.full()
                bias = zero[:, 0:1] if kind == 0 else biasm[:, m, kt, h:h + 1]
            P.act.activation(out=pt[i3].full(), in_=src, func=AF.Exp, scale=scale, bias=bias)
            if WARM:
                P.pe.matmul(out=pb[6][:, 0:256], lhsT=K_sb[i][0:dk, kt * 128:(kt + 1) * 128],
                            rhs=Q_sb[i][0:dk, m * 512:m * 512 + 256], start=True, stop=True)

        def stage_pv(n):
            idx, m, kt, nk = iters[n]
            kind, h = heads[idx]
            i = idx % 2
            oacc = pb[5 + (idx * 4 + m) % 2]
            P.pe.matmul(out=oacc.full(), lhsT=V_sb[i][:, kt, :], rhs=pt[n % NSB].full(), start=(kt == 0), stop=(kt == nk - 1))
            if kt == nk - 1:
                odst = omd if kind == 0 else ofd
                P.dve.reciprocal(out=rl[64:128, :], in_=oacc[64:128, :])
                P.dve.tensor_copy(out=rl2.full(), in_=rl[64:128, :])
                o = ot[m % 2]
                P.dve.tensor_tensor(out=o.full(), in0=oacc[0:64, :], in1=rl2.full(), op=ALU.mult)
                P.dma("sync", out=odst[h * 64:(h + 1) * 64, m * 512:(m + 1) * 512], in_=o.full())

        pcs = [P.sb(f"pcs{i}", [128, 2048], F32) for i in range(2)]
        pcb = [P.sb(f"pcb{i}", [128, 2048], BF16) for i in range(2)]
        jobs = []
        for (src, dst, rows, cols) in ((w_a[l], wab, 512, D), (w_b[l], wbb, 512, D), (w_c[l], wcb, 1024, D), (w_o[l], wob, 1024, D),
                                       (w_up[l], wub, D, 5632), (w_dn[l], wdb, 2816, D)):
            for r0 in range(0, rows, 128):
                for c0 in range(0, cols, 2048):
                    n_ = min(2048, cols - c0)
                    jobs.append((src[r0:r0 + 128, c0:c0 + n_], dst[r0:r0 + 128, c0:c0 + n_], n_))
        jcnt = [0]

        def precast_one():
            if jcnt[0] >= len(jobs):
                return
            src, dst, n_ = jobs[jcnt[0]]
            i = jcnt[0] % 2
            jcnt[0] += 1
            P.dma("gpsimd", out=pcs[i][:, 0:n_], in_=src)
            P.pool.tensor_copy(out=pcb[i][:, 0:n_], in_=pcs[i][:, 0:n_])
            P.dma("gpsimd", out=dst, in_=pcb[i][:, 0:n_])

        every = max(1, len(iters) // (len(jobs) + 4))
        loads(0)
        loads(1)
        for n in range(len(iters) + LA):
            if n < len(iters):
                stage_qk(n)
            if n >= LA:
                stage_pv(n - LA)
                idx_p, m_p, kt_p, nk_p = iters[n - LA]
                if m_p == 3 and kt_p == nk_p - 1 and idx_p + 2 < 16:
                    loads(idx_p + 2)
            if n % every == every - 1:
                precast_one()
        while jcnt[0] < len(jobs):
            precast_one()

    def phase_ssd2(l):
        K = load_consts()
        sel = K["sel"]
        rowc = P.sb("rowc", [128, 56], F32)
        P.dma("sync", out=rowc.full(), in_=rowc_d[l])
        fg = load_fg()
        Hin = P.sb("Hin", [128, 1024], F32)
        Hsel = P.sb("Hsel", [128, 4, 1024], F32)
        Sst = [P.sb(f"Sst{i}", [128, 1024], F32) for i in range(2)]
        P.dve.memset(ap=Hin.full(), constant=0.0)
        P.dve.memset(ap=Hsel.full(), constant=0.0)
        v3 = lambda v: v.re("p (h d) -> p h d", h=16)
        for s_ in range(16):
            m, r = divmod(s_, 4)
            P.dve.scalar_tensor_tensor(out=Hsel[:, m, :], in0=Hin.full(), scalar=sel[:, 8 + s_:9 + s_], in1=Hsel[:, m, :],
                                       op0=ALU.mult, op1=ALU.add)
            if s_ < 15:
                st_ = Sst[s_ % 2]
                P.dma("sync" if s_ % 2 else "gpsimd", out=st_.full(),
                      in_=sxg[m // 2][r * 256 + (m % 2) * 128:r * 256 + (m % 2 + 1) * 128, :])
                dcs = fg[:, r, 160 + m * 16:160 + (m + 1) * 16]
                P.dve.tensor_tensor(out=v3(Hin.full()), in0=v3(Hin.full()),
                                    in1=dcs.f(lambda a: a.unsqueeze(2).to_broadcast([128, 16, 64])), op=ALU.mult)
                P.pool.tensor_tensor(out=Hin.full(), in0=Hin.full(), in1=st_.full(), op=ALU.add)
        ssd_scan(l, K, pass1=False, Hinit=Hsel, rowc=rowc)

    def write_tails(txs):
        P.dma("sync", out=tx.full(), in_=txs.full().re("p m k c -> p (m k c)"))

    def halo_exchange(dst):
        K = load_consts()
        sel = K["sel"]
        P.pool.collective_compute(kind="AllGather", op=ALU.bypass, replica_groups=RG, ins=[tx.full()], outs=[txg.full()])
        tg = P.sb("tg", [128, 4, 128], F32)
        P.dma("sync", out=tg.full(), in_=txg.full().re("(r p) c -> p r c", p=128))
        hl = P.sb("hl", [128, 4, 32], F32)
        P.dve.memset(ap=hl.full(), constant=0.0)
        for m in range(4):
            for r in range(4):
                P.dve.scalar_tensor_tensor(out=hl[:, m, :], in0=tg[:, r, m * 32:(m + 1) * 32], scalar=sel[:, r:r + 1],
                                           in1=hl[:, m, :], op0=ALU.mult, op1=ALU.add)
            if m >= 1:
                P.dve.scalar_tensor_tensor(out=hl[:, m, :], in0=tg[:, 3, (m - 1) * 32:m * 32], scalar=sel[:, 4:5],
                                           in1=hl[:, m, :], op0=ALU.mult, op1=ALU.add)
        dv = dst.full().re("(kc p) n -> p kc n", p=128)
        for m in range(4):
            P.dma("sync", out=dv[:, :, m * SW:m * SW + 4], in_=hl[:, m, :].re("p (k c) -> p k c", c=4))

    def phase_merge(l):
        K = load_consts()
        ones, eps = K["cm"][:, 3, :], K["eps"]
        gsb = P.sb("gsb", [128, 8], F32)
        P.dma("sync", out=gsb.full(), in_=gssm_d[l])
        pb = [P.ps(f"pb{i}", [128, 512], F32) for i in range(8)]
        def load_wb(dram_bf, kc_n, name, q):
            bfb = P.sb(name, [128, kc_n, D], BF16)
            P.dma(q, out=bfb.full(), in_=dram_bf.full().re("(kc p) n -> p kc n", p=128))
            return bfb

        Wa = load_wb(wab, 4, "Wa", "sync")
        Wb = load_wb(wbb, 4, "Wb", "gpsimd")
        Wc = load_wb(wcb, 8, "Wc", "sync")
        Wo = load_wb(wob, 8, "Wo", "gpsimd")
        ys = P.sb("ys", [128, 8, 512], F32)
        szs = P.sb("szs", [128, 8, 512], BF16)
        yn = P.sb("yn", [128, 8, 512], BF16)
        oms = P.sb("oms", [128, 4, 512], BF16)
        ofs = P.sb("ofs", [128, 4, 512], BF16)
        gs = P.sb("gs", [128, 24, 512], BF16)
        xs = P.sb("xs", [128, 8, 512], F32)
        sq = P.sb("sq", [128, 512], F32)
        rstd = P.sb("rstd", [128, 512], F32)
        m1 = [P.sb(f"m1_{i}", [128, 512], F32) for i in range(2)]
        m2 = [P.sb(f"m2_{i}", [128, 512], F32) for i in range(2)]
        m3 = [P.sb(f"m3_{i}", [128, 512], F32) for i in range(2)]
        mg = P.sb("mg", [128, 8, 512], BF16)
        xo = [P.sb(f"xo{i}", [128, 512], F32) for i in range(2)]
        txs = P.sb("txs", [128, 4, 8, 4], F32)
        ch = lambda d: d.full().re("(kc p) n -> p kc n", p=128)
        xmv = ch(xmid)
        for ti in range(4):
            ts = slice(ti * 512, (ti + 1) * 512)
            xsl = slice(ti * SW + 4, ti * SW + 516)
            P.dma("sync", out=ys.full(), in_=ch(yd)[:, :, ts])
            P.dma("gpsimd", out=szs.full(), in_=ch(szd)[:, :, ts])
            P.dma("sync", out=oms.full(), in_=ch(omd)[:, :, ts])
            P.dma("gpsimd", out=ofs.full(), in_=ch(ofd)[:, :, ts])
            P.dma("sync", out=gs.full(), in_=ch(gd)[:, :, ts])
            P.dma("gpsimd", out=xs.full(), in_=ch(xb[l])[:, :, xsl])
            ps = pb[7]
            for kc in range(8):
                P.dve.tensor_tensor(out=ys[:, kc, :], in0=ys[:, kc, :], in1=szs[:, kc, :], op=ALU.mult)
                P.act.activation(out=sq.full(), in_=ys[:, kc, :], func=AF.Square)
                P.pe.matmul(out=ps.full(), lhsT=ones, rhs=sq.full(), start=(kc == 0), stop=(kc == 7))
            P.act.activation(out=rstd.full(), in_=ps.full(), func=AF.Ln, bias=eps[:, 0:1], scale=1.0 / 1024.0)
            P.act.activation(out=rstd.full(), in_=rstd.full(), func=AF.Exp, scale=-0.5)
            for kc in range(8):
                P.dve.scalar_tensor_tensor(out=yn[:, kc, :], in0=ys[:, kc, :], scalar=gsb[:, kc:kc + 1], in1=rstd.full(),
                                           op0=ALU.mult, op1=ALU.mult)
            for oc in range(8):
                i2 = oc % 2
                osl = slice(oc * 128, (oc + 1) * 128)
                pa, pbb, pc = pb[0 + i2 * 3], pb[1 + i2 * 3], pb[2 + i2 * 3]
                for kc in range(4):
                    P.pe.matmul(out=pa.full(), lhsT=Wa[:, kc, osl], rhs=oms[:, kc, :], start=(kc == 0), stop=(kc == 3))
                for kc in range(4):
                    P.pe.matmul(out=pbb.full(), lhsT=Wb[:, kc, osl], rhs=ofs[:, kc, :], start=(kc == 0), stop=(kc == 3))
                for kc in range(8):
                    P.pe.matmul(out=pc.full(), lhsT=Wc[:, kc, osl], rhs=yn[:, kc, :], start=(kc == 0), stop=(kc == 7))
                P.dve.tensor_tensor(out=m1[i2].full(), in0=pa.full(), in1=gs[:, oc, :], op=ALU.mult)
                P.dve.tensor_tensor(out=m2[i2].full(), in0=pbb.full(), in1=gs[:, 8 + oc, :], op=ALU.mult)
                P.dve.tensor_tensor(out=m3[i2].full(), in0=pc.full(), in1=gs[:, 16 + oc, :], op=ALU.mult)
                P.pool.tensor_tensor(out=m1[i2].full(), in0=m1[i2].full(), in1=m2[i2].full(), op=ALU.add)
                P.pool.tensor_tensor(out=mg[:, oc, :], in0=m1[i2].full(), in1=m3[i2].full(), op=ALU.add)
            for oc in range(8):
                i2 = oc % 2
                ps = pb[6 + i2]
                for kc in range(8):
                    P.pe.matmul(out=ps.full(), lhsT=Wo[:, kc, oc * 128:(oc + 1) * 128], rhs=mg[:, kc, :],
                                start=(kc == 0), stop=(kc == 7))
                P.dve.tensor_tensor(out=xo[i2].full(), in0=ps.full(), in1=xs[:, oc, :], op=ALU.add)
                P.pool.tensor_copy(out=txs[:, ti, oc, :], in_=xo[i2][:, 508:512])
                P.dma("sync" if i2 else "gpsimd", out=xmv[:, oc, xsl], in_=xo[i2].full())
        write_tails(txs)

    def phase_ffn(l, last):
        K = load_consts()
        ones, eps = K["cm"][:, 3, :], K["eps"]
        cstb = P.sb("cstb", [128, offD2["_n"]], F32)
        C = Cst(P, cstb, offD2)
        P.dma("sync", out=cstb.full(), in_=cstD_d[l])
        pb = [P.ps(f"pb{i}", [128, 512], F32) for i in range(8)]
        Wu = P.sb("Wu", [128, 8, 5632], BF16)
        Wd = P.sb("Wd", [128, 22, D], BF16)
        wubv = wub.full().re("(kc p) n -> p kc n", p=128)
        for (c0, c1) in ((0, 512), (2816, 3328), (512, 2816), (3328, 5632)):
            P.dma("sync" if c0 < 2816 else "gpsimd", out=Wu[:, :, c0:c1], in_=wubv[:, :, c0:c1])
        wdbv = wdb.full().re("(kc p) n -> p kc n", p=128)
        P.dma("sync", out=Wd[:, 0:11, :], in_=wdbv[:, 0:11, :])
        P.dma("gpsimd", out=Wd[:, 11:22, :], in_=wdbv[:, 11:22, :])
        xst = P.sb("xst", [128, 8, 512], F32)
        hn = P.sb("hn", [128, 8, 512], BF16)
        sq = P.sb("sq", [128, 512], F32)
        rstd = P.sb("rstd", [128, 512], F32)
        act = P.sb("act", [128, 22, 512], BF16)
        upre = [P.sb(f"upre{i}", [128, 516], F32) for i in range(2)]
        acc = [P.sb(f"acc{i}", [128, 512], F32) for i in range(2)]
        sg = P.sb("sg", [128, 512], F32)
        carry = P.sb("carry", [128, 44, 4], F32)
        xo = [P.sb(f"xo{i}", [128, 512], F32) for i in range(2)]
        txs = P.sb("txs", [128, 4, 8, 4], F32)
        xTv = xmid.full().re("(kc p) n -> p kc n", p=128)
        dst = out if last else xb[l + 1]
        dv = dst.full().re("(kc p) n -> p kc n", p=128)
        tiles = []
        for m in range(4):
            tiles.append((m * SW, 4, True, m))
            tiles.append((m * SW + 4, 512, False, m))
        pcnt = [0]
        for (c0, w, is_halo, m) in tiles:
            P.dma("sync", out=xst[:, :, 0:w], in_=xTv[:, :, c0:c0 + w])
            ps = pb[7]
            for kc in range(8):
                P.act.activation(out=sq[:, 0:w], in_=xst[:, kc, 0:w], func=AF.Square)
                P.pe.matmul(out=ps[:, 0:w], lhsT=ones, rhs=sq[:, 0:w], start=(kc == 0), stop=(kc == 7))
            P.act.activation(out=rstd[:, 0:w], in_=ps[:, 0:w], func=AF.Ln, bias=eps[:, 0:1], scale=1.0 / 1024.0)
            P.act.activation(out=rstd[:, 0:w], in_=rstd[:, 0:w], func=AF.Exp, scale=-0.5)
            for kc in range(8):
                P.dve.scalar_tensor_tensor(out=hn[:, kc, 0:w], in0=xst[:, kc, 0:w], scalar=C.col("g_ffn", kc),
                                           in1=rstd[:, 0:w], op0=ALU.mult, op1=ALU.mult)
            for i in range(22):
                accs = []
                for j, cg in enumerate((i, 22 + i)):
                    ps = pb[pcnt[0] % 4]
                    pcnt[0] += 1
                    for kc in range(8):
                        P.pe.matmul(out=ps[:, 0:w], lhsT=Wu[:, kc, cg * 128:(cg + 1) * 128], rhs=hn[:, kc, 0:w],
                                    start=(kc == 0), stop=(kc == 7))
                    if is_halo:
                        P.act.copy(out=carry[:, cg, :], in_=ps[:, 0:4])
                        continue
                    up = upre[j]
                    a0 = acc[j]
                    P.act.copy(out=up[:, 4:516], in_=ps.full())
                    P.act.activation(out=a0.full(), in_=ps.full(), func=AF.Identity, scale=C.col("fw2", cg), bias=C.col("fb", cg))
                    P.pool.tensor_copy(out=up[:, 0:4], in_=carry[:, cg, :])
                    P.dve.scalar_tensor_tensor(out=a0.full(), in0=up[:, 3:515], scalar=C.col("fw1", cg), in1=a0.full(),
                                               op0=ALU.mult, op1=ALU.add)
                    P.dve.scalar_tensor_tensor(out=a0.full(), in0=up[:, 2:514], scalar=C.col("fw0", cg), in1=a0.full(),
                                               op0=ALU.mult, op1=ALU.add)
                    accs.append(a0)
                if is_halo:
                    continue
                P.act.activation(out=sg.full(), in_=accs[0].full(), func=AF.Silu)
                P.pool.tensor_tensor(out=act[:, i, :], in0=sg.full(), in1=accs[1].full(), op=ALU.mult)
            if is_halo:
                continue
            for oc in range(8):
                i2 = oc % 2
                ps = pb[4 + i2]
                for i in range(22):
                    P.pe.matmul(out=ps.full(), lhsT=Wd[:, i, oc * 128:(oc + 1) * 128], rhs=act[:, i, :],
                                start=(i == 0), stop=(i == 21))
                P.dve.tensor_tensor(out=xo[i2].full(), in0=ps.full(), in1=xst[:, oc, :], op=ALU.add)
                if last:
                    P.dma("sync" if i2 else "gpsimd", out=dv[:, oc, m * 512:(m + 1) * 512], in_=xo[i2].full())
                else:
                    P.pool.tensor_copy(out=txs[:, m, oc, :], in_=xo[i2][:, 508:512])
                    P.dma("sync" if i2 else "gpsimd", out=dv[:, oc, m * SW + 4:m * SW + 516], in_=xo[i2].full())
        if not last:
            write_tails(txs)

    def gather_e1():
        gather_pairs(list(zip(sx, sxg)) + [(fx, fxg)])

    nl = L if stop is None else stop[0]
    done = False
    for l in range(nl):
        last_l = (stop is not None and l == nl - 1)
        phase_A(l)
        P.emit(final=False)
        ssd_scan(l, load_consts(), pass1=True)
        P.emit(final=False)
        if last_l and stop[1] == "A":
            break
        gather_e1()
        phase_attn(l)
        P.emit(final=False)
        phase_ssd2(l)
        P.emit(final=False)
        if last_l and stop[1] == "B":
            break
        phase_merge(l)
        P.emit(final=False)
        halo_exchange(xmid)
        P.emit(final=False)
        if last_l and stop[1] == "C":
            break
        phase_ffn(l, last=(l == L - 1))
        P.emit(final=False)
        if l < L - 1:
            halo_exchange(xb[l + 1])
            P.emit(final=False)
    loc = {"kxmg0": kxmg[0], "vxg0": vxg[0], "sxg0": sxg[0], "fxg": fxg, "qm": qm, "qf": qf, "fq": fq, "omd": omd, "ofd": ofd, "yd": yd,
           "xmid": xmid, "xb1": xb[1], "szd": szd, "gd": gd, "xtm": xtm, "btm": btm, "bct": bct, "dtd": dtd, "atd": atd}
    for name in dbg:
        src = loc[name]
        shp = list(src.h.shape) if hasattr(src.h, "shape") else None
        dd = P.dram("dbg_" + name, shp, src.h.dtype, EO)
        P.dma("sync", out=dd.full(), in_=src.full())
    P.emit(final=True)
    return nc, P


def _stripe_tokens(p):
    return np.concatenate([np.arange((4 * m + p) * 512, (4 * m + p + 1) * 512) for m in range(4)])


def fused_in_maps(inp):
    L = 2
    cpsA = [a_colpack(inp, l) for l in range(L)]
    cpsD = [d2_colpack(inp, l) for l in range(L)]
    offA = dict(cpsA[0].off)
    offA["_n"] = cpsA[0].n
    offD = dict(cpsD[0].off)
    offD["_n"] = cpsD[0].n
    cstA = np.stack([c.array() for c in cpsA])
    cstD = np.stack([c.array() for c in cpsD])
    w_kp = np.zeros((L, 256, 8, 96), np.float32)
    wukv = inp["mla_w_ukv"].reshape(L, 256, 8, 128)
    w_kp[:, :, :, 0:64] = wukv[:, :, :, 0:64]
    w_v = np.ascontiguousarray(wukv[:, :, :, 64:128].reshape(L, 256, 512))
    gssm = np.ascontiguousarray(inp["ssm_norm_g"].reshape(L, 8, 128).transpose(0, 2, 1))
    rowc = np.stack([fused_rowpack(inp, l) for l in range(L)])
    msk, cm = _bc_consts()
    mats = _const_mats()
    shared = {
        "w_in": np.ascontiguousarray(inp["w_in"]), "w_uq": np.ascontiguousarray(inp["mla_w_uq"]),
        "w_kp": np.ascontiguousarray(w_kp.reshape(L, 256, 768)), "w_v": w_v,
        "w_a": np.ascontiguousarray(inp["w_br_mla"]), "w_b": np.ascontiguousarray(inp["w_br_fox"]),
        "w_c": np.ascontiguousarray(inp["w_br_ssm"]), "w_o": np.ascontiguousarray(inp["w_out"]),
        "w_up": np.ascontiguousarray(inp["ffn_w_up"]), "w_dn": np.ascontiguousarray(inp["ffn_w_down"]),
        "cstA": cstA, "cstD": cstD, "gssm": gssm, "rowc": rowc, "msk": msk, "cm": cm, "mats": mats,
    }
    in_maps = []
    for c in range(8):
        b, p = c // 4, c % 4
        xT = np.zeros((D, 4 * SW), np.float32)
        xbT = inp["x"][b].T
        for m in range(4):
            s_ = 4 * m + p
            xT[:, m * SW + 4:m * SW + 516] = xbT[:, s_ * 512:(s_ + 1) * 512]
            if s_ > 0:
                xT[:, m * SW:m * SW + 4] = xbT[:, s_ * 512 - 4:s_ * 512]
        sel = np.zeros((128, 32), np.float32)
        if p >= 1:
            sel[:, p - 1] = 1.0
        else:
            sel[:, 4] = 1.0
        for s_ in range(16):
            if s_ % 4 == p:
                sel[:, 8 + s_] = 1.0
        for jr in range(4):
            sel[:, 24 + jr] = 1.0 if jr == p else 0.0
            sel[:, 28 + jr] = NEG if jr > p else 0.0
        d = dict(shared)
        d["x0"] = np.ascontiguousarray(xT)
        d["pos"] = np.ascontiguousarray(inp["positions"][b][_stripe_tokens(p)][None, :]).astype(np.int32)
        d["sel"] = sel
        in_maps.append(d)
    return in_maps, offA, offD


def kernel_fused(**inp):
    inp = {k: np.asarray(v) for k, v in inp.items()}
    in_maps, offA, offD = fused_in_maps(inp)
    if "F" not in _PROG_CACHE:
        _PROG_CACHE["F"] = build_fused(offA, offD)[0]
    res = run_bass_kernel_spmd(_PROG_CACHE["F"], in_maps, core_ids=list(range(8))).results
    xo = np.zeros((2, S_, D), np.float32)
    for c in range(8):
        b, p = c // 4, c % 4
        xo[b, _stripe_tokens(p), :] = np.asarray(res[c]["out"]).T
    return xo
```

```python
from contextlib import ExitStack
import numpy as np
import concourse.bass as bass
import concourse.mybir as mybir

F32 = mybir.dt.float32
BF16 = mybir.dt.bfloat16
I32 = mybir.dt.int32
ALU = mybir.AluOpType
AF = mybir.ActivationFunctionType
AX = mybir.AxisListType

COMPUTE = ("tensor", "vector", "scalar", "gpsimd")
QUEUES = ("sync", "gpsimd", "scalar")
NRING = 8


class View:
    __slots__ = ("buf", "ap", "key")

    def __init__(self, buf, ap, key=None):
        self.buf = buf
        self.ap = ap
        self.key = key

    def __getitem__(self, k):
        return View(self.buf, self.ap[k], self.key)

    def re(self, s, **kw):
        return View(self.buf, self.ap.rearrange(s, **kw), self.key)

    def bc(self, shape):
        return View(self.buf, self.ap.to_broadcast(shape), self.key)

    def bitcast(self, dt):
        return View(self.buf, self.ap.bitcast(dt), self.key)

    def k(self, key):
        return View(self.buf, self.ap, key)

    def f(self, fn):
        return View(self.buf, fn(self.ap), self.key)


class Buf:
    def __init__(self, name, handle, is_dram=False):
        self.name = name
        self.h = handle
        self.is_dram = is_dram
        self.regions = {}

    def full(self):
        ap = self.h.ap() if hasattr(self.h, "ap") and callable(getattr(self.h, "ap")) else self.h[:]
        return View(self, ap)

    def __getitem__(self, k):
        return View(self, self.h[k])


class Op:
    __slots__ = ("id", "eng", "meth", "kw", "deps", "is_dma", "signaled", "sem", "val", "prewait", "eidx")


class Eng:
    def __init__(self, P, name):
        self.P = P
        self.name = name

    def __getattr__(self, meth):
        def call(*a, **kw):
            assert not a, "use kwargs"
            return self.P._record(self.name, meth, kw)
        return call


class Prog:
    def __init__(self, nc):
        self.nc = nc
        self.ops = []
        self.gstack = ExitStack()
        self.stack = ExitStack()
        self.pe = Eng(self, "tensor")
        self.dve = Eng(self, "vector")
        self.act = Eng(self, "scalar")
        self.pool = Eng(self, "gpsimd")
        self.sp = Eng(self, "sync")
        st = self.gstack
        self.csem = {e: st.enter_context(nc.semaphore(f"c_{e}")) for e in COMPUTE}
        self.rings = {q: [st.enter_context(nc.semaphore(f"d_{q}{i}")) for i in range(NRING)] for q in QUEUES}
        self.ccsem = st.enter_context(nc.semaphore("ccsem"))
        self.cccount = 0
        self.ccount = {e: 0 for e in COMPUTE}
        self.dcount = {q: 0 for q in QUEUES}
        self.waited = {e: {} for e in ("sync",) + COMPUTE}
        self.emitted = 0
        self.barrier = []
        self.stats = {}
        self.nwaits = 0

    def sb(self, name, shape, dtype):
        self.nuid = getattr(self, "nuid", 0) + 1
        name = f"{name}_s{self.nuid}"
        t = self.stack.enter_context(self.nc.sbuf_tensor(name, list(shape), dtype))
        return Buf(name, t)

    def ps(self, name, shape, dtype):
        self.nuid = getattr(self, "nuid", 0) + 1
        name = f"{name}_p{self.nuid}"
        t = self.stack.enter_context(self.nc.psum_tensor(name, list(shape), dtype))
        return Buf(name, t)

    def dram(self, name, shape, dtype, kind="Internal"):
        t = self.nc.dram_tensor(name, list(shape), dtype, kind=kind)
        return Buf(name, t, is_dram=True)

    def _record(self, eng, meth, kw):
        op = Op()
        op.id = len(self.ops)
        op.eng = eng
        op.meth = meth
        op.kw = kw
        op.is_dma = meth in ("dma_start", "dma_start_transpose", "collective_compute")
        op.signaled = False
        op.sem = None
        op.val = 0
        op.prewait = None
        deps = set()
        extra_r = kw.pop("_reads", [])
        extra_w = kw.pop("_writes", [])
        writes, reads = [], []
        for k, v in kw.items():
            vs = v if isinstance(v, (list, tuple)) else [v]
            for x in vs:
                if isinstance(x, View):
                    if k in ("out", "accum_out", "outs") or (k == "ap" and meth in ("memset", "memzero")):
                        writes.append(x)
                    else:
                        reads.append(x)
        reads += extra_r
        writes += extra_w
        for v in reads:
            self._gather(v, False, deps)
        for v in writes:
            self._gather(v, True, deps)
        for v in reads:
            self._update(v, False, op.id)
        for v in writes:
            self._update(v, True, op.id)
        deps.discard(op.id)
        op.deps = deps
        self.ops.append(op)
        return op

    def _gather(self, v, is_write, deps):
        R = v.buf.regions
        if v.key is None:
            regs = list(R.values())
        else:
            regs = [R[k] for k in (v.key, None) if k in R]
        for reg in regs:
            if reg[0] is not None:
                deps.add(reg[0])
            if is_write:
                deps.update(reg[1])

    def _update(self, v, is_write, oid):
        R = v.buf.regions
        if is_write:
            if v.key is None:
                R.clear()
            R[v.key] = [oid, []]
        else:
            R.setdefault(v.key, [None, []])[1].append(oid)

    def dma(self, q, out, in_, **kw):
        eng = {"sync": self.sp, "gpsimd": self.pool, "scalar": self.act}[q]
        return eng.dma_start(out=out, in_=in_, **kw)

    def emit(self, final=True):
        nc = self.nc
        ops = self.ops
        phase = ops[self.emitted:]
        first_id = self.emitted
        self.emitted = len(ops)
        for op in phase:
            for d in op.deps:
                dop = ops[d]
                if d < first_id:
                    continue
                if dop.eng == "tensor" and op.eng == "tensor" and not dop.is_dma and not op.is_dma:
                    continue
                dop.signaled = True
        per = {}
        for op in phase:
            per.setdefault(op.eng, []).append(op)
        for e, lst in per.items():
            for op in reversed(lst):
                if not op.is_dma:
                    op.signaled = True
                    break
        for op in phase:
            if op.meth == "collective_compute":
                self.cccount += 1
                op.sem = self.ccsem
                op.val = self.cccount
                op.signaled = True
            elif op.is_dma:
                k = self.dcount[op.eng]
                self.dcount[op.eng] += 1
                op.sem = self.rings[op.eng][k % NRING]
                op.val = 16 * (k // NRING + 1)
                if k >= NRING:
                    op.prewait = (op.sem, 16 * (k // NRING))
                op.signaled = True
            elif op.signaled:
                self.ccount[op.eng] += 1
                op.sem = self.csem[op.eng]
                op.val = self.ccount[op.eng]
        for e, v in per.items():
            self.stats[e] = self.stats.get(e, 0) + len(v)
        barrier_in = list(self.barrier)
        dcount = self.dcount
        rings = self.rings

        def dma_final_waits():
            ws = []
            for q in QUEUES:
                n = dcount[q]
                for i in range(min(n, NRING)):
                    cnt = (n - 1 - i) // NRING + 1
                    ws.append((rings[q][i], 16 * cnt))
            if self.cccount > 0:
                ws.append((self.ccsem, self.cccount))
            return ws

        def run(engname, e):
            waited = self.waited[engname]

            def do_waits(ws):
                for sem, val in ws:
                    key = id(sem)
                    if waited.get(key, 0) >= val:
                        continue
                    waited[key] = val
                    e.wait_ge(sem, val)
                    self.nwaits += 1

            do_waits(barrier_in)
            for op in per.get(engname, []):
                ws = []
                if op.prewait is not None:
                    ws.append(op.prewait)
                for d in sorted(op.deps):
                    dop = ops[d]
                    if dop.sem is None:
                        continue
                    if dop.eng == "tensor" and op.eng == "tensor" and not dop.is_dma and not op.is_dma:
                        continue
                    ws.append((dop.sem, dop.val))
                do_waits(ws)
                kw = {}
                for k, v in op.kw.items():
                    if isinstance(v, View):
                        kw[k] = v.ap
                    elif isinstance(v, (list, tuple)) and v and isinstance(v[0], View):
                        kw[k] = [x.ap for x in v]
                    else:
                        kw[k] = v
                ins = getattr(e, op.meth)(**kw)
                if op.signaled:
                    ins.then_inc(op.sem, 16 if (op.is_dma and op.meth != "collective_compute") else 1)
            if final and engname == "sync":
                do_waits(dma_final_waits())

        with nc.Block() as block:
            @block.sync
            def _(e):
                run("sync", e)

            @block.tensor
            def _(e):
                run("tensor", e)

            @block.vector
            def _(e):
                run("vector", e)

            @block.scalar
            def _(e):
                run("scalar", e)

            @block.gpsimd
            def _(e):
                run("gpsimd", e)
        bar = dma_final_waits()
        for e in COMPUTE:
            if self.ccount[e] > 0:
                bar.append((self.csem[e], self.ccount[e]))
        self.barrier = bar
        self.stats["waits"] = self.nwaits
        self.stack.close()
        self.stack = ExitStack()
        if final:
            self.gstack.close()


from concourse.bass_utils import run_bass_kernel_spmd
import ml_dtypes

NBF = ml_dtypes.bfloat16
D = 1024
T = 2048
HALO = 4
NEG = -30000.0


class ColPack:
    def __init__(self):
        self.cols = []
        self.off = {}
        self.n = 0

    def add(self, name, vec, rows=128):
        vec = np.asarray(vec, np.float32).reshape(-1)
        assert vec.size % rows == 0
        m = vec.reshape(-1, rows).T
        a = np.zeros((128, m.shape[1]), np.float32)
        a[:rows] = m
        self.off[name] = (self.n, m.shape[1], rows)
        self.cols.append(a)
        self.n += m.shape[1]

    def array(self):
        return np.ascontiguousarray(np.concatenate(self.cols, axis=1))


class Cst:
    def __init__(self, P, buf, off):
        self.buf = buf
        self.off = off

    def col(self, name, j=0, rows=None):
        o, n, r = self.off[name]
        r = rows or r
        return self.buf[0:r, o + j:o + j + 1]

    def cols(self, name):
        o, n, r = self.off[name]
        return self.buf[0:r, o:o + n]


def new_nc():
    return bass.Bass("TRN2", target_bir_lowering=False)


def load_cast(P, q, dram_view, stage_view, bf_view, cast_eng):
    P.dma(q, out=stage_view, in_=dram_view)
    cast_eng.tensor_copy(out=bf_view, in_=stage_view)


A_OFF = None


def a_colpack(inp, l):
    cp = ColPack()
    cp.add("g_mix", inp["norm_mix_g"][l])
    cp.add("g_cq", inp["mla_q_norm_g"][l])
    cp.add("g_ckv", inp["mla_kv_norm_g"][l])
    cp.add("g_q", inp["mla_q_gain"][l], 96)
    cp.add("g_k", inp["mla_k_gain"][l], 96)
    cp.add("g_fq", inp["fox_q_gain"][l], 64)
    cp.add("g_fk", inp["fox_k_gain"][l], 64)
    cp.add("b_f", inp["fox_b_f"][l], 8)
    cw = inp["ssm_conv_w"][l]
    for k in range(4):
        cp.add(f"cw{k}", cw[k])
    cp.add("cb", inp["ssm_conv_b"][l])
    cp.add("dt_b", inp["ssm_dt_bias"][l], 16)
    cp.add("A_log", inp["ssm_A_log"][l], 16)
    cp.add("b_gate", inp["b_gate"][l])
    inv = 1.0 / (10000.0 ** (np.arange(0, 32, 2, dtype=np.float32) / 32.0))
    invf = np.zeros(96, np.float32)
    invf[64:80] = inv
    invf[80:96] = inv
    cp.add("invf", invf, 96)
    return cp


def build_A(off):
    nc = new_nc()
    P = Prog(nc)
    TT = T + HALO
    NT = T // 512
    EI, EO = "ExternalInput", "ExternalOutput"
    xT = P.dram("xT", [D, TT], F32, EI)
    pos = P.dram("pos", [1, T], I32, EI)
    w_in = P.dram("w_in", [D, 7864], F32, EI)
    w_uq = P.dram("w_uq", [384, 768], F32, EI)
    w_kp = P.dram("w_kp", [256, 768], F32, EI)
    w_v = P.dram("w_v", [256, 512], F32, EI)
    cst_d = P.dram("cst", [128, off["_n"]], F32, EI)
    mats = P.dram("mats", [128, 2 * 96], F32, EI)
    o_qm = P.dram("o_qm", [8, 96, T], BF16, EO)
    o_km = P.dram("o_km", [8, 96, T], BF16, EO)
    o_vm = P.dram("o_vm", [512, T], BF16, EO)
    o_qf = P.dram("o_qf", [8, 64, T], BF16, EO)
    o_kf = P.dram("o_kf", [8, 64, T], BF16, EO)
    o_vf = P.dram("o_vf", [512, T], BF16, EO)
    o_lf = P.dram("o_lf", [8, T], F32, EO)
    o_sz = P.dram("o_sz", [1024, T], BF16, EO)
    o_xbc = P.dram("o_xbc", [1536, T], BF16, EO)
    o_dt = P.dram("o_dt", [16, T], F32, EO)
    o_a = P.dram("o_a", [16, T], F32, EO)
    o_g = P.dram("o_g", [3072, T], BF16, EO)

    cstb = P.sb("cstb", [128, off["_n"]], F32)
    C = Cst(P, cstb, off)
    P.dma("sync", out=cstb.full(), in_=cst_d.full())
    matf = P.sb("matf", [128, 192], F32)
    matb = P.sb("matb", [128, 192], BF16)
    P.dma("sync", out=matf.full(), in_=mats.full())
    P.dve.tensor_copy(out=matb.full(), in_=matf.full())
    prh = matb[0:96, 0:96]
    sel = matb[0:32, 96:192]
    ones = P.sb("ones", [128, 128], F32)
    P.dve.memset(ap=ones.full(), constant=1.0)
    eps = P.sb("eps", [128, 1], F32)
    P.dve.memset(ap=eps.full(), constant=1e-6)
    one1 = P.sb("one1", [128, 1], F32)
    P.dve.memset(ap=one1.full(), constant=1.0)
    nbf = P.sb("nbf", [8, 1], F32)
    P.dve.tensor_scalar(out=nbf.full(), in0=C.col("b_f"), scalar1=-1.0, scalar2=None, op0=ALU.mult)
    Aneg = P.sb("Aneg", [16, 1], F32)
    P.act.activation(out=Aneg.full(), in_=C.col("A_log"), func=AF.Exp)
    P.dve.tensor_scalar(out=Aneg.full(), in0=Aneg.full(), scalar1=-1.0, scalar2=None, op0=ALU.mult)

    pb = [P.ps(f"pb{i}", [128, 512], F32) for i in range(8)]
    pbi = {}

    def nxt_ps(lo=0, hi=4):
        i = pbi.get(lo, 0)
        pbi[lo] = (i + 1) % (hi - lo)
        return pb[lo + i]

    Ctab = P.sb("Ctab", [96, T], F32)
    Stab = P.sb("Stab", [96, T], F32)
    posi = P.sb("posi", [96, 512], I32)
    posf = P.sb("posf", [96, 512], F32)
    rr_tmp = P.sb("rr_tmp", [96, 512], F32)
    rr_i = P.sb("rr_i", [96, 512], I32)
    rr_m = P.sb("rr_m", [96, 512], F32)

    def sin_table(outv, phase):
        P.dve.tensor_scalar(out=rr_tmp.full(), in0=posf.full(), scalar1=C.col("invf"), scalar2=phase,
                            op0=ALU.mult, op1=ALU.add)
        P.dve.tensor_scalar(out=rr_m.full(), in0=rr_tmp.full(), scalar1=1.0 / (2 * np.pi), scalar2=None, op0=ALU.mult)
        P.dve.tensor_copy(out=rr_i.full(), in_=rr_m.full())
        P.dve.tensor_copy(out=rr_m.full(), in_=rr_i.full())
        P.dve.scalar_tensor_tensor(out=rr_tmp.full(), in0=rr_m.full(), scalar=-2 * np.pi, in1=rr_tmp.full(),
                                   op0=ALU.mult, op1=ALU.add)
        P.dve.tensor_scalar(out=rr_m.full(), in0=rr_tmp.full(), scalar1=np.pi, scalar2=-2 * np.pi, op0=ALU.is_gt, op1=ALU.mult)
        P.dve.tensor_tensor(out=rr_tmp.full(), in0=rr_tmp.full(), in1=rr_m.full(), op=ALU.add)
        P.dve.tensor_scalar(out=rr_m.full(), in0=rr_tmp.full(), scalar1=-np.pi, scalar2=2 * np.pi, op0=ALU.is_lt, op1=ALU.mult)
        P.dve.tensor_tensor(out=rr_tmp.full(), in0=rr_tmp.full(), in1=rr_m.full(), op=ALU.add)
        P.act.activation(out=outv, in_=rr_tmp.full(), func=AF.Sin)

    for i in range(NT):
        P.dma("sync", out=posi.full(), in_=pos[:, i * 512:(i + 1) * 512].f(lambda a: a.partition_broadcast(96)))
        P.dve.tensor_copy(out=posf.full(), in_=posi.full())
        sin_table(Stab[:, i * 512:(i + 1) * 512], 0.0)
        sin_table(Ctab[:, i * 512:(i + 1) * 512], np.pi / 2)
    P.dve.memset(ap=Stab[0:64, :], constant=0.0)
    P.dve.memset(ap=Ctab[0:64, :], constant=1.0)

    hn = P.sb("hn", [128, 8, TT], BF16)
    xst = P.sb("xst", [128, 8, 512], F32)
    sq = P.sb("sq", [128, 512], F32)
    rstd = P.sb("rstd", [128, 512], F32)
    xTv = xT.full().re("(kc p) n -> p kc n", p=128)

    def rstd_from(ps_view, n_feat, rows, width, rstd_view):
        P.act.activation(out=rstd_view, in_=ps_view, func=AF.Sqrt, bias=eps[0:rows, 0:1], scale=1.0 / n_feat)
        P.dve.reciprocal(out=rstd_view, in_=rstd_view)

    tiles = [(0, HALO)] + [(HALO + i * 512, 512) for i in range(NT)]
    for (c0, w) in tiles:
        P.dma("sync", out=xst[:, :, 0:w], in_=xTv[:, :, c0:c0 + w])
        ps = nxt_ps(4, 6)
        for kc in range(8):
            P.act.activation(out=sq[:, 0:w], in_=xst[:, kc, 0:w], func=AF.Square)
            P.pe.matmul(out=ps[:, 0:w], lhsT=ones.full(), rhs=sq[:, 0:w], start=(kc == 0), stop=(kc == 7))
        rstd_from(ps[:, 0:w], 1024.0, 128, w, rstd[:, 0:w])
        for kc in range(8):
            P.dve.scalar_tensor_tensor(out=hn[:, kc, c0:c0 + w], in0=xst[:, kc, 0:w], scalar=C.col("g_mix", kc),
                                       in1=rstd[:, 0:w], op0=ALU.mult, op1=ALU.mult)

    wst = [P.sb(f"wst{i}", [128, 8, 512], F32) for i in range(2)]
    wbf = [P.sb(f"wbf{i}", [128, 8, 512], BF16) for i in range(2)]
    wcnt = [0]
    w_inv = w_in.full().re("(kc p) n -> p kc n", p=128)

    def load_w(c0, ncols):
        i = wcnt[0] % 2
        wcnt[0] += 1
        q = "sync" if i == 0 else "gpsimd"
        P.dma(q, out=wst[i][:, :, 0:ncols], in_=w_inv[:, :, c0:c0 + ncols])
        P.pool.tensor_copy(out=wbf[i][:, :, 0:ncols], in_=wst[i][:, :, 0:ncols])
        return wbf[i]

    def proj(wb, wc0, m, c0, w, ps_view):
        for kc in range(8):
            P.pe.matmul(out=ps_view, lhsT=wb[:, kc, wc0:wc0 + m], rhs=hn[:, kc, c0:c0 + w],
                        start=(kc == 0), stop=(kc == 7))

    ostg_cnt = [0]
    ostg = [P.sb(f"ostg{i}", [128, 512], BF16) for i in range(4)]

    def next_ostg():
        i = ostg_cnt[0] % 4
        ostg_cnt[0] += 1
        return ostg[i]

    def out_dma(dst_view, src_view):
        q = "sync" if ostg_cnt[0] % 2 else "gpsimd"
        P.dma(q, out=dst_view, in_=src_view)

    hraw = P.sb("hraw", [96, 512], F32)
    hsq = P.sb("hsq", [96, 512], F32)
    hrs = P.sb("hrs", [96, 512], F32)
    hnf = P.sb("hnf", [96, 512], F32)
    hnb = P.sb("hnb", [96, 512], BF16)
    ht1 = P.sb("ht1", [96, 512], F32)
    ht2 = P.sb("ht2", [96, 512], F32)

    def headnorm(ps_view, d, gain_col, rope, tok0, dst_view):
        P.act.activation(out=hsq[0:d, :], in_=ps_view, func=AF.Square)
        P.act.copy(out=hraw[0:d, :], in_=ps_view)
        ps2 = nxt_ps(4, 6)
        P.pe.matmul(out=ps2[0:d, :], lhsT=ones[0:d, 0:d], rhs=hsq[0:d, :], start=True, stop=True)
        rstd_from(ps2[0:d, :], float(d), d, 512, hrs[0:d, :])
        og = next_ostg()
        if not rope:
            P.dve.scalar_tensor_tensor(out=og[0:d, :], in0=hraw[0:d, :], scalar=gain_col, in1=hrs[0:d, :],
                                       op0=ALU.mult, op1=ALU.mult)
        else:
            P.dve.scalar_tensor_tensor(out=hnf[0:d, :], in0=hraw[0:d, :], scalar=gain_col, in1=hrs[0:d, :],
                                       op0=ALU.mult, op1=ALU.mult)
            P.act.copy(out=hnb[0:d, :], in_=hnf[0:d, :])
            ps3 = nxt_ps(6, 8)
            P.pe.matmul(out=ps3[0:d, :], lhsT=prh, rhs=hnb[0:d, :], start=True, stop=True)
            P.dve.tensor_tensor(out=ht1[0:d, :], in0=hnf[0:d, :], in1=Ctab[0:d, tok0:tok0 + 512], op=ALU.mult)
            P.dve.tensor_tensor(out=ht2[0:d, :], in0=ps3[0:d, :], in1=Stab[0:d, tok0:tok0 + 512], op=ALU.mult)
            P.pool.tensor_tensor(out=og[0:d, :], in0=ht1[0:d, :], in1=ht2[0:d, :], op=ALU.add)
        out_dma(dst_view, og[0:d, :])

    lat = P.sb("lat", [128, 3, 512], F32)
    latn = P.sb("latn", [128, 3, 512], BF16)

    def latent_norm(ps_list, gname):
        nch = len(ps_list)
        ps2 = nxt_ps(4, 6)
        for i, psv in enumerate(ps_list):
            P.act.activation(out=sq.full(), in_=psv, func=AF.Square)
            P.act.copy(out=lat[:, i, :], in_=psv)
            P.pe.matmul(out=ps2.full(), lhsT=ones.full(), rhs=sq.full(), start=(i == 0), stop=(i == nch - 1))
        rstd_from(ps2.full(), 128.0 * nch, 128, 512, rstd.full())
        for i in range(nch):
            P.dve.scalar_tensor_tensor(out=latn[:, i, :], in0=lat[:, i, :], scalar=C.col(gname, i), in1=rstd.full(),
                                       op0=ALU.mult, op1=ALU.mult)

    def small_w(name, dram, kc_n, ncols, i):
        stg = wst[i].full().re("p a b -> p (a b)")[:, 0:kc_n * ncols].re("p (a b) -> p a b", a=kc_n)
        bfb = P.sb(name, [128, kc_n, ncols], BF16)
        P.dma("gpsimd", out=stg, in_=dram.full().re("(kc p) n -> p kc n", p=128))
        P.pool.tensor_copy(out=bfb.full(), in_=stg)
        return bfb

    uqb = small_w("uqb", w_uq, 3, 768, 0)
    kpb = small_w("kpb", w_kp, 2, 768, 1)
    wvb = small_w("wvb", w_v, 2, 512, 0)
    main = tiles[1:]
    wb = load_w(0, 384)
    for ti, (c0, w) in enumerate(main):
        pss = []
        for ch in range(3):
            ps = nxt_ps(0, 4)
            proj(wb, ch * 128, 128, c0, 512, ps.full())
            pss.append(ps.full())
        latent_norm(pss, "g_cq")
        for h in range(8):
            ps = nxt_ps(0, 4)
            for kc in range(3):
                P.pe.matmul(out=ps[0:96, :], lhsT=uqb[:, kc, h * 96:(h + 1) * 96], rhs=latn[:, kc, :],
                            start=(kc == 0), stop=(kc == 2))
            headnorm(ps[0:96, :], 96, C.col("g_q"), True, ti * 512, o_qm[h, :, ti * 512:(ti + 1) * 512])
    wb = load_w(384, 288)
    krb = P.sb("krb", [32, 512], BF16)
    for ti, (c0, w) in enumerate(main):
        pss = []
        for ch in range(2):
            ps = nxt_ps(0, 4)
            proj(wb, ch * 128, 128, c0, 512, ps.full())
            pss.append(ps.full())
        ps = nxt_ps(0, 4)
        proj(wb, 256, 32, c0, 512, ps[0:32, :])
        P.act.copy(out=krb.full(), in_=ps[0:32, :])
        latent_norm(pss, "g_ckv")
        for h in range(8):
            ps = nxt_ps(0, 4)
            for kc in range(2):
                P.pe.matmul(out=ps[0:96, :], lhsT=kpb[:, kc, h * 96:(h + 1) * 96], rhs=latn[:, kc, :],
                            start=(kc == 0), stop=False)
            P.pe.matmul(out=ps[0:96, :], lhsT=sel, rhs=krb.full(), start=False, stop=True)
            headnorm(ps[0:96, :], 96, C.col("g_k"), True, ti * 512, o_km[h, :, ti * 512:(ti + 1) * 512])
        for ch in range(4):
            ps = nxt_ps(0, 4)
            for kc in range(2):
                P.pe.matmul(out=ps.full(), lhsT=wvb[:, kc, ch * 128:(ch + 1) * 128], rhs=latn[:, kc, :],
                            start=(kc == 0), stop=(kc == 1))
            og = next_ostg()
            P.act.copy(out=og.full(), in_=ps.full())
            out_dma(o_vm[ch * 128:(ch + 1) * 128, ti * 512:(ti + 1) * 512], og.full())
    for (base, gname, dst) in ((672, "g_fq", o_qf), (672 + 512, "g_fk", o_kf)):
        wb = load_w(base, 512)
        for ti, (c0, w) in enumerate(main):
            for h in range(8):
                ps = nxt_ps(0, 4)
                proj(wb, h * 64, 64, c0, 512, ps[0:64, :])
                headnorm(ps[0:64, :], 64, C.col(gname), False, ti * 512, dst[h, :, ti * 512:(ti + 1) * 512])
    def plain_group(base, ncols, func, bias_name, dst, dst_row0):
        wb = load_w(base, ncols)
        for ti, (c0, w) in enumerate(main):
            for ch in range(ncols // 128):
                ps = nxt_ps(0, 4)
                proj(wb, ch * 128, 128, c0, 512, ps.full())
                og = next_ostg()
                if bias_name is None:
                    P.act.activation(out=og.full(), in_=ps.full(), func=func)
                else:
                    P.act.activation(out=og.full(), in_=ps.full(), func=func,
                                     bias=C.col(bias_name, (dst_row0 // 128) + ch))
                out_dma(dst[dst_row0 + ch * 128:dst_row0 + (ch + 1) * 128, ti * 512:(ti + 1) * 512], og.full())

    plain_group(672 + 1024, 512, AF.Copy, None, o_vf, 0)
    FB = 672 + 1536
    SB = 672 + 1544
    wf = load_w(FB, 8)
    lf1 = P.sb("lf1", [16, 512], F32)
    lf2 = P.sb("lf2", [16, 512], F32)
    for ti, (c0, w) in enumerate(main):
        ps = nxt_ps(0, 4)
        proj(wf, 0, 8, c0, 512, ps[0:8, :])
        P.act.activation(out=lf1[0:8, :], in_=ps[0:8, :], func=AF.Exp, bias=nbf[0:8, 0:1], scale=-1.0)
        P.act.activation(out=lf1[0:8, :], in_=lf1[0:8, :], func=AF.Ln, bias=one1[0:8, 0:1], scale=1.0)
        P.dve.tensor_scalar(out=lf2[0:8, :], in0=lf1[0:8, :], scalar1=-1.0, scalar2=None, op0=ALU.mult)
        P.dma("sync", out=o_lf[:, ti * 512:(ti + 1) * 512], in_=lf2[0:8, :])
    wd = load_w(SB + 1024 + 1536, 16)
    dt1 = P.sb("dt1", [16, 512], F32)
    dt2 = P.sb("dt2", [16, 512], F32)
    for ti, (c0, w) in enumerate(main):
        ps = nxt_ps(0, 4)
        proj(wd, 0, 16, c0, 512, ps[0:16, :])
        P.act.activation(out=dt1.full(), in_=ps[0:16, :], func=AF.Exp, bias=C.col("dt_b"), scale=1.0)
        P.act.activation(out=dt1.full(), in_=dt1.full(), func=AF.Ln, bias=one1[0:16, 0:1], scale=1.0)
        P.dma("sync", out=o_dt[:, ti * 512:(ti + 1) * 512], in_=dt1.full())
        P.dve.tensor_scalar(out=dt2.full(), in0=dt1.full(), scalar1=Aneg[:, 0:1], scalar2=None, op0=ALU.mult)
        P.dma("sync", out=o_a[:, ti * 512:(ti + 1) * 512], in_=dt2.full())
    for blk in range(2):
        plain_group(SB + blk * 512, 512, AF.Silu, None, o_sz, blk * 512)
    upre = P.sb("upre", [128, 516], F32)
    carry = P.sb("carry", [128, 12, 4], F32)
    acc = [P.sb(f"acc{i}", [128, 512], F32) for i in range(2)]
    for blk in range(3):
        wb = load_w(SB + 1024 + blk * 512, 512)
        for ch in range(4):
            cg = blk * 4 + ch
            ps = nxt_ps(0, 4)
            proj(wb, ch * 128, 128, 0, HALO, ps[:, 0:HALO])
            P.act.copy(out=carry[:, cg, :], in_=ps[:, 0:HALO])
        for ti, (c0, w) in enumerate(main):
            for ch in range(4):
                cg = blk * 4 + ch
                ps = nxt_ps(0, 4)
                proj(wb, ch * 128, 128, c0, 512, ps.full())
                P.act.copy(out=upre[:, 4:516], in_=ps.full())
                P.dve.tensor_copy(out=upre[:, 0:4], in_=carry[:, cg, :])
                P.pool.tensor_copy(out=carry[:, cg, :], in_=upre[:, 512:516])
                a0 = acc[0]
                P.dve.tensor_scalar(out=a0.full(), in0=upre[:, 4:516], scalar1=C.col("cw3", cg), scalar2=C.col("cb", cg),
                                    op0=ALU.mult, op1=ALU.add)
                for k in range(3):
                    P.dve.scalar_tensor_tensor(out=a0.full(), in0=upre[:, 1 + k:513 + k], scalar=C.col(f"cw{k}", cg),
                                               in1=a0.full(), op0=ALU.mult, op1=ALU.add)
                og = next_ostg()
                P.act.activation(out=og.full(), in_=a0.full(), func=AF.Silu)
                out_dma(o_xbc[cg * 128:(cg + 1) * 128, ti * 512:(ti + 1) * 512], og.full())
    GB = SB + 2576
    for blk in range(6):
        plain_group(GB + blk * 512, 512, AF.Sigmoid, "b_gate", o_g, blk * 512)
    P.emit()
    return nc, P


def _bf(a):
    return np.asarray(a).astype(np.float32)


_PROG_CACHE = {}


def _const_mats():
    m = np.zeros((128, 192), np.float32)
    for i in range(16):
        m[80 + i, 64 + i] = -1.0
        m[64 + i, 80 + i] = 1.0
    for i in range(32):
        m[i, 96 + 64 + i] = 1.0
    return m


def run_A(inp, l, x_full, pos_full):
    cp = a_colpack(inp, l)
    off = dict(cp.off)
    off["_n"] = cp.n
    if "A" not in _PROG_CACHE:
        _PROG_CACHE["A"] = build_A(off)[0]
    nc = _PROG_CACHE["A"]
    cst = cp.array()
    wukv = inp["mla_w_ukv"][l].reshape(256, 8, 128)
    w_kp = np.zeros((256, 8, 96), np.float32)
    w_kp[:, :, 0:64] = wukv[:, :, 0:64]
    w_v = np.ascontiguousarray(wukv[:, :, 64:128].reshape(256, 512))
    mats = _const_mats()
    xf = x_full.reshape(16384, D)
    in_maps = []
    for c in range(8):
        t0 = c * T
        xt = np.zeros((D, T + HALO), np.float32)
        xt[:, HALO:] = xf[t0:t0 + T].T
        if c % 4 != 0:
            xt[:, 0:HALO] = xf[t0 - HALO:t0].T
        in_maps.append({
            "xT": np.ascontiguousarray(xt),
            "pos": np.ascontiguousarray(pos_full.reshape(1, 16384)[:, t0:t0 + T]).astype(np.int32),
            "w_in": np.ascontiguousarray(inp["w_in"][l]),
            "w_uq": np.ascontiguousarray(inp["mla_w_uq"][l]),
            "w_kp": np.ascontiguousarray(w_kp.reshape(256, 768)),
            "w_v": w_v, "cst": cst, "mats": mats,
        })
    res = run_bass_kernel_spmd(nc, in_maps, core_ids=list(range(8)))
    return res.results


S_ = 8192
NKT = S_ // 128
NQT = S_ // 512


def build_BC():
    nc = new_nc()
    P = Prog(nc)
    EI, EO = "ExternalInput", "ExternalOutput"
    qm = P.dram("qm", [2, 96, S_], BF16, EI)
    km = P.dram("km", [2, 96, S_], BF16, EI)
    vm = P.dram("vm", [2, 128, NKT, 64], BF16, EI)
    qf = P.dram("qf", [2, 64, S_], BF16, EI)
    kf = P.dram("kf", [2, 64, S_], BF16, EI)
    vf = P.dram("vf", [2, 128, NKT, 64], BF16, EI)
    lf = P.dram("lf", [2, 128, NKT], F32, EI)
    msk = P.dram("msk", [128, 8, 512], F32, EI)
    cm = P.dram("cm", [128, 4, 128], F32, EI)
    x_tm = P.dram("x_tm", [128, NKT, 256], BF16, EI)
    B_tm = P.dram("B_tm", [128, NKT, 128], BF16, EI)
    BT = P.dram("BT", [128, S_], BF16, EI)
    CT = P.dram("CT", [128, S_], BF16, EI)
    dt_tm = P.dram("dt_tm", [128, NKT, 4], F32, EI)
    a_tm = P.dram("a_tm", [128, NKT, 4], F32, EI)
    Dv = P.dram("Dv", [128, 4], F32, EI)
    o_m = P.dram("o_m", [2, 64, S_], BF16, EO)
    o_f = P.dram("o_f", [2, 64, S_], BF16, EO)
    o_y = P.dram("o_y", [128, NKT, 256], F32, EO)
    fsc = P.dram("fsc", [3, S_], BF16)

    cmb = P.sb("cmb", [128, 4, 128], F32)
    P.dma("sync", out=cmb.full(), in_=cm.full())
    tri, trimask, ident, ones = cmb[:, 0, :], cmb[:, 1, :], cmb[:, 2, :], cmb[:, 3, :]
    mskb = P.sb("mskb", [128, 8, 512], F32)
    P.dma("gpsimd", out=mskb.full(), in_=msk.full())
    zero = P.sb("zero", [128, 1], F32)
    P.dve.memset(ap=zero.full(), constant=0.0)

    pb = [P.ps(f"pb{i}", [128, 512], F32) for i in range(8)]
    K_sb = P.sb("K_sb", [128, S_], BF16)
    Q_sb = P.sb("Q_sb", [128, S_], BF16)
    V_sb = P.sb("V_sb", [128, NKT, 128], BF16)
    P.dve.memset(ap=V_sb[:, :, 64:128], constant=1.0)
    pt = [P.sb(f"pt{i}", [128, 512], BF16) for i in range(3)]
    mt = [P.sb(f"mt{i}", [128, 512], F32) for i in range(2)]
    rl = P.sb("rl", [128, 512], F32)
    rl2 = P.sb("rl2", [64, 512], F32)
    ot = [P.sb(f"ot{i}", [64, 512], BF16) for i in range(2)]
    negF = P.sb("negF", [128, NKT], F32)

    cnt = [0, 0, 0]

    def attention(dk, scale, mask0, bias_fn, out_dram_h):
        for qt in range(NQT):
            oacc = pb[3 + qt % 2]
            nk = 4 * qt + 4
            for kt in range(nk):
                i3 = cnt[0] % 3
                cnt[0] += 1
                ps = pb[i3]
                P.pe.matmul(out=ps.full(), lhsT=K_sb[0:dk, kt * 128:(kt + 1) * 128],
                            rhs=Q_sb[0:dk, qt * 512:(qt + 1) * 512], start=True, stop=True)
                if kt >= 4 * qt:
                    m = mt[cnt[1] % 2]
                    cnt[1] += 1
                    P.dve.tensor_tensor(out=m.full(), in0=ps.full(), in1=mskb[:, mask0 + kt - 4 * qt, :], op=ALU.add)
                    src = m.full()
                else:
                    src = ps.full()
                P.act.activation(out=pt[i3].full(), in_=src, func=AF.Exp, scale=scale, bias=bias_fn(kt))
                P.pe.matmul(out=oacc.full(), lhsT=V_sb[:, kt, :], rhs=pt[i3].full(), start=(kt == 0), stop=(kt == nk - 1))
            P.dve.reciprocal(out=rl[64:128, :], in_=oacc[64:128, :])
            P.dve.tensor_copy(out=rl2.full(), in_=rl[64:128, :])
            o = ot[qt % 2]
            P.dve.tensor_tensor(out=o.full(), in0=oacc[0:64, :], in1=rl2.full(), op=ALU.mult)
            P.dma("sync", out=out_dram_h[:, qt * 512:(qt + 1) * 512], in_=o.full())

    for h in range(2):
        P.dma("sync", out=K_sb[0:96, :], in_=km[h])
        P.dma("gpsimd", out=Q_sb[0:96, :], in_=qm[h])
        P.dma("sync", out=V_sb[:, :, 0:64], in_=vm[h])
        attention(96, 96.0 ** -0.5, 0, lambda kt: zero[:, 0:1], o_m[h])

    lfs = P.sb("lfs", [128, NKT], F32)
    wi = P.sb("wi", [128, NKT], F32)
    sc = [P.sb(f"sc{i}", [128, NKT], F32) for i in range(2)]
    Ff = P.sb("Ff", [128, NKT], F32)
    FT = P.sb("FT", [64, 128], F32)
    r1 = P.sb("r1", [64, 128], F32)
    fh = [P.sb(f"fh{i}", [64, 128], BF16) for i in range(3)]
    for h in range(2):
        P.dma("sync", out=lfs.full(), in_=lf[h])
        ps = pb[5]
        P.pe.matmul(out=ps[:, 0:NKT], lhsT=tri, rhs=lfs.full(), start=True, stop=True)
        P.act.copy(out=wi.full(), in_=ps[:, 0:NKT])
        ps = pb[6]
        P.pe.matmul(out=ps[:, 0:NKT], lhsT=ones, rhs=lfs.full(), start=True, stop=True)
        P.act.copy(out=sc[0].full(), in_=ps[:, 0:NKT])
        P.dve.tensor_tensor(out=wi.full(), in0=wi.full(), in1=sc[0].full(), op=ALU.subtract)
        cur = 0
        d = 1
        while d < NKT:
            nx = 1 - cur
            P.dve.tensor_copy(out=sc[nx][:, 0:d], in_=sc[cur][:, 0:d])
            P.dve.tensor_tensor(out=sc[nx][:, d:NKT], in0=sc[cur][:, d:NKT], in1=sc[cur][:, 0:NKT - d], op=ALU.add)
            cur = nx
            d *= 2
        P.dve.tensor_tensor(out=Ff.full(), in0=wi.full(), in1=sc[cur].full(), op=ALU.add)
        P.dve.tensor_scalar(out=negF.full(), in0=Ff.full(), scalar1=-1.0, scalar2=None, op0=ALU.mult)
        ps = pb[7]
        P.pe.transpose(out=ps[0:64, 0:128], in_=Ff.full(), identity=ident)
        P.act.copy(out=FT.full(), in_=ps[0:64, 0:128])
        P.dve.tensor_copy(out=fh[0].full(), in_=FT.full())
        P.dve.tensor_tensor(out=r1.full(), in0=FT.full(), in1=fh[0].full(), op=ALU.subtract)
        P.dve.tensor_copy(out=fh[1].full(), in_=r1.full())
        P.dve.tensor_tensor(out=r1.full(), in0=r1.full(), in1=fh[1].full(), op=ALU.subtract)
        P.dve.tensor_copy(out=fh[2].full(), in_=r1.full())
        for r in range(3):
            P.dma("sync", out=fsc[r].re("(kt p) -> kt p", p=128), in_=fh[r].full())
        P.dma("sync", out=K_sb[0:64, :], in_=kf[h])
        P.dve.memset(ap=K_sb[64:67, :], constant=8.0)
        P.dma("gpsimd", out=Q_sb[0:64, :], in_=qf[h])
        P.dma("gpsimd", out=Q_sb[64:67, :], in_=fsc.full())
        P.dma("sync", out=V_sb[:, :, 0:64], in_=vf[h])
        attention(67, 0.125, 4, lambda kt: negF[:, kt:kt + 1], o_f[h])

    a_sb = P.sb("a_sb", [128, NKT, 4], F32)
    dt_sb = P.sb("dt_sb", [128, NKT, 4], F32)
    Dsb = P.sb("Dsb", [128, 4], F32)
    P.dma("sync", out=a_sb.full(), in_=a_tm.full())
    P.dma("sync", out=dt_sb.full(), in_=dt_tm.full())
    P.dma("sync", out=Dsb.full(), in_=Dv.full())
    BTs = K_sb
    CTs = Q_sb
    P.dma("sync", out=BTs.full(), in_=BT.full())
    P.dma("gpsimd", out=CTs.full(), in_=CT.full())
    Acum = P.sb("Acum", [128, NKT, 4], F32)
    nAcum = P.sb("nAcum", [128, NKT, 4], F32)
    Atot = P.sb("Atot", [128, NKT, 4], F32)
    eA = P.sb("eA", [128, NKT, 4], F32)
    wdec = P.sb("wdec", [128, NKT, 4], F32)
    eAtot = P.sb("eAtot", [128, NKT, 4], F32)
    fl = lambda b: b.full().re("p c h -> p (c h)")
    ps = pb[0]
    P.pe.matmul(out=ps[:, 0:256], lhsT=tri, rhs=fl(a_sb), start=True, stop=True)
    P.act.copy(out=fl(Acum), in_=ps[:, 0:256])
    ps = pb[1]
    P.pe.matmul(out=ps[:, 0:256], lhsT=ones, rhs=fl(a_sb), start=True, stop=True)
    P.act.copy(out=fl(Atot), in_=ps[:, 0:256])
    P.dve.tensor_scalar(out=fl(nAcum), in0=fl(Acum), scalar1=-1.0, scalar2=None, op0=ALU.mult)
    P.act.activation(out=fl(eA), in_=fl(Acum), func=AF.Exp)
    P.act.activation(out=fl(eAtot), in_=fl(Atot), func=AF.Exp)
    P.dve.tensor_tensor(out=fl(wdec), in0=fl(Atot), in1=fl(Acum), op=ALU.subtract)
    P.act.activation(out=fl(wdec), in_=fl(wdec), func=AF.Exp)

    Hs = P.sb("Hs", [128, 256], F32)
    Hb = P.sb("Hb", [128, 256], BF16)
    P.dve.memset(ap=Hs.full(), constant=0.0)
    P.dve.memset(ap=Hb.full(), constant=0.0)
    xc = [P.sb(f"xc{i}", [128, 256], BF16) for i in range(2)]
    Bc = [P.sb(f"Bc{i}", [128, 128], BF16) for i in range(2)]
    cb = P.sb("cb", [128, 128], F32)
    xdt = P.sb("xdt", [128, 256], BF16)
    xdts = P.sb("xdts", [128, 256], BF16)
    at = [P.sb(f"at{i}", [128, 128], F32) for i in range(2)]
    tm = [P.sb(f"tm{i}", [128, 128], F32) for i in range(2)]
    dec = [P.sb(f"dec{i}", [128, 128], F32) for i in range(2)]
    MT = [P.sb(f"MT{i}", [128, 128], BF16) for i in range(2)]
    t1 = P.sb("t1", [128, 256], F32)
    t3 = P.sb("t3", [128, 256], F32)
    yo = [P.sb(f"yo{i}", [128, 256], F32) for i in range(2)]
    v3 = lambda v: v.re("p (h d) -> p h d", h=4)
    bc3 = lambda v: v.f(lambda a: a.unsqueeze(2).to_broadcast([128, 4, 64]))
    for c in range(NKT):
        x_c = xc[c % 2]
        B_c = Bc[c % 2]
        P.dma("sync", out=x_c.full(), in_=x_tm[:, c, :])
        P.dma("gpsimd", out=B_c.full(), in_=B_tm[:, c, :])
        BT_c = BTs[:, c * 128:(c + 1) * 128]
        CT_c = CTs[:, c * 128:(c + 1) * 128]
        ps_cb = pb[0]
        P.pe.matmul(out=ps_cb[:, 0:128], lhsT=BT_c, rhs=CT_c, start=True, stop=True)
        P.act.copy(out=cb.full(), in_=ps_cb[:, 0:128])
        P.dve.tensor_tensor(out=v3(xdt.full()), in0=v3(x_c.full()), in1=bc3(dt_sb[:, c, :]), op=ALU.mult)
        P.pool.tensor_tensor(out=v3(xdts.full()), in0=v3(xdt.full()), in1=bc3(wdec[:, c, :]), op=ALU.mult)
        ps_off = pb[1]
        P.pe.matmul(out=ps_off[:, 0:256], lhsT=CT_c, rhs=Hb.full(), start=True, stop=True)
        ps_y = pb[2]
        for h in range(4):
            i2 = h % 2
            P.dve.tensor_scalar(out=at[i2].full(), in0=tri, scalar1=a_sb[:, c, h:h + 1], scalar2=None, op0=ALU.mult)
            ps_A = pb[3 + i2]
            P.pe.matmul(out=ps_A[:, 0:128], lhsT=ones, rhs=at[i2].full(), start=True, stop=True)
            P.dve.tensor_tensor(out=tm[i2].full(), in0=ps_A[:, 0:128], in1=trimask, op=ALU.add)
            P.act.activation(out=dec[i2].full(), in_=tm[i2].full(), func=AF.Exp, bias=nAcum[:, c, h:h + 1], scale=1.0)
            P.pool.tensor_tensor(out=MT[i2].full(), in0=cb.full(), in1=dec[i2].full(), op=ALU.mult)
            P.pe.matmul(out=ps_y[:, h * 64:(h + 1) * 64], lhsT=MT[i2].full(), rhs=xdt[:, h * 64:(h + 1) * 64],
                        start=True, stop=True)
        P.dve.tensor_tensor(out=v3(t1.full()), in0=v3(ps_off[:, 0:256]), in1=bc3(eA[:, c, :]), op=ALU.mult)
        P.dve.tensor_tensor(out=t1.full(), in0=t1.full(), in1=ps_y[:, 0:256], op=ALU.add)
        P.pool.tensor_tensor(out=v3(t3.full()), in0=v3(x_c.full()), in1=bc3(Dsb.full()), op=ALU.mult)
        y_ = yo[c % 2]
        P.pool.tensor_tensor(out=y_.full(), in0=t1.full(), in1=t3.full(), op=ALU.add)
        P.dma("sync", out=o_y[:, c, :], in_=y_.full())
        ps_h = pb[5]
        P.pe.matmul(out=ps_h[:, 0:256], lhsT=B_c.full(), rhs=xdts.full(), start=True, stop=True)
        P.dve.tensor_tensor(out=v3(Hs.full()), in0=v3(Hs.full()), in1=bc3(eAtot[:, c, :]), op=ALU.mult)
        P.dve.tensor_tensor(out=Hs.full(), in0=Hs.full(), in1=ps_h[:, 0:256], op=ALU.add)
        P.act.copy(out=Hb.full(), in_=Hs.full())
    P.emit()
    return nc, P


def _bc_consts():
    msk = np.zeros((128, 8, 512), np.float32)
    p = np.arange(128)[:, None]
    q = np.arange(512)[None, :]
    for j in range(4):
        key = j * 128 + p
        msk[:, j, :] = np.where((key // 64) > (q // 64), NEG, 0.0)
        msk[:, 4 + j, :] = np.where(key > q, NEG, 0.0)
    cm = np.zeros((128, 4, 128), np.float32)
    jj = np.arange(128)[:, None]
    ii = np.arange(128)[None, :]
    cm[:, 0, :] = (jj <= ii).astype(np.float32)
    cm[:, 1, :] = np.where(jj > ii, NEG, 0.0)
    cm[:, 2, :] = np.eye(128, dtype=np.float32)
    cm[:, 3, :] = 1.0
    return msk, cm


def _tm(a):
    S, n = a.shape
    return np.ascontiguousarray(a.reshape(S // 128, 128, n).transpose(1, 0, 2))


def run_BC(inp, l, resA):
    if "BC" not in _PROG_CACHE:
        _PROG_CACHE["BC"] = build_BC()[0]
    nc = _PROG_CACHE["BC"]
    msk, cm = _bc_consts()

    def gather(name, b):
        return np.concatenate([np.asarray(resA[b * 4 + i][name]) for i in range(4)], axis=-1)

    in_maps = []
    for c in range(8):
        b, hg = c // 4, c % 4
        qm = gather("o_qm", b)[2 * hg:2 * hg + 2]
        km = gather("o_km", b)[2 * hg:2 * hg + 2]
        vmf = gather("o_vm", b)
        qf = gather("o_qf", b)[2 * hg:2 * hg + 2]
        kf = gather("o_kf", b)[2 * hg:2 * hg + 2]
        vff = gather("o_vf", b)
        lff = gather("o_lf", b)
        xbc = gather("o_xbc", b)
        dtf = gather("o_dt", b)
        af = gather("o_a", b)
        g = hg // 2
        vm = np.stack([_tm(vmf[(2 * hg + h) * 64:(2 * hg + h + 1) * 64].T) for h in range(2)])
        vf = np.stack([_tm(vff[(2 * hg + h) * 64:(2 * hg + h + 1) * 64].T) for h in range(2)])
        lf = np.stack([np.ascontiguousarray(lff[2 * hg + h].reshape(NKT, 128).T) for h in range(2)])
        x_tm = _tm(xbc[hg * 256:(hg + 1) * 256].T)
        Bf = xbc[1024 + g * 128:1024 + (g + 1) * 128]
        Cf = xbc[1280 + g * 128:1280 + (g + 1) * 128]
        Dv = np.broadcast_to(inp["ssm_D"][l][4 * hg:4 * hg + 4][None, :], (128, 4)).astype(np.float32)
        in_maps.append({
            "qm": np.ascontiguousarray(qm), "km": np.ascontiguousarray(km), "vm": vm,
            "qf": np.ascontiguousarray(qf), "kf": np.ascontiguousarray(kf), "vf": vf, "lf": lf,
            "msk": msk, "cm": cm, "x_tm": x_tm, "B_tm": _tm(Bf.T), "BT": np.ascontiguousarray(Bf),
            "CT": np.ascontiguousarray(Cf), "dt_tm": _tm(dtf[4 * hg:4 * hg + 4].T),
            "a_tm": _tm(af[4 * hg:4 * hg + 4].T), "Dv": np.ascontiguousarray(Dv),
        })
    res = run_bass_kernel_spmd(nc, in_maps, core_ids=list(range(8))).results
    om = np.zeros((2, 512, S_), NBF)
    of = np.zeros((2, 512, S_), NBF)
    y = np.zeros((2, 1024, S_), np.float32)
    for c in range(8):
        b, hg = c // 4, c % 4
        om[b, hg * 128:(hg + 1) * 128] = np.asarray(res[c]["o_m"]).reshape(128, S_)
        of[b, hg * 128:(hg + 1) * 128] = np.asarray(res[c]["o_f"]).reshape(128, S_)
        yy = np.asarray(res[c]["o_y"])
        y[b, hg * 256:(hg + 1) * 256] = yy.transpose(2, 1, 0).reshape(256, S_)
    return om, of, y


def build_D1():
    nc = new_nc()
    P = Prog(nc)
    EI, EO = "ExternalInput", "ExternalOutput"
    NT = T // 512
    omT = P.dram("omT", [512, T], BF16, EI)
    ofT = P.dram("ofT", [512, T], BF16, EI)
    yT = P.dram("yT", [1024, T], F32, EI)
    szT = P.dram("szT", [1024, T], BF16, EI)
    gT = P.dram("gT", [3072, T], BF16, EI)
    xT = P.dram("xT", [D, T], F32, EI)
    w_a = P.dram("w_a", [512, D], F32, EI)
    w_b = P.dram("w_b", [512, D], F32, EI)
    w_c = P.dram("w_c", [1024, D], F32, EI)
    w_o = P.dram("w_o", [1024, D], F32, EI)
    cst_d = P.dram("cst", [128, 8], F32, EI)
    o_x = P.dram("o_x", [D, T], F32, EO)

    cstb = P.sb("cstb", [128, 8], F32)
    P.dma("sync", out=cstb.full(), in_=cst_d.full())
    ones = P.sb("ones", [128, 128], F32)
    P.dve.memset(ap=ones.full(), constant=1.0)
    eps = P.sb("eps", [128, 1], F32)
    P.dve.memset(ap=eps.full(), constant=1e-6)
    pb = [P.ps(f"pb{i}", [128, 512], F32) for i in range(8)]
    wst = [P.sb(f"wst{i}", [128, 4, 1024], F32) for i in range(2)]
    wcnt = [0]

    def load_w(dram, kc_n, name):
        bfb = P.sb(name, [128, kc_n, D], BF16)
        v = dram.full().re("(kc p) n -> p kc n", p=128)
        for k0 in range(0, kc_n, 4):
            i = wcnt[0] % 2
            wcnt[0] += 1
            P.dma("sync" if i == 0 else "gpsimd", out=wst[i].full(), in_=v[:, k0:k0 + 4, :])
            P.pool.tensor_copy(out=bfb[:, k0:k0 + 4, :], in_=wst[i].full())
        return bfb

    Wa = load_w(w_a, 4, "Wa")
    Wb = load_w(w_b, 4, "Wb")
    Wc = load_w(w_c, 8, "Wc")
    Wo = load_w(w_o, 8, "Wo")

    ys = P.sb("ys", [128, 8, 512], F32)
    szs = P.sb("szs", [128, 8, 512], BF16)
    yn = P.sb("yn", [128, 8, 512], BF16)
    oms = P.sb("oms", [128, 4, 512], BF16)
    ofs = P.sb("ofs", [128, 4, 512], BF16)
    gs = P.sb("gs", [128, 24, 512], BF16)
    xs = P.sb("xs", [128, 8, 512], F32)
    sq = P.sb("sq", [128, 512], F32)
    rstd = P.sb("rstd", [128, 512], F32)
    m1 = [P.sb(f"m1_{i}", [128, 512], F32) for i in range(2)]
    m2 = [P.sb(f"m2_{i}", [128, 512], F32) for i in range(2)]
    m3 = [P.sb(f"m3_{i}", [128, 512], F32) for i in range(2)]
    mg = P.sb("mg", [128, 8, 512], BF16)
    xo = [P.sb(f"xo{i}", [128, 512], F32) for i in range(2)]
    ch = lambda d: d.full().re("(kc p) n -> p kc n", p=128)
    for ti in range(NT):
        ts = slice(ti * 512, (ti + 1) * 512)
        P.dma("sync", out=ys.full(), in_=ch(yT)[:, :, ts])
        P.dma("gpsimd", out=szs.full(), in_=ch(szT)[:, :, ts])
        P.dma("sync", out=oms.full(), in_=ch(omT)[:, :, ts])
        P.dma("gpsimd", out=ofs.full(), in_=ch(ofT)[:, :, ts])
        P.dma("sync", out=gs.full(), in_=ch(gT)[:, :, ts])
        P.dma("gpsimd", out=xs.full(), in_=ch(xT)[:, :, ts])
        ps = pb[7]
        for kc in range(8):
            P.dve.tensor_tensor(out=ys[:, kc, :], in0=ys[:, kc, :], in1=szs[:, kc, :], op=ALU.mult)
            P.act.activation(out=sq.full(), in_=ys[:, kc, :], func=AF.Square)
            P.pe.matmul(out=ps.full(), lhsT=ones.full(), rhs=sq.full(), start=(kc == 0), stop=(kc == 7))
        P.act.activation(out=rstd.full(), in_=ps.full(), func=AF.Sqrt, bias=eps[:, 0:1], scale=1.0 / 1024.0)
        P.dve.reciprocal(out=rstd.full(), in_=rstd.full())
        for kc in range(8):
            P.dve.scalar_tensor_tensor(out=yn[:, kc, :], in0=ys[:, kc, :], scalar=cstb[:, kc:kc + 1], in1=rstd.full(),
                                       op0=ALU.mult, op1=ALU.mult)
        for oc in range(8):
            i2 = oc % 2
            osl = slice(oc * 128, (oc + 1) * 128)
            pa, pbb, pc = pb[0 + i2 * 3], pb[1 + i2 * 3], pb[2 + i2 * 3]
            for kc in range(4):
                P.pe.matmul(out=pa.full(), lhsT=Wa[:, kc, osl], rhs=oms[:, kc, :], start=(kc == 0), stop=(kc == 3))
            for kc in range(4):
                P.pe.matmul(out=pbb.full(), lhsT=Wb[:, kc, osl], rhs=ofs[:, kc, :], start=(kc == 0), stop=(kc == 3))
            for kc in range(8):
                P.pe.matmul(out=pc.full(), lhsT=Wc[:, kc, osl], rhs=yn[:, kc, :], start=(kc == 0), stop=(kc == 7))
            P.dve.tensor_tensor(out=m1[i2].full(), in0=pa.full(), in1=gs[:, oc, :], op=ALU.mult)
            P.dve.tensor_tensor(out=m2[i2].full(), in0=pbb.full(), in1=gs[:, 8 + oc, :], op=ALU.mult)
            P.dve.tensor_tensor(out=m3[i2].full(), in0=pc.full(), in1=gs[:, 16 + oc, :], op=ALU.mult)
            P.pool.tensor_tensor(out=m1[i2].full(), in0=m1[i2].full(), in1=m2[i2].full(), op=ALU.add)
            P.pool.tensor_tensor(out=mg[:, oc, :], in0=m1[i2].full(), in1=m3[i2].full(), op=ALU.add)
        for oc in range(8):
            i2 = oc % 2
            ps = pb[6 + i2]
            for kc in range(8):
                P.pe.matmul(out=ps.full(), lhsT=Wo[:, kc, oc * 128:(oc + 1) * 128], rhs=mg[:, kc, :],
                            start=(kc == 0), stop=(kc == 7))
            P.dve.tensor_tensor(out=xo[i2].full(), in0=ps.full(), in1=xs[:, oc, :], op=ALU.add)
            P.dma("sync" if i2 else "gpsimd", out=o_x[oc * 128:(oc + 1) * 128, ts], in_=xo[i2].full())
    P.emit()
    return nc, P


def d2_colpack(inp, l):
    cp = ColPack()
    cp.add("g_ffn", inp["norm_ffn_g"][l])
    cw = inp["ffn_conv_w"][l]
    for k in range(3):
        cp.add(f"fw{k}", cw[k])
    cp.add("fb", inp["ffn_conv_b"][l])
    return cp


def build_D2(off):
    nc = new_nc()
    P = Prog(nc)
    EI, EO = "ExternalInput", "ExternalOutput"
    NT = T // 512
    TT = T + HALO
    xT = P.dram("xT", [D, TT], F32, EI)
    w_up = P.dram("w_up", [D, 5632], F32, EI)
    w_dn = P.dram("w_dn", [2816, D], F32, EI)
    cst_d = P.dram("cst", [128, off["_n"]], F32, EI)
    o_x = P.dram("o_x", [D, T], F32, EO)

    cstb = P.sb("cstb", [128, off["_n"]], F32)
    C = Cst(P, cstb, off)
    P.dma("sync", out=cstb.full(), in_=cst_d.full())
    ones = P.sb("ones", [128, 128], F32)
    P.dve.memset(ap=ones.full(), constant=1.0)
    eps = P.sb("eps", [128, 1], F32)
    P.dve.memset(ap=eps.full(), constant=1e-6)
    pb = [P.ps(f"pb{i}", [128, 512], F32) for i in range(8)]
    wst = [P.sb(f"wst{i}", [128, 1024], F32) for i in range(2)]
    wcnt = [0]
    Wu = P.sb("Wu", [128, 8, 5632], BF16)
    Wd = P.sb("Wd", [128, 22, D], BF16)
    wuv = w_up.full().re("(kc p) n -> p kc n", p=128)
    for kc in range(8):
        for c0 in range(0, 5632, 1024):
            n = min(1024, 5632 - c0)
            i = wcnt[0] % 2
            wcnt[0] += 1
            P.dma("sync" if i == 0 else "gpsimd", out=wst[i][:, 0:n], in_=wuv[:, kc, c0:c0 + n])
            P.pool.tensor_copy(out=Wu[:, kc, c0:c0 + n], in_=wst[i][:, 0:n])
    wdv = w_dn.full().re("(kc p) n -> p kc n", p=128)
    for k0 in range(22):
        i = wcnt[0] % 2
        wcnt[0] += 1
        P.dma("sync" if i == 0 else "gpsimd", out=wst[i].full(), in_=wdv[:, k0, :])
        P.pool.tensor_copy(out=Wd[:, k0, :], in_=wst[i].full())

    xst = P.sb("xst", [128, 8, 512], F32)
    hn = P.sb("hn", [128, 8, 512], BF16)
    sq = P.sb("sq", [128, 512], F32)
    rstd = P.sb("rstd", [128, 512], F32)
    act = P.sb("act", [128, 22, 512], BF16)
    upre = [P.sb(f"upre{i}", [128, 516], F32) for i in range(2)]
    acc = [P.sb(f"acc{i}", [128, 512], F32) for i in range(2)]
    sg = P.sb("sg", [128, 512], F32)
    carry = P.sb("carry", [128, 44, 4], F32)
    xo = [P.sb(f"xo{i}", [128, 512], F32) for i in range(2)]
    xTv = xT.full().re("(kc p) n -> p kc n", p=128)
    tiles = [(0, HALO)] + [(HALO + i * 512, 512) for i in range(NT)]
    pcnt = [0]
    for tix, (c0, w) in enumerate(tiles):
        P.dma("sync", out=xst[:, :, 0:w], in_=xTv[:, :, c0:c0 + w])
        ps = pb[7]
        for kc in range(8):
            P.act.activation(out=sq[:, 0:w], in_=xst[:, kc, 0:w], func=AF.Square)
            P.pe.matmul(out=ps[:, 0:w], lhsT=ones.full(), rhs=sq[:, 0:w], start=(kc == 0), stop=(kc == 7))
        P.act.activation(out=rstd[:, 0:w], in_=ps[:, 0:w], func=AF.Sqrt, bias=eps[:, 0:1], scale=1.0 / 1024.0)
        P.dve.reciprocal(out=rstd[:, 0:w], in_=rstd[:, 0:w])
        for kc in range(8):
            P.dve.scalar_tensor_tensor(out=hn[:, kc, 0:w], in0=xst[:, kc, 0:w], scalar=C.col("g_ffn", kc),
                                       in1=rstd[:, 0:w], op0=ALU.mult, op1=ALU.mult)
        for i in range(22):
            accs = []
            for j, cg in enumerate((i, 22 + i)):
                ps = pb[pcnt[0] % 4]
                pcnt[0] += 1
                for kc in range(8):
                    P.pe.matmul(out=ps[:, 0:w], lhsT=Wu[:, kc, cg * 128:(cg + 1) * 128], rhs=hn[:, kc, 0:w],
                                start=(kc == 0), stop=(kc == 7))
                if tix == 0:
                    P.act.copy(out=carry[:, cg, :], in_=ps[:, 0:HALO])
                    continue
                up = upre[j]
                P.act.copy(out=up[:, 4:516], in_=ps.full())
                P.dve.tensor_copy(out=up[:, 0:4], in_=carry[:, cg, :])
                P.pool.tensor_copy(out=carry[:, cg, :], in_=up[:, 512:516])
                a0 = acc[j]
                P.dve.tensor_scalar(out=a0.full(), in0=up[:, 4:516], scalar1=C.col("fw2", cg), scalar2=C.col("fb", cg),
                                    op0=ALU.mult, op1=ALU.add)
                P.dve.scalar_tensor_tensor(out=a0.full(), in0=up[:, 3:515], scalar=C.col("fw1", cg), in1=a0.full(),
                                           op0=ALU.mult, op1=ALU.add)
                P.dve.scalar_tensor_tensor(out=a0.full(), in0=up[:, 2:514], scalar=C.col("fw0", cg), in1=a0.full(),
                                           op0=ALU.mult, op1=ALU.add)
                accs.append(a0)
            if tix == 0:
                continue
            P.act.activation(out=sg.full(), in_=accs[0].full(), func=AF.Silu)
            P.pool.tensor_tensor(out=act[:, i, :], in0=sg.full(), in1=accs[1].full(), op=ALU.mult)
        if tix == 0:
            continue
        ti = tix - 1
        for oc in range(8):
            i2 = oc % 2
            ps = pb[4 + i2]
            for i in range(22):
                P.pe.matmul(out=ps.full(), lhsT=Wd[:, i, oc * 128:(oc + 1) * 128], rhs=act[:, i, :],
                            start=(i == 0), stop=(i == 21))
            P.dve.tensor_tensor(out=xo[i2].full(), in0=ps.full(), in1=xst[:, oc, :], op=ALU.add)
            P.dma("sync" if i2 else "gpsimd", out=o_x[oc * 128:(oc + 1) * 128, ti * 512:(ti + 1) * 512], in_=xo[i2].full())
    P.emit()
    return nc, P


def run_D1(inp, l, resA, om, of, y, x_full):
    if "D1" not in _PROG_CACHE:
        _PROG_CACHE["D1"] = build_D1()[0]
    nc = _PROG_CACHE["D1"]
    cst = np.ascontiguousarray(inp["ssm_norm_g"][l].reshape(8, 128).T)
    xf = x_full.reshape(16384, D)
    in_maps = []
    for c in range(8):
        b, q = c // 4, c % 4
        ts = slice(q * T, (q + 1) * T)
        in_maps.append({
            "omT": np.ascontiguousarray(om[b][:, ts]), "ofT": np.ascontiguousarray(of[b][:, ts]),
            "yT": np.ascontiguousarray(y[b][:, ts]), "szT": np.asarray(resA[c]["o_sz"]),
            "gT": np.asarray(resA[c]["o_g"]), "xT": np.ascontiguousarray(xf[c * T:(c + 1) * T].T),
            "w_a": np.ascontiguousarray(inp["w_br_mla"][l]), "w_b": np.ascontiguousarray(inp["w_br_fox"][l]),
            "w_c": np.ascontiguousarray(inp["w_br_ssm"][l]), "w_o": np.ascontiguousarray(inp["w_out"][l]),
            "cst": cst,
        })
    res = run_bass_kernel_spmd(nc, in_maps, core_ids=list(range(8))).results
    xm = np.concatenate([np.asarray(r["o_x"]).T for r in res], axis=0)
    return xm.reshape(2, S_, D)


def run_D2(inp, l, xm_full):
    cp = d2_colpack(inp, l)
    off = dict(cp.off)
    off["_n"] = cp.n
    if "D2" not in _PROG_CACHE:
        _PROG_CACHE["D2"] = build_D2(off)[0]
    nc = _PROG_CACHE["D2"]
    cst = cp.array()
    xf = xm_full.reshape(16384, D)
    in_maps = []
    for c in range(8):
        t0 = c * T
        xt = np.zeros((D, T + HALO), np.float32)
        xt[:, HALO:] = xf[t0:t0 + T].T
        if c % 4 != 0:
            xt[:, 0:HALO] = xf[t0 - HALO:t0].T
        in_maps.append({"xT": np.ascontiguousarray(xt), "w_up": np.ascontiguousarray(inp["ffn_w_up"][l]),
                        "w_dn": np.ascontiguousarray(inp["ffn_w_down"][l]), "cst": cst})
    res = run_bass_kernel_spmd(nc, in_maps, core_ids=list(range(8))).results
    xo = np.concatenate([np.asarray(r["o_x"]).T for r in res], axis=0)
    return xo.reshape(2, S_, D)


def kernel_unfused(**inp):
    inp = {k: np.asarray(v) for k, v in inp.items()}
    x = inp["x"].astype(np.float32)
    pos = inp["positions"]
    for l in range(2):
        resA = run_A(inp, l, x, pos)
        om, of, y = run_BC(inp, l, resA)
        xm = run_D1(inp, l, resA, om, of, y, x)
        x = run_D2(inp, l, xm)
    return np.ascontiguousarray(x.astype(np.float32))


def kernel(**inp):
    return kernel_fused(**inp)


SW = 516
RG = [[0, 1, 2, 3], [4, 5, 6, 7]]
KT_L = T // 128


def fused_rowpack(inp, l):
    r = np.concatenate([inp["fox_b_f"][l], inp["ssm_dt_bias"][l], inp["ssm_A_log"][l], inp["ssm_D"][l]]).astype(np.float32)
    return np.ascontiguousarray(np.broadcast_to(r[None, :], (128, r.size)))


def build_fused(offA, offD2, stop=None, dbg=()):
    nc = new_nc()
    P = Prog(nc)
    EI, EO = "ExternalInput", "ExternalOutput"
    L = 2
    x0 = P.dram("x0", [D, 4 * SW], F32, EI)
    pos = P.dram("pos", [1, T], I32, EI)
    w_in = P.dram("w_in", [L, D, 7864], F32, EI)
    w_uq = P.dram("w_uq", [L, 384, 768], F32, EI)
    w_kp = P.dram("w_kp", [L, 256, 768], F32, EI)
    w_v = P.dram("w_v", [L, 256, 512], F32, EI)
    w_a = P.dram("w_a", [L, 512, D], F32, EI)
    w_b = P.dram("w_b", [L, 512, D], F32, EI)
    w_c = P.dram("w_c", [L, 1024, D], F32, EI)
    w_o = P.dram("w_o", [L, 1024, D], F32, EI)
    w_up = P.dram("w_up", [L, D, 5632], F32, EI)
    w_dn = P.dram("w_dn", [L, 2816, D], F32, EI)
    cstA_d = P.dram("cstA", [L, 128, offA["_n"]], F32, EI)
    cstD_d = P.dram("cstD", [L, 128, offD2["_n"]], F32, EI)
    gssm_d = P.dram("gssm", [L, 128, 8], F32, EI)
    rowc_d = P.dram("rowc", [L, 128, 56], F32, EI)
    sel_d = P.dram("sel", [128, 32], F32, EI)
    msk_d = P.dram("msk", [128, 8, 512], F32, EI)
    cm_d = P.dram("cm", [128, 4, 128], F32, EI)
    mats_d = P.dram("mats", [128, 192], F32, EI)
    out = P.dram("out", [D, T], F32, EO)
    xb = [x0, P.dram("xb1", [D, 4 * SW], F32)]
    xmid = P.dram("xmid", [D, 4 * SW], F32)
    qm = P.dram("qm", [8, 96, T], BF16)
    qf = P.dram("qf", [8, 64, T], BF16)
    fq = P.dram("fq", [8, 3, T], BF16)
    szd = P.dram("szd", [1024, T], BF16)
    gd = P.dram("gd", [3072, T], BF16)
    xtm = P.dram("xtm", [128, KT_L, 1024], BF16)
    btm = P.dram("btm", [128, KT_L, 256], BF16)
    bct = P.dram("bct", [512, T], BF16)
    dtd = P.dram("dtd", [128, KT_L, 16], F32)
    atd = P.dram("atd", [128, KT_L, 16], F32)
    omd = P.dram("omd", [512, T], BF16)
    ofd = P.dram("ofd", [512, T], BF16)
    yd = P.dram("yd", [1024, T], F32)
    kxm = [P.dram(f"kxm{m}", [768, 512], BF16) for m in range(4)]
    kxmg = [P.dram(f"kxmg{m}", [4 * 768, 512], BF16) for m in range(4)]
    kxf = [P.dram(f"kxf{m}", [512, 512], BF16) for m in range(4)]
    kxfg = [P.dram(f"kxfg{m}", [4 * 512, 512], BF16) for m in range(4)]
    vx = [P.dram(f"vx{m}", [2048, 256], BF16) for m in range(4)]
    vxg = [P.dram(f"vxg{m}", [4 * 2048, 256], BF16) for m in range(4)]
    sx = [P.dram(f"sx{i}", [256, 1024], F32) for i in range(2)]
    sxg = [P.dram(f"sxg{i}", [4 * 256, 1024], F32) for i in range(2)]
    fx = P.dram("fx", [128, 224], F32)
    fxg = P.dram("fxg", [4 * 128, 224], F32)
    tx = P.dram("tx", [128, 128], F32)
    txg = P.dram("txg", [4 * 128, 128], F32)
    wub = P.dram("wub", [D, 5632], BF16)
    wdb = P.dram("wdb", [2816, D], BF16)
    wab = P.dram("wab", [512, D], BF16)
    wbb = P.dram("wbb", [512, D], BF16)
    wcb = P.dram("wcb", [1024, D], BF16)
    wob = P.dram("wob", [1024, D], BF16)
    dbg_out = {}

    def gather_pairs(pairs):
        for (a, b) in pairs:
            P.pool.collective_compute(kind="AllGather", op=ALU.bypass, replica_groups=RG,
                                      ins=[a.full().re("(p a) c -> p (a c)", p=128)],
                                      outs=[b.full().re("(q a) c -> q (a c)", q=512)])

    def load_consts():
        d = {}
        d["cm"] = P.sb("cmb", [128, 4, 128], F32)
        P.dma("sync", out=d["cm"].full(), in_=cm_d.full())
        d["sel"] = P.sb("selb", [128, 32], F32)
        P.dma("sync", out=d["sel"].full(), in_=sel_d.full())
        d["eps"] = P.sb("eps", [128, 1], F32)
        P.dve.memset(ap=d["eps"].full(), constant=1e-6)
        d["one1"] = P.sb("one1", [128, 1], F32)
        P.dve.memset(ap=d["one1"].full(), constant=1.0)
        d["zero"] = P.sb("zero", [128, 1], F32)
        P.dve.memset(ap=d["zero"].full(), constant=0.0)
        return d

    def phase_A(l):
        K = load_consts()
        cmb = K["cm"]
        tri, ident, ones = cmb[:, 0, :], cmb[:, 2, :], cmb[:, 3, :]
        eps, one1 = K["eps"], K["one1"]
        xin = xb[l]
        cstb = P.sb("cstb", [128, offA["_n"]], F32)
        C = Cst(P, cstb, offA)
        P.dma("sync", out=cstb.full(), in_=cstA_d[l])
        rowc = P.sb("rowc", [128, 56], F32)
        P.dma("sync", out=rowc.full(), in_=rowc_d[l])
        matf = P.sb("matf", [128, 192], F32)
        matb = P.sb("matb", [128, 192], BF16)
        P.dma("sync", out=matf.full(), in_=mats_d.full())
        P.dve.tensor_copy(out=matb.full(), in_=matf.full())
        prh = matb[0:96, 0:96]
        selm = matb[0:32, 96:192]
        identb = P.sb("identb", [128, 128], BF16)
        P.dve.tensor_copy(out=identb.full(), in_=ident)
        Aneg_r = P.sb("Aneg_r", [128, 16], F32)
        P.act.activation(out=Aneg_r.full(), in_=rowc[:, 24:40], func=AF.Exp)
        P.dve.tensor_scalar(out=Aneg_r.full(), in0=Aneg_r.full(), scalar1=-1.0, scalar2=None, op0=ALU.mult)

        pb = [P.ps(f"pb{i}", [128, 512], F32) for i in range(7)]
        pbt = P.ps("pbt", [128, 1024], BF16)
        pbi = {}

        def nxt_ps(lo=0, hi=4):
            i = pbi.get(lo, 0)
            pbi[lo] = (i + 1) % (hi - lo)
            return pb[lo + i]

        Ctab = P.sb("Ctab", [96, T], F32)
        Stab = P.sb("Stab", [96, T], F32)
        hraw = P.sb("hraw", [96, 512], F32)
        hsq = P.sb("hsq", [96, 512], F32)
        hrs = P.sb("hrs", [96, 512], F32)
        hnf = P.sb("hnf", [96, 512], F32)
        hnb = P.sb("hnb", [96, 512], BF16)
        ht1 = P.sb("ht1", [96, 512], F32)
        ht2 = P.sb("ht2", [96, 512], F32)
        posf, rr_tmp, rr_m = hrs, hraw, hsq

        class _IV:
            def __init__(self, b):
                self.b = b

            def full(self):
                return self.b.full().bitcast(I32)
        posi, rr_i = _IV(ht1), _IV(ht2)

        def sin_table(outv, phase):
            P.dve.tensor_scalar(out=rr_tmp.full(), in0=posf.full(), scalar1=C.col("invf"), scalar2=phase,
                                op0=ALU.mult, op1=ALU.add)
            P.dve.tensor_scalar(out=rr_m.full(), in0=rr_tmp.full(), scalar1=1.0 / (2 * np.pi), scalar2=None, op0=ALU.mult)
            P.dve.tensor_copy(out=rr_i.full(), in_=rr_m.full())
            P.dve.tensor_copy(out=rr_m.full(), in_=rr_i.full())
            P.dve.scalar_tensor_tensor(out=rr_tmp.full(), in0=rr_m.full(), scalar=-2 * np.pi, in1=rr_tmp.full(),
                                       op0=ALU.mult, op1=ALU.add)
            P.dve.tensor_scalar(out=rr_m.full(), in0=rr_tmp.full(), scalar1=np.pi, scalar2=-2 * np.pi, op0=ALU.is_gt, op1=ALU.mult)
            P.dve.tensor_tensor(out=rr_tmp.full(), in0=rr_tmp.full(), in1=rr_m.full(), op=ALU.add)
            P.dve.tensor_scalar(out=rr_m.full(), in0=rr_tmp.full(), scalar1=-np.pi, scalar2=2 * np.pi, op0=ALU.is_lt, op1=ALU.mult)
            P.dve.tensor_tensor(out=rr_tmp.full(), in0=rr_tmp.full(), in1=rr_m.full(), op=ALU.add)
            P.act.activation(out=outv, in_=rr_tmp.full(), func=AF.Sin)

        for i in range(4):
            P.dma("sync", out=posi.full(), in_=pos[:, i * 512:(i + 1) * 512].f(lambda a: a.partition_broadcast(96)))
            P.dve.tensor_copy(out=posf.full(), in_=posi.full())
            sin_table(Stab[:, i * 512:(i + 1) * 512], 0.0)
            sin_table(Ctab[:, i * 512:(i + 1) * 512], np.pi / 2)
        P.dve.memset(ap=Stab[0:64, :], constant=0.0)
        P.dve.memset(ap=Ctab[0:64, :], constant=1.0)

        hn = P.sb("hn", [128, 8, 4 * SW], BF16)
        xst = P.sb("xst", [128, 8, 512], F32)
        sq = P.sb("sq", [128, 512], F32)
        rstd = P.sb("rstd", [128, 512], F32)
        xTv = xin.full().re("(kc p) n -> p kc n", p=128)

        def rstd_from(ps_view, n_feat, rows, rstd_view):
            P.act.activation(out=rstd_view, in_=ps_view, func=AF.Ln, bias=eps[0:rows, 0:1], scale=1.0 / n_feat)
            P.act.activation(out=rstd_view, in_=rstd_view, func=AF.Exp, scale=-0.5)

        halos = [(m * SW, 4) for m in range(4)]
        main = [(m * SW + 4, 512) for m in range(4)]
        for (c0, w) in halos + main:
            P.dma("sync", out=xst[:, :, 0:w], in_=xTv[:, :, c0:c0 + w])
            ps = nxt_ps(4, 6)
            for kc in range(8):
                P.act.activation(out=sq[:, 0:w], in_=xst[:, kc, 0:w], func=AF.Square)
                P.pe.matmul(out=ps[:, 0:w], lhsT=ones, rhs=sq[:, 0:w], start=(kc == 0), stop=(kc == 7))
            rstd_from(ps[:, 0:w], 1024.0, 128, rstd[:, 0:w])
            for kc in range(8):
                P.dve.scalar_tensor_tensor(out=hn[:, kc, c0:c0 + w], in0=xst[:, kc, 0:w], scalar=C.col("g_mix", kc),
                                           in1=rstd[:, 0:w], op0=ALU.mult, op1=ALU.mult)

        wst = [P.sb(f"wst{i}", [128, 8, 256], F32) for i in range(2)]
        wbf = [P.sb(f"wbf{i}", [128, 8, 512], BF16) for i in range(2)]
        wcnt = [0]
        scnt = [0]
        w_inv = w_in[l].re("(kc p) n -> p kc n", p=128)

        SBv = 672 + 1544
        wplan = [(0, 384), (384, 288), (672, 512), (672 + 512, 512), (672 + 1024, 512), (672 + 1536, 8),
                 (SBv + 1024 + 1536, 16), (SBv, 512), (SBv + 512, 512)]
        wplan += [(SBv + 1024 + b_ * 512, 512) for b_ in range(3)]
        wplan += [(SBv + 2576 + b_ * 512, 512) for b_ in range(6)]
        wpend = {}
        cpend = []

        def w_issue(g):
            c0, ncols = wplan[g]
            lst = []
            for h0 in range(0, ncols, 256):
                n = min(256, ncols - h0)
                si = scnt[0] % 2
                scnt[0] += 1
                P.dma("sync", out=wst[si][:, :, 0:n], in_=w_inv[:, :, c0 + h0:c0 + h0 + n])
                lst.append((si, h0, n))
            wpend[g] = lst

        def load_w(c0, ncols):
            g = wcnt[0]
            wcnt[0] += 1
            assert wplan[g] == (c0, ncols), (g, wplan[g], c0, ncols)
            i = g % 2
            if g not in wpend:
                w_issue(g)
            lst = wpend.pop(g)
            for (si, h0, n) in lst:
                P.act.copy(out=wbf[i][:, :, h0:h0 + n], in_=wst[si][:, :, 0:n])
            if g + 1 < len(wplan):
                w_issue(g + 1)
            if cpend:
                gather_pairs([cpend.pop(0)])
            return wbf[i]

        def proj(wb, wc0, mcols, c0, w, ps_view):
            for kc in range(8):
                P.pe.matmul(out=ps_view, lhsT=wb[:, kc, wc0:wc0 + mcols], rhs=hn[:, kc, c0:c0 + w],
                            start=(kc == 0), stop=(kc == 7))

        def proj_tm(wb, wc0, ncols, tok0, ps_view):
            for kc in range(8):
                P.pe.matmul(out=ps_view, lhsT=hn[:, kc, tok0:tok0 + 128], rhs=wb[:, kc, wc0:wc0 + ncols],
                            start=(kc == 0), stop=(kc == 7))

        ostg_cnt = [0]
        ostg = [P.sb(f"ostg{i}", [128, 512], BF16) for i in range(4)]

        def next_ostg():
            i = ostg_cnt[0] % 4
            ostg_cnt[0] += 1
            return ostg[i]

        def out_dma(dst_view, src_view):
            P.dma("sync" if ostg_cnt[0] % 2 else "scalar", out=dst_view, in_=src_view)

        hsets = [dict(hraw=hraw.full(), hsq=hsq.full(), hrs=hrs.full(), hnf=hnf.full(), hnb=hnb.full(),
                      ht1=ht1.full(), ht2=ht2.full())]
        hnb1 = P.sb("hnb1", [96, 512], BF16)
        hsets.append(dict(hraw=xst[0:96, 0, :].k(0), hsq=xst[0:96, 1, :].k(1), hrs=xst[0:96, 2, :].k(2),
                          hnf=xst[0:96, 3, :].k(3), hnb=hnb1.full(), ht1=xst[0:96, 4, :].k(4), ht2=xst[0:96, 5, :].k(5)))
        hb2 = P.sb("hb2", [96, 6, 512], F32)
        hnb2 = P.sb("hnb2", [96, 512], BF16)
        hsets.append(dict(hraw=hb2[:, 0, :].k(0), hsq=hb2[:, 1, :].k(1), hrs=hb2[:, 2, :].k(2),
                          hnf=hb2[:, 3, :].k(3), hnb=hnb2.full(), ht1=hb2[:, 4, :].k(4), ht2=hb2[:, 5, :].k(5)))
        hcnt = [0]

        def headnorm(projfn, d, gain_col, rope, tok0, dst_view):
            H = hsets[hcnt[0] % 3]
            hcnt[0] += 1
            ps_view = projfn()
            P.act.activation(out=H["hsq"][0:d, :], in_=ps_view, func=AF.Square)
            P.act.copy(out=H["hraw"][0:d, :], in_=ps_view)
            yield
            ps2 = nxt_ps(4, 6)
            P.pe.matmul(out=ps2[0:d, :], lhsT=cmb[0:d, 3, 0:d], rhs=H["hsq"][0:d, :], start=True, stop=True)
            rstd_from(ps2[0:d, :], float(d), d, H["hrs"][0:d, :])
            og = next_ostg()
            if not rope:
                P.dve.scalar_tensor_tensor(out=og[0:d, :], in0=H["hraw"][0:d, :], scalar=gain_col, in1=H["hrs"][0:d, :],
                                           op0=ALU.mult, op1=ALU.mult)
            else:
                P.dve.scalar_tensor_tensor(out=H["hnf"][0:d, :], in0=H["hraw"][0:d, :], scalar=gain_col, in1=H["hrs"][0:d, :],
                                           op0=ALU.mult, op1=ALU.mult)
                P.act.copy(out=H["hnb"][0:d, :], in_=H["hnf"][0:d, :])
                yield
                ps3 = nxt_ps(6, 7)
                P.pe.matmul(out=ps3[0:d, :], lhsT=prh, rhs=H["hnb"][0:d, :], start=True, stop=True)
                P.dve.tensor_tensor(out=H["ht1"][0:d, :], in0=H["hnf"][0:d, :], in1=Ctab[0:d, tok0:tok0 + 512], op=ALU.mult)
                P.dve.tensor_tensor(out=H["ht2"][0:d, :], in0=ps3[0:d, :], in1=Stab[0:d, tok0:tok0 + 512], op=ALU.mult)
                P.pool.tensor_tensor(out=og[0:d, :], in0=H["ht1"][0:d, :], in1=H["ht2"][0:d, :], op=ALU.add)
            out_dma(dst_view, og[0:d, :])

        def run_pipe(gens, depth=3):
            gens = iter(gens)
            active = []
            while True:
                started = False
                if len(active) < depth:
                    g = next(gens, None)
                    if g is not None:
                        started = True
                        try:
                            next(g)
                            active.append(g)
                        except StopIteration:
                            pass
                if not active and not started:
                    break
                olds = active[:-1] if (started and active) else list(active)
                for g in olds:
                    try:
                        next(g)
                    except StopIteration:
                        active.remove(g)

        lat = P.sb("lat", [128, 3, 512], F32)
        latn = P.sb("latn", [128, 3, 512], BF16)

        def latent_norm(ps_list, gname):
            nch = len(ps_list)
            ps2 = nxt_ps(4, 6)
            for i, psv in enumerate(ps_list):
                P.act.activation(out=sq.full(), in_=psv, func=AF.Square)
                P.act.copy(out=lat[:, i, :], in_=psv)
                P.pe.matmul(out=ps2.full(), lhsT=ones, rhs=sq.full(), start=(i == 0), stop=(i == nch - 1))
            rstd_from(ps2.full(), 128.0 * nch, 128, rstd.full())
            for i in range(nch):
                P.dve.scalar_tensor_tensor(out=latn[:, i, :], in0=lat[:, i, :], scalar=C.col(gname, i), in1=rstd.full(),
                                           op0=ALU.mult, op1=ALU.mult)

        def small_w(name, dram_l, kc_n, ncols, i):
            bfb = P.sb(name, [128, kc_n, ncols], BF16)
            dv = dram_l.re("(kc p) n -> p kc n", p=128)
            for kc in range(kc_n):
                si = scnt[0] % 2
                scnt[0] += 1
                stg = wst[si].full().re("p a b -> p (a b)")[:, 0:ncols]
                P.dma("sync", out=stg, in_=dv[:, kc, :])
                P.act.copy(out=bfb[:, kc, :], in_=stg)
            return bfb

        uqb = small_w("uqb", w_uq[l], 3, 768, 0)
        kpb = small_w("kpb", w_kp[l], 2, 768, 1)
        wvb = small_w("wvb", w_v[l], 2, 512, 0)

        vstg = [P.sb(f"vstg{i}", [128, 512], BF16) for i in range(2)]
        vcnt = [0]

        def v_out(kind, ktl, ps_view):
            vs = vstg[vcnt[0] % 2]
            vcnt[0] += 1
            P.act.copy(out=vs.full(), in_=ps_view)
            P.dma("sync" if vcnt[0] % 2 else "scalar",
                  out=vx[ktl // 4][kind * 1024:(kind + 1) * 1024, (ktl % 4) * 64:(ktl % 4 + 1) * 64].re("(h p) d -> p h d", p=128),
                  in_=vs.full().re("p (h d) -> p h d", h=8))

        wb = load_w(0, 384)
        for m, (c0, w) in enumerate(main):
            pss = []
            for ch in range(3):
                ps = nxt_ps(0, 4)
                proj(wb, ch * 128, 128, c0, 512, ps.full())
                pss.append(ps.full())
            latent_norm(pss, "g_cq")
            def mkq(h):
                def f():
                    ps = nxt_ps(0, 4)
                    for kc in range(3):
                        P.pe.matmul(out=ps[0:96, :], lhsT=uqb[:, kc, h * 96:(h + 1) * 96], rhs=latn[:, kc, :],
                                    start=(kc == 0), stop=(kc == 2))
                    return ps[0:96, :]
                return f
            run_pipe(headnorm(mkq(h), 96, C.col("g_q"), True, m * 512, qm[h, :, m * 512:(m + 1) * 512]) for h in range(8))
        wb = load_w(384, 288)
        krb = P.sb("krb", [32, 512], BF16)
        for m, (c0, w) in enumerate(main):
            pss = []
            for ch in range(2):
                ps = nxt_ps(0, 4)
                proj(wb, ch * 128, 128, c0, 512, ps.full())
                pss.append(ps.full())
            ps = nxt_ps(0, 4)
            proj(wb, 256, 32, c0, 512, ps[0:32, :])
            P.act.copy(out=krb.full(), in_=ps[0:32, :])
            latent_norm(pss, "g_ckv")
            def mkk(h):
                def f():
                    ps = nxt_ps(0, 4)
                    for kc in range(2):
                        P.pe.matmul(out=ps[0:96, :], lhsT=kpb[:, kc, h * 96:(h + 1) * 96], rhs=latn[:, kc, :],
                                    start=(kc == 0), stop=False)
                    P.pe.matmul(out=ps[0:96, :], lhsT=selm, rhs=krb.full(), start=False, stop=True)
                    return ps[0:96, :]
                return f
            run_pipe(headnorm(mkk(h), 96, C.col("g_k"), True, m * 512, kxm[m][h * 96:(h + 1) * 96, :]) for h in range(8))
            for j in range(4):
                ps = nxt_ps(0, 4)
                for kc in range(2):
                    P.pe.matmul(out=ps.full(), lhsT=latn[:, kc, j * 128:(j + 1) * 128], rhs=wvb[:, kc, :],
                                start=(kc == 0), stop=(kc == 1))
                v_out(0, m * 4 + j, ps.full())
        for (base, gname, isq) in ((672, "g_fq", True), (672 + 512, "g_fk", False)):
            wb = load_w(base, 512)
            def mkf(wb_, h, c0):
                def f():
                    ps = nxt_ps(0, 4)
                    proj(wb_, h * 64, 64, c0, 512, ps[0:64, :])
                    return ps[0:64, :]
                return f
            gl = []
            for m, (c0, w) in enumerate(main):
                for h in range(8):
                    dst = qf[h, :, m * 512:(m + 1) * 512] if isq else kxf[m][h * 64:(h + 1) * 64, :]
                    gl.append(headnorm(mkf(wb, h, c0), 64, C.col(gname), False, m * 512, dst))
            run_pipe(gl)
        wb = load_w(672 + 1024, 512)
        for m, (c0, w) in enumerate(main):
            for j in range(4):
                ps = nxt_ps(0, 4)
                proj_tm(wb, 0, 512, c0 + j * 128, ps.full())
                v_out(1, m * 4 + j, ps.full())
        cpend.extend(list(zip(kxm, kxmg)) + list(zip(vx, vxg)) + list(zip(kxf, kxfg)))
        FB = 672 + 1536
        SB = 672 + 1544
        lf_tm = P.sb("lf_tm", [128, KT_L, 8], F32)
        dt_tm = P.sb("dt_tm", [128, KT_L, 16], F32)
        a_tm = P.sb("a_tm", [128, KT_L, 16], F32)
        tmpr = P.sb("tmpr", [128, 16], F32)
        wf = load_w(FB, 8)
        for m, (c0, w) in enumerate(main):
            for j in range(4):
                kt = m * 4 + j
                ps = nxt_ps(0, 4)
                proj_tm(wf, 0, 8, c0 + j * 128, ps[:, 0:8])
                P.dve.tensor_tensor(out=tmpr[:, 0:8], in0=ps[:, 0:8], in1=rowc[:, 0:8], op=ALU.add)
                P.act.activation(out=tmpr[:, 0:8], in_=tmpr[:, 0:8], func=AF.Exp, scale=-1.0)
                P.act.activation(out=tmpr[:, 0:8], in_=tmpr[:, 0:8], func=AF.Ln, bias=one1[:, 0:1], scale=1.0)
                P.dve.tensor_scalar(out=lf_tm[:, kt, :], in0=tmpr[:, 0:8], scalar1=-1.0, scalar2=None, op0=ALU.mult)
        wd = load_w(SB + 1024 + 1536, 16)
        for m, (c0, w) in enumerate(main):
            for j in range(4):
                kt = m * 4 + j
                ps = nxt_ps(0, 4)
                proj_tm(wd, 0, 16, c0 + j * 128, ps[:, 0:16])
                P.dve.tensor_tensor(out=tmpr.full(), in0=ps[:, 0:16], in1=rowc[:, 8:24], op=ALU.add)
                P.act.activation(out=tmpr.full(), in_=tmpr.full(), func=AF.Exp)
                P.act.activation(out=dt_tm[:, kt, :], in_=tmpr.full(), func=AF.Ln, bias=one1[:, 0:1], scale=1.0)
                P.dve.tensor_tensor(out=a_tm[:, kt, :], in0=dt_tm[:, kt, :], in1=Aneg_r.full(), op=ALU.mult)
        P.dma("sync", out=dtd.full(), in_=dt_tm.full())
        P.dma("sync", out=atd.full(), in_=a_tm.full())
        def plain_group(base, ncols, func, bias_name, dst, dst_row0):
            wb_ = load_w(base, ncols)
            for m, (c0, w) in enumerate(main):
                for ch in range(ncols // 128):
                    ps = nxt_ps(0, 4)
                    proj(wb_, ch * 128, 128, c0, 512, ps.full())
                    og = next_ostg()
                    if bias_name is None:
                        P.act.activation(out=og.full(), in_=ps.full(), func=func)
                    else:
                        P.act.activation(out=og.full(), in_=ps.full(), func=func,
                                         bias=C.col(bias_name, (dst_row0 // 128) + ch))
                    out_dma(dst[dst_row0 + ch * 128:dst_row0 + (ch + 1) * 128, m * 512:(m + 1) * 512], og.full())

        for blk in range(2):
            plain_group(SB + blk * 512, 512, AF.Silu, None, szd, blk * 512)
        upre = P.sb("upre", [128, 516], F32)
        carry = P.sb("carry", [128, 4], F32)
        acc0 = P.sb("acc0", [128, 512], F32)
        tstg = [P.sb(f"tstg{i}", [128, 4, 128], BF16) for i in range(2)]
        tcnt = [0]
        trq = []
        for blk in range(3):
            wb = load_w(SB + 1024 + blk * 512, 512)
            for m, (c0, w) in enumerate(main):
                for ch in range(4):
                    cg = blk * 4 + ch
                    ps = nxt_ps(0, 4)
                    proj(wb, ch * 128, 128, c0 - 4, 4, ps[:, 0:4])
                    P.act.copy(out=upre[:, 0:4], in_=ps[:, 0:4])
                    ps = nxt_ps(0, 4)
                    proj(wb, ch * 128, 128, c0, 512, ps.full())
                    while len(trq) > 1:
                        trq.pop(0)()
                    P.act.copy(out=upre[:, 4:516], in_=ps.full())
                    P.act.activation(out=acc0.full(), in_=ps.full(), func=AF.Identity, scale=C.col("cw3", cg), bias=C.col("cb", cg))
                    for k in range(3):
                        P.dve.scalar_tensor_tensor(out=acc0.full(), in0=upre[:, 1 + k:513 + k], scalar=C.col(f"cw{k}", cg),
                                                   in1=acc0.full(), op0=ALU.mult, op1=ALU.add)
                    og = next_ostg()
                    P.act.activation(out=og.full(), in_=acc0.full(), func=AF.Silu)
                    if cg >= 8:
                        out_dma(bct[(cg - 8) * 128:(cg - 7) * 128, m * 512:(m + 1) * 512], og.full())
                    if cg < 10:
                        def mk_tr(og=og, cg=cg, m=m):
                            def f():
                                i2 = tcnt[0] % 2
                                tcnt[0] += 1
                                for j in range(4):
                                    P.pe.transpose(out=pbt[:, i2 * 512 + j * 128:i2 * 512 + (j + 1) * 128],
                                                   in_=og[:, j * 128:(j + 1) * 128], identity=identb.full())
                                ts_ = tstg[i2]
                                P.dve.tensor_copy(out=ts_.full().re("p j f -> p (j f)"), in_=pbt[:, i2 * 512:(i2 + 1) * 512])
                                if cg < 8:
                                    P.dma("sync", out=xtm[:, m * 4:(m + 1) * 4, cg * 128:(cg + 1) * 128], in_=ts_.full())
                                else:
                                    P.dma("sync", out=btm[:, m * 4:(m + 1) * 4, (cg - 8) * 128:(cg - 7) * 128], in_=ts_.full())
                            return f
                        trq.append(mk_tr())
        while trq:
            trq.pop(0)()
        GB = SB + 2576
        for blk in range(6):
            plain_group(GB + blk * 512, 512, AF.Sigmoid, "b_gate", gd, blk * 512)
        fxs = P.sb("fxs", [128, 224], F32)
        within = P.sb("within", [128, KT_L, 8], F32)
        ttot = P.sb("ttot", [128, KT_L, 8], F32)
        f2 = lambda b: b.full().re("p a b -> p (a b)")
        ps = nxt_ps(0, 4)
        P.pe.matmul(out=ps[:, 0:128], lhsT=tri, rhs=f2(lf_tm), start=True, stop=True)
        P.act.copy(out=f2(within), in_=ps[:, 0:128])
        ps = nxt_ps(0, 4)
        P.pe.matmul(out=ps[:, 0:128], lhsT=ones, rhs=f2(lf_tm), start=True, stop=True)
        P.act.copy(out=f2(ttot), in_=ps[:, 0:128])
        Floc = fxs[:, 0:128].re("p (a b) -> p a b", b=8)
        totv = fxs[:, 128:160].re("p (a b) -> p a b", b=8)
        cacc = P.sb("cacc", [128, 8], F32)
        for m in range(4):
            P.dve.tensor_copy(out=Floc[:, 4 * m, :], in_=within[:, 4 * m, :])
            P.dve.tensor_copy(out=cacc.full(), in_=ttot[:, 4 * m, :])
            for j in range(1, 4):
                P.dve.tensor_tensor(out=Floc[:, 4 * m + j, :], in0=within[:, 4 * m + j, :], in1=cacc.full(), op=ALU.add)
                P.dve.tensor_tensor(out=cacc.full(), in0=cacc.full(), in1=ttot[:, 4 * m + j, :], op=ALU.add)
            P.dve.tensor_copy(out=totv[:, m, :], in_=cacc.full())
        ps = nxt_ps(0, 4)
        P.pe.transpose(out=ps[:, 0:128], in_=fxs[:, 0:128], identity=ident)
        FT = P.sb("FT", [128, 128], F32)
        r1 = P.sb("r1", [128, 128], F32)
        fh = [P.sb(f"fh{i}", [128, 128], BF16) for i in range(3)]
        P.act.copy(out=FT.full(), in_=ps[:, 0:128])
        P.dve.tensor_copy(out=fh[0].full(), in_=FT.full())
        P.dve.tensor_tensor(out=r1.full(), in0=FT.full(), in1=fh[0].full(), op=ALU.subtract)
        P.dve.tensor_copy(out=fh[1].full(), in_=r1.full())
        P.dve.tensor_tensor(out=r1.full(), in0=r1.full(), in1=fh[1].full(), op=ALU.subtract)
        P.dve.tensor_copy(out=fh[2].full(), in_=r1.full())
        for r in range(3):
            for kt in range(KT_L):
                P.dma("sync" if kt % 2 else "scalar", out=fq[:, r, kt * 128:(kt + 1) * 128], in_=fh[r][kt * 8:(kt + 1) * 8, :])
        P.dma("sync", out=fx[:, 0:160], in_=fxs[:, 0:160])
        gather_pairs(cpend)
        del cpend[:]

    def ssd_scan(l, K, pass1, fxs=None, dt_tm=None, a_tm=None, Hinit=None, rowc=None, pb=None):
        cmb = K["cm"]
        tri, trimask, ones = cmb[:, 0, :], cmb[:, 1, :], cmb[:, 3, :]
        if pb is None:
            pb = [P.ps(f"spb{i}", [128, 512], F32) for i in range(7)]
        if pass1:
            fxs = P.sb("decs", [128, 224], F32)
        if dt_tm is None:
            dt_tm = P.sb("dt_tm", [128, KT_L, 16], F32)
            a_tm = P.sb("a_tm", [128, KT_L, 16], F32)
            P.dma("sync", out=dt_tm.full(), in_=dtd.full())
            P.dma("sync", out=a_tm.full(), in_=atd.full())
        fl = lambda b: b.full().re("p c h -> p (c h)")
        Acum = P.sb("Acum", [128, KT_L, 16], F32)
        Atot = P.sb("Atot", [128, KT_L, 16], F32)
        wdec = P.sb("wdec", [128, KT_L, 16], F32)
        eAtot = P.sb("eAtot", [128, KT_L, 16], F32)
        psA = pb[0]
        P.pe.matmul(out=psA[:, 0:256], lhsT=tri, rhs=fl(a_tm), start=True, stop=True)
        P.act.copy(out=fl(Acum), in_=psA[:, 0:256])
        P.pe.matmul(out=psA[:, 256:512], lhsT=ones, rhs=fl(a_tm), start=True, stop=True)
        P.act.copy(out=fl(Atot), in_=psA[:, 256:512])
        P.act.activation(out=fl(eAtot), in_=fl(Atot), func=AF.Exp)
        P.dve.tensor_tensor(out=fl(wdec), in0=fl(Atot), in1=fl(Acum), op=ALU.subtract)
        P.act.activation(out=fl(wdec), in_=fl(wdec), func=AF.Exp)
        if not pass1:
            nAcum = P.sb("nAcum", [128, KT_L, 16], F32)
            eA = P.sb("eA", [128, KT_L, 16], F32)
            P.dve.tensor_scalar(out=fl(nAcum), in0=fl(Acum), scalar1=-1.0, scalar2=None, op0=ALU.mult)
            P.act.activation(out=fl(eA), in_=fl(Acum), func=AF.Exp)
            BCs = P.sb("BCs", [128, 4, T], BF16)
            P.dma("gpsimd", out=BCs.full(), in_=bct.full().re("(a p) t -> p a t", p=128))
            cb = P.sb("cb", [128, 2, 128], F32)
            NH = 4
            at = [P.sb(f"at{i}", [128, 128], F32) for i in range(NH)]
            tm = [P.sb(f"tm{i}", [128, 128], F32) for i in range(NH)]
            dec = [P.sb(f"dec{i}", [128, 128], F32) for i in range(NH)]
            MT = [P.sb(f"MT{i}", [128, 128], BF16) for i in range(NH)]
            t1 = P.sb("t1", [128, 1024], F32)
            t3 = P.sb("t3", [128, 1024], F32)
            yo = P.sb("yo", [128, 1024], BF16)
            yT = [P.sb(f"yT{i}", [128, 4, 128], F32) for i in range(2)]
        Hs = P.sb("Hs", [128, 1024], F32)
        Hb = P.sb("Hb", [128, 1024], BF16)
        xc = [P.sb(f"xc{i}", [128, 1024], BF16) for i in range(2)]
        Bc = [P.sb(f"Bc{i}", [128, 256], BF16) for i in range(2)]
        xdt = P.sb("xdt", [128, 1024], BF16)
        xdts = P.sb("xdts", [128, 1024], BF16)
        dsum = P.sb("dsum", [128, 16], F32)
        v3 = lambda v: v.re("p (h d) -> p h d", h=16)
        bc3 = lambda v: v.f(lambda a: a.unsqueeze(2).to_broadcast([128, 16, 64]))
        for m in range(4):
            if pass1:
                P.dve.memset(ap=Hs.full(), constant=0.0)
                P.dve.memset(ap=dsum.full(), constant=0.0)
            else:
                P.dve.tensor_copy(out=Hs.full(), in_=Hinit[:, m, :])
                P.act.copy(out=Hb.full(), in_=Hinit[:, m, :])
            for j in range(4):
                c = m * 4 + j
                x_c = xc[c % 2]
                B_c = Bc[c % 2]
                P.dma("sync", out=x_c.full(), in_=xtm[:, c, :])
                P.dma("gpsimd", out=B_c.full(), in_=btm[:, c, :])
                P.dve.tensor_tensor(out=v3(xdt.full()), in0=v3(x_c.full()), in1=bc3(dt_tm[:, c, :]), op=ALU.mult)
                P.pool.tensor_tensor(out=v3(xdts.full()), in0=v3(xdt.full()), in1=bc3(wdec[:, c, :]), op=ALU.mult)
                if not pass1:
                    cs = slice(c * 128, (c + 1) * 128)
                    ps_cb = pb[1]
                    for g in range(2):
                        P.pe.matmul(out=ps_cb[:, g * 128:(g + 1) * 128], lhsT=BCs[:, g, cs], rhs=BCs[:, 2 + g, cs],
                                    start=True, stop=True)
                    P.act.copy(out=cb.full().re("p a b -> p (a b)"), in_=ps_cb[:, 0:256])
                    ps_off = [pb[2], pb[3]]
                    for g in range(2):
                        P.pe.matmul(out=ps_off[g].full(), lhsT=BCs[:, 2 + g, cs], rhs=Hb[:, g * 512:(g + 1) * 512],
                                    start=True, stop=True)
                    ps_y = [pb[4], pb[5]]
                    def st1(h):
                        i2 = h % NH
                        g = h // 8
                        P.dve.tensor_scalar(out=at[i2].full(), in0=tri, scalar1=a_tm[:, c, h:h + 1], scalar2=None, op0=ALU.mult)
                        ps_A = pb[6]
                        P.pe.matmul(out=ps_A[:, i2 * 128:(i2 + 1) * 128], lhsT=ones, rhs=at[i2].full(), start=True, stop=True)
                        P.dve.tensor_tensor(out=tm[i2].full(), in0=ps_A[:, i2 * 128:(i2 + 1) * 128], in1=trimask, op=ALU.add)
                        P.act.activation(out=dec[i2].full(), in_=tm[i2].full(), func=AF.Exp, bias=nAcum[:, c, h:h + 1], scale=1.0)
                        P.pool.tensor_tensor(out=MT[i2].full(), in0=cb[:, g, :], in1=dec[i2].full(), op=ALU.mult)

                    def st2(h):
                        i2 = h % NH
                        g = h // 8
                        hh = h % 8
                        P.pe.matmul(out=ps_y[g][:, hh * 64:(hh + 1) * 64], lhsT=MT[i2].full(), rhs=xdt[:, h * 64:(h + 1) * 64],
                                    start=True, stop=True)

                    for hq in range(16 + 3):
                        if hq < 16:
                            st1(hq)
                        if hq >= 3:
                            st2(hq - 3)
                    for g in range(2):
                        gs_ = slice(g * 512, (g + 1) * 512)
                        v8 = lambda v: v.re("p (h d) -> p h d", h=8)
                        b8 = lambda v: v.f(lambda a: a.unsqueeze(2).to_broadcast([128, 8, 64]))
                        P.dve.tensor_tensor(out=v8(t1[:, gs_]), in0=v8(ps_off[g].full()), in1=b8(eA[:, c, g * 8:(g + 1) * 8]), op=ALU.mult)
                        P.dve.tensor_tensor(out=t1[:, gs_], in0=t1[:, gs_], in1=ps_y[g].full(), op=ALU.add)
                    P.pool.tensor_tensor(out=v3(t3.full()), in0=v3(x_c.full()), in1=bc3(rowc[:, 40:56]), op=ALU.mult)
                    P.pool.tensor_tensor(out=t3.full(), in0=t1.full(), in1=t3.full(), op=ALU.add)
                    for q4 in range(2):
                        pst = pb[2 + q4]
                        for jj in range(4):
                            fc = q4 * 4 + jj
                            P.pe.transpose(out=pst[:, jj * 128:(jj + 1) * 128], in_=t3[:, fc * 128:(fc + 1) * 128],
                                           identity=cmb[:, 2, :])
                        yt = yT[q4]
                        P.act.copy(out=yt.full().re("p a b -> p (a b)"), in_=pst.full())
                        P.dma("sync", out=yd[q4 * 512:(q4 + 1) * 512, c * 128:(c + 1) * 128].re("(a p) t -> p a t", p=128),
                              in_=yt.full())
                ps_h = [pb[0], pb[1]] if pass1 else [pb[4], pb[5]]
                for g in range(2):
                    P.pe.matmul(out=ps_h[g].full(), lhsT=B_c[:, g * 128:(g + 1) * 128], rhs=xdts[:, g * 512:(g + 1) * 512],
                                start=True, stop=True)
                P.dve.tensor_tensor(out=v3(Hs.full()), in0=v3(Hs.full()), in1=bc3(eAtot[:, c, :]), op=ALU.mult)
                for g in range(2):
                    P.dve.tensor_tensor(out=Hs[:, g * 512:(g + 1) * 512], in0=Hs[:, g * 512:(g + 1) * 512], in1=ps_h[g].full(), op=ALU.add)
                if pass1:
                    P.dve.tensor_tensor(out=dsum.full(), in0=dsum.full(), in1=Atot[:, c, :], op=ALU.add)
                else:
                    P.act.copy(out=Hb.full(), in_=Hs.full())
            if pass1:
                P.dma("sync", out=sx[m // 2][(m % 2) * 128:(m % 2 + 1) * 128, :], in_=Hs.full())
                P.act.activation(out=fxs[:, 160 + m * 16:160 + (m + 1) * 16], in_=dsum.full(), func=AF.Exp)
        if pass1:
            P.dma("sync", out=fx[:, 160:224], in_=fxs[:, 160:224])

    def load_fg():
        fg = P.sb("fg", [128, 4, 224], F32)
        P.dma("sync", out=fg.full(), in_=fxg.full().re("(r p) c -> p r c", p=128))
        return fg

    def phase_attn(l):
        K = load_consts()
        sel, zero = K["sel"], K["zero"]
        mskb = P.sb("mskb", [128, 8, 512], F32)
        P.dma("gpsimd", out=mskb.full(), in_=msk_d.full())
        fg = load_fg()
        offs = P.sb("offs", [128, 16, 8], F32)
        run = P.sb("run", [128, 8], F32)
        P.dve.memset(ap=run.full(), constant=0.0)
        for s_ in range(16):
            m, r = divmod(s_, 4)
            P.dve.tensor_copy(out=offs[:, s_, :], in_=run.full())
            P.dve.tensor_tensor(out=run.full(), in0=run.full(), in1=fg[:, r, 128 + m * 8:128 + (m + 1) * 8], op=ALU.add)
        offown = P.sb("offown", [128, 4, 8], F32)
        P.dve.memset(ap=offown.full(), constant=0.0)
        for m in range(4):
            for r in range(4):
                P.dve.scalar_tensor_tensor(out=offown[:, m, :], in0=offs[:, 4 * m + r, :], scalar=sel[:, 8 + 4 * m + r:9 + 4 * m + r],
                                           in1=offown[:, m, :], op0=ALU.mult, op1=ALU.add)
        negFg = P.sb("negFg", [128, 64, 8], F32)
        for s_ in range(16):
            m, r = divmod(s_, 4)
            src = fg[:, r, 0:128].re("p (a b) -> p a b", b=8)[:, 4 * m:4 * m + 4, :]
            P.dve.tensor_tensor(out=negFg[:, 4 * s_:4 * s_ + 4, :], in0=src,
                                in1=offs[:, s_, :].f(lambda a: a.unsqueeze(1).to_broadcast([128, 4, 8])), op=ALU.add)
        P.dve.tensor_scalar(out=negFg.full(), in0=negFg.full(), scalar1=-1.0, scalar2=None, op0=ALU.mult)
        biasm = P.sb("biasm", [128, 4, 64, 8], F32)
        for m in range(4):
            nk = (4 * m + 4) * 4
            P.dve.tensor_tensor(out=biasm[:, m, 0:nk, :], in0=negFg[:, 0:nk, :],
                                in1=offown[:, m, :].f(lambda a: a.unsqueeze(1).to_broadcast([128, nk, 8])), op=ALU.add)
            for jr in range(4):
                k0 = (4 * m + jr) * 4
                P.dve.tensor_scalar(out=biasm[:, m, k0:k0 + 4, :], in0=biasm[:, m, k0:k0 + 4, :],
                                    scalar1=sel[:, 28 + jr:29 + jr], scalar2=None, op0=ALU.add)

        pb = [P.ps(f"pb{i}", [128, 512], F32) for i in range(8)]
        K_sb = [P.sb(f"K_sb{i}", [96, S_], BF16) for i in range(2)]
        Q_sb = [P.sb(f"Q_sb{i}", [96, T], BF16) for i in range(2)]
        V_sb = [P.sb(f"V_sb{i}", [128, NKT, 128], BF16) for i in range(2)]
        for i in range(2):
            P.dve.memset(ap=V_sb[i][:, :, 64:128], constant=1.0)
        NSB = 5
        LA = 3
        WARM = False
        pt = [P.sb(f"pt{i}", [128, 512], BF16) for i in range(NSB)]
        mt = [P.sb(f"mt{i}", [128, 512], F32) for i in range(3)]
        rl = P.sb("rl", [128, 512], F32)
        rl2 = P.sb("rl2", [64, 512], F32)
        ot = [P.sb(f"ot{i}", [64, 512], BF16) for i in range(2)]
        cnt = [0, 0, 0]
        heads = [(0, h) for h in range(8)] + [(1, h) for h in range(8)]

        def loads(idx):
            kind, h = heads[idx]
            i = idx % 2
            nd = 96 if kind == 0 else 64
            for r in range(4):
                vr = r * 2048 + kind * 1024 + h * 128
                for m in range(4):
                    s0 = (4 * m + r) * 512
                    if kind == 0:
                        ksrc = kxmg[m][r * 768 + h * 96:r * 768 + (h + 1) * 96, :]
                    else:
                        ksrc = kxfg[m][r * 512 + h * 64:r * 512 + (h + 1) * 64, :]
                    P.dma("sync" if (r + m) % 2 == 0 else "gpsimd", out=K_sb[i][0:nd, s0:s0 + 512], in_=ksrc)
                    g0 = (4 * m + r) * 4
                    P.dma("gpsimd" if (r + m) % 2 == 0 else "sync",
                          out=V_sb[i][:, g0:g0 + 4, 0:64],
                          in_=vxg[m][vr:vr + 128, :].re("p (j d) -> p j d", j=4))
            if kind == 0:
                P.dma("sync", out=Q_sb[i][0:96, :], in_=qm[h])
            else:
                P.dve.memset(ap=K_sb[i][64:96, :], constant=0.0)
                P.dve.memset(ap=K_sb[i][64:67, :], constant=8.0)
                P.dma("sync", out=Q_sb[i][0:64, :], in_=qf[h])
                P.pool.memset(ap=Q_sb[i][64:96, :], constant=0.0)
                P.dma("gpsimd", out=Q_sb[i][64:67, :], in_=fq[h])

        iters = []
        for idx in range(16):
            for m in range(4):
                nk = (4 * m + 4) * 4
                for kt in range(nk):
                    iters.append((idx, m, kt, nk))

        def stage_qk(n):
            idx, m, kt, nk = iters[n]
            kind, h = heads[idx]
            i = idx % 2
            dk = 96
            scale = 96.0 ** -0.5 if kind == 0 else 0.125
            i3 = n % NSB
            ps = pb[i3]
            P.pe.matmul(out=ps.full(), lhsT=K_sb[i][0:dk, kt * 128:(kt + 1) * 128],
                        rhs=Q_sb[i][0:dk, m * 512:(m + 1) * 512], start=True, stop=True)
            blk = kt // 4
            if blk >= 4 * m:
                jr = blk - 4 * m
                mm = mt[cnt[1] % 3]
                cnt[1] += 1
                P.dve.scalar_tensor_tensor(out=mm.full(), in0=mskb[:, kind * 4 + kt % 4, :],
                                           scalar=sel[:, 24 + jr:25 + jr], in1=ps.full(), op0=ALU.mult, op1=ALU.add)
                src = mm.full()
                bias = sel[:, 28 + jr:29 + jr] if kind == 0 else biasm[:, m, kt, h:h + 1]
            else:
                src = ps.full()
                bias = zero[:, 0:1] if kind == 0 else biasm[:, m, kt, h:h + 1]
            P.act.activation(out=pt[i3].full(), in_=src, func=AF.Exp, scale=scale, bias=bias)
            if WARM:
                P.pe.matmul(out=pb[6][:, 0:256], lhsT=K_sb[i][0:dk, kt * 128:(kt + 1) * 128],
                            rhs=Q_sb[i][0:dk, m * 512:m * 512 + 256], start=True, stop=True)

        def stage_pv(n):
            idx, m, kt, nk = iters[n]
            kind, h = heads[idx]
            i = idx % 2
            oacc = pb[5 + (idx * 4 + m) % 2]
            P.pe.matmul(out=oacc.full(), lhsT=V_sb[i][:, kt, :], rhs=pt[n % NSB].full(), start=(kt == 0), stop=(kt == nk - 1))
            if kt == nk - 1:
                odst = omd if kind == 0 else ofd
                P.dve.reciprocal(out=rl[64:128, :], in_=oacc[64:128, :])
                P.dve.tensor_copy(out=rl2.full(), in_=rl[64:128, :])
                o = ot[m % 2]
                P.dve.tensor_tensor(out=o.full(), in0=oacc[0:64, :], in1=rl2.full(), op=ALU.mult)
                P.dma("sync", out=odst[h * 64:(h + 1) * 64, m * 512:(m + 1) * 512], in_=o.full())

        pcs = [P.sb(f"pcs{i}", [128, 2048], F32) for i in range(2)]
        pcb = [P.sb(f"pcb{i}", [128, 2048], BF16) for i in range(2)]
        jobs = []
        for (src, dst, rows, cols) in ((w_a[l], wab, 512, D), (w_b[l], wbb, 512, D), (w_c[l], wcb, 1024, D), (w_o[l], wob, 1024, D),
                                       (w_up[l], wub, D, 5632), (w_dn[l], wdb, 2816, D)):
            for r0 in range(0, rows, 128):
                for c0 in range(0, cols, 2048):
                    n_ = min(2048, cols - c0)
                    jobs.append((src[r0:r0 + 128, c0:c0 + n_], dst[r0:r0 + 128, c0:c0 + n_], n_))
        jcnt = [0]

        def precast_one():
            if jcnt[0] >= len(jobs):
                return
            src, dst, n_ = jobs[jcnt[0]]
            i = jcnt[0] % 2
            jcnt[0] += 1
            P.dma("gpsimd", out=pcs[i][:, 0:n_], in_=src)
            P.pool.tensor_copy(out=pcb[i][:, 0:n_], in_=pcs[i][:, 0:n_])
            P.dma("gpsimd", out=dst, in_=pcb[i][:, 0:n_])

        every = max(1, len(iters) // (len(jobs) + 4))
        loads(0)
        loads(1)
        for n in range(len(iters) + LA):
            if n < len(iters):
                stage_qk(n)
            if n >= LA:
                stage_pv(n - LA)
                idx_p, m_p, kt_p, nk_p = iters[n - LA]
                if m_p == 3 and kt_p == nk_p - 1 and idx_p + 2 < 16:
                    loads(idx_p + 2)
            if n % every == every - 1:
                precast_one()
        while jcnt[0] < len(jobs):
            precast_one()

    def phase_ssd2(l):
        K = load_consts()
        sel = K["sel"]
        rowc = P.sb("rowc", [128, 56], F32)
        P.dma("sync", out=rowc.full(), in_=rowc_d[l])
        fg = load_fg()
        Hin = P.sb("Hin", [128, 1024], F32)
        Hsel = P.sb("Hsel", [128, 4, 1024], F32)
        Sst = [P.sb(f"Sst{i}", [128, 1024], F32) for i in range(2)]
        P.dve.memset(ap=Hin.full(), constant=0.0)
        P.dve.memset(ap=Hsel.full(), constant=0.0)
        v3 = lambda v: v.re("p (h d) -> p h d", h=16)
        for s_ in range(16):
            m, r = divmod(s_, 4)
            P.dve.scalar_tensor_tensor(out=Hsel[:, m, :], in0=Hin.full(), scalar=sel[:, 8 + s_:9 + s_], in1=Hsel[:, m, :],
                                       op0=ALU.mult, op1=ALU.add)
            if s_ < 15:
                st_ = Sst[s_ % 2]
                P.dma("sync" if s_ % 2 else "gpsimd", out=st_.full(),
                      in_=sxg[m // 2][r * 256 + (m % 2) * 128:r * 256 + (m % 2 + 1) * 128, :])
                dcs = fg[:, r, 160 + m * 16:160 + (m + 1) * 16]
                P.dve.tensor_tensor(out=v3(Hin.full()), in0=v3(Hin.full()),
                                    in1=dcs.f(lambda a: a.unsqueeze(2).to_broadcast([128, 16, 64])), op=ALU.mult)
                P.pool.tensor_tensor(out=Hin.full(), in0=Hin.full(), in1=st_.full(), op=ALU.add)
        ssd_scan(l, K, pass1=False, Hinit=Hsel, rowc=rowc)

    def write_tails(txs):
        P.dma("sync", out=tx.full(), in_=txs.full().re("p m k c -> p (m k c)"))

    def halo_exchange(dst):
        K = load_consts()
        sel = K["sel"]
        P.pool.collective_compute(kind="AllGather", op=ALU.bypass, replica_groups=RG, ins=[tx.full()], outs=[txg.full()])
        tg = P.sb("tg", [128, 4, 128], F32)
        P.dma("sync", out=tg.full(), in_=txg.full().re("(r p) c -> p r c", p=128))
        hl = P.sb("hl", [128, 4, 32], F32)
        P.dve.memset(ap=hl.full(), constant=0.0)
        for m in range(4):
            for r in range(4):
                P.dve.scalar_tensor_tensor(out=hl[:, m, :], in0=tg[:, r, m * 32:(m + 1) * 32], scalar=sel[:, r:r + 1],
                                           in1=hl[:, m, :], op0=ALU.mult, op1=ALU.add)
            if m >= 1:
                P.dve.scalar_tensor_tensor(out=hl[:, m, :], in0=tg[:, 3, (m - 1) * 32:m * 32], scalar=sel[:, 4:5],
                                           in1=hl[:, m, :], op0=ALU.mult, op1=ALU.add)
        dv = dst.full().re("(kc p) n -> p kc n", p=128)
        for m in range(4):
            P.dma("sync", out=dv[:, :, m * SW:m * SW + 4], in_=hl[:, m, :].re("p (k c) -> p k c", c=4))

    def phase_merge(l):
        K = load_consts()
        ones, eps = K["cm"][:, 3, :], K["eps"]
        gsb = P.sb("gsb", [128, 8], F32)
        P.dma("sync", out=gsb.full(), in_=gssm_d[l])
        pb = [P.ps(f"pb{i}", [128, 512], F32) for i in range(8)]
        def load_wb(dram_bf, kc_n, name, q):
            bfb = P.sb(name, [128, kc_n, D], BF16)
            P.dma(q, out=bfb.full(), in_=dram_bf.full().re("(kc p) n -> p kc n", p=128))
            return bfb

        Wa = load_wb(wab, 4, "Wa", "sync")
        Wb = load_wb(wbb, 4, "Wb", "gpsimd")
        Wc = load_wb(wcb, 8, "Wc", "sync")
        Wo = load_wb(wob, 8, "Wo", "gpsimd")
        ys = P.sb("ys", [128, 8, 512], F32)
        szs = P.sb("szs", [128, 8, 512], BF16)
        yn = P.sb("yn", [128, 8, 512], BF16)
        oms = P.sb("oms", [128, 4, 512], BF16)
        ofs = P.sb("ofs", [128, 4, 512], BF16)
        gs = P.sb("gs", [128, 24, 512], BF16)
        xs = P.sb("xs", [128, 8, 512], F32)
        sq = P.sb("sq", [128, 512], F32)
        rstd = P.sb("rstd", [128, 512], F32)
        m1 = [P.sb(f"m1_{i}", [128, 512], F32) for i in range(2)]
        m2 = [P.sb(f"m2_{i}", [128, 512], F32) for i in range(2)]
        m3 = [P.sb(f"m3_{i}", [128, 512], F32) for i in range(2)]
        mg = P.sb("mg", [128, 8, 512], BF16)
        xo = [P.sb(f"xo{i}", [128, 512], F32) for i in range(2)]
        txs = P.sb("txs", [128, 4, 8, 4], F32)
        ch = lambda d: d.full().re("(kc p) n -> p kc n", p=128)
        xmv = ch(xmid)
        for ti in range(4):
            ts = slice(ti * 512, (ti + 1) * 512)
            xsl = slice(ti * SW + 4, ti * SW + 516)
            P.dma("sync", out=ys.full(), in_=ch(yd)[:, :, ts])
            P.dma("gpsimd", out=szs.full(), in_=ch(szd)[:, :, ts])
            P.dma("sync", out=oms.full(), in_=ch(omd)[:, :, ts])
            P.dma("gpsimd", out=ofs.full(), in_=ch(ofd)[:, :, ts])
            P.dma("sync", out=gs.full(), in_=ch(gd)[:, :, ts])
            P.dma("gpsimd", out=xs.full(), in_=ch(xb[l])[:, :, xsl])
            ps = pb[7]
            for kc in range(8):
                P.dve.tensor_tensor(out=ys[:, kc, :], in0=ys[:, kc, :], in1=szs[:, kc, :], op=ALU.mult)
                P.act.activation(out=sq.full(), in_=ys[:, kc, :], func=AF.Square)
                P.pe.matmul(out=ps.full(), lhsT=ones, rhs=sq.full(), start=(kc == 0), stop=(kc == 7))
            P.act.activation(out=rstd.full(), in_=ps.full(), func=AF.Ln, bias=eps[:, 0:1], scale=1.0 / 1024.0)
            P.act.activation(out=rstd.full(), in_=rstd.full(), func=AF.Exp, scale=-0.5)
            for kc in range(8):
                P.dve.scalar_tensor_tensor(out=yn[:, kc, :], in0=ys[:, kc, :], scalar=gsb[:, kc:kc + 1], in1=rstd.full(),
                                           op0=ALU.mult, op1=ALU.mult)
            for oc in range(8):
                i2 = oc % 2
                osl = slice(oc * 128, (oc + 1) * 128)
                pa, pbb, pc = pb[0 + i2 * 3], pb[1 + i2 * 3], pb[2 + i2 * 3]
                for kc in range(4):
                    P.pe.matmul(out=pa.full(), lhsT=Wa[:, kc, osl], rhs=oms[:, kc, :], start=(kc == 0), stop=(kc == 3))
                for kc in range(4):
                    P.pe.matmul(out=pbb.full(), lhsT=Wb[:, kc, osl], rhs=ofs[:, kc, :], start=(kc == 0), stop=(kc == 3))
                for kc in range(8):
                    P.pe.matmul(out=pc.full(), lhsT=Wc[:, kc, osl], rhs=yn[:, kc, :], start=(kc == 0), stop=(kc == 7))
                P.dve.tensor_tensor(out=m1[i2].full(), in0=pa.full(), in1=gs[:, oc, :], op=ALU.mult)
                P.dve.tensor_tensor(out=m2[i2].full(), in0=pbb.full(), in1=gs[:, 8 + oc, :], op=ALU.mult)
                P.dve.tensor_tensor(out=m3[i2].full(), in0=pc.full(), in1=gs[:, 16 + oc, :], op=ALU.mult)
                P.pool.tensor_tensor(out=m1[i2].full(), in0=m1[i2].full(), in1=m2[i2].full(), op=ALU.add)
                P.pool.tensor_tensor(out=mg[:, oc, :], in0=m1[i2].full(), in1=m3[i2].full(), op=ALU.add)
            for oc in range(8):
                i2 = oc % 2
                ps = pb[6 + i2]
                for kc in range(8):
                    P.pe.matmul(out=ps.full(), lhsT=Wo[:, kc, oc * 128:(oc + 1) * 128], rhs=mg[:, kc, :],
                                start=(kc == 0), stop=(kc == 7))
                P.dve.tensor_tensor(out=xo[i2].full(), in0=ps.full(), in1=xs[:, oc, :], op=ALU.add)
                P.pool.tensor_copy(out=txs[:, ti, oc, :], in_=xo[i2][:, 508:512])
                P.dma("sync" if i2 else "gpsimd", out=xmv[:, oc, xsl], in_=xo[i2].full())
        write_tails(txs)

    def phase_ffn(l, last):
        K = load_consts()
        ones, eps = K["cm"][:, 3, :], K["eps"]
        cstb = P.sb("cstb", [128, offD2["_n"]], F32)
        C = Cst(P, cstb, offD2)
        P.dma("sync", out=cstb.full(), in_=cstD_d[l])
        pb = [P.ps(f"pb{i}", [128, 512], F32) for i in range(8)]
        Wu = P.sb("Wu", [128, 8, 5632], BF16)
        Wd = P.sb("Wd", [128, 22, D], BF16)
        wubv = wub.full().re("(kc p) n -> p kc n", p=128)
        for (c0, c1) in ((0, 512), (2816, 3328), (512, 2816), (3328, 5632)):
            P.dma("sync" if c0 < 2816 else "gpsimd", out=Wu[:, :, c0:c1], in_=wubv[:, :, c0:c1])
        wdbv = wdb.full().re("(kc p) n -> p kc n", p=128)
        P.dma("sync", out=Wd[:, 0:11, :], in_=wdbv[:, 0:11, :])
        P.dma("gpsimd", out=Wd[:, 11:22, :], in_=wdbv[:, 11:22, :])
        xst = P.sb("xst", [128, 8, 512], F32)
        hn = P.sb("hn", [128, 8, 512], BF16)
        sq = P.sb("sq", [128, 512], F32)
        rstd = P.sb("rstd", [128, 512], F32)
        act = P.sb("act", [128, 22, 512], BF16)
        upre = [P.sb(f"upre{i}", [128, 516], F32) for i in range(2)]
        acc = [P.sb(f"acc{i}", [128, 512], F32) for i in range(2)]
        sg = P.sb("sg", [128, 512], F32)
        carry = P.sb("carry", [128, 44, 4], F32)
        xo = [P.sb(f"xo{i}", [128, 512], F32) for i in range(2)]
        txs = P.sb("txs", [128, 4, 8, 4], F32)
        xTv = xmid.full().re("(kc p) n -> p kc n", p=128)
        dst = out if last else xb[l + 1]
        dv = dst.full().re("(kc p) n -> p kc n", p=128)
        tiles = []
        for m in range(4):
            tiles.append((m * SW, 4, True, m))
            tiles.append((m * SW + 4, 512, False, m))
        pcnt = [0]
        for (c0, w, is_halo, m) in tiles:
            P.dma("sync", out=xst[:, :, 0:w], in_=xTv[:, :, c0:c0 + w])
            ps = pb[7]
            for kc in range(8):
                P.act.activation(out=sq[:, 0:w], in_=xst[:, kc, 0:w], func=AF.Square)
                P.pe.matmul(out=ps[:, 0:w], lhsT=ones, rhs=sq[:, 0:w], start=(kc == 0), stop=(kc == 7))
            P.act.activation(out=rstd[:, 0:w], in_=ps[:, 0:w], func=AF.Ln, bias=eps[:, 0:1], scale=1.0 / 1024.0)
            P.act.activation(out=rstd[:, 0:w], in_=rstd[:, 0:w], func=AF.Exp, scale=-0.5)
            for kc in range(8):
                P.dve.scalar_tensor_tensor(out=hn[:, kc, 0:w], in0=xst[:, kc, 0:w], scalar=C.col("g_ffn", kc),
                                           in1=rstd[:, 0:w], op0=ALU.mult, op1=ALU.mult)
            for i in range(22):
                accs = []
                for j, cg in enumerate((i, 22 + i)):
                    ps = pb[pcnt[0] % 4]
                    pcnt[0] += 1
                    for kc in range(8):
                        P.pe.matmul(out=ps[:, 0:w], lhsT=Wu[:, kc, cg * 128:(cg + 1) * 128], rhs=hn[:, kc, 0:w],
                                    start=(kc == 0), stop=(kc == 7))
                    if is_halo:
                        P.act.copy(out=carry[:, cg, :], in_=ps[:, 0:4])
                        continue
                    up = upre[j]
                    a0 = acc[j]
                    P.act.copy(out=up[:, 4:516], in_=ps.full())
                    P.act.activation(out=a0.full(), in_=ps.full(), func=AF.Identity, scale=C.col("fw2", cg), bias=C.col("fb", cg))
                    P.pool.tensor_copy(out=up[:, 0:4], in_=carry[:, cg, :])
                    P.dve.scalar_tensor_tensor(out=a0.full(), in0=up[:, 3:515], scalar=C.col("fw1", cg), in1=a0.full(),
                                               op0=ALU.mult, op1=ALU.add)
                    P.dve.scalar_tensor_tensor(out=a0.full(), in0=up[:, 2:514], scalar=C.col("fw0", cg), in1=a0.full(),
                                               op0=ALU.mult, op1=ALU.add)
                    accs.append(a0)
                if is_halo:
                    continue
                P.act.activation(out=sg.full(), in_=accs[0].full(), func=AF.Silu)
                P.pool.tensor_tensor(out=act[:, i, :], in0=sg.full(), in1=accs[1].full(), op=ALU.mult)
            if is_halo:
                continue
            for oc in range(8):
                i2 = oc % 2
                ps = pb[4 + i2]
                for i in range(22):
                    P.pe.matmul(out=ps.full(), lhsT=Wd[:, i, oc * 128:(oc + 1) * 128], rhs=act[:, i, :],
                                start=(i == 0), stop=(i == 21))
                P.dve.tensor_tensor(out=xo[i2].full(), in0=ps.full(), in1=xst[:, oc, :], op=ALU.add)
                if last:
                    P.dma("sync" if i2 else "gpsimd", out=dv[:, oc, m * 512:(m + 1) * 512], in_=xo[i2].full())
                else:
                    P.pool.tensor_copy(out=txs[:, m, oc, :], in_=xo[i2][:, 508:512])
                    P.dma("sync" if i2 else "gpsimd", out=dv[:, oc, m * SW + 4:m * SW + 516], in_=xo[i2].full())
        if not last:
            write_tails(txs)

    def gather_e1():
        gather_pairs(list(zip(sx, sxg)) + [(fx, fxg)])

    nl = L if stop is None else stop[0]
    done = False
    for l in range(nl):
        last_l = (stop is not None and l == nl - 1)
        phase_A(l)
        P.emit(final=False)
        ssd_scan(l, load_consts(), pass1=True)
        P.emit(final=False)
        if last_l and stop[1] == "A":
            break
        gather_e1()
        phase_attn(l)
        P.emit(final=False)
        phase_ssd2(l)
        P.emit(final=False)
        if last_l and stop[1] == "B":
            break
        phase_merge(l)
        P.emit(final=False)
        halo_exchange(xmid)
        P.emit(final=False)
        if last_l and stop[1] == "C":
            break
        phase_ffn(l, last=(l == L - 1))
        P.emit(final=False)
        if l < L - 1:
            halo_exchange(xb[l + 1])
            P.emit(final=False)
    loc = {"kxmg0": kxmg[0], "vxg0": vxg[0], "sxg0": sxg[0], "fxg": fxg, "qm": qm, "qf": qf, "fq": fq, "omd": omd, "ofd": ofd, "yd": yd,
           "xmid": xmid, "xb1": xb[1], "szd": szd, "gd": gd, "xtm": xtm, "btm": btm, "bct": bct, "dtd": dtd, "atd": atd}
    for name in dbg:
        src = loc[name]
        shp = list(src.h.shape) if hasattr(src.h, "shape") else None
        dd = P.dram("dbg_" + name, shp, src.h.dtype, EO)
        P.dma("sync", out=dd.full(), in_=src.full())
    P.emit(final=True)
    return nc, P


def _stripe_tokens(p):
    return np.concatenate([np.arange((4 * m + p) * 512, (4 * m + p + 1) * 512) for m in range(4)])


def fused_in_maps(inp):
    L = 2
    cpsA = [a_colpack(inp, l) for l in range(L)]
    cpsD = [d2_colpack(inp, l) for l in range(L)]
    offA = dict(cpsA[0].off)
    offA["_n"] = cpsA[0].n
    offD = dict(cpsD[0].off)
    offD["_n"] = cpsD[0].n
    cstA = np.stack([c.array() for c in cpsA])
    cstD = np.stack([c.array() for c in cpsD])
    w_kp = np.zeros((L, 256, 8, 96), np.float32)
    wukv = inp["mla_w_ukv"].reshape(L, 256, 8, 128)
    w_kp[:, :, :, 0:64] = wukv[:, :, :, 0:64]
    w_v = np.ascontiguousarray(wukv[:, :, :, 64:128].reshape(L, 256, 512))
    gssm = np.ascontiguousarray(inp["ssm_norm_g"].reshape(L, 8, 128).transpose(0, 2, 1))
    rowc = np.stack([fused_rowpack(inp, l) for l in range(L)])
    msk, cm = _bc_consts()
    mats = _const_mats()
    shared = {
        "w_in": np.ascontiguousarray(inp["w_in"]), "w_uq": np.ascontiguousarray(inp["mla_w_uq"]),
        "w_kp": np.ascontiguousarray(w_kp.reshape(L, 256, 768)), "w_v": w_v,
        "w_a": np.ascontiguousarray(inp["w_br_mla"]), "w_b": np.ascontiguousarray(inp["w_br_fox"]),
        "w_c": np.ascontiguousarray(inp["w_br_ssm"]), "w_o": np.ascontiguousarray(inp["w_out"]),
        "w_up": np.ascontiguousarray(inp["ffn_w_up"]), "w_dn": np.ascontiguousarray(inp["ffn_w_down"]),
        "cstA": cstA, "cstD": cstD, "gssm": gssm, "rowc": rowc, "msk": msk, "cm": cm, "mats": mats,
    }
    in_maps = []
    for c in range(8):
        b, p = c // 4, c % 4
        xT = np.zeros((D, 4 * SW), np.float32)
        xbT = inp["x"][b].T
        for m in range(4):
            s_ = 4 * m + p
            xT[:, m * SW + 4:m * SW + 516] = xbT[:, s_ * 512:(s_ + 1) * 512]
            if s_ > 0:
                xT[:, m * SW:m * SW + 4] = xbT[:, s_ * 512 - 4:s_ * 512]
        sel = np.zeros((128, 32), np.float32)
        if p >= 1:
            sel[:, p - 1] = 1.0
        else:
            sel[:, 4] = 1.0
        for s_ in range(16):
            if s_ % 4 == p:
                sel[:, 8 + s_] = 1.0
        for jr in range(4):
            sel[:, 24 + jr] = 1.0 if jr == p else 0.0
            sel[:, 28 + jr] = NEG if jr > p else 0.0
        d = dict(shared)
        d["x0"] = np.ascontiguousarray(xT)
        d["pos"] = np.ascontiguousarray(inp["positions"][b][_stripe_tokens(p)][None, :]).astype(np.int32)
        d["sel"] = sel
        in_maps.append(d)
    return in_maps, offA, offD


def kernel_fused(**inp):
    inp = {k: np.asarray(v) for k, v in inp.items()}
    in_maps, offA, offD = fused_in_maps(inp)
    if "F" not in _PROG_CACHE:
        _PROG_CACHE["F"] = build_fused(offA, offD)[0]
    res = run_bass_kernel_spmd(_PROG_CACHE["F"], in_maps, core_ids=list(range(8))).results
    xo = np.zeros((2, S_, D), np.float32)
    for c in range(8):
        b, p = c // 4, c % 4
        xo[b, _stripe_tokens(p), :] = np.asarray(res[c]["out"]).T
    return xo
```

```python
from contextlib import ExitStack
import numpy as np
import concourse.bass as bass
import concourse.mybir as mybir

F32 = mybir.dt.float32
BF16 = mybir.dt.bfloat16
I32 = mybir.dt.int32
ALU = mybir.AluOpType
AF = mybir.ActivationFunctionType
AX = mybir.AxisListType

COMPUTE = ("tensor", "vector", "scalar", "gpsimd")
QUEUES = ("sync", "gpsimd", "scalar")
NRING = 8


class View:
    __slots__ = ("buf", "ap", "key")

    def __init__(self, buf, ap, key=None):
        self.buf = buf
        self.ap = ap
        self.key = key

    def __getitem__(self, k):
        return View(self.buf, self.ap[k], self.key)

    def re(self, s, **kw):
        return View(self.buf, self.ap.rearrange(s, **kw), self.key)

    def bc(self, shape):
        return View(self.buf, self.ap.to_broadcast(shape), self.key)

    def bitcast(self, dt):
        return View(self.buf, self.ap.bitcast(dt), self.key)

    def k(self, key):
        return View(self.buf, self.ap, key)

    def f(self, fn):
        return View(self.buf, fn(self.ap), self.key)


class Buf:
    def __init__(self, name, handle, is_dram=False):
        self.name = name
        self.h = handle
        self.is_dram = is_dram
        self.regions = {}

    def full(self):
        ap = self.h.ap() if hasattr(self.h, "ap") and callable(getattr(self.h, "ap")) else self.h[:]
        return View(self, ap)

    def __getitem__(self, k):
        return View(self, self.h[k])


class Op:
    __slots__ = ("id", "eng", "meth", "kw", "deps", "is_dma", "signaled", "sem", "val", "prewait", "eidx")


class Eng:
    def __init__(self, P, name):
        self.P = P
        self.name = name

    def __getattr__(self, meth):
        def call(*a, **kw):
            assert not a, "use kwargs"
            return self.P._record(self.name, meth, kw)
        return call


class Prog:
    def __init__(self, nc):
        self.nc = nc
        self.ops = []
        self.gstack = ExitStack()
        self.stack = ExitStack()
        self.pe = Eng(self, "tensor")
        self.dve = Eng(self, "vector")
        self.act = Eng(self, "scalar")
        self.pool = Eng(self, "gpsimd")
        self.sp = Eng(self, "sync")
        st = self.gstack
        self.csem = {e: st.enter_context(nc.semaphore(f"c_{e}")) for e in COMPUTE}
        self.rings = {q: [st.enter_context(nc.semaphore(f"d_{q}{i}")) for i in range(NRING)] for q in QUEUES}
        self.ccsem = st.enter_context(nc.semaphore("ccsem"))
        self.cccount = 0
        self.ccount = {e: 0 for e in COMPUTE}
        self.dcount = {q: 0 for q in QUEUES}
        self.waited = {e: {} for e in ("sync",) + COMPUTE}
        self.emitted = 0
        self.barrier = []
        self.stats = {}
        self.nwaits = 0

    def sb(self, name, shape, dtype):
        self.nuid = getattr(self, "nuid", 0) + 1
        name = f"{name}_s{self.nuid}"
        t = self.stack.enter_context(self.nc.sbuf_tensor(name, list(shape), dtype))
        return Buf(name, t)

    def ps(self, name, shape, dtype):
        self.nuid = getattr(self, "nuid", 0) + 1
        name = f"{name}_p{self.nuid}"
        t = self.stack.enter_context(self.nc.psum_tensor(name, list(shape), dtype))
        return Buf(name, t)

    def dram(self, name, shape, dtype, kind="Internal"):
        t = self.nc.dram_tensor(name, list(shape), dtype, kind=kind)
        return Buf(name, t, is_dram=True)

    def _record(self, eng, meth, kw):
        op = Op()
        op.id = len(self.ops)
        op.eng = eng
        op.meth = meth
        op.kw = kw
        op.is_dma = meth in ("dma_start", "dma_start_transpose", "collective_compute")
        op.signaled = False
        op.sem = None
        op.val = 0
        op.prewait = None
        deps = set()
        extra_r = kw.pop("_reads", [])
        extra_w = kw.pop("_writes", [])
        writes, reads = [], []
        for k, v in kw.items():
            vs = v if isinstance(v, (list, tuple)) else [v]
            for x in vs:
                if isinstance(x, View):
                    if k in ("out", "accum_out", "outs") or (k == "ap" and meth in ("memset", "memzero")):
                        writes.append(x)
                    else:
                        reads.append(x)
        reads += extra_r
        writes += extra_w
        for v in reads:
            self._gather(v, False, deps)
        for v in writes:
            self._gather(v, True, deps)
        for v in reads:
            self._update(v, False, op.id)
        for v in writes:
            self._update(v, True, op.id)
        deps.discard(op.id)
        op.deps = deps
        self.ops.append(op)
        return op

    def _gather(self, v, is_write, deps):
        R = v.buf.regions
        if v.key is None:
            regs = list(R.values())
        else:
            regs = [R[k] for k in (v.key, None) if k in R]
        for reg in regs:
            if reg[0] is not None:
                deps.add(reg[0])
            if is_write:
                deps.update(reg[1])

    def _update(self, v, is_write, oid):
        R = v.buf.regions
        if is_write:
            if v.key is None:
                R.clear()
            R[v.key] = [oid, []]
        else:
            R.setdefault(v.key, [None, []])[1].append(oid)

    def dma(self, q, out, in_, **kw):
        eng = {"sync": self.sp, "gpsimd": self.pool, "scalar": self.act}[q]
        return eng.dma_start(out=out, in_=in_, **kw)

    def emit(self, final=True):
        nc = self.nc
        ops = self.ops
        phase = ops[self.emitted:]
        first_id = self.emitted
        self.emitted = len(ops)
        for op in phase:
            for d in op.deps:
                dop = ops[d]
                if d < first_id:
                    continue
                if dop.eng == "tensor" and op.eng == "tensor" and not dop.is_dma and not op.is_dma:
                    continue
                dop.signaled = True
        per = {}
        for op in phase:
            per.setdefault(op.eng, []).append(op)
        for e, lst in per.items():
            for op in reversed(lst):
                if not op.is_dma:
                    op.signaled = True
                    break
        for op in phase:
            if op.meth == "collective_compute":
                self.cccount += 1
                op.sem = self.ccsem
                op.val = self.cccount
                op.signaled = True
            elif op.is_dma:
                k = self.dcount[op.eng]
                self.dcount[op.eng] += 1
                op.sem = self.rings[op.eng][k % NRING]
                op.val = 16 * (k // NRING + 1)
                if k >= NRING:
                    op.prewait = (op.sem, 16 * (k // NRING))
                op.signaled = True
            elif op.signaled:
                self.ccount[op.eng] += 1
                op.sem = self.csem[op.eng]
                op.val = self.ccount[op.eng]
        for e, v in per.items():
            self.stats[e] = self.stats.get(e, 0) + len(v)
        barrier_in = list(self.barrier)
        dcount = self.dcount
        rings = self.rings

        def dma_final_waits():
            ws = []
            for q in QUEUES:
                n = dcount[q]
                for i in range(min(n, NRING)):
                    cnt = (n - 1 - i) // NRING + 1
                    ws.append((rings[q][i], 16 * cnt))
            if self.cccount > 0:
                ws.append((self.ccsem, self.cccount))
            return ws

        def run(engname, e):
            waited = self.waited[engname]

            def do_waits(ws):
                for sem, val in ws:
                    key = id(sem)
                    if waited.get(key, 0) >= val:
                        continue
                    waited[key] = val
                    e.wait_ge(sem, val)
                    self.nwaits += 1

            do_waits(barrier_in)
            for op in per.get(engname, []):
                ws = []
                if op.prewait is not None:
                    ws.append(op.prewait)
                for d in sorted(op.deps):
                    dop = ops[d]
                    if dop.sem is None:
                        continue
                    if dop.eng == "tensor" and op.eng == "tensor" and not dop.is_dma and not op.is_dma:
                        continue
                    ws.append((dop.sem, dop.val))
                do_waits(ws)
                kw = {}
                for k, v in op.kw.items():
                    if isinstance(v, View):
                        kw[k] = v.ap
                    elif isinstance(v, (list, tuple)) and v and isinstance(v[0], View):
                        kw[k] = [x.ap for x in v]
                    else:
                        kw[k] = v
                ins = getattr(e, op.meth)(**kw)
                if op.signaled:
                    ins.then_inc(op.sem, 16 if (op.is_dma and op.meth != "collective_compute") else 1)
            if final and engname == "sync":
                do_waits(dma_final_waits())

        with nc.Block() as block:
            @block.sync
            def _(e):
                run("sync", e)

            @block.tensor
            def _(e):
                run("tensor", e)

            @block.vector
            def _(e):
                run("vector", e)

            @block.scalar
            def _(e):
                run("scalar", e)

            @block.gpsimd
            def _(e):
                run("gpsimd", e)
        bar = dma_final_waits()
        for e in COMPUTE:
            if self.ccount[e] > 0:
                bar.append((self.csem[e], self.ccount[e]))
        self.barrier = bar
        self.stats["waits"] = self.nwaits
        self.stack.close()
        self.stack = ExitStack()
        if final:
            self.gstack.close()


from concourse.bass_utils import run_bass_kernel_spmd
import ml_dtypes

NBF = ml_dtypes.bfloat16
D = 1024
T = 2048
HALO = 4
NEG = -30000.0


class ColPack:
    def __init__(self):
        self.cols = []
        self.off = {}
        self.n = 0

    def add(self, name, vec, rows=128):
        vec = np.asarray(vec, np.float32).reshape(-1)
        assert vec.size % rows == 0
        m = vec.reshape(-1, rows).T
        a = np.zeros((128, m.shape[1]), np.float32)
        a[:rows] = m
        self.off[name] = (self.n, m.shape[1], rows)
        self.cols.append(a)
        self.n += m.shape[1]

    def array(self):
        return np.ascontiguousarray(np.concatenate(self.cols, axis=1))


class Cst:
    def __init__(self, P, buf, off):
        self.buf = buf
        self.off = off

    def col(self, name, j=0, rows=None):
        o, n, r = self.off[name]
        r = rows or r
        return self.buf[0:r, o + j:o + j + 1]

    def cols(self, name):
        o, n, r = self.off[name]
        return self.buf[0:r, o:o + n]


def new_nc():
    return bass.Bass("TRN2", target_bir_lowering=False)


def load_cast(P, q, dram_view, stage_view, bf_view, cast_eng):
    P.dma(q, out=stage_view, in_=dram_view)
    cast_eng.tensor_copy(out=bf_view, in_=stage_view)


A_OFF = None


def a_colpack(inp, l):
    cp = ColPack()
    cp.add("g_mix", inp["norm_mix_g"][l])
    cp.add("g_cq", inp["mla_q_norm_g"][l])
    cp.add("g_ckv", inp["mla_kv_norm_g"][l])
    cp.add("g_q", inp["mla_q_gain"][l], 96)
    cp.add("g_k", inp["mla_k_gain"][l], 96)
    cp.add("g_fq", inp["fox_q_gain"][l], 64)
    cp.add("g_fk", inp["fox_k_gain"][l], 64)
    cp.add("b_f", inp["fox_b_f"][l], 8)
    cw = inp["ssm_conv_w"][l]
    for k in range(4):
        cp.add(f"cw{k}", cw[k])
    cp.add("cb", inp["ssm_conv_b"][l])
    cp.add("dt_b", inp["ssm_dt_bias"][l], 16)
    cp.add("A_log", inp["ssm_A_log"][l], 16)
    cp.add("b_gate", inp["b_gate"][l])
    inv = 1.0 / (10000.0 ** (np.arange(0, 32, 2, dtype=np.float32) / 32.0))
    invf = np.zeros(96, np.float32)
    invf[64:80] = inv
    invf[80:96] = inv
    cp.add("invf", invf, 96)
    return cp


def build_A(off):
    nc = new_nc()
    P = Prog(nc)
    TT = T + HALO
    NT = T // 512
    EI, EO = "ExternalInput", "ExternalOutput"
    xT = P.dram("xT", [D, TT], F32, EI)
    pos = P.dram("pos", [1, T], I32, EI)
    w_in = P.dram("w_in", [D, 7864], F32, EI)
    w_uq = P.dram("w_uq", [384, 768], F32, EI)
    w_kp = P.dram("w_kp", [256, 768], F32, EI)
    w_v = P.dram("w_v", [256, 512], F32, EI)
    cst_d = P.dram("cst", [128, off["_n"]], F32, EI)
    mats = P.dram("mats", [128, 2 * 96], F32, EI)
    o_qm = P.dram("o_qm", [8, 96, T], BF16, EO)
    o_km = P.dram("o_km", [8, 96, T], BF16, EO)
    o_vm = P.dram("o_vm", [512, T], BF16, EO)
    o_qf = P.dram("o_qf", [8, 64, T], BF16, EO)
    o_kf = P.dram("o_kf", [8, 64, T], BF16, EO)
    o_vf = P.dram("o_vf", [512, T], BF16, EO)
    o_lf = P.dram("o_lf", [8, T], F32, EO)
    o_sz = P.dram("o_sz", [1024, T], BF16, EO)
    o_xbc = P.dram("o_xbc", [1536, T], BF16, EO)
    o_dt = P.dram("o_dt", [16, T], F32, EO)
    o_a = P.dram("o_a", [16, T], F32, EO)
    o_g = P.dram("o_g", [3072, T], BF16, EO)

    cstb = P.sb("cstb", [128, off["_n"]], F32)
    C = Cst(P, cstb, off)
    P.dma("sync", out=cstb.full(), in_=cst_d.full())
    matf = P.sb("matf", [128, 192], F32)
    matb = P.sb("matb", [128, 192], BF16)
    P.dma("sync", out=matf.full(), in_=mats.full())
    P.dve.tensor_copy(out=matb.full(), in_=matf.full())
    prh = matb[0:96, 0:96]
    sel = matb[0:32, 96:192]
    ones = P.sb("ones", [128, 128], F32)
    P.dve.memset(ap=ones.full(), constant=1.0)
    eps = P.sb("eps", [128, 1], F32)
    P.dve.memset(ap=eps.full(), constant=1e-6)
    one1 = P.sb("one1", [128, 1], F32)
    P.dve.memset(ap=one1.full(), constant=1.0)
    nbf = P.sb("nbf", [8, 1], F32)
    P.dve.tensor_scalar(out=nbf.full(), in0=C.col("b_f"), scalar1=-1.0, scalar2=None, op0=ALU.mult)
    Aneg = P.sb("Aneg", [16, 1], F32)
    P.act.activation(out=Aneg.full(), in_=C.col("A_log"), func=AF.Exp)
    P.dve.tensor_scalar(out=Aneg.full(), in0=Aneg.full(), scalar1=-1.0, scalar2=None, op0=ALU.mult)

    pb = [P.ps(f"pb{i}", [128, 512], F32) for i in range(8)]
    pbi = {}

    def nxt_ps(lo=0, hi=4):
        i = pbi.get(lo, 0)
        pbi[lo] = (i + 1) % (hi - lo)
        return pb[lo + i]

    Ctab = P.sb("Ctab", [96, T], F32)
    Stab = P.sb("Stab", [96, T], F32)
    posi = P.sb("posi", [96, 512], I32)
    posf = P.sb("posf", [96, 512], F32)
    rr_tmp = P.sb("rr_tmp", [96, 512], F32)
    rr_i = P.sb("rr_i", [96, 512], I32)
    rr_m = P.sb("rr_m", [96, 512], F32)

    def sin_table(outv, phase):
        P.dve.tensor_scalar(out=rr_tmp.full(), in0=posf.full(), scalar1=C.col("invf"), scalar2=phase,
                            op0=ALU.mult, op1=ALU.add)
        P.dve.tensor_scalar(out=rr_m.full(), in0=rr_tmp.full(), scalar1=1.0 / (2 * np.pi), scalar2=None, op0=ALU.mult)
        P.dve.tensor_copy(out=rr_i.full(), in_=rr_m.full())
        P.dve.tensor_copy(out=rr_m.full(), in_=rr_i.full())
        P.dve.scalar_tensor_tensor(out=rr_tmp.full(), in0=rr_m.full(), scalar=-2 * np.pi, in1=rr_tmp.full(),
                                   op0=ALU.mult, op1=ALU.add)
        P.dve.tensor_scalar(out=rr_m.full(), in0=rr_tmp.full(), scalar1=np.pi, scalar2=-2 * np.pi, op0=ALU.is_gt, op1=ALU.mult)
        P.dve.tensor_tensor(out=rr_tmp.full(), in0=rr_tmp.full(), in1=rr_m.full(), op=ALU.add)
        P.dve.tensor_scalar(out=rr_m.full(), in0=rr_tmp.full(), scalar1=-np.pi, scalar2=2 * np.pi, op0=ALU.is_lt, op1=ALU.mult)
        P.dve.tensor_tensor(out=rr_tmp.full(), in0=rr_tmp.full(), in1=rr_m.full(), op=ALU.add)
        P.act.activation(out=outv, in_=rr_tmp.full(), func=AF.Sin)

    for i in range(NT):
        P.dma("sync", out=posi.full(), in_=pos[:, i * 512:(i + 1) * 512].f(lambda a: a.partition_broadcast(96)))
        P.dve.tensor_copy(out=posf.full(), in_=posi.full())
        sin_table(Stab[:, i * 512:(i + 1) * 512], 0.0)
        sin_table(Ctab[:, i * 512:(i + 1) * 512], np.pi / 2)
    P.dve.memset(ap=Stab[0:64, :], constant=0.0)
    P.dve.memset(ap=Ctab[0:64, :], constant=1.0)

    hn = P.sb("hn", [128, 8, TT], BF16)
    xst = P.sb("xst", [128, 8, 512], F32)
    sq = P.sb("sq", [128, 512], F32)
    rstd = P.sb("rstd", [128, 512], F32)
    xTv = xT.full().re("(kc p) n -> p kc n", p=128)

    def rstd_from(ps_view, n_feat, rows, width, rstd_view):
        P.act.activation(out=rstd_view, in_=ps_view, func=AF.Sqrt, bias=eps[0:rows, 0:1], scale=1.0 / n_feat)
        P.dve.reciprocal(out=rstd_view, in_=rstd_view)

    tiles = [(0, HALO)] + [(HALO + i * 512, 512) for i in range(NT)]
    for (c0, w) in tiles:
        P.dma("sync", out=xst[:, :, 0:w], in_=xTv[:, :, c0:c0 + w])
        ps = nxt_ps(4, 6)
        for kc in range(8):
            P.act.activation(out=sq[:, 0:w], in_=xst[:, kc, 0:w], func=AF.Square)
            P.pe.matmul(out=ps[:, 0:w], lhsT=ones.full(), rhs=sq[:, 0:w], start=(kc == 0), stop=(kc == 7))
        rstd_from(ps[:, 0:w], 1024.0, 128, w, rstd[:, 0:w])
        for kc in range(8):
            P.dve.scalar_tensor_tensor(out=hn[:, kc, c0:c0 + w], in0=xst[:, kc, 0:w], scalar=C.col("g_mix", kc),
                                       in1=rstd[:, 0:w], op0=ALU.mult, op1=ALU.mult)

    wst = [P.sb(f"wst{i}", [128, 8, 512], F32) for i in range(2)]
    wbf = [P.sb(f"wbf{i}", [128, 8, 512], BF16) for i in range(2)]
    wcnt = [0]
    w_inv = w_in.full().re("(kc p) n -> p kc n", p=128)

    def load_w(c0, ncols):
        i = wcnt[0] % 2
        wcnt[0] += 1
        q = "sync" if i == 0 else "gpsimd"
        P.dma(q, out=wst[i][:, :, 0:ncols], in_=w_inv[:, :, c0:c0 + ncols])
        P.pool.tensor_copy(out=wbf[i][:, :, 0:ncols], in_=wst[i][:, :, 0:ncols])
        return wbf[i]

    def proj(wb, wc0, m, c0, w, ps_view):
        for kc in range(8):
            P.pe.matmul(out=ps_view, lhsT=wb[:, kc, wc0:wc0 + m], rhs=hn[:, kc, c0:c0 + w],
                        start=(kc == 0), stop=(kc == 7))

    ostg_cnt = [0]
    ostg = [P.sb(f"ostg{i}", [128, 512], BF16) for i in range(4)]

    def next_ostg():
        i = ostg_cnt[0] % 4
        ostg_cnt[0] += 1
        return ostg[i]

    def out_dma(dst_view, src_view):
        q = "sync" if ostg_cnt[0] % 2 else "gpsimd"
        P.dma(q, out=dst_view, in_=src_view)

    hraw = P.sb("hraw", [96, 512], F32)
    hsq = P.sb("hsq", [96, 512], F32)
    hrs = P.sb("hrs", [96, 512], F32)
    hnf = P.sb("hnf", [96, 512], F32)
    hnb = P.sb("hnb", [96, 512], BF16)
    ht1 = P.sb("ht1", [96, 512], F32)
    ht2 = P.sb("ht2", [96, 512], F32)

    def headnorm(ps_view, d, gain_col, rope, tok0, dst_view):
        P.act.activation(out=hsq[0:d, :], in_=ps_view, func=AF.Square)
        P.act.copy(out=hraw[0:d, :], in_=ps_view)
        ps2 = nxt_ps(4, 6)
        P.pe.matmul(out=ps2[0:d, :], lhsT=ones[0:d, 0:d], rhs=hsq[0:d, :], start=True, stop=True)
        rstd_from(ps2[0:d, :], float(d), d, 512, hrs[0:d, :])
        og = next_ostg()
        if not rope:
            P.dve.scalar_tensor_tensor(out=og[0:d, :], in0=hraw[0:d, :], scalar=gain_col, in1=hrs[0:d, :],
                                       op0=ALU.mult, op1=ALU.mult)
        else:
            P.dve.scalar_tensor_tensor(out=hnf[0:d, :], in0=hraw[0:d, :], scalar=gain_col, in1=hrs[0:d, :],
                                       op0=ALU.mult, op1=ALU.mult)
            P.act.copy(out=hnb[0:d, :], in_=hnf[0:d, :])
            ps3 = nxt_ps(6, 8)
            P.pe.matmul(out=ps3[0:d, :], lhsT=prh, rhs=hnb[0:d, :], start=True, stop=True)
            P.dve.tensor_tensor(out=ht1[0:d, :], in0=hnf[0:d, :], in1=Ctab[0:d, tok0:tok0 + 512], op=ALU.mult)
            P.dve.tensor_tensor(out=ht2[0:d, :], in0=ps3[0:d, :], in1=Stab[0:d, tok0:tok0 + 512], op=ALU.mult)
            P.pool.tensor_tensor(out=og[0:d, :], in0=ht1[0:d, :], in1=ht2[0:d, :], op=ALU.add)
        out_dma(dst_view, og[0:d, :])

    lat = P.sb("lat", [128, 3, 512], F32)
    latn = P.sb("latn", [128, 3, 512], BF16)

    def latent_norm(ps_list, gname):
        nch = len(ps_list)
        ps2 = nxt_ps(4, 6)
        for i, psv in enumerate(ps_list):
            P.act.activation(out=sq.full(), in_=psv, func=AF.Square)
            P.act.copy(out=lat[:, i, :], in_=psv)
            P.pe.matmul(out=ps2.full(), lhsT=ones.full(), rhs=sq.full(), start=(i == 0), stop=(i == nch - 1))
        rstd_from(ps2.full(), 128.0 * nch, 128, 512, rstd.full())
        for i in range(nch):
            P.dve.scalar_tensor_tensor(out=latn[:, i, :], in0=lat[:, i, :], scalar=C.col(gname, i), in1=rstd.full(),
                                       op0=ALU.mult, op1=ALU.mult)

    def small_w(name, dram, kc_n, ncols, i):
        stg = wst[i].full().re("p a b -> p (a b)")[:, 0:kc_n * ncols].re("p (a b) -> p a b", a=kc_n)
        bfb = P.sb(name, [128, kc_n, ncols], BF16)
        P.dma("gpsimd", out=stg, in_=dram.full().re("(kc p) n -> p kc n", p=128))
        P.pool.tensor_copy(out=bfb.full(), in_=stg)
        return bfb

    uqb = small_w("uqb", w_uq, 3, 768, 0)
    kpb = small_w("kpb", w_kp, 2, 768, 1)
    wvb = small_w("wvb", w_v, 2, 512, 0)
    main = tiles[1:]
    wb = load_w(0, 384)
    for ti, (c0, w) in enumerate(main):
        pss = []
        for ch in range(3):
            ps = nxt_ps(0, 4)
            proj(wb, ch * 128, 128, c0, 512, ps.full())
            pss.append(ps.full())
        latent_norm(pss, "g_cq")
        for h in range(8):
            ps = nxt_ps(0, 4)
            for kc in range(3):
                P.pe.matmul(out=ps[0:96, :], lhsT=uqb[:, kc, h * 96:(h + 1) * 96], rhs=latn[:, kc, :],
                            start=(kc == 0), stop=(kc == 2))
            headnorm(ps[0:96, :], 96, C.col("g_q"), True, ti * 512, o_qm[h, :, ti * 512:(ti + 1) * 512])
    wb = load_w(384, 288)
    krb = P.sb("krb", [32, 512], BF16)
    for ti, (c0, w) in enumerate(main):
        pss = []
        for ch in range(2):
            ps = nxt_ps(0, 4)
            proj(wb, ch * 128, 128, c0, 512, ps.full())
            pss.append(ps.full())
        ps = nxt_ps(0, 4)
        proj(wb, 256, 32, c0, 512, ps[0:32, :])
        P.act.copy(out=krb.full(), in_=ps[0:32, :])
        latent_norm(pss, "g_ckv")
        for h in range(8):
            ps = nxt_ps(0, 4)
            for kc in range(2):
                P.pe.matmul(out=ps[0:96, :], lhsT=kpb[:, kc, h * 96:(h + 1) * 96], rhs=latn[:, kc, :],
                            start=(kc == 0), stop=False)
            P.pe.matmul(out=ps[0:96, :], lhsT=sel, rhs=krb.full(), start=False, stop=True)
            headnorm(ps[0:96, :], 96, C.col("g_k"), True, ti * 512, o_km[h, :, ti * 512:(ti + 1) * 512])
        for ch in range(4):
            ps = nxt_ps(0, 4)
            for kc in range(2):
                P.pe.matmul(out=ps.full(), lhsT=wvb[:, kc, ch * 128:(ch + 1) * 128], rhs=latn[:, kc, :],
                            start=(kc == 0), stop=(kc == 1))
            og = next_ostg()
            P.act.copy(out=og.full(), in_=ps.full())
            out_dma(o_vm[ch * 128:(ch + 1) * 128, ti * 512:(ti + 1) * 512], og.full())
    for (base, gname, dst) in ((672, "g_fq", o_qf), (672 + 512, "g_fk", o_kf)):
        wb = load_w(base, 512)
        for ti, (c0, w) in enumerate(main):
            for h in range(8):
                ps = nxt_ps(0, 4)
                proj(wb, h * 64, 64, c0, 512, ps[0:64, :])
                headnorm(ps[0:64, :], 64, C.col(gname), False, ti * 512, dst[h, :, ti * 512:(ti + 1) * 512])
    def plain_group(base, ncols, func, bias_name, dst, dst_row0):
        wb = load_w(base, ncols)
        for ti, (c0, w) in enumerate(main):
            for ch in range(ncols // 128):
                ps = nxt_ps(0, 4)
                proj(wb, ch * 128, 128, c0, 512, ps.full())
                og = next_ostg()
                if bias_name is None:
                    P.act.activation(out=og.full(), in_=ps.full(), func=func)
                else:
                    P.act.activation(out=og.full(), in_=ps.full(), func=func,
                                     bias=C.col(bias_name, (dst_row0 // 128) + ch))
                out_dma(dst[dst_row0 + ch * 128:dst_row0 + (ch + 1) * 128, ti * 512:(ti + 1) * 512], og.full())

    plain_group(672 + 1024, 512, AF.Copy, None, o_vf, 0)
    FB = 672 + 1536
    SB = 672 + 1544
    wf = load_w(FB, 8)
    lf1 = P.sb("lf1", [16, 512], F32)
    lf2 = P.sb("lf2", [16, 512], F32)
    for ti, (c0, w) in enumerate(main):
        ps = nxt_ps(0, 4)
        proj(wf, 0, 8, c0, 512, ps[0:8, :])
        P.act.activation(out=lf1[0:8, :], in_=ps[0:8, :], func=AF.Exp, bias=nbf[0:8, 0:1], scale=-1.0)
        P.act.activation(out=lf1[0:8, :], in_=lf1[0:8, :], func=AF.Ln, bias=one1[0:8, 0:1], scale=1.0)
        P.dve.tensor_scalar(out=lf2[0:8, :], in0=lf1[0:8, :], scalar1=-1.0, scalar2=None, op0=ALU.mult)
        P.dma("sync", out=o_lf[:, ti * 512:(ti + 1) * 512], in_=lf2[0:8, :])
    wd = load_w(SB + 1024 + 1536, 16)
    dt1 = P.sb("dt1", [16, 512], F32)
    dt2 = P.sb("dt2", [16, 512], F32)
    for ti, (c0, w) in enumerate(main):
        ps = nxt_ps(0, 4)
        proj(wd, 0, 16, c0, 512, ps[0:16, :])
        P.act.activation(out=dt1.full(), in_=ps[0:16, :], func=AF.Exp, bias=C.col("dt_b"), scale=1.0)
        P.act.activation(out=dt1.full(), in_=dt1.full(), func=AF.Ln, bias=one1[0:16, 0:1], scale=1.0)
        P.dma("sync", out=o_dt[:, ti * 512:(ti + 1) * 512], in_=dt1.full())
        P.dve.tensor_scalar(out=dt2.full(), in0=dt1.full(), scalar1=Aneg[:, 0:1], scalar2=None, op0=ALU.mult)
        P.dma("sync", out=o_a[:, ti * 512:(ti + 1) * 512], in_=dt2.full())
    for blk in range(2):
        plain_group(SB + blk * 512, 512, AF.Silu, None, o_sz, blk * 512)
    upre = P.sb("upre", [128, 516], F32)
    carry = P.sb("carry", [128, 12, 4], F32)
    acc = [P.sb(f"acc{i}", [128, 512], F32) for i in range(2)]
    for blk in range(3):
        wb = load_w(SB + 1024 + blk * 512, 512)
        for ch in range(4):
            cg = blk * 4 + ch
            ps = nxt_ps(0, 4)
            proj(wb, ch * 128, 128, 0, HALO, ps[:, 0:HALO])
            P.act.copy(out=carry[:, cg, :], in_=ps[:, 0:HALO])
        for ti, (c0, w) in enumerate(main):
            for ch in range(4):
                cg = blk * 4 + ch
                ps = nxt_ps(0, 4)
                proj(wb, ch * 128, 128, c0, 512, ps.full())
                P.act.copy(out=upre[:, 4:516], in_=ps.full())
                P.dve.tensor_copy(out=upre[:, 0:4], in_=carry[:, cg, :])
                P.pool.tensor_copy(out=carry[:, cg, :], in_=upre[:, 512:516])
                a0 = acc[0]
                P.dve.tensor_scalar(out=a0.full(), in0=upre[:, 4:516], scalar1=C.col("cw3", cg), scalar2=C.col("cb", cg),
                                    op0=ALU.mult, op1=ALU.add)
                for k in range(3):
                    P.dve.scalar_tensor_tensor(out=a0.full(), in0=upre[:, 1 + k:513 + k], scalar=C.col(f"cw{k}", cg),
                                               in1=a0.full(), op0=ALU.mult, op1=ALU.add)
                og = next_ostg()
                P.act.activation(out=og.full(), in_=a0.full(), func=AF.Silu)
                out_dma(o_xbc[cg * 128:(cg + 1) * 128, ti * 512:(ti + 1) * 512], og.full())
    GB = SB + 2576
    for blk in range(6):
        plain_group(GB + blk * 512, 512, AF.Sigmoid, "b_gate", o_g, blk * 512)
    P.emit()
    return nc, P


def _bf(a):
    return np.asarray(a).astype(np.float32)


_PROG_CACHE = {}


def _const_mats():
    m = np.zeros((128, 192), np.float32)
    for i in range(16):
        m[80 + i, 64 + i] = -1.0
        m[64 + i, 80 + i] = 1.0
    for i in range(32):
        m[i, 96 + 64 + i] = 1.0
    return m


def run_A(inp, l, x_full, pos_full):
    cp = a_colpack(inp, l)
    off = dict(cp.off)
    off["_n"] = cp.n
    if "A" not in _PROG_CACHE:
        _PROG_CACHE["A"] = build_A(off)[0]
    nc = _PROG_CACHE["A"]
    cst = cp.array()
    wukv = inp["mla_w_ukv"][l].reshape(256, 8, 128)
    w_kp = np.zeros((256, 8, 96), np.float32)
    w_kp[:, :, 0:64] = wukv[:, :, 0:64]
    w_v = np.ascontiguousarray(wukv[:, :, 64:128].reshape(256, 512))
    mats = _const_mats()
    xf = x_full.reshape(16384, D)
    in_maps = []
    for c in range(8):
        t0 = c * T
        xt = np.zeros((D, T + HALO), np.float32)
        xt[:, HALO:] = xf[t0:t0 + T].T
        if c % 4 != 0:
            xt[:, 0:HALO] = xf[t0 - HALO:t0].T
        in_maps.append({
            "xT": np.ascontiguousarray(xt),
            "pos": np.ascontiguousarray(pos_full.reshape(1, 16384)[:, t0:t0 + T]).astype(np.int32),
            "w_in": np.ascontiguousarray(inp["w_in"][l]),
            "w_uq": np.ascontiguousarray(inp["mla_w_uq"][l]),
            "w_kp": np.ascontiguousarray(w_kp.reshape(256, 768)),
            "w_v": w_v, "cst": cst, "mats": mats,
        })
    res = run_bass_kernel_spmd(nc, in_maps, core_ids=list(range(8)))
    return res.results


S_ = 8192
NKT = S_ // 128
NQT = S_ // 512


def build_BC():
    nc = new_nc()
    P = Prog(nc)
    EI, EO = "ExternalInput", "ExternalOutput"
    qm = P.dram("qm", [2, 96, S_], BF16, EI)
    km = P.dram("km", [2, 96, S_], BF16, EI)
    vm = P.dram("vm", [2, 128, NKT, 64], BF16, EI)
    qf = P.dram("qf", [2, 64, S_], BF16, EI)
    kf = P.dram("kf", [2, 64, S_], BF16, EI)
    vf = P.dram("vf", [2, 128, NKT, 64], BF16, EI)
    lf = P.dram("lf", [2, 128, NKT], F32, EI)
    msk = P.dram("msk", [128, 8, 512], F32, EI)
    cm = P.dram("cm", [128, 4, 128], F32, EI)
    x_tm = P.dram("x_tm", [128, NKT, 256], BF16, EI)
    B_tm = P.dram("B_tm", [128, NKT, 128], BF16, EI)
    BT = P.dram("BT", [128, S_], BF16, EI)
    CT = P.dram("CT", [128, S_], BF16, EI)
    dt_tm = P.dram("dt_tm", [128, NKT, 4], F32, EI)
    a_tm = P.dram("a_tm", [128, NKT, 4], F32, EI)
    Dv = P.dram("Dv", [128, 4], F32, EI)
    o_m = P.dram("o_m", [2, 64, S_], BF16, EO)
    o_f = P.dram("o_f", [2, 64, S_], BF16, EO)
    o_y = P.dram("o_y", [128, NKT, 256], F32, EO)
    fsc = P.dram("fsc", [3, S_], BF16)

    cmb = P.sb("cmb", [128, 4, 128], F32)
    P.dma("sync", out=cmb.full(), in_=cm.full())
    tri, trimask, ident, ones = cmb[:, 0, :], cmb[:, 1, :], cmb[:, 2, :], cmb[:, 3, :]
    mskb = P.sb("mskb", [128, 8, 512], F32)
    P.dma("gpsimd", out=mskb.full(), in_=msk.full())
    zero = P.sb("zero", [128, 1], F32)
    P.dve.memset(ap=zero.full(), constant=0.0)

    pb = [P.ps(f"pb{i}", [128, 512], F32) for i in range(8)]
    K_sb = P.sb("K_sb", [128, S_], BF16)
    Q_sb = P.sb("Q_sb", [128, S_], BF16)
    V_sb = P.sb("V_sb", [128, NKT, 128], BF16)
    P.dve.memset(ap=V_sb[:, :, 64:128], constant=1.0)
    pt = [P.sb(f"pt{i}", [128, 512], BF16) for i in range(3)]
    mt = [P.sb(f"mt{i}", [128, 512], F32) for i in range(2)]
    rl = P.sb("rl", [128, 512], F32)
    rl2 = P.sb("rl2", [64, 512], F32)
    ot = [P.sb(f"ot{i}", [64, 512], BF16) for i in range(2)]
    negF = P.sb("negF", [128, NKT], F32)

    cnt = [0, 0, 0]

    def attention(dk, scale, mask0, bias_fn, out_dram_h):
        for qt in range(NQT):
            oacc = pb[3 + qt % 2]
            nk = 4 * qt + 4
            for kt in range(nk):
                i3 = cnt[0] % 3
                cnt[0] += 1
                ps = pb[i3]
                P.pe.matmul(out=ps.full(), lhsT=K_sb[0:dk, kt * 128:(kt + 1) * 128],
                            rhs=Q_sb[0:dk, qt * 512:(qt + 1) * 512], start=True, stop=True)
                if kt >= 4 * qt:
                    m = mt[cnt[1] % 2]
                    cnt[1] += 1
                    P.dve.tensor_tensor(out=m.full(), in0=ps.full(), in1=mskb[:, mask0 + kt - 4 * qt, :], op=ALU.add)
                    src = m.full()
                else:
                    src = ps.full()
                P.act.activation(out=pt[i3].full(), in_=src, func=AF.Exp, scale=scale, bias=bias_fn(kt))
                P.pe.matmul(out=oacc.full(), lhsT=V_sb[:, kt, :], rhs=pt[i3].full(), start=(kt == 0), stop=(kt == nk - 1))
            P.dve.reciprocal(out=rl[64:128, :], in_=oacc[64:128, :])
            P.dve.tensor_copy(out=rl2.full(), in_=rl[64:128, :])
            o = ot[qt % 2]
            P.dve.tensor_tensor(out=o.full(), in0=oacc[0:64, :], in1=rl2.full(), op=ALU.mult)
            P.dma("sync", out=out_dram_h[:, qt * 512:(qt + 1) * 512], in_=o.full())

    for h in range(2):
        P.dma("sync", out=K_sb[0:96, :], in_=km[h])
        P.dma("gpsimd", out=Q_sb[0:96, :], in_=qm[h])
        P.dma("sync", out=V_sb[:, :, 0:64], in_=vm[h])
        attention(96, 96.0 ** -0.5, 0, lambda kt: zero[:, 0:1], o_m[h])

    lfs = P.sb("lfs", [128, NKT], F32)
    wi = P.sb("wi", [128, NKT], F32)
    sc = [P.sb(f"sc{i}", [128, NKT], F32) for i in range(2)]
    Ff = P.sb("Ff", [128, NKT], F32)
    FT = P.sb("FT", [64, 128], F32)
    r1 = P.sb("r1", [64, 128], F32)
    fh = [P.sb(f"fh{i}", [64, 128], BF16) for i in range(3)]
    for h in range(2):
        P.dma("sync", out=lfs.full(), in_=lf[h])
        ps = pb[5]
        P.pe.matmul(out=ps[:, 0:NKT], lhsT=tri, rhs=lfs.full(), start=True, stop=True)
        P.act.copy(out=wi.full(), in_=ps[:, 0:NKT])
        ps = pb[6]
        P.pe.matmul(out=ps[:, 0:NKT], lhsT=ones, rhs=lfs.full(), start=True, stop=True)
        P.act.copy(out=sc[0].full(), in_=ps[:, 0:NKT])
        P.dve.tensor_tensor(out=wi.full(), in0=wi.full(), in1=sc[0].full(), op=ALU.subtract)
        cur = 0
        d = 1
        while d < NKT:
            nx = 1 - cur
            P.dve.tensor_copy(out=sc[nx][:, 0:d], in_=sc[cur][:, 0:d])
            P.dve.tensor_tensor(out=sc[nx][:, d:NKT], in0=sc[cur][:, d:NKT], in1=sc[cur][:, 0:NKT - d], op=ALU.add)
            cur = nx
            d *= 2
        P.dve.tensor_tensor(out=Ff.full(), in0=wi.full(), in1=sc[cur].full(), op=ALU.add)
        P.dve.tensor_scalar(out=negF.full(), in0=Ff.full(), scalar1=-1.0, scalar2=None, op0=ALU.mult)
        ps = pb[7]
        P.pe.transpose(out=ps[0:64, 0:128], in_=Ff.full(), identity=ident)
        P.act.copy(out=FT.full(), in_=ps[0:64, 0:128])
        P.dve.tensor_copy(out=fh[0].full(), in_=FT.full())
        P.dve.tensor_tensor(out=r1.full(), in0=FT.full(), in1=fh[0].full(), op=ALU.subtract)
        P.dve.tensor_copy(out=fh[1].full(), in_=r1.full())
        P.dve.tensor_tensor(out=r1.full(), in0=r1.full(), in1=fh[1].full(), op=ALU.subtract)
        P.dve.tensor_copy(out=fh[2].full(), in_=r1.full())
        for r in range(3):
            P.dma("sync", out=fsc[r].re("(kt p) -> kt p", p=128), in_=fh[r].full())
        P.dma("sync", out=K_sb[0:64, :], in_=kf[h])
        P.dve.memset(ap=K_sb[64:67, :], constant=8.0)
        P.dma("gpsimd", out=Q_sb[0:64, :], in_=qf[h])
        P.dma("gpsimd", out=Q_sb[64:67, :], in_=fsc.full())
        P.dma("sync", out=V_sb[:, :, 0:64], in_=vf[h])
        attention(67, 0.125, 4, lambda kt: negF[:, kt:kt + 1], o_f[h])

    a_sb = P.sb("a_sb", [128, NKT, 4], F32)
    dt_sb = P.sb("dt_sb", [128, NKT, 4], F32)
    Dsb = P.sb("Dsb", [128, 4], F32)
    P.dma("sync", out=a_sb.full(), in_=a_tm.full())
    P.dma("sync", out=dt_sb.full(), in_=dt_tm.full())
    P.dma("sync", out=Dsb.full(), in_=Dv.full())
    BTs = K_sb
    CTs = Q_sb
    P.dma("sync", out=BTs.full(), in_=BT.full())
    P.dma("gpsimd", out=CTs.full(), in_=CT.full())
    Acum = P.sb("Acum", [128, NKT, 4], F32)
    nAcum = P.sb("nAcum", [128, NKT, 4], F32)
    Atot = P.sb("Atot", [128, NKT, 4], F32)
    eA = P.sb("eA", [128, NKT, 4], F32)
    wdec = P.sb("wdec", [128, NKT, 4], F32)
    eAtot = P.sb("eAtot", [128, NKT, 4], F32)
    fl = lambda b: b.full().re("p c h -> p (c h)")
    ps = pb[0]
    P.pe.matmul(out=ps[:, 0:256], lhsT=tri, rhs=fl(a_sb), start=True, stop=True)
    P.act.copy(out=fl(Acum), in_=ps[:, 0:256])
    ps = pb[1]
    P.pe.matmul(out=ps[:, 0:256], lhsT=ones, rhs=fl(a_sb), start=True, stop=True)
    P.act.copy(out=fl(Atot), in_=ps[:, 0:256])
    P.dve.tensor_scalar(out=fl(nAcum), in0=fl(Acum), scalar1=-1.0, scalar2=None, op0=ALU.mult)
    P.act.activation(out=fl(eA), in_=fl(Acum), func=AF.Exp)
    P.act.activation(out=fl(eAtot), in_=fl(Atot), func=AF.Exp)
    P.dve.tensor_tensor(out=fl(wdec), in0=fl(Atot), in1=fl(Acum), op=ALU.subtract)
    P.act.activation(out=fl(wdec), in_=fl(wdec), func=AF.Exp)

    Hs = P.sb("Hs", [128, 256], F32)
    Hb = P.sb("Hb", [128, 256], BF16)
    P.dve.memset(ap=Hs.full(), constant=0.0)
    P.dve.memset(ap=Hb.full(), constant=0.0)
    xc = [P.sb(f"xc{i}", [128, 256], BF16) for i in range(2)]
    Bc = [P.sb(f"Bc{i}", [128, 128], BF16) for i in range(2)]
    cb = P.sb("cb", [128, 128], F32)
    xdt = P.sb("xdt", [128, 256], BF16)
    xdts = P.sb("xdts", [128, 256], BF16)
    at = [P.sb(f"at{i}", [128, 128], F32) for i in range(2)]
    tm = [P.sb(f"tm{i}", [128, 128], F32) for i in range(2)]
    dec = [P.sb(f"dec{i}", [128, 128], F32) for i in range(2)]
    MT = [P.sb(f"MT{i}", [128, 128], BF16) for i in range(2)]
    t1 = P.sb("t1", [128, 256], F32)
    t3 = P.sb("t3", [128, 256], F32)
    yo = [P.sb(f"yo{i}", [128, 256], F32) for i in range(2)]
    v3 = lambda v: v.re("p (h d) -> p h d", h=4)
    bc3 = lambda v: v.f(lambda a: a.unsqueeze(2).to_broadcast([128, 4, 64]))
    for c in range(NKT):
        x_c = xc[c % 2]
        B_c = Bc[c % 2]
        P.dma("sync", out=x_c.full(), in_=x_tm[:, c, :])
        P.dma("gpsimd", out=B_c.full(), in_=B_tm[:, c, :])
        BT_c = BTs[:, c * 128:(c + 1) * 128]
        CT_c = CTs[:, c * 128:(c + 1) * 128]
        ps_cb = pb[0]
        P.pe.matmul(out=ps_cb[:, 0:128], lhsT=BT_c, rhs=CT_c, start=True, stop=True)
        P.act.copy(out=cb.full(), in_=ps_cb[:, 0:128])
        P.dve.tensor_tensor(out=v3(xdt.full()), in0=v3(x_c.full()), in1=bc3(dt_sb[:, c, :]), op=ALU.mult)
        P.pool.tensor_tensor(out=v3(xdts.full()), in0=v3(xdt.full()), in1=bc3(wdec[:, c, :]), op=ALU.mult)
        ps_off = pb[1]
        P.pe.matmul(out=ps_off[:, 0:256], lhsT=CT_c, rhs=Hb.full(), start=True, stop=True)
        ps_y = pb[2]
        for h in range(4):
            i2 = h % 2
            P.dve.tensor_scalar(out=at[i2].full(), in0=tri, scalar1=a_sb[:, c, h:h + 1], scalar2=None, op0=ALU.mult)
            ps_A = pb[3 + i2]
            P.pe.matmul(out=ps_A[:, 0:128], lhsT=ones, rhs=at[i2].full(), start=True, stop=True)
            P.dve.tensor_tensor(out=tm[i2].full(), in0=ps_A[:, 0:128], in1=trimask, op=ALU.add)
            P.act.activation(out=dec[i2].full(), in_=tm[i2].full(), func=AF.Exp, bias=nAcum[:, c, h:h + 1], scale=1.0)
            P.pool.tensor_tensor(out=MT[i2].full(), in0=cb.full(), in1=dec[i2].full(), op=ALU.mult)
            P.pe.matmul(out=ps_y[:, h * 64:(h + 1) * 64], lhsT=MT[i2].full(), rhs=xdt[:, h * 64:(h + 1) * 64],
                        start=True, stop=True)
        P.dve.tensor_tensor(out=v3(t1.full()), in0=v3(ps_off[:, 0:256]), in1=bc3(eA[:, c, :]), op=ALU.mult)
        P.dve.tensor_tensor(out=t1.full(), in0=t1.full(), in1=ps_y[:, 0:256], op=ALU.add)
        P.pool.tensor_tensor(out=v3(t3.full()), in0=v3(x_c.full()), in1=bc3(Dsb.full()), op=ALU.mult)
        y_ = yo[c % 2]
        P.pool.tensor_tensor(out=y_.full(), in0=t1.full(), in1=t3.full(), op=ALU.add)
        P.dma("sync", out=o_y[:, c, :], in_=y_.full())
        ps_h = pb[5]
        P.pe.matmul(out=ps_h[:, 0:256], lhsT=B_c.full(), rhs=xdts.full(), start=True, stop=True)
        P.dve.tensor_tensor(out=v3(Hs.full()), in0=v3(Hs.full()), in1=bc3(eAtot[:, c, :]), op=ALU.mult)
        P.dve.tensor_tensor(out=Hs.full(), in0=Hs.full(), in1=ps_h[:, 0:256], op=ALU.add)
        P.act.copy(out=Hb.full(), in_=Hs.full())
    P.emit()
    return nc, P


def _bc_consts():
    msk = np.zeros((128, 8, 512), np.float32)
    p = np.arange(128)[:, None]
    q = np.arange(512)[None, :]
    for j in range(4):
        key = j * 128 + p
        msk[:, j, :] = np.where((key // 64) > (q // 64), NEG, 0.0)
        msk[:, 4 + j, :] = np.where(key > q, NEG, 0.0)
    cm = np.zeros((128, 4, 128), np.float32)
    jj = np.arange(128)[:, None]
    ii = np.arange(128)[None, :]
    cm[:, 0, :] = (jj <= ii).astype(np.float32)
    cm[:, 1, :] = np.where(jj > ii, NEG, 0.0)
    cm[:, 2, :] = np.eye(128, dtype=np.float32)
    cm[:, 3, :] = 1.0
    return msk, cm


def _tm(a):
    S, n = a.shape
    return np.ascontiguousarray(a.reshape(S // 128, 128, n).transpose(1, 0, 2))


def run_BC(inp, l, resA):
    if "BC" not in _PROG_CACHE:
        _PROG_CACHE["BC"] = build_BC()[0]
    nc = _PROG_CACHE["BC"]
    msk, cm = _bc_consts()

    def gather(name, b):
        return np.concatenate([np.asarray(resA[b * 4 + i][name]) for i in range(4)], axis=-1)

    in_maps = []
    for c in range(8):
        b, hg = c // 4, c % 4
        qm = gather("o_qm", b)[2 * hg:2 * hg + 2]
        km = gather("o_km", b)[2 * hg:2 * hg + 2]
        vmf = gather("o_vm", b)
        qf = gather("o_qf", b)[2 * hg:2 * hg + 2]
        kf = gather("o_kf", b)[2 * hg:2 * hg + 2]
        vff = gather("o_vf", b)
        lff = gather("o_lf", b)
        xbc = gather("o_xbc", b)
        dtf = gather("o_dt", b)
        af = gather("o_a", b)
        g = hg // 2
        vm = np.stack([_tm(vmf[(2 * hg + h) * 64:(2 * hg + h + 1) * 64].T) for h in range(2)])
        vf = np.stack([_tm(vff[(2 * hg + h) * 64:(2 * hg + h + 1) * 64].T) for h in range(2)])
        lf = np.stack([np.ascontiguousarray(lff[2 * hg + h].reshape(NKT, 128).T) for h in range(2)])
        x_tm = _tm(xbc[hg * 256:(hg + 1) * 256].T)
        Bf = xbc[1024 + g * 128:1024 + (g + 1) * 128]
        Cf = xbc[1280 + g * 128:1280 + (g + 1) * 128]
        Dv = np.broadcast_to(inp["ssm_D"][l][4 * hg:4 * hg + 4][None, :], (128, 4)).astype(np.float32)
        in_maps.append({
            "qm": np.ascontiguousarray(qm), "km": np.ascontiguousarray(km), "vm": vm,
            "qf": np.ascontiguousarray(qf), "kf": np.ascontiguousarray(kf), "vf": vf, "lf": lf,
            "msk": msk, "cm": cm, "x_tm": x_tm, "B_tm": _tm(Bf.T), "BT": np.ascontiguousarray(Bf),
            "CT": np.ascontiguousarray(Cf), "dt_tm": _tm(dtf[4 * hg:4 * hg + 4].T),
            "a_tm": _tm(af[4 * hg:4 * hg + 4].T), "Dv": np.ascontiguousarray(Dv),
        })
    res = run_bass_kernel_spmd(nc, in_maps, core_ids=list(range(8))).results
    om = np.zeros((2, 512, S_), NBF)
    of = np.zeros((2, 512, S_), NBF)
    y = np.zeros((2, 1024, S_), np.float32)
    for c in range(8):
        b, hg = c // 4, c % 4
        om[b, hg * 128:(hg + 1) * 128] = np.asarray(res[c]["o_m"]).reshape(128, S_)
        of[b, hg * 128:(hg + 1) * 128] = np.asarray(res[c]["o_f"]).reshape(128, S_)
        yy = np.asarray(res[c]["o_y"])
        y[b, hg * 256:(hg + 1) * 256] = yy.transpose(2, 1, 0).reshape(256, S_)
    return om, of, y


def build_D1():
    nc = new_nc()
    P = Prog(nc)
    EI, EO = "ExternalInput", "ExternalOutput"
    NT = T // 512
    omT = P.dram("omT", [512, T], BF16, EI)
    ofT = P.dram("ofT", [512, T], BF16, EI)
    yT = P.dram("yT", [1024, T], F32, EI)
    szT = P.dram("szT", [1024, T], BF16, EI)
    gT = P.dram("gT", [3072, T], BF16, EI)
    xT = P.dram("xT", [D, T], F32, EI)
    w_a = P.dram("w_a", [512, D], F32, EI)
    w_b = P.dram("w_b", [512, D], F32, EI)
    w_c = P.dram("w_c", [1024, D], F32, EI)
    w_o = P.dram("w_o", [1024, D], F32, EI)
    cst_d = P.dram("cst", [128, 8], F32, EI)
    o_x = P.dram("o_x", [D, T], F32, EO)

    cstb = P.sb("cstb", [128, 8], F32)
    P.dma("sync", out=cstb.full(), in_=cst_d.full())
    ones = P.sb("ones", [128, 128], F32)
    P.dve.memset(ap=ones.full(), constant=1.0)
    eps = P.sb("eps", [128, 1], F32)
    P.dve.memset(ap=eps.full(), constant=1e-6)
    pb = [P.ps(f"pb{i}", [128, 512], F32) for i in range(8)]
    wst = [P.sb(f"wst{i}", [128, 4, 1024], F32) for i in range(2)]
    wcnt = [0]

    def load_w(dram, kc_n, name):
        bfb = P.sb(name, [128, kc_n, D], BF16)
        v = dram.full().re("(kc p) n -> p kc n", p=128)
        for k0 in range(0, kc_n, 4):
            i = wcnt[0] % 2
            wcnt[0] += 1
            P.dma("sync" if i == 0 else "gpsimd", out=wst[i].full(), in_=v[:, k0:k0 + 4, :])
            P.pool.tensor_copy(out=bfb[:, k0:k0 + 4, :], in_=wst[i].full())
        return bfb

    Wa = load_w(w_a, 4, "Wa")
    Wb = load_w(w_b, 4, "Wb")
    Wc = load_w(w_c, 8, "Wc")
    Wo = load_w(w_o, 8, "Wo")

    ys = P.sb("ys", [128, 8, 512], F32)
    szs = P.sb("szs", [128, 8, 512], BF16)
    yn = P.sb("yn", [128, 8, 512], BF16)
    oms = P.sb("oms", [128, 4, 512], BF16)
    ofs = P.sb("ofs", [128, 4, 512], BF16)
    gs = P.sb("gs", [128, 24, 512], BF16)
    xs = P.sb("xs", [128, 8, 512], F32)
    sq = P.sb("sq", [128, 512], F32)
    rstd = P.sb("rstd", [128, 512], F32)
    m1 = [P.sb(f"m1_{i}", [128, 512], F32) for i in range(2)]
    m2 = [P.sb(f"m2_{i}", [128, 512], F32) for i in range(2)]
    m3 = [P.sb(f"m3_{i}", [128, 512], F32) for i in range(2)]
    mg = P.sb("mg", [128, 8, 512], BF16)
    xo = [P.sb(f"xo{i}", [128, 512], F32) for i in range(2)]
    ch = lambda d: d.full().re("(kc p) n -> p kc n", p=128)
    for ti in range(NT):
        ts = slice(ti * 512, (ti + 1) * 512)
        P.dma("sync", out=ys.full(), in_=ch(yT)[:, :, ts])
        P.dma("gpsimd", out=szs.full(), in_=ch(szT)[:, :, ts])
        P.dma("sync", out=oms.full(), in_=ch(omT)[:, :, ts])
        P.dma("gpsimd", out=ofs.full(), in_=ch(ofT)[:, :, ts])
        P.dma("sync", out=gs.full(), in_=ch(gT)[:, :, ts])
        P.dma("gpsimd", out=xs.full(), in_=ch(xT)[:, :, ts])
        ps = pb[7]
        for kc in range(8):
            P.dve.tensor_tensor(out=ys[:, kc, :], in0=ys[:, kc, :], in1=szs[:, kc, :], op=ALU.mult)
            P.act.activation(out=sq.full(), in_=ys[:, kc, :], func=AF.Square)
            P.pe.matmul(out=ps.full(), lhsT=ones.full(), rhs=sq.full(), start=(kc == 0), stop=(kc == 7))
        P.act.activation(out=rstd.full(), in_=ps.full(), func=AF.Sqrt, bias=eps[:, 0:1], scale=1.0 / 1024.0)
        P.dve.reciprocal(out=rstd.full(), in_=rstd.full())
        for kc in range(8):
            P.dve.scalar_tensor_tensor(out=yn[:, kc, :], in0=ys[:, kc, :], scalar=cstb[:, kc:kc + 1], in1=rstd.full(),
                                       op0=ALU.mult, op1=ALU.mult)
        for oc in range(8):
            i2 = oc % 2
            osl = slice(oc * 128, (oc + 1) * 128)
            pa, pbb, pc = pb[0 + i2 * 3], pb[1 + i2 * 3], pb[2 + i2 * 3]
            for kc in range(4):
                P.pe.matmul(out=pa.full(), lhsT=Wa[:, kc, osl], rhs=oms[:, kc, :], start=(kc == 0), stop=(kc == 3))
            for kc in range(4):
                P.pe.matmul(out=pbb.full(), lhsT=Wb[:, kc, osl], rhs=ofs[:, kc, :], start=(kc == 0), stop=(kc == 3))
            for kc in range(8):
                P.pe.matmul(out=pc.full(), lhsT=Wc[:, kc, osl], rhs=yn[:, kc, :], start=(kc == 0), stop=(kc == 7))
            P.dve.tensor_tensor(out=m1[i2].full(), in0=pa.full(), in1=gs[:, oc, :], op=ALU.mult)
            P.dve.tensor_tensor(out=m2[i2].full(), in0=pbb.full(), in1=gs[:, 8 + oc, :], op=ALU.mult)
            P.dve.tensor_tensor(out=m3[i2].full(), in0=pc.full(), in1=gs[:, 16 + oc, :], op=ALU.mult)
            P.pool.tensor_tensor(out=m1[i2].full(), in0=m1[i2].full(), in1=m2[i2].full(), op=ALU.add)
            P.pool.tensor_tensor(out=mg[:, oc, :], in0=m1[i2].full(), in1=m3[i2].full(), op=ALU.add)
        for oc in range(8):
            i2 = oc % 2
            ps = pb[6 + i2]
            for kc in range(8):
                P.pe.matmul(out=ps.full(), lhsT=Wo[:, kc, oc * 128:(oc + 1) * 128], rhs=mg[:, kc, :],
                            start=(kc == 0), stop=(kc == 7))
            P.dve.tensor_tensor(out=xo[i2].full(), in0=ps.full(), in1=xs[:, oc, :], op=ALU.add)
            P.dma("sync" if i2 else "gpsimd", out=o_x[oc * 128:(oc + 1) * 128, ts], in_=xo[i2].full())
    P.emit()
    return nc, P


def d2_colpack(inp, l):
    cp = ColPack()
    cp.add("g_ffn", inp["norm_ffn_g"][l])
    cw = inp["ffn_conv_w"][l]
    for k in range(3):
        cp.add(f"fw{k}", cw[k])
    cp.add("fb", inp["ffn_conv_b"][l])
    return cp


def build_D2(off):
    nc = new_nc()
    P = Prog(nc)
    EI, EO = "ExternalInput", "ExternalOutput"
    NT = T // 512
    TT = T + HALO
    xT = P.dram("xT", [D, TT], F32, EI)
    w_up = P.dram("w_up", [D, 5632], F32, EI)
    w_dn = P.dram("w_dn", [2816, D], F32, EI)
    cst_d = P.dram("cst", [128, off["_n"]], F32, EI)
    o_x = P.dram("o_x", [D, T], F32, EO)

    cstb = P.sb("cstb", [128, off["_n"]], F32)
    C = Cst(P, cstb, off)
    P.dma("sync", out=cstb.full(), in_=cst_d.full())
    ones = P.sb("ones", [128, 128], F32)
    P.dve.memset(ap=ones.full(), constant=1.0)
    eps = P.sb("eps", [128, 1], F32)
    P.dve.memset(ap=eps.full(), constant=1e-6)
    pb = [P.ps(f"pb{i}", [128, 512], F32) for i in range(8)]
    wst = [P.sb(f"wst{i}", [128, 1024], F32) for i in range(2)]
    wcnt = [0]
    Wu = P.sb("Wu", [128, 8, 5632], BF16)
    Wd = P.sb("Wd", [128, 22, D], BF16)
    wuv = w_up.full().re("(kc p) n -> p kc n", p=128)
    for kc in range(8):
        for c0 in range(0, 5632, 1024):
            n = min(1024, 5632 - c0)
            i = wcnt[0] % 2
            wcnt[0] += 1
            P.dma("sync" if i == 0 else "gpsimd", out=wst[i][:, 0:n], in_=wuv[:, kc, c0:c0 + n])
            P.pool.tensor_copy(out=Wu[:, kc, c0:c0 + n], in_=wst[i][:, 0:n])
    wdv = w_dn.full().re("(kc p) n -> p kc n", p=128)
    for k0 in range(22):
        i = wcnt[0] % 2
        wcnt[0] += 1
        P.dma("sync" if i == 0 else "gpsimd", out=wst[i].full(), in_=wdv[:, k0, :])
        P.pool.tensor_copy(out=Wd[:, k0, :], in_=wst[i].full())

    xst = P.sb("xst", [128, 8, 512], F32)
    hn = P.sb("hn", [128, 8, 512], BF16)
    sq = P.sb("sq", [128, 512], F32)
    rstd = P.sb("rstd", [128, 512], F32)
    act = P.sb("act", [128, 22, 512], BF16)
    upre = [P.sb(f"upre{i}", [128, 516], F32) for i in range(2)]
    acc = [P.sb(f"acc{i}", [128, 512], F32) for i in range(2)]
    sg = P.sb("sg", [128, 512], F32)
    carry = P.sb("carry", [128, 44, 4], F32)
    xo = [P.sb(f"xo{i}", [128, 512], F32) for i in range(2)]
    xTv = xT.full().re("(kc p) n -> p kc n", p=128)
    tiles = [(0, HALO)] + [(HALO + i * 512, 512) for i in range(NT)]
    pcnt = [0]
    for tix, (c0, w) in enumerate(tiles):
        P.dma("sync", out=xst[:, :, 0:w], in_=xTv[:, :, c0:c0 + w])
        ps = pb[7]
        for kc in range(8):
            P.act.activation(out=sq[:, 0:w], in_=xst[:, kc, 0:w], func=AF.Square)
            P.pe.matmul(out=ps[:, 0:w], lhsT=ones.full(), rhs=sq[:, 0:w], start=(kc == 0), stop=(kc == 7))
        P.act.activation(out=rstd[:, 0:w], in_=ps[:, 0:w], func=AF.Sqrt, bias=eps[:, 0:1], scale=1.0 / 1024.0)
        P.dve.reciprocal(out=rstd[:, 0:w], in_=rstd[:, 0:w])
        for kc in range(8):
            P.dve.scalar_tensor_tensor(out=hn[:, kc, 0:w], in0=xst[:, kc, 0:w], scalar=C.col("g_ffn", kc),
                                       in1=rstd[:, 0:w], op0=ALU.mult, op1=ALU.mult)
        for i in range(22):
            accs = []
            for j, cg in enumerate((i, 22 + i)):
                ps = pb[pcnt[0] % 4]
                pcnt[0] += 1
                for kc in range(8):
                    P.pe.matmul(out=ps[:, 0:w], lhsT=Wu[:, kc, cg * 128:(cg + 1) * 128], rhs=hn[:, kc, 0:w],
                                start=(kc == 0), stop=(kc == 7))
                if tix == 0:
                    P.act.copy(out=carry[:, cg, :], in_=ps[:, 0:HALO])
                    continue
                up = upre[j]
                P.act.copy(out=up[:, 4:516], in_=ps.full())
                P.dve.tensor_copy(out=up[:, 0:4], in_=carry[:, cg, :])
                P.pool.tensor_copy(out=carry[:, cg, :], in_=up[:, 512:516])
                a0 = acc[j]
                P.dve.tensor_scalar(out=a0.full(), in0=up[:, 4:516], scalar1=C.col("fw2", cg), scalar2=C.col("fb", cg),
                                    op0=ALU.mult, op1=ALU.add)
                P.dve.scalar_tensor_tensor(out=a0.full(), in0=up[:, 3:515], scalar=C.col("fw1", cg), in1=a0.full(),
                                           op0=ALU.mult, op1=ALU.add)
                P.dve.scalar_tensor_tensor(out=a0.full(), in0=up[:, 2:514], scalar=C.col("fw0", cg), in1=a0.full(),
                                           op0=ALU.mult, op1=ALU.add)
                accs.append(a0)
            if tix == 0:
                continue
            P.act.activation(out=sg.full(), in_=accs[0].full(), func=AF.Silu)
            P.pool.tensor_tensor(out=act[:, i, :], in0=sg.full(), in1=accs[1].full(), op=ALU.mult)
        if tix == 0:
            continue
        ti = tix - 1
        for oc in range(8):
            i2 = oc % 2
            ps = pb[4 + i2]
            for i in range(22):
                P.pe.matmul(out=ps.full(), lhsT=Wd[:, i, oc * 128:(oc + 1) * 128], rhs=act[:, i, :],
                            start=(i == 0), stop=(i == 21))
            P.dve.tensor_tensor(out=xo[i2].full(), in0=ps.full(), in1=xst[:, oc, :], op=ALU.add)
            P.dma("sync" if i2 else "gpsimd", out=o_x[oc * 128:(oc + 1) * 128, ti * 512:(ti + 1) * 512], in_=xo[i2].full())
    P.emit()
    return nc, P


def run_D1(inp, l, resA, om, of, y, x_full):
    if "D1" not in _PROG_CACHE:
        _PROG_CACHE["D1"] = build_D1()[0]
    nc = _PROG_CACHE["D1"]
    cst = np.ascontiguousarray(inp["ssm_norm_g"][l].reshape(8, 128).T)
    xf = x_full.reshape(16384, D)
    in_maps = []
    for c in range(8):
        b, q = c // 4, c % 4
        ts = slice(q * T, (q + 1) * T)
        in_maps.append({
            "omT": np.ascontiguousarray(om[b][:, ts]), "ofT": np.ascontiguousarray(of[b][:, ts]),
            "yT": np.ascontiguousarray(y[b][:, ts]), "szT": np.asarray(resA[c]["o_sz"]),
            "gT": np.asarray(resA[c]["o_g"]), "xT": np.ascontiguousarray(xf[c * T:(c + 1) * T].T),
            "w_a": np.ascontiguousarray(inp["w_br_mla"][l]), "w_b": np.ascontiguousarray(inp["w_br_fox"][l]),
            "w_c": np.ascontiguousarray(inp["w_br_ssm"][l]), "w_o": np.ascontiguousarray(inp["w_out"][l]),
            "cst": cst,
        })
    res = run_bass_kernel_spmd(nc, in_maps, core_ids=list(range(8))).results
    xm = np.concatenate([np.asarray(r["o_x"]).T for r in res], axis=0)
    return xm.reshape(2, S_, D)


def run_D2(inp, l, xm_full):
    cp = d2_colpack(inp, l)
    off = dict(cp.off)
    off["_n"] = cp.n
    if "D2" not in _PROG_CACHE:
        _PROG_CACHE["D2"] = build_D2(off)[0]
    nc = _PROG_CACHE["D2"]
    cst = cp.array()
    xf = xm_full.reshape(16384, D)
    in_maps = []
    for c in range(8):
        t0 = c * T
        xt = np.zeros((D, T + HALO), np.float32)
        xt[:, HALO:] = xf[t0:t0 + T].T
        if c % 4 != 0:
            xt[:, 0:HALO] = xf[t0 - HALO:t0].T
        in_maps.append({"xT": np.ascontiguousarray(xt), "w_up": np.ascontiguousarray(inp["ffn_w_up"][l]),
                        "w_dn": np.ascontiguousarray(inp["ffn_w_down"][l]), "cst": cst})
    res = run_bass_kernel_spmd(nc, in_maps, core_ids=list(range(8))).results
    xo = np.concatenate([np.asarray(r["o_x"]).T for r in res], axis=0)
    return xo.reshape(2, S_, D)


def kernel_unfused(**inp):
    inp = {k: np.asarray(v) for k, v in inp.items()}
    x = inp["x"].astype(np.float32)
    pos = inp["positions"]
    for l in range(2):
        resA = run_A(inp, l, x, pos)
        om, of, y = run_BC(inp, l, resA)
        xm = run_D1(inp, l, resA, om, of, y, x)
        x = run_D2(inp, l, xm)
    return np.ascontiguousarray(x.astype(np.float32))


def kernel(**inp):
    return kernel_fused(**inp)


SW = 516
RG = [[0, 1, 2, 3], [4, 5, 6, 7]]
KT_L = T // 128


def fused_rowpack(inp, l):
    r = np.concatenate([inp["fox_b_f"][l], inp["ssm_dt_bias"][l], inp["ssm_A_log"][l], inp["ssm_D"][l]]).astype(np.float32)
    return np.ascontiguousarray(np.broadcast_to(r[None, :], (128, r.size)))


def build_fused(offA, offD2, stop=None, dbg=()):
    nc = new_nc()
    P = Prog(nc)
    EI, EO = "ExternalInput", "ExternalOutput"
    L = 2
    x0 = P.dram("x0", [D, 4 * SW], F32, EI)
    pos = P.dram("pos", [1, T], I32, EI)
    w_in = P.dram("w_in", [L, D, 7864], F32, EI)
    w_uq = P.dram("w_uq", [L, 384, 768], F32, EI)
    w_kp = P.dram("w_kp", [L, 256, 768], F32, EI)
    w_v = P.dram("w_v", [L, 256, 512], F32, EI)
    w_a = P.dram("w_a", [L, 512, D], F32, EI)
    w_b = P.dram("w_b", [L, 512, D], F32, EI)
    w_c = P.dram("w_c", [L, 1024, D], F32, EI)
    w_o = P.dram("w_o", [L, 1024, D], F32, EI)
    w_up = P.dram("w_up", [L, D, 5632], F32, EI)
    w_dn = P.dram("w_dn", [L, 2816, D], F32, EI)
    cstA_d = P.dram("cstA", [L, 128, offA["_n"]], F32, EI)
    cstD_d = P.dram("cstD", [L, 128, offD2["_n"]], F32, EI)
    gssm_d = P.dram("gssm", [L, 128, 8], F32, EI)
    rowc_d = P.dram("rowc", [L, 128, 56], F32, EI)
    sel_d = P.dram("sel", [128, 32], F32, EI)
    msk_d = P.dram("msk", [128, 8, 512], F32, EI)
    cm_d = P.dram("cm", [128, 4, 128], F32, EI)
    mats_d = P.dram("mats", [128, 192], F32, EI)
    out = P.dram("out", [D, T], F32, EO)
    xb = [x0, P.dram("xb1", [D, 4 * SW], F32)]
    xmid = P.dram("xmid", [D, 4 * SW], F32)
    qm = P.dram("qm", [8, 96, T], BF16)
    qf = P.dram("qf", [8, 64, T], BF16)
    fq = P.dram("fq", [8, 3, T], BF16)
    szd = P.dram("szd", [1024, T], BF16)
    gd = P.dram("gd", [3072, T], BF16)
    xtm = P.dram("xtm", [128, KT_L, 1024], BF16)
    btm = P.dram("btm", [128, KT_L, 256], BF16)
    bct = P.dram("bct", [512, T], BF16)
    dtd = P.dram("dtd", [128, KT_L, 16], F32)
    atd = P.dram("atd", [128, KT_L, 16], F32)
    omd = P.dram("omd", [512, T], BF16)
    ofd = P.dram("ofd", [512, T], BF16)
    yd = P.dram("yd", [1024, T], F32)
    kxm = [P.dram(f"kxm{m}", [768, 512], BF16) for m in range(4)]
    kxmg = [P.dram(f"kxmg{m}", [4 * 768, 512], BF16) for m in range(4)]
    kxf = [P.dram(f"kxf{m}", [512, 512], BF16) for m in range(4)]
    kxfg = [P.dram(f"kxfg{m}", [4 * 512, 512], BF16) for m in range(4)]
    vx = [P.dram(f"vx{m}", [2048, 256], BF16) for m in range(4)]
    vxg = [P.dram(f"vxg{m}", [4 * 2048, 256], BF16) for m in range(4)]
    sx = [P.dram(f"sx{i}", [256, 1024], F32) for i in range(2)]
    sxg = [P.dram(f"sxg{i}", [4 * 256, 1024], F32) for i in range(2)]
    fx = P.dram("fx", [128, 224], F32)
    fxg = P.dram("fxg", [4 * 128, 224], F32)
    tx = P.dram("tx", [128, 128], F32)
    txg = P.dram("txg", [4 * 128, 128], F32)
    wub = P.dram("wub", [D, 5632], BF16)
    wdb = P.dram("wdb", [2816, D], BF16)
    wab = P.dram("wab", [512, D], BF16)
    wbb = P.dram("wbb", [512, D], BF16)
    wcb = P.dram("wcb", [1024, D], BF16)
    wob = P.dram("wob", [1024, D], BF16)
    dbg_out = {}

    def gather_pairs(pairs, after=None):
        for (a, b) in pairs:
            kw = {}
            if after is not None:
                kw["_reads"] = [after]
            P.pool.collective_compute(kind="AllGather", op=ALU.bypass, replica_groups=RG,
                                      ins=[a.full().re("(p a) c -> p (a c)", p=128)],
                                      outs=[b.full().re("(q a) c -> q (a c)", q=512)], **kw)

    def load_consts():
        d = {}
        d["cm"] = P.sb("cmb", [128, 4, 128], F32)
        P.dma("sync", out=d["cm"].full(), in_=cm_d.full())
        d["sel"] = P.sb("selb", [128, 32], F32)
        P.dma("sync", out=d["sel"].full(), in_=sel_d.full())
        d["eps"] = P.sb("eps", [128, 1], F32)
        P.dve.memset(ap=d["eps"].full(), constant=1e-6)
        d["one1"] = P.sb("one1", [128, 1], F32)
        P.dve.memset(ap=d["one1"].full(), constant=1.0)
        d["zero"] = P.sb("zero", [128, 1], F32)
        P.dve.memset(ap=d["zero"].full(), constant=0.0)
        return d

    def phase_A(l):
        K = load_consts()
        cmb = K["cm"]
        tri, ident, ones = cmb[:, 0, :], cmb[:, 2, :], cmb[:, 3, :]
        eps, one1 = K["eps"], K["one1"]
        xin = xb[l]
        cstb = P.sb("cstb", [128, offA["_n"]], F32)
        C = Cst(P, cstb, offA)
        P.dma("sync", out=cstb.full(), in_=cstA_d[l])
        rowc = P.sb("rowc", [128, 56], F32)
        P.dma("sync", out=rowc.full(), in_=rowc_d[l])
        matf = P.sb("matf", [128, 192], F32)
        matb = P.sb("matb", [128, 192], BF16)
        P.dma("sync", out=matf.full(), in_=mats_d.full())
        P.dve.tensor_copy(out=matb.full(), in_=matf.full())
        prh = matb[0:96, 0:96]
        selm = matb[0:32, 96:192]
        identb = P.sb("identb", [128, 128], BF16)
        P.dve.tensor_copy(out=identb.full(), in_=ident)
        Aneg_r = P.sb("Aneg_r", [128, 16], F32)
        P.act.activation(out=Aneg_r.full(), in_=rowc[:, 24:40], func=AF.Exp)
        P.dve.tensor_scalar(out=Aneg_r.full(), in0=Aneg_r.full(), scalar1=-1.0, scalar2=None, op0=ALU.mult)

        pb = [P.ps(f"pb{i}", [128, 512], F32) for i in range(7)]
        pbt = P.ps("pbt", [128, 1024], BF16)
        pbi = {}

        def nxt_ps(lo=0, hi=4):
            i = pbi.get(lo, 0)
            pbi[lo] = (i + 1) % (hi - lo)
            return pb[lo + i]

        Ctab = P.sb("Ctab", [96, T], F32)
        Stab = P.sb("Stab", [96, T], F32)
        hraw = P.sb("hraw", [96, 512], F32)
        hsq = P.sb("hsq", [96, 512], F32)
        hrs = P.sb("hrs", [96, 512], F32)
        hnf = P.sb("hnf", [96, 512], F32)
        hnb = P.sb("hnb", [96, 512], BF16)
        ht1 = P.sb("ht1", [96, 512], F32)
        ht2 = P.sb("ht2", [96, 512], F32)
        posf, rr_tmp, rr_m = hrs, hraw, hsq

        class _IV:
            def __init__(self, b):
                self.b = b

            def full(self):
                return self.b.full().bitcast(I32)
        posi, rr_i = _IV(ht1), _IV(ht2)

        def sin_table(outv, phase):
            P.dve.tensor_scalar(out=rr_tmp.full(), in0=posf.full(), scalar1=C.col("invf"), scalar2=phase,
                                op0=ALU.mult, op1=ALU.add)
            P.dve.tensor_scalar(out=rr_m.full(), in0=rr_tmp.full(), scalar1=1.0 / (2 * np.pi), scalar2=None, op0=ALU.mult)
            P.dve.tensor_copy(out=rr_i.full(), in_=rr_m.full())
            P.dve.tensor_copy(out=rr_m.full(), in_=rr_i.full())
            P.dve.scalar_tensor_tensor(out=rr_tmp.full(), in0=rr_m.full(), scalar=-2 * np.pi, in1=rr_tmp.full(),
                                       op0=ALU.mult, op1=ALU.add)
            P.dve.tensor_scalar(out=rr_m.full(), in0=rr_tmp.full(), scalar1=np.pi, scalar2=-2 * np.pi, op0=ALU.is_gt, op1=ALU.mult)
            P.dve.tensor_tensor(out=rr_tmp.full(), in0=rr_tmp.full(), in1=rr_m.full(), op=ALU.add)
            P.dve.tensor_scalar(out=rr_m.full(), in0=rr_tmp.full(), scalar1=-np.pi, scalar2=2 * np.pi, op0=ALU.is_lt, op1=ALU.mult)
            P.dve.tensor_tensor(out=rr_tmp.full(), in0=rr_tmp.full(), in1=rr_m.full(), op=ALU.add)
            P.act.activation(out=outv, in_=rr_tmp.full(), func=AF.Sin)

        for i in range(4):
            P.dma("sync", out=posi.full(), in_=pos[:, i * 512:(i + 1) * 512].f(lambda a: a.partition_broadcast(96)))
            P.dve.tensor_copy(out=posf.full(), in_=posi.full())
            sin_table(Stab[:, i * 512:(i + 1) * 512], 0.0)
            sin_table(Ctab[:, i * 512:(i + 1) * 512], np.pi / 2)
        P.dve.memset(ap=Stab[0:64, :], constant=0.0)
        P.dve.memset(ap=Ctab[0:64, :], constant=1.0)

        hn = P.sb("hn", [128, 8, 4 * SW], BF16)
        xst = P.sb("xst", [128, 8, 512], F32)
        sq = P.sb("sq", [128, 512], F32)
        rstd = P.sb("rstd", [128, 512], F32)
        xTv = xin.full().re("(kc p) n -> p kc n", p=128)

        def rstd_from(ps_view, n_feat, rows, rstd_view):
            P.act.activation(out=rstd_view, in_=ps_view, func=AF.Ln, bias=eps[0:rows, 0:1], scale=1.0 / n_feat)
            P.act.activation(out=rstd_view, in_=rstd_view, func=AF.Exp, scale=-0.5)

        halos = [(m * SW, 4) for m in range(4)]
        main = [(m * SW + 4, 512) for m in range(4)]
        for (c0, w) in halos + main:
            P.dma("sync", out=xst[:, :, 0:w], in_=xTv[:, :, c0:c0 + w])
            ps = nxt_ps(4, 6)
            for kc in range(8):
                P.act.activation(out=sq[:, 0:w], in_=xst[:, kc, 0:w], func=AF.Square)
                P.pe.matmul(out=ps[:, 0:w], lhsT=ones, rhs=sq[:, 0:w], start=(kc == 0), stop=(kc == 7))
            rstd_from(ps[:, 0:w], 1024.0, 128, rstd[:, 0:w])
            for kc in range(8):
                P.dve.scalar_tensor_tensor(out=hn[:, kc, c0:c0 + w], in0=xst[:, kc, 0:w], scalar=C.col("g_mix", kc),
                                           in1=rstd[:, 0:w], op0=ALU.mult, op1=ALU.mult)

        wst = [P.sb(f"wst{i}", [128, 8, 256], F32) for i in range(2)]
        wbf = [P.sb(f"wbf{i}", [128, 8, 512], BF16) for i in range(2)]
        wcnt = [0]
        scnt = [0]
        w_inv = w_in[l].re("(kc p) n -> p kc n", p=128)

        SBv = 672 + 1544
        wplan = [(0, 384), (384, 288), (672, 512), (672 + 512, 512), (672 + 1024, 512), (672 + 1536, 8),
                 (SBv + 1024 + 1536, 16), (SBv, 512), (SBv + 512, 512)]
        wplan += [(SBv + 1024 + b_ * 512, 512) for b_ in range(3)]
        wplan += [(SBv + 2576 + b_ * 512, 512) for b_ in range(6)]
        wpend = {}
        cpend = []

        def w_issue(g):
            c0, ncols = wplan[g]
            lst = []
            for h0 in range(0, ncols, 256):
                n = min(256, ncols - h0)
                si = scnt[0] % 2
                scnt[0] += 1
                P.dma("sync", out=wst[si][:, :, 0:n], in_=w_inv[:, :, c0 + h0:c0 + h0 + n])
                lst.append((si, h0, n))
            wpend[g] = lst

        def load_w(c0, ncols):
            g = wcnt[0]
            wcnt[0] += 1
            assert wplan[g] == (c0, ncols), (g, wplan[g], c0, ncols)
            i = g % 2
            if g not in wpend:
                w_issue(g)
            lst = wpend.pop(g)
            for (si, h0, n) in lst:
                P.act.copy(out=wbf[i][:, :, h0:h0 + n], in_=wst[si][:, :, 0:n])
            if g + 1 < len(wplan):
                w_issue(g + 1)
            if cpend:
                gather_pairs([cpend.pop(0)], after=wbf[i].full())
            return wbf[i]

        def proj(wb, wc0, mcols, c0, w, ps_view):
            for kc in range(8):
                P.pe.matmul(out=ps_view, lhsT=wb[:, kc, wc0:wc0 + mcols], rhs=hn[:, kc, c0:c0 + w],
                            start=(kc == 0), stop=(kc == 7))

        def proj_tm(wb, wc0, ncols, tok0, ps_view):
            for kc in range(8):
                P.pe.matmul(out=ps_view, lhsT=hn[:, kc, tok0:tok0 + 128], rhs=wb[:, kc, wc0:wc0 + ncols],
                            start=(kc == 0), stop=(kc == 7))

        ostg_cnt = [0]
        ostg = [P.sb(f"ostg{i}", [128, 512], BF16) for i in range(4)]

        def next_ostg():
            i = ostg_cnt[0] % 4
            ostg_cnt[0] += 1
            return ostg[i]

        def out_dma(dst_view, src_view):
            P.dma("sync" if ostg_cnt[0] % 2 else "scalar", out=dst_view, in_=src_view)

        hsets = [dict(hraw=hraw.full(), hsq=hsq.full(), hrs=hrs.full(), hnf=hnf.full(), hnb=hnb.full(),
                      ht1=ht1.full(), ht2=ht2.full())]
        hnb1 = P.sb("hnb1", [96, 512], BF16)
        hsets.append(dict(hraw=xst[0:96, 0, :].k(0), hsq=xst[0:96, 1, :].k(1), hrs=xst[0:96, 2, :].k(2),
                          hnf=xst[0:96, 3, :].k(3), hnb=hnb1.full(), ht1=xst[0:96, 4, :].k(4), ht2=xst[0:96, 5, :].k(5)))
        hb2 = P.sb("hb2", [96, 6, 512], F32)
        hnb2 = P.sb("hnb2", [96, 512], BF16)
        hsets.append(dict(hraw=hb2[:, 0, :].k(0), hsq=hb2[:, 1, :].k(1), hrs=hb2[:, 2, :].k(2),
                          hnf=hb2[:, 3, :].k(3), hnb=hnb2.full(), ht1=hb2[:, 4, :].k(4), ht2=hb2[:, 5, :].k(5)))
        hcnt = [0]

        def headnorm(projfn, d, gain_col, rope, tok0, dst_view):
            H = hsets[hcnt[0] % 3]
            hcnt[0] += 1
            ps_view = projfn()
            P.act.activation(out=H["hsq"][0:d, :], in_=ps_view, func=AF.Square)
            P.act.copy(out=H["hraw"][0:d, :], in_=ps_view)
            yield
            ps2 = nxt_ps(4, 6)
            P.pe.matmul(out=ps2[0:d, :], lhsT=cmb[0:d, 3, 0:d], rhs=H["hsq"][0:d, :], start=True, stop=True)
            rstd_from(ps2[0:d, :], float(d), d, H["hrs"][0:d, :])
            og = next_ostg()
            if not rope:
                P.dve.scalar_tensor_tensor(out=og[0:d, :], in0=H["hraw"][0:d, :], scalar=gain_col, in1=H["hrs"][0:d, :],
                                           op0=ALU.mult, op1=ALU.mult)
            else:
                P.dve.scalar_tensor_tensor(out=H["hnf"][0:d, :], in0=H["hraw"][0:d, :], scalar=gain_col, in1=H["hrs"][0:d, :],
                                           op0=ALU.mult, op1=ALU.mult)
                P.act.copy(out=H["hnb"][0:d, :], in_=H["hnf"][0:d, :])
                yield
                ps3 = nxt_ps(6, 7)
                P.pe.matmul(out=ps3[0:d, :], lhsT=prh, rhs=H["hnb"][0:d, :], start=True, stop=True)
                P.dve.tensor_tensor(out=H["ht1"][0:d, :], in0=H["hnf"][0:d, :], in1=Ctab[0:d, tok0:tok0 + 512], op=ALU.mult)
                P.dve.tensor_tensor(out=H["ht2"][0:d, :], in0=ps3[0:d, :], in1=Stab[0:d, tok0:tok0 + 512], op=ALU.mult)
                P.pool.tensor_tensor(out=og[0:d, :], in0=H["ht1"][0:d, :], in1=H["ht2"][0:d, :], op=ALU.add)
            out_dma(dst_view, og[0:d, :])

        def run_pipe(gens, depth=3):
            gens = iter(gens)
            active = []
            while True:
                started = False
                if len(active) < depth:
                    g = next(gens, None)
                    if g is not None:
                        started = True
                        try:
                            next(g)
                            active.append(g)
                        except StopIteration:
                            pass
                if not active and not started:
                    break
                olds = active[:-1] if (started and active) else list(active)
                for g in olds:
                    try:
                        next(g)
                    except StopIteration:
                        active.remove(g)

        lat = P.sb("lat", [128, 3, 512], F32)
        latn = P.sb("latn", [128, 3, 512], BF16)

        def latent_norm(ps_list, gname):
            nch = len(ps_list)
            ps2 = nxt_ps(4, 6)
            for i, psv in enumerate(ps_list):
                P.act.activation(out=sq.full(), in_=psv, func=AF.Square)
                P.act.copy(out=lat[:, i, :], in_=psv)
                P.pe.matmul(out=ps2.full(), lhsT=ones, rhs=sq.full(), start=(i == 0), stop=(i == nch - 1))
            rstd_from(ps2.full(), 128.0 * nch, 128, rstd.full())
            for i in range(nch):
                P.dve.scalar_tensor_tensor(out=latn[:, i, :], in0=lat[:, i, :], scalar=C.col(gname, i), in1=rstd.full(),
                                           op0=ALU.mult, op1=ALU.mult)

        def small_w(name, dram_l, kc_n, ncols, i):
            bfb = P.sb(name, [128, kc_n, ncols], BF16)
            dv = dram_l.re("(kc p) n -> p kc n", p=128)
            for kc in range(kc_n):
                si = scnt[0] % 2
                scnt[0] += 1
                stg = wst[si].full().re("p a b -> p (a b)")[:, 0:ncols]
                P.dma("sync", out=stg, in_=dv[:, kc, :])
                P.act.copy(out=bfb[:, kc, :], in_=stg)
            return bfb

        uqb = small_w("uqb", w_uq[l], 3, 768, 0)
        kpb = small_w("kpb", w_kp[l], 2, 768, 1)
        wvb = small_w("wvb", w_v[l], 2, 512, 0)

        vstg = [P.sb(f"vstg{i}", [128, 512], BF16) for i in range(2)]
        vcnt = [0]

        def v_out(kind, ktl, ps_view):
            vs = vstg[vcnt[0] % 2]
            vcnt[0] += 1
            P.act.copy(out=vs.full(), in_=ps_view)
            P.dma("sync" if vcnt[0] % 2 else "scalar",
                  out=vx[ktl // 4][kind * 1024:(kind + 1) * 1024, (ktl % 4) * 64:(ktl % 4 + 1) * 64].re("(h p) d -> p h d", p=128),
                  in_=vs.full().re("p (h d) -> p h d", h=8))

        wb = load_w(0, 384)
        for m, (c0, w) in enumerate(main):
            pss = []
            for ch in range(3):
                ps = nxt_ps(0, 4)
                proj(wb, ch * 128, 128, c0, 512, ps.full())
                pss.append(ps.full())
            latent_norm(pss, "g_cq")
            def mkq(h):
                def f():
                    ps = nxt_ps(0, 4)
                    for kc in range(3):
                        P.pe.matmul(out=ps[0:96, :], lhsT=uqb[:, kc, h * 96:(h + 1) * 96], rhs=latn[:, kc, :],
                                    start=(kc == 0), stop=(kc == 2))
                    return ps[0:96, :]
                return f
            run_pipe(headnorm(mkq(h), 96, C.col("g_q"), True, m * 512, qm[h, :, m * 512:(m + 1) * 512]) for h in range(8))
        wb = load_w(384, 288)
        krb = P.sb("krb", [32, 512], BF16)
        for m, (c0, w) in enumerate(main):
            pss = []
            for ch in range(2):
                ps = nxt_ps(0, 4)
                proj(wb, ch * 128, 128, c0, 512, ps.full())
                pss.append(ps.full())
            ps = nxt_ps(0, 4)
            proj(wb, 256, 32, c0, 512, ps[0:32, :])
            P.act.copy(out=krb.full(), in_=ps[0:32, :])
            latent_norm(pss, "g_ckv")
            def mkk(h):
                def f():
                    ps = nxt_ps(0, 4)
                    for kc in range(2):
                        P.pe.matmul(out=ps[0:96, :], lhsT=kpb[:, kc, h * 96:(h + 1) * 96], rhs=latn[:, kc, :],
                                    start=(kc == 0), stop=False)
                    P.pe.matmul(out=ps[0:96, :], lhsT=selm, rhs=krb.full(), start=False, stop=True)
                    return ps[0:96, :]
                return f
            run_pipe(headnorm(mkk(h), 96, C.col("g_k"), True, m * 512, kxm[m][h * 96:(h + 1) * 96, :]) for h in range(8))
            for j in range(4):
                ps = nxt_ps(0, 4)
                for kc in range(2):
                    P.pe.matmul(out=ps.full(), lhsT=latn[:, kc, j * 128:(j + 1) * 128], rhs=wvb[:, kc, :],
                                start=(kc == 0), stop=(kc == 1))
                v_out(0, m * 4 + j, ps.full())
        for (base, gname, isq) in ((672, "g_fq", True), (672 + 512, "g_fk", False)):
            wb = load_w(base, 512)
            def mkf(wb_, h, c0):
                def f():
                    ps = nxt_ps(0, 4)
                    proj(wb_, h * 64, 64, c0, 512, ps[0:64, :])
                    return ps[0:64, :]
                return f
            gl = []
            for m, (c0, w) in enumerate(main):
                for h in range(8):
                    dst = qf[h, :, m * 512:(m + 1) * 512] if isq else kxf[m][h * 64:(h + 1) * 64, :]
                    gl.append(headnorm(mkf(wb, h, c0), 64, C.col(gname), False, m * 512, dst))
            run_pipe(gl)
        wb = load_w(672 + 1024, 512)
        for m, (c0, w) in enumerate(main):
            for j in range(4):
                ps = nxt_ps(0, 4)
                proj_tm(wb, 0, 512, c0 + j * 128, ps.full())
                v_out(1, m * 4 + j, ps.full())
        cpend.extend(list(zip(kxm, kxmg)) + list(zip(vx, vxg)) + list(zip(kxf, kxfg)))
        FB = 672 + 1536
        SB = 672 + 1544
        lf_tm = P.sb("lf_tm", [128, KT_L, 8], F32)
        dt_tm = P.sb("dt_tm", [128, KT_L, 16], F32)
        a_tm = P.sb("a_tm", [128, KT_L, 16], F32)
        tmpr = P.sb("tmpr", [128, 16], F32)
        wf = load_w(FB, 8)
        for m, (c0, w) in enumerate(main):
            for j in range(4):
                kt = m * 4 + j
                ps = nxt_ps(0, 4)
                proj_tm(wf, 0, 8, c0 + j * 128, ps[:, 0:8])
                P.dve.tensor_tensor(out=tmpr[:, 0:8], in0=ps[:, 0:8], in1=rowc[:, 0:8], op=ALU.add)
                P.act.activation(out=tmpr[:, 0:8], in_=tmpr[:, 0:8], func=AF.Exp, scale=-1.0)
                P.act.activation(out=tmpr[:, 0:8], in_=tmpr[:, 0:8], func=AF.Ln, bias=one1[:, 0:1], scale=1.0)
                P.dve.tensor_scalar(out=lf_tm[:, kt, :], in0=tmpr[:, 0:8], scalar1=-1.0, scalar2=None, op0=ALU.mult)
        wd = load_w(SB + 1024 + 1536, 16)
        for m, (c0, w) in enumerate(main):
            for j in range(4):
                kt = m * 4 + j
                ps = nxt_ps(0, 4)
                proj_tm(wd, 0, 16, c0 + j * 128, ps[:, 0:16])
                P.dve.tensor_tensor(out=tmpr.full(), in0=ps[:, 0:16], in1=rowc[:, 8:24], op=ALU.add)
                P.act.activation(out=tmpr.full(), in_=tmpr.full(), func=AF.Exp)
                P.act.activation(out=dt_tm[:, kt, :], in_=tmpr.full(), func=AF.Ln, bias=one1[:, 0:1], scale=1.0)
                P.dve.tensor_tensor(out=a_tm[:, kt, :], in0=dt_tm[:, kt, :], in1=Aneg_r.full(), op=ALU.mult)
        P.dma("sync", out=dtd.full(), in_=dt_tm.full())
        P.dma("sync", out=atd.full(), in_=a_tm.full())
        def plain_group(base, ncols, func, bias_name, dst, dst_row0):
            wb_ = load_w(base, ncols)
            for m, (c0, w) in enumerate(main):
                for ch in range(ncols // 128):
                    ps = nxt_ps(0, 4)
                    proj(wb_, ch * 128, 128, c0, 512, ps.full())
                    og = next_ostg()
                    if bias_name is None:
                        P.act.activation(out=og.full(), in_=ps.full(), func=func)
                    else:
                        P.act.activation(out=og.full(), in_=ps.full(), func=func,
                                         bias=C.col(bias_name, (dst_row0 // 128) + ch))
                    out_dma(dst[dst_row0 + ch * 128:dst_row0 + (ch + 1) * 128, m * 512:(m + 1) * 512], og.full())

        for blk in range(2):
            plain_group(SB + blk * 512, 512, AF.Silu, None, szd, blk * 512)
        upre = P.sb("upre", [128, 516], F32)
        carry = P.sb("carry", [128, 4], F32)
        acc0 = P.sb("acc0", [128, 512], F32)
        tstg = [P.sb(f"tstg{i}", [128, 4, 128], BF16) for i in range(2)]
        tcnt = [0]
        trq = []
        for blk in range(3):
            wb = load_w(SB + 1024 + blk * 512, 512)
            for m, (c0, w) in enumerate(main):
                for ch in range(4):
                    cg = blk * 4 + ch
                    ps = nxt_ps(0, 4)
                    proj(wb, ch * 128, 128, c0 - 4, 4, ps[:, 0:4])
                    P.act.copy(out=upre[:, 0:4], in_=ps[:, 0:4])
                    ps = nxt_ps(0, 4)
                    proj(wb, ch * 128, 128, c0, 512, ps.full())
                    while len(trq) > 1:
                        trq.pop(0)()
                    P.act.copy(out=upre[:, 4:516], in_=ps.full())
                    P.act.activation(out=acc0.full(), in_=ps.full(), func=AF.Identity, scale=C.col("cw3", cg), bias=C.col("cb", cg))
                    for k in range(3):
                        P.dve.scalar_tensor_tensor(out=acc0.full(), in0=upre[:, 1 + k:513 + k], scalar=C.col(f"cw{k}", cg),
                                                   in1=acc0.full(), op0=ALU.mult, op1=ALU.add)
                    og = next_ostg()
                    P.act.activation(out=og.full(), in_=acc0.full(), func=AF.Silu)
                    if cg >= 8:
                        out_dma(bct[(cg - 8) * 128:(cg - 7) * 128, m * 512:(m + 1) * 512], og.full())
                    if cg < 10:
                        def mk_tr(og=og, cg=cg, m=m):
                            def f():
                                i2 = tcnt[0] % 2
                                tcnt[0] += 1
                                for j in range(4):
                                    P.pe.transpose(out=pbt[:, i2 * 512 + j * 128:i2 * 512 + (j + 1) * 128],
                                                   in_=og[:, j * 128:(j + 1) * 128], identity=identb.full())
                                ts_ = tstg[i2]
                                P.dve.tensor_copy(out=ts_.full().re("p j f -> p (j f)"), in_=pbt[:, i2 * 512:(i2 + 1) * 512])
                                if cg < 8:
                                    P.dma("sync", out=xtm[:, m * 4:(m + 1) * 4, cg * 128:(cg + 1) * 128], in_=ts_.full())
                                else:
                                    P.dma("sync", out=btm[:, m * 4:(m + 1) * 4, (cg - 8) * 128:(cg - 7) * 128], in_=ts_.full())
                            return f
                        trq.append(mk_tr())
        while trq:
            trq.pop(0)()
        GB = SB + 2576
        for blk in range(6):
            plain_group(GB + blk * 512, 512, AF.Sigmoid, "b_gate", gd, blk * 512)
        fxs = P.sb("fxs", [128, 224], F32)
        within = P.sb("within", [128, KT_L, 8], F32)
        ttot = P.sb("ttot", [128, KT_L, 8], F32)
        f2 = lambda b: b.full().re("p a b -> p (a b)")
        ps = nxt_ps(0, 4)
        P.pe.matmul(out=ps[:, 0:128], lhsT=tri, rhs=f2(lf_tm), start=True, stop=True)
        P.act.copy(out=f2(within), in_=ps[:, 0:128])
        ps = nxt_ps(0, 4)
        P.pe.matmul(out=ps[:, 0:128], lhsT=ones, rhs=f2(lf_tm), start=True, stop=True)
        P.act.copy(out=f2(ttot), in_=ps[:, 0:128])
        Floc = fxs[:, 0:128].re("p (a b) -> p a b", b=8)
        totv = fxs[:, 128:160].re("p (a b) -> p a b", b=8)
        cacc = P.sb("cacc", [128, 8], F32)
        for m in range(4):
            P.dve.tensor_copy(out=Floc[:, 4 * m, :], in_=within[:, 4 * m, :])
            P.dve.tensor_copy(out=cacc.full(), in_=ttot[:, 4 * m, :])
            for j in range(1, 4):
                P.dve.tensor_tensor(out=Floc[:, 4 * m + j, :], in0=within[:, 4 * m + j, :], in1=cacc.full(), op=ALU.add)
                P.dve.tensor_tensor(out=cacc.full(), in0=cacc.full(), in1=ttot[:, 4 * m + j, :], op=ALU.add)
            P.dve.tensor_copy(out=totv[:, m, :], in_=cacc.full())
        ps = nxt_ps(0, 4)
        P.pe.transpose(out=ps[:, 0:128], in_=fxs[:, 0:128], identity=ident)
        FT = P.sb("FT", [128, 128], F32)
        r1 = P.sb("r1", [128, 128], F32)
        fh = [P.sb(f"fh{i}", [128, 128], BF16) for i in range(3)]
        P.act.copy(out=FT.full(), in_=ps[:, 0:128])
        P.dve.tensor_copy(out=fh[0].full(), in_=FT.full())
        P.dve.tensor_tensor(out=r1.full(), in0=FT.full(), in1=fh[0].full(), op=ALU.subtract)
        P.dve.tensor_copy(out=fh[1].full(), in_=r1.full())
        P.dve.tensor_tensor(out=r1.full(), in0=r1.full(), in1=fh[1].full(), op=ALU.subtract)
        P.dve.tensor_copy(out=fh[2].full(), in_=r1.full())
        for r in range(3):
            for kt in range(KT_L):
                P.dma("sync" if kt % 2 else "scalar", out=fq[:, r, kt * 128:(kt + 1) * 128], in_=fh[r][kt * 8:(kt + 1) * 8, :])
        P.dma("sync", out=fx[:, 0:160], in_=fxs[:, 0:160])
        gather_pairs(cpend)
        del cpend[:]

    def ssd_scan(l, K, pass1, fxs=None, dt_tm=None, a_tm=None, Hinit=None, rowc=None, pb=None):
        cmb = K["cm"]
        tri, trimask, ones = cmb[:, 0, :], cmb[:, 1, :], cmb[:, 3, :]
        if pb is None:
            pb = [P.ps(f"spb{i}", [128, 512], F32) for i in range(7)]
        if pass1:
            fxs = P.sb("decs", [128, 224], F32)
        if dt_tm is None:
            dt_tm = P.sb("dt_tm", [128, KT_L, 16], F32)
            a_tm = P.sb("a_tm", [128, KT_L, 16], F32)
            P.dma("sync", out=dt_tm.full(), in_=dtd.full())
            P.dma("sync", out=a_tm.full(), in_=atd.full())
        fl = lambda b: b.full().re("p c h -> p (c h)")
        Acum = P.sb("Acum", [128, KT_L, 16], F32)
        Atot = P.sb("Atot", [128, KT_L, 16], F32)
        wdec = P.sb("wdec", [128, KT_L, 16], F32)
        eAtot = P.sb("eAtot", [128, KT_L, 16], F32)
        psA = pb[0]
        P.pe.matmul(out=psA[:, 0:256], lhsT=tri, rhs=fl(a_tm), start=True, stop=True)
        P.act.copy(out=fl(Acum), in_=psA[:, 0:256])
        P.pe.matmul(out=psA[:, 256:512], lhsT=ones, rhs=fl(a_tm), start=True, stop=True)
        P.act.copy(out=fl(Atot), in_=psA[:, 256:512])
        P.act.activation(out=fl(eAtot), in_=fl(Atot), func=AF.Exp)
        P.dve.tensor_tensor(out=fl(wdec), in0=fl(Atot), in1=fl(Acum), op=ALU.subtract)
        P.act.activation(out=fl(wdec), in_=fl(wdec), func=AF.Exp)
        if not pass1:
            nAcum = P.sb("nAcum", [128, KT_L, 16], F32)
            eA = P.sb("eA", [128, KT_L, 16], F32)
            P.dve.tensor_scalar(out=fl(nAcum), in0=fl(Acum), scalar1=-1.0, scalar2=None, op0=ALU.mult)
            P.act.activation(out=fl(eA), in_=fl(Acum), func=AF.Exp)
            BCs = P.sb("BCs", [128, 4, T], BF16)
            P.dma("gpsimd", out=BCs.full(), in_=bct.full().re("(a p) t -> p a t", p=128))
            cb = P.sb("cb", [128, 2, 128], F32)
            NH = 4
            at = [P.sb(f"at{i}", [128, 128], F32) for i in range(NH)]
            tm = [P.sb(f"tm{i}", [128, 128], F32) for i in range(NH)]
            dec = [P.sb(f"dec{i}", [128, 128], F32) for i in range(NH)]
            MT = [P.sb(f"MT{i}", [128, 128], BF16) for i in range(NH)]
            t1 = P.sb("t1", [128, 1024], F32)
            t3 = P.sb("t3", [128, 1024], F32)
            yo = P.sb("yo", [128, 1024], BF16)
            yT = [P.sb(f"yT{i}", [128, 4, 128], F32) for i in range(2)]
        Hs = P.sb("Hs", [128, 1024], F32)
        Hb = P.sb("Hb", [128, 1024], BF16)
        xc = [P.sb(f"xc{i}", [128, 1024], BF16) for i in range(2)]
        Bc = [P.sb(f"Bc{i}", [128, 256], BF16) for i in range(2)]
        xdt = P.sb("xdt", [128, 1024], BF16)
        xdts = P.sb("xdts", [128, 1024], BF16)
        dsum = P.sb("dsum", [128, 16], F32)
        v3 = lambda v: v.re("p (h d) -> p h d", h=16)
        bc3 = lambda v: v.f(lambda a: a.unsqueeze(2).to_broadcast([128, 16, 64]))
        for m in range(4):
            if pass1:
                P.dve.memset(ap=Hs.full(), constant=0.0)
                P.dve.memset(ap=dsum.full(), constant=0.0)
            else:
                P.dve.tensor_copy(out=Hs.full(), in_=Hinit[:, m, :])
                P.act.copy(out=Hb.full(), in_=Hinit[:, m, :])
            for j in range(4):
                c = m * 4 + j
                x_c = xc[c % 2]
                B_c = Bc[c % 2]
                P.dma("sync", out=x_c.full(), in_=xtm[:, c, :])
                P.dma("gpsimd", out=B_c.full(), in_=btm[:, c, :])
                P.dve.tensor_tensor(out=v3(xdt.full()), in0=v3(x_c.full()), in1=bc3(dt_tm[:, c, :]), op=ALU.mult)
                P.pool.tensor_tensor(out=v3(xdts.full()), in0=v3(xdt.full()), in1=bc3(wdec[:, c, :]), op=ALU.mult)
                if not pass1:
                    cs = slice(c * 128, (c + 1) * 128)
                    ps_cb = pb[1]
                    for g in range(2):
                        P.pe.matmul(out=ps_cb[:, g * 128:(g + 1) * 128], lhsT=BCs[:, g, cs], rhs=BCs[:, 2 + g, cs],
                                    start=True, stop=True)
                    P.act.copy(out=cb.full().re("p a b -> p (a b)"), in_=ps_cb[:, 0:256])
                    ps_off = [pb[2], pb[3]]
                    for g in range(2):
                        P.pe.matmul(out=ps_off[g].full(), lhsT=BCs[:, 2 + g, cs], rhs=Hb[:, g * 512:(g + 1) * 512],
                                    start=True, stop=True)
                    ps_y = [pb[4], pb[5]]
                    def st1(h):
                        i2 = h % NH
                        g = h // 8
                        P.dve.tensor_scalar(out=at[i2].full(), in0=tri, scalar1=a_tm[:, c, h:h + 1], scalar2=None, op0=ALU.mult)
                        ps_A = pb[6]
                        P.pe.matmul(out=ps_A[:, i2 * 128:(i2 + 1) * 128], lhsT=ones, rhs=at[i2].full(), start=True, stop=True)
                        P.dve.tensor_tensor(out=tm[i2].full(), in0=ps_A[:, i2 * 128:(i2 + 1) * 128], in1=trimask, op=ALU.add)
                        P.act.activation(out=dec[i2].full(), in_=tm[i2].full(), func=AF.Exp, bias=nAcum[:, c, h:h + 1], scale=1.0)
                        P.pool.tensor_tensor(out=MT[i2].full(), in0=cb[:, g, :], in1=dec[i2].full(), op=ALU.mult)

                    def st2(h):
                        i2 = h % NH
                        g = h // 8
                        hh = h % 8
                        P.pe.matmul(out=ps_y[g][:, hh * 64:(hh + 1) * 64], lhsT=MT[i2].full(), rhs=xdt[:, h * 64:(h + 1) * 64],
                                    start=True, stop=True)

                    for hq in range(16 + 3):
                        if hq < 16:
                            st1(hq)
                        if hq >= 3:
                            st2(hq - 3)
                    for g in range(2):
                        gs_ = slice(g * 512, (g + 1) * 512)
                        v8 = lambda v: v.re("p (h d) -> p h d", h=8)
                        b8 = lambda v: v.f(lambda a: a.unsqueeze(2).to_broadcast([128, 8, 64]))
                        P.dve.tensor_tensor(out=v8(t1[:, gs_]), in0=v8(ps_off[g].full()), in1=b8(eA[:, c, g * 8:(g + 1) * 8]), op=ALU.mult)
                        P.dve.tensor_tensor(out=t1[:, gs_], in0=t1[:, gs_], in1=ps_y[g].full(), op=ALU.add)
                    P.pool.tensor_tensor(out=v3(t3.full()), in0=v3(x_c.full()), in1=bc3(rowc[:, 40:56]), op=ALU.mult)
                    P.pool.tensor_tensor(out=t3.full(), in0=t1.full(), in1=t3.full(), op=ALU.add)
                    for q4 in range(2):
                        pst = pb[2 + q4]
                        for jj in range(4):
                            fc = q4 * 4 + jj
                            P.pe.transpose(out=pst[:, jj * 128:(jj + 1) * 128], in_=t3[:, fc * 128:(fc + 1) * 128],
                                           identity=cmb[:, 2, :])
                        yt = yT[q4]
                        P.act.copy(out=yt.full().re("p a b -> p (a b)"), in_=pst.full())
                        P.dma("sync", out=yd[q4 * 512:(q4 + 1) * 512, c * 128:(c + 1) * 128].re("(a p) t -> p a t", p=128),
                              in_=yt.full())
                ps_h = [pb[0], pb[1]] if pass1 else [pb[4], pb[5]]
                for g in range(2):
                    P.pe.matmul(out=ps_h[g].full(), lhsT=B_c[:, g * 128:(g + 1) * 128], rhs=xdts[:, g * 512:(g + 1) * 512],
                                start=True, stop=True)
                P.dve.tensor_tensor(out=v3(Hs.full()), in0=v3(Hs.full()), in1=bc3(eAtot[:, c, :]), op=ALU.mult)
                for g in range(2):
                    P.dve.tensor_tensor(out=Hs[:, g * 512:(g + 1) * 512], in0=Hs[:, g * 512:(g + 1) * 512], in1=ps_h[g].full(), op=ALU.add)
                if pass1:
                    P.dve.tensor_tensor(out=dsum.full(), in0=dsum.full(), in1=Atot[:, c, :], op=ALU.add)
                else:
                    P.act.copy(out=Hb.full(), in_=Hs.full())
            if pass1:
                P.dma("sync", out=sx[m // 2][(m % 2) * 128:(m % 2 + 1) * 128, :], in_=Hs.full())
                P.act.activation(out=fxs[:, 160 + m * 16:160 + (m + 1) * 16], in_=dsum.full(), func=AF.Exp)
        if pass1:
            P.dma("sync", out=fx[:, 160:224], in_=fxs[:, 160:224])

    def load_fg():
        fg = P.sb("fg", [128, 4, 224], F32)
        P.dma("sync", out=fg.full(), in_=fxg.full().re("(r p) c -> p r c", p=128))
        return fg

    def phase_attn(l):
        K = load_consts()
        sel, zero = K["sel"], K["zero"]
        mskb = P.sb("mskb", [128, 8, 512], F32)
        P.dma("gpsimd", out=mskb.full(), in_=msk_d.full())
        fg = load_fg()
        offs = P.sb("offs", [128, 16, 8], F32)
        run = P.sb("run", [128, 8], F32)
        P.dve.memset(ap=run.full(), constant=0.0)
        for s_ in range(16):
            m, r = divmod(s_, 4)
            P.dve.tensor_copy(out=offs[:, s_, :], in_=run.full())
            P.dve.tensor_tensor(out=run.full(), in0=run.full(), in1=fg[:, r, 128 + m * 8:128 + (m + 1) * 8], op=ALU.add)
        offown = P.sb("offown", [128, 4, 8], F32)
        P.dve.memset(ap=offown.full(), constant=0.0)
        for m in range(4):
            for r in range(4):
                P.dve.scalar_tensor_tensor(out=offown[:, m, :], in0=offs[:, 4 * m + r, :], scalar=sel[:, 8 + 4 * m + r:9 + 4 * m + r],
                                           in1=offown[:, m, :], op0=ALU.mult, op1=ALU.add)
        negFg = P.sb("negFg", [128, 64, 8], F32)
        for s_ in range(16):
            m, r = divmod(s_, 4)
            src = fg[:, r, 0:128].re("p (a b) -> p a b", b=8)[:, 4 * m:4 * m + 4, :]
            P.dve.tensor_tensor(out=negFg[:, 4 * s_:4 * s_ + 4, :], in0=src,
                                in1=offs[:, s_, :].f(lambda a: a.unsqueeze(1).to_broadcast([128, 4, 8])), op=ALU.add)
        P.dve.tensor_scalar(out=negFg.full(), in0=negFg.full(), scalar1=-1.0, scalar2=None, op0=ALU.mult)
        biasm = P.sb("biasm", [128, 4, 64, 8], F32)
        for m in range(4):
            nk = (4 * m + 4) * 4
            P.dve.tensor_tensor(out=biasm[:, m, 0:nk, :], in0=negFg[:, 0:nk, :],
                                in1=offown[:, m, :].f(lambda a: a.unsqueeze(1).to_broadcast([128, nk, 8])), op=ALU.add)
            for jr in range(4):
                k0 = (4 * m + jr) * 4
                P.dve.tensor_scalar(out=biasm[:, m, k0:k0 + 4, :], in0=biasm[:, m, k0:k0 + 4, :],
                                    scalar1=sel[:, 28 + jr:29 + jr], scalar2=None, op0=ALU.add)

        pb = [P.ps(f"pb{i}", [128, 512], F32) for i in range(8)]
        K_sb = [P.sb(f"K_sb{i}", [96, S_], BF16) for i in range(2)]
        Q_sb = [P.sb(f"Q_sb{i}", [96, T], BF16) for i in range(2)]
        V_sb = [P.sb(f"V_sb{i}", [128, NKT, 128], BF16) for i in range(2)]
        for i in range(2):
            P.dve.memset(ap=V_sb[i][:, :, 64:128], constant=1.0)
        NSB = 5
        LA = 3
        WARM = False
        pt = [P.sb(f"pt{i}", [128, 512], BF16) for i in range(NSB)]
        mt = [P.sb(f"mt{i}", [128, 512], F32) for i in range(3)]
        rl = P.sb("rl", [128, 512], F32)
        rl2 = P.sb("rl2", [64, 512], F32)
        ot = [P.sb(f"ot{i}", [64, 512], BF16) for i in range(2)]
        cnt = [0, 0, 0]
        heads = [(0, h) for h in range(8)] + [(1, h) for h in range(8)]

        def loads(idx):
            kind, h = heads[idx]
            i = idx % 2
            nd = 96 if kind == 0 else 64
            for r in range(4):
                vr = r * 2048 + kind * 1024 + h * 128
                for m in range(4):
                    s0 = (4 * m + r) * 512
                    if kind == 0:
                        ksrc = kxmg[m][r * 768 + h * 96:r * 768 + (h + 1) * 96, :]
                    else:
                        ksrc = kxfg[m][r * 512 + h * 64:r * 512 + (h + 1) * 64, :]
                    P.dma("sync" if (r + m) % 2 == 0 else "gpsimd", out=K_sb[i][0:nd, s0:s0 + 512], in_=ksrc)
                    g0 = (4 * m + r) * 4
                    P.dma("gpsimd" if (r + m) % 2 == 0 else "sync",
                          out=V_sb[i][:, g0:g0 + 4, 0:64],
                          in_=vxg[m][vr:vr + 128, :].re("p (j d) -> p j d", j=4))
            if kind == 0:
                P.dma("sync", out=Q_sb[i][0:96, :], in_=qm[h])
            else:
                P.dve.memset(ap=K_sb[i][64:96, :], constant=0.0)
                P.dve.memset(ap=K_sb[i][64:67, :], constant=8.0)
                P.dma("sync", out=Q_sb[i][0:64, :], in_=qf[h])
                P.pool.memset(ap=Q_sb[i][64:96, :], constant=0.0)
                P.dma("gpsimd", out=Q_sb[i][64:67, :], in_=fq[h])

        iters = []
        for idx in range(16):
            for m in range(4):
                nk = (4 * m + 4) * 4
                for kt in range(nk):
                    iters.append((idx, m, kt, nk))

        def stage_qk(n):
            idx, m, kt, nk = iters[n]
            kind, h = heads[idx]
            i = idx % 2
            dk = 96
            scale = 96.0 ** -0.5 if kind == 0 else 0.125
            i3 = n % NSB
            ps = pb[i3]
            P.pe.matmul(out=ps.full(), lhsT=K_sb[i][0:dk, kt * 128:(kt + 1) * 128],
                        rhs=Q_sb[i][0:dk, m * 512:(m + 1) * 512], start=True, stop=True)
            blk = kt // 4
            if blk >= 4 * m:
                jr = blk - 4 * m
                mm = mt[cnt[1] % 3]
                cnt[1] += 1
                P.dve.scalar_tensor_tensor(out=mm.full(), in0=mskb[:, kind * 4 + kt % 4, :],
                                           scalar=sel[:, 24 + jr:25 + jr], in1=ps.full(), op0=ALU.mult, op1=ALU.add)
                src = mm.full()
                bias = sel[:, 28 + jr:29 + jr] if kind == 0 else biasm[:, m, kt, h:h + 1]
            else:
                src = ps.full()
                bias = zero[:, 0:1] if kind == 0 else biasm[:, m, kt, h:h + 1]
            P.act.activation(out=pt[i3].full(), in_=src, func=AF.Exp, scale=scale, bias=bias)
            if WARM:
                P.pe.matmul(out=pb[6][:, 0:256], lhsT=K_sb[i][0:dk, kt * 128:(kt + 1) * 128],
                            rhs=Q_sb[i][0:dk, m * 512:m * 512 + 256], start=True, stop=True)

        def stage_pv(n):
            idx, m, kt, nk = iters[n]
            kind, h = heads[idx]
            i = idx % 2
            oacc = pb[5 + (idx * 4 + m) % 2]
            P.pe.matmul(out=oacc.full(), lhsT=V_sb[i][:, kt, :], rhs=pt[n % NSB].full(), start=(kt == 0), stop=(kt == nk - 1))
            if kt == nk - 1:
                odst = omd if kind == 0 else ofd
                P.dve.reciprocal(out=rl[64:128, :], in_=oacc[64:128, :])
                P.dve.tensor_copy(out=rl2.full(), in_=rl[64:128, :])
                o = ot[m % 2]
                P.dve.tensor_tensor(out=o.full(), in0=oacc[0:64, :], in1=rl2.full(), op=ALU.mult)
                P.dma("sync", out=odst[h * 64:(h + 1) * 64, m * 512:(m + 1) * 512], in_=o.full())

        pcs = [P.sb(f"pcs{i}", [128, 2048], F32) for i in range(2)]
        pcb = [P.sb(f"pcb{i}", [128, 2048], BF16) for i in range(2)]
        jobs = []
        for (src, dst, rows, cols) in ((w_a[l], wab, 512, D), (w_b[l], wbb, 512, D), (w_c[l], wcb, 1024, D), (w_o[l], wob, 1024, D),
                                       (w_up[l], wub, D, 5632), (w_dn[l], wdb, 2816, D)):
            for r0 in range(0, rows, 128):
                for c0 in range(0, cols, 2048):
                    n_ = min(2048, cols - c0)
                    jobs.append((src[r0:r0 + 128, c0:c0 + n_], dst[r0:r0 + 128, c0:c0 + n_], n_))
        jcnt = [0]

        def precast_one():
            if jcnt[0] >= len(jobs):
                return
            src, dst, n_ = jobs[jcnt[0]]
            i = jcnt[0] % 2
            jcnt[0] += 1
            P.dma("gpsimd", out=pcs[i][:, 0:n_], in_=src)
            P.pool.tensor_copy(out=pcb[i][:, 0:n_], in_=pcs[i][:, 0:n_])
            P.dma("gpsimd", out=dst, in_=pcb[i][:, 0:n_])

        every = max(1, len(iters) // (len(jobs) + 4))
        loads(0)
        loads(1)
        for n in range(len(iters) + LA):
            if n < len(iters):
                stage_qk(n)
            if n >= LA:
                stage_pv(n - LA)
                idx_p, m_p, kt_p, nk_p = iters[n - LA]
                if m_p == 3 and kt_p == nk_p - 1 and idx_p + 2 < 16:
                    loads(idx_p + 2)
            if n % every == every - 1:
                precast_one()
        while jcnt[0] < len(jobs):
            precast_one()

    def phase_ssd2(l):
        K = load_consts()
        sel = K["sel"]
        rowc = P.sb("rowc", [128, 56], F32)
        P.dma("sync", out=rowc.full(), in_=rowc_d[l])
        fg = load_fg()
        Hin = P.sb("Hin", [128, 1024], F32)
        Hsel = P.sb("Hsel", [128, 4, 1024], F32)
        Sst = [P.sb(f"Sst{i}", [128, 1024], F32) for i in range(2)]
        P.dve.memset(ap=Hin.full(), constant=0.0)
        P.dve.memset(ap=Hsel.full(), constant=0.0)
        v3 = lambda v: v.re("p (h d) -> p h d", h=16)
        for s_ in range(16):
            m, r = divmod(s_, 4)
            P.dve.scalar_tensor_tensor(out=Hsel[:, m, :], in0=Hin.full(), scalar=sel[:, 8 + s_:9 + s_], in1=Hsel[:, m, :],
                                       op0=ALU.mult, op1=ALU.add)
            if s_ < 15:
                st_ = Sst[s_ % 2]
                P.dma("sync" if s_ % 2 else "gpsimd", out=st_.full(),
                      in_=sxg[m // 2][r * 256 + (m % 2) * 128:r * 256 + (m % 2 + 1) * 128, :])
                dcs = fg[:, r, 160 + m * 16:160 + (m + 1) * 16]
                P.dve.tensor_tensor(out=v3(Hin.full()), in0=v3(Hin.full()),
                                    in1=dcs.f(lambda a: a.unsqueeze(2).to_broadcast([128, 16, 64])), op=ALU.mult)
                P.pool.tensor_tensor(out=Hin.full(), in0=Hin.full(), in1=st_.full(), op=ALU.add)
        ssd_scan(l, K, pass1=False, Hinit=Hsel, rowc=rowc)

    def write_tails(txs):
        P.dma("sync", out=tx.full(), in_=txs.full().re("p m k c -> p (m k c)"))

    def halo_exchange(dst):
        K = load_consts()
        sel = K["sel"]
        P.pool.collective_compute(kind="AllGather", op=ALU.bypass, replica_groups=RG, ins=[tx.full()], outs=[txg.full()])
        tg = P.sb("tg", [128, 4, 128], F32)
        P.dma("sync", out=tg.full(), in_=txg.full().re("(r p) c -> p r c", p=128))
        hl = P.sb("hl", [128, 4, 32], F32)
        P.dve.memset(ap=hl.full(), constant=0.0)
        for m in range(4):
            for r in range(4):
                P.dve.scalar_tensor_tensor(out=hl[:, m, :], in0=tg[:, r, m * 32:(m + 1) * 32], scalar=sel[:, r:r + 1],
                                           in1=hl[:, m, :], op0=ALU.mult, op1=ALU.add)
            if m >= 1:
                P.dve.scalar_tensor_tensor(out=hl[:, m, :], in0=tg[:, 3, (m - 1) * 32:m * 32], scalar=sel[:, 4:5],
                                           in1=hl[:, m, :], op0=ALU.mult, op1=ALU.add)
        dv = dst.full().re("(kc p) n -> p kc n", p=128)
        for m in range(4):
            P.dma("sync", out=dv[:, :, m * SW:m * SW + 4], in_=hl[:, m, :].re("p (k c) -> p k c", c=4))

    def phase_merge(l):
        K = load_consts()
        ones, eps = K["cm"][:, 3, :], K["eps"]
        gsb = P.sb("gsb", [128, 8], F32)
        P.dma("sync", out=gsb.full(), in_=gssm_d[l])
        pb = [P.ps(f"pb{i}", [128, 512], F32) for i in range(8)]
        def load_wb(dram_bf, kc_n, name, q):
            bfb = P.sb(name, [128, kc_n, D], BF16)
            P.dma(q, out=bfb.full(), in_=dram_bf.full().re("(kc p) n -> p kc n", p=128))
            return bfb

        Wa = load_wb(wab, 4, "Wa", "sync")
        Wb = load_wb(wbb, 4, "Wb", "gpsimd")
        Wc = load_wb(wcb, 8, "Wc", "sync")
        Wo = load_wb(wob, 8, "Wo", "gpsimd")
        ys = P.sb("ys", [128, 8, 512], F32)
        szs = P.sb("szs", [128, 8, 512], BF16)
        yn = P.sb("yn", [128, 8, 512], BF16)
        oms = P.sb("oms", [128, 4, 512], BF16)
        ofs = P.sb("ofs", [128, 4, 512], BF16)
        gs = P.sb("gs", [128, 24, 512], BF16)
        xs = P.sb("xs", [128, 8, 512], F32)
        sq = P.sb("sq", [128, 512], F32)
        rstd = P.sb("rstd", [128, 512], F32)
        m1 = [P.sb(f"m1_{i}", [128, 512], F32) for i in range(2)]
        m2 = [P.sb(f"m2_{i}", [128, 512], F32) for i in range(2)]
        m3 = [P.sb(f"m3_{i}", [128, 512], F32) for i in range(2)]
        mg = P.sb("mg", [128, 8, 512], BF16)
        xo = [P.sb(f"xo{i}", [128, 512], F32) for i in range(2)]
        txs = P.sb("txs", [128, 4, 8, 4], F32)
        ch = lambda d: d.full().re("(kc p) n -> p kc n", p=128)
        xmv = ch(xmid)
        for ti in range(4):
            ts = slice(ti * 512, (ti + 1) * 512)
            xsl = slice(ti * SW + 4, ti * SW + 516)
            P.dma("sync", out=ys.full(), in_=ch(yd)[:, :, ts])
            P.dma("gpsimd", out=szs.full(), in_=ch(szd)[:, :, ts])
            P.dma("sync", out=oms.full(), in_=ch(omd)[:, :, ts])
            P.dma("gpsimd", out=ofs.full(), in_=ch(ofd)[:, :, ts])
            P.dma("sync", out=gs.full(), in_=ch(gd)[:, :, ts])
            P.dma("gpsimd", out=xs.full(), in_=ch(xb[l])[:, :, xsl])
            ps = pb[7]
            for kc in range(8):
                P.dve.tensor_tensor(out=ys[:, kc, :], in0=ys[:, kc, :], in1=szs[:, kc, :], op=ALU.mult)
                P.act.activation(out=sq.full(), in_=ys[:, kc, :], func=AF.Square)
                P.pe.matmul(out=ps.full(), lhsT=ones, rhs=sq.full(), start=(kc == 0), stop=(kc == 7))
            P.act.activation(out=rstd.full(), in_=ps.full(), func=AF.Ln, bias=eps[:, 0:1], scale=1.0 / 1024.0)
            P.act.activation(out=rstd.full(), in_=rstd.full(), func=AF.Exp, scale=-0.5)
            for kc in range(8):
                P.dve.scalar_tensor_tensor(out=yn[:, kc, :], in0=ys[:, kc, :], scalar=gsb[:, kc:kc + 1], in1=rstd.full(),
                                           op0=ALU.mult, op1=ALU.mult)
            for oc in range(8):
                i2 = oc % 2
                osl = slice(oc * 128, (oc + 1) * 128)
                pa, pbb, pc = pb[0 + i2 * 3], pb[1 + i2 * 3], pb[2 + i2 * 3]
                for kc in range(4):
                    P.pe.matmul(out=pa.full(), lhsT=Wa[:, kc, osl], rhs=oms[:, kc, :], start=(kc == 0), stop=(kc == 3))
                for kc in range(4):
                    P.pe.matmul(out=pbb.full(), lhsT=Wb[:, kc, osl], rhs=ofs[:, kc, :], start=(kc == 0), stop=(kc == 3))
                for kc in range(8):
                    P.pe.matmul(out=pc.full(), lhsT=Wc[:, kc, osl], rhs=yn[:, kc, :], start=(kc == 0), stop=(kc == 7))
                P.dve.tensor_tensor(out=m1[i2].full(), in0=pa.full(), in1=gs[:, oc, :], op=ALU.mult)
                P.dve.tensor_tensor(out=m2[i2].full(), in0=pbb.full(), in1=gs[:, 8 + oc, :], op=ALU.mult)
                P.dve.tensor_tensor(out=m3[i2].full(), in0=pc.full(), in1=gs[:, 16 + oc, :], op=ALU.mult)
                P.pool.tensor_tensor(out=m1[i2].full(), in0=m1[i2].full(), in1=m2[i2].full(), op=ALU.add)
                P.pool.tensor_tensor(out=mg[:, oc, :], in0=m1[i2].full(), in1=m3[i2].full(), op=ALU.add)
            for oc in range(8):
                i2 = oc % 2
                ps = pb[6 + i2]
                for kc in range(8):
                    P.pe.matmul(out=ps.full(), lhsT=Wo[:, kc, oc * 128:(oc + 1) * 128], rhs=mg[:, kc, :],
                                start=(kc == 0), stop=(kc == 7))
                P.dve.tensor_tensor(out=xo[i2].full(), in0=ps.full(), in1=xs[:, oc, :], op=ALU.add)
                P.pool.tensor_copy(out=txs[:, ti, oc, :], in_=xo[i2][:, 508:512])
                P.dma("sync" if i2 else "gpsimd", out=xmv[:, oc, xsl], in_=xo[i2].full())
        write_tails(txs)

    def phase_ffn(l, last):
        K = load_consts()
        ones, eps = K["cm"][:, 3, :], K["eps"]
        cstb = P.sb("cstb", [128, offD2["_n"]], F32)
        C = Cst(P, cstb, offD2)
        P.dma("sync", out=cstb.full(), in_=cstD_d[l])
        pb = [P.ps(f"pb{i}", [128, 512], F32) for i in range(8)]
        Wu = P.sb("Wu", [128, 8, 5632], BF16)
        Wd = P.sb("Wd", [128, 22, D], BF16)
        wubv = wub.full().re("(kc p) n -> p kc n", p=128)
        for (c0, c1) in ((0, 512), (2816, 3328), (512, 2816), (3328, 5632)):
            P.dma("sync" if c0 < 2816 else "gpsimd", out=Wu[:, :, c0:c1], in_=wubv[:, :, c0:c1])
        wdbv = wdb.full().re("(kc p) n -> p kc n", p=128)
        P.dma("sync", out=Wd[:, 0:11, :], in_=wdbv[:, 0:11, :])
        P.dma("gpsimd", out=Wd[:, 11:22, :], in_=wdbv[:, 11:22, :])
        xst = P.sb("xst", [128, 8, 512], F32)
        hn = P.sb("hn", [128, 8, 512], BF16)
        sq = P.sb("sq", [128, 512], F32)
        rstd = P.sb("rstd", [128, 512], F32)
        act = P.sb("act", [128, 22, 512], BF16)
        upre = [P.sb(f"upre{i}", [128, 516], F32) for i in range(2)]
        acc = [P.sb(f"acc{i}", [128, 512], F32) for i in range(2)]
        sg = P.sb("sg", [128, 512], F32)
        carry = P.sb("carry", [128, 44, 4], F32)
        xo = [P.sb(f"xo{i}", [128, 512], F32) for i in range(2)]
        txs = P.sb("txs", [128, 4, 8, 4], F32)
        xTv = xmid.full().re("(kc p) n -> p kc n", p=128)
        dst = out if last else xb[l + 1]
        dv = dst.full().re("(kc p) n -> p kc n", p=128)
        tiles = []
        for m in range(4):
            tiles.append((m * SW, 4, True, m))
            tiles.append((m * SW + 4, 512, False, m))
        pcnt = [0]
        for (c0, w, is_halo, m) in tiles:
            P.dma("sync", out=xst[:, :, 0:w], in_=xTv[:, :, c0:c0 + w])
            ps = pb[7]
            for kc in range(8):
                P.act.activation(out=sq[:, 0:w], in_=xst[:, kc, 0:w], func=AF.Square)
                P.pe.matmul(out=ps[:, 0:w], lhsT=ones, rhs=sq[:, 0:w], start=(kc == 0), stop=(kc == 7))
            P.act.activation(out=rstd[:, 0:w], in_=ps[:, 0:w], func=AF.Ln, bias=eps[:, 0:1], scale=1.0 / 1024.0)
            P.act.activation(out=rstd[:, 0:w], in_=rstd[:, 0:w], func=AF.Exp, scale=-0.5)
            for kc in range(8):
                P.dve.scalar_tensor_tensor(out=hn[:, kc, 0:w], in0=xst[:, kc, 0:w], scalar=C.col("g_ffn", kc),
                                           in1=rstd[:, 0:w], op0=ALU.mult, op1=ALU.mult)
            for i in range(22):
                accs = []
                for j, cg in enumerate((i, 22 + i)):
                    ps = pb[pcnt[0] % 4]
                    pcnt[0] += 1
                    for kc in range(8):
                        P.pe.matmul(out=ps[:, 0:w], lhsT=Wu[:, kc, cg * 128:(cg + 1) * 128], rhs=hn[:, kc, 0:w],
                                    start=(kc == 0), stop=(kc == 7))
                    if is_halo:
                        P.act.copy(out=carry[:, cg, :], in_=ps[:, 0:4])
                        continue
                    up = upre[j]
                    a0 = acc[j]
                    P.act.copy(out=up[:, 4:516], in_=ps.full())
                    P.act.activation(out=a0.full(), in_=ps.full(), func=AF.Identity, scale=C.col("fw2", cg), bias=C.col("fb", cg))
                    P.pool.tensor_copy(out=up[:, 0:4], in_=carry[:, cg, :])
                    P.dve.scalar_tensor_tensor(out=a0.full(), in0=up[:, 3:515], scalar=C.col("fw1", cg), in1=a0.full(),
                                               op0=ALU.mult, op1=ALU.add)
                    P.dve.scalar_tensor_tensor(out=a0.full(), in0=up[:, 2:514], scalar=C.col("fw0", cg), in1=a0.full(),
                                               op0=ALU.mult, op1=ALU.add)
                    accs.append(a0)
                if is_halo:
                    continue
                P.act.activation(out=sg.full(), in_=accs[0].full(), func=AF.Silu)
                P.pool.tensor_tensor(out=act[:, i, :], in0=sg.full(), in1=accs[1].full(), op=ALU.mult)
            if is_halo:
                continue
            for oc in range(8):
                i2 = oc % 2
                ps = pb[4 + i2]
                for i in range(22):
                    P.pe.matmul(out=ps.full(), lhsT=Wd[:, i, oc * 128:(oc + 1) * 128], rhs=act[:, i, :],
                                start=(i == 0), stop=(i == 21))
                P.dve.tensor_tensor(out=xo[i2].full(), in0=ps.full(), in1=xst[:, oc, :], op=ALU.add)
                if last:
                    P.dma("sync" if i2 else "gpsimd", out=dv[:, oc, m * 512:(m + 1) * 512], in_=xo[i2].full())
                else:
                    P.pool.tensor_copy(out=txs[:, m, oc, :], in_=xo[i2][:, 508:512])
                    P.dma("sync" if i2 else "gpsimd", out=dv[:, oc, m * SW + 4:m * SW + 516], in_=xo[i2].full())
        if not last:
            write_tails(txs)

    def gather_e1():
        gather_pairs(list(zip(sx, sxg)) + [(fx, fxg)])

    nl = L if stop is None else stop[0]
    done = False
    for l in range(nl):
        last_l = (stop is not None and l == nl - 1)
        phase_A(l)
        P.emit(final=False)
        ssd_scan(l, load_consts(), pass1=True)
        P.emit(final=False)
        if last_l and stop[1] == "A":
            break
        gather_e1()
        phase_attn(l)
        P.emit(final=False)
        phase_ssd2(l)
        P.emit(final=False)
        if last_l and stop[1] == "B":
            break
        phase_merge(l)
        P.emit(final=False)
        halo_exchange(xmid)
        P.emit(final=False)
        if last_l and stop[1] == "C":
            break
        phase_ffn(l, last=(l == L - 1))
        P.emit(final=False)
        if l < L - 1:
            halo_exchange(xb[l + 1])
            P.emit(final=False)
    loc = {"kxmg0": kxmg[0], "vxg0": vxg[0], "sxg0": sxg[0], "fxg": fxg, "qm": qm, "qf": qf, "fq": fq, "omd": omd, "ofd": ofd, "yd": yd,
           "xmid": xmid, "xb1": xb[1], "szd": szd, "gd": gd, "xtm": xtm, "btm": btm, "bct": bct, "dtd": dtd, "atd": atd}
    for name in dbg:
        src = loc[name]
        shp = list(src.h.shape) if hasattr(src.h, "shape") else None
        dd = P.dram("dbg_" + name, shp, src.h.dtype, EO)
        P.dma("sync", out=dd.full(), in_=src.full())
    P.emit(final=True)
    return nc, P


def _stripe_tokens(p):
    return np.concatenate([np.arange((4 * m + p) * 512, (4 * m + p + 1) * 512) for m in range(4)])


def fused_in_maps(inp):
    L = 2
    cpsA = [a_colpack(inp, l) for l in range(L)]
    cpsD = [d2_colpack(inp, l) for l in range(L)]
    offA = dict(cpsA[0].off)
    offA["_n"] = cpsA[0].n
    offD = dict(cpsD[0].off)
    offD["_n"] = cpsD[0].n
    cstA = np.stack([c.array() for c in cpsA])
    cstD = np.stack([c.array() for c in cpsD])
    w_kp = np.zeros((L, 256, 8, 96), np.float32)
    wukv = inp["mla_w_ukv"].reshape(L, 256, 8, 128)
    w_kp[:, :, :, 0:64] = wukv[:, :, :, 0:64]
    w_v = np.ascontiguousarray(wukv[:, :, :, 64:128].reshape(L, 256, 512))
    gssm = np.ascontiguousarray(inp["ssm_norm_g"].reshape(L, 8, 128).transpose(0, 2, 1))
    rowc = np.stack([fused_rowpack(inp, l) for l in range(L)])
    msk, cm = _bc_consts()
    mats = _const_mats()
    shared = {
        "w_in": np.ascontiguousarray(inp["w_in"]), "w_uq": np.ascontiguousarray(inp["mla_w_uq"]),
        "w_kp": np.ascontiguousarray(w_kp.reshape(L, 256, 768)), "w_v": w_v,
        "w_a": np.ascontiguousarray(inp["w_br_mla"]), "w_b": np.ascontiguousarray(inp["w_br_fox"]),
        "w_c": np.ascontiguousarray(inp["w_br_ssm"]), "w_o": np.ascontiguousarray(inp["w_out"]),
        "w_up": np.ascontiguousarray(inp["ffn_w_up"]), "w_dn": np.ascontiguousarray(inp["ffn_w_down"]),
        "cstA": cstA, "cstD": cstD, "gssm": gssm, "rowc": rowc, "msk": msk, "cm": cm, "mats": mats,
    }
    in_maps = []
    for c in range(8):
        b, p = c // 4, c % 4
        xT = np.zeros((D, 4 * SW), np.float32)
        xbT = inp["x"][b].T
        for m in range(4):
            s_ = 4 * m + p
            xT[:, m * SW + 4:m * SW + 516] = xbT[:, s_ * 512:(s_ + 1) * 512]
            if s_ > 0:
                xT[:, m * SW:m * SW + 4] = xbT[:, s_ * 512 - 4:s_ * 512]
        sel = np.zeros((128, 32), np.float32)
        if p >= 1:
            sel[:, p - 1] = 1.0
        else:
            sel[:, 4] = 1.0
        for s_ in range(16):
            if s_ % 4 == p:
                sel[:, 8 + s_] = 1.0
        for jr in range(4):
            sel[:, 24 + jr] = 1.0 if jr == p else 0.0
            sel[:, 28 + jr] = NEG if jr > p else 0.0
        d = dict(shared)
        d["x0"] = np.ascontiguousarray(xT)
        d["pos"] = np.ascontiguousarray(inp["positions"][b][_stripe_tokens(p)][None, :]).astype(np.int32)
        d["sel"] = sel
        in_maps.append(d)
    return in_maps, offA, offD


def kernel_fused(**inp):
    inp = {k: np.asarray(v) for k, v in inp.items()}
    in_maps, offA, offD = fused_in_maps(inp)
    if "F" not in _PROG_CACHE:
        _PROG_CACHE["F"] = build_fused(offA, offD)[0]
    res = run_bass_kernel_spmd(_PROG_CACHE["F"], in_maps, core_ids=list(range(8))).results
    xo = np.zeros((2, S_, D), np.float32)
    for c in range(8):
        b, p = c // 4, c % 4
        xo[b, _stripe_tokens(p), :] = np.asarray(res[c]["out"]).T
    return xo
```

```python
from contextlib import ExitStack
import numpy as np
import concourse.bass as bass
import concourse.mybir as mybir

F32 = mybir.dt.float32
BF16 = mybir.dt.bfloat16
I32 = mybir.dt.int32
ALU = mybir.AluOpType
AF = mybir.ActivationFunctionType
AX = mybir.AxisListType

COMPUTE = ("tensor", "vector", "scalar", "gpsimd")
QUEUES = ("sync", "gpsimd", "scalar")
NRING = 8


class View:
    __slots__ = ("buf", "ap", "key")

    def __init__(self, buf, ap, key=None):
        self.buf = buf
        self.ap = ap
        self.key = key

    def __getitem__(self, k):
        return View(self.buf, self.ap[k], self.key)

    def re(self, s, **kw):
        return View(self.buf, self.ap.rearrange(s, **kw), self.key)

    def bc(self, shape):
        return View(self.buf, self.ap.to_broadcast(shape), self.key)

    def bitcast(self, dt):
        return View(self.buf, self.ap.bitcast(dt), self.key)

    def k(self, key):
        return View(self.buf, self.ap, key)

    def f(self, fn):
        return View(self.buf, fn(self.ap), self.key)


class Buf:
    def __init__(self, name, handle, is_dram=False):
        self.name = name
        self.h = handle
        self.is_dram = is_dram
        self.regions = {}

    def full(self):
        ap = self.h.ap() if hasattr(self.h, "ap") and callable(getattr(self.h, "ap")) else self.h[:]
        return View(self, ap)

    def __getitem__(self, k):
        return View(self, self.h[k])


class Op:
    __slots__ = ("id", "eng", "meth", "kw", "deps", "is_dma", "signaled", "sem", "val", "prewait", "eidx")


class Eng:
    def __init__(self, P, name):
        self.P = P
        self.name = name

    def __getattr__(self, meth):
        def call(*a, **kw):
            assert not a, "use kwargs"
            return self.P._record(self.name, meth, kw)
        return call


class Prog:
    def __init__(self, nc):
        self.nc = nc
        self.ops = []
        self.gstack = ExitStack()
        self.stack = ExitStack()
        self.pe = Eng(self, "tensor")
        self.dve = Eng(self, "vector")
        self.act = Eng(self, "scalar")
        self.pool = Eng(self, "gpsimd")
        self.sp = Eng(self, "sync")
        st = self.gstack
        self.csem = {e: st.enter_context(nc.semaphore(f"c_{e}")) for e in COMPUTE}
        self.rings = {q: [st.enter_context(nc.semaphore(f"d_{q}{i}")) for i in range(NRING)] for q in QUEUES}
        self.ccsem = st.enter_context(nc.semaphore("ccsem"))
        self.cccount = 0
        self.ccount = {e: 0 for e in COMPUTE}
        self.dcount = {q: 0 for q in QUEUES}
        self.waited = {e: {} for e in ("sync",) + COMPUTE}
        self.emitted = 0
        self.barrier = []
        self.stats = {}
        self.nwaits = 0

    def sb(self, name, shape, dtype):
        self.nuid = getattr(self, "nuid", 0) + 1
        name = f"{name}_s{self.nuid}"
        t = self.stack.enter_context(self.nc.sbuf_tensor(name, list(shape), dtype))
        return Buf(name, t)

    def ps(self, name, shape, dtype):
        self.nuid = getattr(self, "nuid", 0) + 1
        name = f"{name}_p{self.nuid}"
        t = self.stack.enter_context(self.nc.psum_tensor(name, list(shape), dtype))
        return Buf(name, t)

    def dram(self, name, shape, dtype, kind="Internal"):
        t = self.nc.dram_tensor(name, list(shape), dtype, kind=kind)
        return Buf(name, t, is_dram=True)

    def _record(self, eng, meth, kw):
        op = Op()
        op.id = len(self.ops)
        op.eng = eng
        op.meth = meth
        op.kw = kw
        op.is_dma = meth in ("dma_start", "dma_start_transpose", "collective_compute")
        op.signaled = False
        op.sem = None
        op.val = 0
        op.prewait = None
        deps = set()
        extra_r = kw.pop("_reads", [])
        extra_w = kw.pop("_writes", [])
        writes, reads = [], []
        for k, v in kw.items():
            vs = v if isinstance(v, (list, tuple)) else [v]
            for x in vs:
                if isinstance(x, View):
                    if k in ("out", "accum_out", "outs") or (k == "ap" and meth in ("memset", "memzero")):
                        writes.append(x)
                    else:
                        reads.append(x)
        reads += extra_r
        writes += extra_w
        for v in reads:
            self._gather(v, False, deps)
        for v in writes:
            self._gather(v, True, deps)
        for v in reads:
            self._update(v, False, op.id)
        for v in writes:
            self._update(v, True, op.id)
        deps.discard(op.id)
        op.deps = deps
        self.ops.append(op)
        return op

    def _gather(self, v, is_write, deps):
        R = v.buf.regions
        if v.key is None:
            regs = list(R.values())
        else:
            regs = [R[k] for k in (v.key, None) if k in R]
        for reg in regs:
            if reg[0] is not None:
                deps.add(reg[0])
            if is_write:
                deps.update(reg[1])

    def _update(self, v, is_write, oid):
        R = v.buf.regions
        if is_write:
            if v.key is None:
                R.clear()
            R[v.key] = [oid, []]
        else:
            R.setdefault(v.key, [None, []])[1].append(oid)

    def dma(self, q, out, in_, **kw):
        eng = {"sync": self.sp, "gpsimd": self.pool, "scalar": self.act}[q]
        return eng.dma_start(out=out, in_=in_, **kw)

    def emit(self, final=True):
        nc = self.nc
        ops = self.ops
        phase = ops[self.emitted:]
        first_id = self.emitted
        self.emitted = len(ops)
        for op in phase:
            for d in op.deps:
                dop = ops[d]
                if d < first_id:
                    continue
                if dop.eng == "tensor" and op.eng == "tensor" and not dop.is_dma and not op.is_dma:
                    continue
                dop.signaled = True
        per = {}
        for op in phase:
            per.setdefault(op.eng, []).append(op)
        for e, lst in per.items():
            for op in reversed(lst):
                if not op.is_dma:
                    op.signaled = True
                    break
        for op in phase:
            if op.meth == "collective_compute":
                self.cccount += 1
                op.sem = self.ccsem
                op.val = self.cccount
                op.signaled = True
            elif op.is_dma:
                k = self.dcount[op.eng]
                self.dcount[op.eng] += 1
                op.sem = self.rings[op.eng][k % NRING]
                op.val = 16 * (k // NRING + 1)
                if k >= NRING:
                    op.prewait = (op.sem, 16 * (k // NRING))
                op.signaled = True
            elif op.signaled:
                self.ccount[op.eng] += 1
                op.sem = self.csem[op.eng]
                op.val = self.ccount[op.eng]
        for e, v in per.items():
            self.stats[e] = self.stats.get(e, 0) + len(v)
        barrier_in = list(self.barrier)
        dcount = self.dcount
        rings = self.rings

        def dma_final_waits():
            ws = []
            for q in QUEUES:
                n = dcount[q]
                for i in range(min(n, NRING)):
                    cnt = (n - 1 - i) // NRING + 1
                    ws.append((rings[q][i], 16 * cnt))
            if self.cccount > 0:
                ws.append((self.ccsem, self.cccount))
            return ws

        def run(engname, e):
            waited = self.waited[engname]

            def do_waits(ws):
                for sem, val in ws:
                    key = id(sem)
                    if waited.get(key, 0) >= val:
                        continue
                    waited[key] = val
                    e.wait_ge(sem, val)
                    self.nwaits += 1

            do_waits(barrier_in)
            for op in per.get(engname, []):
                ws = []
                if op.prewait is not None:
                    ws.append(op.prewait)
                for d in sorted(op.deps):
                    dop = ops[d]
                    if dop.sem is None:
                        continue
                    if dop.eng == "tensor" and op.eng == "tensor" and not dop.is_dma and not op.is_dma:
                        continue
                    ws.append((dop.sem, dop.val))
                do_waits(ws)
                kw = {}
                for k, v in op.kw.items():
                    if isinstance(v, View):
                        kw[k] = v.ap
                    elif isinstance(v, (list, tuple)) and v and isinstance(v[0], View):
                        kw[k] = [x.ap for x in v]
                    else:
                        kw[k] = v
                ins = getattr(e, op.meth)(**kw)
                if op.signaled:
                    ins.then_inc(op.sem, 16 if (op.is_dma and op.meth != "collective_compute") else 1)
            if final and engname == "sync":
                do_waits(dma_final_waits())

        with nc.Block() as block:
            @block.sync
            def _(e):
                run("sync", e)

            @block.tensor
            def _(e):
                run("tensor", e)

            @block.vector
            def _(e):
                run("vector", e)

            @block.scalar
            def _(e):
                run("scalar", e)

            @block.gpsimd
            def _(e):
                run("gpsimd", e)
        bar = dma_final_waits()
        for e in COMPUTE:
            if self.ccount[e] > 0:
                bar.append((self.csem[e], self.ccount[e]))
        self.barrier = bar
        self.stats["waits"] = self.nwaits
        self.stack.close()
        self.stack = ExitStack()
        if final:
            self.gstack.close()


from concourse.bass_utils import run_bass_kernel_spmd
import ml_dtypes

NBF = ml_dtypes.bfloat16
D = 1024
T = 2048
HALO = 4
NEG = -30000.0


class ColPack:
    def __init__(self):
        self.cols = []
        self.off = {}
        self.n = 0

    def add(self, name, vec, rows=128):
        vec = np.asarray(vec, np.float32).reshape(-1)
        assert vec.size % rows == 0
        m = vec.reshape(-1, rows).T
        a = np.zeros((128, m.shape[1]), np.float32)
        a[:rows] = m
        self.off[name] = (self.n, m.shape[1], rows)
        self.cols.append(a)
        self.n += m.shape[1]

    def array(self):
        return np.ascontiguousarray(np.concatenate(self.cols, axis=1))


class Cst:
    def __init__(self, P, buf, off):
        self.buf = buf
        self.off = off

    def col(self, name, j=0, rows=None):
        o, n, r = self.off[name]
        r = rows or r
        return self.buf[0:r, o + j:o + j + 1]

    def cols(self, name):
        o, n, r = self.off[name]
        return self.buf[0:r, o:o + n]


def new_nc():
    return bass.Bass("TRN2", target_bir_lowering=False)


def load_cast(P, q, dram_view, stage_view, bf_view, cast_eng):
    P.dma(q, out=stage_view, in_=dram_view)
    cast_eng.tensor_copy(out=bf_view, in_=stage_view)


A_OFF = None


def a_colpack(inp, l):
    cp = ColPack()
    cp.add("g_mix", inp["norm_mix_g"][l])
    cp.add("g_cq", inp["mla_q_norm_g"][l])
    cp.add("g_ckv", inp["mla_kv_norm_g"][l])
    cp.add("g_q", inp["mla_q_gain"][l], 96)
    cp.add("g_k", inp["mla_k_gain"][l], 96)
    cp.add("g_fq", inp["fox_q_gain"][l], 64)
    cp.add("g_fk", inp["fox_k_gain"][l], 64)
    cp.add("b_f", inp["fox_b_f"][l], 8)
    cw = inp["ssm_conv_w"][l]
    for k in range(4):
        cp.add(f"cw{k}", cw[k])
    cp.add("cb", inp["ssm_conv_b"][l])
    cp.add("dt_b", inp["ssm_dt_bias"][l], 16)
    cp.add("A_log", inp["ssm_A_log"][l], 16)
    cp.add("b_gate", inp["b_gate"][l])
    inv = 1.0 / (10000.0 ** (np.arange(0, 32, 2, dtype=np.float32) / 32.0))
    invf = np.zeros(96, np.float32)
    invf[64:80] = inv
    invf[80:96] = inv
    cp.add("invf", invf, 96)
    return cp


def build_A(off):
    nc = new_nc()
    P = Prog(nc)
    TT = T + HALO
    NT = T // 512
    EI, EO = "ExternalInput", "ExternalOutput"
    xT = P.dram("xT", [D, TT], F32, EI)
    pos = P.dram("pos", [1, T], I32, EI)
    w_in = P.dram("w_in", [D, 7864], F32, EI)
    w_uq = P.dram("w_uq", [384, 768], F32, EI)
    w_kp = P.dram("w_kp", [256, 768], F32, EI)
    w_v = P.dram("w_v", [256, 512], F32, EI)
    cst_d = P.dram("cst", [128, off["_n"]], F32, EI)
    mats = P.dram("mats", [128, 2 * 96], F32, EI)
    o_qm = P.dram("o_qm", [8, 96, T], BF16, EO)
    o_km = P.dram("o_km", [8, 96, T], BF16, EO)
    o_vm = P.dram("o_vm", [512, T], BF16, EO)
    o_qf = P.dram("o_qf", [8, 64, T], BF16, EO)
    o_kf = P.dram("o_kf", [8, 64, T], BF16, EO)
    o_vf = P.dram("o_vf", [512, T], BF16, EO)
    o_lf = P.dram("o_lf", [8, T], F32, EO)
    o_sz = P.dram("o_sz", [1024, T], BF16, EO)
    o_xbc = P.dram("o_xbc", [1536, T], BF16, EO)
    o_dt = P.dram("o_dt", [16, T], F32, EO)
    o_a = P.dram("o_a", [16, T], F32, EO)
    o_g = P.dram("o_g", [3072, T], BF16, EO)

    cstb = P.sb("cstb", [128, off["_n"]], F32)
    C = Cst(P, cstb, off)
    P.dma("sync", out=cstb.full(), in_=cst_d.full())
    matf = P.sb("matf", [128, 192], F32)
    matb = P.sb("matb", [128, 192], BF16)
    P.dma("sync", out=matf.full(), in_=mats.full())
    P.dve.tensor_copy(out=matb.full(), in_=matf.full())
    prh = matb[0:96, 0:96]
    sel = matb[0:32, 96:192]
    ones = P.sb("ones", [128, 128], F32)
    P.dve.memset(ap=ones.full(), constant=1.0)
    eps = P.sb("eps", [128, 1], F32)
    P.dve.memset(ap=eps.full(), constant=1e-6)
    one1 = P.sb("one1", [128, 1], F32)
    P.dve.memset(ap=one1.full(), constant=1.0)
    nbf = P.sb("nbf", [8, 1], F32)
    P.dve.tensor_scalar(out=nbf.full(), in0=C.col("b_f"), scalar1=-1.0, scalar2=None, op0=ALU.mult)
    Aneg = P.sb("Aneg", [16, 1], F32)
    P.act.activation(out=Aneg.full(), in_=C.col("A_log"), func=AF.Exp)
    P.dve.tensor_scalar(out=Aneg.full(), in0=Aneg.full(), scalar1=-1.0, scalar2=None, op0=ALU.mult)

    pb = [P.ps(f"pb{i}", [128, 512], F32) for i in range(8)]
    pbi = {}

    def nxt_ps(lo=0, hi=4):
        i = pbi.get(lo, 0)
        pbi[lo] = (i + 1) % (hi - lo)
        return pb[lo + i]

    Ctab = P.sb("Ctab", [96, T], F32)
    Stab = P.sb("Stab", [96, T], F32)
    posi = P.sb("posi", [96, 512], I32)
    posf = P.sb("posf", [96, 512], F32)
    rr_tmp = P.sb("rr_tmp", [96, 512], F32)
    rr_i = P.sb("rr_i", [96, 512], I32)
    rr_m = P.sb("rr_m", [96, 512], F32)

    def sin_table(outv, phase):
        P.dve.tensor_scalar(out=rr_tmp.full(), in0=posf.full(), scalar1=C.col("invf"), scalar2=phase,
                            op0=ALU.mult, op1=ALU.add)
        P.dve.tensor_scalar(out=rr_m.full(), in0=rr_tmp.full(), scalar1=1.0 / (2 * np.pi), scalar2=None, op0=ALU.mult)
        P.dve.tensor_copy(out=rr_i.full(), in_=rr_m.full())
        P.dve.tensor_copy(out=rr_m.full(), in_=rr_i.full())
        P.dve.scalar_tensor_tensor(out=rr_tmp.full(), in0=rr_m.full(), scalar=-2 * np.pi, in1=rr_tmp.full(),
                                   op0=ALU.mult, op1=ALU.add)
        P.dve.tensor_scalar(out=rr_m.full(), in0=rr_tmp.full(), scalar1=np.pi, scalar2=-2 * np.pi, op0=ALU.is_gt, op1=ALU.mult)
        P.dve.tensor_tensor(out=rr_tmp.full(), in0=rr_tmp.full(), in1=rr_m.full(), op=ALU.add)
        P.dve.tensor_scalar(out=rr_m.full(), in0=rr_tmp.full(), scalar1=-np.pi, scalar2=2 * np.pi, op0=ALU.is_lt, op1=ALU.mult)
        P.dve.tensor_tensor(out=rr_tmp.full(), in0=rr_tmp.full(), in1=rr_m.full(), op=ALU.add)
        P.act.activation(out=outv, in_=rr_tmp.full(), func=AF.Sin)

    for i in range(NT):
        P.dma("sync", out=posi.full(), in_=pos[:, i * 512:(i + 1) * 512].f(lambda a: a.partition_broadcast(96)))
        P.dve.tensor_copy(out=posf.full(), in_=posi.full())
        sin_table(Stab[:, i * 512:(i + 1) * 512], 0.0)
        sin_table(Ctab[:, i * 512:(i + 1) * 512], np.pi / 2)
    P.dve.memset(ap=Stab[0:64, :], constant=0.0)
    P.dve.memset(ap=Ctab[0:64, :], constant=1.0)

    hn = P.sb("hn", [128, 8, TT], BF16)
    xst = P.sb("xst", [128, 8, 512], F32)
    sq = P.sb("sq", [128, 512], F32)
    rstd = P.sb("rstd", [128, 512], F32)
    xTv = xT.full().re("(kc p) n -> p kc n", p=128)

    def rstd_from(ps_view, n_feat, rows, width, rstd_view):
        P.act.activation(out=rstd_view, in_=ps_view, func=AF.Sqrt, bias=eps[0:rows, 0:1], scale=1.0 / n_feat)
        P.dve.reciprocal(out=rstd_view, in_=rstd_view)

    tiles = [(0, HALO)] + [(HALO + i * 512, 512) for i in range(NT)]
    for (c0, w) in tiles:
        P.dma("sync", out=xst[:, :, 0:w], in_=xTv[:, :, c0:c0 + w])
        ps = nxt_ps(4, 6)
        for kc in range(8):
            P.act.activation(out=sq[:, 0:w], in_=xst[:, kc, 0:w], func=AF.Square)
            P.pe.matmul(out=ps[:, 0:w], lhsT=ones.full(), rhs=sq[:, 0:w], start=(kc == 0), stop=(kc == 7))
        rstd_from(ps[:, 0:w], 1024.0, 128, w, rstd[:, 0:w])
        for kc in range(8):
            P.dve.scalar_tensor_tensor(out=hn[:, kc, c0:c0 + w], in0=xst[:, kc, 0:w], scalar=C.col("g_mix", kc),
                                       in1=rstd[:, 0:w], op0=ALU.mult, op1=ALU.mult)

    wst = [P.sb(f"wst{i}", [128, 8, 512], F32) for i in range(2)]
    wbf = [P.sb(f"wbf{i}", [128, 8, 512], BF16) for i in range(2)]
    wcnt = [0]
    w_inv = w_in.full().re("(kc p) n -> p kc n", p=128)

    def load_w(c0, ncols):
        i = wcnt[0] % 2
        wcnt[0] += 1
        q = "sync" if i == 0 else "gpsimd"
        P.dma(q, out=wst[i][:, :, 0:ncols], in_=w_inv[:, :, c0:c0 + ncols])
        P.pool.tensor_copy(out=wbf[i][:, :, 0:ncols], in_=wst[i][:, :, 0:ncols])
        return wbf[i]

    def proj(wb, wc0, m, c0, w, ps_view):
        for kc in range(8):
            P.pe.matmul(out=ps_view, lhsT=wb[:, kc, wc0:wc0 + m], rhs=hn[:, kc, c0:c0 + w],
                        start=(kc == 0), stop=(kc == 7))

    ostg_cnt = [0]
    ostg = [P.sb(f"ostg{i}", [128, 512], BF16) for i in range(4)]

    def next_ostg():
        i = ostg_cnt[0] % 4
        ostg_cnt[0] += 1
        return ostg[i]

    def out_dma(dst_view, src_view):
        q = "sync" if ostg_cnt[0] % 2 else "gpsimd"
        P.dma(q, out=dst_view, in_=src_view)

    hraw = P.sb("hraw", [96, 512], F32)
    hsq = P.sb("hsq", [96, 512], F32)
    hrs = P.sb("hrs", [96, 512], F32)
    hnf = P.sb("hnf", [96, 512], F32)
    hnb = P.sb("hnb", [96, 512], BF16)
    ht1 = P.sb("ht1", [96, 512], F32)
    ht2 = P.sb("ht2", [96, 512], F32)

    def headnorm(ps_view, d, gain_col, rope, tok0, dst_view):
        P.act.activation(out=hsq[0:d, :], in_=ps_view, func=AF.Square)
        P.act.copy(out=hraw[0:d, :], in_=ps_view)
        ps2 = nxt_ps(4, 6)
        P.pe.matmul(out=ps2[0:d, :], lhsT=ones[0:d, 0:d], rhs=hsq[0:d, :], start=True, stop=True)
        rstd_from(ps2[0:d, :], float(d), d, 512, hrs[0:d, :])
        og = next_ostg()
        if not rope:
            P.dve.scalar_tensor_tensor(out=og[0:d, :], in0=hraw[0:d, :], scalar=gain_col, in1=hrs[0:d, :],
                                       op0=ALU.mult, op1=ALU.mult)
        else:
            P.dve.scalar_tensor_tensor(out=hnf[0:d, :], in0=hraw[0:d, :], scalar=gain_col, in1=hrs[0:d, :],
                                       op0=ALU.mult, op1=ALU.mult)
            P.act.copy(out=hnb[0:d, :], in_=hnf[0:d, :])
            ps3 = nxt_ps(6, 8)
            P.pe.matmul(out=ps3[0:d, :], lhsT=prh, rhs=hnb[0:d, :], start=True, stop=True)
            P.dve.tensor_tensor(out=ht1[0:d, :], in0=hnf[0:d, :], in1=Ctab[0:d, tok0:tok0 + 512], op=ALU.mult)
            P.dve.tensor_tensor(out=ht2[0:d, :], in0=ps3[0:d, :], in1=Stab[0:d, tok0:tok0 + 512], op=ALU.mult)
            P.pool.tensor_tensor(out=og[0:d, :], in0=ht1[0:d, :], in1=ht2[0:d, :], op=ALU.add)
        out_dma(dst_view, og[0:d, :])

    lat = P.sb("lat", [128, 3, 512], F32)
    latn = P.sb("latn", [128, 3, 512], BF16)

    def latent_norm(ps_list, gname):
        nch = len(ps_list)
        ps2 = nxt_ps(4, 6)
        for i, psv in enumerate(ps_list):
            P.act.activation(out=sq.full(), in_=psv, func=AF.Square)
            P.act.copy(out=lat[:, i, :], in_=psv)
            P.pe.matmul(out=ps2.full(), lhsT=ones.full(), rhs=sq.full(), start=(i == 0), stop=(i == nch - 1))
        rstd_from(ps2.full(), 128.0 * nch, 128, 512, rstd.full())
        for i in range(nch):
            P.dve.scalar_tensor_tensor(out=latn[:, i, :], in0=lat[:, i, :], scalar=C.col(gname, i), in1=rstd.full(),
                                       op0=ALU.mult, op1=ALU.mult)

    def small_w(name, dram, kc_n, ncols, i):
        stg = wst[i].full().re("p a b -> p (a b)")[:, 0:kc_n * ncols].re("p (a b) -> p a b", a=kc_n)
        bfb = P.sb(name, [128, kc_n, ncols], BF16)
        P.dma("gpsimd", out=stg, in_=dram.full().re("(kc p) n -> p kc n", p=128))
        P.pool.tensor_copy(out=bfb.full(), in_=stg)
        return bfb

    uqb = small_w("uqb", w_uq, 3, 768, 0)
    kpb = small_w("kpb", w_kp, 2, 768, 1)
    wvb = small_w("wvb", w_v, 2, 512, 0)
    main = tiles[1:]
    wb = load_w(0, 384)
    for ti, (c0, w) in enumerate(main):
        pss = []
        for ch in range(3):
            ps = nxt_ps(0, 4)
            proj(wb, ch * 128, 128, c0, 512, ps.full())
            pss.append(ps.full())
        latent_norm(pss, "g_cq")
        for h in range(8):
            ps = nxt_ps(0, 4)
            for kc in range(3):
                P.pe.matmul(out=ps[0:96, :], lhsT=uqb[:, kc, h * 96:(h + 1) * 96], rhs=latn[:, kc, :],
                            start=(kc == 0), stop=(kc == 2))
            headnorm(ps[0:96, :], 96, C.col("g_q"), True, ti * 512, o_qm[h, :, ti * 512:(ti + 1) * 512])
    wb = load_w(384, 288)
    krb = P.sb("krb", [32, 512], BF16)
    for ti, (c0, w) in enumerate(main):
        pss = []
        for ch in range(2):
            ps = nxt_ps(0, 4)
            proj(wb, ch * 128, 128, c0, 512, ps.full())
            pss.append(ps.full())
        ps = nxt_ps(0, 4)
        proj(wb, 256, 32, c0, 512, ps[0:32, :])
        P.act.copy(out=krb.full(), in_=ps[0:32, :])
        latent_norm(pss, "g_ckv")
        for h in range(8):
            ps = nxt_ps(0, 4)
            for kc in range(2):
                P.pe.matmul(out=ps[0:96, :], lhsT=kpb[:, kc, h * 96:(h + 1) * 96], rhs=latn[:, kc, :],
                            start=(kc == 0), stop=False)
            P.pe.matmul(out=ps[0:96, :], lhsT=sel, rhs=krb.full(), start=False, stop=True)
            headnorm(ps[0:96, :], 96, C.col("g_k"), True, ti * 512, o_km[h, :, ti * 512:(ti + 1) * 512])
        for ch in range(4):
            ps = nxt_ps(0, 4)
            for kc in range(2):
                P.pe.matmul(out=ps.full(), lhsT=wvb[:, kc, ch * 128:(ch + 1) * 128], rhs=latn[:, kc, :],
                            start=(kc == 0), stop=(kc == 1))
            og = next_ostg()
            P.act.copy(out=og.full(), in_=ps.full())
            out_dma(o_vm[ch * 128:(ch + 1) * 128, ti * 512:(ti + 1) * 512], og.full())
    for (base, gname, dst) in ((672, "g_fq", o_qf), (672 + 512, "g_fk", o_kf)):
        wb = load_w(base, 512)
        for ti, (c0, w) in enumerate(main):
            for h in range(8):
                ps = nxt_ps(0, 4)
                proj(wb, h * 64, 64, c0, 512, ps[0:64, :])
                headnorm(ps[0:64, :], 64, C.col(gname), False, ti * 512, dst[h, :, ti * 512:(ti + 1) * 512])
    def plain_group(base, ncols, func, bias_name, dst, dst_row0):
        wb = load_w(base, ncols)
        for ti, (c0, w) in enumerate(main):
            for ch in range(ncols // 128):
                ps = nxt_ps(0, 4)
                proj(wb, ch * 128, 128, c0, 512, ps.full())
                og = next_ostg()
                if bias_name is None:
                    P.act.activation(out=og.full(), in_=ps.full(), func=func)
                else:
                    P.act.activation(out=og.full(), in_=ps.full(), func=func,
                                     bias=C.col(bias_name, (dst_row0 // 128) + ch))
                out_dma(dst[dst_row0 + ch * 128:dst_row0 + (ch + 1) * 128, ti * 512:(ti + 1) * 512], og.full())

    plain_group(672 + 1024, 512, AF.Copy, None, o_vf, 0)
    FB = 672 + 1536
    SB = 672 + 1544
    wf = load_w(FB, 8)
    lf1 = P.sb("lf1", [16, 512], F32)
    lf2 = P.sb("lf2", [16, 512], F32)
    for ti, (c0, w) in enumerate(main):
        ps = nxt_ps(0, 4)
        proj(wf, 0, 8, c0, 512, ps[0:8, :])
        P.act.activation(out=lf1[0:8, :], in_=ps[0:8, :], func=AF.Exp, bias=nbf[0:8, 0:1], scale=-1.0)
        P.act.activation(out=lf1[0:8, :], in_=lf1[0:8, :], func=AF.Ln, bias=one1[0:8, 0:1], scale=1.0)
        P.dve.tensor_scalar(out=lf2[0:8, :], in0=lf1[0:8, :], scalar1=-1.0, scalar2=None, op0=ALU.mult)
        P.dma("sync", out=o_lf[:, ti * 512:(ti + 1) * 512], in_=lf2[0:8, :])
    wd = load_w(SB + 1024 + 1536, 16)
    dt1 = P.sb("dt1", [16, 512], F32)
    dt2 = P.sb("dt2", [16, 512], F32)
    for ti, (c0, w) in enumerate(main):
        ps = nxt_ps(0, 4)
        proj(wd, 0, 16, c0, 512, ps[0:16, :])
        P.act.activation(out=dt1.full(), in_=ps[0:16, :], func=AF.Exp, bias=C.col("dt_b"), scale=1.0)
        P.act.activation(out=dt1.full(), in_=dt1.full(), func=AF.Ln, bias=one1[0:16, 0:1], scale=1.0)
        P.dma("sync", out=o_dt[:, ti * 512:(ti + 1) * 512], in_=dt1.full())
        P.dve.tensor_scalar(out=dt2.full(), in0=dt1.full(), scalar1=Aneg[:, 0:1], scalar2=None, op0=ALU.mult)
        P.dma("sync", out=o_a[:, ti * 512:(ti + 1) * 512], in_=dt2.full())
    for blk in range(2):
        plain_group(SB + blk * 512, 512, AF.Silu, None, o_sz, blk * 512)
    upre = P.sb("upre", [128, 516], F32)
    carry = P.sb("carry", [128, 12, 4], F32)
    acc = [P.sb(f"acc{i}", [128, 512], F32) for i in range(2)]
    for blk in range(3):
        wb = load_w(SB + 1024 + blk * 512, 512)
        for ch in range(4):
            cg = blk * 4 + ch
            ps = nxt_ps(0, 4)
            proj(wb, ch * 128, 128, 0, HALO, ps[:, 0:HALO])
            P.act.copy(out=carry[:, cg, :], in_=ps[:, 0:HALO])
        for ti, (c0, w) in enumerate(main):
            for ch in range(4):
                cg = blk * 4 + ch
                ps = nxt_ps(0, 4)
                proj(wb, ch * 128, 128, c0, 512, ps.full())
                P.act.copy(out=upre[:, 4:516], in_=ps.full())
                P.dve.tensor_copy(out=upre[:, 0:4], in_=carry[:, cg, :])
                P.pool.tensor_copy(out=carry[:, cg, :], in_=upre[:, 512:516])
                a0 = acc[0]
                P.dve.tensor_scalar(out=a0.full(), in0=upre[:, 4:516], scalar1=C.col("cw3", cg), scalar2=C.col("cb", cg),
                                    op0=ALU.mult, op1=ALU.add)
                for k in range(3):
                    P.dve.scalar_tensor_tensor(out=a0.full(), in0=upre[:, 1 + k:513 + k], scalar=C.col(f"cw{k}", cg),
                                               in1=a0.full(), op0=ALU.mult, op1=ALU.add)
                og = next_ostg()
                P.act.activation(out=og.full(), in_=a0.full(), func=AF.Silu)
                out_dma(o_xbc[cg * 128:(cg + 1) * 128, ti * 512:(ti + 1) * 512], og.full())
    GB = SB + 2576
    for blk in range(6):
        plain_group(GB + blk * 512, 512, AF.Sigmoid, "b_gate", o_g, blk * 512)
    P.emit()
    return nc, P


def _bf(a):
    return np.asarray(a).astype(np.float32)


_PROG_CACHE = {}


def _const_mats():
    m = np.zeros((128, 192), np.float32)
    for i in range(16):
        m[80 + i, 64 + i] = -1.0
        m[64 + i, 80 + i] = 1.0
    for i in range(32):
        m[i, 96 + 64 + i] = 1.0
    return m


def run_A(inp, l, x_full, pos_full):
    cp = a_colpack(inp, l)
    off = dict(cp.off)
    off["_n"] = cp.n
    if "A" not in _PROG_CACHE:
        _PROG_CACHE["A"] = build_A(off)[0]
    nc = _PROG_CACHE["A"]
    cst = cp.array()
    wukv = inp["mla_w_ukv"][l].reshape(256, 8, 128)
    w_kp = np.zeros((256, 8, 96), np.float32)
    w_kp[:, :, 0:64] = wukv[:, :, 0:64]
    w_v = np.ascontiguousarray(wukv[:, :, 64:128].reshape(256, 512))
    mats = _const_mats()
    xf = x_full.reshape(16384, D)
    in_maps = []
    for c in range(8):
        t0 = c * T
        xt = np.zeros((D, T + HALO), np.float32)
        xt[:, HALO:] = xf[t0:t0 + T].T
        if c % 4 != 0:
            xt[:, 0:HALO] = xf[t0 - HALO:t0].T
        in_maps.append({
            "xT": np.ascontiguousarray(xt),
            "pos": np.ascontiguousarray(pos_full.reshape(1, 16384)[:, t0:t0 + T]).astype(np.int32),
            "w_in": np.ascontiguousarray(inp["w_in"][l]),
            "w_uq": np.ascontiguousarray(inp["mla_w_uq"][l]),
            "w_kp": np.ascontiguousarray(w_kp.reshape(256, 768)),
            "w_v": w_v, "cst": cst, "mats": mats,
        })
    res = run_bass_kernel_spmd(nc, in_maps, core_ids=list(range(8)))
    return res.results


S_ = 8192
NKT = S_ // 128
NQT = S_ // 512


def build_BC():
    nc = new_nc()
    P = Prog(nc)
    EI, EO = "ExternalInput", "ExternalOutput"
    qm = P.dram("qm", [2, 96, S_], BF16, EI)
    km = P.dram("km", [2, 96, S_], BF16, EI)
    vm = P.dram("vm", [2, 128, NKT, 64], BF16, EI)
    qf = P.dram("qf", [2, 64, S_], BF16, EI)
    kf = P.dram("kf", [2, 64, S_], BF16, EI)
    vf = P.dram("vf", [2, 128, NKT, 64], BF16, EI)
    lf = P.dram("lf", [2, 128, NKT], F32, EI)
    msk = P.dram("msk", [128, 8, 512], F32, EI)
    cm = P.dram("cm", [128, 4, 128], F32, EI)
    x_tm = P.dram("x_tm", [128, NKT, 256], BF16, EI)
    B_tm = P.dram("B_tm", [128, NKT, 128], BF16, EI)
    BT = P.dram("BT", [128, S_], BF16, EI)
    CT = P.dram("CT", [128, S_], BF16, EI)
    dt_tm = P.dram("dt_tm", [128, NKT, 4], F32, EI)
    a_tm = P.dram("a_tm", [128, NKT, 4], F32, EI)
    Dv = P.dram("Dv", [128, 4], F32, EI)
    o_m = P.dram("o_m", [2, 64, S_], BF16, EO)
    o_f = P.dram("o_f", [2, 64, S_], BF16, EO)
    o_y = P.dram("o_y", [128, NKT, 256], F32, EO)
    fsc = P.dram("fsc", [3, S_], BF16)

    cmb = P.sb("cmb", [128, 4, 128], F32)
    P.dma("sync", out=cmb.full(), in_=cm.full())
    tri, trimask, ident, ones = cmb[:, 0, :], cmb[:, 1, :], cmb[:, 2, :], cmb[:, 3, :]
    mskb = P.sb("mskb", [128, 8, 512], F32)
    P.dma("gpsimd", out=mskb.full(), in_=msk.full())
    zero = P.sb("zero", [128, 1], F32)
    P.dve.memset(ap=zero.full(), constant=0.0)

    pb = [P.ps(f"pb{i}", [128, 512], F32) for i in range(8)]
    K_sb = P.sb("K_sb", [128, S_], BF16)
    Q_sb = P.sb("Q_sb", [128, S_], BF16)
    V_sb = P.sb("V_sb", [128, NKT, 128], BF16)
    P.dve.memset(ap=V_sb[:, :, 64:128], constant=1.0)
    pt = [P.sb(f"pt{i}", [128, 512], BF16) for i in range(3)]
    mt = [P.sb(f"mt{i}", [128, 512], F32) for i in range(2)]
    rl = P.sb("rl", [128, 512], F32)
    rl2 = P.sb("rl2", [64, 512], F32)
    ot = [P.sb(f"ot{i}", [64, 512], BF16) for i in range(2)]
    negF = P.sb("negF", [128, NKT], F32)

    cnt = [0, 0, 0]

    def attention(dk, scale, mask0, bias_fn, out_dram_h):
        for qt in range(NQT):
            oacc = pb[3 + qt % 2]
            nk = 4 * qt + 4
            for kt in range(nk):
                i3 = cnt[0] % 3
                cnt[0] += 1
                ps = pb[i3]
                P.pe.matmul(out=ps.full(), lhsT=K_sb[0:dk, kt * 128:(kt + 1) * 128],
                            rhs=Q_sb[0:dk, qt * 512:(qt + 1) * 512], start=True, stop=True)
                if kt >= 4 * qt:
                    m = mt[cnt[1] % 2]
                    cnt[1] += 1
                    P.dve.tensor_tensor(out=m.full(), in0=ps.full(), in1=mskb[:, mask0 + kt - 4 * qt, :], op=ALU.add)
                    src = m.full()
                else:
                    src = ps.full()
                P.act.activation(out=pt[i3].full(), in_=src, func=AF.Exp, scale=scale, bias=bias_fn(kt))
                P.pe.matmul(out=oacc.full(), lhsT=V_sb[:, kt, :], rhs=pt[i3].full(), start=(kt == 0), stop=(kt == nk - 1))
            P.dve.reciprocal(out=rl[64:128, :], in_=oacc[64:128, :])
            P.dve.tensor_copy(out=rl2.full(), in_=rl[64:128, :])
            o = ot[qt % 2]
            P.dve.tensor_tensor(out=o.full(), in0=oacc[0:64, :], in1=rl2.full(), op=ALU.mult)
            P.dma("sync", out=out_dram_h[:, qt * 512:(qt + 1) * 512], in_=o.full())

    for h in range(2):
        P.dma("sync", out=K_sb[0:96, :], in_=km[h])
        P.dma("gpsimd", out=Q_sb[0:96, :], in_=qm[h])
        P.dma("sync", out=V_sb[:, :, 0:64], in_=vm[h])
        attention(96, 96.0 ** -0.5, 0, lambda kt: zero[:, 0:1], o_m[h])

    lfs = P.sb("lfs", [128, NKT], F32)
    wi = P.sb("wi", [128, NKT], F32)
    sc = [P.sb(f"sc{i}", [128, NKT], F32) for i in range(2)]
    Ff = P.sb("Ff", [128, NKT], F32)
    FT = P.sb("FT", [64, 128], F32)
    r1 = P.sb("r1", [64, 128], F32)
    fh = [P.sb(f"fh{i}", [64, 128], BF16) for i in range(3)]
    for h in range(2):
        P.dma("sync", out=lfs.full(), in_=lf[h])
        ps = pb[5]
        P.pe.matmul(out=ps[:, 0:NKT], lhsT=tri, rhs=lfs.full(), start=True, stop=True)
        P.act.copy(out=wi.full(), in_=ps[:, 0:NKT])
        ps = pb[6]
        P.pe.matmul(out=ps[:, 0:NKT], lhsT=ones, rhs=lfs.full(), start=True, stop=True)
        P.act.copy(out=sc[0].full(), in_=ps[:, 0:NKT])
        P.dve.tensor_tensor(out=wi.full(), in0=wi.full(), in1=sc[0].full(), op=ALU.subtract)
        cur = 0
        d = 1
        while d < NKT:
            nx = 1 - cur
            P.dve.tensor_copy(out=sc[nx][:, 0:d], in_=sc[cur][:, 0:d])
            P.dve.tensor_tensor(out=sc[nx][:, d:NKT], in0=sc[cur][:, d:NKT], in1=sc[cur][:, 0:NKT - d], op=ALU.add)
            cur = nx
            d *= 2
        P.dve.tensor_tensor(out=Ff.full(), in0=wi.full(), in1=sc[cur].full(), op=ALU.add)
        P.dve.tensor_scalar(out=negF.full(), in0=Ff.full(), scalar1=-1.0, scalar2=None, op0=ALU.mult)
        ps = pb[7]
        P.pe.transpose(out=ps[0:64, 0:128], in_=Ff.full(), identity=ident)
        P.act.copy(out=FT.full(), in_=ps[0:64, 0:128])
        P.dve.tensor_copy(out=fh[0].full(), in_=FT.full())
        P.dve.tensor_tensor(out=r1.full(), in0=FT.full(), in1=fh[0].full(), op=ALU.subtract)
        P.dve.tensor_copy(out=fh[1].full(), in_=r1.full())
        P.dve.tensor_tensor(out=r1.full(), in0=r1.full(), in1=fh[1].full(), op=ALU.subtract)
        P.dve.tensor_copy(out=fh[2].full(), in_=r1.full())
        for r in range(3):
            P.dma("sync", out=fsc[r].re("(kt p) -> kt p", p=128), in_=fh[r].full())
        P.dma("sync", out=K_sb[0:64, :], in_=kf[h])
        P.dve.memset(ap=K_sb[64:67, :], constant=8.0)
        P.dma("gpsimd", out=Q_sb[0:64, :], in_=qf[h])
        P.dma("gpsimd", out=Q_sb[64:67, :], in_=fsc.full())
        P.dma("sync", out=V_sb[:, :, 0:64], in_=vf[h])
        attention(67, 0.125, 4, lambda kt: negF[:, kt:kt + 1], o_f[h])

    a_sb = P.sb("a_sb", [128, NKT, 4], F32)
    dt_sb = P.sb("dt_sb", [128, NKT, 4], F32)
    Dsb = P.sb("Dsb", [128, 4], F32)
    P.dma("sync", out=a_sb.full(), in_=a_tm.full())
    P.dma("sync", out=dt_sb.full(), in_=dt_tm.full())
    P.dma("sync", out=Dsb.full(), in_=Dv.full())
    BTs = K_sb
    CTs = Q_sb
    P.dma("sync", out=BTs.full(), in_=BT.full())
    P.dma("gpsimd", out=CTs.full(), in_=CT.full())
    Acum = P.sb("Acum", [128, NKT, 4], F32)
    nAcum = P.sb("nAcum", [128, NKT, 4], F32)
    Atot = P.sb("Atot", [128, NKT, 4], F32)
    eA = P.sb("eA", [128, NKT, 4], F32)
    wdec = P.sb("wdec", [128, NKT, 4], F32)
    eAtot = P.sb("eAtot", [128, NKT, 4], F32)
    fl = lambda b: b.full().re("p c h -> p (c h)")
    ps = pb[0]
    P.pe.matmul(out=ps[:, 0:256], lhsT=tri, rhs=fl(a_sb), start=True, stop=True)
    P.act.copy(out=fl(Acum), in_=ps[:, 0:256])
    ps = pb[1]
    P.pe.matmul(out=ps[:, 0:256], lhsT=ones, rhs=fl(a_sb), start=True, stop=True)
    P.act.copy(out=fl(Atot), in_=ps[:, 0:256])
    P.dve.tensor_scalar(out=fl(nAcum), in0=fl(Acum), scalar1=-1.0, scalar2=None, op0=ALU.mult)
    P.act.activation(out=fl(eA), in_=fl(Acum), func=AF.Exp)
    P.act.activation(out=fl(eAtot), in_=fl(Atot), func=AF.Exp)
    P.dve.tensor_tensor(out=fl(wdec), in0=fl(Atot), in1=fl(Acum), op=ALU.subtract)
    P.act.activation(out=fl(wdec), in_=fl(wdec), func=AF.Exp)

    Hs = P.sb("Hs", [128, 256], F32)
    Hb = P.sb("Hb", [128, 256], BF16)
    P.dve.memset(ap=Hs.full(), constant=0.0)
    P.dve.memset(ap=Hb.full(), constant=0.0)
    xc = [P.sb(f"xc{i}", [128, 256], BF16) for i in range(2)]
    Bc = [P.sb(f"Bc{i}", [128, 128], BF16) for i in range(2)]
    cb = P.sb("cb", [128, 128], F32)
    xdt = P.sb("xdt", [128, 256], BF16)
    xdts = P.sb("xdts", [128, 256], BF16)
    at = [P.sb(f"at{i}", [128, 128], F32) for i in range(2)]
    tm = [P.sb(f"tm{i}", [128, 128], F32) for i in range(2)]
    dec = [P.sb(f"dec{i}", [128, 128], F32) for i in range(2)]
    MT = [P.sb(f"MT{i}", [128, 128], BF16) for i in range(2)]
    t1 = P.sb("t1", [128, 256], F32)
    t3 = P.sb("t3", [128, 256], F32)
    yo = [P.sb(f"yo{i}", [128, 256], F32) for i in range(2)]
    v3 = lambda v: v.re("p (h d) -> p h d", h=4)
    bc3 = lambda v: v.f(lambda a: a.unsqueeze(2).to_broadcast([128, 4, 64]))
    for c in range(NKT):
        x_c = xc[c % 2]
        B_c = Bc[c % 2]
        P.dma("sync", out=x_c.full(), in_=x_tm[:, c, :])
        P.dma("gpsimd", out=B_c.full(), in_=B_tm[:, c, :])
        BT_c = BTs[:, c * 128:(c + 1) * 128]
        CT_c = CTs[:, c * 128:(c + 1) * 128]
        ps_cb = pb[0]
        P.pe.matmul(out=ps_cb[:, 0:128], lhsT=BT_c, rhs=CT_c, start=True, stop=True)
        P.act.copy(out=cb.full(), in_=ps_cb[:, 0:128])
        P.dve.tensor_tensor(out=v3(xdt.full()), in0=v3(x_c.full()), in1=bc3(dt_sb[:, c, :]), op=ALU.mult)
        P.pool.tensor_tensor(out=v3(xdts.full()), in0=v3(xdt.full()), in1=bc3(wdec[:, c, :]), op=ALU.mult)
        ps_off = pb[1]
        P.pe.matmul(out=ps_off[:, 0:256], lhsT=CT_c, rhs=Hb.full(), start=True, stop=True)
        ps_y = pb[2]
        for h in range(4):
            i2 = h % 2
            P.dve.tensor_scalar(out=at[i2].full(), in0=tri, scalar1=a_sb[:, c, h:h + 1], scalar2=None, op0=ALU.mult)
            ps_A = pb[3 + i2]
            P.pe.matmul(out=ps_A[:, 0:128], lhsT=ones, rhs=at[i2].full(), start=True, stop=True)
            P.dve.tensor_tensor(out=tm[i2].full(), in0=ps_A[:, 0:128], in1=trimask, op=ALU.add)
            P.act.activation(out=dec[i2].full(), in_=tm[i2].full(), func=AF.Exp, bias=nAcum[:, c, h:h + 1], scale=1.0)
            P.pool.tensor_tensor(out=MT[i2].full(), in0=cb.full(), in1=dec[i2].full(), op=ALU.mult)
            P.pe.matmul(out=ps_y[:, h * 64:(h + 1) * 64], lhsT=MT[i2].full(), rhs=xdt[:, h * 64:(h + 1) * 64],
                        start=True, stop=True)
        P.dve.tensor_tensor(out=v3(t1.full()), in0=v3(ps_off[:, 0:256]), in1=bc3(eA[:, c, :]), op=ALU.mult)
        P.dve.tensor_tensor(out=t1.full(), in0=t1.full(), in1=ps_y[:, 0:256], op=ALU.add)
        P.pool.tensor_tensor(out=v3(t3.full()), in0=v3(x_c.full()), in1=bc3(Dsb.full()), op=ALU.mult)
        y_ = yo[c % 2]
        P.pool.tensor_tensor(out=y_.full(), in0=t1.full(), in1=t3.full(), op=ALU.add)
        P.dma("sync", out=o_y[:, c, :], in_=y_.full())
        ps_h = pb[5]
        P.pe.matmul(out=ps_h[:, 0:256], lhsT=B_c.full(), rhs=xdts.full(), start=True, stop=True)
        P.dve.tensor_tensor(out=v3(Hs.full()), in0=v3(Hs.full()), in1=bc3(eAtot[:, c, :]), op=ALU.mult)
        P.dve.tensor_tensor(out=Hs.full(), in0=Hs.full(), in1=ps_h[:, 0:256], op=ALU.add)
        P.act.copy(out=Hb.full(), in_=Hs.full())
    P.emit()
    return nc, P


def _bc_consts():
    msk = np.zeros((128, 8, 512), np.float32)
    p = np.arange(128)[:, None]
    q = np.arange(512)[None, :]
    for j in range(4):
        key = j * 128 + p
        msk[:, j, :] = np.where((key // 64) > (q // 64), NEG, 0.0)
        msk[:, 4 + j, :] = np.where(key > q, NEG, 0.0)
    cm = np.zeros((128, 4, 128), np.float32)
    jj = np.arange(128)[:, None]
    ii = np.arange(128)[None, :]
    cm[:, 0, :] = (jj <= ii).astype(np.float32)
    cm[:, 1, :] = np.where(jj > ii, NEG, 0.0)
    cm[:, 2, :] = np.eye(128, dtype=np.float32)
    cm[:, 3, :] = 1.0
    return msk, cm


def _tm(a):
    S, n = a.shape
    return np.ascontiguousarray(a.reshape(S // 128, 128, n).transpose(1, 0, 2))


def run_BC(inp, l, resA):
    if "BC" not in _PROG_CACHE:
        _PROG_CACHE["BC"] = build_BC()[0]
    nc = _PROG_CACHE["BC"]
    msk, cm = _bc_consts()

    def gather(name, b):
        return np.concatenate([np.asarray(resA[b * 4 + i][name]) for i in range(4)], axis=-1)

    in_maps = []
    for c in range(8):
        b, hg = c // 4, c % 4
        qm = gather("o_qm", b)[2 * hg:2 * hg + 2]
        km = gather("o_km", b)[2 * hg:2 * hg + 2]
        vmf = gather("o_vm", b)
        qf = gather("o_qf", b)[2 * hg:2 * hg + 2]
        kf = gather("o_kf", b)[2 * hg:2 * hg + 2]
        vff = gather("o_vf", b)
        lff = gather("o_lf", b)
        xbc = gather("o_xbc", b)
        dtf = gather("o_dt", b)
        af = gather("o_a", b)
        g = hg // 2
        vm = np.stack([_tm(vmf[(2 * hg + h) * 64:(2 * hg + h + 1) * 64].T) for h in range(2)])
        vf = np.stack([_tm(vff[(2 * hg + h) * 64:(2 * hg + h + 1) * 64].T) for h in range(2)])
        lf = np.stack([np.ascontiguousarray(lff[2 * hg + h].reshape(NKT, 128).T) for h in range(2)])
        x_tm = _tm(xbc[hg * 256:(hg + 1) * 256].T)
        Bf = xbc[1024 + g * 128:1024 + (g + 1) * 128]
        Cf = xbc[1280 + g * 128:1280 + (g + 1) * 128]
        Dv = np.broadcast_to(inp["ssm_D"][l][4 * hg:4 * hg + 4][None, :], (128, 4)).astype(np.float32)
        in_maps.append({
            "qm": np.ascontiguousarray(qm), "km": np.ascontiguousarray(km), "vm": vm,
            "qf": np.ascontiguousarray(qf), "kf": np.ascontiguousarray(kf), "vf": vf, "lf": lf,
            "msk": msk, "cm": cm, "x_tm": x_tm, "B_tm": _tm(Bf.T), "BT": np.ascontiguousarray(Bf),
            "CT": np.ascontiguousarray(Cf), "dt_tm": _tm(dtf[4 * hg:4 * hg + 4].T),
            "a_tm": _tm(af[4 * hg:4 * hg + 4].T), "Dv": np.ascontiguousarray(Dv),
        })
    res = run_bass_kernel_spmd(nc, in_maps, core_ids=list(range(8))).results
    om = np.zeros((2, 512, S_), NBF)
    of = np.zeros((2, 512, S_), NBF)
    y = np.zeros((2, 1024, S_), np.float32)
    for c in range(8):
        b, hg = c // 4, c % 4
        om[b, hg * 128:(hg + 1) * 128] = np.asarray(res[c]["o_m"]).reshape(128, S_)
        of[b, hg * 128:(hg + 1) * 128] = np.asarray(res[c]["o_f"]).reshape(128, S_)
        yy = np.asarray(res[c]["o_y"])
        y[b, hg * 256:(hg + 1) * 256] = yy.transpose(2, 1, 0).reshape(256, S_)
    return om, of, y


def build_D1():
    nc = new_nc()
    P = Prog(nc)
    EI, EO = "ExternalInput", "ExternalOutput"
    NT = T // 512
    omT = P.dram("omT", [512, T], BF16, EI)
    ofT = P.dram("ofT", [512, T], BF16, EI)
    yT = P.dram("yT", [1024, T], F32, EI)
    szT = P.dram("szT", [1024, T], BF16, EI)
    gT = P.dram("gT", [3072, T], BF16, EI)
    xT = P.dram("xT", [D, T], F32, EI)
    w_a = P.dram("w_a", [512, D], F32, EI)
    w_b = P.dram("w_b", [512, D], F32, EI)
    w_c = P.dram("w_c", [1024, D], F32, EI)
    w_o = P.dram("w_o", [1024, D], F32, EI)
    cst_d = P.dram("cst", [128, 8], F32, EI)
    o_x = P.dram("o_x", [D, T], F32, EO)

    cstb = P.sb("cstb", [128, 8], F32)
    P.dma("sync", out=cstb.full(), in_=cst_d.full())
    ones = P.sb("ones", [128, 128], F32)
    P.dve.memset(ap=ones.full(), constant=1.0)
    eps = P.sb("eps", [128, 1], F32)
    P.dve.memset(ap=eps.full(), constant=1e-6)
    pb = [P.ps(f"pb{i}", [128, 512], F32) for i in range(8)]
    wst = [P.sb(f"wst{i}", [128, 4, 1024], F32) for i in range(2)]
    wcnt = [0]

    def load_w(dram, kc_n, name):
        bfb = P.sb(name, [128, kc_n, D], BF16)
        v = dram.full().re("(kc p) n -> p kc n", p=128)
        for k0 in range(0, kc_n, 4):
            i = wcnt[0] % 2
            wcnt[0] += 1
            P.dma("sync" if i == 0 else "gpsimd", out=wst[i].full(), in_=v[:, k0:k0 + 4, :])
            P.pool.tensor_copy(out=bfb[:, k0:k0 + 4, :], in_=wst[i].full())
        return bfb

    Wa = load_w(w_a, 4, "Wa")
    Wb = load_w(w_b, 4, "Wb")
    Wc = load_w(w_c, 8, "Wc")
    Wo = load_w(w_o, 8, "Wo")

    ys = P.sb("ys", [128, 8, 512], F32)
    szs = P.sb("szs", [128, 8, 512], BF16)
    yn = P.sb("yn", [128, 8, 512], BF16)
    oms = P.sb("oms", [128, 4, 512], BF16)
    ofs = P.sb("ofs", [128, 4, 512], BF16)
    gs = P.sb("gs", [128, 24, 512], BF16)
    xs = P.sb("xs", [128, 8, 512], F32)
    sq = P.sb("sq", [128, 512], F32)
    rstd = P.sb("rstd", [128, 512], F32)
    m1 = [P.sb(f"m1_{i}", [128, 512], F32) for i in range(2)]
    m2 = [P.sb(f"m2_{i}", [128, 512], F32) for i in range(2)]
    m3 = [P.sb(f"m3_{i}", [128, 512], F32) for i in range(2)]
    mg = P.sb("mg", [128, 8, 512], BF16)
    xo = [P.sb(f"xo{i}", [128, 512], F32) for i in range(2)]
    ch = lambda d: d.full().re("(kc p) n -> p kc n", p=128)
    for ti in range(NT):
        ts = slice(ti * 512, (ti + 1) * 512)
        P.dma("sync", out=ys.full(), in_=ch(yT)[:, :, ts])
        P.dma("gpsimd", out=szs.full(), in_=ch(szT)[:, :, ts])
        P.dma("sync", out=oms.full(), in_=ch(omT)[:, :, ts])
        P.dma("gpsimd", out=ofs.full(), in_=ch(ofT)[:, :, ts])
        P.dma("sync", out=gs.full(), in_=ch(gT)[:, :, ts])
        P.dma("gpsimd", out=xs.full(), in_=ch(xT)[:, :, ts])
        ps = pb[7]
        for kc in range(8):
            P.dve.tensor_tensor(out=ys[:, kc, :], in0=ys[:, kc, :], in1=szs[:, kc, :], op=ALU.mult)
            P.act.activation(out=sq.full(), in_=ys[:, kc, :], func=AF.Square)
            P.pe.matmul(out=ps.full(), lhsT=ones.full(), rhs=sq.full(), start=(kc == 0), stop=(kc == 7))
        P.act.activation(out=rstd.full(), in_=ps.full(), func=AF.Sqrt, bias=eps[:, 0:1], scale=1.0 / 1024.0)
        P.dve.reciprocal(out=rstd.full(), in_=rstd.full())
        for kc in range(8):
            P.dve.scalar_tensor_tensor(out=yn[:, kc, :], in0=ys[:, kc, :], scalar=cstb[:, kc:kc + 1], in1=rstd.full(),
                                       op0=ALU.mult, op1=ALU.mult)
        for oc in range(8):
            i2 = oc % 2
            osl = slice(oc * 128, (oc + 1) * 128)
            pa, pbb, pc = pb[0 + i2 * 3], pb[1 + i2 * 3], pb[2 + i2 * 3]
            for kc in range(4):
                P.pe.matmul(out=pa.full(), lhsT=Wa[:, kc, osl], rhs=oms[:, kc, :], start=(kc == 0), stop=(kc == 3))
            for kc in range(4):
                P.pe.matmul(out=pbb.full(), lhsT=Wb[:, kc, osl], rhs=ofs[:, kc, :], start=(kc == 0), stop=(kc == 3))
            for kc in range(8):
                P.pe.matmul(out=pc.full(), lhsT=Wc[:, kc, osl], rhs=yn[:, kc, :], start=(kc == 0), stop=(kc == 7))
            P.dve.tensor_tensor(out=m1[i2].full(), in0=pa.full(), in1=gs[:, oc, :], op=ALU.mult)
            P.dve.tensor_tensor(out=m2[i2].full(), in0=pbb.full(), in1=gs[:, 8 + oc, :], op=ALU.mult)
            P.dve.tensor_tensor(out=m3[i2].full(), in0=pc.full(), in1=gs[:, 16 + oc, :], op=ALU.mult)
            P.pool.tensor_tensor(out=m1[i2].full(), in0=m1[i2].full(), in1=m2[i2].full(), op=ALU.add)
            P.pool.tensor_tensor(out=mg[:, oc, :], in0=m1[i2].full(), in1=m3[i2].full(), op=ALU.add)
        for oc in range(8):
            i2 = oc % 2
            ps = pb[6 + i2]
            for kc in range(8):
                P.pe.matmul(out=ps.full(), lhsT=Wo[:, kc, oc * 128:(oc + 1) * 128], rhs=mg[:, kc, :],
                            start=(kc == 0), stop=(kc == 7))
            P.dve.tensor_tensor(out=xo[i2].full(), in0=ps.full(), in1=xs[:, oc, :], op=ALU.add)
            P.dma("sync" if i2 else "gpsimd", out=o_x[oc * 128:(oc + 1) * 128, ts], in_=xo[i2].full())
    P.emit()
    return nc, P


def d2_colpack(inp, l):
    cp = ColPack()
    cp.add("g_ffn", inp["norm_ffn_g"][l])
    cw = inp["ffn_conv_w"][l]
    for k in range(3):
        cp.add(f"fw{k}", cw[k])
    cp.add("fb", inp["ffn_conv_b"][l])
    return cp


def build_D2(off):
    nc = new_nc()
    P = Prog(nc)
    EI, EO = "ExternalInput", "ExternalOutput"
    NT = T // 512
    TT = T + HALO
    xT = P.dram("xT", [D, TT], F32, EI)
    w_up = P.dram("w_up", [D, 5632], F32, EI)
    w_dn = P.dram("w_dn", [2816, D], F32, EI)
    cst_d = P.dram("cst", [128, off["_n"]], F32, EI)
    o_x = P.dram("o_x", [D, T], F32, EO)

    cstb = P.sb("cstb", [128, off["_n"]], F32)
    C = Cst(P, cstb, off)
    P.dma("sync", out=cstb.full(), in_=cst_d.full())
    ones = P.sb("ones", [128, 128], F32)
    P.dve.memset(ap=ones.full(), constant=1.0)
    eps = P.sb("eps", [128, 1], F32)
    P.dve.memset(ap=eps.full(), constant=1e-6)
    pb = [P.ps(f"pb{i}", [128, 512], F32) for i in range(8)]
    wst = [P.sb(f"wst{i}", [128, 1024], F32) for i in range(2)]
    wcnt = [0]
    Wu = P.sb("Wu", [128, 8, 5632], BF16)
    Wd = P.sb("Wd", [128, 22, D], BF16)
    wuv = w_up.full().re("(kc p) n -> p kc n", p=128)
    for kc in range(8):
        for c0 in range(0, 5632, 1024):
            n = min(1024, 5632 - c0)
            i = wcnt[0] % 2
            wcnt[0] += 1
            P.dma("sync" if i == 0 else "gpsimd", out=wst[i][:, 0:n], in_=wuv[:, kc, c0:c0 + n])
            P.pool.tensor_copy(out=Wu[:, kc, c0:c0 + n], in_=wst[i][:, 0:n])
    wdv = w_dn.full().re("(kc p) n -> p kc n", p=128)
    for k0 in range(22):
        i = wcnt[0] % 2
        wcnt[0] += 1
        P.dma("sync" if i == 0 else "gpsimd", out=wst[i].full(), in_=wdv[:, k0, :])
        P.pool.tensor_copy(out=Wd[:, k0, :], in_=wst[i].full())

    xst = P.sb("xst", [128, 8, 512], F32)
    hn = P.sb("hn", [128, 8, 512], BF16)
    sq = P.sb("sq", [128, 512], F32)
    rstd = P.sb("rstd", [128, 512], F32)
    act = P.sb("act", [128, 22, 512], BF16)
    upre = [P.sb(f"upre{i}", [128, 516], F32) for i in range(2)]
    acc = [P.sb(f"acc{i}", [128, 512], F32) for i in range(2)]
    sg = P.sb("sg", [128, 512], F32)
    carry = P.sb("carry", [128, 44, 4], F32)
    xo = [P.sb(f"xo{i}", [128, 512], F32) for i in range(2)]
    xTv = xT.full().re("(kc p) n -> p kc n", p=128)
    tiles = [(0, HALO)] + [(HALO + i * 512, 512) for i in range(NT)]
    pcnt = [0]
    for tix, (c0, w) in enumerate(tiles):
        P.dma("sync", out=xst[:, :, 0:w], in_=xTv[:, :, c0:c0 + w])
        ps = pb[7]
        for kc in range(8):
            P.act.activation(out=sq[:, 0:w], in_=xst[:, kc, 0:w], func=AF.Square)
            P.pe.matmul(out=ps[:, 0:w], lhsT=ones.full(), rhs=sq[:, 0:w], start=(kc == 0), stop=(kc == 7))
        P.act.activation(out=rstd[:, 0:w], in_=ps[:, 0:w], func=AF.Sqrt, bias=eps[:, 0:1], scale=1.0 / 1024.0)
        P.dve.reciprocal(out=rstd[:, 0:w], in_=rstd[:, 0:w])
        for kc in range(8):
            P.dve.scalar_tensor_tensor(out=hn[:, kc, 0:w], in0=xst[:, kc, 0:w], scalar=C.col("g_ffn", kc),
                                       in1=rstd[:, 0:w], op0=ALU.mult, op1=ALU.mult)
        for i in range(22):
            accs = []
            for j, cg in enumerate((i, 22 + i)):
                ps = pb[pcnt[0] % 4]
                pcnt[0] += 1
                for kc in range(8):
                    P.pe.matmul(out=ps[:, 0:w], lhsT=Wu[:, kc, cg * 128:(cg + 1) * 128], rhs=hn[:, kc, 0:w],
                                start=(kc == 0), stop=(kc == 7))
                if tix == 0:
                    P.act.copy(out=carry[:, cg, :], in_=ps[:, 0:HALO])
                    continue
                up = upre[j]
                P.act.copy(out=up[:, 4:516], in_=ps.full())
                P.dve.tensor_copy(out=up[:, 0:4], in_=carry[:, cg, :])
                P.pool.tensor_copy(out=carry[:, cg, :], in_=up[:, 512:516])
                a0 = acc[j]
                P.dve.tensor_scalar(out=a0.full(), in0=up[:, 4:516], scalar1=C.col("fw2", cg), scalar2=C.col("fb", cg),
                                    op0=ALU.mult, op1=ALU.add)
                P.dve.scalar_tensor_tensor(out=a0.full(), in0=up[:, 3:515], scalar=C.col("fw1", cg), in1=a0.full(),
                                           op0=ALU.mult, op1=ALU.add)
                P.dve.scalar_tensor_tensor(out=a0.full(), in0=up[:, 2:514], scalar=C.col("fw0", cg), in1=a0.full(),
                                           op0=ALU.mult, op1=ALU.add)
                accs.append(a0)
            if tix == 0:
                continue
            P.act.activation(out=sg.full(), in_=accs[0].full(), func=AF.Silu)
            P.pool.tensor_tensor(out=act[:, i, :], in0=sg.full(), in1=accs[1].full(), op=ALU.mult)
        if tix == 0:
            continue
        ti = tix - 1
        for oc in range(8):
            i2 = oc % 2
            ps = pb[4 + i2]
            for i in range(22):
                P.pe.matmul(out=ps.full(), lhsT=Wd[:, i, oc * 128:(oc + 1) * 128], rhs=act[:, i, :],
                            start=(i == 0), stop=(i == 21))
            P.dve.tensor_tensor(out=xo[i2].full(), in0=ps.full(), in1=xst[:, oc, :], op=ALU.add)
            P.dma("sync" if i2 else "gpsimd", out=o_x[oc * 128:(oc + 1) * 128, ti * 512:(ti + 1) * 512], in_=xo[i2].full())
    P.emit()
    return nc, P


def run_D1(inp, l, resA, om, of, y, x_full):
    if "D1" not in _PROG_CACHE:
        _PROG_CACHE["D1"] = build_D1()[0]
    nc = _PROG_CACHE["D1"]
    cst = np.ascontiguousarray(inp["ssm_norm_g"][l].reshape(8, 128).T)
    xf = x_full.reshape(16384, D)
    in_maps = []
    for c in range(8):
        b, q = c // 4, c % 4
        ts = slice(q * T, (q + 1) * T)
        in_maps.append({
            "omT": np.ascontiguousarray(om[b][:, ts]), "ofT": np.ascontiguousarray(of[b][:, ts]),
            "yT": np.ascontiguousarray(y[b][:, ts]), "szT": np.asarray(resA[c]["o_sz"]),
            "gT": np.asarray(resA[c]["o_g"]), "xT": np.ascontiguousarray(xf[c * T:(c + 1) * T].T),
            "w_a": np.ascontiguousarray(inp["w_br_mla"][l]), "w_b": np.ascontiguousarray(inp["w_br_fox"][l]),
            "w_c": np.ascontiguousarray(inp["w_br_ssm"][l]), "w_o": np.ascontiguousarray(inp["w_out"][l]),
            "cst": cst,
        })
    res = run_bass_kernel_spmd(nc, in_maps, core_ids=list(range(8))).results
    xm = np.concatenate([np.asarray(r["o_x"]).T for r in res], axis=0)
    return xm.reshape(2, S_, D)


def run_D2(inp, l, xm_full):
    cp = d2_colpack(inp, l)
    off = dict(cp.off)
    off["_n"] = cp.n
    if "D2" not in _PROG_CACHE:
        _PROG_CACHE["D2"] = build_D2(off)[0]
    nc = _PROG_CACHE["D2"]
    cst = cp.array()
    xf = xm_full.reshape(16384, D)
    in_maps = []
    for c in range(8):
        t0 = c * T
        xt = np.zeros((D, T + HALO), np.float32)
        xt[:, HALO:] = xf[t0:t0 + T].T
        if c % 4 != 0:
            xt[:, 0:HALO] = xf[t0 - HALO:t0].T
        in_maps.append({"xT": np.ascontiguousarray(xt), "w_up": np.ascontiguousarray(inp["ffn_w_up"][l]),
                        "w_dn": np.ascontiguousarray(inp["ffn_w_down"][l]), "cst": cst})
    res = run_bass_kernel_spmd(nc, in_maps, core_ids=list(range(8))).results
    xo = np.concatenate([np.asarray(r["o_x"]).T for r in res], axis=0)
    return xo.reshape(2, S_, D)


def kernel_unfused(**inp):
    inp = {k: np.asarray(v) for k, v in inp.items()}
    x = inp["x"].astype(np.float32)
    pos = inp["positions"]
    for l in range(2):
        resA = run_A(inp, l, x, pos)
        om, of, y = run_BC(inp, l, resA)
        xm = run_D1(inp, l, resA, om, of, y, x)
        x = run_D2(inp, l, xm)
    return np.ascontiguousarray(x.astype(np.float32))


def kernel(**inp):
    return kernel_fused(**inp)


SW = 516
RG = [[0, 1, 2, 3], [4, 5, 6, 7]]
KT_L = T // 128


def fused_rowpack(inp, l):
    r = np.concatenate([inp["fox_b_f"][l], inp["ssm_dt_bias"][l], inp["ssm_A_log"][l], inp["ssm_D"][l]]).astype(np.float32)
    return np.ascontiguousarray(np.broadcast_to(r[None, :], (128, r.size)))


def build_fused(offA, offD2, stop=None, dbg=()):
    nc = new_nc()
    P = Prog(nc)
    EI, EO = "ExternalInput", "ExternalOutput"
    L = 2
    x0 = P.dram("x0", [D, 4 * SW], F32, EI)
    pos = P.dram("pos", [1, T], I32, EI)
    w_in = P.dram("w_in", [L, D, 7864], F32, EI)
    w_uq = P.dram("w_uq", [L, 384, 768], F32, EI)
    w_kp = P.dram("w_kp", [L, 256, 768], F32, EI)
    w_v = P.dram("w_v", [L, 256, 512], F32, EI)
    w_a = P.dram("w_a", [L, 512, D], F32, EI)
    w_b = P.dram("w_b", [L, 512, D], F32, EI)
    w_c = P.dram("w_c", [L, 1024, D], F32, EI)
    w_o = P.dram("w_o", [L, 1024, D], F32, EI)
    w_up = P.dram("w_up", [L, D, 5632], F32, EI)
    w_dn = P.dram("w_dn", [L, 2816, D], F32, EI)
    cstA_d = P.dram("cstA", [L, 128, offA["_n"]], F32, EI)
    cstD_d = P.dram("cstD", [L, 128, offD2["_n"]], F32, EI)
    gssm_d = P.dram("gssm", [L, 128, 8], F32, EI)
    rowc_d = P.dram("rowc", [L, 128, 56], F32, EI)
    sel_d = P.dram("sel", [128, 32], F32, EI)
    msk_d = P.dram("msk", [128, 8, 512], F32, EI)
    cm_d = P.dram("cm", [128, 4, 128], F32, EI)
    mats_d = P.dram("mats", [128, 192], F32, EI)
    out = P.dram("out", [D, T], F32, EO)
    xb = [x0, P.dram("xb1", [D, 4 * SW], F32)]
    xmid = P.dram("xmid", [D, 4 * SW], F32)
    qm = P.dram("qm", [8, 96, T], BF16)
    qf = P.dram("qf", [8, 64, T], BF16)
    fq = P.dram("fq", [8, 3, T], BF16)
    szd = P.dram("szd", [1024, T], BF16)
    gd = P.dram("gd", [3072, T], BF16)
    xtm = P.dram("xtm", [128, KT_L, 1024], BF16)
    btm = P.dram("btm", [128, KT_L, 256], BF16)
    bct = P.dram("bct", [512, T], BF16)
    dtd = P.dram("dtd", [128, KT_L, 16], F32)
    atd = P.dram("atd", [128, KT_L, 16], F32)
    omd = P.dram("omd", [512, T], BF16)
    ofd = P.dram("ofd", [512, T], BF16)
    yd = P.dram("yd", [1024, T], F32)
    kxm = [P.dram(f"kxm{m}", [768, 512], BF16) for m in range(4)]
    kxmg = [P.dram(f"kxmg{m}", [4 * 768, 512], BF16) for m in range(4)]
    kxf = [P.dram(f"kxf{m}", [640, 512], BF16) for m in range(4)]
    kxfg = [P.dram(f"kxfg{m}", [4 * 640, 512], BF16) for m in range(4)]
    nfq = P.dram("nfq", [8, 3, T], BF16)
    vx = [P.dram(f"vx{m}", [2048, 256], BF16) for m in range(4)]
    vxg = [P.dram(f"vxg{m}", [4 * 2048, 256], BF16) for m in range(4)]
    sx = [P.dram(f"sx{i}", [256, 1024], F32) for i in range(2)]
    sxg = [P.dram(f"sxg{i}", [4 * 256, 1024], F32) for i in range(2)]
    fx = P.dram("fx", [128, 224], F32)
    fxg = P.dram("fxg", [4 * 128, 224], F32)
    tx = P.dram("tx", [128, 128], F32)
    txg = P.dram("txg", [4 * 128, 128], F32)
    wub = P.dram("wub", [D, 5632], BF16)
    wdb = P.dram("wdb", [2816, D], BF16)
    wab = P.dram("wab", [512, D], BF16)
    wbb = P.dram("wbb", [512, D], BF16)
    wcb = P.dram("wcb", [1024, D], BF16)
    wob = P.dram("wob", [1024, D], BF16)
    dbg_out = {}

    def gather_pairs(pairs, after=None):
        for (a, b) in pairs:
            kw = {}
            if after is not None:
                kw["_reads"] = [after]
            P.pool.collective_compute(kind="AllGather", op=ALU.bypass, replica_groups=RG,
                                      ins=[a.full().re("(p a) c -> p (a c)", p=128)],
                                      outs=[b.full().re("(q a) c -> q (a c)", q=512)], **kw)

    def load_consts():
        d = {}
        d["cm"] = P.sb("cmb", [128, 4, 128], F32)
        P.dma("sync", out=d["cm"].full(), in_=cm_d.full())
        d["sel"] = P.sb("selb", [128, 32], F32)
        P.dma("sync", out=d["sel"].full(), in_=sel_d.full())
        d["eps"] = P.sb("eps", [128, 1], F32)
        P.dve.memset(ap=d["eps"].full(), constant=1e-6)
        d["one1"] = P.sb("one1", [128, 1], F32)
        P.dve.memset(ap=d["one1"].full(), constant=1.0)
        d["zero"] = P.sb("zero", [128, 1], F32)
        P.dve.memset(ap=d["zero"].full(), constant=0.0)
        return d

    def phase_A(l):
        K = load_consts()
        cmb = K["cm"]
        tri, ident, ones = cmb[:, 0, :], cmb[:, 2, :], cmb[:, 3, :]
        eps, one1 = K["eps"], K["one1"]
        xin = xb[l]
        cstb = P.sb("cstb", [128, offA["_n"]], F32)
        C = Cst(P, cstb, offA)
        P.dma("sync", out=cstb.full(), in_=cstA_d[l])
        rowc = P.sb("rowc", [128, 56], F32)
        P.dma("sync", out=rowc.full(), in_=rowc_d[l])
        matf = P.sb("matf", [128, 192], F32)
        matb = P.sb("matb", [128, 192], BF16)
        P.dma("sync", out=matf.full(), in_=mats_d.full())
        P.dve.tensor_copy(out=matb.full(), in_=matf.full())
        prh = matb[0:96, 0:96]
        selm = matb[0:32, 96:192]
        identb = P.sb("identb", [128, 128], BF16)
        P.dve.tensor_copy(out=identb.full(), in_=ident)
        Aneg_r = P.sb("Aneg_r", [128, 16], F32)
        P.act.activation(out=Aneg_r.full(), in_=rowc[:, 24:40], func=AF.Exp)
        P.dve.tensor_scalar(out=Aneg_r.full(), in0=Aneg_r.full(), scalar1=-1.0, scalar2=None, op0=ALU.mult)

        pb = [P.ps(f"pb{i}", [128, 512], F32) for i in range(7)]
        pbt = P.ps("pbt", [128, 1024], BF16)
        pbi = {}

        def nxt_ps(lo=0, hi=4):
            i = pbi.get(lo, 0)
            pbi[lo] = (i + 1) % (hi - lo)
            return pb[lo + i]

        Ctab = P.sb("Ctab", [96, T], F32)
        Stab = P.sb("Stab", [96, T], F32)
        hraw = P.sb("hraw", [96, 512], F32)
        hsq = P.sb("hsq", [96, 512], F32)
        hrs = P.sb("hrs", [96, 512], F32)
        hnf = P.sb("hnf", [96, 512], F32)
        hnb = P.sb("hnb", [96, 512], BF16)
        ht1 = P.sb("ht1", [96, 512], F32)
        ht2 = P.sb("ht2", [96, 512], F32)
        posf, rr_tmp, rr_m = hrs, hraw, hsq

        class _IV:
            def __init__(self, b):
                self.b = b

            def full(self):
                return self.b.full().bitcast(I32)
        posi, rr_i = _IV(ht1), _IV(ht2)

        def sin_table(outv, phase):
            P.dve.tensor_scalar(out=rr_tmp.full(), in0=posf.full(), scalar1=C.col("invf"), scalar2=phase,
                                op0=ALU.mult, op1=ALU.add)
            P.dve.tensor_scalar(out=rr_m.full(), in0=rr_tmp.full(), scalar1=1.0 / (2 * np.pi), scalar2=None, op0=ALU.mult)
            P.dve.tensor_copy(out=rr_i.full(), in_=rr_m.full())
            P.dve.tensor_copy(out=rr_m.full(), in_=rr_i.full())
            P.dve.scalar_tensor_tensor(out=rr_tmp.full(), in0=rr_m.full(), scalar=-2 * np.pi, in1=rr_tmp.full(),
                                       op0=ALU.mult, op1=ALU.add)
            P.dve.tensor_scalar(out=rr_m.full(), in0=rr_tmp.full(), scalar1=np.pi, scalar2=-2 * np.pi, op0=ALU.is_gt, op1=ALU.mult)
            P.dve.tensor_tensor(out=rr_tmp.full(), in0=rr_tmp.full(), in1=rr_m.full(), op=ALU.add)
            P.dve.tensor_scalar(out=rr_m.full(), in0=rr_tmp.full(), scalar1=-np.pi, scalar2=2 * np.pi, op0=ALU.is_lt, op1=ALU.mult)
            P.dve.tensor_tensor(out=rr_tmp.full(), in0=rr_tmp.full(), in1=rr_m.full(), op=ALU.add)
            P.act.activation(out=outv, in_=rr_tmp.full(), func=AF.Sin)

        for i in range(4):
            P.dma("sync", out=posi.full(), in_=pos[:, i * 512:(i + 1) * 512].f(lambda a: a.partition_broadcast(96)))
            P.dve.tensor_copy(out=posf.full(), in_=posi.full())
            sin_table(Stab[:, i * 512:(i + 1) * 512], 0.0)
            sin_table(Ctab[:, i * 512:(i + 1) * 512], np.pi / 2)
        P.dve.memset(ap=Stab[0:64, :], constant=0.0)
        P.dve.memset(ap=Ctab[0:64, :], constant=1.0)

        hn = P.sb("hn", [128, 8, 4 * SW], BF16)
        xst = P.sb("xst", [128, 8, 512], F32)
        sq = P.sb("sq", [128, 512], F32)
        rstd = P.sb("rstd", [128, 512], F32)
        xTv = xin.full().re("(kc p) n -> p kc n", p=128)

        def rstd_from(ps_view, n_feat, rows, rstd_view):
            P.act.activation(out=rstd_view, in_=ps_view, func=AF.Ln, bias=eps[0:rows, 0:1], scale=1.0 / n_feat)
            P.act.activation(out=rstd_view, in_=rstd_view, func=AF.Exp, scale=-0.5)

        halos = [(m * SW, 4) for m in range(4)]
        main = [(m * SW + 4, 512) for m in range(4)]
        for (c0, w) in halos + main:
            P.dma("sync", out=xst[:, :, 0:w], in_=xTv[:, :, c0:c0 + w])
            ps = nxt_ps(4, 6)
            for kc in range(8):
                P.act.activation(out=sq[:, 0:w], in_=xst[:, kc, 0:w], func=AF.Square)
                P.pe.matmul(out=ps[:, 0:w], lhsT=ones, rhs=sq[:, 0:w], start=(kc == 0), stop=(kc == 7))
            rstd_from(ps[:, 0:w], 1024.0, 128, rstd[:, 0:w])
            for kc in range(8):
                P.dve.scalar_tensor_tensor(out=hn[:, kc, c0:c0 + w], in0=xst[:, kc, 0:w], scalar=C.col("g_mix", kc),
                                           in1=rstd[:, 0:w], op0=ALU.mult, op1=ALU.mult)

        wst = [P.sb(f"wst{i}", [128, 8, 256], F32) for i in range(2)]
        wbf = [P.sb(f"wbf{i}", [128, 8, 512], BF16) for i in range(2)]
        wcnt = [0]
        scnt = [0]
        w_inv = w_in[l].re("(kc p) n -> p kc n", p=128)

        SBv = 672 + 1544
        wplan = [(0, 384), (384, 288), (672, 512), (672 + 512, 512), (672 + 1024, 512), (672 + 1536, 8),
                 (SBv + 1024 + 1536, 16), (SBv, 512), (SBv + 512, 512)]
        wplan += [(SBv + 1024 + b_ * 512, 512) for b_ in range(3)]
        wplan += [(SBv + 2576 + b_ * 512, 512) for b_ in range(6)]
        wpend = {}
        cpend = []

        def w_issue(g):
            c0, ncols = wplan[g]
            lst = []
            for h0 in range(0, ncols, 256):
                n = min(256, ncols - h0)
                si = scnt[0] % 2
                scnt[0] += 1
                P.dma("sync", out=wst[si][:, :, 0:n], in_=w_inv[:, :, c0 + h0:c0 + h0 + n])
                lst.append((si, h0, n))
            wpend[g] = lst

        def load_w(c0, ncols):
            g = wcnt[0]
            wcnt[0] += 1
            assert wplan[g] == (c0, ncols), (g, wplan[g], c0, ncols)
            i = g % 2
            if g not in wpend:
                w_issue(g)
            lst = wpend.pop(g)
            for (si, h0, n) in lst:
                P.act.copy(out=wbf[i][:, :, h0:h0 + n], in_=wst[si][:, :, 0:n])
            if g + 1 < len(wplan):
                w_issue(g + 1)
            if cpend:
                gather_pairs([cpend.pop(0)], after=wbf[i].full())
            return wbf[i]

        def proj(wb, wc0, mcols, c0, w, ps_view):
            for kc in range(8):
                P.pe.matmul(out=ps_view, lhsT=wb[:, kc, wc0:wc0 + mcols], rhs=hn[:, kc, c0:c0 + w],
                            start=(kc == 0), stop=(kc == 7))

        def proj_tm(wb, wc0, ncols, tok0, ps_view):
            for kc in range(8):
                P.pe.matmul(out=ps_view, lhsT=hn[:, kc, tok0:tok0 + 128], rhs=wb[:, kc, wc0:wc0 + ncols],
                            start=(kc == 0), stop=(kc == 7))

        ostg_cnt = [0]
        ostg = [P.sb(f"ostg{i}", [128, 512], BF16) for i in range(4)]

        def next_ostg():
            i = ostg_cnt[0] % 4
            ostg_cnt[0] += 1
            return ostg[i]

        def out_dma(dst_view, src_view):
            P.dma("sync" if ostg_cnt[0] % 2 else "scalar", out=dst_view, in_=src_view)

        hsets = [dict(hraw=hraw.full(), hsq=hsq.full(), hrs=hrs.full(), hnf=hnf.full(), hnb=hnb.full(),
                      ht1=ht1.full(), ht2=ht2.full())]
        hnb1 = P.sb("hnb1", [96, 512], BF16)
        hsets.append(dict(hraw=xst[0:96, 0, :].k(0), hsq=xst[0:96, 1, :].k(1), hrs=xst[0:96, 2, :].k(2),
                          hnf=xst[0:96, 3, :].k(3), hnb=hnb1.full(), ht1=xst[0:96, 4, :].k(4), ht2=xst[0:96, 5, :].k(5)))
        hb2 = P.sb("hb2", [96, 6, 512], F32)
        hnb2 = P.sb("hnb2", [96, 512], BF16)
        hsets.append(dict(hraw=hb2[:, 0, :].k(0), hsq=hb2[:, 1, :].k(1), hrs=hb2[:, 2, :].k(2),
                          hnf=hb2[:, 3, :].k(3), hnb=hnb2.full(), ht1=hb2[:, 4, :].k(4), ht2=hb2[:, 5, :].k(5)))
        hcnt = [0]

        def headnorm(projfn, d, gain_col, rope, tok0, dst_view):
            H = hsets[hcnt[0] % 3]
            hcnt[0] += 1
            ps_view = projfn()
            P.act.activation(out=H["hsq"][0:d, :], in_=ps_view, func=AF.Square)
            P.act.copy(out=H["hraw"][0:d, :], in_=ps_view)
            yield
            ps2 = nxt_ps(4, 6)
            P.pe.matmul(out=ps2[0:d, :], lhsT=cmb[0:d, 3, 0:d], rhs=H["hsq"][0:d, :], start=True, stop=True)
            rstd_from(ps2[0:d, :], float(d), d, H["hrs"][0:d, :])
            og = next_ostg()
            if not rope:
                P.dve.scalar_tensor_tensor(out=og[0:d, :], in0=H["hraw"][0:d, :], scalar=gain_col, in1=H["hrs"][0:d, :],
                                           op0=ALU.mult, op1=ALU.mult)
            else:
                P.dve.scalar_tensor_tensor(out=H["hnf"][0:d, :], in0=H["hraw"][0:d, :], scalar=gain_col, in1=H["hrs"][0:d, :],
                                           op0=ALU.mult, op1=ALU.mult)
                P.act.copy(out=H["hnb"][0:d, :], in_=H["hnf"][0:d, :])
                yield
                ps3 = nxt_ps(6, 7)
                P.pe.matmul(out=ps3[0:d, :], lhsT=prh, rhs=H["hnb"][0:d, :], start=True, stop=True)
                P.dve.tensor_tensor(out=H["ht1"][0:d, :], in0=H["hnf"][0:d, :], in1=Ctab[0:d, tok0:tok0 + 512], op=ALU.mult)
                P.dve.tensor_tensor(out=H["ht2"][0:d, :], in0=ps3[0:d, :], in1=Stab[0:d, tok0:tok0 + 512], op=ALU.mult)
                P.pool.tensor_tensor(out=og[0:d, :], in0=H["ht1"][0:d, :], in1=H["ht2"][0:d, :], op=ALU.add)
            out_dma(dst_view, og[0:d, :])

        def run_pipe(gens, depth=3):
            gens = iter(gens)
            active = []
            while True:
                started = False
                if len(active) < depth:
                    g = next(gens, None)
                    if g is not None:
                        started = True
                        try:
                            next(g)
                            active.append(g)
                        except StopIteration:
                            pass
                if not active and not started:
                    break
                olds = active[:-1] if (started and active) else list(active)
                for g in olds:
                    try:
                        next(g)
                    except StopIteration:
                        active.remove(g)

        lat = P.sb("lat", [128, 3, 512], F32)
        latn = P.sb("latn", [128, 3, 512], BF16)

        def latent_norm(ps_list, gname):
            nch = len(ps_list)
            ps2 = nxt_ps(4, 6)
            for i, psv in enumerate(ps_list):
                P.act.activation(out=sq.full(), in_=psv, func=AF.Square)
                P.act.copy(out=lat[:, i, :], in_=psv)
                P.pe.matmul(out=ps2.full(), lhsT=ones, rhs=sq.full(), start=(i == 0), stop=(i == nch - 1))
            rstd_from(ps2.full(), 128.0 * nch, 128, rstd.full())
            for i in range(nch):
                P.dve.scalar_tensor_tensor(out=latn[:, i, :], in0=lat[:, i, :], scalar=C.col(gname, i), in1=rstd.full(),
                                           op0=ALU.mult, op1=ALU.mult)

        def small_w(name, dram_l, kc_n, ncols, i):
            bfb = P.sb(name, [128, kc_n, ncols], BF16)
            dv = dram_l.re("(kc p) n -> p kc n", p=128)
            for kc in range(kc_n):
                si = scnt[0] % 2
                scnt[0] += 1
                stg = wst[si].full().re("p a b -> p (a b)")[:, 0:ncols]
                P.dma("sync", out=stg, in_=dv[:, kc, :])
                P.act.copy(out=bfb[:, kc, :], in_=stg)
            return bfb

        uqb = small_w("uqb", w_uq[l], 3, 768, 0)
        kpb = small_w("kpb", w_kp[l], 2, 768, 1)
        wvb = small_w("wvb", w_v[l], 2, 512, 0)

        vstg = [P.sb(f"vstg{i}", [128, 512], BF16) for i in range(2)]
        vcnt = [0]

        def v_out(kind, ktl, ps_view):
            vs = vstg[vcnt[0] % 2]
            vcnt[0] += 1
            P.act.copy(out=vs.full(), in_=ps_view)
            P.dma("sync" if vcnt[0] % 2 else "scalar",
                  out=vx[ktl // 4][kind * 1024:(kind + 1) * 1024, (ktl % 4) * 64:(ktl % 4 + 1) * 64].re("(h p) d -> p h d", p=128),
                  in_=vs.full().re("p (h d) -> p h d", h=8))

        wb = load_w(0, 384)
        for m, (c0, w) in enumerate(main):
            pss = []
            for ch in range(3):
                ps = nxt_ps(0, 4)
                proj(wb, ch * 128, 128, c0, 512, ps.full())
                pss.append(ps.full())
            latent_norm(pss, "g_cq")
            def mkq(h):
                def f():
                    ps = nxt_ps(0, 4)
                    for kc in range(3):
                        P.pe.matmul(out=ps[0:96, :], lhsT=uqb[:, kc, h * 96:(h + 1) * 96], rhs=latn[:, kc, :],
                                    start=(kc == 0), stop=(kc == 2))
                    return ps[0:96, :]
                return f
            run_pipe(headnorm(mkq(h), 96, C.col("g_q"), True, m * 512, qm[h, :, m * 512:(m + 1) * 512]) for h in range(8))
        wb = load_w(384, 288)
        krb = P.sb("krb", [32, 512], BF16)
        for m, (c0, w) in enumerate(main):
            pss = []
            for ch in range(2):
                ps = nxt_ps(0, 4)
                proj(wb, ch * 128, 128, c0, 512, ps.full())
                pss.append(ps.full())
            ps = nxt_ps(0, 4)
            proj(wb, 256, 32, c0, 512, ps[0:32, :])
            P.act.copy(out=krb.full(), in_=ps[0:32, :])
            latent_norm(pss, "g_ckv")
            def mkk(h):
                def f():
                    ps = nxt_ps(0, 4)
                    for kc in range(2):
                        P.pe.matmul(out=ps[0:96, :], lhsT=kpb[:, kc, h * 96:(h + 1) * 96], rhs=latn[:, kc, :],
                                    start=(kc == 0), stop=False)
                    P.pe.matmul(out=ps[0:96, :], lhsT=selm, rhs=krb.full(), start=False, stop=True)
                    return ps[0:96, :]
                return f
            run_pipe(headnorm(mkk(h), 96, C.col("g_k"), True, m * 512, kxm[m][h * 96:(h + 1) * 96, :]) for h in range(8))
            for j in range(4):
                ps = nxt_ps(0, 4)
                for kc in range(2):
                    P.pe.matmul(out=ps.full(), lhsT=latn[:, kc, j * 128:(j + 1) * 128], rhs=wvb[:, kc, :],
                                start=(kc == 0), stop=(kc == 1))
                v_out(0, m * 4 + j, ps.full())
        for (base, gname, isq) in ((672, "g_fq", True), (672 + 512, "g_fk", False)):
            wb = load_w(base, 512)
            def mkf(wb_, h, c0):
                def f():
                    ps = nxt_ps(0, 4)
                    proj(wb_, h * 64, 64, c0, 512, ps[0:64, :])
                    return ps[0:64, :]
                return f
            gl = []
            for m, (c0, w) in enumerate(main):
                for h in range(8):
                    dst = qf[h, :, m * 512:(m + 1) * 512] if isq else kxf[m][h * 64:(h + 1) * 64, :]
                    gl.append(headnorm(mkf(wb, h, c0), 64, C.col(gname), False, m * 512, dst))
            run_pipe(gl)
        wb = load_w(672 + 1024, 512)
        for m, (c0, w) in enumerate(main):
            for j in range(4):
                ps = nxt_ps(0, 4)
                proj_tm(wb, 0, 512, c0 + j * 128, ps.full())
                v_out(1, m * 4 + j, ps.full())
        cpend.extend(list(zip(kxm, kxmg)) + list(zip(vx, vxg)))
        FB = 672 + 1536
        SB = 672 + 1544
        lf_tm = P.sb("lf_tm", [128, KT_L, 8], F32)
        dt_tm = P.sb("dt_tm", [128, KT_L, 16], F32)
        a_tm = P.sb("a_tm", [128, KT_L, 16], F32)
        tmpr = P.sb("tmpr", [128, 16], F32)
        wf = load_w(FB, 8)
        for m, (c0, w) in enumerate(main):
            for j in range(4):
                kt = m * 4 + j
                ps = nxt_ps(0, 4)
                proj_tm(wf, 0, 8, c0 + j * 128, ps[:, 0:8])
                P.dve.tensor_tensor(out=tmpr[:, 0:8], in0=ps[:, 0:8], in1=rowc[:, 0:8], op=ALU.add)
                P.act.activation(out=tmpr[:, 0:8], in_=tmpr[:, 0:8], func=AF.Exp, scale=-1.0)
                P.act.activation(out=tmpr[:, 0:8], in_=tmpr[:, 0:8], func=AF.Ln, bias=one1[:, 0:1], scale=1.0)
                P.dve.tensor_scalar(out=lf_tm[:, kt, :], in0=tmpr[:, 0:8], scalar1=-1.0, scalar2=None, op0=ALU.mult)
        wd = load_w(SB + 1024 + 1536, 16)
        for m, (c0, w) in enumerate(main):
            for j in range(4):
                kt = m * 4 + j
                ps = nxt_ps(0, 4)
                proj_tm(wd, 0, 16, c0 + j * 128, ps[:, 0:16])
                P.dve.tensor_tensor(out=tmpr.full(), in0=ps[:, 0:16], in1=rowc[:, 8:24], op=ALU.add)
                P.act.activation(out=tmpr.full(), in_=tmpr.full(), func=AF.Exp)
                P.act.activation(out=dt_tm[:, kt, :], in_=tmpr.full(), func=AF.Ln, bias=one1[:, 0:1], scale=1.0)
                P.dve.tensor_tensor(out=a_tm[:, kt, :], in0=dt_tm[:, kt, :], in1=Aneg_r.full(), op=ALU.mult)
        P.dma("sync", out=dtd.full(), in_=dt_tm.full())
        P.dma("sync", out=atd.full(), in_=a_tm.full())
        def plain_group(base, ncols, func, bias_name, dst, dst_row0):
            wb_ = load_w(base, ncols)
            for m, (c0, w) in enumerate(main):
                for ch in range(ncols // 128):
                    ps = nxt_ps(0, 4)
                    proj(wb_, ch * 128, 128, c0, 512, ps.full())
                    og = next_ostg()
                    if bias_name is None:
                        P.act.activation(out=og.full(), in_=ps.full(), func=func)
                    else:
                        P.act.activation(out=og.full(), in_=ps.full(), func=func,
                                         bias=C.col(bias_name, (dst_row0 // 128) + ch))
                    out_dma(dst[dst_row0 + ch * 128:dst_row0 + (ch + 1) * 128, m * 512:(m + 1) * 512], og.full())

        for blk in range(2):
            plain_group(SB + blk * 512, 512, AF.Silu, None, szd, blk * 512)
        upre = P.sb("upre", [128, 516], F32)
        carry = P.sb("carry", [128, 4], F32)
        acc0 = P.sb("acc0", [128, 512], F32)
        tstg = [P.sb(f"tstg{i}", [128, 4, 128], BF16) for i in range(2)]
        tcnt = [0]
        trq = []
        for blk in range(3):
            wb = load_w(SB + 1024 + blk * 512, 512)
            for m, (c0, w) in enumerate(main):
                for ch in range(4):
                    cg = blk * 4 + ch
                    ps = nxt_ps(0, 4)
                    proj(wb, ch * 128, 128, c0 - 4, 4, ps[:, 0:4])
                    P.act.copy(out=upre[:, 0:4], in_=ps[:, 0:4])
                    ps = nxt_ps(0, 4)
                    proj(wb, ch * 128, 128, c0, 512, ps.full())
                    while len(trq) > 1:
                        trq.pop(0)()
                    P.act.copy(out=upre[:, 4:516], in_=ps.full())
                    P.act.activation(out=acc0.full(), in_=ps.full(), func=AF.Identity, scale=C.col("cw3", cg), bias=C.col("cb", cg))
                    for k in range(3):
                        P.dve.scalar_tensor_tensor(out=acc0.full(), in0=upre[:, 1 + k:513 + k], scalar=C.col(f"cw{k}", cg),
                                                   in1=acc0.full(), op0=ALU.mult, op1=ALU.add)
                    og = next_ostg()
                    P.act.activation(out=og.full(), in_=acc0.full(), func=AF.Silu)
                    if cg >= 8:
                        out_dma(bct[(cg - 8) * 128:(cg - 7) * 128, m * 512:(m + 1) * 512], og.full())
                    if cg < 10:
                        def mk_tr(og=og, cg=cg, m=m):
                            def f():
                                i2 = tcnt[0] % 2
                                tcnt[0] += 1
                                for j in range(4):
                                    P.pe.transpose(out=pbt[:, i2 * 512 + j * 128:i2 * 512 + (j + 1) * 128],
                                                   in_=og[:, j * 128:(j + 1) * 128], identity=identb.full())
                                ts_ = tstg[i2]
                                P.dve.tensor_copy(out=ts_.full().re("p j f -> p (j f)"), in_=pbt[:, i2 * 512:(i2 + 1) * 512])
                                if cg < 8:
                                    P.dma("sync", out=xtm[:, m * 4:(m + 1) * 4, cg * 128:(cg + 1) * 128], in_=ts_.full())
                                else:
                                    P.dma("sync", out=btm[:, m * 4:(m + 1) * 4, (cg - 8) * 128:(cg - 7) * 128], in_=ts_.full())
                            return f
                        trq.append(mk_tr())
        while trq:
            trq.pop(0)()
        GB = SB + 2576
        for blk in range(6):
            plain_group(GB + blk * 512, 512, AF.Sigmoid, "b_gate", gd, blk * 512)
        fxs = P.sb("fxs", [128, 224], F32)
        within = P.sb("within", [128, KT_L, 8], F32)
        ttot = P.sb("ttot", [128, KT_L, 8], F32)
        f2 = lambda b: b.full().re("p a b -> p (a b)")
        ps = nxt_ps(0, 4)
        P.pe.matmul(out=ps[:, 0:128], lhsT=tri, rhs=f2(lf_tm), start=True, stop=True)
        P.act.copy(out=f2(within), in_=ps[:, 0:128])
        ps = nxt_ps(0, 4)
        P.pe.matmul(out=ps[:, 0:128], lhsT=ones, rhs=f2(lf_tm), start=True, stop=True)
        P.act.copy(out=f2(ttot), in_=ps[:, 0:128])
        Floc = fxs[:, 0:128].re("p (a b) -> p a b", b=8)
        totv = fxs[:, 128:160].re("p (a b) -> p a b", b=8)
        cacc = P.sb("cacc", [128, 8], F32)
        for m in range(4):
            P.dve.tensor_copy(out=Floc[:, 4 * m, :], in_=within[:, 4 * m, :])
            P.dve.tensor_copy(out=cacc.full(), in_=ttot[:, 4 * m, :])
            for j in range(1, 4):
                P.dve.tensor_tensor(out=Floc[:, 4 * m + j, :], in0=within[:, 4 * m + j, :], in1=cacc.full(), op=ALU.add)
                P.dve.tensor_tensor(out=cacc.full(), in0=cacc.full(), in1=ttot[:, 4 * m + j, :], op=ALU.add)
            P.dve.tensor_copy(out=totv[:, m, :], in_=cacc.full())
        ps = nxt_ps(0, 4)
        P.pe.transpose(out=ps[:, 0:128], in_=fxs[:, 0:128], identity=ident)
        FT = P.sb("FT", [128, 128], F32)
        r1 = P.sb("r1", [128, 128], F32)
        fh = [P.sb(f"fh{i}", [128, 128], BF16) for i in range(3)]
        P.act.copy(out=FT.full(), in_=ps[:, 0:128])
        P.dve.tensor_copy(out=fh[0].full(), in_=FT.full())
        P.dve.tensor_tensor(out=r1.full(), in0=FT.full(), in1=fh[0].full(), op=ALU.subtract)
        P.dve.tensor_copy(out=fh[1].full(), in_=r1.full())
        P.dve.tensor_tensor(out=r1.full(), in0=r1.full(), in1=fh[1].full(), op=ALU.subtract)
        P.dve.tensor_copy(out=fh[2].full(), in_=r1.full())
        nfh = [P.sb(f"nfh{i}", [128, 128], BF16) for i in range(3)]
        for r in range(3):
            P.dve.tensor_scalar(out=nfh[r].full(), in0=fh[r].full(), scalar1=-1.0, scalar2=None, op0=ALU.mult)
            for kt in range(KT_L):
                P.dma("sync" if kt % 2 else "scalar", out=fq[:, r, kt * 128:(kt + 1) * 128], in_=fh[r][kt * 8:(kt + 1) * 8, :])
                P.dma("scalar" if kt % 2 else "sync", out=nfq[:, r, kt * 128:(kt + 1) * 128], in_=nfh[r][kt * 8:(kt + 1) * 8, :])
        for m in range(4):
            P.dma("sync", out=kxf[m][512:536, :].re("(h r) c -> h r c", r=3), in_=nfq[:, :, m * 512:(m + 1) * 512])
        P.dma("sync", out=fx[:, 0:160], in_=fxs[:, 0:160])
        gather_pairs(cpend + list(zip(kxf, kxfg)))
        del cpend[:]

    def ssd_scan(l, K, pass1, fxs=None, dt_tm=None, a_tm=None, Hinit=None, rowc=None, pb=None):
        cmb = K["cm"]
        tri, trimask, ones = cmb[:, 0, :], cmb[:, 1, :], cmb[:, 3, :]
        if pb is None:
            pb = [P.ps(f"spb{i}", [128, 512], F32) for i in range(7)]
        if pass1:
            fxs = P.sb("decs", [128, 224], F32)
        if dt_tm is None:
            dt_tm = P.sb("dt_tm", [128, KT_L, 16], F32)
            a_tm = P.sb("a_tm", [128, KT_L, 16], F32)
            P.dma("sync", out=dt_tm.full(), in_=dtd.full())
            P.dma("sync", out=a_tm.full(), in_=atd.full())
        fl = lambda b: b.full().re("p c h -> p (c h)")
        Acum = P.sb("Acum", [128, KT_L, 16], F32)
        Atot = P.sb("Atot", [128, KT_L, 16], F32)
        wdec = P.sb("wdec", [128, KT_L, 16], F32)
        eAtot = P.sb("eAtot", [128, KT_L, 16], F32)
        psA = pb[0]
        P.pe.matmul(out=psA[:, 0:256], lhsT=tri, rhs=fl(a_tm), start=True, stop=True)
        P.act.copy(out=fl(Acum), in_=psA[:, 0:256])
        P.pe.matmul(out=psA[:, 256:512], lhsT=ones, rhs=fl(a_tm), start=True, stop=True)
        P.act.copy(out=fl(Atot), in_=psA[:, 256:512])
        P.act.activation(out=fl(eAtot), in_=fl(Atot), func=AF.Exp)
        P.dve.tensor_tensor(out=fl(wdec), in0=fl(Atot), in1=fl(Acum), op=ALU.subtract)
        P.act.activation(out=fl(wdec), in_=fl(wdec), func=AF.Exp)
        if not pass1:
            nAcum = P.sb("nAcum", [128, KT_L, 16], F32)
            eA = P.sb("eA", [128, KT_L, 16], F32)
            P.dve.tensor_scalar(out=fl(nAcum), in0=fl(Acum), scalar1=-1.0, scalar2=None, op0=ALU.mult)
            P.act.activation(out=fl(eA), in_=fl(Acum), func=AF.Exp)
            BCs = P.sb("BCs", [128, 4, T], BF16)
            P.dma("gpsimd", out=BCs.full(), in_=bct.full().re("(a p) t -> p a t", p=128))
            cb = P.sb("cb", [128, 2, 128], F32)
            NH = 4
            at = [P.sb(f"at{i}", [128, 128], F32) for i in range(NH)]
            tm = [P.sb(f"tm{i}", [128, 128], F32) for i in range(NH)]
            dec = [P.sb(f"dec{i}", [128, 128], F32) for i in range(NH)]
            MT = [P.sb(f"MT{i}", [128, 128], BF16) for i in range(NH)]
            t1 = P.sb("t1", [128, 1024], F32)
            t3 = P.sb("t3", [128, 1024], F32)
            yo = P.sb("yo", [128, 1024], BF16)
            yT = [P.sb(f"yT{i}", [128, 4, 128], F32) for i in range(2)]
        Hs = P.sb("Hs", [128, 1024], F32)
        Hb = P.sb("Hb", [128, 1024], BF16)
        xc = [P.sb(f"xc{i}", [128, 1024], BF16) for i in range(2)]
        Bc = [P.sb(f"Bc{i}", [128, 256], BF16) for i in range(2)]
        xdt = P.sb("xdt", [128, 1024], BF16)
        xdts = P.sb("xdts", [128, 1024], BF16)
        dsum = P.sb("dsum", [128, 16], F32)
        v3 = lambda v: v.re("p (h d) -> p h d", h=16)
        bc3 = lambda v: v.f(lambda a: a.unsqueeze(2).to_broadcast([128, 16, 64]))
        for m in range(4):
            if pass1:
                P.dve.memset(ap=Hs.full(), constant=0.0)
                P.dve.memset(ap=dsum.full(), constant=0.0)
            else:
                P.dve.tensor_copy(out=Hs.full(), in_=Hinit[:, m, :])
                P.act.copy(out=Hb.full(), in_=Hinit[:, m, :])
            for j in range(4):
                c = m * 4 + j
                x_c = xc[c % 2]
                B_c = Bc[c % 2]
                P.dma("sync", out=x_c.full(), in_=xtm[:, c, :])
                P.dma("gpsimd", out=B_c.full(), in_=btm[:, c, :])
                P.dve.tensor_tensor(out=v3(xdt.full()), in0=v3(x_c.full()), in1=bc3(dt_tm[:, c, :]), op=ALU.mult)
                P.pool.tensor_tensor(out=v3(xdts.full()), in0=v3(xdt.full()), in1=bc3(wdec[:, c, :]), op=ALU.mult)
                if not pass1:
                    cs = slice(c * 128, (c + 1) * 128)
                    ps_cb = pb[1]
                    for g in range(2):
                        P.pe.matmul(out=ps_cb[:, g * 128:(g + 1) * 128], lhsT=BCs[:, g, cs], rhs=BCs[:, 2 + g, cs],
                                    start=True, stop=True)
                    P.act.copy(out=cb.full().re("p a b -> p (a b)"), in_=ps_cb[:, 0:256])
                    ps_off = [pb[2], pb[3]]
                    for g in range(2):
                        P.pe.matmul(out=ps_off[g].full(), lhsT=BCs[:, 2 + g, cs], rhs=Hb[:, g * 512:(g + 1) * 512],
                                    start=True, stop=True)
                    ps_y = [pb[4], pb[5]]
                    def st1(h):
                        i2 = h % NH
                        g = h // 8
                        P.dve.tensor_scalar(out=at[i2].full(), in0=tri, scalar1=a_tm[:, c, h:h + 1], scalar2=None, op0=ALU.mult)
                        ps_A = pb[6]
                        P.pe.matmul(out=ps_A[:, i2 * 128:(i2 + 1) * 128], lhsT=ones, rhs=at[i2].full(), start=True, stop=True)
                        P.dve.tensor_tensor(out=tm[i2].full(), in0=ps_A[:, i2 * 128:(i2 + 1) * 128], in1=trimask, op=ALU.add)
                        P.act.activation(out=dec[i2].full(), in_=tm[i2].full(), func=AF.Exp, bias=nAcum[:, c, h:h + 1], scale=1.0)
                        P.pool.tensor_tensor(out=MT[i2].full(), in0=cb[:, g, :], in1=dec[i2].full(), op=ALU.mult)

                    def st2(h):
                        i2 = h % NH
                        g = h // 8
                        hh = h % 8
                        P.pe.matmul(out=ps_y[g][:, hh * 64:(hh + 1) * 64], lhsT=MT[i2].full(), rhs=xdt[:, h * 64:(h + 1) * 64],
                                    start=True, stop=True)

                    for hq in range(16 + 3):
                        if hq < 16:
                            st1(hq)
                        if hq >= 3:
                            st2(hq - 3)
                    for g in range(2):
                        gs_ = slice(g * 512, (g + 1) * 512)
                        v8 = lambda v: v.re("p (h d) -> p h d", h=8)
                        b8 = lambda v: v.f(lambda a: a.unsqueeze(2).to_broadcast([128, 8, 64]))
                        P.dve.tensor_tensor(out=v8(t1[:, gs_]), in0=v8(ps_off[g].full()), in1=b8(eA[:, c, g * 8:(g + 1) * 8]), op=ALU.mult)
                        P.dve.tensor_tensor(out=t1[:, gs_], in0=t1[:, gs_], in1=ps_y[g].full(), op=ALU.add)
                    P.pool.tensor_tensor(out=v3(t3.full()), in0=v3(x_c.full()), in1=bc3(rowc[:, 40:56]), op=ALU.mult)
                    P.pool.tensor_tensor(out=t3.full(), in0=t1.full(), in1=t3.full(), op=ALU.add)
                    for q4 in range(2):
                        pst = pb[2 + q4]
                        for jj in range(4):
                            fc = q4 * 4 + jj
                            P.pe.transpose(out=pst[:, jj * 128:(jj + 1) * 128], in_=t3[:, fc * 128:(fc + 1) * 128],
                                           identity=cmb[:, 2, :])
                        yt = yT[q4]
                        P.act.copy(out=yt.full().re("p a b -> p (a b)"), in_=pst.full())
                        P.dma("sync", out=yd[q4 * 512:(q4 + 1) * 512, c * 128:(c + 1) * 128].re("(a p) t -> p a t", p=128),
                              in_=yt.full())
                ps_h = [pb[0], pb[1]] if pass1 else [pb[4], pb[5]]
                for g in range(2):
                    P.pe.matmul(out=ps_h[g].full(), lhsT=B_c[:, g * 128:(g + 1) * 128], rhs=xdts[:, g * 512:(g + 1) * 512],
                                start=True, stop=True)
                P.dve.tensor_tensor(out=v3(Hs.full()), in0=v3(Hs.full()), in1=bc3(eAtot[:, c, :]), op=ALU.mult)
                for g in range(2):
                    P.dve.tensor_tensor(out=Hs[:, g * 512:(g + 1) * 512], in0=Hs[:, g * 512:(g + 1) * 512], in1=ps_h[g].full(), op=ALU.add)
                if pass1:
                    P.dve.tensor_tensor(out=dsum.full(), in0=dsum.full(), in1=Atot[:, c, :], op=ALU.add)
                else:
                    P.act.copy(out=Hb.full(), in_=Hs.full())
            if pass1:
                P.dma("sync", out=sx[m // 2][(m % 2) * 128:(m % 2 + 1) * 128, :], in_=Hs.full())
                P.act.activation(out=fxs[:, 160 + m * 16:160 + (m + 1) * 16], in_=dsum.full(), func=AF.Exp)
        if pass1:
            P.dma("sync", out=fx[:, 160:224], in_=fxs[:, 160:224])

    def load_fg():
        fg = P.sb("fg", [128, 4, 224], F32)
        P.dma("sync", out=fg.full(), in_=fxg.full().re("(r p) c -> p r c", p=128))
        return fg

    def phase_attn(l):
        K = load_consts()
        sel, zero = K["sel"], K["zero"]
        mskb = P.sb("mskb", [128, 8, 512], F32)
        P.dma("gpsimd", out=mskb.full(), in_=msk_d.full())
        fg = load_fg()
        offs = P.sb("offs", [128, 16, 8], F32)
        run = P.sb("run", [128, 8], F32)
        P.dve.memset(ap=run.full(), constant=0.0)
        for s_ in range(16):
            m, r = divmod(s_, 4)
            P.dve.tensor_copy(out=offs[:, s_, :], in_=run.full())
            P.dve.tensor_tensor(out=run.full(), in0=run.full(), in1=fg[:, r, 128 + m * 8:128 + (m + 1) * 8], op=ALU.add)
        offown = P.sb("offown", [128, 4, 8], F32)
        P.dve.memset(ap=offown.full(), constant=0.0)
        for m in range(4):
            for r in range(4):
                P.dve.scalar_tensor_tensor(out=offown[:, m, :], in0=offs[:, 4 * m + r, :], scalar=sel[:, 8 + 4 * m + r:9 + 4 * m + r],
                                           in1=offown[:, m, :], op0=ALU.mult, op1=ALU.add)
        btab = P.sb("btab", [128, 4, 16, 8], F32)
        for m in range(4):
            P.dve.tensor_tensor(out=btab[:, m, :, :], in0=offown[:, m, :].f(lambda a: a.unsqueeze(1).to_broadcast([128, 16, 8])),
                                in1=offs.full(), op=ALU.subtract)
            for jr in range(4):
                s_ = 4 * m + jr
                P.dve.tensor_scalar(out=btab[:, m, s_, :], in0=btab[:, m, s_, :], scalar1=sel[:, 28 + jr:29 + jr], scalar2=None,
                                    op0=ALU.add)

        pss = [P.ps(f"pss{i}", [128, 1024], F32) for i in range(3)]
        pbo = [P.ps(f"pbo{i}", [128, 512], F32) for i in range(2)]
        K_sb = [P.sb(f"K_sb{i}", [96, S_], BF16) for i in range(2)]
        Q_sb = [P.sb(f"Q_sb{i}", [96, T], BF16) for i in range(2)]
        V_sb = [P.sb(f"V_sb{i}", [128, NKT, 128], BF16) for i in range(2)]
        for i in range(2):
            P.dve.memset(ap=V_sb[i][:, :, 64:128], constant=1.0)
        NSB = 3
        LA = 2
        pt = [P.sb(f"pt{i}", [128, 1024], BF16) for i in range(NSB)]
        mt = [P.sb(f"mt{i}", [128, 1024], F32) for i in range(2)]
        rl = P.sb("rl", [128, 512], F32)
        rl2 = P.sb("rl2", [64, 512], F32)
        ot = [P.sb(f"ot{i}", [64, 512], BF16) for i in range(2)]
        cnt = [0, 0, 0]
        heads = [(0, h) for h in range(8)] + [(1, h) for h in range(8)]

        def loads(idx):
            kind, h = heads[idx]
            i = idx % 2
            nd = 96 if kind == 0 else 64
            if kind == 1:
                P.dve.memset(ap=K_sb[i][64:96, :], constant=0.0)
                P.dve.memset(ap=K_sb[i][64:67, :], constant=8.0)
                P.pool.memset(ap=Q_sb[i][64:96, :], constant=0.0)
                P.pool.memset(ap=Q_sb[i][64:70, :], constant=8.0)
            for r in range(4):
                vr = r * 2048 + kind * 1024 + h * 128
                for m in range(4):
                    s0 = (4 * m + r) * 512
                    if kind == 0:
                        ksrc = kxmg[m][r * 768 + h * 96:r * 768 + (h + 1) * 96, :]
                    else:
                        ksrc = kxfg[m][r * 640 + h * 64:r * 640 + (h + 1) * 64, :]
                        P.dma("gpsimd" if (r + m) % 2 == 0 else "sync", out=K_sb[i][67:70, s0:s0 + 512],
                              in_=kxfg[m][r * 640 + 512 + h * 3:r * 640 + 512 + (h + 1) * 3, :])
                    P.dma("sync" if (r + m) % 2 == 0 else "gpsimd", out=K_sb[i][0:nd, s0:s0 + 512], in_=ksrc)
                    g0 = (4 * m + r) * 4
                    P.dma("gpsimd" if (r + m) % 2 == 0 else "sync",
                          out=V_sb[i][:, g0:g0 + 4, 0:64],
                          in_=vxg[m][vr:vr + 128, :].re("p (j d) -> p j d", j=4))
            if kind == 0:
                P.dma("sync", out=Q_sb[i][0:96, :], in_=qm[h])
            else:
                P.dma("sync", out=Q_sb[i][0:64, :], in_=qf[h])
                P.dma("gpsimd", out=Q_sb[i][64:67, :], in_=fq[h])

        iters = []
        for idx in range(16):
            for m in range(4):
                nkp = (4 * m + 4) * 2
                for kp in range(nkp):
                    iters.append((idx, m, kp, nkp))

        def stage_qk(n):
            idx, m, kp, nkp = iters[n]
            kind, h = heads[idx]
            i = idx % 2
            scale = 96.0 ** -0.5 if kind == 0 else 0.125
            i3 = n % NSB
            ps = pss[i3]
            for e in range(2):
                kt = 2 * kp + e
                P.pe.matmul(out=ps[:, e * 512:(e + 1) * 512], lhsT=K_sb[i][0:96, kt * 128:(kt + 1) * 128],
                            rhs=Q_sb[i][0:96, m * 512:(m + 1) * 512], start=True, stop=True)
            blk = (2 * kp) // 4
            if blk >= 4 * m:
                jr = blk - 4 * m
                k4 = (2 * kp) % 4
                mm = mt[cnt[1] % 2]
                cnt[1] += 1
                P.dve.scalar_tensor_tensor(out=mm.full(), in0=mskb[:, kind * 4 + k4:kind * 4 + k4 + 2, :].re("p a b -> p (a b)"),
                                           scalar=sel[:, 24 + jr:25 + jr], in1=ps.full(), op0=ALU.mult, op1=ALU.add)
                src = mm.full()
                bias = sel[:, 28 + jr:29 + jr] if kind == 0 else btab[:, m, blk, h:h + 1]
            else:
                src = ps.full()
                bias = zero[:, 0:1] if kind == 0 else btab[:, m, blk, h:h + 1]
            P.act.activation(out=pt[i3].full(), in_=src, func=AF.Exp, scale=scale, bias=bias)

        def stage_pv(n):
            idx, m, kp, nkp = iters[n]
            kind, h = heads[idx]
            i = idx % 2
            oacc = pbo[(idx * 4 + m) % 2]
            for e in range(2):
                kt = 2 * kp + e
                P.pe.matmul(out=oacc.full(), lhsT=V_sb[i][:, kt, :], rhs=pt[n % NSB][:, e * 512:(e + 1) * 512],
                            start=(kp == 0 and e == 0), stop=(kp == nkp - 1 and e == 1))
            if kp == nkp - 1:
                odst = omd if kind == 0 else ofd
                P.dve.reciprocal(out=rl[64:128, :], in_=oacc[64:128, :])
                P.dve.tensor_copy(out=rl2.full(), in_=rl[64:128, :])
                o = ot[m % 2]
                P.dve.tensor_tensor(out=o.full(), in0=oacc[0:64, :], in1=rl2.full(), op=ALU.mult)
                P.dma("sync", out=odst[h * 64:(h + 1) * 64, m * 512:(m + 1) * 512], in_=o.full())

        pcs = [P.sb(f"pcs{i}", [128, 2048], F32) for i in range(2)]
        pcb = [P.sb(f"pcb{i}", [128, 2048], BF16) for i in range(2)]
        jobs = []
        for (src, dst, rows, cols) in ((w_a[l], wab, 512, D), (w_b[l], wbb, 512, D), (w_c[l], wcb, 1024, D), (w_o[l], wob, 1024, D),
                                       (w_up[l], wub, D, 5632), (w_dn[l], wdb, 2816, D)):
            for r0 in range(0, rows, 128):
                for c0 in range(0, cols, 2048):
                    n_ = min(2048, cols - c0)
                    jobs.append((src[r0:r0 + 128, c0:c0 + n_], dst[r0:r0 + 128, c0:c0 + n_], n_))
        jcnt = [0]

        def precast_one():
            if jcnt[0] >= len(jobs):
                return
            src, dst, n_ = jobs[jcnt[0]]
            i = jcnt[0] % 2
            jcnt[0] += 1
            P.dma("gpsimd", out=pcs[i][:, 0:n_], in_=src)
            P.pool.tensor_copy(out=pcb[i][:, 0:n_], in_=pcs[i][:, 0:n_])
            P.dma("gpsimd", out=dst, in_=pcb[i][:, 0:n_])

        every = max(1, len(iters) // (len(jobs) + 4))
        loads(0)
        loads(1)
        for n in range(len(iters) + LA):
            if n < len(iters):
                stage_qk(n)
            if n >= LA:
                stage_pv(n - LA)
                idx_p, m_p, kt_p, nk_p = iters[n - LA]
                if m_p == 3 and kt_p == nk_p - 1 and idx_p + 2 < 16:
                    loads(idx_p + 2)
            if n % every == every - 1:
                precast_one()
        while jcnt[0] < len(jobs):
            precast_one()

    def phase_ssd2(l):
        K = load_consts()
        sel = K["sel"]
        rowc = P.sb("rowc", [128, 56], F32)
        P.dma("sync", out=rowc.full(), in_=rowc_d[l])
        fg = load_fg()
        Hin = P.sb("Hin", [128, 1024], F32)
        Hsel = P.sb("Hsel", [128, 4, 1024], F32)
        Sst = [P.sb(f"Sst{i}", [128, 1024], F32) for i in range(2)]
        P.dve.memset(ap=Hin.full(), constant=0.0)
        P.dve.memset(ap=Hsel.full(), constant=0.0)
        v3 = lambda v: v.re("p (h d) -> p h d", h=16)
        for s_ in range(16):
            m, r = divmod(s_, 4)
            P.dve.scalar_tensor_tensor(out=Hsel[:, m, :], in0=Hin.full(), scalar=sel[:, 8 + s_:9 + s_], in1=Hsel[:, m, :],
                                       op0=ALU.mult, op1=ALU.add)
            if s_ < 15:
                st_ = Sst[s_ % 2]
                P.dma("sync" if s_ % 2 else "gpsimd", out=st_.full(),
                      in_=sxg[m // 2][r * 256 + (m % 2) * 128:r * 256 + (m % 2 + 1) * 128, :])
                dcs = fg[:, r, 160 + m * 16:160 + (m + 1) * 16]
                P.dve.tensor_tensor(out=v3(Hin.full()), in0=v3(Hin.full()),
                                    in1=dcs.f(lambda a: a.unsqueeze(2).to_broadcast([128, 16, 64])), op=ALU.mult)
                P.pool.tensor_tensor(out=Hin.full(), in0=Hin.full(), in1=st_.full(), op=ALU.add)
        ssd_scan(l, K, pass1=False, Hinit=Hsel, rowc=rowc)

    def write_tails(txs):
        P.dma("sync", out=tx.full(), in_=txs.full().re("p m k c -> p (m k c)"))

    def halo_exchange(dst):
        K = load_consts()
        sel = K["sel"]
        P.pool.collective_compute(kind="AllGather", op=ALU.bypass, replica_groups=RG, ins=[tx.full()], outs=[txg.full()])
        tg = P.sb("tg", [128, 4, 128], F32)
        P.dma("sync", out=tg.full(), in_=txg.full().re("(r p) c -> p r c", p=128))
        hl = P.sb("hl", [128, 4, 32], F32)
        P.dve.memset(ap=hl.full(), constant=0.0)
        for m in range(4):
            for r in range(4):
                P.dve.scalar_tensor_tensor(out=hl[:, m, :], in0=tg[:, r, m * 32:(m + 1) * 32], scalar=sel[:, r:r + 1],
                                           in1=hl[:, m, :], op0=ALU.mult, op1=ALU.add)
            if m >= 1:
                P.dve.scalar_tensor_tensor(out=hl[:, m, :], in0=tg[:, 3, (m - 1) * 32:m * 32], scalar=sel[:, 4:5],
                                           in1=hl[:, m, :], op0=ALU.mult, op1=ALU.add)
        dv = dst.full().re("(kc p) n -> p kc n", p=128)
        for m in range(4):
            P.dma("sync", out=dv[:, :, m * SW:m * SW + 4], in_=hl[:, m, :].re("p (k c) -> p k c", c=4))

    def phase_merge(l):
        K = load_consts()
        ones, eps = K["cm"][:, 3, :], K["eps"]
        gsb = P.sb("gsb", [128, 8], F32)
        P.dma("sync", out=gsb.full(), in_=gssm_d[l])
        pb = [P.ps(f"pb{i}", [128, 512], F32) for i in range(8)]
        def load_wb(dram_bf, kc_n, name, q):
            bfb = P.sb(name, [128, kc_n, D], BF16)
            P.dma(q, out=bfb.full(), in_=dram_bf.full().re("(kc p) n -> p kc n", p=128))
            return bfb

        Wa = load_wb(wab, 4, "Wa", "sync")
        Wb = load_wb(wbb, 4, "Wb", "gpsimd")
        Wc = load_wb(wcb, 8, "Wc", "sync")
        Wo = load_wb(wob, 8, "Wo", "gpsimd")
        ys = P.sb("ys", [128, 8, 512], F32)
        szs = P.sb("szs", [128, 8, 512], BF16)
        yn = P.sb("yn", [128, 8, 512], BF16)
        oms = P.sb("oms", [128, 4, 512], BF16)
        ofs = P.sb("ofs", [128, 4, 512], BF16)
        gs = P.sb("gs", [128, 24, 512], BF16)
        xs = P.sb("xs", [128, 8, 512], F32)
        sq = P.sb("sq", [128, 512], F32)
        rstd = P.sb("rstd", [128, 512], F32)
        m1 = [P.sb(f"m1_{i}", [128, 512], F32) for i in range(2)]
        m2 = [P.sb(f"m2_{i}", [128, 512], F32) for i in range(2)]
        m3 = [P.sb(f"m3_{i}", [128, 512], F32) for i in range(2)]
        mg = P.sb("mg", [128, 8, 512], BF16)
        xo = [P.sb(f"xo{i}", [128, 512], F32) for i in range(2)]
        txs = P.sb("txs", [128, 4, 8, 4], F32)
        ch = lambda d: d.full().re("(kc p) n -> p kc n", p=128)
        xmv = ch(xmid)
        for ti in range(4):
            ts = slice(ti * 512, (ti + 1) * 512)
            xsl = slice(ti * SW + 4, ti * SW + 516)
            P.dma("sync", out=ys.full(), in_=ch(yd)[:, :, ts])
            P.dma("gpsimd", out=szs.full(), in_=ch(szd)[:, :, ts])
            P.dma("sync", out=oms.full(), in_=ch(omd)[:, :, ts])
            P.dma("gpsimd", out=ofs.full(), in_=ch(ofd)[:, :, ts])
            P.dma("sync", out=gs.full(), in_=ch(gd)[:, :, ts])
            P.dma("gpsimd", out=xs.full(), in_=ch(xb[l])[:, :, xsl])
            ps = pb[7]
            for kc in range(8):
                P.dve.tensor_tensor(out=ys[:, kc, :], in0=ys[:, kc, :], in1=szs[:, kc, :], op=ALU.mult)
                P.act.activation(out=sq.full(), in_=ys[:, kc, :], func=AF.Square)
                P.pe.matmul(out=ps.full(), lhsT=ones, rhs=sq.full(), start=(kc == 0), stop=(kc == 7))
            P.act.activation(out=rstd.full(), in_=ps.full(), func=AF.Ln, bias=eps[:, 0:1], scale=1.0 / 1024.0)
            P.act.activation(out=rstd.full(), in_=rstd.full(), func=AF.Exp, scale=-0.5)
            for kc in range(8):
                P.dve.scalar_tensor_tensor(out=yn[:, kc, :], in0=ys[:, kc, :], scalar=gsb[:, kc:kc + 1], in1=rstd.full(),
                                           op0=ALU.mult, op1=ALU.mult)
            for oc in range(8):
                i2 = oc % 2
                osl = slice(oc * 128, (oc + 1) * 128)
                pa, pbb, pc = pb[0 + i2 * 3], pb[1 + i2 * 3], pb[2 + i2 * 3]
                for kc in range(4):
                    P.pe.matmul(out=pa.full(), lhsT=Wa[:, kc, osl], rhs=oms[:, kc, :], start=(kc == 0), stop=(kc == 3))
                for kc in range(4):
                    P.pe.matmul(out=pbb.full(), lhsT=Wb[:, kc, osl], rhs=ofs[:, kc, :], start=(kc == 0), stop=(kc == 3))
                for kc in range(8):
                    P.pe.matmul(out=pc.full(), lhsT=Wc[:, kc, osl], rhs=yn[:, kc, :], start=(kc == 0), stop=(kc == 7))
                P.dve.tensor_tensor(out=m1[i2].full(), in0=pa.full(), in1=gs[:, oc, :], op=ALU.mult)
                P.dve.tensor_tensor(out=m2[i2].full(), in0=pbb.full(), in1=gs[:, 8 + oc, :], op=ALU.mult)
                P.dve.tensor_tensor(out=m3[i2].full(), in0=pc.full(), in1=gs[:, 16 + oc, :], op=ALU.mult)
                P.pool.tensor_tensor(out=m1[i2].full(), in0=m1[i2].full(), in1=m2[i2].full(), op=ALU.add)
                P.pool.tensor_tensor(out=mg[:, oc, :], in0=m1[i2].full(), in1=m3[i2].full(), op=ALU.add)
            for oc in range(8):
                i2 = oc % 2
                ps = pb[6 + i2]
                for kc in range(8):
                    P.pe.matmul(out=ps.full(), lhsT=Wo[:, kc, oc * 128:(oc + 1) * 128], rhs=mg[:, kc, :],
                                start=(kc == 0), stop=(kc == 7))
                P.dve.tensor_tensor(out=xo[i2].full(), in0=ps.full(), in1=xs[:, oc, :], op=ALU.add)
                P.pool.tensor_copy(out=txs[:, ti, oc, :], in_=xo[i2][:, 508:512])
                P.dma("sync" if i2 else "gpsimd", out=xmv[:, oc, xsl], in_=xo[i2].full())
        write_tails(txs)

    def phase_ffn(l, last):
        K = load_consts()
        ones, eps = K["cm"][:, 3, :], K["eps"]
        cstb = P.sb("cstb", [128, offD2["_n"]], F32)
        C = Cst(P, cstb, offD2)
        P.dma("sync", out=cstb.full(), in_=cstD_d[l])
        pb = [P.ps(f"pb{i}", [128, 512], F32) for i in range(8)]
        Wu = P.sb("Wu", [128, 8, 5632], BF16)
        Wd = P.sb("Wd", [128, 22, D], BF16)
        wubv = wub.full().re("(kc p) n -> p kc n", p=128)
        for (c0, c1) in ((0, 512), (2816, 3328), (512, 2816), (3328, 5632)):
            P.dma("sync" if c0 < 2816 else "gpsimd", out=Wu[:, :, c0:c1], in_=wubv[:, :, c0:c1])
        wdbv = wdb.full().re("(kc p) n -> p kc n", p=128)
        P.dma("sync", out=Wd[:, 0:11, :], in_=wdbv[:, 0:11, :])
        P.dma("gpsimd", out=Wd[:, 11:22, :], in_=wdbv[:, 11:22, :])
        xst = P.sb("xst", [128, 8, 512], F32)
        hn = P.sb("hn", [128, 8, 512], BF16)
        sq = P.sb("sq", [128, 512], F32)
        rstd = P.sb("rstd", [128, 512], F32)
        act = P.sb("act", [128, 22, 512], BF16)
        upre = [P.sb(f"upre{i}", [128, 516], F32) for i in range(2)]
        acc = [P.sb(f"acc{i}", [128, 512], F32) for i in range(2)]
        sg = P.sb("sg", [128, 512], F32)
        carry = P.sb("carry", [128, 44, 4], F32)
        xo = [P.sb(f"xo{i}", [128, 512], F32) for i in range(2)]
        txs = P.sb("txs", [128, 4, 8, 4], F32)
        xTv = xmid.full().re("(kc p) n -> p kc n", p=128)
        dst = out if last else xb[l + 1]
        dv = dst.full().re("(kc p) n -> p kc n", p=128)
        tiles = []
        for m in range(4):
            tiles.append((m * SW, 4, True, m))
            tiles.append((m * SW + 4, 512, False, m))
        pcnt = [0]
        for (c0, w, is_halo, m) in tiles:
            P.dma("sync", out=xst[:, :, 0:w], in_=xTv[:, :, c0:c0 + w])
            ps = pb[7]
            for kc in range(8):
                P.act.activation(out=sq[:, 0:w], in_=xst[:, kc, 0:w], func=AF.Square)
                P.pe.matmul(out=ps[:, 0:w], lhsT=ones, rhs=sq[:, 0:w], start=(kc == 0), stop=(kc == 7))
            P.act.activation(out=rstd[:, 0:w], in_=ps[:, 0:w], func=AF.Ln, bias=eps[:, 0:1], scale=1.0 / 1024.0)
            P.act.activation(out=rstd[:, 0:w], in_=rstd[:, 0:w], func=AF.Exp, scale=-0.5)
            for kc in range(8):
                P.dve.scalar_tensor_tensor(out=hn[:, kc, 0:w], in0=xst[:, kc, 0:w], scalar=C.col("g_ffn", kc),
                                           in1=rstd[:, 0:w], op0=ALU.mult, op1=ALU.mult)
            for i in range(22):
                accs = []
                for j, cg in enumerate((i, 22 + i)):
                    ps = pb[pcnt[0] % 4]
                    pcnt[0] += 1
                    for kc in range(8):
                        P.pe.matmul(out=ps[:, 0:w], lhsT=Wu[:, kc, cg * 128:(cg + 1) * 128], rhs=hn[:, kc, 0:w],
                                    start=(kc == 0), stop=(kc == 7))
                    if is_halo:
                        P.act.copy(out=carry[:, cg, :], in_=ps[:, 0:4])
                        continue
                    up = upre[j]
                    a0 = acc[j]
                    P.act.copy(out=up[:, 4:516], in_=ps.full())
                    P.act.activation(out=a0.full(), in_=ps.full(), func=AF.Identity, scale=C.col("fw2", cg), bias=C.col("fb", cg))
                    P.pool.tensor_copy(out=up[:, 0:4], in_=carry[:, cg, :])
                    P.dve.scalar_tensor_tensor(out=a0.full(), in0=up[:, 3:515], scalar=C.col("fw1", cg), in1=a0.full(),
                                               op0=ALU.mult, op1=ALU.add)
                    P.dve.scalar_tensor_tensor(out=a0.full(), in0=up[:, 2:514], scalar=C.col("fw0", cg), in1=a0.full(),
                                               op0=ALU.mult, op1=ALU.add)
                    accs.append(a0)
                if is_halo:
                    continue
                P.act.activation(out=sg.full(), in_=accs[0].full(), func=AF.Silu)
                P.pool.tensor_tensor(out=act[:, i, :], in0=sg.full(), in1=accs[1].full(), op=ALU.mult)
            if is_halo:
                continue
            for oc in range(8):
                i2 = oc % 2
                ps = pb[4 + i2]
                for i in range(22):
                    P.pe.matmul(out=ps.full(), lhsT=Wd[:, i, oc * 128:(oc + 1) * 128], rhs=act[:, i, :],
                                start=(i == 0), stop=(i == 21))
                P.dve.tensor_tensor(out=xo[i2].full(), in0=ps.full(), in1=xst[:, oc, :], op=ALU.add)
                if last:
                    P.dma("sync" if i2 else "gpsimd", out=dv[:, oc, m * 512:(m + 1) * 512], in_=xo[i2].full())
                else:
                    P.pool.tensor_copy(out=txs[:, m, oc, :], in_=xo[i2][:, 508:512])
                    P.dma("sync" if i2 else "gpsimd", out=dv[:, oc, m * SW + 4:m * SW + 516], in_=xo[i2].full())
        if not last:
            write_tails(txs)

    def gather_e1():
        gather_pairs(list(zip(sx, sxg)) + [(fx, fxg)])

    nl = L if stop is None else stop[0]
    done = False
    for l in range(nl):
        last_l = (stop is not None and l == nl - 1)
        phase_A(l)
        P.emit(final=False)
        ssd_scan(l, load_consts(), pass1=True)
        P.emit(final=False)
        if last_l and stop[1] == "A":
            break
        gather_e1()
        phase_attn(l)
        P.emit(final=False)
        phase_ssd2(l)
        P.emit(final=False)
        if last_l and stop[1] == "B":
            break
        phase_merge(l)
        P.emit(final=False)
        halo_exchange(xmid)
        P.emit(final=False)
        if last_l and stop[1] == "C":
            break
        phase_ffn(l, last=(l == L - 1))
        P.emit(final=False)
        if l < L - 1:
            halo_exchange(xb[l + 1])
            P.emit(final=False)
    loc = {"kxmg0": kxmg[0], "vxg0": vxg[0], "sxg0": sxg[0], "fxg": fxg, "qm": qm, "qf": qf, "fq": fq, "omd": omd, "ofd": ofd, "yd": yd,
           "xmid": xmid, "xb1": xb[1], "szd": szd, "gd": gd, "xtm": xtm, "btm": btm, "bct": bct, "dtd": dtd, "atd": atd}
    for name in dbg:
        src = loc[name]
        shp = list(src.h.shape) if hasattr(src.h, "shape") else None
        dd = P.dram("dbg_" + name, shp, src.h.dtype, EO)
        P.dma("sync", out=dd.full(), in_=src.full())
    P.emit(final=True)
    return nc, P


def _stripe_tokens(p):
    return np.concatenate([np.arange((4 * m + p) * 512, (4 * m + p + 1) * 512) for m in range(4)])


def fused_in_maps(inp):
    L = 2
    cpsA = [a_colpack(inp, l) for l in range(L)]
    cpsD = [d2_colpack(inp, l) for l in range(L)]
    offA = dict(cpsA[0].off)
    offA["_n"] = cpsA[0].n
    offD = dict(cpsD[0].off)
    offD["_n"] = cpsD[0].n
    cstA = np.stack([c.array() for c in cpsA])
    cstD = np.stack([c.array() for c in cpsD])
    w_kp = np.zeros((L, 256, 8, 96), np.float32)
    wukv = inp["mla_w_ukv"].reshape(L, 256, 8, 128)
    w_kp[:, :, :, 0:64] = wukv[:, :, :, 0:64]
    w_v = np.ascontiguousarray(wukv[:, :, :, 64:128].reshape(L, 256, 512))
    gssm = np.ascontiguousarray(inp["ssm_norm_g"].reshape(L, 8, 128).transpose(0, 2, 1))
    rowc = np.stack([fused_rowpack(inp, l) for l in range(L)])
    msk, cm = _bc_consts()
    mats = _const_mats()
    shared = {
        "w_in": np.ascontiguousarray(inp["w_in"]), "w_uq": np.ascontiguousarray(inp["mla_w_uq"]),
        "w_kp": np.ascontiguousarray(w_kp.reshape(L, 256, 768)), "w_v": w_v,
        "w_a": np.ascontiguousarray(inp["w_br_mla"]), "w_b": np.ascontiguousarray(inp["w_br_fox"]),
        "w_c": np.ascontiguousarray(inp["w_br_ssm"]), "w_o": np.ascontiguousarray(inp["w_out"]),
        "w_up": np.ascontiguousarray(inp["ffn_w_up"]), "w_dn": np.ascontiguousarray(inp["ffn_w_down"]),
        "cstA": cstA, "cstD": cstD, "gssm": gssm, "rowc": rowc, "msk": msk, "cm": cm, "mats": mats,
    }
    in_maps = []
    for c in range(8):
        b, p = c // 4, c % 4
        xT = np.zeros((D, 4 * SW), np.float32)
        xbT = inp["x"][b].T
        for m in range(4):
            s_ = 4 * m + p
            xT[:, m * SW + 4:m * SW + 516] = xbT[:, s_ * 512:(s_ + 1) * 512]
            if s_ > 0:
                xT[:, m * SW:m * SW + 4] = xbT[:, s_ * 512 - 4:s_ * 512]
        sel = np.zeros((128, 32), np.float32)
        if p >= 1:
            sel[:, p - 1] = 1.0
        else:
            sel[:, 4] = 1.0
        for s_ in range(16):
            if s_ % 4 == p:
                sel[:, 8 + s_] = 1.0
        for jr in range(4):
            sel[:, 24 + jr] = 1.0 if jr == p else 0.0
            sel[:, 28 + jr] = NEG if jr > p else 0.0
        d = dict(shared)
        d["x0"] = np.ascontiguousarray(xT)
        d["pos"] = np.ascontiguousarray(inp["positions"][b][_stripe_tokens(p)][None, :]).astype(np.int32)
        d["sel"] = sel
        in_maps.append(d)
    return in_maps, offA, offD


def kernel_fused(**inp):
    inp = {k: np.asarray(v) for k, v in inp.items()}
    in_maps, offA, offD = fused_in_maps(inp)
    if "F" not in _PROG_CACHE:
        _PROG_CACHE["F"] = build_fused(offA, offD)[0]
    res = run_bass_kernel_spmd(_PROG_CACHE["F"], in_maps, core_ids=list(range(8))).results
    xo = np.zeros((2, S_, D), np.float32)
    for c in range(8):
        b, p = c // 4, c % 4
        xo[b, _stripe_tokens(p), :] = np.asarray(res[c]["out"]).T
    return xo
```

```python
from contextlib import ExitStack
import numpy as np
import concourse.bass as bass
import concourse.mybir as mybir

F32 = mybir.dt.float32
BF16 = mybir.dt.bfloat16
I32 = mybir.dt.int32
ALU = mybir.AluOpType
AF = mybir.ActivationFunctionType
AX = mybir.AxisListType

COMPUTE = ("tensor", "vector", "scalar", "gpsimd")
QUEUES = ("sync", "gpsimd", "scalar")
NRING = 8


class View:
    __slots__ = ("buf", "ap", "key")

    def __init__(self, buf, ap, key=None):
        self.buf = buf
        self.ap = ap
        self.key = key

    def __getitem__(self, k):
        return View(self.buf, self.ap[k], self.key)

    def re(self, s, **kw):
        return View(self.buf, self.ap.rearrange(s, **kw), self.key)

    def bc(self, shape):
        return View(self.buf, self.ap.to_broadcast(shape), self.key)

    def bitcast(self, dt):
        return View(self.buf, self.ap.bitcast(dt), self.key)

    def k(self, key):
        return View(self.buf, self.ap, key)

    def f(self, fn):
        return View(self.buf, fn(self.ap), self.key)


class Buf:
    def __init__(self, name, handle, is_dram=False):
        self.name = name
        self.h = handle
        self.is_dram = is_dram
        self.regions = {}

    def full(self):
        ap = self.h.ap() if hasattr(self.h, "ap") and callable(getattr(self.h, "ap")) else self.h[:]
        return View(self, ap)

    def __getitem__(self, k):
        return View(self, self.h[k])


class Op:
    __slots__ = ("id", "eng", "meth", "kw", "deps", "is_dma", "signaled", "sem", "val", "prewait", "eidx")


class Eng:
    def __init__(self, P, name):
        self.P = P
        self.name = name

    def __getattr__(self, meth):
        def call(*a, **kw):
            assert not a, "use kwargs"
            return self.P._record(self.name, meth, kw)
        return call


class Prog:
    def __init__(self, nc):
        self.nc = nc
        self.ops = []
        self.gstack = ExitStack()
        self.stack = ExitStack()
        self.pe = Eng(self, "tensor")
        self.dve = Eng(self, "vector")
        self.act = Eng(self, "scalar")
        self.pool = Eng(self, "gpsimd")
        self.sp = Eng(self, "sync")
        st = self.gstack
        self.csem = {e: st.enter_context(nc.semaphore(f"c_{e}")) for e in COMPUTE}
        self.rings = {q: [st.enter_context(nc.semaphore(f"d_{q}{i}")) for i in range(NRING)] for q in QUEUES}
        self.ccsem = st.enter_context(nc.semaphore("ccsem"))
        self.cccount = 0
        self.ccount = {e: 0 for e in COMPUTE}
        self.dcount = {q: 0 for q in QUEUES}
        self.waited = {e: {} for e in ("sync",) + COMPUTE}
        self.emitted = 0
        self.barrier = []
        self.stats = {}
        self.nwaits = 0

    def sb(self, name, shape, dtype):
        self.nuid = getattr(self, "nuid", 0) + 1
        name = f"{name}_s{self.nuid}"
        t = self.stack.enter_context(self.nc.sbuf_tensor(name, list(shape), dtype))
        return Buf(name, t)

    def ps(self, name, shape, dtype):
        self.nuid = getattr(self, "nuid", 0) + 1
        name = f"{name}_p{self.nuid}"
        t = self.stack.enter_context(self.nc.psum_tensor(name, list(shape), dtype))
        return Buf(name, t)

    def dram(self, name, shape, dtype, kind="Internal"):
        t = self.nc.dram_tensor(name, list(shape), dtype, kind=kind)
        return Buf(name, t, is_dram=True)

    def _record(self, eng, meth, kw):
        op = Op()
        op.id = len(self.ops)
        op.eng = eng
        op.meth = meth
        op.kw = kw
        op.is_dma = meth in ("dma_start", "dma_start_transpose", "collective_compute")
        op.signaled = False
        op.sem = None
        op.val = 0
        op.prewait = None
        deps = set()
        extra_r = kw.pop("_reads", [])
        extra_w = kw.pop("_writes", [])
        writes, reads = [], []
        for k, v in kw.items():
            vs = v if isinstance(v, (list, tuple)) else [v]
            for x in vs:
                if isinstance(x, View):
                    if k in ("out", "accum_out", "outs") or (k == "ap" and meth in ("memset", "memzero")):
                        writes.append(x)
                    else:
                        reads.append(x)
        reads += extra_r
        writes += extra_w
        for v in reads:
            self._gather(v, False, deps)
        for v in writes:
            self._gather(v, True, deps)
        for v in reads:
            self._update(v, False, op.id)
        for v in writes:
            self._update(v, True, op.id)
        deps.discard(op.id)
        op.deps = deps
        self.ops.append(op)
        return op

    def _gather(self, v, is_write, deps):
        R = v.buf.regions
        if v.key is None:
            regs = list(R.values())
        else:
            regs = [R[k] for k in (v.key, None) if k in R]
        for reg in regs:
            if reg[0] is not None:
                deps.add(reg[0])
            if is_write:
                deps.update(reg[1])

    def _update(self, v, is_write, oid):
        R = v.buf.regions
        if is_write:
            if v.key is None:
                R.clear()
            R[v.key] = [oid, []]
        else:
            R.setdefault(v.key, [None, []])[1].append(oid)

    def dma(self, q, out, in_, **kw):
        eng = {"sync": self.sp, "gpsimd": self.pool, "scalar": self.act}[q]
        return eng.dma_start(out=out, in_=in_, **kw)

    def emit(self, final=True):
        nc = self.nc
        ops = self.ops
        phase = ops[self.emitted:]
        first_id = self.emitted
        self.emitted = len(ops)
        for op in phase:
            for d in op.deps:
                dop = ops[d]
                if d < first_id:
                    continue
                if dop.eng == "tensor" and op.eng == "tensor" and not dop.is_dma and not op.is_dma:
                    continue
                dop.signaled = True
        per = {}
        for op in phase:
            per.setdefault(op.eng, []).append(op)
        for e, lst in per.items():
            for op in reversed(lst):
                if not op.is_dma:
                    op.signaled = True
                    break
        for op in phase:
            if op.meth == "collective_compute":
                self.cccount += 1
                op.sem = self.ccsem
                op.val = self.cccount
                op.signaled = True
            elif op.is_dma:
                k = self.dcount[op.eng]
                self.dcount[op.eng] += 1
                op.sem = self.rings[op.eng][k % NRING]
                op.val = 16 * (k // NRING + 1)
                if k >= NRING:
                    op.prewait = (op.sem, 16 * (k // NRING))
                op.signaled = True
            elif op.signaled:
                self.ccount[op.eng] += 1
                op.sem = self.csem[op.eng]
                op.val = self.ccount[op.eng]
        for e, v in per.items():
            self.stats[e] = self.stats.get(e, 0) + len(v)
        barrier_in = list(self.barrier)
        dcount = self.dcount
        rings = self.rings

        def dma_final_waits():
            ws = []
            for q in QUEUES:
                n = dcount[q]
                for i in range(min(n, NRING)):
                    cnt = (n - 1 - i) // NRING + 1
                    ws.append((rings[q][i], 16 * cnt))
            if self.cccount > 0:
                ws.append((self.ccsem, self.cccount))
            return ws

        def run(engname, e):
            waited = self.waited[engname]

            def do_waits(ws):
                for sem, val in ws:
                    key = id(sem)
                    if waited.get(key, 0) >= val:
                        continue
                    waited[key] = val
                    e.wait_ge(sem, val)
                    self.nwaits += 1

            do_waits(barrier_in)
            for op in per.get(engname, []):
                ws = []
                if op.prewait is not None:
                    ws.append(op.prewait)
                for d in sorted(op.deps):
                    dop = ops[d]
                    if dop.sem is None:
                        continue
                    if dop.eng == "tensor" and op.eng == "tensor" and not dop.is_dma and not op.is_dma:
                        continue
                    ws.append((dop.sem, dop.val))
                do_waits(ws)
                kw = {}
                for k, v in op.kw.items():
                    if isinstance(v, View):
                        kw[k] = v.ap
                    elif isinstance(v, (list, tuple)) and v and isinstance(v[0], View):
                        kw[k] = [x.ap for x in v]
                    else:
                        kw[k] = v
                ins = getattr(e, op.meth)(**kw)
                if op.signaled:
                    ins.then_inc(op.sem, 16 if (op.is_dma and op.meth != "collective_compute") else 1)
            if final and engname == "sync":
                do_waits(dma_final_waits())

        with nc.Block() as block:
            @block.sync
            def _(e):
                run("sync", e)

            @block.tensor
            def _(e):
                run("tensor", e)

            @block.vector
            def _(e):
                run("vector", e)

            @block.scalar
            def _(e):
                run("scalar", e)

            @block.gpsimd
            def _(e):
                run("gpsimd", e)
        bar = dma_final_waits()
        for e in COMPUTE:
            if self.ccount[e] > 0:
                bar.append((self.csem[e], self.ccount[e]))
        self.barrier = bar
        self.stats["waits"] = self.nwaits
        self.stack.close()
        self.stack = ExitStack()
        if final:
            self.gstack.close()


from concourse.bass_utils import run_bass_kernel_spmd
import ml_dtypes

NBF = ml_dtypes.bfloat16
D = 1024
T = 2048
HALO = 4
NEG = -30000.0


class ColPack:
    def __init__(self):
        self.cols = []
        self.off = {}
        self.n = 0

    def add(self, name, vec, rows=128):
        vec = np.asarray(vec, np.float32).reshape(-1)
        assert vec.size % rows == 0
        m = vec.reshape(-1, rows).T
        a = np.zeros((128, m.shape[1]), np.float32)
        a[:rows] = m
        self.off[name] = (self.n, m.shape[1], rows)
        self.cols.append(a)
        self.n += m.shape[1]

    def array(self):
        return np.ascontiguousarray(np.concatenate(self.cols, axis=1))


class Cst:
    def __init__(self, P, buf, off):
        self.buf = buf
        self.off = off

    def col(self, name, j=0, rows=None):
        o, n, r = self.off[name]
        r = rows or r
        return self.buf[0:r, o + j:o + j + 1]

    def cols(self, name):
        o, n, r = self.off[name]
        return self.buf[0:r, o:o + n]


def new_nc():
    return bass.Bass("TRN2", target_bir_lowering=False)


def load_cast(P, q, dram_view, stage_view, bf_view, cast_eng):
    P.dma(q, out=stage_view, in_=dram_view)
    cast_eng.tensor_copy(out=bf_view, in_=stage_view)


A_OFF = None


def a_colpack(inp, l):
    cp = ColPack()
    cp.add("g_mix", inp["norm_mix_g"][l])
    cp.add("g_cq", inp["mla_q_norm_g"][l])
    cp.add("g_ckv", inp["mla_kv_norm_g"][l])
    cp.add("g_q", inp["mla_q_gain"][l], 96)
    cp.add("g_k", inp["mla_k_gain"][l], 96)
    cp.add("g_fq", inp["fox_q_gain"][l], 64)
    cp.add("g_fk", inp["fox_k_gain"][l], 64)
    cp.add("b_f", inp["fox_b_f"][l], 8)
    cw = inp["ssm_conv_w"][l]
    for k in range(4):
        cp.add(f"cw{k}", cw[k])
    cp.add("cb", inp["ssm_conv_b"][l])
    cp.add("dt_b", inp["ssm_dt_bias"][l], 16)
    cp.add("A_log", inp["ssm_A_log"][l], 16)
    cp.add("b_gate", inp["b_gate"][l])
    inv = 1.0 / (10000.0 ** (np.arange(0, 32, 2, dtype=np.float32) / 32.0))
    invf = np.zeros(96, np.float32)
    invf[64:80] = inv
    invf[80:96] = inv
    cp.add("invf", invf, 96)
    return cp


def build_A(off):
    nc = new_nc()
    P = Prog(nc)
    TT = T + HALO
    NT = T // 512
    EI, EO = "ExternalInput", "ExternalOutput"
    xT = P.dram("xT", [D, TT], F32, EI)
    pos = P.dram("pos", [1, T], I32, EI)
    w_in = P.dram("w_in", [D, 7864], F32, EI)
    w_uq = P.dram("w_uq", [384, 768], F32, EI)
    w_kp = P.dram("w_kp", [256, 768], F32, EI)
    w_v = P.dram("w_v", [256, 512], F32, EI)
    cst_d = P.dram("cst", [128, off["_n"]], F32, EI)
    mats = P.dram("mats", [128, 2 * 96], F32, EI)
    o_qm = P.dram("o_qm", [8, 96, T], BF16, EO)
    o_km = P.dram("o_km", [8, 96, T], BF16, EO)
    o_vm = P.dram("o_vm", [512, T], BF16, EO)
    o_qf = P.dram("o_qf", [8, 64, T], BF16, EO)
    o_kf = P.dram("o_kf", [8, 64, T], BF16, EO)
    o_vf = P.dram("o_vf", [512, T], BF16, EO)
    o_lf = P.dram("o_lf", [8, T], F32, EO)
    o_sz = P.dram("o_sz", [1024, T], BF16, EO)
    o_xbc = P.dram("o_xbc", [1536, T], BF16, EO)
    o_dt = P.dram("o_dt", [16, T], F32, EO)
    o_a = P.dram("o_a", [16, T], F32, EO)
    o_g = P.dram("o_g", [3072, T], BF16, EO)

    cstb = P.sb("cstb", [128, off["_n"]], F32)
    C = Cst(P, cstb, off)
    P.dma("sync", out=cstb.full(), in_=cst_d.full())
    matf = P.sb("matf", [128, 192], F32)
    matb = P.sb("matb", [128, 192], BF16)
    P.dma("sync", out=matf.full(), in_=mats.full())
    P.dve.tensor_copy(out=matb.full(), in_=matf.full())
    prh = matb[0:96, 0:96]
    sel = matb[0:32, 96:192]
    ones = P.sb("ones", [128, 128], F32)
    P.dve.memset(ap=ones.full(), constant=1.0)
    eps = P.sb("eps", [128, 1], F32)
    P.dve.memset(ap=eps.full(), constant=1e-6)
    one1 = P.sb("one1", [128, 1], F32)
    P.dve.memset(ap=one1.full(), constant=1.0)
    nbf = P.sb("nbf", [8, 1], F32)
    P.dve.tensor_scalar(out=nbf.full(), in0=C.col("b_f"), scalar1=-1.0, scalar2=None, op0=ALU.mult)
    Aneg = P.sb("Aneg", [16, 1], F32)
    P.act.activation(out=Aneg.full(), in_=C.col("A_log"), func=AF.Exp)
    P.dve.tensor_scalar(out=Aneg.full(), in0=Aneg.full(), scalar1=-1.0, scalar2=None, op0=ALU.mult)

    pb = [P.ps(f"pb{i}", [128, 512], F32) for i in range(8)]
    pbi = {}

    def nxt_ps(lo=0, hi=4):
        i = pbi.get(lo, 0)
        pbi[lo] = (i + 1) % (hi - lo)
        return pb[lo + i]

    Ctab = P.sb("Ctab", [96, T], F32)
    Stab = P.sb("Stab", [96, T], F32)
    posi = P.sb("posi", [96, 512], I32)
    posf = P.sb("posf", [96, 512], F32)
    rr_tmp = P.sb("rr_tmp", [96, 512], F32)
    rr_i = P.sb("rr_i", [96, 512], I32)
    rr_m = P.sb("rr_m", [96, 512], F32)

    def sin_table(outv, phase):
        P.dve.tensor_scalar(out=rr_tmp.full(), in0=posf.full(), scalar1=C.col("invf"), scalar2=phase,
                            op0=ALU.mult, op1=ALU.add)
        P.dve.tensor_scalar(out=rr_m.full(), in0=rr_tmp.full(), scalar1=1.0 / (2 * np.pi), scalar2=None, op0=ALU.mult)
        P.dve.tensor_copy(out=rr_i.full(), in_=rr_m.full())
        P.dve.tensor_copy(out=rr_m.full(), in_=rr_i.full())
        P.dve.scalar_tensor_tensor(out=rr_tmp.full(), in0=rr_m.full(), scalar=-2 * np.pi, in1=rr_tmp.full(),
                                   op0=ALU.mult, op1=ALU.add)
        P.dve.tensor_scalar(out=rr_m.full(), in0=rr_tmp.full(), scalar1=np.pi, scalar2=-2 * np.pi, op0=ALU.is_gt, op1=ALU.mult)
        P.dve.tensor_tensor(out=rr_tmp.full(), in0=rr_tmp.full(), in1=rr_m.full(), op=ALU.add)
        P.dve.tensor_scalar(out=rr_m.full(), in0=rr_tmp.full(), scalar1=-np.pi, scalar2=2 * np.pi, op0=ALU.is_lt, op1=ALU.mult)
        P.dve.tensor_tensor(out=rr_tmp.full(), in0=rr_tmp.full(), in1=rr_m.full(), op=ALU.add)
        P.act.activation(out=outv, in_=rr_tmp.full(), func=AF.Sin)

    for i in range(NT):
        P.dma("sync", out=posi.full(), in_=pos[:, i * 512:(i + 1) * 512].f(lambda a: a.partition_broadcast(96)))
        P.dve.tensor_copy(out=posf.full(), in_=posi.full())
        sin_table(Stab[:, i * 512:(i + 1) * 512], 0.0)
        sin_table(Ctab[:, i * 512:(i + 1) * 512], np.pi / 2)
    P.dve.memset(ap=Stab[0:64, :], constant=0.0)
    P.dve.memset(ap=Ctab[0:64, :], constant=1.0)

    hn = P.sb("hn", [128, 8, TT], BF16)
    xst = P.sb("xst", [128, 8, 512], F32)
    sq = P.sb("sq", [128, 512], F32)
    rstd = P.sb("rstd", [128, 512], F32)
    xTv = xT.full().re("(kc p) n -> p kc n", p=128)

    def rstd_from(ps_view, n_feat, rows, width, rstd_view):
        P.act.activation(out=rstd_view, in_=ps_view, func=AF.Sqrt, bias=eps[0:rows, 0:1], scale=1.0 / n_feat)
        P.dve.reciprocal(out=rstd_view, in_=rstd_view)

    tiles = [(0, HALO)] + [(HALO + i * 512, 512) for i in range(NT)]
    for (c0, w) in tiles:
        P.dma("sync", out=xst[:, :, 0:w], in_=xTv[:, :, c0:c0 + w])
        ps = nxt_ps(4, 6)
        for kc in range(8):
            P.act.activation(out=sq[:, 0:w], in_=xst[:, kc, 0:w], func=AF.Square)
            P.pe.matmul(out=ps[:, 0:w], lhsT=ones.full(), rhs=sq[:, 0:w], start=(kc == 0), stop=(kc == 7))
        rstd_from(ps[:, 0:w], 1024.0, 128, w, rstd[:, 0:w])
        for kc in range(8):
            P.dve.scalar_tensor_tensor(out=hn[:, kc, c0:c0 + w], in0=xst[:, kc, 0:w], scalar=C.col("g_mix", kc),
                                       in1=rstd[:, 0:w], op0=ALU.mult, op1=ALU.mult)

    wst = [P.sb(f"wst{i}", [128, 8, 512], F32) for i in range(2)]
    wbf = [P.sb(f"wbf{i}", [128, 8, 512], BF16) for i in range(2)]
    wcnt = [0]
    w_inv = w_in.full().re("(kc p) n -> p kc n", p=128)

    def load_w(c0, ncols):
        i = wcnt[0] % 2
        wcnt[0] += 1
        q = "sync" if i == 0 else "gpsimd"
        P.dma(q, out=wst[i][:, :, 0:ncols], in_=w_inv[:, :, c0:c0 + ncols])
        P.pool.tensor_copy(out=wbf[i][:, :, 0:ncols], in_=wst[i][:, :, 0:ncols])
        return wbf[i]

    def proj(wb, wc0, m, c0, w, ps_view):
        for kc in range(8):
            P.pe.matmul(out=ps_view, lhsT=wb[:, kc, wc0:wc0 + m], rhs=hn[:, kc, c0:c0 + w],
                        start=(kc == 0), stop=(kc == 7))

    ostg_cnt = [0]
    ostg = [P.sb(f"ostg{i}", [128, 512], BF16) for i in range(4)]

    def next_ostg():
        i = ostg_cnt[0] % 4
        ostg_cnt[0] += 1
        return ostg[i]

    def out_dma(dst_view, src_view):
        q = "sync" if ostg_cnt[0] % 2 else "gpsimd"
        P.dma(q, out=dst_view, in_=src_view)

    hraw = P.sb("hraw", [96, 512], F32)
    hsq = P.sb("hsq", [96, 512], F32)
    hrs = P.sb("hrs", [96, 512], F32)
    hnf = P.sb("hnf", [96, 512], F32)
    hnb = P.sb("hnb", [96, 512], BF16)
    ht1 = P.sb("ht1", [96, 512], F32)
    ht2 = P.sb("ht2", [96, 512], F32)

    def headnorm(ps_view, d, gain_col, rope, tok0, dst_view):
        P.act.activation(out=hsq[0:d, :], in_=ps_view, func=AF.Square)
        P.act.copy(out=hraw[0:d, :], in_=ps_view)
        ps2 = nxt_ps(4, 6)
        P.pe.matmul(out=ps2[0:d, :], lhsT=ones[0:d, 0:d], rhs=hsq[0:d, :], start=True, stop=True)
        rstd_from(ps2[0:d, :], float(d), d, 512, hrs[0:d, :])
        og = next_ostg()
        if not rope:
            P.dve.scalar_tensor_tensor(out=og[0:d, :], in0=hraw[0:d, :], scalar=gain_col, in1=hrs[0:d, :],
                                       op0=ALU.mult, op1=ALU.mult)
        else:
            P.dve.scalar_tensor_tensor(out=hnf[0:d, :], in0=hraw[0:d, :], scalar=gain_col, in1=hrs[0:d, :],
                                       op0=ALU.mult, op1=ALU.mult)
            P.act.copy(out=hnb[0:d, :], in_=hnf[0:d, :])
            ps3 = nxt_ps(6, 8)
            P.pe.matmul(out=ps3[0:d, :], lhsT=prh, rhs=hnb[0:d, :], start=True, stop=True)
            P.dve.tensor_tensor(out=ht1[0:d, :], in0=hnf[0:d, :], in1=Ctab[0:d, tok0:tok0 + 512], op=ALU.mult)
            P.dve.tensor_tensor(out=ht2[0:d, :], in0=ps3[0:d, :], in1=Stab[0:d, tok0:tok0 + 512], op=ALU.mult)
            P.pool.tensor_tensor(out=og[0:d, :], in0=ht1[0:d, :], in1=ht2[0:d, :], op=ALU.add)
        out_dma(dst_view, og[0:d, :])

    lat = P.sb("lat", [128, 3, 512], F32)
    latn = P.sb("latn", [128, 3, 512], BF16)

    def latent_norm(ps_list, gname):
        nch = len(ps_list)
        ps2 = nxt_ps(4, 6)
        for i, psv in enumerate(ps_list):
            P.act.activation(out=sq.full(), in_=psv, func=AF.Square)
            P.act.copy(out=lat[:, i, :], in_=psv)
            P.pe.matmul(out=ps2.full(), lhsT=ones.full(), rhs=sq.full(), start=(i == 0), stop=(i == nch - 1))
        rstd_from(ps2.full(), 128.0 * nch, 128, 512, rstd.full())
        for i in range(nch):
            P.dve.scalar_tensor_tensor(out=latn[:, i, :], in0=lat[:, i, :], scalar=C.col(gname, i), in1=rstd.full(),
                                       op0=ALU.mult, op1=ALU.mult)

    def small_w(name, dram, kc_n, ncols, i):
        stg = wst[i].full().re("p a b -> p (a b)")[:, 0:kc_n * ncols].re("p (a b) -> p a b", a=kc_n)
        bfb = P.sb(name, [128, kc_n, ncols], BF16)
        P.dma("gpsimd", out=stg, in_=dram.full().re("(kc p) n -> p kc n", p=128))
        P.pool.tensor_copy(out=bfb.full(), in_=stg)
        return bfb

    uqb = small_w("uqb", w_uq, 3, 768, 0)
    kpb = small_w("kpb", w_kp, 2, 768, 1)
    wvb = small_w("wvb", w_v, 2, 512, 0)
    main = tiles[1:]
    wb = load_w(0, 384)
    for ti, (c0, w) in enumerate(main):
        pss = []
        for ch in range(3):
            ps = nxt_ps(0, 4)
            proj(wb, ch * 128, 128, c0, 512, ps.full())
            pss.append(ps.full())
        latent_norm(pss, "g_cq")
        for h in range(8):
            ps = nxt_ps(0, 4)
            for kc in range(3):
                P.pe.matmul(out=ps[0:96, :], lhsT=uqb[:, kc, h * 96:(h + 1) * 96], rhs=latn[:, kc, :],
                            start=(kc == 0), stop=(kc == 2))
            headnorm(ps[0:96, :], 96, C.col("g_q"), True, ti * 512, o_qm[h, :, ti * 512:(ti + 1) * 512])
    wb = load_w(384, 288)
    krb = P.sb("krb", [32, 512], BF16)
    for ti, (c0, w) in enumerate(main):
        pss = []
        for ch in range(2):
            ps = nxt_ps(0, 4)
            proj(wb, ch * 128, 128, c0, 512, ps.full())
            pss.append(ps.full())
        ps = nxt_ps(0, 4)
        proj(wb, 256, 32, c0, 512, ps[0:32, :])
        P.act.copy(out=krb.full(), in_=ps[0:32, :])
        latent_norm(pss, "g_ckv")
        for h in range(8):
            ps = nxt_ps(0, 4)
            for kc in range(2):
                P.pe.matmul(out=ps[0:96, :], lhsT=kpb[:, kc, h * 96:(h + 1) * 96], rhs=latn[:, kc, :],
                            start=(kc == 0), stop=False)
            P.pe.matmul(out=ps[0:96, :], lhsT=sel, rhs=krb.full(), start=False, stop=True)
            headnorm(ps[0:96, :], 96, C.col("g_k"), True, ti * 512, o_km[h, :, ti * 512:(ti + 1) * 512])
        for ch in range(4):
            ps = nxt_ps(0, 4)
            for kc in range(2):
                P.pe.matmul(out=ps.full(), lhsT=wvb[:, kc, ch * 128:(ch + 1) * 128], rhs=latn[:, kc, :],
                            start=(kc == 0), stop=(kc == 1))
            og = next_ostg()
            P.act.copy(out=og.full(), in_=ps.full())
            out_dma(o_vm[ch * 128:(ch + 1) * 128, ti * 512:(ti + 1) * 512], og.full())
    for (base, gname, dst) in ((672, "g_fq", o_qf), (672 + 512, "g_fk", o_kf)):
        wb = load_w(base, 512)
        for ti, (c0, w) in enumerate(main):
            for h in range(8):
                ps = nxt_ps(0, 4)
                proj(wb, h * 64, 64, c0, 512, ps[0:64, :])
                headnorm(ps[0:64, :], 64, C.col(gname), False, ti * 512, dst[h, :, ti * 512:(ti + 1) * 512])
    def plain_group(base, ncols, func, bias_name, dst, dst_row0):
        wb = load_w(base, ncols)
        for ti, (c0, w) in enumerate(main):
            for ch in range(ncols // 128):
                ps = nxt_ps(0, 4)
                proj(wb, ch * 128, 128, c0, 512, ps.full())
                og = next_ostg()
                if bias_name is None:
                    P.act.activation(out=og.full(), in_=ps.full(), func=func)
                else:
                    P.act.activation(out=og.full(), in_=ps.full(), func=func,
                                     bias=C.col(bias_name, (dst_row0 // 128) + ch))
                out_dma(dst[dst_row0 + ch * 128:dst_row0 + (ch + 1) * 128, ti * 512:(ti + 1) * 512], og.full())

    plain_group(672 + 1024, 512, AF.Copy, None, o_vf, 0)
    FB = 672 + 1536
    SB = 672 + 1544
    wf = load_w(FB, 8)
    lf1 = P.sb("lf1", [16, 512], F32)
    lf2 = P.sb("lf2", [16, 512], F32)
    for ti, (c0, w) in enumerate(main):
        ps = nxt_ps(0, 4)
        proj(wf, 0, 8, c0, 512, ps[0:8, :])
        P.act.activation(out=lf1[0:8, :], in_=ps[0:8, :], func=AF.Exp, bias=nbf[0:8, 0:1], scale=-1.0)
        P.act.activation(out=lf1[0:8, :], in_=lf1[0:8, :], func=AF.Ln, bias=one1[0:8, 0:1], scale=1.0)
        P.dve.tensor_scalar(out=lf2[0:8, :], in0=lf1[0:8, :], scalar1=-1.0, scalar2=None, op0=ALU.mult)
        P.dma("sync", out=o_lf[:, ti * 512:(ti + 1) * 512], in_=lf2[0:8, :])
    wd = load_w(SB + 1024 + 1536, 16)
    dt1 = P.sb("dt1", [16, 512], F32)
    dt2 = P.sb("dt2", [16, 512], F32)
    for ti, (c0, w) in enumerate(main):
        ps = nxt_ps(0, 4)
        proj(wd, 0, 16, c0, 512, ps[0:16, :])
        P.act.activation(out=dt1.full(), in_=ps[0:16, :], func=AF.Exp, bias=C.col("dt_b"), scale=1.0)
        P.act.activation(out=dt1.full(), in_=dt1.full(), func=AF.Ln, bias=one1[0:16, 0:1], scale=1.0)
        P.dma("sync", out=o_dt[:, ti * 512:(ti + 1) * 512], in_=dt1.full())
        P.dve.tensor_scalar(out=dt2.full(), in0=dt1.full(), scalar1=Aneg[:, 0:1], scalar2=None, op0=ALU.mult)
        P.dma("sync", out=o_a[:, ti * 512:(ti + 1) * 512], in_=dt2.full())
    for blk in range(2):
        plain_group(SB + blk * 512, 512, AF.Silu, None, o_sz, blk * 512)
    upre = P.sb("upre", [128, 516], F32)
    carry = P.sb("carry", [128, 12, 4], F32)
    acc = [P.sb(f"acc{i}", [128, 512], F32) for i in range(2)]
    for blk in range(3):
        wb = load_w(SB + 1024 + blk * 512, 512)
        for ch in range(4):
            cg = blk * 4 + ch
            ps = nxt_ps(0, 4)
            proj(wb, ch * 128, 128, 0, HALO, ps[:, 0:HALO])
            P.act.copy(out=carry[:, cg, :], in_=ps[:, 0:HALO])
        for ti, (c0, w) in enumerate(main):
            for ch in range(4):
                cg = blk * 4 + ch
                ps = nxt_ps(0, 4)
                proj(wb, ch * 128, 128, c0, 512, ps.full())
                P.act.copy(out=upre[:, 4:516], in_=ps.full())
                P.dve.tensor_copy(out=upre[:, 0:4], in_=carry[:, cg, :])
                P.pool.tensor_copy(out=carry[:, cg, :], in_=upre[:, 512:516])
                a0 = acc[0]
                P.dve.tensor_scalar(out=a0.full(), in0=upre[:, 4:516], scalar1=C.col("cw3", cg), scalar2=C.col("cb", cg),
                                    op0=ALU.mult, op1=ALU.add)
                for k in range(3):
                    P.dve.scalar_tensor_tensor(out=a0.full(), in0=upre[:, 1 + k:513 + k], scalar=C.col(f"cw{k}", cg),
                                               in1=a0.full(), op0=ALU.mult, op1=ALU.add)
                og = next_ostg()
                P.act.activation(out=og.full(), in_=a0.full(), func=AF.Silu)
                out_dma(o_xbc[cg * 128:(cg + 1) * 128, ti * 512:(ti + 1) * 512], og.full())
    GB = SB + 2576
    for blk in range(6):
        plain_group(GB + blk * 512, 512, AF.Sigmoid, "b_gate", o_g, blk * 512)
    P.emit()
    return nc, P


def _bf(a):
    return np.asarray(a).astype(np.float32)


_PROG_CACHE = {}


def _const_mats():
    m = np.zeros((128, 192), np.float32)
    for i in range(16):
        m[80 + i, 64 + i] = -1.0
        m[64 + i, 80 + i] = 1.0
    for i in range(32):
        m[i, 96 + 64 + i] = 1.0
    return m


def run_A(inp, l, x_full, pos_full):
    cp = a_colpack(inp, l)
    off = dict(cp.off)
    off["_n"] = cp.n
    if "A" not in _PROG_CACHE:
        _PROG_CACHE["A"] = build_A(off)[0]
    nc = _PROG_CACHE["A"]
    cst = cp.array()
    wukv = inp["mla_w_ukv"][l].reshape(256, 8, 128)
    w_kp = np.zeros((256, 8, 96), np.float32)
    w_kp[:, :, 0:64] = wukv[:, :, 0:64]
    w_v = np.ascontiguousarray(wukv[:, :, 64:128].reshape(256, 512))
    mats = _const_mats()
    xf = x_full.reshape(16384, D)
    in_maps = []
    for c in range(8):
        t0 = c * T
        xt = np.zeros((D, T + HALO), np.float32)
        xt[:, HALO:] = xf[t0:t0 + T].T
        if c % 4 != 0:
            xt[:, 0:HALO] = xf[t0 - HALO:t0].T
        in_maps.append({
            "xT": np.ascontiguousarray(xt),
            "pos": np.ascontiguousarray(pos_full.reshape(1, 16384)[:, t0:t0 + T]).astype(np.int32),
            "w_in": np.ascontiguousarray(inp["w_in"][l]),
            "w_uq": np.ascontiguousarray(inp["mla_w_uq"][l]),
            "w_kp": np.ascontiguousarray(w_kp.reshape(256, 768)),
            "w_v": w_v, "cst": cst, "mats": mats,
        })
    res = run_bass_kernel_spmd(nc, in_maps, core_ids=list(range(8)))
    return res.results


S_ = 8192
NKT = S_ // 128
NQT = S_ // 512


def build_BC():
    nc = new_nc()
    P = Prog(nc)
    EI, EO = "ExternalInput", "ExternalOutput"
    qm = P.dram("qm", [2, 96, S_], BF16, EI)
    km = P.dram("km", [2, 96, S_], BF16, EI)
    vm = P.dram("vm", [2, 128, NKT, 64], BF16, EI)
    qf = P.dram("qf", [2, 64, S_], BF16, EI)
    kf = P.dram("kf", [2, 64, S_], BF16, EI)
    vf = P.dram("vf", [2, 128, NKT, 64], BF16, EI)
    lf = P.dram("lf", [2, 128, NKT], F32, EI)
    msk = P.dram("msk", [128, 8, 512], F32, EI)
    cm = P.dram("cm", [128, 4, 128], F32, EI)
    x_tm = P.dram("x_tm", [128, NKT, 256], BF16, EI)
    B_tm = P.dram("B_tm", [128, NKT, 128], BF16, EI)
    BT = P.dram("BT", [128, S_], BF16, EI)
    CT = P.dram("CT", [128, S_], BF16, EI)
    dt_tm = P.dram("dt_tm", [128, NKT, 4], F32, EI)
    a_tm = P.dram("a_tm", [128, NKT, 4], F32, EI)
    Dv = P.dram("Dv", [128, 4], F32, EI)
    o_m = P.dram("o_m", [2, 64, S_], BF16, EO)
    o_f = P.dram("o_f", [2, 64, S_], BF16, EO)
    o_y = P.dram("o_y", [128, NKT, 256], F32, EO)
    fsc = P.dram("fsc", [3, S_], BF16)

    cmb = P.sb("cmb", [128, 4, 128], F32)
    P.dma("sync", out=cmb.full(), in_=cm.full())
    tri, trimask, ident, ones = cmb[:, 0, :], cmb[:, 1, :], cmb[:, 2, :], cmb[:, 3, :]
    mskb = P.sb("mskb", [128, 8, 512], F32)
    P.dma("gpsimd", out=mskb.full(), in_=msk.full())
    zero = P.sb("zero", [128, 1], F32)
    P.dve.memset(ap=zero.full(), constant=0.0)

    pb = [P.ps(f"pb{i}", [128, 512], F32) for i in range(8)]
    K_sb = P.sb("K_sb", [128, S_], BF16)
    Q_sb = P.sb("Q_sb", [128, S_], BF16)
    V_sb = P.sb("V_sb", [128, NKT, 128], BF16)
    P.dve.memset(ap=V_sb[:, :, 64:128], constant=1.0)
    pt = [P.sb(f"pt{i}", [128, 512], BF16) for i in range(3)]
    mt = [P.sb(f"mt{i}", [128, 512], F32) for i in range(2)]
    rl = P.sb("rl", [128, 512], F32)
    rl2 = P.sb("rl2", [64, 512], F32)
    ot = [P.sb(f"ot{i}", [64, 512], BF16) for i in range(2)]
    negF = P.sb("negF", [128, NKT], F32)

    cnt = [0, 0, 0]

    def attention(dk, scale, mask0, bias_fn, out_dram_h):
        for qt in range(NQT):
            oacc = pb[3 + qt % 2]
            nk = 4 * qt + 4
            for kt in range(nk):
                i3 = cnt[0] % 3
                cnt[0] += 1
                ps = pb[i3]
                P.pe.matmul(out=ps.full(), lhsT=K_sb[0:dk, kt * 128:(kt + 1) * 128],
                            rhs=Q_sb[0:dk, qt * 512:(qt + 1) * 512], start=True, stop=True)
                if kt >= 4 * qt:
                    m = mt[cnt[1] % 2]
                    cnt[1] += 1
                    P.dve.tensor_tensor(out=m.full(), in0=ps.full(), in1=mskb[:, mask0 + kt - 4 * qt, :], op=ALU.add)
                    src = m.full()
                else:
                    src = ps.full()
                P.act.activation(out=pt[i3].full(), in_=src, func=AF.Exp, scale=scale, bias=bias_fn(kt))
                P.pe.matmul(out=oacc.full(), lhsT=V_sb[:, kt, :], rhs=pt[i3].full(), start=(kt == 0), stop=(kt == nk - 1))
            P.dve.reciprocal(out=rl[64:128, :], in_=oacc[64:128, :])
            P.dve.tensor_copy(out=rl2.full(), in_=rl[64:128, :])
            o = ot[qt % 2]
            P.dve.tensor_tensor(out=o.full(), in0=oacc[0:64, :], in1=rl2.full(), op=ALU.mult)
            P.dma("sync", out=out_dram_h[:, qt * 512:(qt + 1) * 512], in_=o.full())

    for h in range(2):
        P.dma("sync", out=K_sb[0:96, :], in_=km[h])
        P.dma("gpsimd", out=Q_sb[0:96, :], in_=qm[h])
        P.dma("sync", out=V_sb[:, :, 0:64], in_=vm[h])
        attention(96, 96.0 ** -0.5, 0, lambda kt: zero[:, 0:1], o_m[h])

    lfs = P.sb("lfs", [128, NKT], F32)
    wi = P.sb("wi", [128, NKT], F32)
    sc = [P.sb(f"sc{i}", [128, NKT], F32) for i in range(2)]
    Ff = P.sb("Ff", [128, NKT], F32)
    FT = P.sb("FT", [64, 128], F32)
    r1 = P.sb("r1", [64, 128], F32)
    fh = [P.sb(f"fh{i}", [64, 128], BF16) for i in range(3)]
    for h in range(2):
        P.dma("sync", out=lfs.full(), in_=lf[h])
        ps = pb[5]
        P.pe.matmul(out=ps[:, 0:NKT], lhsT=tri, rhs=lfs.full(), start=True, stop=True)
        P.act.copy(out=wi.full(), in_=ps[:, 0:NKT])
        ps = pb[6]
        P.pe.matmul(out=ps[:, 0:NKT], lhsT=ones, rhs=lfs.full(), start=True, stop=True)
        P.act.copy(out=sc[0].full(), in_=ps[:, 0:NKT])
        P.dve.tensor_tensor(out=wi.full(), in0=wi.full(), in1=sc[0].full(), op=ALU.subtract)
        cur = 0
        d = 1
        while d < NKT:
            nx = 1 - cur
            P.dve.tensor_copy(out=sc[nx][:, 0:d], in_=sc[cur][:, 0:d])
            P.dve.tensor_tensor(out=sc[nx][:, d:NKT], in0=sc[cur][:, d:NKT], in1=sc[cur][:, 0:NKT - d], op=ALU.add)
            cur = nx
            d *= 2
        P.dve.tensor_tensor(out=Ff.full(), in0=wi.full(), in1=sc[cur].full(), op=ALU.add)
        P.dve.tensor_scalar(out=negF.full(), in0=Ff.full(), scalar1=-1.0, scalar2=None, op0=ALU.mult)
        ps = pb[7]
        P.pe.transpose(out=ps[0:64, 0:128], in_=Ff.full(), identity=ident)
        P.act.copy(out=FT.full(), in_=ps[0:64, 0:128])
        P.dve.tensor_copy(out=fh[0].full(), in_=FT.full())
        P.dve.tensor_tensor(out=r1.full(), in0=FT.full(), in1=fh[0].full(), op=ALU.subtract)
        P.dve.tensor_copy(out=fh[1].full(), in_=r1.full())
        P.dve.tensor_tensor(out=r1.full(), in0=r1.full(), in1=fh[1].full(), op=ALU.subtract)
        P.dve.tensor_copy(out=fh[2].full(), in_=r1.full())
        for r in range(3):
            P.dma("sync", out=fsc[r].re("(kt p) -> kt p", p=128), in_=fh[r].full())
        P.dma("sync", out=K_sb[0:64, :], in_=kf[h])
        P.dve.memset(ap=K_sb[64:67, :], constant=8.0)
        P.dma("gpsimd", out=Q_sb[0:64, :], in_=qf[h])
        P.dma("gpsimd", out=Q_sb[64:67, :], in_=fsc.full())
        P.dma("sync", out=V_sb[:, :, 0:64], in_=vf[h])
        attention(67, 0.125, 4, lambda kt: negF[:, kt:kt + 1], o_f[h])

    a_sb = P.sb("a_sb", [128, NKT, 4], F32)
    dt_sb = P.sb("dt_sb", [128, NKT, 4], F32)
    Dsb = P.sb("Dsb", [128, 4], F32)
    P.dma("sync", out=a_sb.full(), in_=a_tm.full())
    P.dma("sync", out=dt_sb.full(), in_=dt_tm.full())
    P.dma("sync", out=Dsb.full(), in_=Dv.full())
    BTs = K_sb
    CTs = Q_sb
    P.dma("sync", out=BTs.full(), in_=BT.full())
    P.dma("gpsimd", out=CTs.full(), in_=CT.full())
    Acum = P.sb("Acum", [128, NKT, 4], F32)
    nAcum = P.sb("nAcum", [128, NKT, 4], F32)
    Atot = P.sb("Atot", [128, NKT, 4], F32)
    eA = P.sb("eA", [128, NKT, 4], F32)
    wdec = P.sb("wdec", [128, NKT, 4], F32)
    eAtot = P.sb("eAtot", [128, NKT, 4], F32)
    fl = lambda b: b.full().re("p c h -> p (c h)")
    ps = pb[0]
    P.pe.matmul(out=ps[:, 0:256], lhsT=tri, rhs=fl(a_sb), start=True, stop=True)
    P.act.copy(out=fl(Acum), in_=ps[:, 0:256])
    ps = pb[1]
    P.pe.matmul(out=ps[:, 0:256], lhsT=ones, rhs=fl(a_sb), start=True, stop=True)
    P.act.copy(out=fl(Atot), in_=ps[:, 0:256])
    P.dve.tensor_scalar(out=fl(nAcum), in0=fl(Acum), scalar1=-1.0, scalar2=None, op0=ALU.mult)
    P.act.activation(out=fl(eA), in_=fl(Acum), func=AF.Exp)
    P.act.activation(out=fl(eAtot), in_=fl(Atot), func=AF.Exp)
    P.dve.tensor_tensor(out=fl(wdec), in0=fl(Atot), in1=fl(Acum), op=ALU.subtract)
    P.act.activation(out=fl(wdec), in_=fl(wdec), func=AF.Exp)

    Hs = P.sb("Hs", [128, 256], F32)
    Hb = P.sb("Hb", [128, 256], BF16)
    P.dve.memset(ap=Hs.full(), constant=0.0)
    P.dve.memset(ap=Hb.full(), constant=0.0)
    xc = [P.sb(f"xc{i}", [128, 256], BF16) for i in range(2)]
    Bc = [P.sb(f"Bc{i}", [128, 128], BF16) for i in range(2)]
    cb = P.sb("cb", [128, 128], F32)
    xdt = P.sb("xdt", [128, 256], BF16)
    xdts = P.sb("xdts", [128, 256], BF16)
    at = [P.sb(f"at{i}", [128, 128], F32) for i in range(2)]
    tm = [P.sb(f"tm{i}", [128, 128], F32) for i in range(2)]
    dec = [P.sb(f"dec{i}", [128, 128], F32) for i in range(2)]
    MT = [P.sb(f"MT{i}", [128, 128], BF16) for i in range(2)]
    t1 = P.sb("t1", [128, 256], F32)
    t3 = P.sb("t3", [128, 256], F32)
    yo = [P.sb(f"yo{i}", [128, 256], F32) for i in range(2)]
    v3 = lambda v: v.re("p (h d) -> p h d", h=4)
    bc3 = lambda v: v.f(lambda a: a.unsqueeze(2).to_broadcast([128, 4, 64]))
    for c in range(NKT):
        x_c = xc[c % 2]
        B_c = Bc[c % 2]
        P.dma("sync", out=x_c.full(), in_=x_tm[:, c, :])
        P.dma("gpsimd", out=B_c.full(), in_=B_tm[:, c, :])
        BT_c = BTs[:, c * 128:(c + 1) * 128]
        CT_c = CTs[:, c * 128:(c + 1) * 128]
        ps_cb = pb[0]
        P.pe.matmul(out=ps_cb[:, 0:128], lhsT=BT_c, rhs=CT_c, start=True, stop=True)
        P.act.copy(out=cb.full(), in_=ps_cb[:, 0:128])
        P.dve.tensor_tensor(out=v3(xdt.full()), in0=v3(x_c.full()), in1=bc3(dt_sb[:, c, :]), op=ALU.mult)
        P.pool.tensor_tensor(out=v3(xdts.full()), in0=v3(xdt.full()), in1=bc3(wdec[:, c, :]), op=ALU.mult)
        ps_off = pb[1]
        P.pe.matmul(out=ps_off[:, 0:256], lhsT=CT_c, rhs=Hb.full(), start=True, stop=True)
        ps_y = pb[2]
        for h in range(4):
            i2 = h % 2
            P.dve.tensor_scalar(out=at[i2].full(), in0=tri, scalar1=a_sb[:, c, h:h + 1], scalar2=None, op0=ALU.mult)
            ps_A = pb[3 + i2]
            P.pe.matmul(out=ps_A[:, 0:128], lhsT=ones, rhs=at[i2].full(), start=True, stop=True)
            P.dve.tensor_tensor(out=tm[i2].full(), in0=ps_A[:, 0:128], in1=trimask, op=ALU.add)
            P.act.activation(out=dec[i2].full(), in_=tm[i2].full(), func=AF.Exp, bias=nAcum[:, c, h:h + 1], scale=1.0)
            P.pool.tensor_tensor(out=MT[i2].full(), in0=cb.full(), in1=dec[i2].full(), op=ALU.mult)
            P.pe.matmul(out=ps_y[:, h * 64:(h + 1) * 64], lhsT=MT[i2].full(), rhs=xdt[:, h * 64:(h + 1) * 64],
                        start=True, stop=True)
        P.dve.tensor_tensor(out=v3(t1.full()), in0=v3(ps_off[:, 0:256]), in1=bc3(eA[:, c, :]), op=ALU.mult)
        P.dve.tensor_tensor(out=t1.full(), in0=t1.full(), in1=ps_y[:, 0:256], op=ALU.add)
        P.pool.tensor_tensor(out=v3(t3.full()), in0=v3(x_c.full()), in1=bc3(Dsb.full()), op=ALU.mult)
        y_ = yo[c % 2]
        P.pool.tensor_tensor(out=y_.full(), in0=t1.full(), in1=t3.full(), op=ALU.add)
        P.dma("sync", out=o_y[:, c, :], in_=y_.full())
        ps_h = pb[5]
        P.pe.matmul(out=ps_h[:, 0:256], lhsT=B_c.full(), rhs=xdts.full(), start=True, stop=True)
        P.dve.tensor_tensor(out=v3(Hs.full()), in0=v3(Hs.full()), in1=bc3(eAtot[:, c, :]), op=ALU.mult)
        P.dve.tensor_tensor(out=Hs.full(), in0=Hs.full(), in1=ps_h[:, 0:256], op=ALU.add)
        P.act.copy(out=Hb.full(), in_=Hs.full())
    P.emit()
    return nc, P


def _bc_consts():
    msk = np.zeros((128, 8, 512), np.float32)
    p = np.arange(128)[:, None]
    q = np.arange(512)[None, :]
    for j in range(4):
        key = j * 128 + p
        msk[:, j, :] = np.where((key // 64) > (q // 64), NEG, 0.0)
        msk[:, 4 + j, :] = np.where(key > q, NEG, 0.0)
    cm = np.zeros((128, 4, 128), np.float32)
    jj = np.arange(128)[:, None]
    ii = np.arange(128)[None, :]
    cm[:, 0, :] = (jj <= ii).astype(np.float32)
    cm[:, 1, :] = np.where(jj > ii, NEG, 0.0)
    cm[:, 2, :] = np.eye(128, dtype=np.float32)
    cm[:, 3, :] = 1.0
    return msk, cm


def _tm(a):
    S, n = a.shape
    return np.ascontiguousarray(a.reshape(S // 128, 128, n).transpose(1, 0, 2))


def run_BC(inp, l, resA):
    if "BC" not in _PROG_CACHE:
        _PROG_CACHE["BC"] = build_BC()[0]
    nc = _PROG_CACHE["BC"]
    msk, cm = _bc_consts()

    def gather(name, b):
        return np.concatenate([np.asarray(resA[b * 4 + i][name]) for i in range(4)], axis=-1)

    in_maps = []
    for c in range(8):
        b, hg = c // 4, c % 4
        qm = gather("o_qm", b)[2 * hg:2 * hg + 2]
        km = gather("o_km", b)[2 * hg:2 * hg + 2]
        vmf = gather("o_vm", b)
        qf = gather("o_qf", b)[2 * hg:2 * hg + 2]
        kf = gather("o_kf", b)[2 * hg:2 * hg + 2]
        vff = gather("o_vf", b)
        lff = gather("o_lf", b)
        xbc = gather("o_xbc", b)
        dtf = gather("o_dt", b)
        af = gather("o_a", b)
        g = hg // 2
        vm = np.stack([_tm(vmf[(2 * hg + h) * 64:(2 * hg + h + 1) * 64].T) for h in range(2)])
        vf = np.stack([_tm(vff[(2 * hg + h) * 64:(2 * hg + h + 1) * 64].T) for h in range(2)])
        lf = np.stack([np.ascontiguousarray(lff[2 * hg + h].reshape(NKT, 128).T) for h in range(2)])
        x_tm = _tm(xbc[hg * 256:(hg + 1) * 256].T)
        Bf = xbc[1024 + g * 128:1024 + (g + 1) * 128]
        Cf = xbc[1280 + g * 128:1280 + (g + 1) * 128]
        Dv = np.broadcast_to(inp["ssm_D"][l][4 * hg:4 * hg + 4][None, :], (128, 4)).astype(np.float32)
        in_maps.append({
            "qm": np.ascontiguousarray(qm), "km": np.ascontiguousarray(km), "vm": vm,
            "qf": np.ascontiguousarray(qf), "kf": np.ascontiguousarray(kf), "vf": vf, "lf": lf,
            "msk": msk, "cm": cm, "x_tm": x_tm, "B_tm": _tm(Bf.T), "BT": np.ascontiguousarray(Bf),
            "CT": np.ascontiguousarray(Cf), "dt_tm": _tm(dtf[4 * hg:4 * hg + 4].T),
            "a_tm": _tm(af[4 * hg:4 * hg + 4].T), "Dv": np.ascontiguousarray(Dv),
        })
    res = run_bass_kernel_spmd(nc, in_maps, core_ids=list(range(8))).results
    om = np.zeros((2, 512, S_), NBF)
    of = np.zeros((2, 512, S_), NBF)
    y = np.zeros((2, 1024, S_), np.float32)
    for c in range(8):
        b, hg = c // 4, c % 4
        om[b, hg * 128:(hg + 1) * 128] = np.asarray(res[c]["o_m"]).reshape(128, S_)
        of[b, hg * 128:(hg + 1) * 128] = np.asarray(res[c]["o_f"]).reshape(128, S_)
        yy = np.asarray(res[c]["o_y"])
        y[b, hg * 256:(hg + 1) * 256] = yy.transpose(2, 1, 0).reshape(256, S_)
    return om, of, y


def build_D1():
    nc = new_nc()
    P = Prog(nc)
    EI, EO = "ExternalInput", "ExternalOutput"
    NT = T // 512
    omT = P.dram("omT", [512, T], BF16, EI)
    ofT = P.dram("ofT", [512, T], BF16, EI)
    yT = P.dram("yT", [1024, T], F32, EI)
    szT = P.dram("szT", [1024, T], BF16, EI)
    gT = P.dram("gT", [3072, T], BF16, EI)
    xT = P.dram("xT", [D, T], F32, EI)
    w_a = P.dram("w_a", [512, D], F32, EI)
    w_b = P.dram("w_b", [512, D], F32, EI)
    w_c = P.dram("w_c", [1024, D], F32, EI)
    w_o = P.dram("w_o", [1024, D], F32, EI)
    cst_d = P.dram("cst", [128, 8], F32, EI)
    o_x = P.dram("o_x", [D, T], F32, EO)

    cstb = P.sb("cstb", [128, 8], F32)
    P.dma("sync", out=cstb.full(), in_=cst_d.full())
    ones = P.sb("ones", [128, 128], F32)
    P.dve.memset(ap=ones.full(), constant=1.0)
    eps = P.sb("eps", [128, 1], F32)
    P.dve.memset(ap=eps.full(), constant=1e-6)
    pb = [P.ps(f"pb{i}", [128, 512], F32) for i in range(8)]
    wst = [P.sb(f"wst{i}", [128, 4, 1024], F32) for i in range(2)]
    wcnt = [0]

    def load_w(dram, kc_n, name):
        bfb = P.sb(name, [128, kc_n, D], BF16)
        v = dram.full().re("(kc p) n -> p kc n", p=128)
        for k0 in range(0, kc_n, 4):
            i = wcnt[0] % 2
            wcnt[0] += 1
            P.dma("sync" if i == 0 else "gpsimd", out=wst[i].full(), in_=v[:, k0:k0 + 4, :])
            P.pool.tensor_copy(out=bfb[:, k0:k0 + 4, :], in_=wst[i].full())
        return bfb

    Wa = load_w(w_a, 4, "Wa")
    Wb = load_w(w_b, 4, "Wb")
    Wc = load_w(w_c, 8, "Wc")
    Wo = load_w(w_o, 8, "Wo")

    ys = P.sb("ys", [128, 8, 512], F32)
    szs = P.sb("szs", [128, 8, 512], BF16)
    yn = P.sb("yn", [128, 8, 512], BF16)
    oms = P.sb("oms", [128, 4, 512], BF16)
    ofs = P.sb("ofs", [128, 4, 512], BF16)
    gs = P.sb("gs", [128, 24, 512], BF16)
    xs = P.sb("xs", [128, 8, 512], F32)
    sq = P.sb("sq", [128, 512], F32)
    rstd = P.sb("rstd", [128, 512], F32)
    m1 = [P.sb(f"m1_{i}", [128, 512], F32) for i in range(2)]
    m2 = [P.sb(f"m2_{i}", [128, 512], F32) for i in range(2)]
    m3 = [P.sb(f"m3_{i}", [128, 512], F32) for i in range(2)]
    mg = P.sb("mg", [128, 8, 512], BF16)
    xo = [P.sb(f"xo{i}", [128, 512], F32) for i in range(2)]
    ch = lambda d: d.full().re("(kc p) n -> p kc n", p=128)
    for ti in range(NT):
        ts = slice(ti * 512, (ti + 1) * 512)
        P.dma("sync", out=ys.full(), in_=ch(yT)[:, :, ts])
        P.dma("gpsimd", out=szs.full(), in_=ch(szT)[:, :, ts])
        P.dma("sync", out=oms.full(), in_=ch(omT)[:, :, ts])
        P.dma("gpsimd", out=ofs.full(), in_=ch(ofT)[:, :, ts])
        P.dma("sync", out=gs.full(), in_=ch(gT)[:, :, ts])
        P.dma("gpsimd", out=xs.full(), in_=ch(xT)[:, :, ts])
        ps = pb[7]
        for kc in range(8):
            P.dve.tensor_tensor(out=ys[:, kc, :], in0=ys[:, kc, :], in1=szs[:, kc, :], op=ALU.mult)
            P.act.activation(out=sq.full(), in_=ys[:, kc, :], func=AF.Square)
            P.pe.matmul(out=ps.full(), lhsT=ones.full(), rhs=sq.full(), start=(kc == 0), stop=(kc == 7))
        P.act.activation(out=rstd.full(), in_=ps.full(), func=AF.Sqrt, bias=eps[:, 0:1], scale=1.0 / 1024.0)
        P.dve.reciprocal(out=rstd.full(), in_=rstd.full())
        for kc in range(8):
            P.dve.scalar_tensor_tensor(out=yn[:, kc, :], in0=ys[:, kc, :], scalar=cstb[:, kc:kc + 1], in1=rstd.full(),
                                       op0=ALU.mult, op1=ALU.mult)
        for oc in range(8):
            i2 = oc % 2
            osl = slice(oc * 128, (oc + 1) * 128)
            pa, pbb, pc = pb[0 + i2 * 3], pb[1 + i2 * 3], pb[2 + i2 * 3]
            for kc in range(4):
                P.pe.matmul(out=pa.full(), lhsT=Wa[:, kc, osl], rhs=oms[:, kc, :], start=(kc == 0), stop=(kc == 3))
            for kc in range(4):
                P.pe.matmul(out=pbb.full(), lhsT=Wb[:, kc, osl], rhs=ofs[:, kc, :], start=(kc == 0), stop=(kc == 3))
            for kc in range(8):
                P.pe.matmul(out=pc.full(), lhsT=Wc[:, kc, osl], rhs=yn[:, kc, :], start=(kc == 0), stop=(kc == 7))
            P.dve.tensor_tensor(out=m1[i2].full(), in0=pa.full(), in1=gs[:, oc, :], op=ALU.mult)
            P.dve.tensor_tensor(out=m2[i2].full(), in0=pbb.full(), in1=gs[:, 8 + oc, :], op=ALU.mult)
            P.dve.tensor_tensor(out=m3[i2].full(), in0=pc.full(), in1=gs[:, 16 + oc, :], op=ALU.mult)
            P.pool.tensor_tensor(out=m1[i2].full(), in0=m1[i2].full(), in1=m2[i2].full(), op=ALU.add)
            P.pool.tensor_tensor(out=mg[:, oc, :], in0=m1[i2].full(), in1=m3[i2].full(), op=ALU.add)
        for oc in range(8):
            i2 = oc % 2
            ps = pb[6 + i2]
            for kc in range(8):
                P.pe.matmul(out=ps.full(), lhsT=Wo[:, kc, oc * 128:(oc + 1) * 128], rhs=mg[:, kc, :],
                            start=(kc == 0), stop=(kc == 7))
            P.dve.tensor_tensor(out=xo[i2].full(), in0=ps.full(), in1=xs[:, oc, :], op=ALU.add)
            P.dma("sync" if i2 else "gpsimd", out=o_x[oc * 128:(oc + 1) * 128, ts], in_=xo[i2].full())
    P.emit()
    return nc, P


def d2_colpack(inp, l):
    cp = ColPack()
    cp.add("g_ffn", inp["norm_ffn_g"][l])
    cw = inp["ffn_conv_w"][l]
    for k in range(3):
        cp.add(f"fw{k}", cw[k])
    cp.add("fb", inp["ffn_conv_b"][l])
    return cp


def build_D2(off):
    nc = new_nc()
    P = Prog(nc)
    EI, EO = "ExternalInput", "ExternalOutput"
    NT = T // 512
    TT = T + HALO
    xT = P.dram("xT", [D, TT], F32, EI)
    w_up = P.dram("w_up", [D, 5632], F32, EI)
    w_dn = P.dram("w_dn", [2816, D], F32, EI)
    cst_d = P.dram("cst", [128, off["_n"]], F32, EI)
    o_x = P.dram("o_x", [D, T], F32, EO)

    cstb = P.sb("cstb", [128, off["_n"]], F32)
    C = Cst(P, cstb, off)
    P.dma("sync", out=cstb.full(), in_=cst_d.full())
    ones = P.sb("ones", [128, 128], F32)
    P.dve.memset(ap=ones.full(), constant=1.0)
    eps = P.sb("eps", [128, 1], F32)
    P.dve.memset(ap=eps.full(), constant=1e-6)
    pb = [P.ps(f"pb{i}", [128, 512], F32) for i in range(8)]
    wst = [P.sb(f"wst{i}", [128, 1024], F32) for i in range(2)]
    wcnt = [0]
    Wu = P.sb("Wu", [128, 8, 5632], BF16)
    Wd = P.sb("Wd", [128, 22, D], BF16)
    wuv = w_up.full().re("(kc p) n -> p kc n", p=128)
    for kc in range(8):
        for c0 in range(0, 5632, 1024):
            n = min(1024, 5632 - c0)
            i = wcnt[0] % 2
            wcnt[0] += 1
            P.dma("sync" if i == 0 else "gpsimd", out=wst[i][:, 0:n], in_=wuv[:, kc, c0:c0 + n])
            P.pool.tensor_copy(out=Wu[:, kc, c0:c0 + n], in_=wst[i][:, 0:n])
    wdv = w_dn.full().re("(kc p) n -> p kc n", p=128)
    for k0 in range(22):
        i = wcnt[0] % 2
        wcnt[0] += 1
        P.dma("sync" if i == 0 else "gpsimd", out=wst[i].full(), in_=wdv[:, k0, :])
        P.pool.tensor_copy(out=Wd[:, k0, :], in_=wst[i].full())

    xst = P.sb("xst", [128, 8, 512], F32)
    hn = P.sb("hn", [128, 8, 512], BF16)
    sq = P.sb("sq", [128, 512], F32)
    rstd = P.sb("rstd", [128, 512], F32)
    act = P.sb("act", [128, 22, 512], BF16)
    upre = [P.sb(f"upre{i}", [128, 516], F32) for i in range(2)]
    acc = [P.sb(f"acc{i}", [128, 512], F32) for i in range(2)]
    sg = P.sb("sg", [128, 512], F32)
    carry = P.sb("carry", [128, 44, 4], F32)
    xo = [P.sb(f"xo{i}", [128, 512], F32) for i in range(2)]
    xTv = xT.full().re("(kc p) n -> p kc n", p=128)
    tiles = [(0, HALO)] + [(HALO + i * 512, 512) for i in range(NT)]
    pcnt = [0]
    for tix, (c0, w) in enumerate(tiles):
        P.dma("sync", out=xst[:, :, 0:w], in_=xTv[:, :, c0:c0 + w])
        ps = pb[7]
        for kc in range(8):
            P.act.activation(out=sq[:, 0:w], in_=xst[:, kc, 0:w], func=AF.Square)
            P.pe.matmul(out=ps[:, 0:w], lhsT=ones.full(), rhs=sq[:, 0:w], start=(kc == 0), stop=(kc == 7))
        P.act.activation(out=rstd[:, 0:w], in_=ps[:, 0:w], func=AF.Sqrt, bias=eps[:, 0:1], scale=1.0 / 1024.0)
        P.dve.reciprocal(out=rstd[:, 0:w], in_=rstd[:, 0:w])
        for kc in range(8):
            P.dve.scalar_tensor_tensor(out=hn[:, kc, 0:w], in0=xst[:, kc, 0:w], scalar=C.col("g_ffn", kc),
                                       in1=rstd[:, 0:w], op0=ALU.mult, op1=ALU.mult)
        for i in range(22):
            accs = []
            for j, cg in enumerate((i, 22 + i)):
                ps = pb[pcnt[0] % 4]
                pcnt[0] += 1
                for kc in range(8):
                    P.pe.matmul(out=ps[:, 0:w], lhsT=Wu[:, kc, cg * 128:(cg + 1) * 128], rhs=hn[:, kc, 0:w],
                                start=(kc == 0), stop=(kc == 7))
                if tix == 0:
                    P.act.copy(out=carry[:, cg, :], in_=ps[:, 0:HALO])
                    continue
                up = upre[j]
                P.act.copy(out=up[:, 4:516], in_=ps.full())
                P.dve.tensor_copy(out=up[:, 0:4], in_=carry[:, cg, :])
                P.pool.tensor_copy(out=carry[:, cg, :], in_=up[:, 512:516])
                a0 = acc[j]
                P.dve.tensor_scalar(out=a0.full(), in0=up[:, 4:516], scalar1=C.col("fw2", cg), scalar2=C.col("fb", cg),
                                    op0=ALU.mult, op1=ALU.add)
                P.dve.scalar_tensor_tensor(out=a0.full(), in0=up[:, 3:515], scalar=C.col("fw1", cg), in1=a0.full(),
                                           op0=ALU.mult, op1=ALU.add)
                P.dve.scalar_tensor_tensor(out=a0.full(), in0=up[:, 2:514], scalar=C.col("fw0", cg), in1=a0.full(),
                                           op0=ALU.mult, op1=ALU.add)
                accs.append(a0)
            if tix == 0:
                continue
            P.act.activation(out=sg.full(), in_=accs[0].full(), func=AF.Silu)
            P.pool.tensor_tensor(out=act[:, i, :], in0=sg.full(), in1=accs[1].full(), op=ALU.mult)
        if tix == 0:
            continue
        ti = tix - 1
        for oc in range(8):
            i2 = oc % 2
            ps = pb[4 + i2]
            for i in range(22):
                P.pe.matmul(out=ps.full(), lhsT=Wd[:, i, oc * 128:(oc + 1) * 128], rhs=act[:, i, :],
                            start=(i == 0), stop=(i == 21))
            P.dve.tensor_tensor(out=xo[i2].full(), in0=ps.full(), in1=xst[:, oc, :], op=ALU.add)
            P.dma("sync" if i2 else "gpsimd", out=o_x[oc * 128:(oc + 1) * 128, ti * 512:(ti + 1) * 512], in_=xo[i2].full())
    P.emit()
    return nc, P


def run_D1(inp, l, resA, om, of, y, x_full):
    if "D1" not in _PROG_CACHE:
        _PROG_CACHE["D1"] = build_D1()[0]
    nc = _PROG_CACHE["D1"]
    cst = np.ascontiguousarray(inp["ssm_norm_g"][l].reshape(8, 128).T)
    xf = x_full.reshape(16384, D)
    in_maps = []
    for c in range(8):
        b, q = c // 4, c % 4
        ts = slice(q * T, (q + 1) * T)
        in_maps.append({
            "omT": np.ascontiguousarray(om[b][:, ts]), "ofT": np.ascontiguousarray(of[b][:, ts]),
            "yT": np.ascontiguousarray(y[b][:, ts]), "szT": np.asarray(resA[c]["o_sz"]),
            "gT": np.asarray(resA[c]["o_g"]), "xT": np.ascontiguousarray(xf[c * T:(c + 1) * T].T),
            "w_a": np.ascontiguousarray(inp["w_br_mla"][l]), "w_b": np.ascontiguousarray(inp["w_br_fox"][l]),
            "w_c": np.ascontiguousarray(inp["w_br_ssm"][l]), "w_o": np.ascontiguousarray(inp["w_out"][l]),
            "cst": cst,
        })
    res = run_bass_kernel_spmd(nc, in_maps, core_ids=list(range(8))).results
    xm = np.concatenate([np.asarray(r["o_x"]).T for r in res], axis=0)
    return xm.reshape(2, S_, D)


def run_D2(inp, l, xm_full):
    cp = d2_colpack(inp, l)
    off = dict(cp.off)
    off["_n"] = cp.n
    if "D2" not in _PROG_CACHE:
        _PROG_CACHE["D2"] = build_D2(off)[0]
    nc = _PROG_CACHE["D2"]
    cst = cp.array()
    xf = xm_full.reshape(16384, D)
    in_maps = []
    for c in range(8):
        t0 = c * T
        xt = np.zeros((D, T + HALO), np.float32)
        xt[:, HALO:] = xf[t0:t0 + T].T
        if c % 4 != 0:
            xt[:, 0:HALO] = xf[t0 - HALO:t0].T
        in_maps.append({"xT": np.ascontiguousarray(xt), "w_up": np.ascontiguousarray(inp["ffn_w_up"][l]),
                        "w_dn": np.ascontiguousarray(inp["ffn_w_down"][l]), "cst": cst})
    res = run_bass_kernel_spmd(nc, in_maps, core_ids=list(range(8))).results
    xo = np.concatenate([np.asarray(r["o_x"]).T for r in res], axis=0)
    return xo.reshape(2, S_, D)


def kernel_unfused(**inp):
    inp = {k: np.asarray(v) for k, v in inp.items()}
    x = inp["x"].astype(np.float32)
    pos = inp["positions"]
    for l in range(2):
        resA = run_A(inp, l, x, pos)
        om, of, y = run_BC(inp, l, resA)
        xm = run_D1(inp, l, resA, om, of, y, x)
        x = run_D2(inp, l, xm)
    return np.ascontiguousarray(x.astype(np.float32))


def kernel(**inp):
    return kernel_fused(**inp)


SW = 516
RG = [[0, 1, 2, 3], [4, 5, 6, 7]]
KT_L = T // 128


def fused_rowpack(inp, l):
    r = np.concatenate([inp["fox_b_f"][l], inp["ssm_dt_bias"][l], inp["ssm_A_log"][l], inp["ssm_D"][l]]).astype(np.float32)
    return np.ascontiguousarray(np.broadcast_to(r[None, :], (128, r.size)))


def build_fused(offA, offD2, stop=None, dbg=()):
    nc = new_nc()
    P = Prog(nc)
    EI, EO = "ExternalInput", "ExternalOutput"
    L = 2
    x0 = P.dram("x0", [D, 4 * SW], F32, EI)
    pos = P.dram("pos", [1, T], I32, EI)
    w_in = P.dram("w_in", [L, D, 7864], F32, EI)
    w_uq = P.dram("w_uq", [L, 384, 768], F32, EI)
    w_kp = P.dram("w_kp", [L, 256, 768], F32, EI)
    w_v = P.dram("w_v", [L, 256, 512], F32, EI)
    w_a = P.dram("w_a", [L, 512, D], F32, EI)
    w_b = P.dram("w_b", [L, 512, D], F32, EI)
    w_c = P.dram("w_c", [L, 1024, D], F32, EI)
    w_o = P.dram("w_o", [L, 1024, D], F32, EI)
    w_up = P.dram("w_up", [L, D, 5632], F32, EI)
    w_dn = P.dram("w_dn", [L, 2816, D], F32, EI)
    cstA_d = P.dram("cstA", [L, 128, offA["_n"]], F32, EI)
    cstD_d = P.dram("cstD", [L, 128, offD2["_n"]], F32, EI)
    gssm_d = P.dram("gssm", [L, 128, 8], F32, EI)
    rowc_d = P.dram("rowc", [L, 128, 56], F32, EI)
    sel_d = P.dram("sel", [128, 32], F32, EI)
    msk_d = P.dram("msk", [128, 8, 512], F32, EI)
    cm_d = P.dram("cm", [128, 4, 128], F32, EI)
    mats_d = P.dram("mats", [128, 192], F32, EI)
    out = P.dram("out", [D, T], F32, EO)
    xb = [x0, P.dram("xb1", [D, 4 * SW], F32)]
    xmid = P.dram("xmid", [D, 4 * SW], F32)
    qm = P.dram("qm", [8, 96, T], BF16)
    qf = P.dram("qf", [8, 64, T], BF16)
    fq = P.dram("fq", [8, 3, T], BF16)
    szd = P.dram("szd", [1024, T], BF16)
    gd = P.dram("gd", [3072, T], BF16)
    xtm = P.dram("xtm", [128, KT_L, 1024], BF16)
    btm = P.dram("btm", [128, KT_L, 256], BF16)
    bct = P.dram("bct", [512, T], BF16)
    dtd = P.dram("dtd", [128, KT_L, 16], F32)
    atd = P.dram("atd", [128, KT_L, 16], F32)
    omd = P.dram("omd", [512, T], BF16)
    ofd = P.dram("ofd", [512, T], BF16)
    yd = P.dram("yd", [1024, T], F32)
    kxm = [P.dram(f"kxm{m}", [768, 512], BF16) for m in range(4)]
    kxmg = [P.dram(f"kxmg{m}", [4 * 768, 512], BF16) for m in range(4)]
    kxf = [P.dram(f"kxf{m}", [640, 512], BF16) for m in range(4)]
    kxfg = [P.dram(f"kxfg{m}", [4 * 640, 512], BF16) for m in range(4)]
    nfq = P.dram("nfq", [8, 3, T], BF16)
    vx = [P.dram(f"vx{m}", [2048, 256], BF16) for m in range(4)]
    vxg = [P.dram(f"vxg{m}", [4 * 2048, 256], BF16) for m in range(4)]
    sx = [P.dram(f"sx{i}", [256, 1024], F32) for i in range(2)]
    sxg = [P.dram(f"sxg{i}", [4 * 256, 1024], F32) for i in range(2)]
    fx = P.dram("fx", [128, 224], F32)
    fxg = P.dram("fxg", [4 * 128, 224], F32)
    tx = P.dram("tx", [128, 128], F32)
    txg = P.dram("txg", [4 * 128, 128], F32)
    wub = P.dram("wub", [D, 5632], BF16)
    wdb = P.dram("wdb", [2816, D], BF16)
    wab = P.dram("wab", [512, D], BF16)
    wbb = P.dram("wbb", [512, D], BF16)
    wcb = P.dram("wcb", [1024, D], BF16)
    wob = P.dram("wob", [1024, D], BF16)
    dbg_out = {}

    def gather_pairs(pairs, after=None):
        for (a, b) in pairs:
            kw = {}
            if after is not None:
                kw["_reads"] = [after]
            P.pool.collective_compute(kind="AllGather", op=ALU.bypass, replica_groups=RG,
                                      ins=[a.full().re("(p a) c -> p (a c)", p=128)],
                                      outs=[b.full().re("(q a) c -> q (a c)", q=512)], **kw)

    def load_consts():
        d = {}
        d["cm"] = P.sb("cmb", [128, 4, 128], F32)
        P.dma("sync", out=d["cm"].full(), in_=cm_d.full())
        d["sel"] = P.sb("selb", [128, 32], F32)
        P.dma("sync", out=d["sel"].full(), in_=sel_d.full())
        d["eps"] = P.sb("eps", [128, 1], F32)
        P.dve.memset(ap=d["eps"].full(), constant=1e-6)
        d["one1"] = P.sb("one1", [128, 1], F32)
        P.dve.memset(ap=d["one1"].full(), constant=1.0)
        d["zero"] = P.sb("zero", [128, 1], F32)
        P.dve.memset(ap=d["zero"].full(), constant=0.0)
        return d

    def phase_A(l):
        K = load_consts()
        cmb = K["cm"]
        tri, ident, ones = cmb[:, 0, :], cmb[:, 2, :], cmb[:, 3, :]
        eps, one1 = K["eps"], K["one1"]
        xin = xb[l]
        cstb = P.sb("cstb", [128, offA["_n"]], F32)
        C = Cst(P, cstb, offA)
        P.dma("sync", out=cstb.full(), in_=cstA_d[l])
        rowc = P.sb("rowc", [128, 56], F32)
        P.dma("sync", out=rowc.full(), in_=rowc_d[l])
        matf = P.sb("matf", [128, 192], F32)
        matb = P.sb("matb", [128, 192], BF16)
        P.dma("sync", out=matf.full(), in_=mats_d.full())
        P.dve.tensor_copy(out=matb.full(), in_=matf.full())
        prh = matb[0:96, 0:96]
        selm = matb[0:32, 96:192]
        identb = P.sb("identb", [128, 128], BF16)
        P.dve.tensor_copy(out=identb.full(), in_=ident)
        Aneg_r = P.sb("Aneg_r", [128, 16], F32)
        P.act.activation(out=Aneg_r.full(), in_=rowc[:, 24:40], func=AF.Exp)
        P.dve.tensor_scalar(out=Aneg_r.full(), in0=Aneg_r.full(), scalar1=-1.0, scalar2=None, op0=ALU.mult)

        pb = [P.ps(f"pb{i}", [128, 512], F32) for i in range(7)]
        pbt = P.ps("pbt", [128, 1024], BF16)
        pbi = {}

        def nxt_ps(lo=0, hi=4):
            i = pbi.get(lo, 0)
            pbi[lo] = (i + 1) % (hi - lo)
            return pb[lo + i]

        Ctab = P.sb("Ctab", [96, T], F32)
        Stab = P.sb("Stab", [96, T], F32)
        hraw = P.sb("hraw", [96, 512], F32)
        hsq = P.sb("hsq", [96, 512], F32)
        hrs = P.sb("hrs", [96, 512], F32)
        hnf = P.sb("hnf", [96, 512], F32)
        hnb = P.sb("hnb", [96, 512], BF16)
        ht1 = P.sb("ht1", [96, 512], F32)
        ht2 = P.sb("ht2", [96, 512], F32)
        posf, rr_tmp, rr_m = hrs, hraw, hsq

        class _IV:
            def __init__(self, b):
                self.b = b

            def full(self):
                return self.b.full().bitcast(I32)
        posi, rr_i = _IV(ht1), _IV(ht2)

        def sin_table(outv, phase):
            P.dve.tensor_scalar(out=rr_tmp.full(), in0=posf.full(), scalar1=C.col("invf"), scalar2=phase,
                                op0=ALU.mult, op1=ALU.add)
            P.dve.tensor_scalar(out=rr_m.full(), in0=rr_tmp.full(), scalar1=1.0 / (2 * np.pi), scalar2=None, op0=ALU.mult)
            P.dve.tensor_copy(out=rr_i.full(), in_=rr_m.full())
            P.dve.tensor_copy(out=rr_m.full(), in_=rr_i.full())
            P.dve.scalar_tensor_tensor(out=rr_tmp.full(), in0=rr_m.full(), scalar=-2 * np.pi, in1=rr_tmp.full(),
                                       op0=ALU.mult, op1=ALU.add)
            P.dve.tensor_scalar(out=rr_m.full(), in0=rr_tmp.full(), scalar1=np.pi, scalar2=-2 * np.pi, op0=ALU.is_gt, op1=ALU.mult)
            P.dve.tensor_tensor(out=rr_tmp.full(), in0=rr_tmp.full(), in1=rr_m.full(), op=ALU.add)
            P.dve.tensor_scalar(out=rr_m.full(), in0=rr_tmp.full(), scalar1=-np.pi, scalar2=2 * np.pi, op0=ALU.is_lt, op1=ALU.mult)
            P.dve.tensor_tensor(out=rr_tmp.full(), in0=rr_tmp.full(), in1=rr_m.full(), op=ALU.add)
            P.act.activation(out=outv, in_=rr_tmp.full(), func=AF.Sin)

        for i in range(4):
            P.dma("sync", out=posi.full(), in_=pos[:, i * 512:(i + 1) * 512].f(lambda a: a.partition_broadcast(96)))
            P.dve.tensor_copy(out=posf.full(), in_=posi.full())
            sin_table(Stab[:, i * 512:(i + 1) * 512], 0.0)
            sin_table(Ctab[:, i * 512:(i + 1) * 512], np.pi / 2)
        P.dve.memset(ap=Stab[0:64, :], constant=0.0)
        P.dve.memset(ap=Ctab[0:64, :], constant=1.0)

        hn = P.sb("hn", [128, 8, 4 * SW], BF16)
        xst = P.sb("xst", [128, 8, 512], F32)
        sq = P.sb("sq", [128, 512], F32)
        rstd = P.sb("rstd", [128, 512], F32)
        xTv = xin.full().re("(kc p) n -> p kc n", p=128)

        def rstd_from(ps_view, n_feat, rows, rstd_view):
            P.act.activation(out=rstd_view, in_=ps_view, func=AF.Ln, bias=eps[0:rows, 0:1], scale=1.0 / n_feat)
            P.act.activation(out=rstd_view, in_=rstd_view, func=AF.Exp, scale=-0.5)

        halos = [(m * SW, 4) for m in range(4)]
        main = [(m * SW + 4, 512) for m in range(4)]
        for (c0, w) in halos + main:
            P.dma("sync", out=xst[:, :, 0:w], in_=xTv[:, :, c0:c0 + w])
            ps = nxt_ps(4, 6)
            for kc in range(8):
                P.act.activation(out=sq[:, 0:w], in_=xst[:, kc, 0:w], func=AF.Square)
                P.pe.matmul(out=ps[:, 0:w], lhsT=ones, rhs=sq[:, 0:w], start=(kc == 0), stop=(kc == 7))
            rstd_from(ps[:, 0:w], 1024.0, 128, rstd[:, 0:w])
            for kc in range(8):
                P.dve.scalar_tensor_tensor(out=hn[:, kc, c0:c0 + w], in0=xst[:, kc, 0:w], scalar=C.col("g_mix", kc),
                                           in1=rstd[:, 0:w], op0=ALU.mult, op1=ALU.mult)

        wst = [P.sb(f"wst{i}", [128, 8, 256], F32) for i in range(2)]
        wbf = [P.sb(f"wbf{i}", [128, 8, 512], BF16) for i in range(2)]
        wcnt = [0]
        scnt = [0]
        w_inv = w_in[l].re("(kc p) n -> p kc n", p=128)

        SBv = 672 + 1544
        wplan = [(0, 384), (384, 288), (672, 512), (672 + 512, 512), (672 + 1024, 512), (672 + 1536, 8),
                 (SBv + 1024 + 1536, 16), (SBv, 512), (SBv + 512, 512)]
        wplan += [(SBv + 1024 + b_ * 512, 512) for b_ in range(3)]
        wplan += [(SBv + 2576 + b_ * 512, 512) for b_ in range(6)]
        wpend = {}
        cpend = []

        def w_issue(g):
            c0, ncols = wplan[g]
            lst = []
            for h0 in range(0, ncols, 256):
                n = min(256, ncols - h0)
                si = scnt[0] % 2
                scnt[0] += 1
                P.dma("sync", out=wst[si][:, :, 0:n], in_=w_inv[:, :, c0 + h0:c0 + h0 + n])
                lst.append((si, h0, n))
            wpend[g] = lst

        def load_w(c0, ncols):
            g = wcnt[0]
            wcnt[0] += 1
            assert wplan[g] == (c0, ncols), (g, wplan[g], c0, ncols)
            i = g % 2
            if g not in wpend:
                w_issue(g)
            lst = wpend.pop(g)
            for (si, h0, n) in lst:
                P.act.copy(out=wbf[i][:, :, h0:h0 + n], in_=wst[si][:, :, 0:n])
            if g + 1 < len(wplan):
                w_issue(g + 1)
            if cpend:
                gather_pairs([cpend.pop(0)], after=wbf[i].full())
            return wbf[i]

        def proj(wb, wc0, mcols, c0, w, ps_view):
            for kc in range(8):
                P.pe.matmul(out=ps_view, lhsT=wb[:, kc, wc0:wc0 + mcols], rhs=hn[:, kc, c0:c0 + w],
                            start=(kc == 0), stop=(kc == 7))

        def proj_tm(wb, wc0, ncols, tok0, ps_view):
            for kc in range(8):
                P.pe.matmul(out=ps_view, lhsT=hn[:, kc, tok0:tok0 + 128], rhs=wb[:, kc, wc0:wc0 + ncols],
                            start=(kc == 0), stop=(kc == 7))

        ostg_cnt = [0]
        ostg = [P.sb(f"ostg{i}", [128, 512], BF16) for i in range(4)]

        def next_ostg():
            i = ostg_cnt[0] % 4
            ostg_cnt[0] += 1
            return ostg[i]

        def out_dma(dst_view, src_view):
            P.dma("sync" if ostg_cnt[0] % 2 else "scalar", out=dst_view, in_=src_view)

        hsets = [dict(hraw=hraw.full(), hsq=hsq.full(), hrs=hrs.full(), hnf=hnf.full(), hnb=hnb.full(),
                      ht1=ht1.full(), ht2=ht2.full())]
        hnb1 = P.sb("hnb1", [96, 512], BF16)
        hsets.append(dict(hraw=xst[0:96, 0, :].k(0), hsq=xst[0:96, 1, :].k(1), hrs=xst[0:96, 2, :].k(2),
                          hnf=xst[0:96, 3, :].k(3), hnb=hnb1.full(), ht1=xst[0:96, 4, :].k(4), ht2=xst[0:96, 5, :].k(5)))
        hb2 = P.sb("hb2", [96, 6, 512], F32)
        hnb2 = P.sb("hnb2", [96, 512], BF16)
        hsets.append(dict(hraw=hb2[:, 0, :].k(0), hsq=hb2[:, 1, :].k(1), hrs=hb2[:, 2, :].k(2),
                          hnf=hb2[:, 3, :].k(3), hnb=hnb2.full(), ht1=hb2[:, 4, :].k(4), ht2=hb2[:, 5, :].k(5)))
        hcnt = [0]

        def headnorm(projfn, d, gain_col, rope, tok0, dst_view):
            H = hsets[hcnt[0] % 3]
            hcnt[0] += 1
            ps_view = projfn()
            P.act.activation(out=H["hsq"][0:d, :], in_=ps_view, func=AF.Square)
            P.act.copy(out=H["hraw"][0:d, :], in_=ps_view)
            yield
            ps2 = nxt_ps(4, 6)
            P.pe.matmul(out=ps2[0:d, :], lhsT=cmb[0:d, 3, 0:d], rhs=H["hsq"][0:d, :], start=True, stop=True)
            rstd_from(ps2[0:d, :], float(d), d, H["hrs"][0:d, :])
            og = next_ostg()
            if not rope:
                P.dve.scalar_tensor_tensor(out=og[0:d, :], in0=H["hraw"][0:d, :], scalar=gain_col, in1=H["hrs"][0:d, :],
                                           op0=ALU.mult, op1=ALU.mult)
            else:
                P.dve.scalar_tensor_tensor(out=H["hnf"][0:d, :], in0=H["hraw"][0:d, :], scalar=gain_col, in1=H["hrs"][0:d, :],
                                           op0=ALU.mult, op1=ALU.mult)
                P.act.copy(out=H["hnb"][0:d, :], in_=H["hnf"][0:d, :])
                yield
                ps3 = nxt_ps(6, 7)
                P.pe.matmul(out=ps3[0:d, :], lhsT=prh, rhs=H["hnb"][0:d, :], start=True, stop=True)
                P.dve.tensor_tensor(out=H["ht1"][0:d, :], in0=H["hnf"][0:d, :], in1=Ctab[0:d, tok0:tok0 + 512], op=ALU.mult)
                P.dve.tensor_tensor(out=H["ht2"][0:d, :], in0=ps3[0:d, :], in1=Stab[0:d, tok0:tok0 + 512], op=ALU.mult)
                P.pool.tensor_tensor(out=og[0:d, :], in0=H["ht1"][0:d, :], in1=H["ht2"][0:d, :], op=ALU.add)
            out_dma(dst_view, og[0:d, :])

        def run_pipe(gens, depth=3):
            gens = iter(gens)
            active = []
            while True:
                started = False
                if len(active) < depth:
                    g = next(gens, None)
                    if g is not None:
                        started = True
                        try:
                            next(g)
                            active.append(g)
                        except StopIteration:
                            pass
                if not active and not started:
                    break
                olds = active[:-1] if (started and active) else list(active)
                for g in olds:
                    try:
                        next(g)
                    except StopIteration:
                        active.remove(g)

        lat = P.sb("lat", [128, 3, 512], F32)
        latn = P.sb("latn", [128, 3, 512], BF16)

        def latent_norm(ps_list, gname):
            nch = len(ps_list)
            ps2 = nxt_ps(4, 6)
            for i, psv in enumerate(ps_list):
                P.act.activation(out=sq.full(), in_=psv, func=AF.Square)
                P.act.copy(out=lat[:, i, :], in_=psv)
                P.pe.matmul(out=ps2.full(), lhsT=ones, rhs=sq.full(), start=(i == 0), stop=(i == nch - 1))
            rstd_from(ps2.full(), 128.0 * nch, 128, rstd.full())
            for i in range(nch):
                P.dve.scalar_tensor_tensor(out=latn[:, i, :], in0=lat[:, i, :], scalar=C.col(gname, i), in1=rstd.full(),
                                           op0=ALU.mult, op1=ALU.mult)

        def small_w(name, dram_l, kc_n, ncols, i):
            bfb = P.sb(name, [128, kc_n, ncols], BF16)
            dv = dram_l.re("(kc p) n -> p kc n", p=128)
            for kc in range(kc_n):
                si = scnt[0] % 2
                scnt[0] += 1
                stg = wst[si].full().re("p a b -> p (a b)")[:, 0:ncols]
                P.dma("sync", out=stg, in_=dv[:, kc, :])
                P.act.copy(out=bfb[:, kc, :], in_=stg)
            return bfb

        uqb = small_w("uqb", w_uq[l], 3, 768, 0)
        kpb = small_w("kpb", w_kp[l], 2, 768, 1)
        wvb = small_w("wvb", w_v[l], 2, 512, 0)

        vstg = [P.sb(f"vstg{i}", [128, 512], BF16) for i in range(2)]
        vcnt = [0]

        def v_out(kind, ktl, ps_view):
            vs = vstg[vcnt[0] % 2]
            vcnt[0] += 1
            P.act.copy(out=vs.full(), in_=ps_view)
            P.dma("sync" if vcnt[0] % 2 else "scalar",
                  out=vx[ktl // 4][kind * 1024:(kind + 1) * 1024, (ktl % 4) * 64:(ktl % 4 + 1) * 64].re("(h p) d -> p h d", p=128),
                  in_=vs.full().re("p (h d) -> p h d", h=8))

        wb = load_w(0, 384)
        for m, (c0, w) in enumerate(main):
            pss = []
            for ch in range(3):
                ps = nxt_ps(0, 4)
                proj(wb, ch * 128, 128, c0, 512, ps.full())
                pss.append(ps.full())
            latent_norm(pss, "g_cq")
            def mkq(h):
                def f():
                    ps = nxt_ps(0, 4)
                    for kc in range(3):
                        P.pe.matmul(out=ps[0:96, :], lhsT=uqb[:, kc, h * 96:(h + 1) * 96], rhs=latn[:, kc, :],
                                    start=(kc == 0), stop=(kc == 2))
                    return ps[0:96, :]
                return f
            run_pipe(headnorm(mkq(h), 96, C.col("g_q"), True, m * 512, qm[h, :, m * 512:(m + 1) * 512]) for h in range(8))
        wb = load_w(384, 288)
        krb = P.sb("krb", [32, 512], BF16)
        for m, (c0, w) in enumerate(main):
            pss = []
            for ch in range(2):
                ps = nxt_ps(0, 4)
                proj(wb, ch * 128, 128, c0, 512, ps.full())
                pss.append(ps.full())
            ps = nxt_ps(0, 4)
            proj(wb, 256, 32, c0, 512, ps[0:32, :])
            P.act.copy(out=krb.full(), in_=ps[0:32, :])
            latent_norm(pss, "g_ckv")
            def mkk(h):
                def f():
                    ps = nxt_ps(0, 4)
                    for kc in range(2):
                        P.pe.matmul(out=ps[0:96, :], lhsT=kpb[:, kc, h * 96:(h + 1) * 96], rhs=latn[:, kc, :],
                                    start=(kc == 0), stop=False)
                    P.pe.matmul(out=ps[0:96, :], lhsT=selm, rhs=krb.full(), start=False, stop=True)
                    return ps[0:96, :]
                return f
            run_pipe(headnorm(mkk(h), 96, C.col("g_k"), True, m * 512, kxm[m][h * 96:(h + 1) * 96, :]) for h in range(8))
            for j in range(4):
                ps = nxt_ps(0, 4)
                for kc in range(2):
                    P.pe.matmul(out=ps.full(), lhsT=latn[:, kc, j * 128:(j + 1) * 128], rhs=wvb[:, kc, :],
                                start=(kc == 0), stop=(kc == 1))
                v_out(0, m * 4 + j, ps.full())
        for (base, gname, isq) in ((672, "g_fq", True), (672 + 512, "g_fk", False)):
            wb = load_w(base, 512)
            def mkf(wb_, h, c0):
                def f():
                    ps = nxt_ps(0, 4)
                    proj(wb_, h * 64, 64, c0, 512, ps[0:64, :])
                    return ps[0:64, :]
                return f
            gl = []
            for m, (c0, w) in enumerate(main):
                for h in range(8):
                    dst = qf[h, :, m * 512:(m + 1) * 512] if isq else kxf[m][h * 64:(h + 1) * 64, :]
                    gl.append(headnorm(mkf(wb, h, c0), 64, C.col(gname), False, m * 512, dst))
            run_pipe(gl)
        wb = load_w(672 + 1024, 512)
        for m, (c0, w) in enumerate(main):
            for j in range(4):
                ps = nxt_ps(0, 4)
                proj_tm(wb, 0, 512, c0 + j * 128, ps.full())
                v_out(1, m * 4 + j, ps.full())
        cpend.extend(list(zip(kxm, kxmg)) + list(zip(vx, vxg)))
        FB = 672 + 1536
        SB = 672 + 1544
        lf_tm = P.sb("lf_tm", [128, KT_L, 8], F32)
        dt_tm = P.sb("dt_tm", [128, KT_L, 16], F32)
        a_tm = P.sb("a_tm", [128, KT_L, 16], F32)
        tmpr = P.sb("tmpr", [128, 16], F32)
        wf = load_w(FB, 8)
        for m, (c0, w) in enumerate(main):
            for j in range(4):
                kt = m * 4 + j
                ps = nxt_ps(0, 4)
                proj_tm(wf, 0, 8, c0 + j * 128, ps[:, 0:8])
                P.dve.tensor_tensor(out=tmpr[:, 0:8], in0=ps[:, 0:8], in1=rowc[:, 0:8], op=ALU.add)
                P.act.activation(out=tmpr[:, 0:8], in_=tmpr[:, 0:8], func=AF.Exp, scale=-1.0)
                P.act.activation(out=tmpr[:, 0:8], in_=tmpr[:, 0:8], func=AF.Ln, bias=one1[:, 0:1], scale=1.0)
                P.dve.tensor_scalar(out=lf_tm[:, kt, :], in0=tmpr[:, 0:8], scalar1=-1.0, scalar2=None, op0=ALU.mult)
        wd = load_w(SB + 1024 + 1536, 16)
        for m, (c0, w) in enumerate(main):
            for j in range(4):
                kt = m * 4 + j
                ps = nxt_ps(0, 4)
                proj_tm(wd, 0, 16, c0 + j * 128, ps[:, 0:16])
                P.dve.tensor_tensor(out=tmpr.full(), in0=ps[:, 0:16], in1=rowc[:, 8:24], op=ALU.add)
                P.act.activation(out=tmpr.full(), in_=tmpr.full(), func=AF.Exp)
                P.act.activation(out=dt_tm[:, kt, :], in_=tmpr.full(), func=AF.Ln, bias=one1[:, 0:1], scale=1.0)
                P.dve.tensor_tensor(out=a_tm[:, kt, :], in0=dt_tm[:, kt, :], in1=Aneg_r.full(), op=ALU.mult)
        P.dma("sync", out=dtd.full(), in_=dt_tm.full())
        P.dma("sync", out=atd.full(), in_=a_tm.full())
        def plain_group(base, ncols, func, bias_name, dst, dst_row0):
            wb_ = load_w(base, ncols)
            for m, (c0, w) in enumerate(main):
                for ch in range(ncols // 128):
                    ps = nxt_ps(0, 4)
                    proj(wb_, ch * 128, 128, c0, 512, ps.full())
                    og = next_ostg()
                    if bias_name is None:
                        P.act.activation(out=og.full(), in_=ps.full(), func=func)
                    else:
                        P.act.activation(out=og.full(), in_=ps.full(), func=func,
                                         bias=C.col(bias_name, (dst_row0 // 128) + ch))
                    out_dma(dst[dst_row0 + ch * 128:dst_row0 + (ch + 1) * 128, m * 512:(m + 1) * 512], og.full())

        for blk in range(2):
            plain_group(SB + blk * 512, 512, AF.Silu, None, szd, blk * 512)
        upre = P.sb("upre", [128, 516], F32)
        carry = P.sb("carry", [128, 4], F32)
        acc0 = P.sb("acc0", [128, 512], F32)
        tstg = [P.sb(f"tstg{i}", [128, 4, 128], BF16) for i in range(2)]
        tcnt = [0]
        trq = []
        for blk in range(3):
            wb = load_w(SB + 1024 + blk * 512, 512)
            for m, (c0, w) in enumerate(main):
                for ch in range(4):
                    cg = blk * 4 + ch
                    ps = nxt_ps(0, 4)
                    proj(wb, ch * 128, 128, c0 - 4, 4, ps[:, 0:4])
                    P.act.copy(out=upre[:, 0:4], in_=ps[:, 0:4])
                    ps = nxt_ps(0, 4)
                    proj(wb, ch * 128, 128, c0, 512, ps.full())
                    while len(trq) > 1:
                        trq.pop(0)()
                    P.act.copy(out=upre[:, 4:516], in_=ps.full())
                    P.act.activation(out=acc0.full(), in_=ps.full(), func=AF.Identity, scale=C.col("cw3", cg), bias=C.col("cb", cg))
                    for k in range(3):
                        P.dve.scalar_tensor_tensor(out=acc0.full(), in0=upre[:, 1 + k:513 + k], scalar=C.col(f"cw{k}", cg),
                                                   in1=acc0.full(), op0=ALU.mult, op1=ALU.add)
                    og = next_ostg()
                    P.act.activation(out=og.full(), in_=acc0.full(), func=AF.Silu)
                    if cg >= 8:
                        out_dma(bct[(cg - 8) * 128:(cg - 7) * 128, m * 512:(m + 1) * 512], og.full())
                    if cg < 10:
                        def mk_tr(og=og, cg=cg, m=m):
                            def f():
                                i2 = tcnt[0] % 2
                                tcnt[0] += 1
                                for j in range(4):
                                    P.pe.transpose(out=pbt[:, i2 * 512 + j * 128:i2 * 512 + (j + 1) * 128],
                                                   in_=og[:, j * 128:(j + 1) * 128], identity=identb.full())
                                ts_ = tstg[i2]
                                P.dve.tensor_copy(out=ts_.full().re("p j f -> p (j f)"), in_=pbt[:, i2 * 512:(i2 + 1) * 512])
                                if cg < 8:
                                    P.dma("sync", out=xtm[:, m * 4:(m + 1) * 4, cg * 128:(cg + 1) * 128], in_=ts_.full())
                                else:
                                    P.dma("sync", out=btm[:, m * 4:(m + 1) * 4, (cg - 8) * 128:(cg - 7) * 128], in_=ts_.full())
                            return f
                        trq.append(mk_tr())
        while trq:
            trq.pop(0)()
        GB = SB + 2576
        for blk in range(6):
            plain_group(GB + blk * 512, 512, AF.Sigmoid, "b_gate", gd, blk * 512)
        fxs = P.sb("fxs", [128, 224], F32)
        within = P.sb("within", [128, KT_L, 8], F32)
        ttot = P.sb("ttot", [128, KT_L, 8], F32)
        f2 = lambda b: b.full().re("p a b -> p (a b)")
        ps = nxt_ps(0, 4)
        P.pe.matmul(out=ps[:, 0:128], lhsT=tri, rhs=f2(lf_tm), start=True, stop=True)
        P.act.copy(out=f2(within), in_=ps[:, 0:128])
        ps = nxt_ps(0, 4)
        P.pe.matmul(out=ps[:, 0:128], lhsT=ones, rhs=f2(lf_tm), start=True, stop=True)
        P.act.copy(out=f2(ttot), in_=ps[:, 0:128])
        Floc = fxs[:, 0:128].re("p (a b) -> p a b", b=8)
        totv = fxs[:, 128:160].re("p (a b) -> p a b", b=8)
        cacc = P.sb("cacc", [128, 8], F32)
        for m in range(4):
            P.dve.tensor_copy(out=Floc[:, 4 * m, :], in_=within[:, 4 * m, :])
            P.dve.tensor_copy(out=cacc.full(), in_=ttot[:, 4 * m, :])
            for j in range(1, 4):
                P.dve.tensor_tensor(out=Floc[:, 4 * m + j, :], in0=within[:, 4 * m + j, :], in1=cacc.full(), op=ALU.add)
                P.dve.tensor_tensor(out=cacc.full(), in0=cacc.full(), in1=ttot[:, 4 * m + j, :], op=ALU.add)
            P.dve.tensor_copy(out=totv[:, m, :], in_=cacc.full())
        ps = nxt_ps(0, 4)
        P.pe.transpose(out=ps[:, 0:128], in_=fxs[:, 0:128], identity=ident)
        FT = P.sb("FT", [128, 128], F32)
        r1 = P.sb("r1", [128, 128], F32)
        fh = [P.sb(f"fh{i}", [128, 128], BF16) for i in range(3)]
        P.act.copy(out=FT.full(), in_=ps[:, 0:128])
        P.dve.tensor_copy(out=fh[0].full(), in_=FT.full())
        P.dve.tensor_tensor(out=r1.full(), in0=FT.full(), in1=fh[0].full(), op=ALU.subtract)
        P.dve.tensor_copy(out=fh[1].full(), in_=r1.full())
        P.dve.tensor_tensor(out=r1.full(), in0=r1.full(), in1=fh[1].full(), op=ALU.subtract)
        P.dve.tensor_copy(out=fh[2].full(), in_=r1.full())
        nfh = [P.sb(f"nfh{i}", [128, 128], BF16) for i in range(3)]
        for r in range(3):
            P.dve.tensor_scalar(out=nfh[r].full(), in0=fh[r].full(), scalar1=-1.0, scalar2=None, op0=ALU.mult)
            for kt in range(KT_L):
                P.dma("sync" if kt % 2 else "scalar", out=fq[:, r, kt * 128:(kt + 1) * 128], in_=fh[r][kt * 8:(kt + 1) * 8, :])
                P.dma("scalar" if kt % 2 else "sync", out=nfq[:, r, kt * 128:(kt + 1) * 128], in_=nfh[r][kt * 8:(kt + 1) * 8, :])
        for m in range(4):
            P.dma("sync", out=kxf[m][512:536, :].re("(h r) c -> h r c", r=3), in_=nfq[:, :, m * 512:(m + 1) * 512])
        P.dma("sync", out=fx[:, 0:160], in_=fxs[:, 0:160])
        gather_pairs(cpend + list(zip(kxf, kxfg)))
        del cpend[:]

    def ssd_scan(l, K, pass1, fxs=None, dt_tm=None, a_tm=None, Hinit=None, rowc=None, pb=None):
        cmb = K["cm"]
        tri, trimask, ones = cmb[:, 0, :], cmb[:, 1, :], cmb[:, 3, :]
        if pb is None:
            pb = [P.ps(f"spb{i}", [128, 512], F32) for i in range(7)]
        if pass1:
            fxs = P.sb("decs", [128, 224], F32)
        if dt_tm is None:
            dt_tm = P.sb("dt_tm", [128, KT_L, 16], F32)
            a_tm = P.sb("a_tm", [128, KT_L, 16], F32)
            P.dma("sync", out=dt_tm.full(), in_=dtd.full())
            P.dma("sync", out=a_tm.full(), in_=atd.full())
        fl = lambda b: b.full().re("p c h -> p (c h)")
        Acum = P.sb("Acum", [128, KT_L, 16], F32)
        Atot = P.sb("Atot", [128, KT_L, 16], F32)
        wdec = P.sb("wdec", [128, KT_L, 16], F32)
        eAtot = P.sb("eAtot", [128, KT_L, 16], F32)
        psA = pb[0]
        P.pe.matmul(out=psA[:, 0:256], lhsT=tri, rhs=fl(a_tm), start=True, stop=True)
        P.act.copy(out=fl(Acum), in_=psA[:, 0:256])
        P.pe.matmul(out=psA[:, 256:512], lhsT=ones, rhs=fl(a_tm), start=True, stop=True)
        P.act.copy(out=fl(Atot), in_=psA[:, 256:512])
        P.act.activation(out=fl(eAtot), in_=fl(Atot), func=AF.Exp)
        P.dve.tensor_tensor(out=fl(wdec), in0=fl(Atot), in1=fl(Acum), op=ALU.subtract)
        P.act.activation(out=fl(wdec), in_=fl(wdec), func=AF.Exp)
        if not pass1:
            nAcum = P.sb("nAcum", [128, KT_L, 16], F32)
            eA = P.sb("eA", [128, KT_L, 16], F32)
            P.dve.tensor_scalar(out=fl(nAcum), in0=fl(Acum), scalar1=-1.0, scalar2=None, op0=ALU.mult)
            P.act.activation(out=fl(eA), in_=fl(Acum), func=AF.Exp)
            BCs = P.sb("BCs", [128, 4, T], BF16)
            P.dma("gpsimd", out=BCs.full(), in_=bct.full().re("(a p) t -> p a t", p=128))
            cb = P.sb("cb", [128, 2, 128], F32)
            NH = 4
            at = [P.sb(f"at{i}", [128, 128], F32) for i in range(NH)]
            tm = [P.sb(f"tm{i}", [128, 128], F32) for i in range(NH)]
            dec = [P.sb(f"dec{i}", [128, 128], F32) for i in range(NH)]
            MT = [P.sb(f"MT{i}", [128, 128], BF16) for i in range(NH)]
            t1 = P.sb("t1", [128, 1024], F32)
            t3 = P.sb("t3", [128, 1024], F32)
            yo = P.sb("yo", [128, 1024], BF16)
            yT = [P.sb(f"yT{i}", [128, 4, 128], F32) for i in range(2)]
        Hs = P.sb("Hs", [128, 1024], F32)
        Hb = P.sb("Hb", [128, 1024], BF16)
        xc = [P.sb(f"xc{i}", [128, 1024], BF16) for i in range(2)]
        Bc = [P.sb(f"Bc{i}", [128, 256], BF16) for i in range(2)]
        xdt = P.sb("xdt", [128, 1024], BF16)
        xdts = P.sb("xdts", [128, 1024], BF16)
        dsum = P.sb("dsum", [128, 16], F32)
        v3 = lambda v: v.re("p (h d) -> p h d", h=16)
        bc3 = lambda v: v.f(lambda a: a.unsqueeze(2).to_broadcast([128, 16, 64]))
        for m in range(4):
            if pass1:
                P.dve.memset(ap=Hs.full(), constant=0.0)
                P.dve.memset(ap=dsum.full(), constant=0.0)
            else:
                P.dve.tensor_copy(out=Hs.full(), in_=Hinit[:, m, :])
                P.act.copy(out=Hb.full(), in_=Hinit[:, m, :])
            for j in range(4):
                c = m * 4 + j
                x_c = xc[c % 2]
                B_c = Bc[c % 2]
                P.dma("sync", out=x_c.full(), in_=xtm[:, c, :])
                P.dma("scalar" if pass1 else "gpsimd", out=B_c.full(), in_=btm[:, c, :])
                P.dve.tensor_tensor(out=v3(xdt.full()), in0=v3(x_c.full()), in1=bc3(dt_tm[:, c, :]), op=ALU.mult)
                (P.dve if pass1 else P.pool).tensor_tensor(out=v3(xdts.full()), in0=v3(xdt.full()), in1=bc3(wdec[:, c, :]), op=ALU.mult)
                if not pass1:
                    cs = slice(c * 128, (c + 1) * 128)
                    ps_cb = pb[1]
                    for g in range(2):
                        P.pe.matmul(out=ps_cb[:, g * 128:(g + 1) * 128], lhsT=BCs[:, g, cs], rhs=BCs[:, 2 + g, cs],
                                    start=True, stop=True)
                    P.act.copy(out=cb.full().re("p a b -> p (a b)"), in_=ps_cb[:, 0:256])
                    ps_off = [pb[2], pb[3]]
                    for g in range(2):
                        P.pe.matmul(out=ps_off[g].full(), lhsT=BCs[:, 2 + g, cs], rhs=Hb[:, g * 512:(g + 1) * 512],
                                    start=True, stop=True)
                    ps_y = [pb[4], pb[5]]
                    def st1(h):
                        i2 = h % NH
                        g = h // 8
                        P.dve.tensor_scalar(out=at[i2].full(), in0=tri, scalar1=a_tm[:, c, h:h + 1], scalar2=None, op0=ALU.mult)
                        ps_A = pb[6]
                        P.pe.matmul(out=ps_A[:, i2 * 128:(i2 + 1) * 128], lhsT=ones, rhs=at[i2].full(), start=True, stop=True)
                        P.dve.tensor_tensor(out=tm[i2].full(), in0=ps_A[:, i2 * 128:(i2 + 1) * 128], in1=trimask, op=ALU.add)
                        P.act.activation(out=dec[i2].full(), in_=tm[i2].full(), func=AF.Exp, bias=nAcum[:, c, h:h + 1], scale=1.0)
                        P.pool.tensor_tensor(out=MT[i2].full(), in0=cb[:, g, :], in1=dec[i2].full(), op=ALU.mult)

                    def st2(h):
                        i2 = h % NH
                        g = h // 8
                        hh = h % 8
                        P.pe.matmul(out=ps_y[g][:, hh * 64:(hh + 1) * 64], lhsT=MT[i2].full(), rhs=xdt[:, h * 64:(h + 1) * 64],
                                    start=True, stop=True)

                    for hq in range(16 + 3):
                        if hq < 16:
                            st1(hq)
                        if hq >= 3:
                            st2(hq - 3)
                    for g in range(2):
                        gs_ = slice(g * 512, (g + 1) * 512)
                        v8 = lambda v: v.re("p (h d) -> p h d", h=8)
                        b8 = lambda v: v.f(lambda a: a.unsqueeze(2).to_broadcast([128, 8, 64]))
                        P.dve.tensor_tensor(out=v8(t1[:, gs_]), in0=v8(ps_off[g].full()), in1=b8(eA[:, c, g * 8:(g + 1) * 8]), op=ALU.mult)
                        P.dve.tensor_tensor(out=t1[:, gs_], in0=t1[:, gs_], in1=ps_y[g].full(), op=ALU.add)
                    P.pool.tensor_tensor(out=v3(t3.full()), in0=v3(x_c.full()), in1=bc3(rowc[:, 40:56]), op=ALU.mult)
                    P.pool.tensor_tensor(out=t3.full(), in0=t1.full(), in1=t3.full(), op=ALU.add)
                    for q4 in range(2):
                        pst = pb[2 + q4]
                        for jj in range(4):
                            fc = q4 * 4 + jj
                            P.pe.transpose(out=pst[:, jj * 128:(jj + 1) * 128], in_=t3[:, fc * 128:(fc + 1) * 128],
                                           identity=cmb[:, 2, :])
                        yt = yT[q4]
                        P.act.copy(out=yt.full().re("p a b -> p (a b)"), in_=pst.full())
                        P.dma("sync", out=yd[q4 * 512:(q4 + 1) * 512, c * 128:(c + 1) * 128].re("(a p) t -> p a t", p=128),
                              in_=yt.full())
                ps_h = [pb[0], pb[1]] if pass1 else [pb[4], pb[5]]
                for g in range(2):
                    P.pe.matmul(out=ps_h[g].full(), lhsT=B_c[:, g * 128:(g + 1) * 128], rhs=xdts[:, g * 512:(g + 1) * 512],
                                start=True, stop=True)
                P.dve.tensor_tensor(out=v3(Hs.full()), in0=v3(Hs.full()), in1=bc3(eAtot[:, c, :]), op=ALU.mult)
                for g in range(2):
                    P.dve.tensor_tensor(out=Hs[:, g * 512:(g + 1) * 512], in0=Hs[:, g * 512:(g + 1) * 512], in1=ps_h[g].full(), op=ALU.add)
                if pass1:
                    P.dve.tensor_tensor(out=dsum.full(), in0=dsum.full(), in1=Atot[:, c, :], op=ALU.add)
                else:
                    P.act.copy(out=Hb.full(), in_=Hs.full())
            if pass1:
                P.dma("sync", out=sx[m // 2][(m % 2) * 128:(m % 2 + 1) * 128, :], in_=Hs.full())
                P.act.activation(out=fxs[:, 160 + m * 16:160 + (m + 1) * 16], in_=dsum.full(), func=AF.Exp)
        if pass1:
            P.dma("sync", out=fx[:, 160:224], in_=fxs[:, 160:224])

    def load_fg():
        fg = P.sb("fg", [128, 4, 224], F32)
        P.dma("sync", out=fg.full(), in_=fxg.full().re("(r p) c -> p r c", p=128))
        return fg

    def phase_attn(l):
        K = load_consts()
        sel, zero = K["sel"], K["zero"]
        mskb = P.sb("mskb", [128, 8, 512], F32)
        P.dma("gpsimd", out=mskb.full(), in_=msk_d.full())
        fg = load_fg()
        offs = P.sb("offs", [128, 16, 8], F32)
        run = P.sb("run", [128, 8], F32)
        P.dve.memset(ap=run.full(), constant=0.0)
        for s_ in range(16):
            m, r = divmod(s_, 4)
            P.dve.tensor_copy(out=offs[:, s_, :], in_=run.full())
            P.dve.tensor_tensor(out=run.full(), in0=run.full(), in1=fg[:, r, 128 + m * 8:128 + (m + 1) * 8], op=ALU.add)
        offown = P.sb("offown", [128, 4, 8], F32)
        P.dve.memset(ap=offown.full(), constant=0.0)
        for m in range(4):
            for r in range(4):
                P.dve.scalar_tensor_tensor(out=offown[:, m, :], in0=offs[:, 4 * m + r, :], scalar=sel[:, 8 + 4 * m + r:9 + 4 * m + r],
                                           in1=offown[:, m, :], op0=ALU.mult, op1=ALU.add)
        btab = P.sb("btab", [128, 4, 16, 8], F32)
        for m in range(4):
            P.dve.tensor_tensor(out=btab[:, m, :, :], in0=offown[:, m, :].f(lambda a: a.unsqueeze(1).to_broadcast([128, 16, 8])),
                                in1=offs.full(), op=ALU.subtract)
            for jr in range(4):
                s_ = 4 * m + jr
                P.dve.tensor_scalar(out=btab[:, m, s_, :], in0=btab[:, m, s_, :], scalar1=sel[:, 28 + jr:29 + jr], scalar2=None,
                                    op0=ALU.add)

        pss = [P.ps(f"pss{i}", [128, 1024], F32) for i in range(3)]
        pbo = [P.ps(f"pbo{i}", [128, 512], F32) for i in range(2)]
        K_sb = [P.sb(f"K_sb{i}", [96, S_], BF16) for i in range(2)]
        Q_sb = [P.sb(f"Q_sb{i}", [96, T], BF16) for i in range(2)]
        V_sb = [P.sb(f"V_sb{i}", [128, NKT, 128], BF16) for i in range(2)]
        for i in range(2):
            P.dve.memset(ap=V_sb[i][:, :, 64:128], constant=1.0)
        NSB = 3
        LA = 2
        pt = [P.sb(f"pt{i}", [128, 1024], BF16) for i in range(NSB)]
        mt = [P.sb(f"mt{i}", [128, 1024], F32) for i in range(2)]
        rl = P.sb("rl", [128, 512], F32)
        rl2 = P.sb("rl2", [64, 512], F32)
        ot = [P.sb(f"ot{i}", [64, 512], BF16) for i in range(2)]
        cnt = [0, 0, 0]
        heads = [(0, h) for h in range(8)] + [(1, h) for h in range(8)]

        def loads(idx):
            kind, h = heads[idx]
            i = idx % 2
            nd = 96 if kind == 0 else 64
            if kind == 1:
                P.dve.memset(ap=K_sb[i][64:96, :], constant=0.0)
                P.dve.memset(ap=K_sb[i][64:67, :], constant=8.0)
                P.pool.memset(ap=Q_sb[i][64:96, :], constant=0.0)
                P.pool.memset(ap=Q_sb[i][64:70, :], constant=8.0)
            for r in range(4):
                vr = r * 2048 + kind * 1024 + h * 128
                for m in range(4):
                    s0 = (4 * m + r) * 512
                    if kind == 0:
                        ksrc = kxmg[m][r * 768 + h * 96:r * 768 + (h + 1) * 96, :]
                    else:
                        ksrc = kxfg[m][r * 640 + h * 64:r * 640 + (h + 1) * 64, :]
                        P.dma("gpsimd" if (r + m) % 2 == 0 else "sync", out=K_sb[i][67:70, s0:s0 + 512],
                              in_=kxfg[m][r * 640 + 512 + h * 3:r * 640 + 512 + (h + 1) * 3, :])
                    P.dma("sync" if (r + m) % 2 == 0 else "gpsimd", out=K_sb[i][0:nd, s0:s0 + 512], in_=ksrc)
                    g0 = (4 * m + r) * 4
                    P.dma("gpsimd" if (r + m) % 2 == 0 else "sync",
                          out=V_sb[i][:, g0:g0 + 4, 0:64],
                          in_=vxg[m][vr:vr + 128, :].re("p (j d) -> p j d", j=4))
            if kind == 0:
                P.dma("sync", out=Q_sb[i][0:96, :], in_=qm[h])
            else:
                P.dma("sync", out=Q_sb[i][0:64, :], in_=qf[h])
                P.dma("gpsimd", out=Q_sb[i][64:67, :], in_=fq[h])

        iters = []
        for idx in range(16):
            for m in range(4):
                nkp = (4 * m + 4) * 2
                for kp in range(nkp):
                    iters.append((idx, m, kp, nkp))

        def stage_qk(n):
            idx, m, kp, nkp = iters[n]
            kind, h = heads[idx]
            i = idx % 2
            scale = 96.0 ** -0.5 if kind == 0 else 0.125
            i3 = n % NSB
            ps = pss[i3]
            for e in range(2):
                kt = 2 * kp + e
                P.pe.matmul(out=ps[:, e * 512:(e + 1) * 512], lhsT=K_sb[i][0:96, kt * 128:(kt + 1) * 128],
                            rhs=Q_sb[i][0:96, m * 512:(m + 1) * 512], start=True, stop=True)
            blk = (2 * kp) // 4
            if blk >= 4 * m:
                jr = blk - 4 * m
                k4 = (2 * kp) % 4
                mm = mt[cnt[1] % 2]
                cnt[1] += 1
                P.dve.scalar_tensor_tensor(out=mm.full(), in0=mskb[:, kind * 4 + k4:kind * 4 + k4 + 2, :].re("p a b -> p (a b)"),
                                           scalar=sel[:, 24 + jr:25 + jr], in1=ps.full(), op0=ALU.mult, op1=ALU.add)
                src = mm.full()
                bias = sel[:, 28 + jr:29 + jr] if kind == 0 else btab[:, m, blk, h:h + 1]
            else:
                src = ps.full()
                bias = zero[:, 0:1] if kind == 0 else btab[:, m, blk, h:h + 1]
            P.act.activation(out=pt[i3].full(), in_=src, func=AF.Exp, scale=scale, bias=bias)

        def stage_pv(n):
            idx, m, kp, nkp = iters[n]
            kind, h = heads[idx]
            i = idx % 2
            oacc = pbo[(idx * 4 + m) % 2]
            for e in range(2):
                kt = 2 * kp + e
                P.pe.matmul(out=oacc.full(), lhsT=V_sb[i][:, kt, :], rhs=pt[n % NSB][:, e * 512:(e + 1) * 512],
                            start=(kp == 0 and e == 0), stop=(kp == nkp - 1 and e == 1))
            if kp == nkp - 1:
                odst = omd if kind == 0 else ofd
                P.dve.reciprocal(out=rl[64:128, :], in_=oacc[64:128, :])
                P.dve.tensor_copy(out=rl2.full(), in_=rl[64:128, :])
                o = ot[m % 2]
                P.dve.tensor_tensor(out=o.full(), in0=oacc[0:64, :], in1=rl2.full(), op=ALU.mult)
                P.dma("sync", out=odst[h * 64:(h + 1) * 64, m * 512:(m + 1) * 512], in_=o.full())

        pcs = [P.sb(f"pcs{i}", [128, 2048], F32) for i in range(2)]
        pcb = [P.sb(f"pcb{i}", [128, 2048], BF16) for i in range(2)]
        jobs = []
        for (src, dst, rows, cols) in ((w_a[l], wab, 512, D), (w_b[l], wbb, 512, D), (w_c[l], wcb, 1024, D), (w_o[l], wob, 1024, D),
                                       (w_up[l], wub, D, 5632), (w_dn[l], wdb, 2816, D)):
            for r0 in range(0, rows, 128):
                for c0 in range(0, cols, 2048):
                    n_ = min(2048, cols - c0)
                    jobs.append((src[r0:r0 + 128, c0:c0 + n_], dst[r0:r0 + 128, c0:c0 + n_], n_))
        jcnt = [0]

        def precast_one():
            if jcnt[0] >= len(jobs):
                return
            src, dst, n_ = jobs[jcnt[0]]
            i = jcnt[0] % 2
            jcnt[0] += 1
            P.dma("gpsimd", out=pcs[i][:, 0:n_], in_=src)
            P.pool.tensor_copy(out=pcb[i][:, 0:n_], in_=pcs[i][:, 0:n_])
            P.dma("gpsimd", out=dst, in_=pcb[i][:, 0:n_])

        every = max(1, len(iters) // (len(jobs) + 4))
        loads(0)
        loads(1)
        for n in range(len(iters) + LA):
            if n < len(iters):
                stage_qk(n)
            if n >= LA:
                stage_pv(n - LA)
                idx_p, m_p, kt_p, nk_p = iters[n - LA]
                if m_p == 3 and kt_p == nk_p - 1 and idx_p + 2 < 16:
                    loads(idx_p + 2)
            if n % every == every - 1:
                precast_one()
        while jcnt[0] < len(jobs):
            precast_one()

    def phase_ssd2(l):
        K = load_consts()
        sel = K["sel"]
        rowc = P.sb("rowc", [128, 56], F32)
        P.dma("sync", out=rowc.full(), in_=rowc_d[l])
        fg = load_fg()
        Hin = P.sb("Hin", [128, 1024], F32)
        Hsel = P.sb("Hsel", [128, 4, 1024], F32)
        Sst = [P.sb(f"Sst{i}", [128, 1024], F32) for i in range(2)]
        P.dve.memset(ap=Hin.full(), constant=0.0)
        P.dve.memset(ap=Hsel.full(), constant=0.0)
        v3 = lambda v: v.re("p (h d) -> p h d", h=16)
        for s_ in range(16):
            m, r = divmod(s_, 4)
            P.dve.scalar_tensor_tensor(out=Hsel[:, m, :], in0=Hin.full(), scalar=sel[:, 8 + s_:9 + s_], in1=Hsel[:, m, :],
                                       op0=ALU.mult, op1=ALU.add)
            if s_ < 15:
                st_ = Sst[s_ % 2]
                P.dma("sync" if s_ % 2 else "gpsimd", out=st_.full(),
                      in_=sxg[m // 2][r * 256 + (m % 2) * 128:r * 256 + (m % 2 + 1) * 128, :])
                dcs = fg[:, r, 160 + m * 16:160 + (m + 1) * 16]
                P.dve.tensor_tensor(out=v3(Hin.full()), in0=v3(Hin.full()),
                                    in1=dcs.f(lambda a: a.unsqueeze(2).to_broadcast([128, 16, 64])), op=ALU.mult)
                P.pool.tensor_tensor(out=Hin.full(), in0=Hin.full(), in1=st_.full(), op=ALU.add)
        ssd_scan(l, K, pass1=False, Hinit=Hsel, rowc=rowc)

    def write_tails(txs):
        P.dma("sync", out=tx.full(), in_=txs.full().re("p m k c -> p (m k c)"))

    def halo_exchange(dst):
        K = load_consts()
        sel = K["sel"]
        P.pool.collective_compute(kind="AllGather", op=ALU.bypass, replica_groups=RG, ins=[tx.full()], outs=[txg.full()])
        tg = P.sb("tg", [128, 4, 128], F32)
        P.dma("sync", out=tg.full(), in_=txg.full().re("(r p) c -> p r c", p=128))
        hl = P.sb("hl", [128, 4, 32], F32)
        P.dve.memset(ap=hl.full(), constant=0.0)
        for m in range(4):
            for r in range(4):
                P.dve.scalar_tensor_tensor(out=hl[:, m, :], in0=tg[:, r, m * 32:(m + 1) * 32], scalar=sel[:, r:r + 1],
                                           in1=hl[:, m, :], op0=ALU.mult, op1=ALU.add)
            if m >= 1:
                P.dve.scalar_tensor_tensor(out=hl[:, m, :], in0=tg[:, 3, (m - 1) * 32:m * 32], scalar=sel[:, 4:5],
                                           in1=hl[:, m, :], op0=ALU.mult, op1=ALU.add)
        dv = dst.full().re("(kc p) n -> p kc n", p=128)
        for m in range(4):
            P.dma("sync", out=dv[:, :, m * SW:m * SW + 4], in_=hl[:, m, :].re("p (k c) -> p k c", c=4))

    def phase_merge(l):
        K = load_consts()
        ones, eps = K["cm"][:, 3, :], K["eps"]
        gsb = P.sb("gsb", [128, 8], F32)
        P.dma("sync", out=gsb.full(), in_=gssm_d[l])
        pb = [P.ps(f"pb{i}", [128, 512], F32) for i in range(8)]
        def load_wb(dram_bf, kc_n, name, q):
            bfb = P.sb(name, [128, kc_n, D], BF16)
            P.dma(q, out=bfb.full(), in_=dram_bf.full().re("(kc p) n -> p kc n", p=128))
            return bfb

        Wa = load_wb(wab, 4, "Wa", "sync")
        Wb = load_wb(wbb, 4, "Wb", "gpsimd")
        Wc = load_wb(wcb, 8, "Wc", "sync")
        Wo = load_wb(wob, 8, "Wo", "gpsimd")
        ys = P.sb("ys", [128, 8, 512], F32)
        szs = P.sb("szs", [128, 8, 512], BF16)
        yn = P.sb("yn", [128, 8, 512], BF16)
        omsL = [P.sb(f"oms{i}", [128, 4, 512], BF16) for i in range(2)]
        ofsL = [P.sb(f"ofs{i}", [128, 4, 512], BF16) for i in range(2)]
        gsL = [P.sb(f"gs{i}", [128, 24, 512], BF16) for i in range(2)]
        xs = P.sb("xs", [128, 8, 512], F32)
        sq = P.sb("sq", [128, 512], F32)
        rstd = P.sb("rstd", [128, 512], F32)
        m1 = [P.sb(f"m1_{i}", [128, 512], F32) for i in range(2)]
        m2 = [P.sb(f"m2_{i}", [128, 512], F32) for i in range(2)]
        m3 = [P.sb(f"m3_{i}", [128, 512], F32) for i in range(2)]
        mg = P.sb("mg", [128, 8, 512], BF16)
        xo = [P.sb(f"xo{i}", [128, 512], F32) for i in range(2)]
        txs = P.sb("txs", [128, 4, 8, 4], F32)
        ch = lambda d: d.full().re("(kc p) n -> p kc n", p=128)
        xmv = ch(xmid)
        def ld_norm(ti):
            ts = slice(ti * 512, (ti + 1) * 512)
            P.dma("sync", out=ys.full(), in_=ch(yd)[:, :, ts])
            P.dma("gpsimd", out=szs.full(), in_=ch(szd)[:, :, ts])

        def ld_rest(ti):
            ts = slice(ti * 512, (ti + 1) * 512)
            P.dma("sync", out=omsL[ti % 2].full(), in_=ch(omd)[:, :, ts])
            P.dma("gpsimd", out=ofsL[ti % 2].full(), in_=ch(ofd)[:, :, ts])
            P.dma("sync", out=gsL[ti % 2].full(), in_=ch(gd)[:, :, ts])

        ld_norm(0)
        ld_rest(0)
        for ti in range(4):
            ts = slice(ti * 512, (ti + 1) * 512)
            xsl = slice(ti * SW + 4, ti * SW + 516)
            oms, ofs, gs = omsL[ti % 2], ofsL[ti % 2], gsL[ti % 2]
            P.dma("gpsimd", out=xs.full(), in_=ch(xb[l])[:, :, xsl])
            ps = pb[7]
            for kc in range(8):
                P.dve.tensor_tensor(out=ys[:, kc, :], in0=ys[:, kc, :], in1=szs[:, kc, :], op=ALU.mult)
                P.act.activation(out=sq.full(), in_=ys[:, kc, :], func=AF.Square)
                P.pe.matmul(out=ps.full(), lhsT=ones, rhs=sq.full(), start=(kc == 0), stop=(kc == 7))
            P.act.activation(out=rstd.full(), in_=ps.full(), func=AF.Ln, bias=eps[:, 0:1], scale=1.0 / 1024.0)
            P.act.activation(out=rstd.full(), in_=rstd.full(), func=AF.Exp, scale=-0.5)
            for kc in range(8):
                P.dve.scalar_tensor_tensor(out=yn[:, kc, :], in0=ys[:, kc, :], scalar=gsb[:, kc:kc + 1], in1=rstd.full(),
                                           op0=ALU.mult, op1=ALU.mult)
            if ti + 1 < 4:
                ld_norm(ti + 1)
                ld_rest(ti + 1)
            for oc in range(8):
                i2 = oc % 2
                osl = slice(oc * 128, (oc + 1) * 128)
                pa, pbb, pc = pb[0 + i2 * 3], pb[1 + i2 * 3], pb[2 + i2 * 3]
                for kc in range(4):
                    P.pe.matmul(out=pa.full(), lhsT=Wa[:, kc, osl], rhs=oms[:, kc, :], start=(kc == 0), stop=(kc == 3))
                for kc in range(4):
                    P.pe.matmul(out=pbb.full(), lhsT=Wb[:, kc, osl], rhs=ofs[:, kc, :], start=(kc == 0), stop=(kc == 3))
                for kc in range(8):
                    P.pe.matmul(out=pc.full(), lhsT=Wc[:, kc, osl], rhs=yn[:, kc, :], start=(kc == 0), stop=(kc == 7))
                P.dve.tensor_tensor(out=m1[i2].full(), in0=pa.full(), in1=gs[:, oc, :], op=ALU.mult)
                P.dve.tensor_tensor(out=m2[i2].full(), in0=pbb.full(), in1=gs[:, 8 + oc, :], op=ALU.mult)
                P.dve.tensor_tensor(out=m3[i2].full(), in0=pc.full(), in1=gs[:, 16 + oc, :], op=ALU.mult)
                P.pool.tensor_tensor(out=m1[i2].full(), in0=m1[i2].full(), in1=m2[i2].full(), op=ALU.add)
                P.pool.tensor_tensor(out=mg[:, oc, :], in0=m1[i2].full(), in1=m3[i2].full(), op=ALU.add)
            for oc in range(8):
                i2 = oc % 2
                ps = pb[6 + i2]
                for kc in range(8):
                    P.pe.matmul(out=ps.full(), lhsT=Wo[:, kc, oc * 128:(oc + 1) * 128], rhs=mg[:, kc, :],
                                start=(kc == 0), stop=(kc == 7))
                P.dve.tensor_tensor(out=xo[i2].full(), in0=ps.full(), in1=xs[:, oc, :], op=ALU.add)
                P.pool.tensor_copy(out=txs[:, ti, oc, :], in_=xo[i2][:, 508:512])
                P.dma("sync" if i2 else "gpsimd", out=xmv[:, oc, xsl], in_=xo[i2].full())
        write_tails(txs)

    def phase_ffn(l, last):
        K = load_consts()
        ones, eps = K["cm"][:, 3, :], K["eps"]
        cstb = P.sb("cstb", [128, offD2["_n"]], F32)
        C = Cst(P, cstb, offD2)
        P.dma("sync", out=cstb.full(), in_=cstD_d[l])
        pb = [P.ps(f"pb{i}", [128, 512], F32) for i in range(8)]
        Wu = P.sb("Wu", [128, 8, 5632], BF16)
        Wd = P.sb("Wd", [128, 22, D], BF16)
        wubv = wub.full().re("(kc p) n -> p kc n", p=128)
        for (c0, c1) in ((0, 512), (2816, 3328), (512, 2816), (3328, 5632)):
            P.dma("sync" if c0 < 2816 else "gpsimd", out=Wu[:, :, c0:c1], in_=wubv[:, :, c0:c1])
        wdbv = wdb.full().re("(kc p) n -> p kc n", p=128)
        P.dma("sync", out=Wd[:, 0:11, :], in_=wdbv[:, 0:11, :])
        P.dma("gpsimd", out=Wd[:, 11:22, :], in_=wdbv[:, 11:22, :])
        xst = P.sb("xst", [128, 8, 512], F32)
        hn = P.sb("hn", [128, 8, 512], BF16)
        sq = P.sb("sq", [128, 512], F32)
        rstd = P.sb("rstd", [128, 512], F32)
        act = P.sb("act", [128, 22, 512], BF16)
        upre = [P.sb(f"upre{i}", [128, 516], F32) for i in range(2)]
        acc = [P.sb(f"acc{i}", [128, 512], F32) for i in range(2)]
        sg = P.sb("sg", [128, 512], F32)
        carry = P.sb("carry", [128, 44, 4], F32)
        xo = [P.sb(f"xo{i}", [128, 512], F32) for i in range(2)]
        txs = P.sb("txs", [128, 4, 8, 4], F32)
        xTv = xmid.full().re("(kc p) n -> p kc n", p=128)
        dst = out if last else xb[l + 1]
        dv = dst.full().re("(kc p) n -> p kc n", p=128)
        tiles = []
        for m in range(4):
            tiles.append((m * SW, 4, True, m))
            tiles.append((m * SW + 4, 512, False, m))
        pcnt = [0]
        for (c0, w, is_halo, m) in tiles:
            P.dma("sync", out=xst[:, :, 0:w], in_=xTv[:, :, c0:c0 + w])
            ps = pb[7]
            for kc in range(8):
                P.act.activation(out=sq[:, 0:w], in_=xst[:, kc, 0:w], func=AF.Square)
                P.pe.matmul(out=ps[:, 0:w], lhsT=ones, rhs=sq[:, 0:w], start=(kc == 0), stop=(kc == 7))
            P.act.activation(out=rstd[:, 0:w], in_=ps[:, 0:w], func=AF.Ln, bias=eps[:, 0:1], scale=1.0 / 1024.0)
            P.act.activation(out=rstd[:, 0:w], in_=rstd[:, 0:w], func=AF.Exp, scale=-0.5)
            for kc in range(8):
                P.dve.scalar_tensor_tensor(out=hn[:, kc, 0:w], in0=xst[:, kc, 0:w], scalar=C.col("g_ffn", kc),
                                           in1=rstd[:, 0:w], op0=ALU.mult, op1=ALU.mult)
            for i in range(22):
                accs = []
                for j, cg in enumerate((i, 22 + i)):
                    ps = pb[pcnt[0] % 4]
                    pcnt[0] += 1
                    for kc in range(8):
                        P.pe.matmul(out=ps[:, 0:w], lhsT=Wu[:, kc, cg * 128:(cg + 1) * 128], rhs=hn[:, kc, 0:w],
                                    start=(kc == 0), stop=(kc == 7))
                    if is_halo:
                        P.act.copy(out=carry[:, cg, :], in_=ps[:, 0:4])
                        continue
                    up = upre[j]
                    a0 = acc[j]
                    P.act.copy(out=up[:, 4:516], in_=ps.full())
                    P.act.activation(out=a0.full(), in_=ps.full(), func=AF.Identity, scale=C.col("fw2", cg), bias=C.col("fb", cg))
                    P.pool.tensor_copy(out=up[:, 0:4], in_=carry[:, cg, :])
                    P.dve.scalar_tensor_tensor(out=a0.full(), in0=up[:, 3:515], scalar=C.col("fw1", cg), in1=a0.full(),
                                               op0=ALU.mult, op1=ALU.add)
                    P.dve.scalar_tensor_tensor(out=a0.full(), in0=up[:, 2:514], scalar=C.col("fw0", cg), in1=a0.full(),
                                               op0=ALU.mult, op1=ALU.add)
                    accs.append(a0)
                if is_halo:
                    continue
                P.act.activation(out=sg.full(), in_=accs[0].full(), func=AF.Silu)
                P.pool.tensor_tensor(out=act[:, i, :], in0=sg.full(), in1=accs[1].full(), op=ALU.mult)
            if is_halo:
                continue
            for oc in range(8):
                i2 = oc % 2
                ps = pb[4 + i2]
                for i in range(22):
                    P.pe.matmul(out=ps.full(), lhsT=Wd[:, i, oc * 128:(oc + 1) * 128], rhs=act[:, i, :],
                                start=(i == 0), stop=(i == 21))
                P.dve.tensor_tensor(out=xo[i2].full(), in0=ps.full(), in1=xst[:, oc, :], op=ALU.add)
                if last:
                    P.dma("sync" if i2 else "gpsimd", out=dv[:, oc, m * 512:(m + 1) * 512], in_=xo[i2].full())
                else:
                    P.pool.tensor_copy(out=txs[:, m, oc, :], in_=xo[i2][:, 508:512])
                    P.dma("sync" if i2 else "gpsimd", out=dv[:, oc, m * SW + 4:m * SW + 516], in_=xo[i2].full())
        if not last:
            write_tails(txs)

    def gather_e1():
        gather_pairs(list(zip(sx, sxg)) + [(fx, fxg)])

    nl = L if stop is None else stop[0]
    done = False
    for l in range(nl):
        last_l = (stop is not None and l == nl - 1)
        phase_A(l)
        P.emit(final=False)
        ssd_scan(l, load_consts(), pass1=True)
        P.emit(final=False)
        if last_l and stop[1] == "A":
            break
        gather_e1()
        phase_attn(l)
        P.emit(final=False)
        phase_ssd2(l)
        P.emit(final=False)
        if last_l and stop[1] == "B":
            break
        phase_merge(l)
        P.emit(final=False)
        halo_exchange(xmid)
        P.emit(final=False)
        if last_l and stop[1] == "C":
            break
        phase_ffn(l, last=(l == L - 1))
        P.emit(final=False)
        if l < L - 1:
            halo_exchange(xb[l + 1])
            P.emit(final=False)
    loc = {"kxmg0": kxmg[0], "vxg0": vxg[0], "sxg0": sxg[0], "fxg": fxg, "qm": qm, "qf": qf, "fq": fq, "omd": omd, "ofd": ofd, "yd": yd,
           "xmid": xmid, "xb1": xb[1], "szd": szd, "gd": gd, "xtm": xtm, "btm": btm, "bct": bct, "dtd": dtd, "atd": atd}
    for name in dbg:
        src = loc[name]
        shp = list(src.h.shape) if hasattr(src.h, "shape") else None
        dd = P.dram("dbg_" + name, shp, src.h.dtype, EO)
        P.dma("sync", out=dd.full(), in_=src.full())
    P.emit(final=True)
    return nc, P


def _stripe_tokens(p):
    return np.concatenate([np.arange((4 * m + p) * 512, (4 * m + p + 1) * 512) for m in range(4)])


def fused_in_maps(inp):
    L = 2
    cpsA = [a_colpack(inp, l) for l in range(L)]
    cpsD = [d2_colpack(inp, l) for l in range(L)]
    offA = dict(cpsA[0].off)
    offA["_n"] = cpsA[0].n
    offD = dict(cpsD[0].off)
    offD["_n"] = cpsD[0].n
    cstA = np.stack([c.array() for c in cpsA])
    cstD = np.stack([c.array() for c in cpsD])
    w_kp = np.zeros((L, 256, 8, 96), np.float32)
    wukv = inp["mla_w_ukv"].reshape(L, 256, 8, 128)
    w_kp[:, :, :, 0:64] = wukv[:, :, :, 0:64]
    w_v = np.ascontiguousarray(wukv[:, :, :, 64:128].reshape(L, 256, 512))
    gssm = np.ascontiguousarray(inp["ssm_norm_g"].reshape(L, 8, 128).transpose(0, 2, 1))
    rowc = np.stack([fused_rowpack(inp, l) for l in range(L)])
    msk, cm = _bc_consts()
    mats = _const_mats()
    shared = {
        "w_in": np.ascontiguousarray(inp["w_in"]), "w_uq": np.ascontiguousarray(inp["mla_w_uq"]),
        "w_kp": np.ascontiguousarray(w_kp.reshape(L, 256, 768)), "w_v": w_v,
        "w_a": np.ascontiguousarray(inp["w_br_mla"]), "w_b": np.ascontiguousarray(inp["w_br_fox"]),
        "w_c": np.ascontiguousarray(inp["w_br_ssm"]), "w_o": np.ascontiguousarray(inp["w_out"]),
        "w_up": np.ascontiguousarray(inp["ffn_w_up"]), "w_dn": np.ascontiguousarray(inp["ffn_w_down"]),
        "cstA": cstA, "cstD": cstD, "gssm": gssm, "rowc": rowc, "msk": msk, "cm": cm, "mats": mats,
    }
    in_maps = []
    for c in range(8):
        b, p = c // 4, c % 4
        xT = np.zeros((D, 4 * SW), np.float32)
        xbT = inp["x"][b].T
        for m in range(4):
            s_ = 4 * m + p
            xT[:, m * SW + 4:m * SW + 516] = xbT[:, s_ * 512:(s_ + 1) * 512]
            if s_ > 0:
                xT[:, m * SW:m * SW + 4] = xbT[:, s_ * 512 - 4:s_ * 512]
        sel = np.zeros((128, 32), np.float32)
        if p >= 1:
            sel[:, p - 1] = 1.0
        else:
            sel[:, 4] = 1.0
        for s_ in range(16):
            if s_ % 4 == p:
                sel[:, 8 + s_] = 1.0
        for jr in range(4):
            sel[:, 24 + jr] = 1.0 if jr == p else 0.0
            sel[:, 28 + jr] = NEG if jr > p else 0.0
        d = dict(shared)
        d["x0"] = np.ascontiguousarray(xT)
        d["pos"] = np.ascontiguousarray(inp["positions"][b][_stripe_tokens(p)][None, :]).astype(np.int32)
        d["sel"] = sel
        in_maps.append(d)
    return in_maps, offA, offD


def kernel_fused(**inp):
    inp = {k: np.asarray(v) for k, v in inp.items()}
    in_maps, offA, offD = fused_in_maps(inp)
    if "F" not in _PROG_CACHE:
        _PROG_CACHE["F"] = build_fused(offA, offD)[0]
    res = run_bass_kernel_spmd(_PROG_CACHE["F"], in_maps, core_ids=list(range(8))).results
    xo = np.zeros((2, S_, D), np.float32)
    for c in range(8):
        b, p = c // 4, c % 4
        xo[b, _stripe_tokens(p), :] = np.asarray(res[c]["out"]).T
    return xo
```

```python
from contextlib import ExitStack
import numpy as np
import concourse.bass as bass
import concourse.mybir as mybir

F32 = mybir.dt.float32
BF16 = mybir.dt.bfloat16
I32 = mybir.dt.int32
ALU = mybir.AluOpType
AF = mybir.ActivationFunctionType
AX = mybir.AxisListType

COMPUTE = ("tensor", "vector", "scalar", "gpsimd")
QUEUES = ("sync", "gpsimd", "scalar")
NRING = 8


class View:
    __slots__ = ("buf", "ap", "key")

    def __init__(self, buf, ap, key=None):
        self.buf = buf
        self.ap = ap
        self.key = key

    def __getitem__(self, k):
        return View(self.buf, self.ap[k], self.key)

    def re(self, s, **kw):
        return View(self.buf, self.ap.rearrange(s, **kw), self.key)

    def bc(self, shape):
        return View(self.buf, self.ap.to_broadcast(shape), self.key)

    def bitcast(self, dt):
        return View(self.buf, self.ap.bitcast(dt), self.key)

    def k(self, key):
        return View(self.buf, self.ap, key)

    def f(self, fn):
        return View(self.buf, fn(self.ap), self.key)


class Buf:
    def __init__(self, name, handle, is_dram=False):
        self.name = name
        self.h = handle
        self.is_dram = is_dram
        self.regions = {}

    def full(self):
        ap = self.h.ap() if hasattr(self.h, "ap") and callable(getattr(self.h, "ap")) else self.h[:]
        return View(self, ap)

    def __getitem__(self, k):
        return View(self, self.h[k])


class Op:
    __slots__ = ("id", "eng", "meth", "kw", "deps", "is_dma", "signaled", "sem", "val", "prewait", "eidx")


class Eng:
    def __init__(self, P, name):
        self.P = P
        self.name = name

    def __getattr__(self, meth):
        def call(*a, **kw):
            assert not a, "use kwargs"
            return self.P._record(self.name, meth, kw)
        return call


class Prog:
    def __init__(self, nc):
        self.nc = nc
        self.ops = []
        self.gstack = ExitStack()
        self.stack = ExitStack()
        self.pe = Eng(self, "tensor")
        self.dve = Eng(self, "vector")
        self.act = Eng(self, "scalar")
        self.pool = Eng(self, "gpsimd")
        self.sp = Eng(self, "sync")
        st = self.gstack
        self.csem = {e: st.enter_context(nc.semaphore(f"c_{e}")) for e in COMPUTE}
        self.rings = {q: [st.enter_context(nc.semaphore(f"d_{q}{i}")) for i in range(NRING)] for q in QUEUES}
        self.ccsem = st.enter_context(nc.semaphore("ccsem"))
        self.cccount = 0
        self.ccount = {e: 0 for e in COMPUTE}
        self.dcount = {q: 0 for q in QUEUES}
        self.waited = {e: {} for e in ("sync",) + COMPUTE}
        self.emitted = 0
        self.barrier = []
        self.stats = {}
        self.nwaits = 0

    def sb(self, name, shape, dtype):
        self.nuid = getattr(self, "nuid", 0) + 1
        name = f"{name}_s{self.nuid}"
        t = self.stack.enter_context(self.nc.sbuf_tensor(name, list(shape), dtype))
        return Buf(name, t)

    def ps(self, name, shape, dtype):
        self.nuid = getattr(self, "nuid", 0) + 1
        name = f"{name}_p{self.nuid}"
        t = self.stack.enter_context(self.nc.psum_tensor(name, list(shape), dtype))
        return Buf(name, t)

    def dram(self, name, shape, dtype, kind="Internal"):
        t = self.nc.dram_tensor(name, list(shape), dtype, kind=kind)
        return Buf(name, t, is_dram=True)

    def _record(self, eng, meth, kw):
        op = Op()
        op.id = len(self.ops)
        op.eng = eng
        op.meth = meth
        op.kw = kw
        op.is_dma = meth in ("dma_start", "dma_start_transpose", "collective_compute")
        op.signaled = False
        op.sem = None
        op.val = 0
        op.prewait = None
        deps = set()
        extra_r = kw.pop("_reads", [])
        extra_w = kw.pop("_writes", [])
        writes, reads = [], []
        for k, v in kw.items():
            vs = v if isinstance(v, (list, tuple)) else [v]
            for x in vs:
                if isinstance(x, View):
                    if k in ("out", "accum_out", "outs") or (k == "ap" and meth in ("memset", "memzero")):
                        writes.append(x)
                    else:
                        reads.append(x)
        reads += extra_r
        writes += extra_w
        for v in reads:
            self._gather(v, False, deps)
        for v in writes:
            self._gather(v, True, deps)
        for v in reads:
            self._update(v, False, op.id)
        for v in writes:
            self._update(v, True, op.id)
        deps.discard(op.id)
        op.deps = deps
        self.ops.append(op)
        return op

    def _gather(self, v, is_write, deps):
        R = v.buf.regions
        if v.key is None:
            regs = list(R.values())
        else:
            regs = [R[k] for k in (v.key, None) if k in R]
        for reg in regs:
            if reg[0] is not None:
                deps.add(reg[0])
            if is_write:
                deps.update(reg[1])

    def _update(self, v, is_write, oid):
        R = v.buf.regions
        if is_write:
            if v.key is None:
                R.clear()
            R[v.key] = [oid, []]
        else:
            R.setdefault(v.key, [None, []])[1].append(oid)

    def dma(self, q, out, in_, **kw):
        eng = {"sync": self.sp, "gpsimd": self.pool, "scalar": self.act}[q]
        return eng.dma_start(out=out, in_=in_, **kw)

    def emit(self, final=True):
        nc = self.nc
        ops = self.ops
        phase = ops[self.emitted:]
        first_id = self.emitted
        self.emitted = len(ops)
        for op in phase:
            for d in op.deps:
                dop = ops[d]
                if d < first_id:
                    continue
                if dop.eng == "tensor" and op.eng == "tensor" and not dop.is_dma and not op.is_dma:
                    continue
                dop.signaled = True
        per = {}
        for op in phase:
            per.setdefault(op.eng, []).append(op)
        for e, lst in per.items():
            for op in reversed(lst):
                if not op.is_dma:
                    op.signaled = True
                    break
        for op in phase:
            if op.meth == "collective_compute":
                self.cccount += 1
                op.sem = self.ccsem
                op.val = self.cccount
                op.signaled = True
            elif op.is_dma:
                k = self.dcount[op.eng]
                self.dcount[op.eng] += 1
                op.sem = self.rings[op.eng][k % NRING]
                op.val = 16 * (k // NRING + 1)
                if k >= NRING:
                    op.prewait = (op.sem, 16 * (k // NRING))
                op.signaled = True
            elif op.signaled:
                self.ccount[op.eng] += 1
                op.sem = self.csem[op.eng]
                op.val = self.ccount[op.eng]
        for e, v in per.items():
            self.stats[e] = self.stats.get(e, 0) + len(v)
        barrier_in = list(self.barrier)
        dcount = self.dcount
        rings = self.rings

        def dma_final_waits():
            ws = []
            for q in QUEUES:
                n = dcount[q]
                for i in range(min(n, NRING)):
                    cnt = (n - 1 - i) // NRING + 1
                    ws.append((rings[q][i], 16 * cnt))
            if self.cccount > 0:
                ws.append((self.ccsem, self.cccount))
            return ws

        def run(engname, e):
            waited = self.waited[engname]

            def do_waits(ws):
                for sem, val in ws:
                    key = id(sem)
                    if waited.get(key, 0) >= val:
                        continue
                    waited[key] = val
                    e.wait_ge(sem, val)
                    self.nwaits += 1

            do_waits(barrier_in)
            for op in per.get(engname, []):
                ws = []
                if op.prewait is not None:
                    ws.append(op.prewait)
                for d in sorted(op.deps):
                    dop = ops[d]
                    if dop.sem is None:
                        continue
                    if dop.eng == "tensor" and op.eng == "tensor" and not dop.is_dma and not op.is_dma:
                        continue
                    ws.append((dop.sem, dop.val))
                do_waits(ws)
                kw = {}
                for k, v in op.kw.items():
                    if isinstance(v, View):
                        kw[k] = v.ap
                    elif isinstance(v, (list, tuple)) and v and isinstance(v[0], View):
                        kw[k] = [x.ap for x in v]
                    else:
                        kw[k] = v
                ins = getattr(e, op.meth)(**kw)
                if op.signaled:
                    ins.then_inc(op.sem, 16 if (op.is_dma and op.meth != "collective_compute") else 1)
            if final and engname == "sync":
                do_waits(dma_final_waits())

        with nc.Block() as block:
            @block.sync
            def _(e):
                run("sync", e)

            @block.tensor
            def _(e):
                run("tensor", e)

            @block.vector
            def _(e):
                run("vector", e)

            @block.scalar
            def _(e):
                run("scalar", e)

            @block.gpsimd
            def _(e):
                run("gpsimd", e)
        bar = dma_final_waits()
        for e in COMPUTE:
            if self.ccount[e] > 0:
                bar.append((self.csem[e], self.ccount[e]))
        self.barrier = bar
        self.stats["waits"] = self.nwaits
        self.stack.close()
        self.stack = ExitStack()
        if final:
            self.gstack.close()


from concourse.bass_utils import run_bass_kernel_spmd
import ml_dtypes

NBF = ml_dtypes.bfloat16
D = 1024
T = 2048
HALO = 4
NEG = -30000.0


class ColPack:
    def __init__(self):
        self.cols = []
        self.off = {}
        self.n = 0

    def add(self, name, vec, rows=128):
        vec = np.asarray(vec, np.float32).reshape(-1)
        assert vec.size % rows == 0
        m = vec.reshape(-1, rows).T
        a = np.zeros((128, m.shape[1]), np.float32)
        a[:rows] = m
        self.off[name] = (self.n, m.shape[1], rows)
        self.cols.append(a)
        self.n += m.shape[1]

    def array(self):
        return np.ascontiguousarray(np.concatenate(self.cols, axis=1))


class Cst:
    def __init__(self, P, buf, off):
        self.buf = buf
        self.off = off

    def col(self, name, j=0, rows=None):
        o, n, r = self.off[name]
        r = rows or r
        return self.buf[0:r, o + j:o + j + 1]

    def cols(self, name):
        o, n, r = self.off[name]
        return self.buf[0:r, o:o + n]


def new_nc():
    return bass.Bass("TRN2", target_bir_lowering=False)


def load_cast(P, q, dram_view, stage_view, bf_view, cast_eng):
    P.dma(q, out=stage_view, in_=dram_view)
    cast_eng.tensor_copy(out=bf_view, in_=stage_view)


A_OFF = None


def a_colpack(inp, l):
    cp = ColPack()
    cp.add("g_mix", inp["norm_mix_g"][l])
    cp.add("g_cq", inp["mla_q_norm_g"][l])
    cp.add("g_ckv", inp["mla_kv_norm_g"][l])
    cp.add("g_q", inp["mla_q_gain"][l], 96)
    cp.add("g_k", inp["mla_k_gain"][l], 96)
    cp.add("g_fq", inp["fox_q_gain"][l], 64)
    cp.add("g_fk", inp["fox_k_gain"][l], 64)
    cp.add("b_f", inp["fox_b_f"][l], 8)
    cw = inp["ssm_conv_w"][l]
    for k in range(4):
        cp.add(f"cw{k}", cw[k])
    cp.add("cb", inp["ssm_conv_b"][l])
    cp.add("dt_b", inp["ssm_dt_bias"][l], 16)
    cp.add("A_log", inp["ssm_A_log"][l], 16)
    cp.add("b_gate", inp["b_gate"][l])
    inv = 1.0 / (10000.0 ** (np.arange(0, 32, 2, dtype=np.float32) / 32.0))
    invf = np.zeros(96, np.float32)
    invf[64:80] = inv
    invf[80:96] = inv
    cp.add("invf", invf, 96)
    return cp


def build_A(off):
    nc = new_nc()
    P = Prog(nc)
    TT = T + HALO
    NT = T // 512
    EI, EO = "ExternalInput", "ExternalOutput"
    xT = P.dram("xT", [D, TT], F32, EI)
    pos = P.dram("pos", [1, T], I32, EI)
    w_in = P.dram("w_in", [D, 7864], F32, EI)
    w_uq = P.dram("w_uq", [384, 768], F32, EI)
    w_kp = P.dram("w_kp", [256, 768], F32, EI)
    w_v = P.dram("w_v", [256, 512], F32, EI)
    cst_d = P.dram("cst", [128, off["_n"]], F32, EI)
    mats = P.dram("mats", [128, 2 * 96], F32, EI)
    o_qm = P.dram("o_qm", [8, 96, T], BF16, EO)
    o_km = P.dram("o_km", [8, 96, T], BF16, EO)
    o_vm = P.dram("o_vm", [512, T], BF16, EO)
    o_qf = P.dram("o_qf", [8, 64, T], BF16, EO)
    o_kf = P.dram("o_kf", [8, 64, T], BF16, EO)
    o_vf = P.dram("o_vf", [512, T], BF16, EO)
    o_lf = P.dram("o_lf", [8, T], F32, EO)
    o_sz = P.dram("o_sz", [1024, T], BF16, EO)
    o_xbc = P.dram("o_xbc", [1536, T], BF16, EO)
    o_dt = P.dram("o_dt", [16, T], F32, EO)
    o_a = P.dram("o_a", [16, T], F32, EO)
    o_g = P.dram("o_g", [3072, T], BF16, EO)

    cstb = P.sb("cstb", [128, off["_n"]], F32)
    C = Cst(P, cstb, off)
    P.dma("sync", out=cstb.full(), in_=cst_d.full())
    matf = P.sb("matf", [128, 192], F32)
    matb = P.sb("matb", [128, 192], BF16)
    P.dma("sync", out=matf.full(), in_=mats.full())
    P.dve.tensor_copy(out=matb.full(), in_=matf.full())
    prh = matb[0:96, 0:96]
    sel = matb[0:32, 96:192]
    ones = P.sb("ones", [128, 128], F32)
    P.dve.memset(ap=ones.full(), constant=1.0)
    eps = P.sb("eps", [128, 1], F32)
    P.dve.memset(ap=eps.full(), constant=1e-6)
    one1 = P.sb("one1", [128, 1], F32)
    P.dve.memset(ap=one1.full(), constant=1.0)
    nbf = P.sb("nbf", [8, 1], F32)
    P.dve.tensor_scalar(out=nbf.full(), in0=C.col("b_f"), scalar1=-1.0, scalar2=None, op0=ALU.mult)
    Aneg = P.sb("Aneg", [16, 1], F32)
    P.act.activation(out=Aneg.full(), in_=C.col("A_log"), func=AF.Exp)
    P.dve.tensor_scalar(out=Aneg.full(), in0=Aneg.full(), scalar1=-1.0, scalar2=None, op0=ALU.mult)

    pb = [P.ps(f"pb{i}", [128, 512], F32) for i in range(8)]
    pbi = {}

    def nxt_ps(lo=0, hi=4):
        i = pbi.get(lo, 0)
        pbi[lo] = (i + 1) % (hi - lo)
        return pb[lo + i]

    Ctab = P.sb("Ctab", [96, T], F32)
    Stab = P.sb("Stab", [96, T], F32)
    posi = P.sb("posi", [96, 512], I32)
    posf = P.sb("posf", [96, 512], F32)
    rr_tmp = P.sb("rr_tmp", [96, 512], F32)
    rr_i = P.sb("rr_i", [96, 512], I32)
    rr_m = P.sb("rr_m", [96, 512], F32)

    def sin_table(outv, phase):
        P.dve.tensor_scalar(out=rr_tmp.full(), in0=posf.full(), scalar1=C.col("invf"), scalar2=phase,
                            op0=ALU.mult, op1=ALU.add)
        P.dve.tensor_scalar(out=rr_m.full(), in0=rr_tmp.full(), scalar1=1.0 / (2 * np.pi), scalar2=None, op0=ALU.mult)
        P.dve.tensor_copy(out=rr_i.full(), in_=rr_m.full())
        P.dve.tensor_copy(out=rr_m.full(), in_=rr_i.full())
        P.dve.scalar_tensor_tensor(out=rr_tmp.full(), in0=rr_m.full(), scalar=-2 * np.pi, in1=rr_tmp.full(),
                                   op0=ALU.mult, op1=ALU.add)
        P.dve.tensor_scalar(out=rr_m.full(), in0=rr_tmp.full(), scalar1=np.pi, scalar2=-2 * np.pi, op0=ALU.is_gt, op1=ALU.mult)
        P.dve.tensor_tensor(out=rr_tmp.full(), in0=rr_tmp.full(), in1=rr_m.full(), op=ALU.add)
        P.dve.tensor_scalar(out=rr_m.full(), in0=rr_tmp.full(), scalar1=-np.pi, scalar2=2 * np.pi, op0=ALU.is_lt, op1=ALU.mult)
        P.dve.tensor_tensor(out=rr_tmp.full(), in0=rr_tmp.full(), in1=rr_m.full(), op=ALU.add)
        P.act.activation(out=outv, in_=rr_tmp.full(), func=AF.Sin)

    for i in range(NT):
        P.dma("sync", out=posi.full(), in_=pos[:, i * 512:(i + 1) * 512].f(lambda a: a.partition_broadcast(96)))
        P.dve.tensor_copy(out=posf.full(), in_=posi.full())
        sin_table(Stab[:, i * 512:(i + 1) * 512], 0.0)
        sin_table(Ctab[:, i * 512:(i + 1) * 512], np.pi / 2)
    P.dve.memset(ap=Stab[0:64, :], constant=0.0)
    P.dve.memset(ap=Ctab[0:64, :], constant=1.0)

    hn = P.sb("hn", [128, 8, TT], BF16)
    xst = P.sb("xst", [128, 8, 512], F32)
    sq = P.sb("sq", [128, 512], F32)
    rstd = P.sb("rstd", [128, 512], F32)
    xTv = xT.full().re("(kc p) n -> p kc n", p=128)

    def rstd_from(ps_view, n_feat, rows, width, rstd_view):
        P.act.activation(out=rstd_view, in_=ps_view, func=AF.Sqrt, bias=eps[0:rows, 0:1], scale=1.0 / n_feat)
        P.dve.reciprocal(out=rstd_view, in_=rstd_view)

    tiles = [(0, HALO)] + [(HALO + i * 512, 512) for i in range(NT)]
    for (c0, w) in tiles:
        P.dma("sync", out=xst[:, :, 0:w], in_=xTv[:, :, c0:c0 + w])
        ps = nxt_ps(4, 6)
        for kc in range(8):
            P.act.activation(out=sq[:, 0:w], in_=xst[:, kc, 0:w], func=AF.Square)
            P.pe.matmul(out=ps[:, 0:w], lhsT=ones.full(), rhs=sq[:, 0:w], start=(kc == 0), stop=(kc == 7))
        rstd_from(ps[:, 0:w], 1024.0, 128, w, rstd[:, 0:w])
        for kc in range(8):
            P.dve.scalar_tensor_tensor(out=hn[:, kc, c0:c0 + w], in0=xst[:, kc, 0:w], scalar=C.col("g_mix", kc),
                                       in1=rstd[:, 0:w], op0=ALU.mult, op1=ALU.mult)

    wst = [P.sb(f"wst{i}", [128, 8, 512], F32) for i in range(2)]
    wbf = [P.sb(f"wbf{i}", [128, 8, 512], BF16) for i in range(2)]
    wcnt = [0]
    w_inv = w_in.full().re("(kc p) n -> p kc n", p=128)

    def load_w(c0, ncols):
        i = wcnt[0] % 2
        wcnt[0] += 1
        q = "sync" if i == 0 else "gpsimd"
        P.dma(q, out=wst[i][:, :, 0:ncols], in_=w_inv[:, :, c0:c0 + ncols])
        P.pool.tensor_copy(out=wbf[i][:, :, 0:ncols], in_=wst[i][:, :, 0:ncols])
        return wbf[i]

    def proj(wb, wc0, m, c0, w, ps_view):
        for kc in range(8):
            P.pe.matmul(out=ps_view, lhsT=wb[:, kc, wc0:wc0 + m], rhs=hn[:, kc, c0:c0 + w],
                        start=(kc == 0), stop=(kc == 7))

    ostg_cnt = [0]
    ostg = [P.sb(f"ostg{i}", [128, 512], BF16) for i in range(4)]

    def next_ostg():
        i = ostg_cnt[0] % 4
        ostg_cnt[0] += 1
        return ostg[i]

    def out_dma(dst_view, src_view):
        q = "sync" if ostg_cnt[0] % 2 else "gpsimd"
        P.dma(q, out=dst_view, in_=src_view)

    hraw = P.sb("hraw", [96, 512], F32)
    hsq = P.sb("hsq", [96, 512], F32)
    hrs = P.sb("hrs", [96, 512], F32)
    hnf = P.sb("hnf", [96, 512], F32)
    hnb = P.sb("hnb", [96, 512], BF16)
    ht1 = P.sb("ht1", [96, 512], F32)
    ht2 = P.sb("ht2", [96, 512], F32)

    def headnorm(ps_view, d, gain_col, rope, tok0, dst_view):
        P.act.activation(out=hsq[0:d, :], in_=ps_view, func=AF.Square)
        P.act.copy(out=hraw[0:d, :], in_=ps_view)
        ps2 = nxt_ps(4, 6)
        P.pe.matmul(out=ps2[0:d, :], lhsT=ones[0:d, 0:d], rhs=hsq[0:d, :], start=True, stop=True)
        rstd_from(ps2[0:d, :], float(d), d, 512, hrs[0:d, :])
        og = next_ostg()
        if not rope:
            P.dve.scalar_tensor_tensor(out=og[0:d, :], in0=hraw[0:d, :], scalar=gain_col, in1=hrs[0:d, :],
                                       op0=ALU.mult, op1=ALU.mult)
        else:
            P.dve.scalar_tensor_tensor(out=hnf[0:d, :], in0=hraw[0:d, :], scalar=gain_col, in1=hrs[0:d, :],
                                       op0=ALU.mult, op1=ALU.mult)
            P.act.copy(out=hnb[0:d, :], in_=hnf[0:d, :])
            ps3 = nxt_ps(6, 8)
            P.pe.matmul(out=ps3[0:d, :], lhsT=prh, rhs=hnb[0:d, :], start=True, stop=True)
            P.dve.tensor_tensor(out=ht1[0:d, :], in0=hnf[0:d, :], in1=Ctab[0:d, tok0:tok0 + 512], op=ALU.mult)
            P.dve.tensor_tensor(out=ht2[0:d, :], in0=ps3[0:d, :], in1=Stab[0:d, tok0:tok0 + 512], op=ALU.mult)
            P.pool.tensor_tensor(out=og[0:d, :], in0=ht1[0:d, :], in1=ht2[0:d, :], op=ALU.add)
        out_dma(dst_view, og[0:d, :])

    lat = P.sb("lat", [128, 3, 512], F32)
    latn = P.sb("latn", [128, 3, 512], BF16)

    def latent_norm(ps_list, gname):
        nch = len(ps_list)
        ps2 = nxt_ps(4, 6)
        for i, psv in enumerate(ps_list):
            P.act.activation(out=sq.full(), in_=psv, func=AF.Square)
            P.act.copy(out=lat[:, i, :], in_=psv)
            P.pe.matmul(out=ps2.full(), lhsT=ones.full(), rhs=sq.full(), start=(i == 0), stop=(i == nch - 1))
        rstd_from(ps2.full(), 128.0 * nch, 128, 512, rstd.full())
        for i in range(nch):
            P.dve.scalar_tensor_tensor(out=latn[:, i, :], in0=lat[:, i, :], scalar=C.col(gname, i), in1=rstd.full(),
                                       op0=ALU.mult, op1=ALU.mult)

    def small_w(name, dram, kc_n, ncols, i):
        stg = wst[i].full().re("p a b -> p (a b)")[:, 0:kc_n * ncols].re("p (a b) -> p a b", a=kc_n)
        bfb = P.sb(name, [128, kc_n, ncols], BF16)
        P.dma("gpsimd", out=stg, in_=dram.full().re("(kc p) n -> p kc n", p=128))
        P.pool.tensor_copy(out=bfb.full(), in_=stg)
        return bfb

    uqb = small_w("uqb", w_uq, 3, 768, 0)
    kpb = small_w("kpb", w_kp, 2, 768, 1)
    wvb = small_w("wvb", w_v, 2, 512, 0)
    main = tiles[1:]
    wb = load_w(0, 384)
    for ti, (c0, w) in enumerate(main):
        pss = []
        for ch in range(3):
            ps = nxt_ps(0, 4)
            proj(wb, ch * 128, 128, c0, 512, ps.full())
            pss.append(ps.full())
        latent_norm(pss, "g_cq")
        for h in range(8):
            ps = nxt_ps(0, 4)
            for kc in range(3):
                P.pe.matmul(out=ps[0:96, :], lhsT=uqb[:, kc, h * 96:(h + 1) * 96], rhs=latn[:, kc, :],
                            start=(kc == 0), stop=(kc == 2))
            headnorm(ps[0:96, :], 96, C.col("g_q"), True, ti * 512, o_qm[h, :, ti * 512:(ti + 1) * 512])
    wb = load_w(384, 288)
    krb = P.sb("krb", [32, 512], BF16)
    for ti, (c0, w) in enumerate(main):
        pss = []
        for ch in range(2):
            ps = nxt_ps(0, 4)
            proj(wb, ch * 128, 128, c0, 512, ps.full())
            pss.append(ps.full())
        ps = nxt_ps(0, 4)
        proj(wb, 256, 32, c0, 512, ps[0:32, :])
        P.act.copy(out=krb.full(), in_=ps[0:32, :])
        latent_norm(pss, "g_ckv")
        for h in range(8):
            ps = nxt_ps(0, 4)
            for kc in range(2):
                P.pe.matmul(out=ps[0:96, :], lhsT=kpb[:, kc, h * 96:(h + 1) * 96], rhs=latn[:, kc, :],
                            start=(kc == 0), stop=False)
            P.pe.matmul(out=ps[0:96, :], lhsT=sel, rhs=krb.full(), start=False, stop=True)
            headnorm(ps[0:96, :], 96, C.col("g_k"), True, ti * 512, o_km[h, :, ti * 512:(ti + 1) * 512])
        for ch in range(4):
            ps = nxt_ps(0, 4)
            for kc in range(2):
                P.pe.matmul(out=ps.full(), lhsT=wvb[:, kc, ch * 128:(ch + 1) * 128], rhs=latn[:, kc, :],
                            start=(kc == 0), stop=(kc == 1))
            og = next_ostg()
            P.act.copy(out=og.full(), in_=ps.full())
            out_dma(o_vm[ch * 128:(ch + 1) * 128, ti * 512:(ti + 1) * 512], og.full())
    for (base, gname, dst) in ((672, "g_fq", o_qf), (672 + 512, "g_fk", o_kf)):
        wb = load_w(base, 512)
        for ti, (c0, w) in enumerate(main):
            for h in range(8):
                ps = nxt_ps(0, 4)
                proj(wb, h * 64, 64, c0, 512, ps[0:64, :])
                headnorm(ps[0:64, :], 64, C.col(gname), False, ti * 512, dst[h, :, ti * 512:(ti + 1) * 512])
    def plain_group(base, ncols, func, bias_name, dst, dst_row0):
        wb = load_w(base, ncols)
        for ti, (c0, w) in enumerate(main):
            for ch in range(ncols // 128):
                ps = nxt_ps(0, 4)
                proj(wb, ch * 128, 128, c0, 512, ps.full())
                og = next_ostg()
                if bias_name is None:
                    P.act.activation(out=og.full(), in_=ps.full(), func=func)
                else:
                    P.act.activation(out=og.full(), in_=ps.full(), func=func,
                                     bias=C.col(bias_name, (dst_row0 // 128) + ch))
                out_dma(dst[dst_row0 + ch * 128:dst_row0 + (ch + 1) * 128, ti * 512:(ti + 1) * 512], og.full())

    plain_group(672 + 1024, 512, AF.Copy, None, o_vf, 0)
    FB = 672 + 1536
    SB = 672 + 1544
    wf = load_w(FB, 8)
    lf1 = P.sb("lf1", [16, 512], F32)
    lf2 = P.sb("lf2", [16, 512], F32)
    for ti, (c0, w) in enumerate(main):
        ps = nxt_ps(0, 4)
        proj(wf, 0, 8, c0, 512, ps[0:8, :])
        P.act.activation(out=lf1[0:8, :], in_=ps[0:8, :], func=AF.Exp, bias=nbf[0:8, 0:1], scale=-1.0)
        P.act.activation(out=lf1[0:8, :], in_=lf1[0:8, :], func=AF.Ln, bias=one1[0:8, 0:1], scale=1.0)
        P.dve.tensor_scalar(out=lf2[0:8, :], in0=lf1[0:8, :], scalar1=-1.0, scalar2=None, op0=ALU.mult)
        P.dma("sync", out=o_lf[:, ti * 512:(ti + 1) * 512], in_=lf2[0:8, :])
    wd = load_w(SB + 1024 + 1536, 16)
    dt1 = P.sb("dt1", [16, 512], F32)
    dt2 = P.sb("dt2", [16, 512], F32)
    for ti, (c0, w) in enumerate(main):
        ps = nxt_ps(0, 4)
        proj(wd, 0, 16, c0, 512, ps[0:16, :])
        P.act.activation(out=dt1.full(), in_=ps[0:16, :], func=AF.Exp, bias=C.col("dt_b"), scale=1.0)
        P.act.activation(out=dt1.full(), in_=dt1.full(), func=AF.Ln, bias=one1[0:16, 0:1], scale=1.0)
        P.dma("sync", out=o_dt[:, ti * 512:(ti + 1) * 512], in_=dt1.full())
        P.dve.tensor_scalar(out=dt2.full(), in0=dt1.full(), scalar1=Aneg[:, 0:1], scalar2=None, op0=ALU.mult)
        P.dma("sync", out=o_a[:, ti * 512:(ti + 1) * 512], in_=dt2.full())
    for blk in range(2):
        plain_group(SB + blk * 512, 512, AF.Silu, None, o_sz, blk * 512)
    upre = P.sb("upre", [128, 516], F32)
    carry = P.sb("carry", [128, 12, 4], F32)
    acc = [P.sb(f"acc{i}", [128, 512], F32) for i in range(2)]
    for blk in range(3):
        wb = load_w(SB + 1024 + blk * 512, 512)
        for ch in range(4):
            cg = blk * 4 + ch
            ps = nxt_ps(0, 4)
            proj(wb, ch * 128, 128, 0, HALO, ps[:, 0:HALO])
            P.act.copy(out=carry[:, cg, :], in_=ps[:, 0:HALO])
        for ti, (c0, w) in enumerate(main):
            for ch in range(4):
                cg = blk * 4 + ch
                ps = nxt_ps(0, 4)
                proj(wb, ch * 128, 128, c0, 512, ps.full())
                P.act.copy(out=upre[:, 4:516], in_=ps.full())
                P.dve.tensor_copy(out=upre[:, 0:4], in_=carry[:, cg, :])
                P.pool.tensor_copy(out=carry[:, cg, :], in_=upre[:, 512:516])
                a0 = acc[0]
                P.dve.tensor_scalar(out=a0.full(), in0=upre[:, 4:516], scalar1=C.col("cw3", cg), scalar2=C.col("cb", cg),
                                    op0=ALU.mult, op1=ALU.add)
                for k in range(3):
                    P.dve.scalar_tensor_tensor(out=a0.full(), in0=upre[:, 1 + k:513 + k], scalar=C.col(f"cw{k}", cg),
                                               in1=a0.full(), op0=ALU.mult, op1=ALU.add)
                og = next_ostg()
                P.act.activation(out=og.full(), in_=a0.full(), func=AF.Silu)
                out_dma(o_xbc[cg * 128:(cg + 1) * 128, ti * 512:(ti + 1) * 512], og.full())
    GB = SB + 2576
    for blk in range(6):
        plain_group(GB + blk * 512, 512, AF.Sigmoid, "b_gate", o_g, blk * 512)
    P.emit()
    return nc, P


def _bf(a):
    return np.asarray(a).astype(np.float32)


_PROG_CACHE = {}


def _const_mats():
    m = np.zeros((128, 192), np.float32)
    for i in range(16):
        m[80 + i, 64 + i] = -1.0
        m[64 + i, 80 + i] = 1.0
    for i in range(32):
        m[i, 96 + 64 + i] = 1.0
    return m


def run_A(inp, l, x_full, pos_full):
    cp = a_colpack(inp, l)
    off = dict(cp.off)
    off["_n"] = cp.n
    if "A" not in _PROG_CACHE:
        _PROG_CACHE["A"] = build_A(off)[0]
    nc = _PROG_CACHE["A"]
    cst = cp.array()
    wukv = inp["mla_w_ukv"][l].reshape(256, 8, 128)
    w_kp = np.zeros((256, 8, 96), np.float32)
    w_kp[:, :, 0:64] = wukv[:, :, 0:64]
    w_v = np.ascontiguousarray(wukv[:, :, 64:128].reshape(256, 512))
    mats = _const_mats()
    xf = x_full.reshape(16384, D)
    in_maps = []
    for c in range(8):
        t0 = c * T
        xt = np.zeros((D, T + HALO), np.float32)
        xt[:, HALO:] = xf[t0:t0 + T].T
        if c % 4 != 0:
            xt[:, 0:HALO] = xf[t0 - HALO:t0].T
        in_maps.append({
            "xT": np.ascontiguousarray(xt),
            "pos": np.ascontiguousarray(pos_full.reshape(1, 16384)[:, t0:t0 + T]).astype(np.int32),
            "w_in": np.ascontiguousarray(inp["w_in"][l]),
            "w_uq": np.ascontiguousarray(inp["mla_w_uq"][l]),
            "w_kp": np.ascontiguousarray(w_kp.reshape(256, 768)),
            "w_v": w_v, "cst": cst, "mats": mats,
        })
    res = run_bass_kernel_spmd(nc, in_maps, core_ids=list(range(8)))
    return res.results


S_ = 8192
NKT = S_ // 128
NQT = S_ // 512


def build_BC():
    nc = new_nc()
    P = Prog(nc)
    EI, EO = "ExternalInput", "ExternalOutput"
    qm = P.dram("qm", [2, 96, S_], BF16, EI)
    km = P.dram("km", [2, 96, S_], BF16, EI)
    vm = P.dram("vm", [2, 128, NKT, 64], BF16, EI)
    qf = P.dram("qf", [2, 64, S_], BF16, EI)
    kf = P.dram("kf", [2, 64, S_], BF16, EI)
    vf = P.dram("vf", [2, 128, NKT, 64], BF16, EI)
    lf = P.dram("lf", [2, 128, NKT], F32, EI)
    msk = P.dram("msk", [128, 8, 512], F32, EI)
    cm = P.dram("cm", [128, 4, 128], F32, EI)
    x_tm = P.dram("x_tm", [128, NKT, 256], BF16, EI)
    B_tm = P.dram("B_tm", [128, NKT, 128], BF16, EI)
    BT = P.dram("BT", [128, S_], BF16, EI)
    CT = P.dram("CT", [128, S_], BF16, EI)
    dt_tm = P.dram("dt_tm", [128, NKT, 4], F32, EI)
    a_tm = P.dram("a_tm", [128, NKT, 4], F32, EI)
    Dv = P.dram("Dv", [128, 4], F32, EI)
    o_m = P.dram("o_m", [2, 64, S_], BF16, EO)
    o_f = P.dram("o_f", [2, 64, S_], BF16, EO)
    o_y = P.dram("o_y", [128, NKT, 256], F32, EO)
    fsc = P.dram("fsc", [3, S_], BF16)

    cmb = P.sb("cmb", [128, 4, 128], F32)
    P.dma("sync", out=cmb.full(), in_=cm.full())
    tri, trimask, ident, ones = cmb[:, 0, :], cmb[:, 1, :], cmb[:, 2, :], cmb[:, 3, :]
    mskb = P.sb("mskb", [128, 8, 512], F32)
    P.dma("gpsimd", out=mskb.full(), in_=msk.full())
    zero = P.sb("zero", [128, 1], F32)
    P.dve.memset(ap=zero.full(), constant=0.0)

    pb = [P.ps(f"pb{i}", [128, 512], F32) for i in range(8)]
    K_sb = P.sb("K_sb", [128, S_], BF16)
    Q_sb = P.sb("Q_sb", [128, S_], BF16)
    V_sb = P.sb("V_sb", [128, NKT, 128], BF16)
    P.dve.memset(ap=V_sb[:, :, 64:128], constant=1.0)
    pt = [P.sb(f"pt{i}", [128, 512], BF16) for i in range(3)]
    mt = [P.sb(f"mt{i}", [128, 512], F32) for i in range(2)]
    rl = P.sb("rl", [128, 512], F32)
    rl2 = P.sb("rl2", [64, 512], F32)
    ot = [P.sb(f"ot{i}", [64, 512], BF16) for i in range(2)]
    negF = P.sb("negF", [128, NKT], F32)

    cnt = [0, 0, 0]

    def attention(dk, scale, mask0, bias_fn, out_dram_h):
        for qt in range(NQT):
            oacc = pb[3 + qt % 2]
            nk = 4 * qt + 4
            for kt in range(nk):
                i3 = cnt[0] % 3
                cnt[0] += 1
                ps = pb[i3]
                P.pe.matmul(out=ps.full(), lhsT=K_sb[0:dk, kt * 128:(kt + 1) * 128],
                            rhs=Q_sb[0:dk, qt * 512:(qt + 1) * 512], start=True, stop=True)
                if kt >= 4 * qt:
                    m = mt[cnt[1] % 2]
                    cnt[1] += 1
                    P.dve.tensor_tensor(out=m.full(), in0=ps.full(), in1=mskb[:, mask0 + kt - 4 * qt, :], op=ALU.add)
                    src = m.full()
                else:
                    src = ps.full()
                P.act.activation(out=pt[i3].full(), in_=src, func=AF.Exp, scale=scale, bias=bias_fn(kt))
                P.pe.matmul(out=oacc.full(), lhsT=V_sb[:, kt, :], rhs=pt[i3].full(), start=(kt == 0), stop=(kt == nk - 1))
            P.dve.reciprocal(out=rl[64:128, :], in_=oacc[64:128, :])
            P.dve.tensor_copy(out=rl2.full(), in_=rl[64:128, :])
            o = ot[qt % 2]
            P.dve.tensor_tensor(out=o.full(), in0=oacc[0:64, :], in1=rl2.full(), op=ALU.mult)
            P.dma("sync", out=out_dram_h[:, qt * 512:(qt + 1) * 512], in_=o.full())

    for h in range(2):
        P.dma("sync", out=K_sb[0:96, :], in_=km[h])
        P.dma("gpsimd", out=Q_sb[0:96, :], in_=qm[h])
        P.dma("sync", out=V_sb[:, :, 0:64], in_=vm[h])
        attention(96, 96.0 ** -0.5, 0, lambda kt: zero[:, 0:1], o_m[h])

    lfs = P.sb("lfs", [128, NKT], F32)
    wi = P.sb("wi", [128, NKT], F32)
    sc = [P.sb(f"sc{i}", [128, NKT], F32) for i in range(2)]
    Ff = P.sb("Ff", [128, NKT], F32)
    FT = P.sb("FT", [64, 128], F32)
    r1 = P.sb("r1", [64, 128], F32)
    fh = [P.sb(f"fh{i}", [64, 128], BF16) for i in range(3)]
    for h in range(2):
        P.dma("sync", out=lfs.full(), in_=lf[h])
        ps = pb[5]
        P.pe.matmul(out=ps[:, 0:NKT], lhsT=tri, rhs=lfs.full(), start=True, stop=True)
        P.act.copy(out=wi.full(), in_=ps[:, 0:NKT])
        ps = pb[6]
        P.pe.matmul(out=ps[:, 0:NKT], lhsT=ones, rhs=lfs.full(), start=True, stop=True)
        P.act.copy(out=sc[0].full(), in_=ps[:, 0:NKT])
        P.dve.tensor_tensor(out=wi.full(), in0=wi.full(), in1=sc[0].full(), op=ALU.subtract)
        cur = 0
        d = 1
        while d < NKT:
            nx = 1 - cur
            P.dve.tensor_copy(out=sc[nx][:, 0:d], in_=sc[cur][:, 0:d])
            P.dve.tensor_tensor(out=sc[nx][:, d:NKT], in0=sc[cur][:, d:NKT], in1=sc[cur][:, 0:NKT - d], op=ALU.add)
            cur = nx
            d *= 2
        P.dve.tensor_tensor(out=Ff.full(), in0=wi.full(), in1=sc[cur].full(), op=ALU.add)
        P.dve.tensor_scalar(out=negF.full(), in0=Ff.full(), scalar1=-1.0, scalar2=None, op0=ALU.mult)
        ps = pb[7]
        P.pe.transpose(out=ps[0:64, 0:128], in_=Ff.full(), identity=ident)
        P.act.copy(out=FT.full(), in_=ps[0:64, 0:128])
        P.dve.tensor_copy(out=fh[0].full(), in_=FT.full())
        P.dve.tensor_tensor(out=r1.full(), in0=FT.full(), in1=fh[0].full(), op=ALU.subtract)
        P.dve.tensor_copy(out=fh[1].full(), in_=r1.full())
        P.dve.tensor_tensor(out=r1.full(), in0=r1.full(), in1=fh[1].full(), op=ALU.subtract)
        P.dve.tensor_copy(out=fh[2].full(), in_=r1.full())
        for r in range(3):
            P.dma("sync", out=fsc[r].re("(kt p) -> kt p", p=128), in_=fh[r].full())
        P.dma("sync", out=K_sb[0:64, :], in_=kf[h])
        P.dve.memset(ap=K_sb[64:67, :], constant=8.0)
        P.dma("gpsimd", out=Q_sb[0:64, :], in_=qf[h])
        P.dma("gpsimd", out=Q_sb[64:67, :], in_=fsc.full())
        P.dma("sync", out=V_sb[:, :, 0:64], in_=vf[h])
        attention(67, 0.125, 4, lambda kt: negF[:, kt:kt + 1], o_f[h])

    a_sb = P.sb("a_sb", [128, NKT, 4], F32)
    dt_sb = P.sb("dt_sb", [128, NKT, 4], F32)
    Dsb = P.sb("Dsb", [128, 4], F32)
    P.dma("sync", out=a_sb.full(), in_=a_tm.full())
    P.dma("sync", out=dt_sb.full(), in_=dt_tm.full())
    P.dma("sync", out=Dsb.full(), in_=Dv.full())
    BTs = K_sb
    CTs = Q_sb
    P.dma("sync", out=BTs.full(), in_=BT.full())
    P.dma("gpsimd", out=CTs.full(), in_=CT.full())
    Acum = P.sb("Acum", [128, NKT, 4], F32)
    nAcum = P.sb("nAcum", [128, NKT, 4], F32)
    Atot = P.sb("Atot", [128, NKT, 4], F32)
    eA = P.sb("eA", [128, NKT, 4], F32)
    wdec = P.sb("wdec", [128, NKT, 4], F32)
    eAtot = P.sb("eAtot", [128, NKT, 4], F32)
    fl = lambda b: b.full().re("p c h -> p (c h)")
    ps = pb[0]
    P.pe.matmul(out=ps[:, 0:256], lhsT=tri, rhs=fl(a_sb), start=True, stop=True)
    P.act.copy(out=fl(Acum), in_=ps[:, 0:256])
    ps = pb[1]
    P.pe.matmul(out=ps[:, 0:256], lhsT=ones, rhs=fl(a_sb), start=True, stop=True)
    P.act.copy(out=fl(Atot), in_=ps[:, 0:256])
    P.dve.tensor_scalar(out=fl(nAcum), in0=fl(Acum), scalar1=-1.0, scalar2=None, op0=ALU.mult)
    P.act.activation(out=fl(eA), in_=fl(Acum), func=AF.Exp)
    P.act.activation(out=fl(eAtot), in_=fl(Atot), func=AF.Exp)
    P.dve.tensor_tensor(out=fl(wdec), in0=fl(Atot), in1=fl(Acum), op=ALU.subtract)
    P.act.activation(out=fl(wdec), in_=fl(wdec), func=AF.Exp)

    Hs = P.sb("Hs", [128, 256], F32)
    Hb = P.sb("Hb", [128, 256], BF16)
    P.dve.memset(ap=Hs.full(), constant=0.0)
    P.dve.memset(ap=Hb.full(), constant=0.0)
    xc = [P.sb(f"xc{i}", [128, 256], BF16) for i in range(2)]
    Bc = [P.sb(f"Bc{i}", [128, 128], BF16) for i in range(2)]
    cb = P.sb("cb", [128, 128], F32)
    xdt = P.sb("xdt", [128, 256], BF16)
    xdts = P.sb("xdts", [128, 256], BF16)
    at = [P.sb(f"at{i}", [128, 128], F32) for i in range(2)]
    tm = [P.sb(f"tm{i}", [128, 128], F32) for i in range(2)]
    dec = [P.sb(f"dec{i}", [128, 128], F32) for i in range(2)]
    MT = [P.sb(f"MT{i}", [128, 128], BF16) for i in range(2)]
    t1 = P.sb("t1", [128, 256], F32)
    t3 = P.sb("t3", [128, 256], F32)
    yo = [P.sb(f"yo{i}", [128, 256], F32) for i in range(2)]
    v3 = lambda v: v.re("p (h d) -> p h d", h=4)
    bc3 = lambda v: v.f(lambda a: a.unsqueeze(2).to_broadcast([128, 4, 64]))
    for c in range(NKT):
        x_c = xc[c % 2]
        B_c = Bc[c % 2]
        P.dma("sync", out=x_c.full(), in_=x_tm[:, c, :])
        P.dma("gpsimd", out=B_c.full(), in_=B_tm[:, c, :])
        BT_c = BTs[:, c * 128:(c + 1) * 128]
        CT_c = CTs[:, c * 128:(c + 1) * 128]
        ps_cb = pb[0]
        P.pe.matmul(out=ps_cb[:, 0:128], lhsT=BT_c, rhs=CT_c, start=True, stop=True)
        P.act.copy(out=cb.full(), in_=ps_cb[:, 0:128])
        P.dve.tensor_tensor(out=v3(xdt.full()), in0=v3(x_c.full()), in1=bc3(dt_sb[:, c, :]), op=ALU.mult)
        P.pool.tensor_tensor(out=v3(xdts.full()), in0=v3(xdt.full()), in1=bc3(wdec[:, c, :]), op=ALU.mult)
        ps_off = pb[1]
        P.pe.matmul(out=ps_off[:, 0:256], lhsT=CT_c, rhs=Hb.full(), start=True, stop=True)
        ps_y = pb[2]
        for h in range(4):
            i2 = h % 2
            P.dve.tensor_scalar(out=at[i2].full(), in0=tri, scalar1=a_sb[:, c, h:h + 1], scalar2=None, op0=ALU.mult)
            ps_A = pb[3 + i2]
            P.pe.matmul(out=ps_A[:, 0:128], lhsT=ones, rhs=at[i2].full(), start=True, stop=True)
            P.dve.tensor_tensor(out=tm[i2].full(), in0=ps_A[:, 0:128], in1=trimask, op=ALU.add)
            P.act.activation(out=dec[i2].full(), in_=tm[i2].full(), func=AF.Exp, bias=nAcum[:, c, h:h + 1], scale=1.0)
            P.pool.tensor_tensor(out=MT[i2].full(), in0=cb.full(), in1=dec[i2].full(), op=ALU.mult)
            P.pe.matmul(out=ps_y[:, h * 64:(h + 1) * 64], lhsT=MT[i2].full(), rhs=xdt[:, h * 64:(h + 1) * 64],
                        start=True, stop=True)
        P.dve.tensor_tensor(out=v3(t1.full()), in0=v3(ps_off[:, 0:256]), in1=bc3(eA[:, c, :]), op=ALU.mult)
        P.dve.tensor_tensor(out=t1.full(), in0=t1.full(), in1=ps_y[:, 0:256], op=ALU.add)
        P.pool.tensor_tensor(out=v3(t3.full()), in0=v3(x_c.full()), in1=bc3(Dsb.full()), op=ALU.mult)
        y_ = yo[c % 2]
        P.pool.tensor_tensor(out=y_.full(), in0=t1.full(), in1=t3.full(), op=ALU.add)
        P.dma("sync", out=o_y[:, c, :], in_=y_.full())
        ps_h = pb[5]
        P.pe.matmul(out=ps_h[:, 0:256], lhsT=B_c.full(), rhs=xdts.full(), start=True, stop=True)
        P.dve.tensor_tensor(out=v3(Hs.full()), in0=v3(Hs.full()), in1=bc3(eAtot[:, c, :]), op=ALU.mult)
        P.dve.tensor_tensor(out=Hs.full(), in0=Hs.full(), in1=ps_h[:, 0:256], op=ALU.add)
        P.act.copy(out=Hb.full(), in_=Hs.full())
    P.emit()
    return nc, P


def _bc_consts():
    msk = np.zeros((128, 8, 512), np.float32)
    p = np.arange(128)[:, None]
    q = np.arange(512)[None, :]
    for j in range(4):
        key = j * 128 + p
        msk[:, j, :] = np.where((key // 64) > (q // 64), NEG, 0.0)
        msk[:, 4 + j, :] = np.where(key > q, NEG, 0.0)
    cm = np.zeros((128, 4, 128), np.float32)
    jj = np.arange(128)[:, None]
    ii = np.arange(128)[None, :]
    cm[:, 0, :] = (jj <= ii).astype(np.float32)
    cm[:, 1, :] = np.where(jj > ii, NEG, 0.0)
    cm[:, 2, :] = np.eye(128, dtype=np.float32)
    cm[:, 3, :] = 1.0
    return msk, cm


def _tm(a):
    S, n = a.shape
    return np.ascontiguousarray(a.reshape(S // 128, 128, n).transpose(1, 0, 2))


def run_BC(inp, l, resA):
    if "BC" not in _PROG_CACHE:
        _PROG_CACHE["BC"] = build_BC()[0]
    nc = _PROG_CACHE["BC"]
    msk, cm = _bc_consts()

    def gather(name, b):
        return np.concatenate([np.asarray(resA[b * 4 + i][name]) for i in range(4)], axis=-1)

    in_maps = []
    for c in range(8):
        b, hg = c // 4, c % 4
        qm = gather("o_qm", b)[2 * hg:2 * hg + 2]
        km = gather("o_km", b)[2 * hg:2 * hg + 2]
        vmf = gather("o_vm", b)
        qf = gather("o_qf", b)[2 * hg:2 * hg + 2]
        kf = gather("o_kf", b)[2 * hg:2 * hg + 2]
        vff = gather("o_vf", b)
        lff = gather("o_lf", b)
        xbc = gather("o_xbc", b)
        dtf = gather("o_dt", b)
        af = gather("o_a", b)
        g = hg // 2
        vm = np.stack([_tm(vmf[(2 * hg + h) * 64:(2 * hg + h + 1) * 64].T) for h in range(2)])
        vf = np.stack([_tm(vff[(2 * hg + h) * 64:(2 * hg + h + 1) * 64].T) for h in range(2)])
        lf = np.stack([np.ascontiguousarray(lff[2 * hg + h].reshape(NKT, 128).T) for h in range(2)])
        x_tm = _tm(xbc[hg * 256:(hg + 1) * 256].T)
        Bf = xbc[1024 + g * 128:1024 + (g + 1) * 128]
        Cf = xbc[1280 + g * 128:1280 + (g + 1) * 128]
        Dv = np.broadcast_to(inp["ssm_D"][l][4 * hg:4 * hg + 4][None, :], (128, 4)).astype(np.float32)
        in_maps.append({
            "qm": np.ascontiguousarray(qm), "km": np.ascontiguousarray(km), "vm": vm,
            "qf": np.ascontiguousarray(qf), "kf": np.ascontiguousarray(kf), "vf": vf, "lf": lf,
            "msk": msk, "cm": cm, "x_tm": x_tm, "B_tm": _tm(Bf.T), "BT": np.ascontiguousarray(Bf),
            "CT": np.ascontiguousarray(Cf), "dt_tm": _tm(dtf[4 * hg:4 * hg + 4].T),
            "a_tm": _tm(af[4 * hg:4 * hg + 4].T), "Dv": np.ascontiguousarray(Dv),
        })
    res = run_bass_kernel_spmd(nc, in_maps, core_ids=list(range(8))).results
    om = np.zeros((2, 512, S_), NBF)
    of = np.zeros((2, 512, S_), NBF)
    y = np.zeros((2, 1024, S_), np.float32)
    for c in range(8):
        b, hg = c // 4, c % 4
        om[b, hg * 128:(hg + 1) * 128] = np.asarray(res[c]["o_m"]).reshape(128, S_)
        of[b, hg * 128:(hg + 1) * 128] = np.asarray(res[c]["o_f"]).reshape(128, S_)
        yy = np.asarray(res[c]["o_y"])
        y[b, hg * 256:(hg + 1) * 256] = yy.transpose(2, 1, 0).reshape(256, S_)
    return om, of, y


def build_D1():
    nc = new_nc()
    P = Prog(nc)
    EI, EO = "ExternalInput", "ExternalOutput"
    NT = T // 512
    omT = P.dram("omT", [512, T], BF16, EI)
    ofT = P.dram("ofT", [512, T], BF16, EI)
    yT = P.dram("yT", [1024, T], F32, EI)
    szT = P.dram("szT", [1024, T], BF16, EI)
    gT = P.dram("gT", [3072, T], BF16, EI)
    xT = P.dram("xT", [D, T], F32, EI)
    w_a = P.dram("w_a", [512, D], F32, EI)
    w_b = P.dram("w_b", [512, D], F32, EI)
    w_c = P.dram("w_c", [1024, D], F32, EI)
    w_o = P.dram("w_o", [1024, D], F32, EI)
    cst_d = P.dram("cst", [128, 8], F32, EI)
    o_x = P.dram("o_x", [D, T], F32, EO)

    cstb = P.sb("cstb", [128, 8], F32)
    P.dma("sync", out=cstb.full(), in_=cst_d.full())
    ones = P.sb("ones", [128, 128], F32)
    P.dve.memset(ap=ones.full(), constant=1.0)
    eps = P.sb("eps", [128, 1], F32)
    P.dve.memset(ap=eps.full(), constant=1e-6)
    pb = [P.ps(f"pb{i}", [128, 512], F32) for i in range(8)]
    wst = [P.sb(f"wst{i}", [128, 4, 1024], F32) for i in range(2)]
    wcnt = [0]

    def load_w(dram, kc_n, name):
        bfb = P.sb(name, [128, kc_n, D], BF16)
        v = dram.full().re("(kc p) n -> p kc n", p=128)
        for k0 in range(0, kc_n, 4):
            i = wcnt[0] % 2
            wcnt[0] += 1
            P.dma("sync" if i == 0 else "gpsimd", out=wst[i].full(), in_=v[:, k0:k0 + 4, :])
            P.pool.tensor_copy(out=bfb[:, k0:k0 + 4, :], in_=wst[i].full())
        return bfb

    Wa = load_w(w_a, 4, "Wa")
    Wb = load_w(w_b, 4, "Wb")
    Wc = load_w(w_c, 8, "Wc")
    Wo = load_w(w_o, 8, "Wo")

    ys = P.sb("ys", [128, 8, 512], F32)
    szs = P.sb("szs", [128, 8, 512], BF16)
    yn = P.sb("yn", [128, 8, 512], BF16)
    oms = P.sb("oms", [128, 4, 512], BF16)
    ofs = P.sb("ofs", [128, 4, 512], BF16)
    gs = P.sb("gs", [128, 24, 512], BF16)
    xs = P.sb("xs", [128, 8, 512], F32)
    sq = P.sb("sq", [128, 512], F32)
    rstd = P.sb("rstd", [128, 512], F32)
    m1 = [P.sb(f"m1_{i}", [128, 512], F32) for i in range(2)]
    m2 = [P.sb(f"m2_{i}", [128, 512], F32) for i in range(2)]
    m3 = [P.sb(f"m3_{i}", [128, 512], F32) for i in range(2)]
    mg = P.sb("mg", [128, 8, 512], BF16)
    xo = [P.sb(f"xo{i}", [128, 512], F32) for i in range(2)]
    ch = lambda d: d.full().re("(kc p) n -> p kc n", p=128)
    for ti in range(NT):
        ts = slice(ti * 512, (ti + 1) * 512)
        P.dma("sync", out=ys.full(), in_=ch(yT)[:, :, ts])
        P.dma("gpsimd", out=szs.full(), in_=ch(szT)[:, :, ts])
        P.dma("sync", out=oms.full(), in_=ch(omT)[:, :, ts])
        P.dma("gpsimd", out=ofs.full(), in_=ch(ofT)[:, :, ts])
        P.dma("sync", out=gs.full(), in_=ch(gT)[:, :, ts])
        P.dma("gpsimd", out=xs.full(), in_=ch(xT)[:, :, ts])
        ps = pb[7]
        for kc in range(8):
            P.dve.tensor_tensor(out=ys[:, kc, :], in0=ys[:, kc, :], in1=szs[:, kc, :], op=ALU.mult)
            P.act.activation(out=sq.full(), in_=ys[:, kc, :], func=AF.Square)
            P.pe.matmul(out=ps.full(), lhsT=ones.full(), rhs=sq.full(), start=(kc == 0), stop=(kc == 7))
        P.act.activation(out=rstd.full(), in_=ps.full(), func=AF.Sqrt, bias=eps[:, 0:1], scale=1.0 / 1024.0)
        P.dve.reciprocal(out=rstd.full(), in_=rstd.full())
        for kc in range(8):
            P.dve.scalar_tensor_tensor(out=yn[:, kc, :], in0=ys[:, kc, :], scalar=cstb[:, kc:kc + 1], in1=rstd.full(),
                                       op0=ALU.mult, op1=ALU.mult)
        for oc in range(8):
            i2 = oc % 2
            osl = slice(oc * 128, (oc + 1) * 128)
            pa, pbb, pc = pb[0 + i2 * 3], pb[1 + i2 * 3], pb[2 + i2 * 3]
            for kc in range(4):
                P.pe.matmul(out=pa.full(), lhsT=Wa[:, kc, osl], rhs=oms[:, kc, :], start=(kc == 0), stop=(kc == 3))
            for kc in range(4):
                P.pe.matmul(out=pbb.full(), lhsT=Wb[:, kc, osl], rhs=ofs[:, kc, :], start=(kc == 0), stop=(kc == 3))
            for kc in range(8):
                P.pe.matmul(out=pc.full(), lhsT=Wc[:, kc, osl], rhs=yn[:, kc, :], start=(kc == 0), stop=(kc == 7))
            P.dve.tensor_tensor(out=m1[i2].full(), in0=pa.full(), in1=gs[:, oc, :], op=ALU.mult)
            P.dve.tensor_tensor(out=m2[i2].full(), in0=pbb.full(), in1=gs[:, 8 + oc, :], op=ALU.mult)
            P.dve.tensor_tensor(out=m3[i2].full(), in0=pc.full(), in1=gs[:, 16 + oc, :], op=ALU.mult)
            P.pool.tensor_tensor(out=m1[i2].full(), in0=m1[i2].full(), in1=m2[i2].full(), op=ALU.add)
            P.pool.tensor_tensor(out=mg[:, oc, :], in0=m1[i2].full(), in1=m3[i2].full(), op=ALU.add)
        for oc in range(8):
            i2 = oc % 2
            ps = pb[6 + i2]
            for kc in range(8):
                P.pe.matmul(out=ps.full(), lhsT=Wo[:, kc, oc * 128:(oc + 1) * 128], rhs=mg[:, kc, :],
                            start=(kc == 0), stop=(kc == 7))
            P.dve.tensor_tensor(out=xo[i2].full(), in0=ps.full(), in1=xs[:, oc, :], op=ALU.add)
            P.dma("sync" if i2 else "gpsimd", out=o_x[oc * 128:(oc + 1) * 128, ts], in_=xo[i2].full())
    P.emit()
    return nc, P


def d2_colpack(inp, l):
    cp = ColPack()
    cp.add("g_ffn", inp["norm_ffn_g"][l])
    cw = inp["ffn_conv_w"][l]
    for k in range(3):
        cp.add(f"fw{k}", cw[k])
    cp.add("fb", inp["ffn_conv_b"][l])
    return cp


def build_D2(off):
    nc = new_nc()
    P = Prog(nc)
    EI, EO = "ExternalInput", "ExternalOutput"
    NT = T // 512
    TT = T + HALO
    xT = P.dram("xT", [D, TT], F32, EI)
    w_up = P.dram("w_up", [D, 5632], F32, EI)
    w_dn = P.dram("w_dn", [2816, D], F32, EI)
    cst_d = P.dram("cst", [128, off["_n"]], F32, EI)
    o_x = P.dram("o_x", [D, T], F32, EO)

    cstb = P.sb("cstb", [128, off["_n"]], F32)
    C = Cst(P, cstb, off)
    P.dma("sync", out=cstb.full(), in_=cst_d.full())
    ones = P.sb("ones", [128, 128], F32)
    P.dve.memset(ap=ones.full(), constant=1.0)
    eps = P.sb("eps", [128, 1], F32)
    P.dve.memset(ap=eps.full(), constant=1e-6)
    pb = [P.ps(f"pb{i}", [128, 512], F32) for i in range(8)]
    wst = [P.sb(f"wst{i}", [128, 1024], F32) for i in range(2)]
    wcnt = [0]
    Wu = P.sb("Wu", [128, 8, 5632], BF16)
    Wd = P.sb("Wd", [128, 22, D], BF16)
    wuv = w_up.full().re("(kc p) n -> p kc n", p=128)
    for kc in range(8):
        for c0 in range(0, 5632, 1024):
            n = min(1024, 5632 - c0)
            i = wcnt[0] % 2
            wcnt[0] += 1
            P.dma("sync" if i == 0 else "gpsimd", out=wst[i][:, 0:n], in_=wuv[:, kc, c0:c0 + n])
            P.pool.tensor_copy(out=Wu[:, kc, c0:c0 + n], in_=wst[i][:, 0:n])
    wdv = w_dn.full().re("(kc p) n -> p kc n", p=128)
    for k0 in range(22):
        i = wcnt[0] % 2
        wcnt[0] += 1
        P.dma("sync" if i == 0 else "gpsimd", out=wst[i].full(), in_=wdv[:, k0, :])
        P.pool.tensor_copy(out=Wd[:, k0, :], in_=wst[i].full())

    xst = P.sb("xst", [128, 8, 512], F32)
    hn = P.sb("hn", [128, 8, 512], BF16)
    sq = P.sb("sq", [128, 512], F32)
    rstd = P.sb("rstd", [128, 512], F32)
    act = P.sb("act", [128, 22, 512], BF16)
    upre = [P.sb(f"upre{i}", [128, 516], F32) for i in range(2)]
    acc = [P.sb(f"acc{i}", [128, 512], F32) for i in range(2)]
    sg = P.sb("sg", [128, 512], F32)
    carry = P.sb("carry", [128, 44, 4], F32)
    xo = [P.sb(f"xo{i}", [128, 512], F32) for i in range(2)]
    xTv = xT.full().re("(kc p) n -> p kc n", p=128)
    tiles = [(0, HALO)] + [(HALO + i * 512, 512) for i in range(NT)]
    pcnt = [0]
    for tix, (c0, w) in enumerate(tiles):
        P.dma("sync", out=xst[:, :, 0:w], in_=xTv[:, :, c0:c0 + w])
        ps = pb[7]
        for kc in range(8):
            P.act.activation(out=sq[:, 0:w], in_=xst[:, kc, 0:w], func=AF.Square)
            P.pe.matmul(out=ps[:, 0:w], lhsT=ones.full(), rhs=sq[:, 0:w], start=(kc == 0), stop=(kc == 7))
        P.act.activation(out=rstd[:, 0:w], in_=ps[:, 0:w], func=AF.Sqrt, bias=eps[:, 0:1], scale=1.0 / 1024.0)
        P.dve.reciprocal(out=rstd[:, 0:w], in_=rstd[:, 0:w])
        for kc in range(8):
            P.dve.scalar_tensor_tensor(out=hn[:, kc, 0:w], in0=xst[:, kc, 0:w], scalar=C.col("g_ffn", kc),
                                       in1=rstd[:, 0:w], op0=ALU.mult, op1=ALU.mult)
        for i in range(22):
            accs = []
            for j, cg in enumerate((i, 22 + i)):
                ps = pb[pcnt[0] % 4]
                pcnt[0] += 1
                for kc in range(8):
                    P.pe.matmul(out=ps[:, 0:w], lhsT=Wu[:, kc, cg * 128:(cg + 1) * 128], rhs=hn[:, kc, 0:w],
                                start=(kc == 0), stop=(kc == 7))
                if tix == 0:
                    P.act.copy(out=carry[:, cg, :], in_=ps[:, 0:HALO])
                    continue
                up = upre[j]
                P.act.copy(out=up[:, 4:516], in_=ps.full())
                P.dve.tensor_copy(out=up[:, 0:4], in_=carry[:, cg, :])
                P.pool.tensor_copy(out=carry[:, cg, :], in_=up[:, 512:516])
                a0 = acc[j]
                P.dve.tensor_scalar(out=a0.full(), in0=up[:, 4:516], scalar1=C.col("fw2", cg), scalar2=C.col("fb", cg),
                                    op0=ALU.mult, op1=ALU.add)
                P.dve.scalar_tensor_tensor(out=a0.full(), in0=up[:, 3:515], scalar=C.col("fw1", cg), in1=a0.full(),
                                           op0=ALU.mult, op1=ALU.add)
                P.dve.scalar_tensor_tensor(out=a0.full(), in0=up[:, 2:514], scalar=C.col("fw0", cg), in1=a0.full(),
                                           op0=ALU.mult, op1=ALU.add)
                accs.append(a0)
            if tix == 0:
                continue
            P.act.activation(out=sg.full(), in_=accs[0].full(), func=AF.Silu)
            P.pool.tensor_tensor(out=act[:, i, :], in0=sg.full(), in1=accs[1].full(), op=ALU.mult)
        if tix == 0:
            continue
        ti = tix - 1
        for oc in range(8):
            i2 = oc % 2
            ps = pb[4 + i2]
            for i in range(22):
                P.pe.matmul(out=ps.full(), lhsT=Wd[:, i, oc * 128:(oc + 1) * 128], rhs=act[:, i, :],
                            start=(i == 0), stop=(i == 21))
            P.dve.tensor_tensor(out=xo[i2].full(), in0=ps.full(), in1=xst[:, oc, :], op=ALU.add)
            P.dma("sync" if i2 else "gpsimd", out=o_x[oc * 128:(oc + 1) * 128, ti * 512:(ti + 1) * 512], in_=xo[i2].full())
    P.emit()
    return nc, P


def run_D1(inp, l, resA, om, of, y, x_full):
    if "D1" not in _PROG_CACHE:
        _PROG_CACHE["D1"] = build_D1()[0]
    nc = _PROG_CACHE["D1"]
    cst = np.ascontiguousarray(inp["ssm_norm_g"][l].reshape(8, 128).T)
    xf = x_full.reshape(16384, D)
    in_maps = []
    for c in range(8):
        b, q = c // 4, c % 4
        ts = slice(q * T, (q + 1) * T)
        in_maps.append({
            "omT": np.ascontiguousarray(om[b][:, ts]), "ofT": np.ascontiguousarray(of[b][:, ts]),
            "yT": np.ascontiguousarray(y[b][:, ts]), "szT": np.asarray(resA[c]["o_sz"]),
            "gT": np.asarray(resA[c]["o_g"]), "xT": np.ascontiguousarray(xf[c * T:(c + 1) * T].T),
            "w_a": np.ascontiguousarray(inp["w_br_mla"][l]), "w_b": np.ascontiguousarray(inp["w_br_fox"][l]),
            "w_c": np.ascontiguousarray(inp["w_br_ssm"][l]), "w_o": np.ascontiguousarray(inp["w_out"][l]),
            "cst": cst,
        })
    res = run_bass_kernel_spmd(nc, in_maps, core_ids=list(range(8))).results
    xm = np.concatenate([np.asarray(r["o_x"]).T for r in res], axis=0)
    return xm.reshape(2, S_, D)


def run_D2(inp, l, xm_full):
    cp = d2_colpack(inp, l)
    off = dict(cp.off)
    off["_n"] = cp.n
    if "D2" not in _PROG_CACHE:
        _PROG_CACHE["D2"] = build_D2(off)[0]
    nc = _PROG_CACHE["D2"]
    cst = cp.array()
    xf = xm_full.reshape(16384, D)
    in_maps = []
    for c in range(8):
        t0 = c * T
        xt = np.zeros((D, T + HALO), np.float32)
        xt[:, HALO:] = xf[t0:t0 + T].T
        if c % 4 != 0:
            xt[:, 0:HALO] = xf[t0 - HALO:t0].T
        in_maps.append({"xT": np.ascontiguousarray(xt), "w_up": np.ascontiguousarray(inp["ffn_w_up"][l]),
                        "w_dn": np.ascontiguousarray(inp["ffn_w_down"][l]), "cst": cst})
    res = run_bass_kernel_spmd(nc, in_maps, core_ids=list(range(8))).results
    xo = np.concatenate([np.asarray(r["o_x"]).T for r in res], axis=0)
    return xo.reshape(2, S_, D)


def kernel_unfused(**inp):
    inp = {k: np.asarray(v) for k, v in inp.items()}
    x = inp["x"].astype(np.float32)
    pos = inp["positions"]
    for l in range(2):
        resA = run_A(inp, l, x, pos)
        om, of, y = run_BC(inp, l, resA)
        xm = run_D1(inp, l, resA, om, of, y, x)
        x = run_D2(inp, l, xm)
    return np.ascontiguousarray(x.astype(np.float32))


def kernel(**inp):
    return kernel_fused(**inp)


SW = 516
RG = [[0, 1, 2, 3], [4, 5, 6, 7]]
KT_L = T // 128


def fused_rowpack(inp, l):
    r = np.concatenate([inp["fox_b_f"][l], inp["ssm_dt_bias"][l], inp["ssm_A_log"][l], inp["ssm_D"][l]]).astype(np.float32)
    return np.ascontiguousarray(np.broadcast_to(r[None, :], (128, r.size)))


def build_fused(offA, offD2, stop=None, dbg=()):
    nc = new_nc()
    P = Prog(nc)
    EI, EO = "ExternalInput", "ExternalOutput"
    L = 2
    x0 = P.dram("x0", [D, 4 * SW], F32, EI)
    pos = P.dram("pos", [1, T], I32, EI)
    w_in = P.dram("w_in", [L, D, 7864], F32, EI)
    w_uq = P.dram("w_uq", [L, 384, 768], F32, EI)
    w_kp = P.dram("w_kp", [L, 256, 768], F32, EI)
    w_v = P.dram("w_v", [L, 256, 512], F32, EI)
    w_a = P.dram("w_a", [L, 512, D], F32, EI)
    w_b = P.dram("w_b", [L, 512, D], F32, EI)
    w_c = P.dram("w_c", [L, 1024, D], F32, EI)
    w_o = P.dram("w_o", [L, 1024, D], F32, EI)
    w_up = P.dram("w_up", [L, D, 5632], F32, EI)
    w_dn = P.dram("w_dn", [L, 2816, D], F32, EI)
    cstA_d = P.dram("cstA", [L, 128, offA["_n"]], F32, EI)
    cstD_d = P.dram("cstD", [L, 128, offD2["_n"]], F32, EI)
    gssm_d = P.dram("gssm", [L, 128, 8], F32, EI)
    rowc_d = P.dram("rowc", [L, 128, 56], F32, EI)
    sel_d = P.dram("sel", [128, 32], F32, EI)
    msk_d = P.dram("msk", [128, 8, 512], F32, EI)
    cm_d = P.dram("cm", [128, 4, 128], F32, EI)
    mats_d = P.dram("mats", [128, 192], F32, EI)
    out = P.dram("out", [D, T], F32, EO)
    xb = [x0, P.dram("xb1", [D, 4 * SW], F32)]
    xmid = P.dram("xmid", [D, 4 * SW], F32)
    qm = P.dram("qm", [8, 96, T], BF16)
    qf = P.dram("qf", [8, 64, T], BF16)
    fq = P.dram("fq", [8, 3, T], BF16)
    szd = P.dram("szd", [1024, T], BF16)
    gd = P.dram("gd", [3072, T], BF16)
    xtm = P.dram("xtm", [128, KT_L, 1024], BF16)
    btm = P.dram("btm", [128, KT_L, 256], BF16)
    bct = P.dram("bct", [512, T], BF16)
    dtd = P.dram("dtd", [128, KT_L, 16], F32)
    atd = P.dram("atd", [128, KT_L, 16], F32)
    omd = P.dram("omd", [512, T], BF16)
    ofd = P.dram("ofd", [512, T], BF16)
    yd = P.dram("yd", [1024, T], F32)
    kxm = [P.dram(f"kxm{m}", [768, 512], BF16) for m in range(4)]
    kxmg = [P.dram(f"kxmg{m}", [4 * 768, 512], BF16) for m in range(4)]
    kxf = [P.dram(f"kxf{m}", [640, 512], BF16) for m in range(4)]
    kxfg = [P.dram(f"kxfg{m}", [4 * 640, 512], BF16) for m in range(4)]
    nfq = P.dram("nfq", [8, 3, T], BF16)
    vx = [P.dram(f"vx{m}", [2048, 256], BF16) for m in range(4)]
    vxg = [P.dram(f"vxg{m}", [4 * 2048, 256], BF16) for m in range(4)]
    sx = [P.dram(f"sx{i}", [256, 1024], F32) for i in range(2)]
    sxg = [P.dram(f"sxg{i}", [4 * 256, 1024], F32) for i in range(2)]
    fx = P.dram("fx", [128, 224], F32)
    fxg = P.dram("fxg", [4 * 128, 224], F32)
    tx = P.dram("tx", [128, 128], F32)
    txg = P.dram("txg", [4 * 128, 128], F32)
    wub = P.dram("wub", [D, 5632], BF16)
    wdb = P.dram("wdb", [2816, D], BF16)
    wab = P.dram("wab", [512, D], BF16)
    wbb = P.dram("wbb", [512, D], BF16)
    wcb = P.dram("wcb", [1024, D], BF16)
    wob = P.dram("wob", [1024, D], BF16)
    dbg_out = {}

    def gather_pairs(pairs, after=None):
        for (a, b) in pairs:
            kw = {}
            if after is not None:
                kw["_reads"] = [after]
            P.pool.collective_compute(kind="AllGather", op=ALU.bypass, replica_groups=RG,
                                      ins=[a.full().re("(p a) c -> p (a c)", p=128)],
                                      outs=[b.full().re("(q a) c -> q (a c)", q=512)], **kw)

    def load_consts():
        d = {}
        d["cm"] = P.sb("cmb", [128, 4, 128], F32)
        P.dma("sync", out=d["cm"].full(), in_=cm_d.full())
        d["sel"] = P.sb("selb", [128, 32], F32)
        P.dma("sync", out=d["sel"].full(), in_=sel_d.full())
        d["eps"] = P.sb("eps", [128, 1], F32)
        P.dve.memset(ap=d["eps"].full(), constant=1e-6)
        d["one1"] = P.sb("one1", [128, 1], F32)
        P.dve.memset(ap=d["one1"].full(), constant=1.0)
        d["zero"] = P.sb("zero", [128, 1], F32)
        P.dve.memset(ap=d["zero"].full(), constant=0.0)
        return d

    def phase_A(l):
        K = load_consts()
        cmb = K["cm"]
        tri, ident, ones = cmb[:, 0, :], cmb[:, 2, :], cmb[:, 3, :]
        eps, one1 = K["eps"], K["one1"]
        xin = xb[l]
        cstb = P.sb("cstb", [128, offA["_n"]], F32)
        C = Cst(P, cstb, offA)
        P.dma("sync", out=cstb.full(), in_=cstA_d[l])
        rowc = P.sb("rowc", [128, 56], F32)
        P.dma("sync", out=rowc.full(), in_=rowc_d[l])
        matf = P.sb("matf", [128, 192], F32)
        matb = P.sb("matb", [128, 192], BF16)
        P.dma("sync", out=matf.full(), in_=mats_d.full())
        P.dve.tensor_copy(out=matb.full(), in_=matf.full())
        prh = matb[0:96, 0:96]
        selm = matb[0:32, 96:192]
        identb = P.sb("identb", [128, 128], BF16)
        P.dve.tensor_copy(out=identb.full(), in_=ident)
        Aneg_r = P.sb("Aneg_r", [128, 16], F32)
        P.act.activation(out=Aneg_r.full(), in_=rowc[:, 24:40], func=AF.Exp)
        P.dve.tensor_scalar(out=Aneg_r.full(), in0=Aneg_r.full(), scalar1=-1.0, scalar2=None, op0=ALU.mult)

        pb = [P.ps(f"pb{i}", [128, 512], F32) for i in range(7)]
        pbt = P.ps("pbt", [128, 1024], BF16)
        pbi = {}

        def nxt_ps(lo=0, hi=4):
            i = pbi.get(lo, 0)
            pbi[lo] = (i + 1) % (hi - lo)
            return pb[lo + i]

        Ctab = P.sb("Ctab", [96, T], F32)
        Stab = P.sb("Stab", [96, T], F32)
        hraw = P.sb("hraw", [96, 512], F32)
        hsq = P.sb("hsq", [96, 512], F32)
        hrs = P.sb("hrs", [96, 512], F32)
        hnf = P.sb("hnf", [96, 512], F32)
        hnb = P.sb("hnb", [96, 512], BF16)
        ht1 = P.sb("ht1", [96, 512], F32)
        ht2 = P.sb("ht2", [96, 512], F32)
        posf, rr_tmp, rr_m = hrs, hraw, hsq

        class _IV:
            def __init__(self, b):
                self.b = b

            def full(self):
                return self.b.full().bitcast(I32)
        posi, rr_i = _IV(ht1), _IV(ht2)

        def sin_table(outv, phase):
            P.dve.tensor_scalar(out=rr_tmp.full(), in0=posf.full(), scalar1=C.col("invf"), scalar2=phase,
                                op0=ALU.mult, op1=ALU.add)
            P.dve.tensor_scalar(out=rr_m.full(), in0=rr_tmp.full(), scalar1=1.0 / (2 * np.pi), scalar2=None, op0=ALU.mult)
            P.dve.tensor_copy(out=rr_i.full(), in_=rr_m.full())
            P.dve.tensor_copy(out=rr_m.full(), in_=rr_i.full())
            P.dve.scalar_tensor_tensor(out=rr_tmp.full(), in0=rr_m.full(), scalar=-2 * np.pi, in1=rr_tmp.full(),
                                       op0=ALU.mult, op1=ALU.add)
            P.dve.tensor_scalar(out=rr_m.full(), in0=rr_tmp.full(), scalar1=np.pi, scalar2=-2 * np.pi, op0=ALU.is_gt, op1=ALU.mult)
            P.dve.tensor_tensor(out=rr_tmp.full(), in0=rr_tmp.full(), in1=rr_m.full(), op=ALU.add)
            P.dve.tensor_scalar(out=rr_m.full(), in0=rr_tmp.full(), scalar1=-np.pi, scalar2=2 * np.pi, op0=ALU.is_lt, op1=ALU.mult)
            P.dve.tensor_tensor(out=rr_tmp.full(), in0=rr_tmp.full(), in1=rr_m.full(), op=ALU.add)
            P.act.activation(out=outv, in_=rr_tmp.full(), func=AF.Sin)

        for i in range(4):
            P.dma("sync", out=posi.full(), in_=pos[:, i * 512:(i + 1) * 512].f(lambda a: a.partition_broadcast(96)))
            P.dve.tensor_copy(out=posf.full(), in_=posi.full())
            sin_table(Stab[:, i * 512:(i + 1) * 512], 0.0)
            sin_table(Ctab[:, i * 512:(i + 1) * 512], np.pi / 2)
        P.dve.memset(ap=Stab[0:64, :], constant=0.0)
        P.dve.memset(ap=Ctab[0:64, :], constant=1.0)

        hn = P.sb("hn", [128, 8, 4 * SW], BF16)
        xst = P.sb("xst", [128, 8, 512], F32)
        sq = P.sb("sq", [128, 512], F32)
        rstd = P.sb("rstd", [128, 512], F32)
        xTv = xin.full().re("(kc p) n -> p kc n", p=128)

        def rstd_from(ps_view, n_feat, rows, rstd_view):
            P.act.activation(out=rstd_view, in_=ps_view, func=AF.Ln, bias=eps[0:rows, 0:1], scale=1.0 / n_feat)
            P.act.activation(out=rstd_view, in_=rstd_view, func=AF.Exp, scale=-0.5)

        halos = [(m * SW, 4) for m in range(4)]
        main = [(m * SW + 4, 512) for m in range(4)]
        for (c0, w) in halos + main:
            P.dma("sync", out=xst[:, :, 0:w], in_=xTv[:, :, c0:c0 + w])
            ps = nxt_ps(4, 6)
            for kc in range(8):
                P.act.activation(out=sq[:, 0:w], in_=xst[:, kc, 0:w], func=AF.Square)
                P.pe.matmul(out=ps[:, 0:w], lhsT=ones, rhs=sq[:, 0:w], start=(kc == 0), stop=(kc == 7))
            rstd_from(ps[:, 0:w], 1024.0, 128, rstd[:, 0:w])
            for kc in range(8):
                P.dve.scalar_tensor_tensor(out=hn[:, kc, c0:c0 + w], in0=xst[:, kc, 0:w], scalar=C.col("g_mix", kc),
                                           in1=rstd[:, 0:w], op0=ALU.mult, op1=ALU.mult)

        wst = [P.sb(f"wst{i}", [128, 8, 256], F32) for i in range(2)]
        wbf = [P.sb(f"wbf{i}", [128, 8, 512], BF16) for i in range(2)]
        wcnt = [0]
        scnt = [0]
        w_inv = w_in[l].re("(kc p) n -> p kc n", p=128)

        SBv = 672 + 1544
        wplan = [(0, 384), (384, 288), (672, 512), (672 + 512, 512), (672 + 1024, 512), (672 + 1536, 8),
                 (SBv + 1024 + 1536, 16), (SBv, 512), (SBv + 512, 512)]
        wplan += [(SBv + 1024 + b_ * 512, 512) for b_ in range(3)]
        wplan += [(SBv + 2576 + b_ * 512, 512) for b_ in range(6)]
        wpend = {}
        cpend = []

        def w_issue(g):
            c0, ncols = wplan[g]
            lst = []
            for h0 in range(0, ncols, 256):
                n = min(256, ncols - h0)
                si = scnt[0] % 2
                scnt[0] += 1
                P.dma("sync", out=wst[si][:, :, 0:n], in_=w_inv[:, :, c0 + h0:c0 + h0 + n])
                lst.append((si, h0, n))
            wpend[g] = lst

        def load_w(c0, ncols):
            g = wcnt[0]
            wcnt[0] += 1
            assert wplan[g] == (c0, ncols), (g, wplan[g], c0, ncols)
            i = g % 2
            if g not in wpend:
                w_issue(g)
            lst = wpend.pop(g)
            for (si, h0, n) in lst:
                P.act.copy(out=wbf[i][:, :, h0:h0 + n], in_=wst[si][:, :, 0:n])
            if g + 1 < len(wplan):
                w_issue(g + 1)
            if cpend:
                gather_pairs([cpend.pop(0)], after=wbf[i].full())
            return wbf[i]

        def proj(wb, wc0, mcols, c0, w, ps_view):
            for kc in range(8):
                P.pe.matmul(out=ps_view, lhsT=wb[:, kc, wc0:wc0 + mcols], rhs=hn[:, kc, c0:c0 + w],
                            start=(kc == 0), stop=(kc == 7))

        def proj_tm(wb, wc0, ncols, tok0, ps_view):
            for kc in range(8):
                P.pe.matmul(out=ps_view, lhsT=hn[:, kc, tok0:tok0 + 128], rhs=wb[:, kc, wc0:wc0 + ncols],
                            start=(kc == 0), stop=(kc == 7))

        ostg_cnt = [0]
        ostg = [P.sb(f"ostg{i}", [128, 512], BF16) for i in range(4)]

        def next_ostg():
            i = ostg_cnt[0] % 4
            ostg_cnt[0] += 1
            return ostg[i]

        def out_dma(dst_view, src_view):
            P.dma("sync" if ostg_cnt[0] % 2 else "scalar", out=dst_view, in_=src_view)

        hsets = [dict(hraw=hraw.full(), hsq=hsq.full(), hrs=hrs.full(), hnf=hnf.full(), hnb=hnb.full(),
                      ht1=ht1.full(), ht2=ht2.full())]
        hnb1 = P.sb("hnb1", [96, 512], BF16)
        hsets.append(dict(hraw=xst[0:96, 0, :].k(0), hsq=xst[0:96, 1, :].k(1), hrs=xst[0:96, 2, :].k(2),
                          hnf=xst[0:96, 3, :].k(3), hnb=hnb1.full(), ht1=xst[0:96, 4, :].k(4), ht2=xst[0:96, 5, :].k(5)))
        hb2 = P.sb("hb2", [96, 6, 512], F32)
        hnb2 = P.sb("hnb2", [96, 512], BF16)
        hsets.append(dict(hraw=hb2[:, 0, :].k(0), hsq=hb2[:, 1, :].k(1), hrs=hb2[:, 2, :].k(2),
                          hnf=hb2[:, 3, :].k(3), hnb=hnb2.full(), ht1=hb2[:, 4, :].k(4), ht2=hb2[:, 5, :].k(5)))
        hcnt = [0]

        def headnorm(projfn, d, gain_col, rope, tok0, dst_view):
            H = hsets[hcnt[0] % 3]
            hcnt[0] += 1
            ps_view = projfn()
            P.act.activation(out=H["hsq"][0:d, :], in_=ps_view, func=AF.Square)
            P.act.copy(out=H["hraw"][0:d, :], in_=ps_view)
            yield
            ps2 = nxt_ps(4, 6)
            P.pe.matmul(out=ps2[0:d, :], lhsT=cmb[0:d, 3, 0:d], rhs=H["hsq"][0:d, :], start=True, stop=True)
            rstd_from(ps2[0:d, :], float(d), d, H["hrs"][0:d, :])
            og = next_ostg()
            if not rope:
                P.dve.scalar_tensor_tensor(out=og[0:d, :], in0=H["hraw"][0:d, :], scalar=gain_col, in1=H["hrs"][0:d, :],
                                           op0=ALU.mult, op1=ALU.mult)
            else:
                P.dve.scalar_tensor_tensor(out=H["hnf"][0:d, :], in0=H["hraw"][0:d, :], scalar=gain_col, in1=H["hrs"][0:d, :],
                                           op0=ALU.mult, op1=ALU.mult)
                P.act.copy(out=H["hnb"][0:d, :], in_=H["hnf"][0:d, :])
                yield
                ps3 = nxt_ps(6, 7)
                P.pe.matmul(out=ps3[0:d, :], lhsT=prh, rhs=H["hnb"][0:d, :], start=True, stop=True)
                P.dve.tensor_tensor(out=H["ht1"][0:d, :], in0=H["hnf"][0:d, :], in1=Ctab[0:d, tok0:tok0 + 512], op=ALU.mult)
                P.dve.tensor_tensor(out=H["ht2"][0:d, :], in0=ps3[0:d, :], in1=Stab[0:d, tok0:tok0 + 512], op=ALU.mult)
                P.pool.tensor_tensor(out=og[0:d, :], in0=H["ht1"][0:d, :], in1=H["ht2"][0:d, :], op=ALU.add)
            out_dma(dst_view, og[0:d, :])

        def run_pipe(gens, depth=3):
            gens = iter(gens)
            active = []
            while True:
                started = False
                if len(active) < depth:
                    g = next(gens, None)
                    if g is not None:
                        started = True
                        try:
                            next(g)
                            active.append(g)
                        except StopIteration:
                            pass
                if not active and not started:
                    break
                olds = active[:-1] if (started and active) else list(active)
                for g in olds:
                    try:
                        next(g)
                    except StopIteration:
                        active.remove(g)

        lat = P.sb("lat", [128, 3, 512], F32)
        latn = P.sb("latn", [128, 3, 512], BF16)

        def latent_norm(ps_list, gname):
            nch = len(ps_list)
            ps2 = nxt_ps(4, 6)
            for i, psv in enumerate(ps_list):
                P.act.activation(out=sq.full(), in_=psv, func=AF.Square)
                P.act.copy(out=lat[:, i, :], in_=psv)
                P.pe.matmul(out=ps2.full(), lhsT=ones, rhs=sq.full(), start=(i == 0), stop=(i == nch - 1))
            rstd_from(ps2.full(), 128.0 * nch, 128, rstd.full())
            for i in range(nch):
                P.dve.scalar_tensor_tensor(out=latn[:, i, :], in0=lat[:, i, :], scalar=C.col(gname, i), in1=rstd.full(),
                                           op0=ALU.mult, op1=ALU.mult)

        def small_w(name, dram_l, kc_n, ncols, i):
            bfb = P.sb(name, [128, kc_n, ncols], BF16)
            dv = dram_l.re("(kc p) n -> p kc n", p=128)
            for kc in range(kc_n):
                si = scnt[0] % 2
                scnt[0] += 1
                stg = wst[si].full().re("p a b -> p (a b)")[:, 0:ncols]
                P.dma("sync", out=stg, in_=dv[:, kc, :])
                P.act.copy(out=bfb[:, kc, :], in_=stg)
            return bfb

        uqb = small_w("uqb", w_uq[l], 3, 768, 0)
        kpb = small_w("kpb", w_kp[l], 2, 768, 1)
        wvb = small_w("wvb", w_v[l], 2, 512, 0)

        vstg = [P.sb(f"vstg{i}", [128, 512], BF16) for i in range(2)]
        vcnt = [0]

        def v_out(kind, ktl, ps_view):
            vs = vstg[vcnt[0] % 2]
            vcnt[0] += 1
            P.act.copy(out=vs.full(), in_=ps_view)
            P.dma("sync" if vcnt[0] % 2 else "scalar",
                  out=vx[ktl // 4][kind * 1024:(kind + 1) * 1024, (ktl % 4) * 64:(ktl % 4 + 1) * 64].re("(h p) d -> p h d", p=128),
                  in_=vs.full().re("p (h d) -> p h d", h=8))

        wb = load_w(0, 384)
        for m, (c0, w) in enumerate(main):
            pss = []
            for ch in range(3):
                ps = nxt_ps(0, 4)
                proj(wb, ch * 128, 128, c0, 512, ps.full())
                pss.append(ps.full())
            latent_norm(pss, "g_cq")
            def mkq(h):
                def f():
                    ps = nxt_ps(0, 4)
                    for kc in range(3):
                        P.pe.matmul(out=ps[0:96, :], lhsT=uqb[:, kc, h * 96:(h + 1) * 96], rhs=latn[:, kc, :],
                                    start=(kc == 0), stop=(kc == 2))
                    return ps[0:96, :]
                return f
            run_pipe(headnorm(mkq(h), 96, C.col("g_q"), True, m * 512, qm[h, :, m * 512:(m + 1) * 512]) for h in range(8))
        wb = load_w(384, 288)
        krb = P.sb("krb", [32, 512], BF16)
        for m, (c0, w) in enumerate(main):
            pss = []
            for ch in range(2):
                ps = nxt_ps(0, 4)
                proj(wb, ch * 128, 128, c0, 512, ps.full())
                pss.append(ps.full())
            ps = nxt_ps(0, 4)
            proj(wb, 256, 32, c0, 512, ps[0:32, :])
            P.act.copy(out=krb.full(), in_=ps[0:32, :])
            latent_norm(pss, "g_ckv")
            def mkk(h):
                def f():
                    ps = nxt_ps(0, 4)
                    for kc in range(2):
                        P.pe.matmul(out=ps[0:96, :], lhsT=kpb[:, kc, h * 96:(h + 1) * 96], rhs=latn[:, kc, :],
                                    start=(kc == 0), stop=False)
                    P.pe.matmul(out=ps[0:96, :], lhsT=selm, rhs=krb.full(), start=False, stop=True)
                    return ps[0:96, :]
                return f
            run_pipe(headnorm(mkk(h), 96, C.col("g_k"), True, m * 512, kxm[m][h * 96:(h + 1) * 96, :]) for h in range(8))
            for j in range(4):
                ps = nxt_ps(0, 4)
                for kc in range(2):
                    P.pe.matmul(out=ps.full(), lhsT=latn[:, kc, j * 128:(j + 1) * 128], rhs=wvb[:, kc, :],
                                start=(kc == 0), stop=(kc == 1))
                v_out(0, m * 4 + j, ps.full())
        for (base, gname, isq) in ((672, "g_fq", True), (672 + 512, "g_fk", False)):
            wb = load_w(base, 512)
            def mkf(wb_, h, c0):
                def f():
                    ps = nxt_ps(0, 4)
                    proj(wb_, h * 64, 64, c0, 512, ps[0:64, :])
                    return ps[0:64, :]
                return f
            gl = []
            for m, (c0, w) in enumerate(main):
                for h in range(8):
                    dst = qf[h, :, m * 512:(m + 1) * 512] if isq else kxf[m][h * 64:(h + 1) * 64, :]
                    gl.append(headnorm(mkf(wb, h, c0), 64, C.col(gname), False, m * 512, dst))
            run_pipe(gl)
        wb = load_w(672 + 1024, 512)
        for m, (c0, w) in enumerate(main):
            for j in range(4):
                ps = nxt_ps(0, 4)
                proj_tm(wb, 0, 512, c0 + j * 128, ps.full())
                v_out(1, m * 4 + j, ps.full())
        cpend.extend(list(zip(kxm, kxmg)) + list(zip(vx, vxg)))
        FB = 672 + 1536
        SB = 672 + 1544
        lf_tm = P.sb("lf_tm", [128, KT_L, 8], F32)
        dt_tm = P.sb("dt_tm", [128, KT_L, 16], F32)
        a_tm = P.sb("a_tm", [128, KT_L, 16], F32)
        tmpr = P.sb("tmpr", [128, 16], F32)
        wf = load_w(FB, 8)
        for m, (c0, w) in enumerate(main):
            for j in range(4):
                kt = m * 4 + j
                ps = nxt_ps(0, 4)
                proj_tm(wf, 0, 8, c0 + j * 128, ps[:, 0:8])
                P.dve.tensor_tensor(out=tmpr[:, 0:8], in0=ps[:, 0:8], in1=rowc[:, 0:8], op=ALU.add)
                P.act.activation(out=tmpr[:, 0:8], in_=tmpr[:, 0:8], func=AF.Exp, scale=-1.0)
                P.act.activation(out=tmpr[:, 0:8], in_=tmpr[:, 0:8], func=AF.Ln, bias=one1[:, 0:1], scale=1.0)
                P.dve.tensor_scalar(out=lf_tm[:, kt, :], in0=tmpr[:, 0:8], scalar1=-1.0, scalar2=None, op0=ALU.mult)
        wd = load_w(SB + 1024 + 1536, 16)
        for m, (c0, w) in enumerate(main):
            for j in range(4):
                kt = m * 4 + j
                ps = nxt_ps(0, 4)
                proj_tm(wd, 0, 16, c0 + j * 128, ps[:, 0:16])
                P.dve.tensor_tensor(out=tmpr.full(), in0=ps[:, 0:16], in1=rowc[:, 8:24], op=ALU.add)
                P.act.activation(out=tmpr.full(), in_=tmpr.full(), func=AF.Exp)
                P.act.activation(out=dt_tm[:, kt, :], in_=tmpr.full(), func=AF.Ln, bias=one1[:, 0:1], scale=1.0)
                P.dve.tensor_tensor(out=a_tm[:, kt, :], in0=dt_tm[:, kt, :], in1=Aneg_r.full(), op=ALU.mult)
        P.dma("sync", out=dtd.full(), in_=dt_tm.full())
        P.dma("sync", out=atd.full(), in_=a_tm.full())
        fxs = P.sb("fxs", [128, 224], F32)
        within = P.sb("within", [128, KT_L, 8], F32)
        ttot = P.sb("ttot", [128, KT_L, 8], F32)
        f2 = lambda b: b.full().re("p a b -> p (a b)")
        ps = nxt_ps(0, 4)
        P.pe.matmul(out=ps[:, 0:128], lhsT=tri, rhs=f2(lf_tm), start=True, stop=True)
        P.act.copy(out=f2(within), in_=ps[:, 0:128])
        ps = nxt_ps(0, 4)
        P.pe.matmul(out=ps[:, 0:128], lhsT=ones, rhs=f2(lf_tm), start=True, stop=True)
        P.act.copy(out=f2(ttot), in_=ps[:, 0:128])
        Floc = fxs[:, 0:128].re("p (a b) -> p a b", b=8)
        totv = fxs[:, 128:160].re("p (a b) -> p a b", b=8)
        cacc = P.sb("cacc", [128, 8], F32)
        for m in range(4):
            P.dve.tensor_copy(out=Floc[:, 4 * m, :], in_=within[:, 4 * m, :])
            P.dve.tensor_copy(out=cacc.full(), in_=ttot[:, 4 * m, :])
            for j in range(1, 4):
                P.dve.tensor_tensor(out=Floc[:, 4 * m + j, :], in0=within[:, 4 * m + j, :], in1=cacc.full(), op=ALU.add)
                P.dve.tensor_tensor(out=cacc.full(), in0=cacc.full(), in1=ttot[:, 4 * m + j, :], op=ALU.add)
            P.dve.tensor_copy(out=totv[:, m, :], in_=cacc.full())
        ps = nxt_ps(0, 4)
        P.pe.transpose(out=ps[:, 0:128], in_=fxs[:, 0:128], identity=ident)
        FT = P.sb("FT", [128, 128], F32)
        r1 = P.sb("r1", [128, 128], F32)
        fh = [P.sb(f"fh{i}", [128, 128], BF16) for i in range(3)]
        P.act.copy(out=FT.full(), in_=ps[:, 0:128])
        P.dve.tensor_copy(out=fh[0].full(), in_=FT.full())
        P.dve.tensor_tensor(out=r1.full(), in0=FT.full(), in1=fh[0].full(), op=ALU.subtract)
        P.dve.tensor_copy(out=fh[1].full(), in_=r1.full())
        P.dve.tensor_tensor(out=r1.full(), in0=r1.full(), in1=fh[1].full(), op=ALU.subtract)
        P.dve.tensor_copy(out=fh[2].full(), in_=r1.full())
        nfh = [P.sb(f"nfh{i}", [128, 128], BF16) for i in range(3)]
        for r in range(3):
            P.dve.tensor_scalar(out=nfh[r].full(), in0=fh[r].full(), scalar1=-1.0, scalar2=None, op0=ALU.mult)
            for kt in range(KT_L):
                P.dma("sync" if kt % 2 else "scalar", out=fq[:, r, kt * 128:(kt + 1) * 128], in_=fh[r][kt * 8:(kt + 1) * 8, :])
                P.dma("scalar" if kt % 2 else "sync", out=nfq[:, r, kt * 128:(kt + 1) * 128], in_=nfh[r][kt * 8:(kt + 1) * 8, :])
        for m in range(4):
            P.dma("sync", out=kxf[m][512:536, :].re("(h r) c -> h r c", r=3), in_=nfq[:, :, m * 512:(m + 1) * 512])
        cpend.extend(list(zip(kxf, kxfg)))
        def plain_group(base, ncols, func, bias_name, dst, dst_row0):
            wb_ = load_w(base, ncols)
            for m, (c0, w) in enumerate(main):
                for ch in range(ncols // 128):
                    ps = nxt_ps(0, 4)
                    proj(wb_, ch * 128, 128, c0, 512, ps.full())
                    og = next_ostg()
                    if bias_name is None:
                        P.act.activation(out=og.full(), in_=ps.full(), func=func)
                    else:
                        P.act.activation(out=og.full(), in_=ps.full(), func=func,
                                         bias=C.col(bias_name, (dst_row0 // 128) + ch))
                    out_dma(dst[dst_row0 + ch * 128:dst_row0 + (ch + 1) * 128, m * 512:(m + 1) * 512], og.full())

        for blk in range(2):
            plain_group(SB + blk * 512, 512, AF.Silu, None, szd, blk * 512)
        upre = P.sb("upre", [128, 516], F32)
        carry = P.sb("carry", [128, 4], F32)
        acc0 = P.sb("acc0", [128, 512], F32)
        tstg = [P.sb(f"tstg{i}", [128, 4, 128], BF16) for i in range(2)]
        tcnt = [0]
        trq = []
        for blk in range(3):
            wb = load_w(SB + 1024 + blk * 512, 512)
            for m, (c0, w) in enumerate(main):
                for ch in range(4):
                    cg = blk * 4 + ch
                    ps = nxt_ps(0, 4)
                    proj(wb, ch * 128, 128, c0 - 4, 4, ps[:, 0:4])
                    P.act.copy(out=upre[:, 0:4], in_=ps[:, 0:4])
                    ps = nxt_ps(0, 4)
                    proj(wb, ch * 128, 128, c0, 512, ps.full())
                    while len(trq) > 1:
                        trq.pop(0)()
                    P.act.copy(out=upre[:, 4:516], in_=ps.full())
                    P.act.activation(out=acc0.full(), in_=ps.full(), func=AF.Identity, scale=C.col("cw3", cg), bias=C.col("cb", cg))
                    for k in range(3):
                        P.dve.scalar_tensor_tensor(out=acc0.full(), in0=upre[:, 1 + k:513 + k], scalar=C.col(f"cw{k}", cg),
                                                   in1=acc0.full(), op0=ALU.mult, op1=ALU.add)
                    og = next_ostg()
                    P.act.activation(out=og.full(), in_=acc0.full(), func=AF.Silu)
                    if cg >= 8:
                        out_dma(bct[(cg - 8) * 128:(cg - 7) * 128, m * 512:(m + 1) * 512], og.full())
                    if cg < 10:
                        def mk_tr(og=og, cg=cg, m=m):
                            def f():
                                i2 = tcnt[0] % 2
                                tcnt[0] += 1
                                for j in range(4):
                                    P.pe.transpose(out=pbt[:, i2 * 512 + j * 128:i2 * 512 + (j + 1) * 128],
                                                   in_=og[:, j * 128:(j + 1) * 128], identity=identb.full())
                                ts_ = tstg[i2]
                                P.dve.tensor_copy(out=ts_.full().re("p j f -> p (j f)"), in_=pbt[:, i2 * 512:(i2 + 1) * 512])
                                if cg < 8:
                                    P.dma("sync", out=xtm[:, m * 4:(m + 1) * 4, cg * 128:(cg + 1) * 128], in_=ts_.full())
                                else:
                                    P.dma("sync", out=btm[:, m * 4:(m + 1) * 4, (cg - 8) * 128:(cg - 7) * 128], in_=ts_.full())
                            return f
                        trq.append(mk_tr())
        while trq:
            trq.pop(0)()
        GB = SB + 2576
        for blk in range(6):
            plain_group(GB + blk * 512, 512, AF.Sigmoid, "b_gate", gd, blk * 512)
        P.dma("sync", out=fx[:, 0:160], in_=fxs[:, 0:160])
        gather_pairs(cpend)
        del cpend[:]

    def ssd_scan(l, K, pass1, fxs=None, dt_tm=None, a_tm=None, Hinit=None, rowc=None, pb=None):
        cmb = K["cm"]
        tri, trimask, ones = cmb[:, 0, :], cmb[:, 1, :], cmb[:, 3, :]
        if pb is None:
            pb = [P.ps(f"spb{i}", [128, 512], F32) for i in range(7)]
        if pass1:
            fxs = P.sb("decs", [128, 224], F32)
        if dt_tm is None:
            dt_tm = P.sb("dt_tm", [128, KT_L, 16], F32)
            a_tm = P.sb("a_tm", [128, KT_L, 16], F32)
            P.dma("sync", out=dt_tm.full(), in_=dtd.full())
            P.dma("sync", out=a_tm.full(), in_=atd.full())
        fl = lambda b: b.full().re("p c h -> p (c h)")
        Acum = P.sb("Acum", [128, KT_L, 16], F32)
        Atot = P.sb("Atot", [128, KT_L, 16], F32)
        wdec = P.sb("wdec", [128, KT_L, 16], F32)
        eAtot = P.sb("eAtot", [128, KT_L, 16], F32)
        psA = pb[0]
        P.pe.matmul(out=psA[:, 0:256], lhsT=tri, rhs=fl(a_tm), start=True, stop=True)
        P.act.copy(out=fl(Acum), in_=psA[:, 0:256])
        P.pe.matmul(out=psA[:, 256:512], lhsT=ones, rhs=fl(a_tm), start=True, stop=True)
        P.act.copy(out=fl(Atot), in_=psA[:, 256:512])
        P.act.activation(out=fl(eAtot), in_=fl(Atot), func=AF.Exp)
        P.dve.tensor_tensor(out=fl(wdec), in0=fl(Atot), in1=fl(Acum), op=ALU.subtract)
        P.act.activation(out=fl(wdec), in_=fl(wdec), func=AF.Exp)
        if not pass1:
            nAcum = P.sb("nAcum", [128, KT_L, 16], F32)
            eA = P.sb("eA", [128, KT_L, 16], F32)
            P.dve.tensor_scalar(out=fl(nAcum), in0=fl(Acum), scalar1=-1.0, scalar2=None, op0=ALU.mult)
            P.act.activation(out=fl(eA), in_=fl(Acum), func=AF.Exp)
            BCs = P.sb("BCs", [128, 4, T], BF16)
            P.dma("gpsimd", out=BCs.full(), in_=bct.full().re("(a p) t -> p a t", p=128))
            cb = P.sb("cb", [128, 2, 128], F32)
            NH = 4
            at = [P.sb(f"at{i}", [128, 128], F32) for i in range(NH)]
            tm = [P.sb(f"tm{i}", [128, 128], F32) for i in range(NH)]
            dec = [P.sb(f"dec{i}", [128, 128], F32) for i in range(NH)]
            MT = [P.sb(f"MT{i}", [128, 128], BF16) for i in range(NH)]
            t1 = P.sb("t1", [128, 1024], F32)
            t3 = P.sb("t3", [128, 1024], F32)
            yo = P.sb("yo", [128, 1024], BF16)
            yT = [P.sb(f"yT{i}", [128, 4, 128], F32) for i in range(2)]
        Hs = P.sb("Hs", [128, 1024], F32)
        Hb = P.sb("Hb", [128, 1024], BF16)
        xc = [P.sb(f"xc{i}", [128, 1024], BF16) for i in range(2)]
        Bc = [P.sb(f"Bc{i}", [128, 256], BF16) for i in range(2)]
        xdt = P.sb("xdt", [128, 1024], BF16)
        xdts = P.sb("xdts", [128, 1024], BF16)
        dsum = P.sb("dsum", [128, 16], F32)
        v3 = lambda v: v.re("p (h d) -> p h d", h=16)
        bc3 = lambda v: v.f(lambda a: a.unsqueeze(2).to_broadcast([128, 16, 64]))
        for m in range(4):
            if pass1:
                P.dve.memset(ap=Hs.full(), constant=0.0)
                P.dve.memset(ap=dsum.full(), constant=0.0)
            else:
                P.dve.tensor_copy(out=Hs.full(), in_=Hinit[:, m, :])
                P.act.copy(out=Hb.full(), in_=Hinit[:, m, :])
            for j in range(4):
                c = m * 4 + j
                x_c = xc[c % 2]
                B_c = Bc[c % 2]
                P.dma("sync", out=x_c.full(), in_=xtm[:, c, :])
                P.dma("scalar" if pass1 else "gpsimd", out=B_c.full(), in_=btm[:, c, :])
                P.dve.tensor_tensor(out=v3(xdt.full()), in0=v3(x_c.full()), in1=bc3(dt_tm[:, c, :]), op=ALU.mult)
                (P.dve if pass1 else P.pool).tensor_tensor(out=v3(xdts.full()), in0=v3(xdt.full()), in1=bc3(wdec[:, c, :]), op=ALU.mult)
                if not pass1:
                    cs = slice(c * 128, (c + 1) * 128)
                    ps_cb = pb[1]
                    for g in range(2):
                        P.pe.matmul(out=ps_cb[:, g * 128:(g + 1) * 128], lhsT=BCs[:, g, cs], rhs=BCs[:, 2 + g, cs],
                                    start=True, stop=True)
                    P.act.copy(out=cb.full().re("p a b -> p (a b)"), in_=ps_cb[:, 0:256])
                    ps_off = [pb[2], pb[3]]
                    for g in range(2):
                        P.pe.matmul(out=ps_off[g].full(), lhsT=BCs[:, 2 + g, cs], rhs=Hb[:, g * 512:(g + 1) * 512],
                                    start=True, stop=True)
                    ps_y = [pb[4], pb[5]]
                    def st1(h):
                        i2 = h % NH
                        g = h // 8
                        P.dve.tensor_scalar(out=at[i2].full(), in0=tri, scalar1=a_tm[:, c, h:h + 1], scalar2=None, op0=ALU.mult)
                        ps_A = pb[6]
                        P.pe.matmul(out=ps_A[:, i2 * 128:(i2 + 1) * 128], lhsT=ones, rhs=at[i2].full(), start=True, stop=True)
                        P.dve.tensor_tensor(out=tm[i2].full(), in0=ps_A[:, i2 * 128:(i2 + 1) * 128], in1=trimask, op=ALU.add)
                        P.act.activation(out=dec[i2].full(), in_=tm[i2].full(), func=AF.Exp, bias=nAcum[:, c, h:h + 1], scale=1.0)
                        P.pool.tensor_tensor(out=MT[i2].full(), in0=cb[:, g, :], in1=dec[i2].full(), op=ALU.mult)

                    def st2(h):
                        i2 = h % NH
                        g = h // 8
                        hh = h % 8
                        P.pe.matmul(out=ps_y[g][:, hh * 64:(hh + 1) * 64], lhsT=MT[i2].full(), rhs=xdt[:, h * 64:(h + 1) * 64],
                                    start=True, stop=True)

                    for hq in range(16 + 3):
                        if hq < 16:
                            st1(hq)
                        if hq >= 3:
                            st2(hq - 3)
                    for g in range(2):
                        gs_ = slice(g * 512, (g + 1) * 512)
                        v8 = lambda v: v.re("p (h d) -> p h d", h=8)
                        b8 = lambda v: v.f(lambda a: a.unsqueeze(2).to_broadcast([128, 8, 64]))
                        P.dve.tensor_tensor(out=v8(t1[:, gs_]), in0=v8(ps_off[g].full()), in1=b8(eA[:, c, g * 8:(g + 1) * 8]), op=ALU.mult)
                        P.dve.tensor_tensor(out=t1[:, gs_], in0=t1[:, gs_], in1=ps_y[g].full(), op=ALU.add)
                    P.pool.tensor_tensor(out=v3(t3.full()), in0=v3(x_c.full()), in1=bc3(rowc[:, 40:56]), op=ALU.mult)
                    P.pool.tensor_tensor(out=t3.full(), in0=t1.full(), in1=t3.full(), op=ALU.add)
                    for q4 in range(2):
                        pst = pb[2 + q4]
                        for jj in range(4):
                            fc = q4 * 4 + jj
                            P.pe.transpose(out=pst[:, jj * 128:(jj + 1) * 128], in_=t3[:, fc * 128:(fc + 1) * 128],
                                           identity=cmb[:, 2, :])
                        yt = yT[q4]
                        P.act.copy(out=yt.full().re("p a b -> p (a b)"), in_=pst.full())
                        P.dma("sync", out=yd[q4 * 512:(q4 + 1) * 512, c * 128:(c + 1) * 128].re("(a p) t -> p a t", p=128),
                              in_=yt.full())
                ps_h = [pb[0], pb[1]] if pass1 else [pb[4], pb[5]]
                for g in range(2):
                    P.pe.matmul(out=ps_h[g].full(), lhsT=B_c[:, g * 128:(g + 1) * 128], rhs=xdts[:, g * 512:(g + 1) * 512],
                                start=True, stop=True)
                P.dve.tensor_tensor(out=v3(Hs.full()), in0=v3(Hs.full()), in1=bc3(eAtot[:, c, :]), op=ALU.mult)
                for g in range(2):
                    P.dve.tensor_tensor(out=Hs[:, g * 512:(g + 1) * 512], in0=Hs[:, g * 512:(g + 1) * 512], in1=ps_h[g].full(), op=ALU.add)
                if pass1:
                    P.dve.tensor_tensor(out=dsum.full(), in0=dsum.full(), in1=Atot[:, c, :], op=ALU.add)
                else:
                    P.act.copy(out=Hb.full(), in_=Hs.full())
            if pass1:
                P.dma("sync", out=sx[m // 2][(m % 2) * 128:(m % 2 + 1) * 128, :], in_=Hs.full())
                P.act.activation(out=fxs[:, 160 + m * 16:160 + (m + 1) * 16], in_=dsum.full(), func=AF.Exp)
        if pass1:
            P.dma("sync", out=fx[:, 160:224], in_=fxs[:, 160:224])

    def load_fg():
        fg = P.sb("fg", [128, 4, 224], F32)
        P.dma("sync", out=fg.full(), in_=fxg.full().re("(r p) c -> p r c", p=128))
        return fg

    def phase_attn(l):
        K = load_consts()
        sel, zero = K["sel"], K["zero"]
        mskb = P.sb("mskb", [128, 8, 512], F32)
        P.dma("gpsimd", out=mskb.full(), in_=msk_d.full())
        fg = load_fg()
        offs = P.sb("offs", [128, 16, 8], F32)
        run = P.sb("run", [128, 8], F32)
        P.dve.memset(ap=run.full(), constant=0.0)
        for s_ in range(16):
            m, r = divmod(s_, 4)
            P.dve.tensor_copy(out=offs[:, s_, :], in_=run.full())
            P.dve.tensor_tensor(out=run.full(), in0=run.full(), in1=fg[:, r, 128 + m * 8:128 + (m + 1) * 8], op=ALU.add)
        offown = P.sb("offown", [128, 4, 8], F32)
        P.dve.memset(ap=offown.full(), constant=0.0)
        for m in range(4):
            for r in range(4):
                P.dve.scalar_tensor_tensor(out=offown[:, m, :], in0=offs[:, 4 * m + r, :], scalar=sel[:, 8 + 4 * m + r:9 + 4 * m + r],
                                           in1=offown[:, m, :], op0=ALU.mult, op1=ALU.add)
        btab = P.sb("btab", [128, 4, 16, 8], F32)
        for m in range(4):
            P.dve.tensor_tensor(out=btab[:, m, :, :], in0=offown[:, m, :].f(lambda a: a.unsqueeze(1).to_broadcast([128, 16, 8])),
                                in1=offs.full(), op=ALU.subtract)
            for jr in range(4):
                s_ = 4 * m + jr
                P.dve.tensor_scalar(out=btab[:, m, s_, :], in0=btab[:, m, s_, :], scalar1=sel[:, 28 + jr:29 + jr], scalar2=None,
                                    op0=ALU.add)

        pss = [P.ps(f"pss{i}", [128, 1024], F32) for i in range(3)]
        pbo = [P.ps(f"pbo{i}", [128, 512], F32) for i in range(2)]
        K_sb = [P.sb(f"K_sb{i}", [96, S_], BF16) for i in range(2)]
        Q_sb = [P.sb(f"Q_sb{i}", [96, T], BF16) for i in range(2)]
        V_sb = [P.sb(f"V_sb{i}", [128, NKT, 128], BF16) for i in range(2)]
        for i in range(2):
            P.dve.memset(ap=V_sb[i][:, :, 64:128], constant=1.0)
        NSB = 3
        LA = 2
        pt = [P.sb(f"pt{i}", [128, 1024], BF16) for i in range(NSB)]
        mt = [P.sb(f"mt{i}", [128, 1024], F32) for i in range(2)]
        rl = P.sb("rl", [128, 512], F32)
        rl2 = P.sb("rl2", [64, 512], F32)
        ot = [P.sb(f"ot{i}", [64, 512], BF16) for i in range(2)]
        cnt = [0, 0, 0]
        heads = [(0, h) for h in range(8)] + [(1, h) for h in range(8)]

        def loads(idx):
            kind, h = heads[idx]
            i = idx % 2
            nd = 96 if kind == 0 else 64
            if kind == 1:
                P.dve.memset(ap=K_sb[i][64:96, :], constant=0.0)
                P.dve.memset(ap=K_sb[i][64:67, :], constant=8.0)
                P.pool.memset(ap=Q_sb[i][64:96, :], constant=0.0)
                P.pool.memset(ap=Q_sb[i][64:70, :], constant=8.0)
            for r in range(4):
                vr = r * 2048 + kind * 1024 + h * 128
                for m in range(4):
                    s0 = (4 * m + r) * 512
                    if kind == 0:
                        ksrc = kxmg[m][r * 768 + h * 96:r * 768 + (h + 1) * 96, :]
                    else:
                        ksrc = kxfg[m][r * 640 + h * 64:r * 640 + (h + 1) * 64, :]
                        P.dma("gpsimd" if (r + m) % 2 == 0 else "sync", out=K_sb[i][67:70, s0:s0 + 512].k(("Kr", s0)),
                              in_=kxfg[m][r * 640 + 512 + h * 3:r * 640 + 512 + (h + 1) * 3, :])
                    P.dma("sync" if (r + m) % 2 == 0 else "gpsimd", out=K_sb[i][0:nd, s0:s0 + 512].k(("K", s0)), in_=ksrc)
                    g0 = (4 * m + r) * 4
                    P.dma("gpsimd" if (r + m) % 2 == 0 else "sync",
                          out=V_sb[i][:, g0:g0 + 4, 0:64].k(("V", g0)),
                          in_=vxg[m][vr:vr + 128, :].re("p (j d) -> p j d", j=4))
            if kind == 0:
                P.dma("sync", out=Q_sb[i][0:96, :].k("Q0"), in_=qm[h])
            else:
                P.dma("sync", out=Q_sb[i][0:64, :].k("Q0"), in_=qf[h])
                P.dma("gpsimd", out=Q_sb[i][64:67, :].k("Q1"), in_=fq[h])

        iters = []
        for idx in range(16):
            for m in range(4):
                nkp = (4 * m + 4) * 2
                for kp in range(nkp):
                    iters.append((idx, m, kp, nkp))

        def stage_qk(n):
            idx, m, kp, nkp = iters[n]
            kind, h = heads[idx]
            i = idx % 2
            scale = 96.0 ** -0.5 if kind == 0 else 0.125
            i3 = n % NSB
            ps = pss[i3]
            for e in range(2):
                kt = 2 * kp + e
                P.pe.matmul(out=ps[:, e * 512:(e + 1) * 512], lhsT=K_sb[i][0:96, kt * 128:(kt + 1) * 128],
                            rhs=Q_sb[i][0:96, m * 512:(m + 1) * 512], start=True, stop=True)
            blk = (2 * kp) // 4
            if blk >= 4 * m:
                jr = blk - 4 * m
                k4 = (2 * kp) % 4
                mm = mt[cnt[1] % 2]
                cnt[1] += 1
                P.dve.scalar_tensor_tensor(out=mm.full(), in0=mskb[:, kind * 4 + k4:kind * 4 + k4 + 2, :].re("p a b -> p (a b)"),
                                           scalar=sel[:, 24 + jr:25 + jr], in1=ps.full(), op0=ALU.mult, op1=ALU.add)
                src = mm.full()
                bias = sel[:, 28 + jr:29 + jr] if kind == 0 else btab[:, m, blk, h:h + 1]
            else:
                src = ps.full()
                bias = zero[:, 0:1] if kind == 0 else btab[:, m, blk, h:h + 1]
            P.act.activation(out=pt[i3].full(), in_=src, func=AF.Exp, scale=scale, bias=bias)

        def stage_pv(n):
            idx, m, kp, nkp = iters[n]
            kind, h = heads[idx]
            i = idx % 2
            oacc = pbo[(idx * 4 + m) % 2]
            for e in range(2):
                kt = 2 * kp + e
                P.pe.matmul(out=oacc.full(), lhsT=V_sb[i][:, kt, :], rhs=pt[n % NSB][:, e * 512:(e + 1) * 512],
                            start=(kp == 0 and e == 0), stop=(kp == nkp - 1 and e == 1))
            if kp == nkp - 1:
                odst = omd if kind == 0 else ofd
                P.dve.reciprocal(out=rl[64:128, :], in_=oacc[64:128, :])
                P.dve.tensor_copy(out=rl2.full(), in_=rl[64:128, :])
                o = ot[m % 2]
                P.dve.tensor_tensor(out=o.full(), in0=oacc[0:64, :], in1=rl2.full(), op=ALU.mult)
                P.dma("sync", out=odst[h * 64:(h + 1) * 64, m * 512:(m + 1) * 512], in_=o.full())

        pcs = [P.sb(f"pcs{i}", [128, 2048], F32) for i in range(2)]
        pcb = [P.sb(f"pcb{i}", [128, 2048], BF16) for i in range(2)]
        jobs = []
        for (src, dst, rows, cols) in ((w_a[l], wab, 512, D), (w_b[l], wbb, 512, D), (w_c[l], wcb, 1024, D), (w_o[l], wob, 1024, D),
                                       (w_up[l], wub, D, 5632), (w_dn[l], wdb, 2816, D)):
            for r0 in range(0, rows, 128):
                for c0 in range(0, cols, 2048):
                    n_ = min(2048, cols - c0)
                    jobs.append((src[r0:r0 + 128, c0:c0 + n_], dst[r0:r0 + 128, c0:c0 + n_], n_))
        jcnt = [0]

        def precast_one():
            if jcnt[0] >= len(jobs):
                return
            src, dst, n_ = jobs[jcnt[0]]
            i = jcnt[0] % 2
            jcnt[0] += 1
            P.dma("gpsimd", out=pcs[i][:, 0:n_], in_=src)
            P.pool.tensor_copy(out=pcb[i][:, 0:n_], in_=pcs[i][:, 0:n_])
            P.dma("gpsimd", out=dst, in_=pcb[i][:, 0:n_])

        every = max(1, len(iters) // (len(jobs) + 4))
        loads(0)
        loads(1)
        for n in range(len(iters) + LA):
            if n < len(iters):
                stage_qk(n)
            if n >= LA:
                stage_pv(n - LA)
                idx_p, m_p, kt_p, nk_p = iters[n - LA]
                if m_p == 3 and kt_p == nk_p - 1 and idx_p + 2 < 16:
                    loads(idx_p + 2)
            if n % every == every - 1:
                precast_one()
        while jcnt[0] < len(jobs):
            precast_one()

    def phase_ssd2(l):
        K = load_consts()
        sel = K["sel"]
        rowc = P.sb("rowc", [128, 56], F32)
        P.dma("sync", out=rowc.full(), in_=rowc_d[l])
        fg = load_fg()
        Hin = P.sb("Hin", [128, 1024], F32)
        Hsel = P.sb("Hsel", [128, 4, 1024], F32)
        Sst = [P.sb(f"Sst{i}", [128, 1024], F32) for i in range(2)]
        P.dve.memset(ap=Hin.full(), constant=0.0)
        P.dve.memset(ap=Hsel.full(), constant=0.0)
        v3 = lambda v: v.re("p (h d) -> p h d", h=16)
        for s_ in range(16):
            m, r = divmod(s_, 4)
            P.dve.scalar_tensor_tensor(out=Hsel[:, m, :], in0=Hin.full(), scalar=sel[:, 8 + s_:9 + s_], in1=Hsel[:, m, :],
                                       op0=ALU.mult, op1=ALU.add)
            if s_ < 15:
                st_ = Sst[s_ % 2]
                P.dma("sync" if s_ % 2 else "gpsimd", out=st_.full(),
                      in_=sxg[m // 2][r * 256 + (m % 2) * 128:r * 256 + (m % 2 + 1) * 128, :])
                dcs = fg[:, r, 160 + m * 16:160 + (m + 1) * 16]
                P.dve.tensor_tensor(out=v3(Hin.full()), in0=v3(Hin.full()),
                                    in1=dcs.f(lambda a: a.unsqueeze(2).to_broadcast([128, 16, 64])), op=ALU.mult)
                P.pool.tensor_tensor(out=Hin.full(), in0=Hin.full(), in1=st_.full(), op=ALU.add)
        ssd_scan(l, K, pass1=False, Hinit=Hsel, rowc=rowc)

    def write_tails(txs):
        P.dma("sync", out=tx.full(), in_=txs.full().re("p m k c -> p (m k c)"))

    def halo_exchange(dst):
        K = load_consts()
        sel = K["sel"]
        P.pool.collective_compute(kind="AllGather", op=ALU.bypass, replica_groups=RG, ins=[tx.full()], outs=[txg.full()])
        tg = P.sb("tg", [128, 4, 128], F32)
        P.dma("sync", out=tg.full(), in_=txg.full().re("(r p) c -> p r c", p=128))
        hl = P.sb("hl", [128, 4, 32], F32)
        P.dve.memset(ap=hl.full(), constant=0.0)
        for m in range(4):
            for r in range(4):
                P.dve.scalar_tensor_tensor(out=hl[:, m, :], in0=tg[:, r, m * 32:(m + 1) * 32], scalar=sel[:, r:r + 1],
                                           in1=hl[:, m, :], op0=ALU.mult, op1=ALU.add)
            if m >= 1:
                P.dve.scalar_tensor_tensor(out=hl[:, m, :], in0=tg[:, 3, (m - 1) * 32:m * 32], scalar=sel[:, 4:5],
                                           in1=hl[:, m, :], op0=ALU.mult, op1=ALU.add)
        dv = dst.full().re("(kc p) n -> p kc n", p=128)
        for m in range(4):
            P.dma("sync", out=dv[:, :, m * SW:m * SW + 4], in_=hl[:, m, :].re("p (k c) -> p k c", c=4))

    def phase_merge(l):
        K = load_consts()
        ones, eps = K["cm"][:, 3, :], K["eps"]
        gsb = P.sb("gsb", [128, 8], F32)
        P.dma("sync", out=gsb.full(), in_=gssm_d[l])
        pb = [P.ps(f"pb{i}", [128, 512], F32) for i in range(8)]
        def load_wb(dram_bf, kc_n, name, q):
            bfb = P.sb(name, [128, kc_n, D], BF16)
            P.dma(q, out=bfb.full(), in_=dram_bf.full().re("(kc p) n -> p kc n", p=128))
            return bfb

        Wa = load_wb(wab, 4, "Wa", "sync")
        Wb = load_wb(wbb, 4, "Wb", "gpsimd")
        Wc = load_wb(wcb, 8, "Wc", "sync")
        Wo = load_wb(wob, 8, "Wo", "gpsimd")
        ys = P.sb("ys", [128, 8, 512], F32)
        szs = P.sb("szs", [128, 8, 512], BF16)
        yn = P.sb("yn", [128, 8, 512], BF16)
        omsL = [P.sb(f"oms{i}", [128, 4, 512], BF16) for i in range(2)]
        ofsL = [P.sb(f"ofs{i}", [128, 4, 512], BF16) for i in range(2)]
        gsL = [P.sb(f"gs{i}", [128, 24, 512], BF16) for i in range(2)]
        xs = P.sb("xs", [128, 8, 512], F32)
        sq = P.sb("sq", [128, 512], F32)
        rstd = P.sb("rstd", [128, 512], F32)
        m1 = [P.sb(f"m1_{i}", [128, 512], F32) for i in range(2)]
        m2 = [P.sb(f"m2_{i}", [128, 512], F32) for i in range(2)]
        m3 = [P.sb(f"m3_{i}", [128, 512], F32) for i in range(2)]
        mg = P.sb("mg", [128, 8, 512], BF16)
        xo = [P.sb(f"xo{i}", [128, 512], F32) for i in range(2)]
        txs = P.sb("txs", [128, 4, 8, 4], F32)
        ch = lambda d: d.full().re("(kc p) n -> p kc n", p=128)
        xmv = ch(xmid)
        def ld_norm(ti):
            ts = slice(ti * 512, (ti + 1) * 512)
            P.dma("sync", out=ys.full(), in_=ch(yd)[:, :, ts])
            P.dma("gpsimd", out=szs.full(), in_=ch(szd)[:, :, ts])

        def ld_rest(ti):
            ts = slice(ti * 512, (ti + 1) * 512)
            P.dma("sync", out=omsL[ti % 2].full(), in_=ch(omd)[:, :, ts])
            P.dma("gpsimd", out=ofsL[ti % 2].full(), in_=ch(ofd)[:, :, ts])
            P.dma("sync", out=gsL[ti % 2].full(), in_=ch(gd)[:, :, ts])

        ld_norm(0)
        ld_rest(0)
        for ti in range(4):
            ts = slice(ti * 512, (ti + 1) * 512)
            xsl = slice(ti * SW + 4, ti * SW + 516)
            oms, ofs, gs = omsL[ti % 2], ofsL[ti % 2], gsL[ti % 2]
            P.dma("gpsimd", out=xs.full(), in_=ch(xb[l])[:, :, xsl])
            ps = pb[7]
            for kc in range(8):
                P.dve.tensor_tensor(out=ys[:, kc, :], in0=ys[:, kc, :], in1=szs[:, kc, :], op=ALU.mult)
                P.act.activation(out=sq.full(), in_=ys[:, kc, :], func=AF.Square)
                P.pe.matmul(out=ps.full(), lhsT=ones, rhs=sq.full(), start=(kc == 0), stop=(kc == 7))
            P.act.activation(out=rstd.full(), in_=ps.full(), func=AF.Ln, bias=eps[:, 0:1], scale=1.0 / 1024.0)
            P.act.activation(out=rstd.full(), in_=rstd.full(), func=AF.Exp, scale=-0.5)
            for kc in range(8):
                P.dve.scalar_tensor_tensor(out=yn[:, kc, :], in0=ys[:, kc, :], scalar=gsb[:, kc:kc + 1], in1=rstd.full(),
                                           op0=ALU.mult, op1=ALU.mult)
            if ti + 1 < 4:
                ld_norm(ti + 1)
                ld_rest(ti + 1)
            for oc in range(8):
                i2 = oc % 2
                osl = slice(oc * 128, (oc + 1) * 128)
                pa, pbb, pc = pb[0 + i2 * 3], pb[1 + i2 * 3], pb[2 + i2 * 3]
                for kc in range(4):
                    P.pe.matmul(out=pa.full(), lhsT=Wa[:, kc, osl], rhs=oms[:, kc, :], start=(kc == 0), stop=(kc == 3))
                for kc in range(4):
                    P.pe.matmul(out=pbb.full(), lhsT=Wb[:, kc, osl], rhs=ofs[:, kc, :], start=(kc == 0), stop=(kc == 3))
                for kc in range(8):
                    P.pe.matmul(out=pc.full(), lhsT=Wc[:, kc, osl], rhs=yn[:, kc, :], start=(kc == 0), stop=(kc == 7))
                P.dve.tensor_tensor(out=m1[i2].full(), in0=pa.full(), in1=gs[:, oc, :], op=ALU.mult)
                P.dve.tensor_tensor(out=m2[i2].full(), in0=pbb.full(), in1=gs[:, 8 + oc, :], op=ALU.mult)
                P.dve.tensor_tensor(out=m3[i2].full(), in0=pc.full(), in1=gs[:, 16 + oc, :], op=ALU.mult)
                P.pool.tensor_tensor(out=m1[i2].full(), in0=m1[i2].full(), in1=m2[i2].full(), op=ALU.add)
                P.pool.tensor_tensor(out=mg[:, oc, :], in0=m1[i2].full(), in1=m3[i2].full(), op=ALU.add)
            for oc in range(8):
                i2 = oc % 2
                ps = pb[6 + i2]
                for kc in range(8):
                    P.pe.matmul(out=ps.full(), lhsT=Wo[:, kc, oc * 128:(oc + 1) * 128], rhs=mg[:, kc, :],
                                start=(kc == 0), stop=(kc == 7))
                P.dve.tensor_tensor(out=xo[i2].full(), in0=ps.full(), in1=xs[:, oc, :], op=ALU.add)
                P.pool.tensor_copy(out=txs[:, ti, oc, :], in_=xo[i2][:, 508:512])
                P.dma("sync" if i2 else "gpsimd", out=xmv[:, oc, xsl], in_=xo[i2].full())
        write_tails(txs)

    def phase_ffn(l, last):
        K = load_consts()
        ones, eps = K["cm"][:, 3, :], K["eps"]
        cstb = P.sb("cstb", [128, offD2["_n"]], F32)
        C = Cst(P, cstb, offD2)
        P.dma("sync", out=cstb.full(), in_=cstD_d[l])
        pb = [P.ps(f"pb{i}", [128, 512], F32) for i in range(8)]
        Wu = P.sb("Wu", [128, 8, 5632], BF16)
        Wd = P.sb("Wd", [128, 22, D], BF16)
        wubv = wub.full().re("(kc p) n -> p kc n", p=128)
        for (c0, c1) in ((0, 512), (2816, 3328), (512, 2816), (3328, 5632)):
            P.dma("sync" if c0 < 2816 else "gpsimd", out=Wu[:, :, c0:c1], in_=wubv[:, :, c0:c1])
        wdbv = wdb.full().re("(kc p) n -> p kc n", p=128)
        P.dma("sync", out=Wd[:, 0:11, :], in_=wdbv[:, 0:11, :])
        P.dma("gpsimd", out=Wd[:, 11:22, :], in_=wdbv[:, 11:22, :])
        xst = P.sb("xst", [128, 8, 512], F32)
        hn = P.sb("hn", [128, 8, 512], BF16)
        sq = P.sb("sq", [128, 512], F32)
        rstd = P.sb("rstd", [128, 512], F32)
        act = P.sb("act", [128, 22, 512], BF16)
        upre = [P.sb(f"upre{i}", [128, 516], F32) for i in range(2)]
        acc = [P.sb(f"acc{i}", [128, 512], F32) for i in range(2)]
        sg = P.sb("sg", [128, 512], F32)
        carry = P.sb("carry", [128, 44, 4], F32)
        xo = [P.sb(f"xo{i}", [128, 512], F32) for i in range(2)]
        txs = P.sb("txs", [128, 4, 8, 4], F32)
        xTv = xmid.full().re("(kc p) n -> p kc n", p=128)
        dst = out if last else xb[l + 1]
        dv = dst.full().re("(kc p) n -> p kc n", p=128)
        tiles = []
        for m in range(4):
            tiles.append((m * SW, 4, True, m))
            tiles.append((m * SW + 4, 512, False, m))
        pcnt = [0]
        for (c0, w, is_halo, m) in tiles:
            P.dma("sync", out=xst[:, :, 0:w], in_=xTv[:, :, c0:c0 + w])
            ps = pb[7]
            for kc in range(8):
                P.act.activation(out=sq[:, 0:w], in_=xst[:, kc, 0:w], func=AF.Square)
                P.pe.matmul(out=ps[:, 0:w], lhsT=ones, rhs=sq[:, 0:w], start=(kc == 0), stop=(kc == 7))
            P.act.activation(out=rstd[:, 0:w], in_=ps[:, 0:w], func=AF.Ln, bias=eps[:, 0:1], scale=1.0 / 1024.0)
            P.act.activation(out=rstd[:, 0:w], in_=rstd[:, 0:w], func=AF.Exp, scale=-0.5)
            for kc in range(8):
                P.dve.scalar_tensor_tensor(out=hn[:, kc, 0:w], in0=xst[:, kc, 0:w], scalar=C.col("g_ffn", kc),
                                           in1=rstd[:, 0:w], op0=ALU.mult, op1=ALU.mult)
            for i in range(22):
                accs = []
                for j, cg in enumerate((i, 22 + i)):
                    ps = pb[pcnt[0] % 4]
                    pcnt[0] += 1
                    for kc in range(8):
                        P.pe.matmul(out=ps[:, 0:w], lhsT=Wu[:, kc, cg * 128:(cg + 1) * 128], rhs=hn[:, kc, 0:w],
                                    start=(kc == 0), stop=(kc == 7))
                    if is_halo:
                        P.act.copy(out=carry[:, cg, :], in_=ps[:, 0:4])
                        continue
                    up = upre[j]
                    a0 = acc[j]
                    P.act.copy(out=up[:, 4:516], in_=ps.full())
                    P.act.activation(out=a0.full(), in_=ps.full(), func=AF.Identity, scale=C.col("fw2", cg), bias=C.col("fb", cg))
                    P.pool.tensor_copy(out=up[:, 0:4], in_=carry[:, cg, :])
                    P.dve.scalar_tensor_tensor(out=a0.full(), in0=up[:, 3:515], scalar=C.col("fw1", cg), in1=a0.full(),
                                               op0=ALU.mult, op1=ALU.add)
                    P.dve.scalar_tensor_tensor(out=a0.full(), in0=up[:, 2:514], scalar=C.col("fw0", cg), in1=a0.full(),
                                               op0=ALU.mult, op1=ALU.add)
                    accs.append(a0)
                if is_halo:
                    continue
                P.act.activation(out=sg.full(), in_=accs[0].full(), func=AF.Silu)
                P.pool.tensor_tensor(out=act[:, i, :], in0=sg.full(), in1=accs[1].full(), op=ALU.mult)
            if is_halo:
                continue
            for oc in range(8):
                i2 = oc % 2
                ps = pb[4 + i2]
                for i in range(22):
                    P.pe.matmul(out=ps.full(), lhsT=Wd[:, i, oc * 128:(oc + 1) * 128], rhs=act[:, i, :],
                                start=(i == 0), stop=(i == 21))
                P.dve.tensor_tensor(out=xo[i2].full(), in0=ps.full(), in1=xst[:, oc, :], op=ALU.add)
                if last:
                    P.dma("sync" if i2 else "gpsimd", out=dv[:, oc, m * 512:(m + 1) * 512], in_=xo[i2].full())
                else:
                    P.pool.tensor_copy(out=txs[:, m, oc, :], in_=xo[i2][:, 508:512])
                    P.dma("sync" if i2 else "gpsimd", out=dv[:, oc, m * SW + 4:m * SW + 516], in_=xo[i2].full())
        if not last:
            write_tails(txs)

    def gather_e1():
        gather_pairs(list(zip(sx, sxg)) + [(fx, fxg)])

    nl = L if stop is None else stop[0]
    done = False
    for l in range(nl):
        last_l = (stop is not None and l == nl - 1)
        phase_A(l)
        P.emit(final=False)
        ssd_scan(l, load_consts(), pass1=True)
        P.emit(final=False)
        if last_l and stop[1] == "A":
            break
        gather_e1()
        phase_attn(l)
        P.emit(final=False)
        phase_ssd2(l)
        P.emit(final=False)
        if last_l and stop[1] == "B":
            break
        phase_merge(l)
        P.emit(final=False)
        halo_exchange(xmid)
        P.emit(final=False)
        if last_l and stop[1] == "C":
            break
        phase_ffn(l, last=(l == L - 1))
        P.emit(final=False)
        if l < L - 1:
            halo_exchange(xb[l + 1])
            P.emit(final=False)
    loc = {"kxmg0": kxmg[0], "vxg0": vxg[0], "sxg0": sxg[0], "fxg": fxg, "qm": qm, "qf": qf, "fq": fq, "omd": omd, "ofd": ofd, "yd": yd,
           "xmid": xmid, "xb1": xb[1], "szd": szd, "gd": gd, "xtm": xtm, "btm": btm, "bct": bct, "dtd": dtd, "atd": atd}
    for name in dbg:
        src = loc[name]
        shp = list(src.h.shape) if hasattr(src.h, "shape") else None
        dd = P.dram("dbg_" + name, shp, src.h.dtype, EO)
        P.dma("sync", out=dd.full(), in_=src.full())
    P.emit(final=True)
    return nc, P


def _stripe_tokens(p):
    return np.concatenate([np.arange((4 * m + p) * 512, (4 * m + p + 1) * 512) for m in range(4)])


def fused_in_maps(inp):
    L = 2
    cpsA = [a_colpack(inp, l) for l in range(L)]
    cpsD = [d2_colpack(inp, l) for l in range(L)]
    offA = dict(cpsA[0].off)
    offA["_n"] = cpsA[0].n
    offD = dict(cpsD[0].off)
    offD["_n"] = cpsD[0].n
    cstA = np.stack([c.array() for c in cpsA])
    cstD = np.stack([c.array() for c in cpsD])
    w_kp = np.zeros((L, 256, 8, 96), np.float32)
    wukv = inp["mla_w_ukv"].reshape(L, 256, 8, 128)
    w_kp[:, :, :, 0:64] = wukv[:, :, :, 0:64]
    w_v = np.ascontiguousarray(wukv[:, :, :, 64:128].reshape(L, 256, 512))
    gssm = np.ascontiguousarray(inp["ssm_norm_g"].reshape(L, 8, 128).transpose(0, 2, 1))
    rowc = np.stack([fused_rowpack(inp, l) for l in range(L)])
    msk, cm = _bc_consts()
    mats = _const_mats()
    shared = {
        "w_in": np.ascontiguousarray(inp["w_in"]), "w_uq": np.ascontiguousarray(inp["mla_w_uq"]),
        "w_kp": np.ascontiguousarray(w_kp.reshape(L, 256, 768)), "w_v": w_v,
        "w_a": np.ascontiguousarray(inp["w_br_mla"]), "w_b": np.ascontiguousarray(inp["w_br_fox"]),
        "w_c": np.ascontiguousarray(inp["w_br_ssm"]), "w_o": np.ascontiguousarray(inp["w_out"]),
        "w_up": np.ascontiguousarray(inp["ffn_w_up"]), "w_dn": np.ascontiguousarray(inp["ffn_w_down"]),
        "cstA": cstA, "cstD": cstD, "gssm": gssm, "rowc": rowc, "msk": msk, "cm": cm, "mats": mats,
    }
    in_maps = []
    for c in range(8):
        b, p = c // 4, c % 4
        xT = np.zeros((D, 4 * SW), np.float32)
        xbT = inp["x"][b].T
        for m in range(4):
            s_ = 4 * m + p
            xT[:, m * SW + 4:m * SW + 516] = xbT[:, s_ * 512:(s_ + 1) * 512]
            if s_ > 0:
                xT[:, m * SW:m * SW + 4] = xbT[:, s_ * 512 - 4:s_ * 512]
        sel = np.zeros((128, 32), np.float32)
        if p >= 1:
            sel[:, p - 1] = 1.0
        else:
            sel[:, 4] = 1.0
        for s_ in range(16):
            if s_ % 4 == p:
                sel[:, 8 + s_] = 1.0
        for jr in range(4):
            sel[:, 24 + jr] = 1.0 if jr == p else 0.0
            sel[:, 28 + jr] = NEG if jr > p else 0.0
        d = dict(shared)
        d["x0"] = np.ascontiguousarray(xT)
        d["pos"] = np.ascontiguousarray(inp["positions"][b][_stripe_tokens(p)][None, :]).astype(np.int32)
        d["sel"] = sel
        in_maps.append(d)
    return in_maps, offA, offD


def kernel_fused(**inp):
    inp = {k: np.asarray(v) for k, v in inp.items()}
    in_maps, offA, offD = fused_in_maps(inp)
    if "F" not in _PROG_CACHE:
        _PROG_CACHE["F"] = build_fused(offA, offD)[0]
    res = run_bass_kernel_spmd(_PROG_CACHE["F"], in_maps, core_ids=list(range(8))).results
    xo = np.zeros((2, S_, D), np.float32)
    for c in range(8):
        b, p = c // 4, c % 4
        xo[b, _stripe_tokens(p), :] = np.asarray(res[c]["out"]).T
    return xo
```

```python
from contextlib import ExitStack
import numpy as np
import concourse.bass as bass
import concourse.mybir as mybir

F32 = mybir.dt.float32
BF16 = mybir.dt.bfloat16
I32 = mybir.dt.int32
ALU = mybir.AluOpType
AF = mybir.ActivationFunctionType
AX = mybir.AxisListType

COMPUTE = ("tensor", "vector", "scalar", "gpsimd")
QUEUES = ("sync", "gpsimd", "scalar")
NRING = 8


class View:
    __slots__ = ("buf", "ap", "key")

    def __init__(self, buf, ap, key=None):
        self.buf = buf
        self.ap = ap
        self.key = key

    def __getitem__(self, k):
        return View(self.buf, self.ap[k], self.key)

    def re(self, s, **kw):
        return View(self.buf, self.ap.rearrange(s, **kw), self.key)

    def bc(self, shape):
        return View(self.buf, self.ap.to_broadcast(shape), self.key)

    def bitcast(self, dt):
        return View(self.buf, self.ap.bitcast(dt), self.key)

    def k(self, key):
        return View(self.buf, self.ap, key)

    def f(self, fn):
        return View(self.buf, fn(self.ap), self.key)


class Buf:
    def __init__(self, name, handle, is_dram=False):
        self.name = name
        self.h = handle
        self.is_dram = is_dram
        self.regions = {}

    def full(self):
        ap = self.h.ap() if hasattr(self.h, "ap") and callable(getattr(self.h, "ap")) else self.h[:]
        return View(self, ap)

    def __getitem__(self, k):
        return View(self, self.h[k])


class Op:
    __slots__ = ("id", "eng", "meth", "kw", "deps", "is_dma", "signaled", "sem", "val", "prewait", "eidx")


class Eng:
    def __init__(self, P, name):
        self.P = P
        self.name = name

    def __getattr__(self, meth):
        def call(*a, **kw):
            assert not a, "use kwargs"
            return self.P._record(self.name, meth, kw)
        return call


class Prog:
    def __init__(self, nc):
        self.nc = nc
        self.ops = []
        self.gstack = ExitStack()
        self.stack = ExitStack()
        self.pe = Eng(self, "tensor")
        self.dve = Eng(self, "vector")
        self.act = Eng(self, "scalar")
        self.pool = Eng(self, "gpsimd")
        self.sp = Eng(self, "sync")
        st = self.gstack
        self.csem = {e: st.enter_context(nc.semaphore(f"c_{e}")) for e in COMPUTE}
        self.rings = {q: [st.enter_context(nc.semaphore(f"d_{q}{i}")) for i in range(NRING)] for q in QUEUES}
        self.ccsem = st.enter_context(nc.semaphore("ccsem"))
        self.cccount = 0
        self.ccount = {e: 0 for e in COMPUTE}
        self.dcount = {q: 0 for q in QUEUES}
        self.waited = {e: {} for e in ("sync",) + COMPUTE}
        self.emitted = 0
        self.barrier = []
        self.stats = {}
        self.nwaits = 0

    def sb(self, name, shape, dtype):
        self.nuid = getattr(self, "nuid", 0) + 1
        name = f"{name}_s{self.nuid}"
        t = self.stack.enter_context(self.nc.sbuf_tensor(name, list(shape), dtype))
        return Buf(name, t)

    def ps(self, name, shape, dtype):
        self.nuid = getattr(self, "nuid", 0) + 1
        name = f"{name}_p{self.nuid}"
        t = self.stack.enter_context(self.nc.psum_tensor(name, list(shape), dtype))
        return Buf(name, t)

    def dram(self, name, shape, dtype, kind="Internal"):
        t = self.nc.dram_tensor(name, list(shape), dtype, kind=kind)
        return Buf(name, t, is_dram=True)

    def _record(self, eng, meth, kw):
        op = Op()
        op.id = len(self.ops)
        op.eng = eng
        op.meth = meth
        op.kw = kw
        op.is_dma = meth in ("dma_start", "dma_start_transpose", "collective_compute")
        op.signaled = False
        op.sem = None
        op.val = 0
        op.prewait = None
        deps = set()
        extra_r = kw.pop("_reads", [])
        extra_w = kw.pop("_writes", [])
        writes, reads = [], []
        for k, v in kw.items():
            vs = v if isinstance(v, (list, tuple)) else [v]
            for x in vs:
                if isinstance(x, View):
                    if k in ("out", "accum_out", "outs") or (k == "ap" and meth in ("memset", "memzero")):
                        writes.append(x)
                    else:
                        reads.append(x)
        reads += extra_r
        writes += extra_w
        for v in reads:
            self._gather(v, False, deps)
        for v in writes:
            self._gather(v, True, deps)
        for v in reads:
            self._update(v, False, op.id)
        for v in writes:
            self._update(v, True, op.id)
        deps.discard(op.id)
        op.deps = deps
        self.ops.append(op)
        return op

    def _gather(self, v, is_write, deps):
        R = v.buf.regions
        if v.key is None:
            regs = list(R.values())
        else:
            regs = [R[k] for k in (v.key, None) if k in R]
        for reg in regs:
            if reg[0] is not None:
                deps.add(reg[0])
            if is_write:
                deps.update(reg[1])

    def _update(self, v, is_write, oid):
        R = v.buf.regions
        if is_write:
            if v.key is None:
                R.clear()
            R[v.key] = [oid, []]
        else:
            R.setdefault(v.key, [None, []])[1].append(oid)

    def dma(self, q, out, in_, **kw):
        eng = {"sync": self.sp, "gpsimd": self.pool, "scalar": self.act}[q]
        return eng.dma_start(out=out, in_=in_, **kw)

    def emit(self, final=True):
        nc = self.nc
        ops = self.ops
        phase = ops[self.emitted:]
        first_id = self.emitted
        self.emitted = len(ops)
        for op in phase:
            for d in op.deps:
                dop = ops[d]
                if d < first_id:
                    continue
                if dop.eng == "tensor" and op.eng == "tensor" and not dop.is_dma and not op.is_dma:
                    continue
                dop.signaled = True
        per = {}
        for op in phase:
            per.setdefault(op.eng, []).append(op)
        for e, lst in per.items():
            for op in reversed(lst):
                if not op.is_dma:
                    op.signaled = True
                    break
        for op in phase:
            if op.meth == "collective_compute":
                self.cccount += 1
                op.sem = self.ccsem
                op.val = self.cccount
                op.signaled = True
            elif op.is_dma:
                k = self.dcount[op.eng]
                self.dcount[op.eng] += 1
                op.sem = self.rings[op.eng][k % NRING]
                op.val = 16 * (k // NRING + 1)
                if k >= NRING:
                    op.prewait = (op.sem, 16 * (k // NRING))
                op.signaled = True
            elif op.signaled:
                self.ccount[op.eng] += 1
                op.sem = self.csem[op.eng]
                op.val = self.ccount[op.eng]
        for e, v in per.items():
            self.stats[e] = self.stats.get(e, 0) + len(v)
        barrier_in = list(self.barrier)
        dcount = self.dcount
        rings = self.rings

        def dma_final_waits():
            ws = []
            for q in QUEUES:
                n = dcount[q]
                for i in range(min(n, NRING)):
                    cnt = (n - 1 - i) // NRING + 1
                    ws.append((rings[q][i], 16 * cnt))
            if self.cccount > 0:
                ws.append((self.ccsem, self.cccount))
            return ws

        def run(engname, e):
            waited = self.waited[engname]

            def do_waits(ws):
                for sem, val in ws:
                    key = id(sem)
                    if waited.get(key, 0) >= val:
                        continue
                    waited[key] = val
                    e.wait_ge(sem, val)
                    self.nwaits += 1

            do_waits(barrier_in)
            for op in per.get(engname, []):
                ws = []
                if op.prewait is not None:
                    ws.append(op.prewait)
                for d in sorted(op.deps):
                    dop = ops[d]
                    if dop.sem is None:
                        continue
                    if dop.eng == "tensor" and op.eng == "tensor" and not dop.is_dma and not op.is_dma:
                        continue
                    ws.append((dop.sem, dop.val))
                do_waits(ws)
                kw = {}
                for k, v in op.kw.items():
                    if isinstance(v, View):
                        kw[k] = v.ap
                    elif isinstance(v, (list, tuple)) and v and isinstance(v[0], View):
                        kw[k] = [x.ap for x in v]
                    else:
                        kw[k] = v
                ins = getattr(e, op.meth)(**kw)
                if op.signaled:
                    ins.then_inc(op.sem, 16 if (op.is_dma and op.meth != "collective_compute") else 1)
            if final and engname == "sync":
                do_waits(dma_final_waits())

        with nc.Block() as block:
            @block.sync
            def _(e):
                run("sync", e)

            @block.tensor
            def _(e):
                run("tensor", e)

            @block.vector
            def _(e):
                run("vector", e)

            @block.scalar
            def _(e):
                run("scalar", e)

            @block.gpsimd
            def _(e):
                run("gpsimd", e)
        bar = dma_final_waits()
        for e in COMPUTE:
            if self.ccount[e] > 0:
                bar.append((self.csem[e], self.ccount[e]))
        self.barrier = bar
        self.stats["waits"] = self.nwaits
        self.stack.close()
        self.stack = ExitStack()
        if final:
            self.gstack.close()


from concourse.bass_utils import run_bass_kernel_spmd
import ml_dtypes

NBF = ml_dtypes.bfloat16
D = 1024
T = 2048
HALO = 4
NEG = -30000.0


class ColPack:
    def __init__(self):
        self.cols = []
        self.off = {}
        self.n = 0

    def add(self, name, vec, rows=128):
        vec = np.asarray(vec, np.float32).reshape(-1)
        assert vec.size % rows == 0
        m = vec.reshape(-1, rows).T
        a = np.zeros((128, m.shape[1]), np.float32)
        a[:rows] = m
        self.off[name] = (self.n, m.shape[1], rows)
        self.cols.append(a)
        self.n += m.shape[1]

    def array(self):
        return np.ascontiguousarray(np.concatenate(self.cols, axis=1))


class Cst:
    def __init__(self, P, buf, off):
        self.buf = buf
        self.off = off

    def col(self, name, j=0, rows=None):
        o, n, r = self.off[name]
        r = rows or r
        return self.buf[0:r, o + j:o + j + 1]

    def cols(self, name):
        o, n, r = self.off[name]
        return self.buf[0:r, o:o + n]


def new_nc():
    return bass.Bass("TRN2", target_bir_lowering=False)


def load_cast(P, q, dram_view, stage_view, bf_view, cast_eng):
    P.dma(q, out=stage_view, in_=dram_view)
    cast_eng.tensor_copy(out=bf_view, in_=stage_view)


A_OFF = None


def a_colpack(inp, l):
    cp = ColPack()
    cp.add("g_mix", inp["norm_mix_g"][l])
    cp.add("g_cq", inp["mla_q_norm_g"][l])
    cp.add("g_ckv", inp["mla_kv_norm_g"][l])
    cp.add("g_q", inp["mla_q_gain"][l], 96)
    cp.add("g_k", inp["mla_k_gain"][l], 96)
    cp.add("g_fq", inp["fox_q_gain"][l], 64)
    cp.add("g_fk", inp["fox_k_gain"][l], 64)
    cp.add("b_f", inp["fox_b_f"][l], 8)
    cw = inp["ssm_conv_w"][l]
    for k in range(4):
        cp.add(f"cw{k}", cw[k])
    cp.add("cb", inp["ssm_conv_b"][l])
    cp.add("dt_b", inp["ssm_dt_bias"][l], 16)
    cp.add("A_log", inp["ssm_A_log"][l], 16)
    cp.add("b_gate", inp["b_gate"][l])
    inv = 1.0 / (10000.0 ** (np.arange(0, 32, 2, dtype=np.float32) / 32.0))
    invf = np.zeros(96, np.float32)
    invf[64:80] = inv
    invf[80:96] = inv
    cp.add("invf", invf, 96)
    return cp


def build_A(off):
    nc = new_nc()
    P = Prog(nc)
    TT = T + HALO
    NT = T // 512
    EI, EO = "ExternalInput", "ExternalOutput"
    xT = P.dram("xT", [D, TT], F32, EI)
    pos = P.dram("pos", [1, T], I32, EI)
    w_in = P.dram("w_in", [D, 7864], F32, EI)
    w_uq = P.dram("w_uq", [384, 768], F32, EI)
    w_kp = P.dram("w_kp", [256, 768], F32, EI)
    w_v = P.dram("w_v", [256, 512], F32, EI)
    cst_d = P.dram("cst", [128, off["_n"]], F32, EI)
    mats = P.dram("mats", [128, 2 * 96], F32, EI)
    o_qm = P.dram("o_qm", [8, 96, T], BF16, EO)
    o_km = P.dram("o_km", [8, 96, T], BF16, EO)
    o_vm = P.dram("o_vm", [512, T], BF16, EO)
    o_qf = P.dram("o_qf", [8, 64, T], BF16, EO)
    o_kf = P.dram("o_kf", [8, 64, T], BF16, EO)
    o_vf = P.dram("o_vf", [512, T], BF16, EO)
    o_lf = P.dram("o_lf", [8, T], F32, EO)
    o_sz = P.dram("o_sz", [1024, T], BF16, EO)
    o_xbc = P.dram("o_xbc", [1536, T], BF16, EO)
    o_dt = P.dram("o_dt", [16, T], F32, EO)
    o_a = P.dram("o_a", [16, T], F32, EO)
    o_g = P.dram("o_g", [3072, T], BF16, EO)

    cstb = P.sb("cstb", [128, off["_n"]], F32)
    C = Cst(P, cstb, off)
    P.dma("sync", out=cstb.full(), in_=cst_d.full())
    matf = P.sb("matf", [128, 192], F32)
    matb = P.sb("matb", [128, 192], BF16)
    P.dma("sync", out=matf.full(), in_=mats.full())
    P.dve.tensor_copy(out=matb.full(), in_=matf.full())
    prh = matb[0:96, 0:96]
    sel = matb[0:32, 96:192]
    ones = P.sb("ones", [128, 128], F32)
    P.dve.memset(ap=ones.full(), constant=1.0)
    eps = P.sb("eps", [128, 1], F32)
    P.dve.memset(ap=eps.full(), constant=1e-6)
    one1 = P.sb("one1", [128, 1], F32)
    P.dve.memset(ap=one1.full(), constant=1.0)
    nbf = P.sb("nbf", [8, 1], F32)
    P.dve.tensor_scalar(out=nbf.full(), in0=C.col("b_f"), scalar1=-1.0, scalar2=None, op0=ALU.mult)
    Aneg = P.sb("Aneg", [16, 1], F32)
    P.act.activation(out=Aneg.full(), in_=C.col("A_log"), func=AF.Exp)
    P.dve.tensor_scalar(out=Aneg.full(), in0=Aneg.full(), scalar1=-1.0, scalar2=None, op0=ALU.mult)

    pb = [P.ps(f"pb{i}", [128, 512], F32) for i in range(8)]
    pbi = {}

    def nxt_ps(lo=0, hi=4):
        i = pbi.get(lo, 0)
        pbi[lo] = (i + 1) % (hi - lo)
        return pb[lo + i]

    Ctab = P.sb("Ctab", [96, T], F32)
    Stab = P.sb("Stab", [96, T], F32)
    posi = P.sb("posi", [96, 512], I32)
    posf = P.sb("posf", [96, 512], F32)
    rr_tmp = P.sb("rr_tmp", [96, 512], F32)
    rr_i = P.sb("rr_i", [96, 512], I32)
    rr_m = P.sb("rr_m", [96, 512], F32)

    def sin_table(outv, phase):
        P.dve.tensor_scalar(out=rr_tmp.full(), in0=posf.full(), scalar1=C.col("invf"), scalar2=phase,
                            op0=ALU.mult, op1=ALU.add)
        P.dve.tensor_scalar(out=rr_m.full(), in0=rr_tmp.full(), scalar1=1.0 / (2 * np.pi), scalar2=None, op0=ALU.mult)
        P.dve.tensor_copy(out=rr_i.full(), in_=rr_m.full())
        P.dve.tensor_copy(out=rr_m.full(), in_=rr_i.full())
        P.dve.scalar_tensor_tensor(out=rr_tmp.full(), in0=rr_m.full(), scalar=-2 * np.pi, in1=rr_tmp.full(),
                                   op0=ALU.mult, op1=ALU.add)
        P.dve.tensor_scalar(out=rr_m.full(), in0=rr_tmp.full(), scalar1=np.pi, scalar2=-2 * np.pi, op0=ALU.is_gt, op1=ALU.mult)
        P.dve.tensor_tensor(out=rr_tmp.full(), in0=rr_tmp.full(), in1=rr_m.full(), op=ALU.add)
        P.dve.tensor_scalar(out=rr_m.full(), in0=rr_tmp.full(), scalar1=-np.pi, scalar2=2 * np.pi, op0=ALU.is_lt, op1=ALU.mult)
        P.dve.tensor_tensor(out=rr_tmp.full(), in0=rr_tmp.full(), in1=rr_m.full(), op=ALU.add)
        P.act.activation(out=outv, in_=rr_tmp.full(), func=AF.Sin)

    for i in range(NT):
        P.dma("sync", out=posi.full(), in_=pos[:, i * 512:(i + 1) * 512].f(lambda a: a.partition_broadcast(96)))
        P.dve.tensor_copy(out=posf.full(), in_=posi.full())
        sin_table(Stab[:, i * 512:(i + 1) * 512], 0.0)
        sin_table(Ctab[:, i * 512:(i + 1) * 512], np.pi / 2)
    P.dve.memset(ap=Stab[0:64, :], constant=0.0)
    P.dve.memset(ap=Ctab[0:64, :], constant=1.0)

    hn = P.sb("hn", [128, 8, TT], BF16)
    xst = P.sb("xst", [128, 8, 512], F32)
    sq = P.sb("sq", [128, 512], F32)
    rstd = P.sb("rstd", [128, 512], F32)
    xTv = xT.full().re("(kc p) n -> p kc n", p=128)

    def rstd_from(ps_view, n_feat, rows, width, rstd_view):
        P.act.activation(out=rstd_view, in_=ps_view, func=AF.Sqrt, bias=eps[0:rows, 0:1], scale=1.0 / n_feat)
        P.dve.reciprocal(out=rstd_view, in_=rstd_view)

    tiles = [(0, HALO)] + [(HALO + i * 512, 512) for i in range(NT)]
    for (c0, w) in tiles:
        P.dma("sync", out=xst[:, :, 0:w], in_=xTv[:, :, c0:c0 + w])
        ps = nxt_ps(4, 6)
        for kc in range(8):
            P.act.activation(out=sq[:, 0:w], in_=xst[:, kc, 0:w], func=AF.Square)
            P.pe.matmul(out=ps[:, 0:w], lhsT=ones.full(), rhs=sq[:, 0:w], start=(kc == 0), stop=(kc == 7))
        rstd_from(ps[:, 0:w], 1024.0, 128, w, rstd[:, 0:w])
        for kc in range(8):
            P.dve.scalar_tensor_tensor(out=hn[:, kc, c0:c0 + w], in0=xst[:, kc, 0:w], scalar=C.col("g_mix", kc),
                                       in1=rstd[:, 0:w], op0=ALU.mult, op1=ALU.mult)

    wst = [P.sb(f"wst{i}", [128, 8, 512], F32) for i in range(2)]
    wbf = [P.sb(f"wbf{i}", [128, 8, 512], BF16) for i in range(2)]
    wcnt = [0]
    w_inv = w_in.full().re("(kc p) n -> p kc n", p=128)

    def load_w(c0, ncols):
        i = wcnt[0] % 2
        wcnt[0] += 1
        q = "sync" if i == 0 else "gpsimd"
        P.dma(q, out=wst[i][:, :, 0:ncols], in_=w_inv[:, :, c0:c0 + ncols])
        P.pool.tensor_copy(out=wbf[i][:, :, 0:ncols], in_=wst[i][:, :, 0:ncols])
        return wbf[i]

    def proj(wb, wc0, m, c0, w, ps_view):
        for kc in range(8):
            P.pe.matmul(out=ps_view, lhsT=wb[:, kc, wc0:wc0 + m], rhs=hn[:, kc, c0:c0 + w],
                        start=(kc == 0), stop=(kc == 7))

    ostg_cnt = [0]
    ostg = [P.sb(f"ostg{i}", [128, 512], BF16) for i in range(4)]

    def next_ostg():
        i = ostg_cnt[0] % 4
        ostg_cnt[0] += 1
        return ostg[i]

    def out_dma(dst_view, src_view):
        q = "sync" if ostg_cnt[0] % 2 else "gpsimd"
        P.dma(q, out=dst_view, in_=src_view)

    hraw = P.sb("hraw", [96, 512], F32)
    hsq = P.sb("hsq", [96, 512], F32)
    hrs = P.sb("hrs", [96, 512], F32)
    hnf = P.sb("hnf", [96, 512], F32)
    hnb = P.sb("hnb", [96, 512], BF16)
    ht1 = P.sb("ht1", [96, 512], F32)
    ht2 = P.sb("ht2", [96, 512], F32)

    def headnorm(ps_view, d, gain_col, rope, tok0, dst_view):
        P.act.activation(out=hsq[0:d, :], in_=ps_view, func=AF.Square)
        P.act.copy(out=hraw[0:d, :], in_=ps_view)
        ps2 = nxt_ps(4, 6)
        P.pe.matmul(out=ps2[0:d, :], lhsT=ones[0:d, 0:d], rhs=hsq[0:d, :], start=True, stop=True)
        rstd_from(ps2[0:d, :], float(d), d, 512, hrs[0:d, :])
        og = next_ostg()
        if not rope:
            P.dve.scalar_tensor_tensor(out=og[0:d, :], in0=hraw[0:d, :], scalar=gain_col, in1=hrs[0:d, :],
                                       op0=ALU.mult, op1=ALU.mult)
        else:
            P.dve.scalar_tensor_tensor(out=hnf[0:d, :], in0=hraw[0:d, :], scalar=gain_col, in1=hrs[0:d, :],
                                       op0=ALU.mult, op1=ALU.mult)
            P.act.copy(out=hnb[0:d, :], in_=hnf[0:d, :])
            ps3 = nxt_ps(6, 8)
            P.pe.matmul(out=ps3[0:d, :], lhsT=prh, rhs=hnb[0:d, :], start=True, stop=True)
            P.dve.tensor_tensor(out=ht1[0:d, :], in0=hnf[0:d, :], in1=Ctab[0:d, tok0:tok0 + 512], op=ALU.mult)
            P.dve.tensor_tensor(out=ht2[0:d, :], in0=ps3[0:d, :], in1=Stab[0:d, tok0:tok0 + 512], op=ALU.mult)
            P.pool.tensor_tensor(out=og[0:d, :], in0=ht1[0:d, :], in1=ht2[0:d, :], op=ALU.add)
        out_dma(dst_view, og[0:d, :])

    lat = P.sb("lat", [128, 3, 512], F32)
    latn = P.sb("latn", [128, 3, 512], BF16)

    def latent_norm(ps_list, gname):
        nch = len(ps_list)
        ps2 = nxt_ps(4, 6)
        for i, psv in enumerate(ps_list):
            P.act.activation(out=sq.full(), in_=psv, func=AF.Square)
            P.act.copy(out=lat[:, i, :], in_=psv)
            P.pe.matmul(out=ps2.full(), lhsT=ones.full(), rhs=sq.full(), start=(i == 0), stop=(i == nch - 1))
        rstd_from(ps2.full(), 128.0 * nch, 128, 512, rstd.full())
        for i in range(nch):
            P.dve.scalar_tensor_tensor(out=latn[:, i, :], in0=lat[:, i, :], scalar=C.col(gname, i), in1=rstd.full(),
                                       op0=ALU.mult, op1=ALU.mult)

    def small_w(name, dram, kc_n, ncols, i):
        stg = wst[i].full().re("p a b -> p (a b)")[:, 0:kc_n * ncols].re("p (a b) -> p a b", a=kc_n)
        bfb = P.sb(name, [128, kc_n, ncols], BF16)
        P.dma("gpsimd", out=stg, in_=dram.full().re("(kc p) n -> p kc n", p=128))
        P.pool.tensor_copy(out=bfb.full(), in_=stg)
        return bfb

    uqb = small_w("uqb", w_uq, 3, 768, 0)
    kpb = small_w("kpb", w_kp, 2, 768, 1)
    wvb = small_w("wvb", w_v, 2, 512, 0)
    main = tiles[1:]
    wb = load_w(0, 384)
    for ti, (c0, w) in enumerate(main):
        pss = []
        for ch in range(3):
            ps = nxt_ps(0, 4)
            proj(wb, ch * 128, 128, c0, 512, ps.full())
            pss.append(ps.full())
        latent_norm(pss, "g_cq")
        for h in range(8):
            ps = nxt_ps(0, 4)
            for kc in range(3):
                P.pe.matmul(out=ps[0:96, :], lhsT=uqb[:, kc, h * 96:(h + 1) * 96], rhs=latn[:, kc, :],
                            start=(kc == 0), stop=(kc == 2))
            headnorm(ps[0:96, :], 96, C.col("g_q"), True, ti * 512, o_qm[h, :, ti * 512:(ti + 1) * 512])
    wb = load_w(384, 288)
    krb = P.sb("krb", [32, 512], BF16)
    for ti, (c0, w) in enumerate(main):
        pss = []
        for ch in range(2):
            ps = nxt_ps(0, 4)
            proj(wb, ch * 128, 128, c0, 512, ps.full())
            pss.append(ps.full())
        ps = nxt_ps(0, 4)
        proj(wb, 256, 32, c0, 512, ps[0:32, :])
        P.act.copy(out=krb.full(), in_=ps[0:32, :])
        latent_norm(pss, "g_ckv")
        for h in range(8):
            ps = nxt_ps(0, 4)
            for kc in range(2):
                P.pe.matmul(out=ps[0:96, :], lhsT=kpb[:, kc, h * 96:(h + 1) * 96], rhs=latn[:, kc, :],
                            start=(kc == 0), stop=False)
            P.pe.matmul(out=ps[0:96, :], lhsT=sel, rhs=krb.full(), start=False, stop=True)
            headnorm(ps[0:96, :], 96, C.col("g_k"), True, ti * 512, o_km[h, :, ti * 512:(ti + 1) * 512])
        for ch in range(4):
            ps = nxt_ps(0, 4)
            for kc in range(2):
                P.pe.matmul(out=ps.full(), lhsT=wvb[:, kc, ch * 128:(ch + 1) * 128], rhs=latn[:, kc, :],
                            start=(kc == 0), stop=(kc == 1))
            og = next_ostg()
            P.act.copy(out=og.full(), in_=ps.full())
            out_dma(o_vm[ch * 128:(ch + 1) * 128, ti * 512:(ti + 1) * 512], og.full())
    for (base, gname, dst) in ((672, "g_fq", o_qf), (672 + 512, "g_fk", o_kf)):
        wb = load_w(base, 512)
        for ti, (c0, w) in enumerate(main):
            for h in range(8):
                ps = nxt_ps(0, 4)
                proj(wb, h * 64, 64, c0, 512, ps[0:64, :])
                headnorm(ps[0:64, :], 64, C.col(gname), False, ti * 512, dst[h, :, ti * 512:(ti + 1) * 512])
    def plain_group(base, ncols, func, bias_name, dst, dst_row0):
        wb = load_w(base, ncols)
        for ti, (c0, w) in enumerate(main):
            for ch in range(ncols // 128):
                ps = nxt_ps(0, 4)
                proj(wb, ch * 128, 128, c0, 512, ps.full())
                og = next_ostg()
                if bias_name is None:
                    P.act.activation(out=og.full(), in_=ps.full(), func=func)
                else:
                    P.act.activation(out=og.full(), in_=ps.full(), func=func,
                                     bias=C.col(bias_name, (dst_row0 // 128) + ch))
                out_dma(dst[dst_row0 + ch * 128:dst_row0 + (ch + 1) * 128, ti * 512:(ti + 1) * 512], og.full())

    plain_group(672 + 1024, 512, AF.Copy, None, o_vf, 0)
    FB = 672 + 1536
    SB = 672 + 1544
    wf = load_w(FB, 8)
    lf1 = P.sb("lf1", [16, 512], F32)
    lf2 = P.sb("lf2", [16, 512], F32)
    for ti, (c0, w) in enumerate(main):
        ps = nxt_ps(0, 4)
        proj(wf, 0, 8, c0, 512, ps[0:8, :])
        P.act.activation(out=lf1[0:8, :], in_=ps[0:8, :], func=AF.Exp, bias=nbf[0:8, 0:1], scale=-1.0)
        P.act.activation(out=lf1[0:8, :], in_=lf1[0:8, :], func=AF.Ln, bias=one1[0:8, 0:1], scale=1.0)
        P.dve.tensor_scalar(out=lf2[0:8, :], in0=lf1[0:8, :], scalar1=-1.0, scalar2=None, op0=ALU.mult)
        P.dma("sync", out=o_lf[:, ti * 512:(ti + 1) * 512], in_=lf2[0:8, :])
    wd = load_w(SB + 1024 + 1536, 16)
    dt1 = P.sb("dt1", [16, 512], F32)
    dt2 = P.sb("dt2", [16, 512], F32)
    for ti, (c0, w) in enumerate(main):
        ps = nxt_ps(0, 4)
        proj(wd, 0, 16, c0, 512, ps[0:16, :])
        P.act.activation(out=dt1.full(), in_=ps[0:16, :], func=AF.Exp, bias=C.col("dt_b"), scale=1.0)
        P.act.activation(out=dt1.full(), in_=dt1.full(), func=AF.Ln, bias=one1[0:16, 0:1], scale=1.0)
        P.dma("sync", out=o_dt[:, ti * 512:(ti + 1) * 512], in_=dt1.full())
        P.dve.tensor_scalar(out=dt2.full(), in0=dt1.full(), scalar1=Aneg[:, 0:1], scalar2=None, op0=ALU.mult)
        P.dma("sync", out=o_a[:, ti * 512:(ti + 1) * 512], in_=dt2.full())
    for blk in range(2):
        plain_group(SB + blk * 512, 512, AF.Silu, None, o_sz, blk * 512)
    upre = P.sb("upre", [128, 516], F32)
    carry = P.sb("carry", [128, 12, 4], F32)
    acc = [P.sb(f"acc{i}", [128, 512], F32) for i in range(2)]
    for blk in range(3):
        wb = load_w(SB + 1024 + blk * 512, 512)
        for ch in range(4):
            cg = blk * 4 + ch
            ps = nxt_ps(0, 4)
            proj(wb, ch * 128, 128, 0, HALO, ps[:, 0:HALO])
            P.act.copy(out=carry[:, cg, :], in_=ps[:, 0:HALO])
        for ti, (c0, w) in enumerate(main):
            for ch in range(4):
                cg = blk * 4 + ch
                ps = nxt_ps(0, 4)
                proj(wb, ch * 128, 128, c0, 512, ps.full())
                P.act.copy(out=upre[:, 4:516], in_=ps.full())
                P.dve.tensor_copy(out=upre[:, 0:4], in_=carry[:, cg, :])
                P.pool.tensor_copy(out=carry[:, cg, :], in_=upre[:, 512:516])
                a0 = acc[0]
                P.dve.tensor_scalar(out=a0.full(), in0=upre[:, 4:516], scalar1=C.col("cw3", cg), scalar2=C.col("cb", cg),
                                    op0=ALU.mult, op1=ALU.add)
                for k in range(3):
                    P.dve.scalar_tensor_tensor(out=a0.full(), in0=upre[:, 1 + k:513 + k], scalar=C.col(f"cw{k}", cg),
                                               in1=a0.full(), op0=ALU.mult, op1=ALU.add)
                og = next_ostg()
                P.act.activation(out=og.full(), in_=a0.full(), func=AF.Silu)
                out_dma(o_xbc[cg * 128:(cg + 1) * 128, ti * 512:(ti + 1) * 512], og.full())
    GB = SB + 2576
    for blk in range(6):
        plain_group(GB + blk * 512, 512, AF.Sigmoid, "b_gate", o_g, blk * 512)
    P.emit()
    return nc, P


def _bf(a):
    return np.asarray(a).astype(np.float32)


_PROG_CACHE = {}


def _const_mats():
    m = np.zeros((128, 192), np.float32)
    for i in range(16):
        m[80 + i, 64 + i] = -1.0
        m[64 + i, 80 + i] = 1.0
    for i in range(32):
        m[i, 96 + 64 + i] = 1.0
    return m


def run_A(inp, l, x_full, pos_full):
    cp = a_colpack(inp, l)
    off = dict(cp.off)
    off["_n"] = cp.n
    if "A" not in _PROG_CACHE:
        _PROG_CACHE["A"] = build_A(off)[0]
    nc = _PROG_CACHE["A"]
    cst = cp.array()
    wukv = inp["mla_w_ukv"][l].reshape(256, 8, 128)
    w_kp = np.zeros((256, 8, 96), np.float32)
    w_kp[:, :, 0:64] = wukv[:, :, 0:64]
    w_v = np.ascontiguousarray(wukv[:, :, 64:128].reshape(256, 512))
    mats = _const_mats()
    xf = x_full.reshape(16384, D)
    in_maps = []
    for c in range(8):
        t0 = c * T
        xt = np.zeros((D, T + HALO), np.float32)
        xt[:, HALO:] = xf[t0:t0 + T].T
        if c % 4 != 0:
            xt[:, 0:HALO] = xf[t0 - HALO:t0].T
        in_maps.append({
            "xT": np.ascontiguousarray(xt),
            "pos": np.ascontiguousarray(pos_full.reshape(1, 16384)[:, t0:t0 + T]).astype(np.int32),
            "w_in": np.ascontiguousarray(inp["w_in"][l]),
            "w_uq": np.ascontiguousarray(inp["mla_w_uq"][l]),
            "w_kp": np.ascontiguousarray(w_kp.reshape(256, 768)),
            "w_v": w_v, "cst": cst, "mats": mats,
        })
    res = run_bass_kernel_spmd(nc, in_maps, core_ids=list(range(8)))
    return res.results


S_ = 8192
NKT = S_ // 128
NQT = S_ // 512


def build_BC():
    nc = new_nc()
    P = Prog(nc)
    EI, EO = "ExternalInput", "ExternalOutput"
    qm = P.dram("qm", [2, 96, S_], BF16, EI)
    km = P.dram("km", [2, 96, S_], BF16, EI)
    vm = P.dram("vm", [2, 128, NKT, 64], BF16, EI)
    qf = P.dram("qf", [2, 64, S_], BF16, EI)
    kf = P.dram("kf", [2, 64, S_], BF16, EI)
    vf = P.dram("vf", [2, 128, NKT, 64], BF16, EI)
    lf = P.dram("lf", [2, 128, NKT], F32, EI)
    msk = P.dram("msk", [128, 8, 512], F32, EI)
    cm = P.dram("cm", [128, 4, 128], F32, EI)
    x_tm = P.dram("x_tm", [128, NKT, 256], BF16, EI)
    B_tm = P.dram("B_tm", [128, NKT, 128], BF16, EI)
    BT = P.dram("BT", [128, S_], BF16, EI)
    CT = P.dram("CT", [128, S_], BF16, EI)
    dt_tm = P.dram("dt_tm", [128, NKT, 4], F32, EI)
    a_tm = P.dram("a_tm", [128, NKT, 4], F32, EI)
    Dv = P.dram("Dv", [128, 4], F32, EI)
    o_m = P.dram("o_m", [2, 64, S_], BF16, EO)
    o_f = P.dram("o_f", [2, 64, S_], BF16, EO)
    o_y = P.dram("o_y", [128, NKT, 256], F32, EO)
    fsc = P.dram("fsc", [3, S_], BF16)

    cmb = P.sb("cmb", [128, 4, 128], F32)
    P.dma("sync", out=cmb.full(), in_=cm.full())
    tri, trimask, ident, ones = cmb[:, 0, :], cmb[:, 1, :], cmb[:, 2, :], cmb[:, 3, :]
    mskb = P.sb("mskb", [128, 8, 512], F32)
    P.dma("gpsimd", out=mskb.full(), in_=msk.full())
    zero = P.sb("zero", [128, 1], F32)
    P.dve.memset(ap=zero.full(), constant=0.0)

    pb = [P.ps(f"pb{i}", [128, 512], F32) for i in range(8)]
    K_sb = P.sb("K_sb", [128, S_], BF16)
    Q_sb = P.sb("Q_sb", [128, S_], BF16)
    V_sb = P.sb("V_sb", [128, NKT, 128], BF16)
    P.dve.memset(ap=V_sb[:, :, 64:128], constant=1.0)
    pt = [P.sb(f"pt{i}", [128, 512], BF16) for i in range(3)]
    mt = [P.sb(f"mt{i}", [128, 512], F32) for i in range(2)]
    rl = P.sb("rl", [128, 512], F32)
    rl2 = P.sb("rl2", [64, 512], F32)
    ot = [P.sb(f"ot{i}", [64, 512], BF16) for i in range(2)]
    negF = P.sb("negF", [128, NKT], F32)

    cnt = [0, 0, 0]

    def attention(dk, scale, mask0, bias_fn, out_dram_h):
        for qt in range(NQT):
            oacc = pb[3 + qt % 2]
            nk = 4 * qt + 4
            for kt in range(nk):
                i3 = cnt[0] % 3
                cnt[0] += 1
                ps = pb[i3]
                P.pe.matmul(out=ps.full(), lhsT=K_sb[0:dk, kt * 128:(kt + 1) * 128],
                            rhs=Q_sb[0:dk, qt * 512:(qt + 1) * 512], start=True, stop=True)
                if kt >= 4 * qt:
                    m = mt[cnt[1] % 2]
                    cnt[1] += 1
                    P.dve.tensor_tensor(out=m.full(), in0=ps.full(), in1=mskb[:, mask0 + kt - 4 * qt, :], op=ALU.add)
                    src = m.full()
                else:
                    src = ps.full()
                P.act.activation(out=pt[i3].full(), in_=src, func=AF.Exp, scale=scale, bias=bias_fn(kt))
                P.pe.matmul(out=oacc.full(), lhsT=V_sb[:, kt, :], rhs=pt[i3].full(), start=(kt == 0), stop=(kt == nk - 1))
            P.dve.reciprocal(out=rl[64:128, :], in_=oacc[64:128, :])
            P.dve.tensor_copy(out=rl2.full(), in_=rl[64:128, :])
            o = ot[qt % 2]
            P.dve.tensor_tensor(out=o.full(), in0=oacc[0:64, :], in1=rl2.full(), op=ALU.mult)
            P.dma("sync", out=out_dram_h[:, qt * 512:(qt + 1) * 512], in_=o.full())

    for h in range(2):
        P.dma("sync", out=K_sb[0:96, :], in_=km[h])
        P.dma("gpsimd", out=Q_sb[0:96, :], in_=qm[h])
        P.dma("sync", out=V_sb[:, :, 0:64], in_=vm[h])
        attention(96, 96.0 ** -0.5, 0, lambda kt: zero[:, 0:1], o_m[h])

    lfs = P.sb("lfs", [128, NKT], F32)
    wi = P.sb("wi", [128, NKT], F32)
    sc = [P.sb(f"sc{i}", [128, NKT], F32) for i in range(2)]
    Ff = P.sb("Ff", [128, NKT], F32)
    FT = P.sb("FT", [64, 128], F32)
    r1 = P.sb("r1", [64, 128], F32)
    fh = [P.sb(f"fh{i}", [64, 128], BF16) for i in range(3)]
    for h in range(2):
        P.dma("sync", out=lfs.full(), in_=lf[h])
        ps = pb[5]
        P.pe.matmul(out=ps[:, 0:NKT], lhsT=tri, rhs=lfs.full(), start=True, stop=True)
        P.act.copy(out=wi.full(), in_=ps[:, 0:NKT])
        ps = pb[6]
        P.pe.matmul(out=ps[:, 0:NKT], lhsT=ones, rhs=lfs.full(), start=True, stop=True)
        P.act.copy(out=sc[0].full(), in_=ps[:, 0:NKT])
        P.dve.tensor_tensor(out=wi.full(), in0=wi.full(), in1=sc[0].full(), op=ALU.subtract)
        cur = 0
        d = 1
        while d < NKT:
            nx = 1 - cur
            P.dve.tensor_copy(out=sc[nx][:, 0:d], in_=sc[cur][:, 0:d])
            P.dve.tensor_tensor(out=sc[nx][:, d:NKT], in0=sc[cur][:, d:NKT], in1=sc[cur][:, 0:NKT - d], op=ALU.add)
            cur = nx
            d *= 2
        P.dve.tensor_tensor(out=Ff.full(), in0=wi.full(), in1=sc[cur].full(), op=ALU.add)
        P.dve.tensor_scalar(out=negF.full(), in0=Ff.full(), scalar1=-1.0, scalar2=None, op0=ALU.mult)
        ps = pb[7]
        P.pe.transpose(out=ps[0:64, 0:128], in_=Ff.full(), identity=ident)
        P.act.copy(out=FT.full(), in_=ps[0:64, 0:128])
        P.dve.tensor_copy(out=fh[0].full(), in_=FT.full())
        P.dve.tensor_tensor(out=r1.full(), in0=FT.full(), in1=fh[0].full(), op=ALU.subtract)
        P.dve.tensor_copy(out=fh[1].full(), in_=r1.full())
        P.dve.tensor_tensor(out=r1.full(), in0=r1.full(), in1=fh[1].full(), op=ALU.subtract)
        P.dve.tensor_copy(out=fh[2].full(), in_=r1.full())
        for r in range(3):
            P.dma("sync", out=fsc[r].re("(kt p) -> kt p", p=128), in_=fh[r].full())
        P.dma("sync", out=K_sb[0:64, :], in_=kf[h])
        P.dve.memset(ap=K_sb[64:67, :], constant=8.0)
        P.dma("gpsimd", out=Q_sb[0:64, :], in_=qf[h])
        P.dma("gpsimd", out=Q_sb[64:67, :], in_=fsc.full())
        P.dma("sync", out=V_sb[:, :, 0:64], in_=vf[h])
        attention(67, 0.125, 4, lambda kt: negF[:, kt:kt + 1], o_f[h])

    a_sb = P.sb("a_sb", [128, NKT, 4], F32)
    dt_sb = P.sb("dt_sb", [128, NKT, 4], F32)
    Dsb = P.sb("Dsb", [128, 4], F32)
    P.dma("sync", out=a_sb.full(), in_=a_tm.full())
    P.dma("sync", out=dt_sb.full(), in_=dt_tm.full())
    P.dma("sync", out=Dsb.full(), in_=Dv.full())
    BTs = K_sb
    CTs = Q_sb
    P.dma("sync", out=BTs.full(), in_=BT.full())
    P.dma("gpsimd", out=CTs.full(), in_=CT.full())
    Acum = P.sb("Acum", [128, NKT, 4], F32)
    nAcum = P.sb("nAcum", [128, NKT, 4], F32)
    Atot = P.sb("Atot", [128, NKT, 4], F32)
    eA = P.sb("eA", [128, NKT, 4], F32)
    wdec = P.sb("wdec", [128, NKT, 4], F32)
    eAtot = P.sb("eAtot", [128, NKT, 4], F32)
    fl = lambda b: b.full().re("p c h -> p (c h)")
    ps = pb[0]
    P.pe.matmul(out=ps[:, 0:256], lhsT=tri, rhs=fl(a_sb), start=True, stop=True)
    P.act.copy(out=fl(Acum), in_=ps[:, 0:256])
    ps = pb[1]
    P.pe.matmul(out=ps[:, 0:256], lhsT=ones, rhs=fl(a_sb), start=True, stop=True)
    P.act.copy(out=fl(Atot), in_=ps[:, 0:256])
    P.dve.tensor_scalar(out=fl(nAcum), in0=fl(Acum), scalar1=-1.0, scalar2=None, op0=ALU.mult)
    P.act.activation(out=fl(eA), in_=fl(Acum), func=AF.Exp)
    P.act.activation(out=fl(eAtot), in_=fl(Atot), func=AF.Exp)
    P.dve.tensor_tensor(out=fl(wdec), in0=fl(Atot), in1=fl(Acum), op=ALU.subtract)
    P.act.activation(out=fl(wdec), in_=fl(wdec), func=AF.Exp)

    Hs = P.sb("Hs", [128, 256], F32)
    Hb = P.sb("Hb", [128, 256], BF16)
    P.dve.memset(ap=Hs.full(), constant=0.0)
    P.dve.memset(ap=Hb.full(), constant=0.0)
    xc = [P.sb(f"xc{i}", [128, 256], BF16) for i in range(2)]
    Bc = [P.sb(f"Bc{i}", [128, 128], BF16) for i in range(2)]
    cb = P.sb("cb", [128, 128], F32)
    xdt = P.sb("xdt", [128, 256], BF16)
    xdts = P.sb("xdts", [128, 256], BF16)
    at = [P.sb(f"at{i}", [128, 128], F32) for i in range(2)]
    tm = [P.sb(f"tm{i}", [128, 128], F32) for i in range(2)]
    dec = [P.sb(f"dec{i}", [128, 128], F32) for i in range(2)]
    MT = [P.sb(f"MT{i}", [128, 128], BF16) for i in range(2)]
    t1 = P.sb("t1", [128, 256], F32)
    t3 = P.sb("t3", [128, 256], F32)
    yo = [P.sb(f"yo{i}", [128, 256], F32) for i in range(2)]
    v3 = lambda v: v.re("p (h d) -> p h d", h=4)
    bc3 = lambda v: v.f(lambda a: a.unsqueeze(2).to_broadcast([128, 4, 64]))
    for c in range(NKT):
        x_c = xc[c % 2]
        B_c = Bc[c % 2]
        P.dma("sync", out=x_c.full(), in_=x_tm[:, c, :])
        P.dma("gpsimd", out=B_c.full(), in_=B_tm[:, c, :])
        BT_c = BTs[:, c * 128:(c + 1) * 128]
        CT_c = CTs[:, c * 128:(c + 1) * 128]
        ps_cb = pb[0]
        P.pe.matmul(out=ps_cb[:, 0:128], lhsT=BT_c, rhs=CT_c, start=True, stop=True)
        P.act.copy(out=cb.full(), in_=ps_cb[:, 0:128])
        P.dve.tensor_tensor(out=v3(xdt.full()), in0=v3(x_c.full()), in1=bc3(dt_sb[:, c, :]), op=ALU.mult)
        P.pool.tensor_tensor(out=v3(xdts.full()), in0=v3(xdt.full()), in1=bc3(wdec[:, c, :]), op=ALU.mult)
        ps_off = pb[1]
        P.pe.matmul(out=ps_off[:, 0:256], lhsT=CT_c, rhs=Hb.full(), start=True, stop=True)
        ps_y = pb[2]
        for h in range(4):
            i2 = h % 2
            P.dve.tensor_scalar(out=at[i2].full(), in0=tri, scalar1=a_sb[:, c, h:h + 1], scalar2=None, op0=ALU.mult)
            ps_A = pb[3 + i2]
            P.pe.matmul(out=ps_A[:, 0:128], lhsT=ones, rhs=at[i2].full(), start=True, stop=True)
            P.dve.tensor_tensor(out=tm[i2].full(), in0=ps_A[:, 0:128], in1=trimask, op=ALU.add)
            P.act.activation(out=dec[i2].full(), in_=tm[i2].full(), func=AF.Exp, bias=nAcum[:, c, h:h + 1], scale=1.0)
            P.pool.tensor_tensor(out=MT[i2].full(), in0=cb.full(), in1=dec[i2].full(), op=ALU.mult)
            P.pe.matmul(out=ps_y[:, h * 64:(h + 1) * 64], lhsT=MT[i2].full(), rhs=xdt[:, h * 64:(h + 1) * 64],
                        start=True, stop=True)
        P.dve.tensor_tensor(out=v3(t1.full()), in0=v3(ps_off[:, 0:256]), in1=bc3(eA[:, c, :]), op=ALU.mult)
        P.dve.tensor_tensor(out=t1.full(), in0=t1.full(), in1=ps_y[:, 0:256], op=ALU.add)
        P.pool.tensor_tensor(out=v3(t3.full()), in0=v3(x_c.full()), in1=bc3(Dsb.full()), op=ALU.mult)
        y_ = yo[c % 2]
        P.pool.tensor_tensor(out=y_.full(), in0=t1.full(), in1=t3.full(), op=ALU.add)
        P.dma("sync", out=o_y[:, c, :], in_=y_.full())
        ps_h = pb[5]
        P.pe.matmul(out=ps_h[:, 0:256], lhsT=B_c.full(), rhs=xdts.full(), start=True, stop=True)
        P.dve.tensor_tensor(out=v3(Hs.full()), in0=v3(Hs.full()), in1=bc3(eAtot[:, c, :]), op=ALU.mult)
        P.dve.tensor_tensor(out=Hs.full(), in0=Hs.full(), in1=ps_h[:, 0:256], op=ALU.add)
        P.act.copy(out=Hb.full(), in_=Hs.full())
    P.emit()
    return nc, P


def _bc_consts():
    msk = np.zeros((128, 8, 512), np.float32)
    p = np.arange(128)[:, None]
    q = np.arange(512)[None, :]
    for j in range(4):
        key = j * 128 + p
        msk[:, j, :] = np.where((key // 64) > (q // 64), NEG, 0.0)
        msk[:, 4 + j, :] = np.where(key > q, NEG, 0.0)
    cm = np.zeros((128, 4, 128), np.float32)
    jj = np.arange(128)[:, None]
    ii = np.arange(128)[None, :]
    cm[:, 0, :] = (jj <= ii).astype(np.float32)
    cm[:, 1, :] = np.where(jj > ii, NEG, 0.0)
    cm[:, 2, :] = np.eye(128, dtype=np.float32)
    cm[:, 3, :] = 1.0
    return msk, cm


def _tm(a):
    S, n = a.shape
    return np.ascontiguousarray(a.reshape(S // 128, 128, n).transpose(1, 0, 2))


def run_BC(inp, l, resA):
    if "BC" not in _PROG_CACHE:
        _PROG_CACHE["BC"] = build_BC()[0]
    nc = _PROG_CACHE["BC"]
    msk, cm = _bc_consts()

    def gather(name, b):
        return np.concatenate([np.asarray(resA[b * 4 + i][name]) for i in range(4)], axis=-1)

    in_maps = []
    for c in range(8):
        b, hg = c // 4, c % 4
        qm = gather("o_qm", b)[2 * hg:2 * hg + 2]
        km = gather("o_km", b)[2 * hg:2 * hg + 2]
        vmf = gather("o_vm", b)
        qf = gather("o_qf", b)[2 * hg:2 * hg + 2]
        kf = gather("o_kf", b)[2 * hg:2 * hg + 2]
        vff = gather("o_vf", b)
        lff = gather("o_lf", b)
        xbc = gather("o_xbc", b)
        dtf = gather("o_dt", b)
        af = gather("o_a", b)
        g = hg // 2
        vm = np.stack([_tm(vmf[(2 * hg + h) * 64:(2 * hg + h + 1) * 64].T) for h in range(2)])
        vf = np.stack([_tm(vff[(2 * hg + h) * 64:(2 * hg + h + 1) * 64].T) for h in range(2)])
        lf = np.stack([np.ascontiguousarray(lff[2 * hg + h].reshape(NKT, 128).T) for h in range(2)])
        x_tm = _tm(xbc[hg * 256:(hg + 1) * 256].T)
        Bf = xbc[1024 + g * 128:1024 + (g + 1) * 128]
        Cf = xbc[1280 + g * 128:1280 + (g + 1) * 128]
        Dv = np.broadcast_to(inp["ssm_D"][l][4 * hg:4 * hg + 4][None, :], (128, 4)).astype(np.float32)
        in_maps.append({
            "qm": np.ascontiguousarray(qm), "km": np.ascontiguousarray(km), "vm": vm,
            "qf": np.ascontiguousarray(qf), "kf": np.ascontiguousarray(kf), "vf": vf, "lf": lf,
            "msk": msk, "cm": cm, "x_tm": x_tm, "B_tm": _tm(Bf.T), "BT": np.ascontiguousarray(Bf),
            "CT": np.ascontiguousarray(Cf), "dt_tm": _tm(dtf[4 * hg:4 * hg + 4].T),
            "a_tm": _tm(af[4 * hg:4 * hg + 4].T), "Dv": np.ascontiguousarray(Dv),
        })
    res = run_bass_kernel_spmd(nc, in_maps, core_ids=list(range(8))).results
    om = np.zeros((2, 512, S_), NBF)
    of = np.zeros((2, 512, S_), NBF)
    y = np.zeros((2, 1024, S_), np.float32)
    for c in range(8):
        b, hg = c // 4, c % 4
        om[b, hg * 128:(hg + 1) * 128] = np.asarray(res[c]["o_m"]).reshape(128, S_)
        of[b, hg * 128:(hg + 1) * 128] = np.asarray(res[c]["o_f"]).reshape(128, S_)
        yy = np.asarray(res[c]["o_y"])
        y[b, hg * 256:(hg + 1) * 256] = yy.transpose(2, 1, 0).reshape(256, S_)
    return om, of, y


def build_D1():
    nc = new_nc()
    P = Prog(nc)
    EI, EO = "ExternalInput", "ExternalOutput"
    NT = T // 512
    omT = P.dram("omT", [512, T], BF16, EI)
    ofT = P.dram("ofT", [512, T], BF16, EI)
    yT = P.dram("yT", [1024, T], F32, EI)
    szT = P.dram("szT", [1024, T], BF16, EI)
    gT = P.dram("gT", [3072, T], BF16, EI)
    xT = P.dram("xT", [D, T], F32, EI)
    w_a = P.dram("w_a", [512, D], F32, EI)
    w_b = P.dram("w_b", [512, D], F32, EI)
    w_c = P.dram("w_c", [1024, D], F32, EI)
    w_o = P.dram("w_o", [1024, D], F32, EI)
    cst_d = P.dram("cst", [128, 8], F32, EI)
    o_x = P.dram("o_x", [D, T], F32, EO)

    cstb = P.sb("cstb", [128, 8], F32)
    P.dma("sync", out=cstb.full(), in_=cst_d.full())
    ones = P.sb("ones", [128, 128], F32)
    P.dve.memset(ap=ones.full(), constant=1.0)
    eps = P.sb("eps", [128, 1], F32)
    P.dve.memset(ap=eps.full(), constant=1e-6)
    pb = [P.ps(f"pb{i}", [128, 512], F32) for i in range(8)]
    wst = [P.sb(f"wst{i}", [128, 4, 1024], F32) for i in range(2)]
    wcnt = [0]

    def load_w(dram, kc_n, name):
        bfb = P.sb(name, [128, kc_n, D], BF16)
        v = dram.full().re("(kc p) n -> p kc n", p=128)
        for k0 in range(0, kc_n, 4):
            i = wcnt[0] % 2
            wcnt[0] += 1
            P.dma("sync" if i == 0 else "gpsimd", out=wst[i].full(), in_=v[:, k0:k0 + 4, :])
            P.pool.tensor_copy(out=bfb[:, k0:k0 + 4, :], in_=wst[i].full())
        return bfb

    Wa = load_w(w_a, 4, "Wa")
    Wb = load_w(w_b, 4, "Wb")
    Wc = load_w(w_c, 8, "Wc")
    Wo = load_w(w_o, 8, "Wo")

    ys = P.sb("ys", [128, 8, 512], F32)
    szs = P.sb("szs", [128, 8, 512], BF16)
    yn = P.sb("yn", [128, 8, 512], BF16)
    oms = P.sb("oms", [128, 4, 512], BF16)
    ofs = P.sb("ofs", [128, 4, 512], BF16)
    gs = P.sb("gs", [128, 24, 512], BF16)
    xs = P.sb("xs", [128, 8, 512], F32)
    sq = P.sb("sq", [128, 512], F32)
    rstd = P.sb("rstd", [128, 512], F32)
    m1 = [P.sb(f"m1_{i}", [128, 512], F32) for i in range(2)]
    m2 = [P.sb(f"m2_{i}", [128, 512], F32) for i in range(2)]
    m3 = [P.sb(f"m3_{i}", [128, 512], F32) for i in range(2)]
    mg = P.sb("mg", [128, 8, 512], BF16)
    xo = [P.sb(f"xo{i}", [128, 512], F32) for i in range(2)]
    ch = lambda d: d.full().re("(kc p) n -> p kc n", p=128)
    for ti in range(NT):
        ts = slice(ti * 512, (ti + 1) * 512)
        P.dma("sync", out=ys.full(), in_=ch(yT)[:, :, ts])
        P.dma("gpsimd", out=szs.full(), in_=ch(szT)[:, :, ts])
        P.dma("sync", out=oms.full(), in_=ch(omT)[:, :, ts])
        P.dma("gpsimd", out=ofs.full(), in_=ch(ofT)[:, :, ts])
        P.dma("sync", out=gs.full(), in_=ch(gT)[:, :, ts])
        P.dma("gpsimd", out=xs.full(), in_=ch(xT)[:, :, ts])
        ps = pb[7]
        for kc in range(8):
            P.dve.tensor_tensor(out=ys[:, kc, :], in0=ys[:, kc, :], in1=szs[:, kc, :], op=ALU.mult)
            P.act.activation(out=sq.full(), in_=ys[:, kc, :], func=AF.Square)
            P.pe.matmul(out=ps.full(), lhsT=ones.full(), rhs=sq.full(), start=(kc == 0), stop=(kc == 7))
        P.act.activation(out=rstd.full(), in_=ps.full(), func=AF.Sqrt, bias=eps[:, 0:1], scale=1.0 / 1024.0)
        P.dve.reciprocal(out=rstd.full(), in_=rstd.full())
        for kc in range(8):
            P.dve.scalar_tensor_tensor(out=yn[:, kc, :], in0=ys[:, kc, :], scalar=cstb[:, kc:kc + 1], in1=rstd.full(),
                                       op0=ALU.mult, op1=ALU.mult)
        for oc in range(8):
            i2 = oc % 2
            osl = slice(oc * 128, (oc + 1) * 128)
            pa, pbb, pc = pb[0 + i2 * 3], pb[1 + i2 * 3], pb[2 + i2 * 3]
            for kc in range(4):
                P.pe.matmul(out=pa.full(), lhsT=Wa[:, kc, osl], rhs=oms[:, kc, :], start=(kc == 0), stop=(kc == 3))
            for kc in range(4):
                P.pe.matmul(out=pbb.full(), lhsT=Wb[:, kc, osl], rhs=ofs[:, kc, :], start=(kc == 0), stop=(kc == 3))
            for kc in range(8):
                P.pe.matmul(out=pc.full(), lhsT=Wc[:, kc, osl], rhs=yn[:, kc, :], start=(kc == 0), stop=(kc == 7))
            P.dve.tensor_tensor(out=m1[i2].full(), in0=pa.full(), in1=gs[:, oc, :], op=ALU.mult)
            P.dve.tensor_tensor(out=m2[i2].full(), in0=pbb.full(), in1=gs[:, 8 + oc, :], op=ALU.mult)
            P.dve.tensor_tensor(out=m3[i2].full(), in0=pc.full(), in1=gs[:, 16 + oc, :], op=ALU.mult)
            P.pool.tensor_tensor(out=m1[i2].full(), in0=m1[i2].full(), in1=m2[i2].full(), op=ALU.add)
            P.pool.tensor_tensor(out=mg[:, oc, :], in0=m1[i2].full(), in1=m3[i2].full(), op=ALU.add)
        for oc in range(8):
            i2 = oc % 2
            ps = pb[6 + i2]
            for kc in range(8):
                P.pe.matmul(out=ps.full(), lhsT=Wo[:, kc, oc * 128:(oc + 1) * 128], rhs=mg[:, kc, :],
                            start=(kc == 0), stop=(kc == 7))
            P.dve.tensor_tensor(out=xo[i2].full(), in0=ps.full(), in1=xs[:, oc, :], op=ALU.add)
            P.dma("sync" if i2 else "gpsimd", out=o_x[oc * 128:(oc + 1) * 128, ts], in_=xo[i2].full())
    P.emit()
    return nc, P


def d2_colpack(inp, l):
    cp = ColPack()
    cp.add("g_ffn", inp["norm_ffn_g"][l])
    cw = inp["ffn_conv_w"][l]
    for k in range(3):
        cp.add(f"fw{k}", cw[k])
    cp.add("fb", inp["ffn_conv_b"][l])
    return cp


def build_D2(off):
    nc = new_nc()
    P = Prog(nc)
    EI, EO = "ExternalInput", "ExternalOutput"
    NT = T // 512
    TT = T + HALO
    xT = P.dram("xT", [D, TT], F32, EI)
    w_up = P.dram("w_up", [D, 5632], F32, EI)
    w_dn = P.dram("w_dn", [2816, D], F32, EI)
    cst_d = P.dram("cst", [128, off["_n"]], F32, EI)
    o_x = P.dram("o_x", [D, T], F32, EO)

    cstb = P.sb("cstb", [128, off["_n"]], F32)
    C = Cst(P, cstb, off)
    P.dma("sync", out=cstb.full(), in_=cst_d.full())
    ones = P.sb("ones", [128, 128], F32)
    P.dve.memset(ap=ones.full(), constant=1.0)
    eps = P.sb("eps", [128, 1], F32)
    P.dve.memset(ap=eps.full(), constant=1e-6)
    pb = [P.ps(f"pb{i}", [128, 512], F32) for i in range(8)]
    wst = [P.sb(f"wst{i}", [128, 1024], F32) for i in range(2)]
    wcnt = [0]
    Wu = P.sb("Wu", [128, 8, 5632], BF16)
    Wd = P.sb("Wd", [128, 22, D], BF16)
    wuv = w_up.full().re("(kc p) n -> p kc n", p=128)
    for kc in range(8):
        for c0 in range(0, 5632, 1024):
            n = min(1024, 5632 - c0)
            i = wcnt[0] % 2
            wcnt[0] += 1
            P.dma("sync" if i == 0 else "gpsimd", out=wst[i][:, 0:n], in_=wuv[:, kc, c0:c0 + n])
            P.pool.tensor_copy(out=Wu[:, kc, c0:c0 + n], in_=wst[i][:, 0:n])
    wdv = w_dn.full().re("(kc p) n -> p kc n", p=128)
    for k0 in range(22):
        i = wcnt[0] % 2
        wcnt[0] += 1
        P.dma("sync" if i == 0 else "gpsimd", out=wst[i].full(), in_=wdv[:, k0, :])
        P.pool.tensor_copy(out=Wd[:, k0, :], in_=wst[i].full())

    xst = P.sb("xst", [128, 8, 512], F32)
    hn = P.sb("hn", [128, 8, 512], BF16)
    sq = P.sb("sq", [128, 512], F32)
    rstd = P.sb("rstd", [128, 512], F32)
    act = P.sb("act", [128, 22, 512], BF16)
    upre = [P.sb(f"upre{i}", [128, 516], F32) for i in range(2)]
    acc = [P.sb(f"acc{i}", [128, 512], F32) for i in range(2)]
    sg = P.sb("sg", [128, 512], F32)
    carry = P.sb("carry", [128, 44, 4], F32)
    xo = [P.sb(f"xo{i}", [128, 512], F32) for i in range(2)]
    xTv = xT.full().re("(kc p) n -> p kc n", p=128)
    tiles = [(0, HALO)] + [(HALO + i * 512, 512) for i in range(NT)]
    pcnt = [0]
    for tix, (c0, w) in enumerate(tiles):
        P.dma("sync", out=xst[:, :, 0:w], in_=xTv[:, :, c0:c0 + w])
        ps = pb[7]
        for kc in range(8):
            P.act.activation(out=sq[:, 0:w], in_=xst[:, kc, 0:w], func=AF.Square)
            P.pe.matmul(out=ps[:, 0:w], lhsT=ones.full(), rhs=sq[:, 0:w], start=(kc == 0), stop=(kc == 7))
        P.act.activation(out=rstd[:, 0:w], in_=ps[:, 0:w], func=AF.Sqrt, bias=eps[:, 0:1], scale=1.0 / 1024.0)
        P.dve.reciprocal(out=rstd[:, 0:w], in_=rstd[:, 0:w])
        for kc in range(8):
            P.dve.scalar_tensor_tensor(out=hn[:, kc, 0:w], in0=xst[:, kc, 0:w], scalar=C.col("g_ffn", kc),
                                       in1=rstd[:, 0:w], op0=ALU.mult, op1=ALU.mult)
        for i in range(22):
            accs = []
            for j, cg in enumerate((i, 22 + i)):
                ps = pb[pcnt[0] % 4]
                pcnt[0] += 1
                for kc in range(8):
                    P.pe.matmul(out=ps[:, 0:w], lhsT=Wu[:, kc, cg * 128:(cg + 1) * 128], rhs=hn[:, kc, 0:w],
                                start=(kc == 0), stop=(kc == 7))
                if tix == 0:
                    P.act.copy(out=carry[:, cg, :], in_=ps[:, 0:HALO])
                    continue
                up = upre[j]
                P.act.copy(out=up[:, 4:516], in_=ps.full())
                P.dve.tensor_copy(out=up[:, 0:4], in_=carry[:, cg, :])
                P.pool.tensor_copy(out=carry[:, cg, :], in_=up[:, 512:516])
                a0 = acc[j]
                P.dve.tensor_scalar(out=a0.full(), in0=up[:, 4:516], scalar1=C.col("fw2", cg), scalar2=C.col("fb", cg),
                                    op0=ALU.mult, op1=ALU.add)
                P.dve.scalar_tensor_tensor(out=a0.full(), in0=up[:, 3:515], scalar=C.col("fw1", cg), in1=a0.full(),
                                           op0=ALU.mult, op1=ALU.add)
                P.dve.scalar_tensor_tensor(out=a0.full(), in0=up[:, 2:514], scalar=C.col("fw0", cg), in1=a0.full(),
                                           op0=ALU.mult, op1=ALU.add)
                accs.append(a0)
            if tix == 0:
                continue
            P.act.activation(out=sg.full(), in_=accs[0].full(), func=AF.Silu)
            P.pool.tensor_tensor(out=act[:, i, :], in0=sg.full(), in1=accs[1].full(), op=ALU.mult)
        if tix == 0:
            continue
        ti = tix - 1
        for oc in range(8):
            i2 = oc % 2
            ps = pb[4 + i2]
            for i in range(22):
                P.pe.matmul(out=ps.full(), lhsT=Wd[:, i, oc * 128:(oc + 1) * 128], rhs=act[:, i, :],
                            start=(i == 0), stop=(i == 21))
            P.dve.tensor_tensor(out=xo[i2].full(), in0=ps.full(), in1=xst[:, oc, :], op=ALU.add)
            P.dma("sync" if i2 else "gpsimd", out=o_x[oc * 128:(oc + 1) * 128, ti * 512:(ti + 1) * 512], in_=xo[i2].full())
    P.emit()
    return nc, P


def run_D1(inp, l, resA, om, of, y, x_full):
    if "D1" not in _PROG_CACHE:
        _PROG_CACHE["D1"] = build_D1()[0]
    nc = _PROG_CACHE["D1"]
    cst = np.ascontiguousarray(inp["ssm_norm_g"][l].reshape(8, 128).T)
    xf = x_full.reshape(16384, D)
    in_maps = []
    for c in range(8):
        b, q = c // 4, c % 4
        ts = slice(q * T, (q + 1) * T)
        in_maps.append({
            "omT": np.ascontiguousarray(om[b][:, ts]), "ofT": np.ascontiguousarray(of[b][:, ts]),
            "yT": np.ascontiguousarray(y[b][:, ts]), "szT": np.asarray(resA[c]["o_sz"]),
            "gT": np.asarray(resA[c]["o_g"]), "xT": np.ascontiguousarray(xf[c * T:(c + 1) * T].T),
            "w_a": np.ascontiguousarray(inp["w_br_mla"][l]), "w_b": np.ascontiguousarray(inp["w_br_fox"][l]),
            "w_c": np.ascontiguousarray(inp["w_br_ssm"][l]), "w_o": np.ascontiguousarray(inp["w_out"][l]),
            "cst": cst,
        })
    res = run_bass_kernel_spmd(nc, in_maps, core_ids=list(range(8))).results
    xm = np.concatenate([np.asarray(r["o_x"]).T for r in res], axis=0)
    return xm.reshape(2, S_, D)


def run_D2(inp, l, xm_full):
    cp = d2_colpack(inp, l)
    off = dict(cp.off)
    off["_n"] = cp.n
    if "D2" not in _PROG_CACHE:
        _PROG_CACHE["D2"] = build_D2(off)[0]
    nc = _PROG_CACHE["D2"]
    cst = cp.array()
    xf = xm_full.reshape(16384, D)
    in_maps = []
    for c in range(8):
        t0 = c * T
        xt = np.zeros((D, T + HALO), np.float32)
        xt[:, HALO:] = xf[t0:t0 + T].T
        if c % 4 != 0:
            xt[:, 0:HALO] = xf[t0 - HALO:t0].T
        in_maps.append({"xT": np.ascontiguousarray(xt), "w_up": np.ascontiguousarray(inp["ffn_w_up"][l]),
                        "w_dn": np.ascontiguousarray(inp["ffn_w_down"][l]), "cst": cst})
    res = run_bass_kernel_spmd(nc, in_maps, core_ids=list(range(8))).results
    xo = np.concatenate([np.asarray(r["o_x"]).T for r in res], axis=0)
    return xo.reshape(2, S_, D)


def kernel_unfused(**inp):
    inp = {k: np.asarray(v) for k, v in inp.items()}
    x = inp["x"].astype(np.float32)
    pos = inp["positions"]
    for l in range(2):
        resA = run_A(inp, l, x, pos)
        om, of, y = run_BC(inp, l, resA)
        xm = run_D1(inp, l, resA, om, of, y, x)
        x = run_D2(inp, l, xm)
    return np.ascontiguousarray(x.astype(np.float32))


def kernel(**inp):
    return kernel_fused(**inp)


SW = 516
RG = [[0, 1, 2, 3], [4, 5, 6, 7]]
KT_L = T // 128


def fused_rowpack(inp, l):
    r = np.concatenate([inp["fox_b_f"][l], inp["ssm_dt_bias"][l], inp["ssm_A_log"][l], inp["ssm_D"][l]]).astype(np.float32)
    return np.ascontiguousarray(np.broadcast_to(r[None, :], (128, r.size)))


def build_fused(offA, offD2, stop=None, dbg=()):
    nc = new_nc()
    P = Prog(nc)
    EI, EO = "ExternalInput", "ExternalOutput"
    L = 2
    x0 = P.dram("x0", [D, 4 * SW], F32, EI)
    pos = P.dram("pos", [1, T], I32, EI)
    w_in = P.dram("w_in", [L, D, 7864], F32, EI)
    w_uq = P.dram("w_uq", [L, 384, 768], F32, EI)
    w_kp = P.dram("w_kp", [L, 256, 768], F32, EI)
    w_v = P.dram("w_v", [L, 256, 512], F32, EI)
    w_a = P.dram("w_a", [L, 512, D], F32, EI)
    w_b = P.dram("w_b", [L, 512, D], F32, EI)
    w_c = P.dram("w_c", [L, 1024, D], F32, EI)
    w_o = P.dram("w_o", [L, 1024, D], F32, EI)
    w_up = P.dram("w_up", [L, D, 5632], F32, EI)
    w_dn = P.dram("w_dn", [L, 2816, D], F32, EI)
    cstA_d = P.dram("cstA", [L, 128, offA["_n"]], F32, EI)
    cstD_d = P.dram("cstD", [L, 128, offD2["_n"]], F32, EI)
    gssm_d = P.dram("gssm", [L, 128, 8], F32, EI)
    rowc_d = P.dram("rowc", [L, 128, 56], F32, EI)
    sel_d = P.dram("sel", [128, 32], F32, EI)
    msk_d = P.dram("msk", [128, 8, 512], F32, EI)
    cm_d = P.dram("cm", [128, 4, 128], F32, EI)
    mats_d = P.dram("mats", [128, 192], F32, EI)
    out = P.dram("out", [D, T], F32, EO)
    xb = [x0, P.dram("xb1", [D, 4 * SW], F32)]
    xmid = P.dram("xmid", [D, 4 * SW], F32)
    qm = P.dram("qm", [8, 96, T], BF16)
    qf = P.dram("qf", [8, 64, T], BF16)
    fq = P.dram("fq", [8, 3, T], BF16)
    szd = P.dram("szd", [1024, T], BF16)
    gd = P.dram("gd", [3072, T], BF16)
    xtm = P.dram("xtm", [128, KT_L, 1024], BF16)
    btm = P.dram("btm", [128, KT_L, 256], BF16)
    bct = P.dram("bct", [512, T], BF16)
    dtd = P.dram("dtd", [128, KT_L, 16], F32)
    atd = P.dram("atd", [128, KT_L, 16], F32)
    omd = P.dram("omd", [512, T], BF16)
    ofd = P.dram("ofd", [512, T], BF16)
    yd = P.dram("yd", [1024, T], F32)
    kxm = [P.dram(f"kxm{m}", [768, 512], BF16) for m in range(4)]
    kxmg = [P.dram(f"kxmg{m}", [4 * 768, 512], BF16) for m in range(4)]
    kxf = [P.dram(f"kxf{m}", [640, 512], BF16) for m in range(4)]
    kxfg = [P.dram(f"kxfg{m}", [4 * 640, 512], BF16) for m in range(4)]
    nfq = P.dram("nfq", [8, 3, T], BF16)
    vx = [P.dram(f"vx{m}", [2048, 256], BF16) for m in range(4)]
    vxg = [P.dram(f"vxg{m}", [4 * 2048, 256], BF16) for m in range(4)]
    sx = [P.dram(f"sx{i}", [256, 1024], F32) for i in range(2)]
    sxg = [P.dram(f"sxg{i}", [4 * 256, 1024], F32) for i in range(2)]
    fx = P.dram("fx", [128, 224], F32)
    fxg = P.dram("fxg", [4 * 128, 224], F32)
    tx = P.dram("tx", [128, 128], F32)
    txg = P.dram("txg", [4 * 128, 128], F32)
    wub = P.dram("wub", [D, 5632], BF16)
    wdb = P.dram("wdb", [2816, D], BF16)
    wab = P.dram("wab", [512, D], BF16)
    wbb = P.dram("wbb", [512, D], BF16)
    wcb = P.dram("wcb", [1024, D], BF16)
    wob = P.dram("wob", [1024, D], BF16)
    dbg_out = {}

    def gather_pairs(pairs, after=None):
        for (a, b) in pairs:
            kw = {}
            if after is not None:
                kw["_reads"] = [after]
            P.pool.collective_compute(kind="AllGather", op=ALU.bypass, replica_groups=RG,
                                      ins=[a.full().re("(p a) c -> p (a c)", p=128)],
                                      outs=[b.full().re("(q a) c -> q (a c)", q=512)], **kw)

    def load_consts():
        d = {}
        d["cm"] = P.sb("cmb", [128, 4, 128], F32)
        P.dma("sync", out=d["cm"].full(), in_=cm_d.full())
        d["sel"] = P.sb("selb", [128, 32], F32)
        P.dma("sync", out=d["sel"].full(), in_=sel_d.full())
        d["eps"] = P.sb("eps", [128, 1], F32)
        P.dve.memset(ap=d["eps"].full(), constant=1e-6)
        d["one1"] = P.sb("one1", [128, 1], F32)
        P.dve.memset(ap=d["one1"].full(), constant=1.0)
        d["zero"] = P.sb("zero", [128, 1], F32)
        P.dve.memset(ap=d["zero"].full(), constant=0.0)
        return d

    def phase_A(l):
        K = load_consts()
        cmb = K["cm"]
        tri, ident, ones = cmb[:, 0, :], cmb[:, 2, :], cmb[:, 3, :]
        eps, one1 = K["eps"], K["one1"]
        xin = xb[l]
        cstb = P.sb("cstb", [128, offA["_n"]], F32)
        C = Cst(P, cstb, offA)
        P.dma("sync", out=cstb.full(), in_=cstA_d[l])
        rowc = P.sb("rowc", [128, 56], F32)
        P.dma("sync", out=rowc.full(), in_=rowc_d[l])
        matf = P.sb("matf", [128, 192], F32)
        matb = P.sb("matb", [128, 192], BF16)
        P.dma("sync", out=matf.full(), in_=mats_d.full())
        P.dve.tensor_copy(out=matb.full(), in_=matf.full())
        prh = matb[0:96, 0:96]
        selm = matb[0:32, 96:192]
        identb = P.sb("identb", [128, 128], BF16)
        P.dve.tensor_copy(out=identb.full(), in_=ident)
        Aneg_r = P.sb("Aneg_r", [128, 16], F32)
        P.act.activation(out=Aneg_r.full(), in_=rowc[:, 24:40], func=AF.Exp)
        P.dve.tensor_scalar(out=Aneg_r.full(), in0=Aneg_r.full(), scalar1=-1.0, scalar2=None, op0=ALU.mult)

        pb = [P.ps(f"pb{i}", [128, 512], F32) for i in range(7)]
        pbt = P.ps("pbt", [128, 1024], BF16)
        pbi = {}

        def nxt_ps(lo=0, hi=4):
            i = pbi.get(lo, 0)
            pbi[lo] = (i + 1) % (hi - lo)
            return pb[lo + i]

        Ctab = P.sb("Ctab", [96, T], F32)
        Stab = P.sb("Stab", [96, T], F32)
        hraw = P.sb("hraw", [96, 512], F32)
        hsq = P.sb("hsq", [96, 512], F32)
        hrs = P.sb("hrs", [96, 512], F32)
        hnf = P.sb("hnf", [96, 512], F32)
        hnb = P.sb("hnb", [96, 512], BF16)
        ht1 = P.sb("ht1", [96, 512], F32)
        ht2 = P.sb("ht2", [96, 512], F32)
        posf, rr_tmp, rr_m = hrs, hraw, hsq

        class _IV:
            def __init__(self, b):
                self.b = b

            def full(self):
                return self.b.full().bitcast(I32)
        posi, rr_i = _IV(ht1), _IV(ht2)

        def sin_table(outv, phase):
            P.dve.tensor_scalar(out=rr_tmp.full(), in0=posf.full(), scalar1=C.col("invf"), scalar2=phase,
                                op0=ALU.mult, op1=ALU.add)
            P.dve.tensor_scalar(out=rr_m.full(), in0=rr_tmp.full(), scalar1=1.0 / (2 * np.pi), scalar2=None, op0=ALU.mult)
            P.dve.tensor_copy(out=rr_i.full(), in_=rr_m.full())
            P.dve.tensor_copy(out=rr_m.full(), in_=rr_i.full())
            P.dve.scalar_tensor_tensor(out=rr_tmp.full(), in0=rr_m.full(), scalar=-2 * np.pi, in1=rr_tmp.full(),
                                       op0=ALU.mult, op1=ALU.add)
            P.dve.tensor_scalar(out=rr_m.full(), in0=rr_tmp.full(), scalar1=np.pi, scalar2=-2 * np.pi, op0=ALU.is_gt, op1=ALU.mult)
            P.dve.tensor_tensor(out=rr_tmp.full(), in0=rr_tmp.full(), in1=rr_m.full(), op=ALU.add)
            P.dve.tensor_scalar(out=rr_m.full(), in0=rr_tmp.full(), scalar1=-np.pi, scalar2=2 * np.pi, op0=ALU.is_lt, op1=ALU.mult)
            P.dve.tensor_tensor(out=rr_tmp.full(), in0=rr_tmp.full(), in1=rr_m.full(), op=ALU.add)
            P.act.activation(out=outv, in_=rr_tmp.full(), func=AF.Sin)

        for i in range(4):
            P.dma("sync", out=posi.full(), in_=pos[:, i * 512:(i + 1) * 512].f(lambda a: a.partition_broadcast(96)))
            P.dve.tensor_copy(out=posf.full(), in_=posi.full())
            sin_table(Stab[:, i * 512:(i + 1) * 512], 0.0)
            sin_table(Ctab[:, i * 512:(i + 1) * 512], np.pi / 2)
        P.dve.memset(ap=Stab[0:64, :], constant=0.0)
        P.dve.memset(ap=Ctab[0:64, :], constant=1.0)

        hn = P.sb("hn", [128, 8, 4 * SW], BF16)
        xst = P.sb("xst", [128, 8, 512], F32)
        sq = P.sb("sq", [128, 512], F32)
        rstd = P.sb("rstd", [128, 512], F32)
        xTv = xin.full().re("(kc p) n -> p kc n", p=128)

        def rstd_from(ps_view, n_feat, rows, rstd_view):
            P.act.activation(out=rstd_view, in_=ps_view, func=AF.Ln, bias=eps[0:rows, 0:1], scale=1.0 / n_feat)
            P.act.activation(out=rstd_view, in_=rstd_view, func=AF.Exp, scale=-0.5)

        halos = [(m * SW, 4) for m in range(4)]
        main = [(m * SW + 4, 512) for m in range(4)]
        for (c0, w) in halos + main:
            P.dma("sync", out=xst[:, :, 0:w], in_=xTv[:, :, c0:c0 + w])
            ps = nxt_ps(4, 6)
            for kc in range(8):
                P.act.activation(out=sq[:, 0:w], in_=xst[:, kc, 0:w], func=AF.Square)
                P.pe.matmul(out=ps[:, 0:w], lhsT=ones, rhs=sq[:, 0:w], start=(kc == 0), stop=(kc == 7))
            rstd_from(ps[:, 0:w], 1024.0, 128, rstd[:, 0:w])
            for kc in range(8):
                P.dve.scalar_tensor_tensor(out=hn[:, kc, c0:c0 + w], in0=xst[:, kc, 0:w], scalar=C.col("g_mix", kc),
                                           in1=rstd[:, 0:w], op0=ALU.mult, op1=ALU.mult)

        wst = [P.sb(f"wst{i}", [128, 8, 256], F32) for i in range(2)]
        wbf = [P.sb(f"wbf{i}", [128, 8, 512], BF16) for i in range(2)]
        wcnt = [0]
        scnt = [0]
        w_inv = w_in[l].re("(kc p) n -> p kc n", p=128)

        SBv = 672 + 1544
        wplan = [(0, 384), (384, 288), (672, 512), (672 + 512, 512), (672 + 1024, 512), (672 + 1536, 8),
                 (SBv + 1024 + 1536, 16), (SBv, 512), (SBv + 512, 512)]
        wplan += [(SBv + 1024 + b_ * 512, 512) for b_ in range(3)]
        wplan += [(SBv + 2576 + b_ * 512, 512) for b_ in range(6)]
        wpend = {}
        cpend = []

        def w_issue(g):
            c0, ncols = wplan[g]
            lst = []
            for h0 in range(0, ncols, 256):
                n = min(256, ncols - h0)
                si = scnt[0] % 2
                scnt[0] += 1
                P.dma("sync", out=wst[si][:, :, 0:n], in_=w_inv[:, :, c0 + h0:c0 + h0 + n])
                lst.append((si, h0, n))
            wpend[g] = lst

        def load_w(c0, ncols):
            g = wcnt[0]
            wcnt[0] += 1
            assert wplan[g] == (c0, ncols), (g, wplan[g], c0, ncols)
            i = g % 2
            if g not in wpend:
                w_issue(g)
            lst = wpend.pop(g)
            for (si, h0, n) in lst:
                P.act.copy(out=wbf[i][:, :, h0:h0 + n], in_=wst[si][:, :, 0:n])
            if g + 1 < len(wplan):
                w_issue(g + 1)
            if cpend:
                gather_pairs([cpend.pop(0)], after=wbf[i].full())
            return wbf[i]

        def proj(wb, wc0, mcols, c0, w, ps_view):
            for kc in range(8):
                P.pe.matmul(out=ps_view, lhsT=wb[:, kc, wc0:wc0 + mcols], rhs=hn[:, kc, c0:c0 + w],
                            start=(kc == 0), stop=(kc == 7))

        def proj_tm(wb, wc0, ncols, tok0, ps_view):
            for kc in range(8):
                P.pe.matmul(out=ps_view, lhsT=hn[:, kc, tok0:tok0 + 128], rhs=wb[:, kc, wc0:wc0 + ncols],
                            start=(kc == 0), stop=(kc == 7))

        ostg_cnt = [0]
        ostg = [P.sb(f"ostg{i}", [128, 512], BF16) for i in range(4)]

        def next_ostg():
            i = ostg_cnt[0] % 4
            ostg_cnt[0] += 1
            return ostg[i]

        def out_dma(dst_view, src_view):
            P.dma("sync" if ostg_cnt[0] % 2 else "scalar", out=dst_view, in_=src_view)

        hsets = [dict(hraw=hraw.full(), hsq=hsq.full(), hrs=hrs.full(), hnf=hnf.full(), hnb=hnb.full(),
                      ht1=ht1.full(), ht2=ht2.full())]
        hnb1 = P.sb("hnb1", [96, 512], BF16)
        hsets.append(dict(hraw=xst[0:96, 0, :].k(0), hsq=xst[0:96, 1, :].k(1), hrs=xst[0:96, 2, :].k(2),
                          hnf=xst[0:96, 3, :].k(3), hnb=hnb1.full(), ht1=xst[0:96, 4, :].k(4), ht2=xst[0:96, 5, :].k(5)))
        hb2 = P.sb("hb2", [96, 6, 512], F32)
        hnb2 = P.sb("hnb2", [96, 512], BF16)
        hsets.append(dict(hraw=hb2[:, 0, :].k(0), hsq=hb2[:, 1, :].k(1), hrs=hb2[:, 2, :].k(2),
                          hnf=hb2[:, 3, :].k(3), hnb=hnb2.full(), ht1=hb2[:, 4, :].k(4), ht2=hb2[:, 5, :].k(5)))
        hcnt = [0]

        def headnorm(projfn, d, gain_col, rope, tok0, dst_view):
            H = hsets[hcnt[0] % 3]
            hcnt[0] += 1
            ps_view = projfn()
            P.act.activation(out=H["hsq"][0:d, :], in_=ps_view, func=AF.Square)
            P.act.copy(out=H["hraw"][0:d, :], in_=ps_view)
            yield
            ps2 = nxt_ps(4, 6)
            P.pe.matmul(out=ps2[0:d, :], lhsT=cmb[0:d, 3, 0:d], rhs=H["hsq"][0:d, :], start=True, stop=True)
            rstd_from(ps2[0:d, :], float(d), d, H["hrs"][0:d, :])
            og = next_ostg()
            if not rope:
                P.dve.scalar_tensor_tensor(out=og[0:d, :], in0=H["hraw"][0:d, :], scalar=gain_col, in1=H["hrs"][0:d, :],
                                           op0=ALU.mult, op1=ALU.mult)
            else:
                P.dve.scalar_tensor_tensor(out=H["hnf"][0:d, :], in0=H["hraw"][0:d, :], scalar=gain_col, in1=H["hrs"][0:d, :],
                                           op0=ALU.mult, op1=ALU.mult)
                P.act.copy(out=H["hnb"][0:d, :], in_=H["hnf"][0:d, :])
                yield
                ps3 = nxt_ps(6, 7)
                P.pe.matmul(out=ps3[0:d, :], lhsT=prh, rhs=H["hnb"][0:d, :], start=True, stop=True)
                P.dve.tensor_tensor(out=H["ht1"][0:d, :], in0=H["hnf"][0:d, :], in1=Ctab[0:d, tok0:tok0 + 512], op=ALU.mult)
                P.dve.tensor_tensor(out=H["ht2"][0:d, :], in0=ps3[0:d, :], in1=Stab[0:d, tok0:tok0 + 512], op=ALU.mult)
                P.pool.tensor_tensor(out=og[0:d, :], in0=H["ht1"][0:d, :], in1=H["ht2"][0:d, :], op=ALU.add)
            out_dma(dst_view, og[0:d, :])

        def run_pipe(gens, depth=3):
            gens = iter(gens)
            active = []
            while True:
                started = False
                if len(active) < depth:
                    g = next(gens, None)
                    if g is not None:
                        started = True
                        try:
                            next(g)
                            active.append(g)
                        except StopIteration:
                            pass
                if not active and not started:
                    break
                olds = active[:-1] if (started and active) else list(active)
                for g in olds:
                    try:
                        next(g)
                    except StopIteration:
                        active.remove(g)

        lat = P.sb("lat", [128, 3, 512], F32)
        latn = P.sb("latn", [128, 3, 512], BF16)

        def latent_norm(ps_list, gname):
            nch = len(ps_list)
            ps2 = nxt_ps(4, 6)
            for i, psv in enumerate(ps_list):
                P.act.activation(out=sq.full(), in_=psv, func=AF.Square)
                P.act.copy(out=lat[:, i, :], in_=psv)
                P.pe.matmul(out=ps2.full(), lhsT=ones, rhs=sq.full(), start=(i == 0), stop=(i == nch - 1))
            rstd_from(ps2.full(), 128.0 * nch, 128, rstd.full())
            for i in range(nch):
                P.dve.scalar_tensor_tensor(out=latn[:, i, :], in0=lat[:, i, :], scalar=C.col(gname, i), in1=rstd.full(),
                                           op0=ALU.mult, op1=ALU.mult)

        def small_w(name, dram_l, kc_n, ncols, i):
            bfb = P.sb(name, [128, kc_n, ncols], BF16)
            dv = dram_l.re("(kc p) n -> p kc n", p=128)
            for kc in range(kc_n):
                si = scnt[0] % 2
                scnt[0] += 1
                stg = wst[si].full().re("p a b -> p (a b)")[:, 0:ncols]
                P.dma("sync", out=stg, in_=dv[:, kc, :])
                P.act.copy(out=bfb[:, kc, :], in_=stg)
            return bfb

        uqb = small_w("uqb", w_uq[l], 3, 768, 0)
        kpb = small_w("kpb", w_kp[l], 2, 768, 1)
        wvb = small_w("wvb", w_v[l], 2, 512, 0)

        vstg = [P.sb(f"vstg{i}", [128, 512], BF16) for i in range(2)]
        vcnt = [0]

        def v_out(kind, ktl, ps_view):
            vs = vstg[vcnt[0] % 2]
            vcnt[0] += 1
            P.act.copy(out=vs.full(), in_=ps_view)
            P.dma("sync" if vcnt[0] % 2 else "scalar",
                  out=vx[ktl // 4][kind * 1024:(kind + 1) * 1024, (ktl % 4) * 64:(ktl % 4 + 1) * 64].re("(h p) d -> p h d", p=128),
                  in_=vs.full().re("p (h d) -> p h d", h=8))

        wb = load_w(0, 384)
        for m, (c0, w) in enumerate(main):
            pss = []
            for ch in range(3):
                ps = nxt_ps(0, 4)
                proj(wb, ch * 128, 128, c0, 512, ps.full())
                pss.append(ps.full())
            latent_norm(pss, "g_cq")
            def mkq(h):
                def f():
                    ps = nxt_ps(0, 4)
                    for kc in range(3):
                        P.pe.matmul(out=ps[0:96, :], lhsT=uqb[:, kc, h * 96:(h + 1) * 96], rhs=latn[:, kc, :],
                                    start=(kc == 0), stop=(kc == 2))
                    return ps[0:96, :]
                return f
            run_pipe(headnorm(mkq(h), 96, C.col("g_q"), True, m * 512, qm[h, :, m * 512:(m + 1) * 512]) for h in range(8))
        wb = load_w(384, 288)
        krb = P.sb("krb", [32, 512], BF16)
        for m, (c0, w) in enumerate(main):
            pss = []
            for ch in range(2):
                ps = nxt_ps(0, 4)
                proj(wb, ch * 128, 128, c0, 512, ps.full())
                pss.append(ps.full())
            ps = nxt_ps(0, 4)
            proj(wb, 256, 32, c0, 512, ps[0:32, :])
            P.act.copy(out=krb.full(), in_=ps[0:32, :])
            latent_norm(pss, "g_ckv")
            def mkk(h):
                def f():
                    ps = nxt_ps(0, 4)
                    for kc in range(2):
                        P.pe.matmul(out=ps[0:96, :], lhsT=kpb[:, kc, h * 96:(h + 1) * 96], rhs=latn[:, kc, :],
                                    start=(kc == 0), stop=False)
                    P.pe.matmul(out=ps[0:96, :], lhsT=selm, rhs=krb.full(), start=False, stop=True)
                    return ps[0:96, :]
                return f
            run_pipe(headnorm(mkk(h), 96, C.col("g_k"), True, m * 512, kxm[m][h * 96:(h + 1) * 96, :]) for h in range(8))
            for j in range(4):
                ps = nxt_ps(0, 4)
                for kc in range(2):
                    P.pe.matmul(out=ps.full(), lhsT=latn[:, kc, j * 128:(j + 1) * 128], rhs=wvb[:, kc, :],
                                start=(kc == 0), stop=(kc == 1))
                v_out(0, m * 4 + j, ps.full())
        for (base, gname, isq) in ((672, "g_fq", True), (672 + 512, "g_fk", False)):
            wb = load_w(base, 512)
            def mkf(wb_, h, c0):
                def f():
                    ps = nxt_ps(0, 4)
                    proj(wb_, h * 64, 64, c0, 512, ps[0:64, :])
                    return ps[0:64, :]
                return f
            gl = []
            for m, (c0, w) in enumerate(main):
                for h in range(8):
                    dst = qf[h, :, m * 512:(m + 1) * 512] if isq else kxf[m][h * 64:(h + 1) * 64, :]
                    gl.append(headnorm(mkf(wb, h, c0), 64, C.col(gname), False, m * 512, dst))
            run_pipe(gl)
        wb = load_w(672 + 1024, 512)
        for m, (c0, w) in enumerate(main):
            for j in range(4):
                ps = nxt_ps(0, 4)
                proj_tm(wb, 0, 512, c0 + j * 128, ps.full())
                v_out(1, m * 4 + j, ps.full())
        cpend.extend(list(zip(kxm, kxmg)) + list(zip(vx, vxg)))
        FB = 672 + 1536
        SB = 672 + 1544
        lf_tm = P.sb("lf_tm", [128, KT_L, 8], F32)
        dt_tm = P.sb("dt_tm", [128, KT_L, 16], F32)
        a_tm = P.sb("a_tm", [128, KT_L, 16], F32)
        tmpr = P.sb("tmpr", [128, 16], F32)
        wf = load_w(FB, 8)
        for m, (c0, w) in enumerate(main):
            for j in range(4):
                kt = m * 4 + j
                ps = nxt_ps(0, 4)
                proj_tm(wf, 0, 8, c0 + j * 128, ps[:, 0:8])
                P.dve.tensor_tensor(out=tmpr[:, 0:8], in0=ps[:, 0:8], in1=rowc[:, 0:8], op=ALU.add)
                P.act.activation(out=tmpr[:, 0:8], in_=tmpr[:, 0:8], func=AF.Exp, scale=-1.0)
                P.act.activation(out=tmpr[:, 0:8], in_=tmpr[:, 0:8], func=AF.Ln, bias=one1[:, 0:1], scale=1.0)
                P.dve.tensor_scalar(out=lf_tm[:, kt, :], in0=tmpr[:, 0:8], scalar1=-1.0, scalar2=None, op0=ALU.mult)
        wd = load_w(SB + 1024 + 1536, 16)
        for m, (c0, w) in enumerate(main):
            for j in range(4):
                kt = m * 4 + j
                ps = nxt_ps(0, 4)
                proj_tm(wd, 0, 16, c0 + j * 128, ps[:, 0:16])
                P.dve.tensor_tensor(out=tmpr.full(), in0=ps[:, 0:16], in1=rowc[:, 8:24], op=ALU.add)
                P.act.activation(out=tmpr.full(), in_=tmpr.full(), func=AF.Exp)
                P.act.activation(out=dt_tm[:, kt, :], in_=tmpr.full(), func=AF.Ln, bias=one1[:, 0:1], scale=1.0)
                P.dve.tensor_tensor(out=a_tm[:, kt, :], in0=dt_tm[:, kt, :], in1=Aneg_r.full(), op=ALU.mult)
        P.dma("sync", out=dtd.full(), in_=dt_tm.full())
        P.dma("sync", out=atd.full(), in_=a_tm.full())
        fxs = P.sb("fxs", [128, 224], F32)
        within = P.sb("within", [128, KT_L, 8], F32)
        ttot = P.sb("ttot", [128, KT_L, 8], F32)
        f2 = lambda b: b.full().re("p a b -> p (a b)")
        ps = nxt_ps(0, 4)
        P.pe.matmul(out=ps[:, 0:128], lhsT=tri, rhs=f2(lf_tm), start=True, stop=True)
        P.act.copy(out=f2(within), in_=ps[:, 0:128])
        ps = nxt_ps(0, 4)
        P.pe.matmul(out=ps[:, 0:128], lhsT=ones, rhs=f2(lf_tm), start=True, stop=True)
        P.act.copy(out=f2(ttot), in_=ps[:, 0:128])
        Floc = fxs[:, 0:128].re("p (a b) -> p a b", b=8)
        totv = fxs[:, 128:160].re("p (a b) -> p a b", b=8)
        cacc = P.sb("cacc", [128, 8], F32)
        for m in range(4):
            P.dve.tensor_copy(out=Floc[:, 4 * m, :], in_=within[:, 4 * m, :])
            P.dve.tensor_copy(out=cacc.full(), in_=ttot[:, 4 * m, :])
            for j in range(1, 4):
                P.dve.tensor_tensor(out=Floc[:, 4 * m + j, :], in0=within[:, 4 * m + j, :], in1=cacc.full(), op=ALU.add)
                P.dve.tensor_tensor(out=cacc.full(), in0=cacc.full(), in1=ttot[:, 4 * m + j, :], op=ALU.add)
            P.dve.tensor_copy(out=totv[:, m, :], in_=cacc.full())
        ps = nxt_ps(0, 4)
        P.pe.transpose(out=ps[:, 0:128], in_=fxs[:, 0:128], identity=ident)
        FT = P.sb("FT", [128, 128], F32)
        r1 = P.sb("r1", [128, 128], F32)
        fh = [P.sb(f"fh{i}", [128, 128], BF16) for i in range(3)]
        P.act.copy(out=FT.full(), in_=ps[:, 0:128])
        P.dve.tensor_copy(out=fh[0].full(), in_=FT.full())
        P.dve.tensor_tensor(out=r1.full(), in0=FT.full(), in1=fh[0].full(), op=ALU.subtract)
        P.dve.tensor_copy(out=fh[1].full(), in_=r1.full())
        P.dve.tensor_tensor(out=r1.full(), in0=r1.full(), in1=fh[1].full(), op=ALU.subtract)
        P.dve.tensor_copy(out=fh[2].full(), in_=r1.full())
        nfh = [P.sb(f"nfh{i}", [128, 128], BF16) for i in range(3)]
        for r in range(3):
            P.dve.tensor_scalar(out=nfh[r].full(), in0=fh[r].full(), scalar1=-1.0, scalar2=None, op0=ALU.mult)
            for kt in range(KT_L):
                P.dma("sync" if kt % 2 else "scalar", out=fq[:, r, kt * 128:(kt + 1) * 128], in_=fh[r][kt * 8:(kt + 1) * 8, :])
                P.dma("scalar" if kt % 2 else "sync", out=nfq[:, r, kt * 128:(kt + 1) * 128], in_=nfh[r][kt * 8:(kt + 1) * 8, :])
        for m in range(4):
            P.dma("sync", out=kxf[m][512:536, :].re("(h r) c -> h r c", r=3), in_=nfq[:, :, m * 512:(m + 1) * 512])
        cpend.extend(list(zip(kxf, kxfg)))
        def plain_group(base, ncols, func, bias_name, dst, dst_row0):
            wb_ = load_w(base, ncols)
            for m, (c0, w) in enumerate(main):
                for ch in range(ncols // 128):
                    ps = nxt_ps(0, 4)
                    proj(wb_, ch * 128, 128, c0, 512, ps.full())
                    og = next_ostg()
                    if bias_name is None:
                        P.act.activation(out=og.full(), in_=ps.full(), func=func)
                    else:
                        P.act.activation(out=og.full(), in_=ps.full(), func=func,
                                         bias=C.col(bias_name, (dst_row0 // 128) + ch))
                    out_dma(dst[dst_row0 + ch * 128:dst_row0 + (ch + 1) * 128, m * 512:(m + 1) * 512], og.full())

        for blk in range(2):
            plain_group(SB + blk * 512, 512, AF.Silu, None, szd, blk * 512)
        upre = P.sb("upre", [128, 516], F32)
        carry = P.sb("carry", [128, 4], F32)
        acc0 = P.sb("acc0", [128, 512], F32)
        tstg = [P.sb(f"tstg{i}", [128, 4, 128], BF16) for i in range(2)]
        tcnt = [0]
        trq = []
        for blk in range(3):
            wb = load_w(SB + 1024 + blk * 512, 512)
            for m, (c0, w) in enumerate(main):
                for ch in range(4):
                    cg = blk * 4 + ch
                    ps = nxt_ps(0, 4)
                    proj(wb, ch * 128, 128, c0 - 4, 4, ps[:, 0:4])
                    P.act.copy(out=upre[:, 0:4], in_=ps[:, 0:4])
                    ps = nxt_ps(0, 4)
                    proj(wb, ch * 128, 128, c0, 512, ps.full())
                    while len(trq) > 1:
                        trq.pop(0)()
                    P.act.copy(out=upre[:, 4:516], in_=ps.full())
                    P.act.activation(out=acc0.full(), in_=ps.full(), func=AF.Identity, scale=C.col("cw3", cg), bias=C.col("cb", cg))
                    for k in range(3):
                        P.dve.scalar_tensor_tensor(out=acc0.full(), in0=upre[:, 1 + k:513 + k], scalar=C.col(f"cw{k}", cg),
                                                   in1=acc0.full(), op0=ALU.mult, op1=ALU.add)
                    og = next_ostg()
                    P.act.activation(out=og.full(), in_=acc0.full(), func=AF.Silu)
                    if cg >= 8:
                        out_dma(bct[(cg - 8) * 128:(cg - 7) * 128, m * 512:(m + 1) * 512], og.full())
                    if cg < 10:
                        def mk_tr(og=og, cg=cg, m=m):
                            def f():
                                i2 = tcnt[0] % 2
                                tcnt[0] += 1
                                for j in range(4):
                                    P.pe.transpose(out=pbt[:, i2 * 512 + j * 128:i2 * 512 + (j + 1) * 128],
                                                   in_=og[:, j * 128:(j + 1) * 128], identity=identb.full())
                                ts_ = tstg[i2]
                                P.dve.tensor_copy(out=ts_.full().re("p j f -> p (j f)"), in_=pbt[:, i2 * 512:(i2 + 1) * 512])
                                if cg < 8:
                                    P.dma("sync", out=xtm[:, m * 4:(m + 1) * 4, cg * 128:(cg + 1) * 128], in_=ts_.full())
                                else:
                                    P.dma("sync", out=btm[:, m * 4:(m + 1) * 4, (cg - 8) * 128:(cg - 7) * 128], in_=ts_.full())
                            return f
                        trq.append(mk_tr())
        while trq:
            trq.pop(0)()
        GB = SB + 2576
        for blk in range(6):
            plain_group(GB + blk * 512, 512, AF.Sigmoid, "b_gate", gd, blk * 512)
        P.dma("sync", out=fx[:, 0:160], in_=fxs[:, 0:160])
        gather_pairs(cpend)
        del cpend[:]

    def ssd_scan(l, K, pass1, fxs=None, dt_tm=None, a_tm=None, Hinit=None, rowc=None, pb=None):
        cmb = K["cm"]
        tri, trimask, ones = cmb[:, 0, :], cmb[:, 1, :], cmb[:, 3, :]
        if pb is None:
            pb = [P.ps(f"spb{i}", [128, 512], F32) for i in range(7)]
        if pass1:
            fxs = P.sb("decs", [128, 224], F32)
        if dt_tm is None:
            dt_tm = P.sb("dt_tm", [128, KT_L, 16], F32)
            a_tm = P.sb("a_tm", [128, KT_L, 16], F32)
            P.dma("sync", out=dt_tm.full(), in_=dtd.full())
            P.dma("sync", out=a_tm.full(), in_=atd.full())
        fl = lambda b: b.full().re("p c h -> p (c h)")
        Acum = P.sb("Acum", [128, KT_L, 16], F32)
        Atot = P.sb("Atot", [128, KT_L, 16], F32)
        wdec = P.sb("wdec", [128, KT_L, 16], F32)
        eAtot = P.sb("eAtot", [128, KT_L, 16], F32)
        psA = pb[0]
        P.pe.matmul(out=psA[:, 0:256], lhsT=tri, rhs=fl(a_tm), start=True, stop=True)
        P.act.copy(out=fl(Acum), in_=psA[:, 0:256])
        P.pe.matmul(out=psA[:, 256:512], lhsT=ones, rhs=fl(a_tm), start=True, stop=True)
        P.act.copy(out=fl(Atot), in_=psA[:, 256:512])
        P.act.activation(out=fl(eAtot), in_=fl(Atot), func=AF.Exp)
        P.dve.tensor_tensor(out=fl(wdec), in0=fl(Atot), in1=fl(Acum), op=ALU.subtract)
        P.act.activation(out=fl(wdec), in_=fl(wdec), func=AF.Exp)
        if not pass1:
            nAcum = P.sb("nAcum", [128, KT_L, 16], F32)
            eA = P.sb("eA", [128, KT_L, 16], F32)
            P.dve.tensor_scalar(out=fl(nAcum), in0=fl(Acum), scalar1=-1.0, scalar2=None, op0=ALU.mult)
            P.act.activation(out=fl(eA), in_=fl(Acum), func=AF.Exp)
            BCs = P.sb("BCs", [128, 4, T], BF16)
            P.dma("gpsimd", out=BCs.full(), in_=bct.full().re("(a p) t -> p a t", p=128))
            cb = P.sb("cb", [128, 2, 128], F32)
            NH = 4
            at = [P.sb(f"at{i}", [128, 128], F32) for i in range(NH)]
            tm = [P.sb(f"tm{i}", [128, 128], F32) for i in range(NH)]
            dec = [P.sb(f"dec{i}", [128, 128], F32) for i in range(NH)]
            MT = [P.sb(f"MT{i}", [128, 128], BF16) for i in range(NH)]
            t1 = P.sb("t1", [128, 1024], F32)
            t3 = P.sb("t3", [128, 1024], F32)
            yo = P.sb("yo", [128, 1024], BF16)
            yT = [P.sb(f"yT{i}", [128, 4, 128], F32) for i in range(2)]
        Hs = P.sb("Hs", [128, 1024], F32)
        Hb = P.sb("Hb", [128, 1024], BF16)
        xc = [P.sb(f"xc{i}", [128, 1024], BF16) for i in range(2)]
        Bc = [P.sb(f"Bc{i}", [128, 256], BF16) for i in range(2)]
        xdt = P.sb("xdt", [128, 1024], BF16)
        xdts = P.sb("xdts", [128, 1024], BF16)
        dsum = P.sb("dsum", [128, 16], F32)
        v3 = lambda v: v.re("p (h d) -> p h d", h=16)
        bc3 = lambda v: v.f(lambda a: a.unsqueeze(2).to_broadcast([128, 16, 64]))
        for m in range(4):
            if pass1:
                P.dve.memset(ap=Hs.full(), constant=0.0)
                P.dve.memset(ap=dsum.full(), constant=0.0)
            else:
                P.dve.tensor_copy(out=Hs.full(), in_=Hinit[:, m, :])
                P.act.copy(out=Hb.full(), in_=Hinit[:, m, :])
            for j in range(4):
                c = m * 4 + j
                x_c = xc[c % 2]
                B_c = Bc[c % 2]
                P.dma("sync", out=x_c.full(), in_=xtm[:, c, :])
                P.dma("scalar" if pass1 else "gpsimd", out=B_c.full(), in_=btm[:, c, :])
                P.dve.tensor_tensor(out=v3(xdt.full()), in0=v3(x_c.full()), in1=bc3(dt_tm[:, c, :]), op=ALU.mult)
                (P.dve if pass1 else P.pool).tensor_tensor(out=v3(xdts.full()), in0=v3(xdt.full()), in1=bc3(wdec[:, c, :]), op=ALU.mult)
                if not pass1:
                    cs = slice(c * 128, (c + 1) * 128)
                    ps_cb = pb[1]
                    for g in range(2):
                        P.pe.matmul(out=ps_cb[:, g * 128:(g + 1) * 128], lhsT=BCs[:, g, cs], rhs=BCs[:, 2 + g, cs],
                                    start=True, stop=True)
                    P.act.copy(out=cb.full().re("p a b -> p (a b)"), in_=ps_cb[:, 0:256])
                    ps_off = [pb[2], pb[3]]
                    for g in range(2):
                        P.pe.matmul(out=ps_off[g].full(), lhsT=BCs[:, 2 + g, cs], rhs=Hb[:, g * 512:(g + 1) * 512],
                                    start=True, stop=True)
                    ps_y = [pb[4], pb[5]]
                    def st1(h):
                        i2 = h % NH
                        g = h // 8
                        P.dve.tensor_scalar(out=at[i2].full(), in0=tri, scalar1=a_tm[:, c, h:h + 1], scalar2=None, op0=ALU.mult)
                        ps_A = pb[6]
                        P.pe.matmul(out=ps_A[:, i2 * 128:(i2 + 1) * 128], lhsT=ones, rhs=at[i2].full(), start=True, stop=True)
                        P.dve.tensor_tensor(out=tm[i2].full(), in0=ps_A[:, i2 * 128:(i2 + 1) * 128], in1=trimask, op=ALU.add)
                        P.act.activation(out=dec[i2].full(), in_=tm[i2].full(), func=AF.Exp, bias=nAcum[:, c, h:h + 1], scale=1.0)
                        P.pool.tensor_tensor(out=MT[i2].full(), in0=cb[:, g, :], in1=dec[i2].full(), op=ALU.mult)

                    def st2(h):
                        i2 = h % NH
                        g = h // 8
                        hh = h % 8
                        P.pe.matmul(out=ps_y[g][:, hh * 64:(hh + 1) * 64], lhsT=MT[i2].full(), rhs=xdt[:, h * 64:(h + 1) * 64],
                                    start=True, stop=True)

                    for hq in range(16 + 3):
                        if hq < 16:
                            st1(hq)
                        if hq >= 3:
                            st2(hq - 3)
                    for g in range(2):
                        gs_ = slice(g * 512, (g + 1) * 512)
                        v8 = lambda v: v.re("p (h d) -> p h d", h=8)
                        b8 = lambda v: v.f(lambda a: a.unsqueeze(2).to_broadcast([128, 8, 64]))
                        P.dve.tensor_tensor(out=v8(t1[:, gs_]), in0=v8(ps_off[g].full()), in1=b8(eA[:, c, g * 8:(g + 1) * 8]), op=ALU.mult)
                        P.dve.tensor_tensor(out=t1[:, gs_], in0=t1[:, gs_], in1=ps_y[g].full(), op=ALU.add)
                    P.pool.tensor_tensor(out=v3(t3.full()), in0=v3(x_c.full()), in1=bc3(rowc[:, 40:56]), op=ALU.mult)
                    P.pool.tensor_tensor(out=t3.full(), in0=t1.full(), in1=t3.full(), op=ALU.add)
                    for q4 in range(2):
                        pst = pb[2 + q4]
                        for jj in range(4):
                            fc = q4 * 4 + jj
                            P.pe.transpose(out=pst[:, jj * 128:(jj + 1) * 128], in_=t3[:, fc * 128:(fc + 1) * 128],
                                           identity=cmb[:, 2, :])
                        yt = yT[q4]
                        P.act.copy(out=yt.full().re("p a b -> p (a b)"), in_=pst.full())
                        P.dma("sync", out=yd[q4 * 512:(q4 + 1) * 512, c * 128:(c + 1) * 128].re("(a p) t -> p a t", p=128),
                              in_=yt.full())
                ps_h = [pb[0], pb[1]] if pass1 else [pb[4], pb[5]]
                for g in range(2):
                    P.pe.matmul(out=ps_h[g].full(), lhsT=B_c[:, g * 128:(g + 1) * 128], rhs=xdts[:, g * 512:(g + 1) * 512],
                                start=True, stop=True)
                P.dve.tensor_tensor(out=v3(Hs.full()), in0=v3(Hs.full()), in1=bc3(eAtot[:, c, :]), op=ALU.mult)
                for g in range(2):
                    P.dve.tensor_tensor(out=Hs[:, g * 512:(g + 1) * 512], in0=Hs[:, g * 512:(g + 1) * 512], in1=ps_h[g].full(), op=ALU.add)
                if pass1:
                    P.dve.tensor_tensor(out=dsum.full(), in0=dsum.full(), in1=Atot[:, c, :], op=ALU.add)
                else:
                    P.act.copy(out=Hb.full(), in_=Hs.full())
            if pass1:
                P.dma("sync", out=sx[m // 2][(m % 2) * 128:(m % 2 + 1) * 128, :], in_=Hs.full())
                P.act.activation(out=fxs[:, 160 + m * 16:160 + (m + 1) * 16], in_=dsum.full(), func=AF.Exp)
        if pass1:
            P.dma("sync", out=fx[:, 160:224], in_=fxs[:, 160:224])

    def load_fg():
        fg = P.sb("fg", [128, 4, 224], F32)
        P.dma("sync", out=fg.full(), in_=fxg.full().re("(r p) c -> p r c", p=128))
        return fg

    def phase_attn(l):
        K = load_consts()
        sel, zero = K["sel"], K["zero"]
        mskb = P.sb("mskb", [128, 8, 512], F32)
        P.dma("gpsimd", out=mskb.full(), in_=msk_d.full())
        FP = {}

        def f_prep():
            fg = load_fg()
            offs = P.sb("offs", [128, 16, 8], F32)
            run = P.sb("run", [128, 8], F32)
            P.dve.memset(ap=run.full(), constant=0.0)
            for s_ in range(16):
                m, r = divmod(s_, 4)
                P.dve.tensor_copy(out=offs[:, s_, :], in_=run.full())
                P.dve.tensor_tensor(out=run.full(), in0=run.full(), in1=fg[:, r, 128 + m * 8:128 + (m + 1) * 8], op=ALU.add)
            offown = P.sb("offown", [128, 4, 8], F32)
            P.dve.memset(ap=offown.full(), constant=0.0)
            for m in range(4):
                for r in range(4):
                    P.dve.scalar_tensor_tensor(out=offown[:, m, :], in0=offs[:, 4 * m + r, :], scalar=sel[:, 8 + 4 * m + r:9 + 4 * m + r],
                                               in1=offown[:, m, :], op0=ALU.mult, op1=ALU.add)
            btab = P.sb("btab", [128, 4, 16, 8], F32)
            for m in range(4):
                P.dve.tensor_tensor(out=btab[:, m, :, :], in0=offown[:, m, :].f(lambda a: a.unsqueeze(1).to_broadcast([128, 16, 8])),
                                    in1=offs.full(), op=ALU.subtract)
                for jr in range(4):
                    s_ = 4 * m + jr
                    P.dve.tensor_scalar(out=btab[:, m, s_, :], in0=btab[:, m, s_, :], scalar1=sel[:, 28 + jr:29 + jr], scalar2=None,
                                        op0=ALU.add)
            FP["btab"] = btab

        pss = [P.ps(f"pss{i}", [128, 1024], F32) for i in range(3)]
        pbo = [P.ps(f"pbo{i}", [128, 512], F32) for i in range(2)]
        K_sb = [P.sb(f"K_sb{i}", [96, S_], BF16) for i in range(2)]
        Q_sb = [P.sb(f"Q_sb{i}", [96, T], BF16) for i in range(2)]
        V_sb = [P.sb(f"V_sb{i}", [128, NKT, 128], BF16) for i in range(2)]
        for i in range(2):
            P.dve.memset(ap=V_sb[i][:, :, 64:128], constant=1.0)
        NSB = 3
        LA = 2
        pt = [P.sb(f"pt{i}", [128, 1024], BF16) for i in range(NSB)]
        mt = [P.sb(f"mt{i}", [128, 1024], F32) for i in range(2)]
        rl = P.sb("rl", [128, 512], F32)
        rl2 = P.sb("rl2", [64, 512], F32)
        ot = [P.sb(f"ot{i}", [64, 512], BF16) for i in range(2)]
        cnt = [0, 0, 0]
        heads = [(0, h) for h in range(8)] + [(1, h) for h in range(8)]

        def loads(idx):
            kind, h = heads[idx]
            i = idx % 2
            nd = 96 if kind == 0 else 64
            if kind == 1 and idx in (8, 9):
                P.pool.memset(ap=K_sb[i][64:96, :], constant=0.0)
                P.pool.memset(ap=K_sb[i][64:67, :], constant=8.0)
                P.pool.memset(ap=Q_sb[i][64:96, :], constant=0.0)
                P.pool.memset(ap=Q_sb[i][64:70, :], constant=8.0)
            for r in range(4):
                vr = r * 2048 + kind * 1024 + h * 128
                for m in range(4):
                    s0 = (4 * m + r) * 512
                    if kind == 0:
                        ksrc = kxmg[m][r * 768 + h * 96:r * 768 + (h + 1) * 96, :]
                    else:
                        ksrc = kxfg[m][r * 640 + h * 64:r * 640 + (h + 1) * 64, :]
                        P.dma("gpsimd" if (r + m) % 2 == 0 else "sync", out=K_sb[i][67:70, s0:s0 + 512].k(("Kr", s0)),
                              in_=kxfg[m][r * 640 + 512 + h * 3:r * 640 + 512 + (h + 1) * 3, :])
                    P.dma("sync" if (r + m) % 2 == 0 else "gpsimd", out=K_sb[i][0:nd, s0:s0 + 512].k(("K", s0)), in_=ksrc)
                    g0 = (4 * m + r) * 4
                    P.dma("gpsimd" if (r + m) % 2 == 0 else "sync",
                          out=V_sb[i][:, g0:g0 + 4, 0:64].k(("V", g0)),
                          in_=vxg[m][vr:vr + 128, :].re("p (j d) -> p j d", j=4))
            if kind == 0:
                P.dma("sync", out=Q_sb[i][0:96, :].k("Q0"), in_=qm[h])
            else:
                P.dma("sync", out=Q_sb[i][0:64, :].k("Q0"), in_=qf[h])
                P.dma("gpsimd", out=Q_sb[i][64:67, :].k("Q1"), in_=fq[h])

        iters = []
        for idx in range(16):
            for m in range(4):
                nkp = (4 * m + 4) * 2
                for kp in range(nkp):
                    iters.append((idx, m, kp, nkp))

        def stage_qk(n):
            idx, m, kp, nkp = iters[n]
            kind, h = heads[idx]
            i = idx % 2
            scale = 96.0 ** -0.5 if kind == 0 else 0.125
            i3 = n % NSB
            ps = pss[i3]
            for e in range(2):
                kt = 2 * kp + e
                P.pe.matmul(out=ps[:, e * 512:(e + 1) * 512], lhsT=K_sb[i][0:96, kt * 128:(kt + 1) * 128],
                            rhs=Q_sb[i][0:96, m * 512:(m + 1) * 512], start=True, stop=True)
            blk = (2 * kp) // 4
            if blk >= 4 * m:
                jr = blk - 4 * m
                k4 = (2 * kp) % 4
                mm = mt[cnt[1] % 2]
                cnt[1] += 1
                P.dve.scalar_tensor_tensor(out=mm.full(), in0=mskb[:, kind * 4 + k4:kind * 4 + k4 + 2, :].re("p a b -> p (a b)"),
                                           scalar=sel[:, 24 + jr:25 + jr], in1=ps.full(), op0=ALU.mult, op1=ALU.add)
                src = mm.full()
                bias = sel[:, 28 + jr:29 + jr] if kind == 0 else FP["btab"][:, m, blk, h:h + 1]
            else:
                src = ps.full()
                bias = zero[:, 0:1] if kind == 0 else FP["btab"][:, m, blk, h:h + 1]
            P.act.activation(out=pt[i3].full(), in_=src, func=AF.Exp, scale=scale, bias=bias)

        def stage_pv(n):
            idx, m, kp, nkp = iters[n]
            kind, h = heads[idx]
            i = idx % 2
            oacc = pbo[(idx * 4 + m) % 2]
            for e in range(2):
                kt = 2 * kp + e
                P.pe.matmul(out=oacc.full(), lhsT=V_sb[i][:, kt, :], rhs=pt[n % NSB][:, e * 512:(e + 1) * 512],
                            start=(kp == 0 and e == 0), stop=(kp == nkp - 1 and e == 1))
            if kp == nkp - 1:
                odst = omd if kind == 0 else ofd
                P.dve.reciprocal(out=rl[64:128, :], in_=oacc[64:128, :])
                P.dve.tensor_copy(out=rl2.full(), in_=rl[64:128, :])
                o = ot[m % 2]
                P.dve.tensor_tensor(out=o.full(), in0=oacc[0:64, :], in1=rl2.full(), op=ALU.mult)
                P.dma("sync", out=odst[h * 64:(h + 1) * 64, m * 512:(m + 1) * 512], in_=o.full())

        pcs = [P.sb(f"pcs{i}", [128, 2048], F32) for i in range(2)]
        pcb = [P.sb(f"pcb{i}", [128, 2048], BF16) for i in range(2)]
        jobs = []
        for (src, dst, rows, cols) in ((w_a[l], wab, 512, D), (w_b[l], wbb, 512, D), (w_c[l], wcb, 1024, D), (w_o[l], wob, 1024, D),
                                       (w_up[l], wub, D, 5632), (w_dn[l], wdb, 2816, D)):
            for r0 in range(0, rows, 128):
                for c0 in range(0, cols, 2048):
                    n_ = min(2048, cols - c0)
                    jobs.append((src[r0:r0 + 128, c0:c0 + n_], dst[r0:r0 + 128, c0:c0 + n_], n_))
        jcnt = [0]

        def precast_one():
            if jcnt[0] >= len(jobs):
                return
            src, dst, n_ = jobs[jcnt[0]]
            i = jcnt[0] % 2
            jcnt[0] += 1
            P.dma("gpsimd", out=pcs[i][:, 0:n_], in_=src)
            P.pool.tensor_copy(out=pcb[i][:, 0:n_], in_=pcs[i][:, 0:n_])
            P.dma("gpsimd", out=dst, in_=pcb[i][:, 0:n_])

        every = max(1, len(iters) // (len(jobs) + 4))
        loads(0)
        loads(1)
        for n in range(len(iters) + LA):
            if n < len(iters):
                if iters[n][0] == 5 and iters[n][1] == 0 and iters[n][2] == 0:
                    f_prep()
                stage_qk(n)
            if n >= LA:
                stage_pv(n - LA)
                idx_p, m_p, kt_p, nk_p = iters[n - LA]
                if m_p == 3 and kt_p == nk_p - 1 and idx_p + 2 < 16:
                    loads(idx_p + 2)
            if n % every == every - 1:
                precast_one()
        while jcnt[0] < len(jobs):
            precast_one()

    def phase_ssd2(l):
        K = load_consts()
        sel = K["sel"]
        rowc = P.sb("rowc", [128, 56], F32)
        P.dma("sync", out=rowc.full(), in_=rowc_d[l])
        fg = load_fg()
        Hin = P.sb("Hin", [128, 1024], F32)
        Hsel = P.sb("Hsel", [128, 4, 1024], F32)
        Sst = [P.sb(f"Sst{i}", [128, 1024], F32) for i in range(2)]
        P.dve.memset(ap=Hin.full(), constant=0.0)
        P.dve.memset(ap=Hsel.full(), constant=0.0)
        v3 = lambda v: v.re("p (h d) -> p h d", h=16)
        for s_ in range(16):
            m, r = divmod(s_, 4)
            P.dve.scalar_tensor_tensor(out=Hsel[:, m, :], in0=Hin.full(), scalar=sel[:, 8 + s_:9 + s_], in1=Hsel[:, m, :],
                                       op0=ALU.mult, op1=ALU.add)
            if s_ < 15:
                st_ = Sst[s_ % 2]
                P.dma("sync" if s_ % 2 else "gpsimd", out=st_.full(),
                      in_=sxg[m // 2][r * 256 + (m % 2) * 128:r * 256 + (m % 2 + 1) * 128, :])
                dcs = fg[:, r, 160 + m * 16:160 + (m + 1) * 16]
                P.dve.tensor_tensor(out=v3(Hin.full()), in0=v3(Hin.full()),
                                    in1=dcs.f(lambda a: a.unsqueeze(2).to_broadcast([128, 16, 64])), op=ALU.mult)
                P.pool.tensor_tensor(out=Hin.full(), in0=Hin.full(), in1=st_.full(), op=ALU.add)
        ssd_scan(l, K, pass1=False, Hinit=Hsel, rowc=rowc)

    def write_tails(txs):
        P.dma("sync", out=tx.full(), in_=txs.full().re("p m k c -> p (m k c)"))

    def halo_exchange(dst):
        K = load_consts()
        sel = K["sel"]
        P.pool.collective_compute(kind="AllGather", op=ALU.bypass, replica_groups=RG, ins=[tx.full()], outs=[txg.full()])
        tg = P.sb("tg", [128, 4, 128], F32)
        P.dma("sync", out=tg.full(), in_=txg.full().re("(r p) c -> p r c", p=128))
        hl = P.sb("hl", [128, 4, 32], F32)
        P.dve.memset(ap=hl.full(), constant=0.0)
        for m in range(4):
            for r in range(4):
                P.dve.scalar_tensor_tensor(out=hl[:, m, :], in0=tg[:, r, m * 32:(m + 1) * 32], scalar=sel[:, r:r + 1],
                                           in1=hl[:, m, :], op0=ALU.mult, op1=ALU.add)
            if m >= 1:
                P.dve.scalar_tensor_tensor(out=hl[:, m, :], in0=tg[:, 3, (m - 1) * 32:m * 32], scalar=sel[:, 4:5],
                                           in1=hl[:, m, :], op0=ALU.mult, op1=ALU.add)
        dv = dst.full().re("(kc p) n -> p kc n", p=128)
        for m in range(4):
            P.dma("sync", out=dv[:, :, m * SW:m * SW + 4], in_=hl[:, m, :].re("p (k c) -> p k c", c=4))

    def phase_merge(l):
        K = load_consts()
        ones, eps = K["cm"][:, 3, :], K["eps"]
        gsb = P.sb("gsb", [128, 8], F32)
        P.dma("sync", out=gsb.full(), in_=gssm_d[l])
        pb = [P.ps(f"pb{i}", [128, 512], F32) for i in range(8)]
        def load_wb(dram_bf, kc_n, name, q):
            bfb = P.sb(name, [128, kc_n, D], BF16)
            P.dma(q, out=bfb.full(), in_=dram_bf.full().re("(kc p) n -> p kc n", p=128))
            return bfb

        Wa = load_wb(wab, 4, "Wa", "sync")
        Wb = load_wb(wbb, 4, "Wb", "gpsimd")
        Wc = load_wb(wcb, 8, "Wc", "sync")
        Wo = load_wb(wob, 8, "Wo", "gpsimd")
        ys = P.sb("ys", [128, 8, 512], F32)
        szs = P.sb("szs", [128, 8, 512], BF16)
        yn = P.sb("yn", [128, 8, 512], BF16)
        omsL = [P.sb(f"oms{i}", [128, 4, 512], BF16) for i in range(2)]
        ofsL = [P.sb(f"ofs{i}", [128, 4, 512], BF16) for i in range(2)]
        gsL = [P.sb(f"gs{i}", [128, 24, 512], BF16) for i in range(2)]
        xs = P.sb("xs", [128, 8, 512], F32)
        sq = P.sb("sq", [128, 512], F32)
        rstd = P.sb("rstd", [128, 512], F32)
        m1 = [P.sb(f"m1_{i}", [128, 512], F32) for i in range(2)]
        m2 = [P.sb(f"m2_{i}", [128, 512], F32) for i in range(2)]
        m3 = [P.sb(f"m3_{i}", [128, 512], F32) for i in range(2)]
        mg = P.sb("mg", [128, 8, 512], BF16)
        xo = [P.sb(f"xo{i}", [128, 512], F32) for i in range(2)]
        txs = P.sb("txs", [128, 4, 8, 4], F32)
        ch = lambda d: d.full().re("(kc p) n -> p kc n", p=128)
        xmv = ch(xmid)
        def ld_norm(ti):
            ts = slice(ti * 512, (ti + 1) * 512)
            P.dma("sync", out=ys.full(), in_=ch(yd)[:, :, ts])
            P.dma("gpsimd", out=szs.full(), in_=ch(szd)[:, :, ts])

        def ld_rest(ti):
            ts = slice(ti * 512, (ti + 1) * 512)
            P.dma("sync", out=omsL[ti % 2].full(), in_=ch(omd)[:, :, ts])
            P.dma("gpsimd", out=ofsL[ti % 2].full(), in_=ch(ofd)[:, :, ts])
            P.dma("sync", out=gsL[ti % 2].full(), in_=ch(gd)[:, :, ts])

        ld_norm(0)
        ld_rest(0)
        for ti in range(4):
            ts = slice(ti * 512, (ti + 1) * 512)
            xsl = slice(ti * SW + 4, ti * SW + 516)
            oms, ofs, gs = omsL[ti % 2], ofsL[ti % 2], gsL[ti % 2]
            P.dma("gpsimd", out=xs.full(), in_=ch(xb[l])[:, :, xsl])
            ps = pb[7]
            for kc in range(8):
                P.dve.tensor_tensor(out=ys[:, kc, :], in0=ys[:, kc, :], in1=szs[:, kc, :], op=ALU.mult)
                P.act.activation(out=sq.full(), in_=ys[:, kc, :], func=AF.Square)
                P.pe.matmul(out=ps.full(), lhsT=ones, rhs=sq.full(), start=(kc == 0), stop=(kc == 7))
            P.act.activation(out=rstd.full(), in_=ps.full(), func=AF.Ln, bias=eps[:, 0:1], scale=1.0 / 1024.0)
            P.act.activation(out=rstd.full(), in_=rstd.full(), func=AF.Exp, scale=-0.5)
            for kc in range(8):
                P.dve.scalar_tensor_tensor(out=yn[:, kc, :], in0=ys[:, kc, :], scalar=gsb[:, kc:kc + 1], in1=rstd.full(),
                                           op0=ALU.mult, op1=ALU.mult)
            if ti + 1 < 4:
                ld_norm(ti + 1)
                ld_rest(ti + 1)
            for oc in range(8):
                i2 = oc % 2
                osl = slice(oc * 128, (oc + 1) * 128)
                pa, pbb, pc = pb[0 + i2 * 3], pb[1 + i2 * 3], pb[2 + i2 * 3]
                for kc in range(4):
                    P.pe.matmul(out=pa.full(), lhsT=Wa[:, kc, osl], rhs=oms[:, kc, :], start=(kc == 0), stop=(kc == 3))
                for kc in range(4):
                    P.pe.matmul(out=pbb.full(), lhsT=Wb[:, kc, osl], rhs=ofs[:, kc, :], start=(kc == 0), stop=(kc == 3))
                for kc in range(8):
                    P.pe.matmul(out=pc.full(), lhsT=Wc[:, kc, osl], rhs=yn[:, kc, :], start=(kc == 0), stop=(kc == 7))
                P.dve.tensor_tensor(out=m1[i2].full(), in0=pa.full(), in1=gs[:, oc, :], op=ALU.mult)
                P.dve.tensor_tensor(out=m2[i2].full(), in0=pbb.full(), in1=gs[:, 8 + oc, :], op=ALU.mult)
                P.dve.tensor_tensor(out=m3[i2].full(), in0=pc.full(), in1=gs[:, 16 + oc, :], op=ALU.mult)
                P.pool.tensor_tensor(out=m1[i2].full(), in0=m1[i2].full(), in1=m2[i2].full(), op=ALU.add)
                P.pool.tensor_tensor(out=mg[:, oc, :], in0=m1[i2].full(), in1=m3[i2].full(), op=ALU.add)
            for oc in range(8):
                i2 = oc % 2
                ps = pb[6 + i2]
                for kc in range(8):
                    P.pe.matmul(out=ps.full(), lhsT=Wo[:, kc, oc * 128:(oc + 1) * 128], rhs=mg[:, kc, :],
                                start=(kc == 0), stop=(kc == 7))
                P.dve.tensor_tensor(out=xo[i2].full(), in0=ps.full(), in1=xs[:, oc, :], op=ALU.add)
                P.pool.tensor_copy(out=txs[:, ti, oc, :], in_=xo[i2][:, 508:512])
                P.dma("sync" if i2 else "gpsimd", out=xmv[:, oc, xsl], in_=xo[i2].full())
        write_tails(txs)

    def phase_ffn(l, last):
        K = load_consts()
        ones, eps = K["cm"][:, 3, :], K["eps"]
        cstb = P.sb("cstb", [128, offD2["_n"]], F32)
        C = Cst(P, cstb, offD2)
        P.dma("sync", out=cstb.full(), in_=cstD_d[l])
        pb = [P.ps(f"pb{i}", [128, 512], F32) for i in range(8)]
        Wu = P.sb("Wu", [128, 8, 5632], BF16)
        Wd = P.sb("Wd", [128, 22, D], BF16)
        wubv = wub.full().re("(kc p) n -> p kc n", p=128)
        for (c0, c1) in ((0, 512), (2816, 3328), (512, 2816), (3328, 5632)):
            P.dma("sync" if c0 < 2816 else "gpsimd", out=Wu[:, :, c0:c1], in_=wubv[:, :, c0:c1])
        wdbv = wdb.full().re("(kc p) n -> p kc n", p=128)
        P.dma("sync", out=Wd[:, 0:11, :], in_=wdbv[:, 0:11, :])
        P.dma("gpsimd", out=Wd[:, 11:22, :], in_=wdbv[:, 11:22, :])
        xst = P.sb("xst", [128, 8, 512], F32)
        hn = P.sb("hn", [128, 8, 512], BF16)
        sq = P.sb("sq", [128, 512], F32)
        rstd = P.sb("rstd", [128, 512], F32)
        act = P.sb("act", [128, 22, 512], BF16)
        upre = [P.sb(f"upre{i}", [128, 516], F32) for i in range(2)]
        acc = [P.sb(f"acc{i}", [128, 512], F32) for i in range(2)]
        sg = P.sb("sg", [128, 512], F32)
        carry = P.sb("carry", [128, 44, 4], F32)
        xo = [P.sb(f"xo{i}", [128, 512], F32) for i in range(2)]
        txs = P.sb("txs", [128, 4, 8, 4], F32)
        xTv = xmid.full().re("(kc p) n -> p kc n", p=128)
        dst = out if last else xb[l + 1]
        dv = dst.full().re("(kc p) n -> p kc n", p=128)
        tiles = []
        for m in range(4):
            tiles.append((m * SW, 4, True, m))
            tiles.append((m * SW + 4, 512, False, m))
        pcnt = [0]
        for (c0, w, is_halo, m) in tiles:
            P.dma("sync", out=xst[:, :, 0:w], in_=xTv[:, :, c0:c0 + w])
            ps = pb[7]
            for kc in range(8):
                P.act.activation(out=sq[:, 0:w], in_=xst[:, kc, 0:w], func=AF.Square)
                P.pe.matmul(out=ps[:, 0:w], lhsT=ones, rhs=sq[:, 0:w], start=(kc == 0), stop=(kc == 7))
            P.act.activation(out=rstd[:, 0:w], in_=ps[:, 0:w], func=AF.Ln, bias=eps[:, 0:1], scale=1.0 / 1024.0)
            P.act.activation(out=rstd[:, 0:w], in_=rstd[:, 0:w], func=AF.Exp, scale=-0.5)
            for kc in range(8):
                P.dve.scalar_tensor_tensor(out=hn[:, kc, 0:w], in0=xst[:, kc, 0:w], scalar=C.col("g_ffn", kc),
                                           in1=rstd[:, 0:w], op0=ALU.mult, op1=ALU.mult)
            for i in range(22):
                accs = []
                for j, cg in enumerate((i, 22 + i)):
                    ps = pb[pcnt[0] % 4]
                    pcnt[0] += 1
                    for kc in range(8):
                        P.pe.matmul(out=ps[:, 0:w], lhsT=Wu[:, kc, cg * 128:(cg + 1) * 128], rhs=hn[:, kc, 0:w],
                                    start=(kc == 0), stop=(kc == 7))
                    if is_halo:
                        P.act.copy(out=carry[:, cg, :], in_=ps[:, 0:4])
                        continue
                    up = upre[j]
                    a0 = acc[j]
                    P.act.copy(out=up[:, 4:516], in_=ps.full())
                    P.act.activation(out=a0.full(), in_=ps.full(), func=AF.Identity, scale=C.col("fw2", cg), bias=C.col("fb", cg))
                    P.pool.tensor_copy(out=up[:, 0:4], in_=carry[:, cg, :])
                    P.dve.scalar_tensor_tensor(out=a0.full(), in0=up[:, 3:515], scalar=C.col("fw1", cg), in1=a0.full(),
                                               op0=ALU.mult, op1=ALU.add)
                    P.dve.scalar_tensor_tensor(out=a0.full(), in0=up[:, 2:514], scalar=C.col("fw0", cg), in1=a0.full(),
                                               op0=ALU.mult, op1=ALU.add)
                    accs.append(a0)
                if is_halo:
                    continue
                P.act.activation(out=sg.full(), in_=accs[0].full(), func=AF.Silu)
                P.pool.tensor_tensor(out=act[:, i, :], in0=sg.full(), in1=accs[1].full(), op=ALU.mult)
            if is_halo:
                continue
            for oc in range(8):
                i2 = oc % 2
                ps = pb[4 + i2]
                for i in range(22):
                    P.pe.matmul(out=ps.full(), lhsT=Wd[:, i, oc * 128:(oc + 1) * 128], rhs=act[:, i, :],
                                start=(i == 0), stop=(i == 21))
                P.dve.tensor_tensor(out=xo[i2].full(), in0=ps.full(), in1=xst[:, oc, :], op=ALU.add)
                if last:
                    P.dma("sync" if i2 else "gpsimd", out=dv[:, oc, m * 512:(m + 1) * 512], in_=xo[i2].full())
                else:
                    P.pool.tensor_copy(out=txs[:, m, oc, :], in_=xo[i2][:, 508:512])
                    P.dma("sync" if i2 else "gpsimd", out=dv[:, oc, m * SW + 4:m * SW + 516], in_=xo[i2].full())
        if not last:
            write_tails(txs)

    def gather_e1():
        gather_pairs(list(zip(sx, sxg)) + [(fx, fxg)])

    nl = L if stop is None else stop[0]
    done = False
    for l in range(nl):
        last_l = (stop is not None and l == nl - 1)
        phase_A(l)
        P.emit(final=False)
        ssd_scan(l, load_consts(), pass1=True)
        P.emit(final=False)
        if last_l and stop[1] == "A":
            break
        gather_e1()
        phase_attn(l)
        P.emit(final=False)
        phase_ssd2(l)
        P.emit(final=False)
        if last_l and stop[1] == "B":
            break
        phase_merge(l)
        P.emit(final=False)
        halo_exchange(xmid)
        P.emit(final=False)
        if last_l and stop[1] == "C":
            break
        phase_ffn(l, last=(l == L - 1))
        P.emit(final=False)
        if l < L - 1:
            halo_exchange(xb[l + 1])
            P.emit(final=False)
    loc = {"kxmg0": kxmg[0], "vxg0": vxg[0], "sxg0": sxg[0], "fxg": fxg, "qm": qm, "qf": qf, "fq": fq, "omd": omd, "ofd": ofd, "yd": yd,
           "xmid": xmid, "xb1": xb[1], "szd": szd, "gd": gd, "xtm": xtm, "btm": btm, "bct": bct, "dtd": dtd, "atd": atd}
    for name in dbg:
        src = loc[name]
        shp = list(src.h.shape) if hasattr(src.h, "shape") else None
        dd = P.dram("dbg_" + name, shp, src.h.dtype, EO)
        P.dma("sync", out=dd.full(), in_=src.full())
    P.emit(final=True)
    return nc, P


def _stripe_tokens(p):
    return np.concatenate([np.arange((4 * m + p) * 512, (4 * m + p + 1) * 512) for m in range(4)])


def fused_in_maps(inp):
    L = 2
    cpsA = [a_colpack(inp, l) for l in range(L)]
    cpsD = [d2_colpack(inp, l) for l in range(L)]
    offA = dict(cpsA[0].off)
    offA["_n"] = cpsA[0].n
    offD = dict(cpsD[0].off)
    offD["_n"] = cpsD[0].n
    cstA = np.stack([c.array() for c in cpsA])
    cstD = np.stack([c.array() for c in cpsD])
    w_kp = np.zeros((L, 256, 8, 96), np.float32)
    wukv = inp["mla_w_ukv"].reshape(L, 256, 8, 128)
    w_kp[:, :, :, 0:64] = wukv[:, :, :, 0:64]
    w_v = np.ascontiguousarray(wukv[:, :, :, 64:128].reshape(L, 256, 512))
    gssm = np.ascontiguousarray(inp["ssm_norm_g"].reshape(L, 8, 128).transpose(0, 2, 1))
    rowc = np.stack([fused_rowpack(inp, l) for l in range(L)])
    msk, cm = _bc_consts()
    mats = _const_mats()
    shared = {
        "w_in": np.ascontiguousarray(inp["w_in"]), "w_uq": np.ascontiguousarray(inp["mla_w_uq"]),
        "w_kp": np.ascontiguousarray(w_kp.reshape(L, 256, 768)), "w_v": w_v,
        "w_a": np.ascontiguousarray(inp["w_br_mla"]), "w_b": np.ascontiguousarray(inp["w_br_fox"]),
        "w_c": np.ascontiguousarray(inp["w_br_ssm"]), "w_o": np.ascontiguousarray(inp["w_out"]),
        "w_up": np.ascontiguousarray(inp["ffn_w_up"]), "w_dn": np.ascontiguousarray(inp["ffn_w_down"]),
        "cstA": cstA, "cstD": cstD, "gssm": gssm, "rowc": rowc, "msk": msk, "cm": cm, "mats": mats,
    }
    in_maps = []
    for c in range(8):
        b, p = c // 4, c % 4
        xT = np.zeros((D, 4 * SW), np.float32)
        xbT = inp["x"][b].T
        for m in range(4):
            s_ = 4 * m + p
            xT[:, m * SW + 4:m * SW + 516] = xbT[:, s_ * 512:(s_ + 1) * 512]
            if s_ > 0:
                xT[:, m * SW:m * SW + 4] = xbT[:, s_ * 512 - 4:s_ * 512]
        sel = np.zeros((128, 32), np.float32)
        if p >= 1:
            sel[:, p - 1] = 1.0
        else:
            sel[:, 4] = 1.0
        for s_ in range(16):
            if s_ % 4 == p:
                sel[:, 8 + s_] = 1.0
        for jr in range(4):
            sel[:, 24 + jr] = 1.0 if jr == p else 0.0
            sel[:, 28 + jr] = NEG if jr > p else 0.0
        d = dict(shared)
        d["x0"] = np.ascontiguousarray(xT)
        d["pos"] = np.ascontiguousarray(inp["positions"][b][_stripe_tokens(p)][None, :]).astype(np.int32)
        d["sel"] = sel
        in_maps.append(d)
    return in_maps, offA, offD


def kernel_fused(**inp):
    inp = {k: np.asarray(v) for k, v in inp.items()}
    in_maps, offA, offD = fused_in_maps(inp)
    if "F" not in _PROG_CACHE:
        _PROG_CACHE["F"] = build_fused(offA, offD)[0]
    res = run_bass_kernel_spmd(_PROG_CACHE["F"], in_maps, core_ids=list(range(8))).results
    xo = np.zeros((2, S_, D), np.float32)
    for c in range(8):
        b, p = c // 4, c % 4
        xo[b, _stripe_tokens(p), :] = np.asarray(res[c]["out"]).T
    return xo
```
